# Optimizing a Trainium2 kernel written in Bass

```python
import math
import jax
import jax.numpy as jnp
from jax import lax
import numpy as np

D_MODEL = 1024
BATCH = 8
SEQ = 2048
DEPTH = 4
DEC_BATCH = 128
DEC_SEQ = 1
PAST_LEN = 16384
PAGE_SIZE = 128

M_WIDTH = D_MODEL
M_HEADS = 4
M_HD = M_WIDTH // M_HEADS
CONV_W = 4
M_CHUNK = 128
R_WIDTH = D_MODEL
R_HD = 64
R_HEADS = R_WIDTH // R_HD
R_LORA_W = 64
R_LORA_A = 64
R_SHIFT_W = 3 * R_WIDTH + R_LORA_W + R_LORA_A
S_WIDTH = D_MODEL
S_GROUP = 16
S_GROUPS = S_WIDTH // S_GROUP
S_STATE = 64
N_BRANCH = 3
IN_SPLITS = (M_WIDTH, M_HEADS, M_HEADS, M_WIDTH, M_WIDTH, R_SHIFT_W, R_WIDTH, S_WIDTH, S_WIDTH, N_BRANCH * D_MODEL)
N_IN = sum(IN_SPLITS)
ALPHA = (2.0 * DEPTH) ** 0.25
BETA = (8.0 * DEPTH) ** -0.25
LN_EPS = 1e-5
RWKV_GN_EPS = 64e-5
NEG = -1e30

kernel_name = 'hybrid_mlstm_rwkv7_s5_step'


def _f32(a):
    return a.astype(jnp.float32)


def _layer_norm(x, g, b, eps=LN_EPS):
    xf = _f32(x)
    mu = xf.mean(-1, keepdims=True)
    var = jnp.square(xf - mu).mean(-1, keepdims=True)
    y = (xf - mu) * lax.rsqrt(var + eps) * _f32(g) + _f32(b)
    return y.astype(x.dtype)


def _head_norm(h, eps):
    mu = h.mean(-1, keepdims=True)
    var = jnp.square(h - mu).mean(-1, keepdims=True)
    return (h - mu) * lax.rsqrt(var + eps)


def _split_cols(z):
    offsets = [int(o) for o in np.cumsum(IN_SPLITS)[:-1]]
    return jnp.split(z, offsets, axis=-1)


def _causal_conv(xm, buf, w, b):
    t = xm.shape[1]
    xp = jnp.concatenate([buf.astype(xm.dtype), xm], axis=1)
    out = b
    for j in range(CONV_W):
        out = out + w[j] * xp[:, j:j + t]
    return out, xp[:, -(CONV_W - 1):]


def _mlstm_chunkwise(q, k, v, ig, fg, c0, n0, m0):
    bsz, t, h, d = q.shape
    L = min(M_CHUNK, t)
    nc = -(-t // L)
    pad = nc * L - t
    logf = jax.nn.log_sigmoid(fg)
    if pad:
        pw = ((0, 0), (0, pad), (0, 0), (0, 0))
        q = jnp.pad(q, pw)
        k = jnp.pad(k, pw)
        v = jnp.pad(v, pw)
        logf = jnp.pad(logf, pw[:3])
        ig = jnp.pad(ig, pw[:3], constant_values=NEG)

    def to_chunks(a):
        a = a.reshape((bsz, nc, L) + a.shape[2:])
        return jnp.transpose(a, (1, 0, 3, 2) + tuple(range(4, a.ndim)))

    causal = jnp.tril(jnp.ones((L, L), dtype=bool))

    def body(carry, xs):
        c, n, m = carry
        qc, kc, vc, igc, lfc = xs
        b = jnp.cumsum(lfc, axis=-1)
        log_intra = jnp.where(causal, b[..., :, None] - b[..., None, :] + igc[..., None, :], NEG)
        log_init = b + m[..., None]
        m_t = jnp.maximum(log_init, log_intra.max(-1))
        w_intra = jnp.exp(log_intra - m_t[..., None])
        w_init = jnp.exp(log_init - m_t)
        s = jnp.einsum('bhtd,bhsd->bhts', qc, kc) * w_intra
        num = jnp.einsum('bhts,bhsv->bhtv', s, vc) + w_init[..., None] * jnp.einsum('bhtk,bhkv->bhtv', qc, c)
        den = s.sum(-1) + w_init * jnp.einsum('bhtk,bhk->bht', qc, n)
        hc = num / jnp.maximum(jnp.abs(den), jnp.exp(-m_t))[..., None]
        b_last = b[..., -1]
        log_state = b_last[..., None] - b + igc
        m_new = jnp.maximum(b_last + m, log_state.max(-1))
        w_state = jnp.exp(log_state - m_new[..., None])
        decay = jnp.exp(b_last + m - m_new)
        c_new = decay[..., None, None] * c + jnp.einsum('bhs,bhsk,bhsv->bhkv', w_state, kc, vc)
        n_new = decay[..., None] * n + jnp.einsum('bhs,bhsk->bhk', w_state, kc)
        return (c_new, n_new, m_new), hc

    xs = (to_chunks(q), to_chunks(k), to_chunks(v), to_chunks(ig), to_chunks(logf))
    (c_f, n_f, m_f), hs = lax.scan(body, (c0, n0, m0), xs)
    hs = jnp.transpose(hs, (1, 0, 3, 2, 4)).reshape(bsz, nc * L, h, d)[:, :t]
    return hs, c_f, n_f, m_f


def _rwkv7_recurrence(r, w, k, v, kk, a, s0):
    def step(s, xs):
        rt, wt, kt, vt, kkt, at = xs
        sa = jnp.einsum('bhij,bhj->bhi', s, -kkt)
        s = s * wt[:, :, None, :] + sa[..., None] * (kkt * at)[:, :, None, :] + vt[..., None] * kt[:, :, None, :]
        return s, jnp.einsum('bhij,bhj->bhi', s, rt)
    xs = tuple(jnp.moveaxis(z, 1, 0) for z in (r, w, k, v, kk, a))
    s_f, y = lax.scan(step, s0, xs)
    return jnp.moveaxis(y, 0, 1), s_f


def _s5_scan(u, s0_re, s0_im, lam_re, lam_im, log_dt, b_re, b_im):
    lam_re = jnp.minimum(lam_re, -1e-4)
    dt = jnp.exp(log_dt)[:, None]
    mag = jnp.exp(lam_re * dt)
    lb_re = mag * jnp.cos(lam_im * dt)
    lb_im = mag * jnp.sin(lam_im * dt)
    den = lam_re * lam_re + lam_im * lam_im
    nr = lb_re - 1.0
    coef_re = (nr * lam_re + lb_im * lam_im) / den
    coef_im = (lb_im * lam_re - nr * lam_im) / den
    bb_re = coef_re[..., None] * b_re - coef_im[..., None] * b_im
    bb_im = coef_re[..., None] * b_im + coef_im[..., None] * b_re
    bu_re = jnp.einsum('gpc,btgc->btgp', bb_re, u)
    bu_im = jnp.einsum('gpc,btgc->btgp', bb_im, u)
    bu_re = bu_re.at[:, 0].add(lb_re * s0_re - lb_im * s0_im)
    bu_im = bu_im.at[:, 0].add(lb_re * s0_im + lb_im * s0_re)
    a_re = jnp.broadcast_to(lb_re, bu_re.shape)
    a_im = jnp.broadcast_to(lb_im, bu_im.shape)

    def combine(e1, e2):
        a1r, a1i, b1r, b1i = e1
        a2r, a2i, b2r, b2i = e2
        return (a2r * a1r - a2i * a1i, a2r * a1i + a2i * a1r,
                a2r * b1r - a2i * b1i + b2r, a2r * b1i + a2i * b1r + b2i)

    _, _, s_re, s_im = lax.associative_scan(combine, (a_re, a_im, bu_re, bu_im), axis=1)
    return s_re, s_im


def _layer(x, st, p):
    c0, n0, m0, conv0, wkv0, shift0, sre0, sim0 = st
    bsz, t, _ = x.shape
    dt_ = x.dtype
    xm, ig, fg, og, zm, rcols, zr, u, zs, gm = _split_cols(x @ p['w_in'])

    xc, conv_new = _causal_conv(xm, conv0, p['m_conv_w'], p['m_conv_b'])
    xc = jax.nn.silu(xc)
    xch = xc.reshape(bsz, t, M_HEADS, M_HD)
    xmh = xm.reshape(bsz, t, M_HEADS, M_HD)
    q = _f32(jnp.einsum('bthd,hde->bthe', xch, p['m_wq'])) * (M_HD ** -0.5)
    k = _f32(jnp.einsum('bthd,hde->bthe', xch, p['m_wk']))
    v = _f32(jnp.einsum('bthd,hde->bthe', xmh, p['m_wv']))
    hm, c_new, n_new, m_new = _mlstm_chunkwise(
        q, k, v, _f32(ig) + _f32(p['m_ig_b']), _f32(fg) + _f32(p['m_fg_b']),
        _f32(c0), _f32(n0), _f32(m0))
    hm = jax.nn.sigmoid(_f32(og)).reshape(bsz, t, M_HEADS, M_HD) * hm
    hm = _head_norm(hm, LN_EPS).reshape(bsz, t, M_WIDTH) * _f32(p['m_norm_g'])
    hm = hm.astype(dt_) + p['m_skip'] * xc
    y_m = (hm * jax.nn.silu(zm)) @ p['w_bm']

    prev = jnp.concatenate([shift0[:, None].astype(dt_), rcols[:, :-1]], axis=1)
    shift_new = rcols[:, -1]
    xr = rcols + p['r_mu'] * (prev - rcols)
    r, kr, vr, wd, ad = jnp.split(xr, [R_WIDTH, 2 * R_WIDTH, 3 * R_WIDTH, 3 * R_WIDTH + R_LORA_W], axis=-1)
    heads = lambda a: a.reshape(bsz, t, R_HEADS, R_HD)
    wlog = -jax.nn.softplus(-_f32(p['r_w0'] + jnp.tanh(wd) @ p['r_w2'])) - 0.5
    decay = jnp.exp(-jnp.exp(wlog))
    a = jax.nn.sigmoid(_f32(p['r_a0'] + ad @ p['r_a2']))
    kk = heads(_f32(kr * p['r_k_k']))
    kk = kk / jnp.maximum(jnp.sqrt(jnp.sum(kk * kk, -1, keepdims=True)), 1e-12)
    kr = _f32(kr) * (1.0 + (a - 1.0) * _f32(p['r_k_a']))
    rh, kh, vh = heads(_f32(r)), heads(kr), heads(_f32(vr))
    yr, wkv_new = _rwkv7_recurrence(rh, heads(decay), kh, vh, kk, heads(a), _f32(wkv0))
    yr = _head_norm(yr, RWKV_GN_EPS).reshape(bsz, t, R_WIDTH) * _f32(p['r_ln_g']) + _f32(p['r_ln_b'])
    bonus = jnp.sum(rh * kh * _f32(p['r_r_k']), -1, keepdims=True) * vh
    yr = yr + bonus.reshape(bsz, t, R_WIDTH)
    y_r = (yr.astype(dt_) * jax.nn.silu(zr)) @ p['w_br']

    uf = _f32(u)
    s_re, s_im = _s5_scan(uf.reshape(bsz, t, S_GROUPS, S_GROUP), _f32(sre0), _f32(sim0),
                          _f32(p['s_lam_re']), _f32(p['s_lam_im']), _f32(p['s_log_dt']),
                          _f32(p['s_b_re']), _f32(p['s_b_im']))
    ys = (jnp.einsum('gcp,btgp->btgc', _f32(p['s_c_re']), s_re)
          - jnp.einsum('gcp,btgp->btgc', _f32(p['s_c_im']), s_im))
    ys = ys.reshape(bsz, t, S_WIDTH) + _f32(p['s_d']) * uf
    ys = jax.nn.gelu(ys).astype(dt_)
    ys = ys * jax.nn.sigmoid(ys @ p['s_glu_w'] + p['s_glu_b'])
    y_s = (ys * jax.nn.silu(zs)) @ p['w_bs']

    g = jax.nn.sigmoid(gm).reshape(bsz, t, N_BRANCH, D_MODEL)
    merged = g[:, :, 0] * y_m + g[:, :, 1] * y_r + g[:, :, 2] * y_s
    x_new = _layer_norm(ALPHA * x + merged @ p['w_out'], p['ln_g'], p['ln_b'])
    new_state = (c_new, n_new, m_new, conv_new, wkv_new, shift_new, s_re[:, -1], s_im[:, -1])
    return x_new, new_state


def setup_inputs(seed: int = 0) -> dict:
    key = jax.random.key(seed)
    ks = iter(jax.random.split(key, 64))
    f32 = jnp.float32

    def nrm(shape, scale):
        return scale * jax.random.normal(next(ks), shape, f32)

    def unif(shape, lo, hi):
        return jax.random.uniform(next(ks), shape, f32, lo, hi)

    L = DEPTH
    return {
        'x_prompt': nrm((BATCH, SEQ, D_MODEL), 1.0),
        'x_sample': nrm((DEC_BATCH, DEC_SEQ, D_MODEL), 1.0),
        'state_mlstm_c': nrm((L, DEC_BATCH, M_HEADS, M_HD, M_HD), 0.05),
        'state_mlstm_n': nrm((L, DEC_BATCH, M_HEADS, M_HD), 0.1),
        'state_mlstm_m': nrm((L, DEC_BATCH, M_HEADS), 1.0),
        'state_mlstm_conv': nrm((L, DEC_BATCH, CONV_W - 1, M_WIDTH), 1.0),
        'state_rwkv_wkv': nrm((L, DEC_BATCH, R_HEADS, R_HD, R_HD), 0.1),
        'state_rwkv_shift': nrm((L, DEC_BATCH, R_SHIFT_W), 1.0),
        'state_s5_re': nrm((L, DEC_BATCH, S_GROUPS, S_STATE), 0.1),
        'state_s5_im': nrm((L, DEC_BATCH, S_GROUPS, S_STATE), 0.1),
        'ln_in_g': 1.0 + nrm((D_MODEL,), 0.02),
        'ln_in_b': nrm((D_MODEL,), 0.02),
        'w_in': nrm((L, D_MODEL, N_IN), D_MODEL ** -0.5),
        'm_conv_w': nrm((L, CONV_W, M_WIDTH), CONV_W ** -0.5),
        'm_conv_b': nrm((L, M_WIDTH), 0.02),
        'm_wq': nrm((L, M_HEADS, M_HD, M_HD), M_HD ** -0.5),
        'm_wk': nrm((L, M_HEADS, M_HD, M_HD), M_HD ** -0.5),
        'm_wv': nrm((L, M_HEADS, M_HD, M_HD), M_HD ** -0.5),
        'm_ig_b': nrm((L, M_HEADS), 0.1),
        'm_fg_b': jnp.linspace(3.0, 6.0, M_HEADS) + nrm((L, M_HEADS), 0.1),
        'm_norm_g': 1.0 + nrm((L, M_WIDTH), 0.02),
        'm_skip': 1.0 + nrm((L, M_WIDTH), 0.02),
        'r_mu': unif((L, R_SHIFT_W), 0.0, 1.0),
        'r_w0': jnp.linspace(-6.0, -1.0, R_WIDTH) + nrm((L, R_WIDTH), 0.1),
        'r_w2': nrm((L, R_LORA_W, R_WIDTH), 0.5 * R_LORA_W ** -0.5),
        'r_a0': nrm((L, R_WIDTH), 0.1),
        'r_a2': nrm((L, R_LORA_A, R_WIDTH), 0.5 * R_LORA_A ** -0.5),
        'r_k_k': 0.85 + nrm((L, R_WIDTH), 0.02),
        'r_k_a': 1.0 + nrm((L, R_WIDTH), 0.02),
        'r_r_k': nrm((L, R_HEADS, R_HD), 0.1),
        'r_ln_g': 1.0 + nrm((L, R_WIDTH), 0.02),
        'r_ln_b': nrm((L, R_WIDTH), 0.02),
        's_lam_re': -0.5 + nrm((L, S_GROUPS, S_STATE), 0.01),
        's_lam_im': math.pi * jnp.arange(S_STATE, dtype=f32) + nrm((L, S_GROUPS, S_STATE), 0.01),
        's_log_dt': unif((L, S_GROUPS), math.log(1e-3), math.log(1e-1)),
        's_b_re': nrm((L, S_GROUPS, S_STATE, S_GROUP), (2.0 * S_GROUP) ** -0.5),
        's_b_im': nrm((L, S_GROUPS, S_STATE, S_GROUP), (2.0 * S_GROUP) ** -0.5),
        's_c_re': nrm((L, S_GROUPS, S_GROUP, S_STATE), (2.0 * S_STATE) ** -0.5),
        's_c_im': nrm((L, S_GROUPS, S_GROUP, S_STATE), (2.0 * S_STATE) ** -0.5),
        's_d': nrm((L, S_WIDTH), 1.0),
        's_glu_w': nrm((L, S_WIDTH, S_WIDTH), S_WIDTH ** -0.5),
        's_glu_b': nrm((L, S_WIDTH), 0.02),
        'w_bm': nrm((L, M_WIDTH, D_MODEL), BETA * M_WIDTH ** -0.5),
        'w_br': nrm((L, R_WIDTH, D_MODEL), BETA * R_WIDTH ** -0.5),
        'w_bs': nrm((L, S_WIDTH, D_MODEL), BETA * S_WIDTH ** -0.5),
        'w_out': nrm((L, D_MODEL, D_MODEL), BETA * D_MODEL ** -0.5),
        'ln_g': 1.0 + nrm((L, D_MODEL), 0.02),
        'ln_b': nrm((L, D_MODEL), 0.02),
    }


def reference(x_prompt, x_sample, state_mlstm_c, state_mlstm_n, state_mlstm_m, state_mlstm_conv,
              state_rwkv_wkv, state_rwkv_shift, state_s5_re, state_s5_im,
              ln_in_g, ln_in_b, w_in, m_conv_w, m_conv_b, m_wq, m_wk, m_wv, m_ig_b, m_fg_b,
              m_norm_g, m_skip, r_mu, r_w0, r_w2, r_a0, r_a2, r_k_k, r_k_a, r_r_k, r_ln_g, r_ln_b,
              s_lam_re, s_lam_im, s_log_dt, s_b_re, s_b_im, s_c_re, s_c_im, s_d, s_glu_w, s_glu_b,
              w_bm, w_br, w_bs, w_out, ln_g, ln_b):
    caches = (state_mlstm_c, state_mlstm_n, state_mlstm_m, state_mlstm_conv,
              state_rwkv_wkv, state_rwkv_shift, state_s5_re, state_s5_im)
    bp = x_prompt.shape[0]
    xp = _layer_norm(x_prompt, ln_in_g, ln_in_b)
    xs = _layer_norm(x_sample, ln_in_g, ln_in_b)
    new_p = [[] for _ in caches]
    new_s = [[] for _ in caches]
    for l in range(DEPTH):
        p = {'w_in': w_in[l], 'm_conv_w': m_conv_w[l], 'm_conv_b': m_conv_b[l],
             'm_wq': m_wq[l], 'm_wk': m_wk[l], 'm_wv': m_wv[l], 'm_ig_b': m_ig_b[l], 'm_fg_b': m_fg_b[l],
             'm_norm_g': m_norm_g[l], 'm_skip': m_skip[l], 'r_mu': r_mu[l], 'r_w0': r_w0[l],
             'r_w2': r_w2[l], 'r_a0': r_a0[l], 'r_a2': r_a2[l], 'r_k_k': r_k_k[l], 'r_k_a': r_k_a[l],
             'r_r_k': r_r_k[l], 'r_ln_g': r_ln_g[l], 'r_ln_b': r_ln_b[l], 's_lam_re': s_lam_re[l],
             's_lam_im': s_lam_im[l], 's_log_dt': s_log_dt[l], 's_b_re': s_b_re[l], 's_b_im': s_b_im[l],
             's_c_re': s_c_re[l], 's_c_im': s_c_im[l], 's_d': s_d[l], 's_glu_w': s_glu_w[l],
             's_glu_b': s_glu_b[l], 'w_bm': w_bm[l], 'w_br': w_br[l], 'w_bs': w_bs[l],
             'w_out': w_out[l], 'ln_g': ln_g[l], 'ln_b': ln_b[l]}
        fresh = (jnp.zeros((bp, M_HEADS, M_HD, M_HD), jnp.float32),
                 jnp.zeros((bp, M_HEADS, M_HD), jnp.float32),
                 jnp.full((bp, M_HEADS), NEG, jnp.float32),
                 jnp.zeros((bp, CONV_W - 1, M_WIDTH), x_prompt.dtype),
                 jnp.zeros((bp, R_HEADS, R_HD, R_HD), jnp.float32),
                 jnp.zeros((bp, R_SHIFT_W), x_prompt.dtype),
                 jnp.zeros((bp, S_GROUPS, S_STATE), jnp.float32),
                 jnp.zeros((bp, S_GROUPS, S_STATE), jnp.float32))
        xp, sp = _layer(xp, fresh, p)
        xs, ss = _layer(xs, tuple(c[l] for c in caches), p)
        for i in range(len(caches)):
            new_p[i].append(sp[i].astype(caches[i].dtype))
            new_s[i].append(ss[i].astype(caches[i].dtype))
    pc = [jnp.stack(a) for a in new_p]
    sc = [jnp.stack(a) for a in new_s]
    return (xp, xs, pc[0], sc[0], pc[1], sc[1], pc[2], sc[2], pc[3], sc[3],
            pc[4], sc[4], pc[5], sc[5], pc[6], sc[6], pc[7], sc[7])
```

```python
import math
from contextlib import ExitStack
import numpy as np
import concourse.bass as bass
import concourse.mybir as mybir
from concourse.bass_utils import run_bass_kernel_spmd

F32 = mybir.dt.float32
BF16 = mybir.dt.bfloat16
I32 = mybir.dt.int32
ALU = mybir.AluOpType
AF = mybir.ActivationFunctionType
AX = mybir.AxisListType

D = 1024
KC = 8
N_IN = 12424
M_HEADS = 4
M_HD = 256
R_HEADS = 16
R_HD = 64
R_SHIFT_W = 3200
S_GROUPS = 64
S_STATE = 64
DEPTH_FULL = 4
ALPHA = (2.0 * DEPTH_FULL) ** 0.25
LN_EPS = 1e-5
RWKV_GN_EPS = 64e-5
NEG = -1e30

O_XM, O_IG, O_FG, O_OG, O_ZM = 0, 1024, 1028, 1032, 2056
O_RC, O_ZR, O_U, O_ZS, O_GM = 3080, 6280, 7304, 8328, 9352


class KB:
    def __init__(self, nc, es):
        self.nc = nc
        self.eng = dict(pe=nc.tensor, act=nc.scalar, dve=nc.vector, pool=nc.gpsimd, sp=nc.sync)
        self.stream = {e: [] for e in self.eng}
        self.csem = {e: es.enter_context(nc.semaphore("c_" + e)) for e in ("pe", "act", "dve", "pool")}
        self.ccnt = {e: 0 for e in self.csem}
        self.ring = {}
        for q, n in (("sp", 24), ("pool", 12), ("act", 6)):
            self.ring[q] = [[es.enter_context(nc.semaphore("d_%s%d" % (q, i))), 0] for i in range(n)]
        self.rpos = {q: 0 for q in self.ring}
        self.known = {e: {} for e in self.eng}
        self.lastw = {}
        self.readers = {}
        self.out_tokens = []
        self.n_ops = 0

    def _need(self, e, tok, same_ok=False):
        if tok is None:
            return
        sem, val, owner = tok
        if owner == e and (e == "pe" or same_ok):
            return
        k = id(sem)
        if self.known[e].get(k, 0) >= val:
            return
        self.known[e][k] = val
        self.eng[e].wait_ge(sem, val)

    def _deps(self, e, reads, writes):
        for b in reads:
            self._need(e, self.lastw.get(b))
        for b in writes:
            self._need(e, self.lastw.get(b))
            for tok in self.readers.get(b, ()):
                self._need(e, tok)

    def _commit(self, tok, reads, writes):
        for b in writes:
            self.lastw[b] = tok
            self.readers[b] = []
        for b in reads:
            self.readers.setdefault(b, []).append(tok)

    def op(self, e, fn, reads=(), writes=()):
        self._deps(e, reads, writes)
        self.ccnt[e] += 1
        tok = (self.csem[e], self.ccnt[e], e)
        fn(self.eng[e]).then_inc(self.csem[e], 1)
        self._commit(tok, reads, writes)
        self.n_ops += 1

    def dma(self, q, out, in_, reads=(), writes=(), is_output=False, **kw):
        self._deps(q, reads, writes)
        ring = self.ring[q]
        slot = ring[self.rpos[q] % len(ring)]
        self.rpos[q] += 1
        sem = slot[0]
        if slot[1] > 0:
            self._need(q, (sem, slot[1], None))
        slot[1] += 16
        tok = (sem, slot[1], None)
        self.eng[q].dma_start(out=out, in_=in_, **kw).then_inc(sem, 16)
        self._commit(tok, reads, writes)
        if is_output:
            self.out_tokens.append(tok)
        self.n_ops += 1

    def finish(self):
        for q in self.ring:
            for sem, val in self.ring[q]:
                if val > 0:
                    self._need("sp", (sem, val, None))
        for e in self.csem:
            if self.ccnt[e] > 0:
                self._need("sp", (self.csem[e], self.ccnt[e], e))

    def barrier(self):
        for e in self.eng:
            for q in self.ring:
                for sem, val in self.ring[q]:
                    if val > 0:
                        self._need(e, (sem, val, None))
            for e2 in self.csem:
                if self.ccnt[e2] > 0 and e2 != e:
                    self._need(e, (self.csem[e2], self.ccnt[e2], e2))

    def emit(self):
        pass


BRANCHES = "mrs"


def host_consts():
    c = {}
    c["ident"] = np.eye(128, dtype=np.float32)
    m3 = np.zeros((128, 4, 2), np.float32)
    m4 = np.zeros((128, 4, 8), np.float32)
    for p in range(128):
        g8 = p // 16
        gl = p // 64
        for q in range(4):
            for g in range(2):
                if g8 == 2 * q + g:
                    m3[p, q, g] = 1.0
            for gg in range(8):
                if gg == 2 * q + gl:
                    m4[p, q, gg] = 1.0
    c["mask3"], c["mask4"] = m3, m4
    selh = np.zeros((4, 4, 128), np.float32)
    for h in range(4):
        selh[h, h, :] = 1.0
    c["selh"] = selh
    c["ones4"] = np.ones((4, 128), np.float32)
    cn = np.zeros((128, 128), np.float32)
    for s_ in range(128):
        cn[s_, :s_] = 1.0e30
    c["causneg"] = cn
    rm = np.zeros((128, 384), np.float32)
    for p in range(128):
        for q in range(128):
            if p // 64 == q // 64:
                s_, t_ = p % 64, q % 64
                rm[p, q] = 1.0 if s_ < t_ else 0.0
                rm[p, 128 + q] = 1.0 if s_ <= t_ else 0.0
                rm[p, 256 + q] = 1.0 if s_ > t_ else 0.0
    c["rmasks"] = rm
    c["iota1"] = np.tile(np.arange(1, 129, dtype=np.float32)[None, :], (128, 1))
    return c


class Prog:
    def __init__(self, T, NS, DEPTH):
        self.T, self.NS, self.DEPTH = T, NS, DEPTH
        self.NT = T + NS
        assert T % 128 == 0
        self.ntile = T // 128
        self.tiles = [(i * 128, 128) for i in range(self.ntile)] + [(T, NS)]
        self.blocks = []
        t = 0
        while t < T:
            n = min(512, T - t)
            self.blocks.append((t, n))
            t += n
        self.blocks.append((T, NS))

    def build(self):
        nc = bass.Bass("TRN2", target_bir_lowering=False)
        self.nc = nc
        T, NS, L, NT = self.T, self.NS, self.DEPTH, self.NT
        dt = nc.dram_tensor

        def inp(name, shape, dtype=F32):
            return dt(name, list(shape), dtype, kind="ExternalInput").ap()

        def outp(name, shape):
            return dt(name, list(shape), F32, kind="ExternalOutput").ap()

        def scr(name, shape, dtype=F32):
            return dt(name, list(shape), dtype, kind="Internal").ap()

        I = {}
        I["x_prompt"] = inp("x_prompt", (T, D))
        I["x_sample"] = inp("x_sample", (NS, D))
        I["ident"] = inp("ident", (128, 128))
        for n, s in (("ln_in_g", (D,)), ("ln_in_b", (D,)), ("w_in", (L, D, N_IN)), ("w_out", (L, D, D)),
                     ("ln_g", (L, D)), ("ln_b", (L, D)), ("w_bm", (L, D, D)), ("w_br", (L, D, D)),
                     ("w_bs", (L, D, D)), ("s_lam_re", (L, 64, 64)), ("s_lam_im", (L, 64, 64)), ("s_log_dt", (L, 64)),
                     ("s_b_re", (L, 64, 64, 16)), ("s_b_im", (L, 64, 64, 16)), ("s_c_re", (L, 64, 16, 64)),
                     ("s_c_im", (L, 64, 16, 64)), ("s_d", (L, D)), ("s_glu_w", (L, D, D)), ("s_glu_b", (L, D)),
                     ("state_s5_re", (L, NS, 64, 64)), ("state_s5_im", (L, NS, 64, 64)),
                     ("state_mlstm_conv", (L, NS, 3, D)), ("state_mlstm_c", (L, NS, 4, 256, 256)),
                     ("state_mlstm_n", (L, NS, 4, 256)), ("state_mlstm_m", (L, NS, 4)),
                     ("m_conv_w", (L, 4, D)), ("m_conv_b", (L, D)), ("m_wq", (L, 4, 256, 256)), ("m_wk", (L, 4, 256, 256)),
                     ("m_wv", (L, 4, 256, 256)), ("m_ig_b", (L, 4)), ("m_fg_b", (L, 4)), ("m_norm_g", (L, D)), ("m_skip", (L, D)),
                     ("state_rwkv_wkv", (L, NS, 16, 64, 64)), ("state_rwkv_shift", (L, NS, R_SHIFT_W)),
                     ("r_mu", (L, R_SHIFT_W)), ("r_w0", (L, D)), ("r_w2", (L, 64, D)), ("r_a0", (L, D)), ("r_a2", (L, 64, D)),
                     ("r_k_k", (L, D)), ("r_k_a", (L, D)), ("r_r_k", (L, 16, 64)), ("r_ln_g", (L, D)), ("r_ln_b", (L, D)),
                     ("rmasks", (128, 384)),
                     ("selh", (4, 4, 128)), ("causneg", (128, 128)), ("ones4", (4, 128)),
                     ("mask3", (128, 4, 2)), ("mask4", (128, 4, 8)), ("iota1", (128, 128))):
            I[n] = inp(n, s)
        self.I = I
        O = {}
        O["y_prompt"] = outp("y_prompt", (T, D))
        O["y_sample"] = outp("y_sample", (NS, D))
        O["c_prompt"] = outp("c_prompt", (L, 4, 256, 256))
        O["c_sample"] = outp("c_sample", (L, NS, 4, 256, 256))
        O["n_prompt"] = outp("n_prompt", (L, 4, 256))
        O["n_sample"] = outp("n_sample", (L, NS, 4, 256))
        O["m_prompt"] = outp("m_prompt", (L, 4))
        O["m_sample"] = outp("m_sample", (L, NS, 4))
        O["conv_prompt"] = outp("conv_prompt", (L, 3, D))
        O["conv_sample"] = outp("conv_sample", (L, NS, 3, D))
        O["wkv_prompt"] = outp("wkv_prompt", (L, 16, 64, 64))
        O["wkv_sample"] = outp("wkv_sample", (L, NS, 16, 64, 64))
        O["shift_prompt"] = outp("shift_prompt", (L, R_SHIFT_W))
        O["shift_sample"] = outp("shift_sample", (L, NS, R_SHIFT_W))
        for nm in ("s5_re", "s5_im"):
            O[nm + "_prompt"] = outp(nm + "_prompt", (L, 4096))
            O[nm + "_sample"] = outp(nm + "_sample", (L, NS, 64, 64))
        self.O = O
        S = {}
        S["xs"] = scr("xs", (NT, D))
        S["z"] = scr("z", (NT, N_IN))
        S["zT"] = scr("zT", (N_IN, NT))
        for b in "mrs":
            S["act_" + b] = scr("act_" + b, (D, NT), BF16)
        S["vs"] = scr("vs", (NT, D))
        S["ys"] = scr("ys", (NT, D))
        self.S = S

        with ExitStack() as es:
            self.es = es
            kb = KB(nc, es)
            self.kb = kb
            self.alloc()
            self.phase0()
            for l in range(L):
                self.phase1(l)
                self.easy_states(l)
                if self.have_branch("m"):
                    self.mlstm_phase(l)
                if self.have_branch("r"):
                    self.rwkv_phase(l)
                if self.have_branch("s"):
                    self.s5_phase(l)
                self.phase3(l)
            kb.finish()
            kb.emit()
        return nc

    def sb(self, name, shape, dtype=F32):
        return self.es.enter_context(self.nc.sbuf_tensor("sb_" + name, list(shape), dtype))

    def sbp(self, name, shape, dtype=F32):
        self.uid = getattr(self, "uid", 0) + 1
        return self.pes.enter_context(self.nc.sbuf_tensor("sp%d_%s" % (self.uid, name), list(shape), dtype))

    def begin_phase(self):
        self.pes = ExitStack()

    def end_phase(self):
        self.kb.barrier()
        self.pes.close()

    def ps(self, name, shape, dtype=F32):
        return self.es.enter_context(self.nc.psum_tensor("ps_" + name, list(shape), dtype))

    def alloc(self):
        NT = self.NT
        self.ident = self.sb("ident", (128, 128))
        self.xT = self.sb("xT", (128, KC, NT), BF16)
        self.pp = [self.ps("pp%d" % i, (128, 512)) for i in range(4)]
        self.ptr = [self.ps("ptr%d" % i, (128, 512)) for i in range(2)]
        self.cnt = {}
        kb = self.kb
        kb.dma("sp", self.ident[:], self.I["ident"], writes=["ident"])

    def rr(self, key, n):
        v = self.cnt.get(key, 0)
        self.cnt[key] = v + 1
        return v % n

    def ln_alloc(self):
        self.gbc = self.sbp("gbc", (128, D))
        self.bbc = self.sbp("bbc", (128, D))
        self.xt = [self.sbp("xt%d" % i, (128, D)) for i in range(2)]
        self.xc = [self.sbp("xc%d" % i, (128, D)) for i in range(2)]
        self.st = [self.sbp("st%d" % i, (128, 8)) for i in range(2)]

    def ln_tile(self, src, srckey, row0, nr, ti, xs_out, final_out=None):
        kb = self.kb
        i = self.rr("ln", 2)
        xc, st = self.xc[i], self.st[i]
        kxc, kst = "xc%d" % i, "st%d" % i
        kb.op("dve", lambda e: e.tensor_reduce(out=st[:nr, 0:1], in_=src[:nr, :], axis=AX.X, op=ALU.add),
              reads=[srckey], writes=[kst])
        kb.op("dve", lambda e: e.tensor_scalar(out=st[:nr, 1:2], in0=st[:nr, 0:1], scalar1=-1.0 / D, scalar2=None,
                                               op0=ALU.mult), reads=[kst], writes=[kst])
        kb.op("dve", lambda e: e.tensor_scalar(out=xc[:nr, :], in0=src[:nr, :], scalar1=st[:nr, 1:2], scalar2=None,
                                               op0=ALU.add), reads=[srckey, kst], writes=[kxc])
        j = self.rr("tmpsq", 2)
        sq = self.xt[j]
        ksq = "xt%d" % j
        kb.op("act", lambda e: e.activation(out=sq[:nr, :], in_=xc[:nr, :], func=AF.Square, accum_out=st[:nr, 2:3]),
              reads=[kxc], writes=[ksq, kst])
        kb.op("act", lambda e: e.activation(out=st[:nr, 3:4], in_=st[:nr, 2:3], func=AF.Sqrt, scale=1.0 / D,
                                            bias=self.epsc[:nr, 0:1]), reads=[kst, "epsc"], writes=[kst])
        kb.op("dve", lambda e: e.reciprocal(out=st[:nr, 4:5], in_=st[:nr, 3:4]), reads=[kst], writes=[kst])
        kb.op("dve", lambda e: e.scalar_tensor_tensor(out=xc[:nr, :], in0=xc[:nr, :], scalar=st[:nr, 4:5],
                                                      in1=self.gbc[:nr, :], op0=ALU.mult, op1=ALU.mult),
              reads=[kxc, kst, "gbc"], writes=[kxc])
        kb.op("dve", lambda e: e.tensor_tensor(out=xc[:nr, :], in0=xc[:nr, :], in1=self.bbc[:nr, :], op=ALU.add),
              reads=[kxc, "bbc"], writes=[kxc])
        kb.dma("pool", xs_out, xc[:nr, :], reads=[kxc], writes=[("xs", ti)])
        if final_out is not None:
            kb.dma("pool", final_out, xc[:nr, :], reads=[kxc], writes=[], is_output=True)
        for half in range(2):
            p = self.rr("ptr", 2)
            pt, kpt = self.ptr[p], "ptr%d" % p
            for k4 in range(4):
                kc = half * 4 + k4
                kb.op("pe", lambda e, kc=kc, k4=k4, pt=pt: e.transpose(out=pt[:, k4 * 128:k4 * 128 + nr],
                                                                      in_=xc[:nr, kc * 128:(kc + 1) * 128],
                                                                      identity=self.ident[:nr, :nr]),
                      reads=[kxc, "ident"], writes=[kpt])
            dst = self.xT[:, half * 4:half * 4 + 4, row0:row0 + nr]
            srcp = pt[:, :].rearrange("p (k t) -> p k t", k=4)[:, :, :nr]
            eng = "act" if half == 0 else "dve"
            if eng == "act":
                kb.op("act", lambda e, dst=dst, srcp=srcp: e.activation(out=dst, in_=srcp, func=AF.Copy),
                      reads=[kpt], writes=[("xT", ti)])
            else:
                kb.op("dve", lambda e, dst=dst, srcp=srcp: e.tensor_copy(out=dst, in_=srcp),
                      reads=[kpt], writes=[("xT", ti)])

    def load_ln_params(self, g_ap, b_ap):
        kb = self.kb
        kb.dma("sp", self.gbc[:], g_ap.partition_broadcast(128), writes=["gbc"])
        kb.dma("sp", self.bbc[:], b_ap.partition_broadcast(128), writes=["bbc"])

    def phase0(self):
        kb = self.kb
        self.epsc = self.sb("epsc", (128, 1))
        kb.op("dve", lambda e: e.memset(self.epsc[:], LN_EPS), writes=["epsc"])
        self.onesf = self.sb("onesf", (128, 64))
        kb.op("dve", lambda e: e.memset(self.onesf[:], 1.0), writes=["onesf"])
        self.begin_phase()
        self.ln_alloc()
        self.load_ln_params(self.I["ln_in_g"], self.I["ln_in_b"])
        for ti, (r0, nr) in enumerate(self.tiles):
            j = self.rr("xt", 2)
            xt, kxt = self.xt[j], "xt%d" % j
            src = self.I["x_prompt"][r0:r0 + nr, :] if r0 < self.T else self.I["x_sample"][:, :]
            kb.dma("sp", xt[:nr, :], src, writes=[kxt])
            self.ln_tile(xt, kxt, r0, nr, ti, self.S["xs"][r0:r0 + nr, :])
        self.end_phase()

    def load_w(self, dram_cols, width):
        kb = self.kb
        i = self.rr("w", 2)
        ws, wb = self.wst[i], self.wbf[i]
        kws, kwb = "wst%d" % i, "wbf%d" % i
        kb.dma("sp", ws[:, :, :width], dram_cols.rearrange("(k p) c -> p k c", p=128), writes=[kws])
        kb.op("pool", lambda e: e.tensor_copy(out=wb[:, :, :width], in_=ws[:, :, :width]), reads=[kws], writes=[kwb])
        return wb, kwb

    def evac(self, dst, src, reads, writes):
        kb = self.kb
        if self.rr("evac", 2) == 0:
            kb.op("act", lambda e: e.activation(out=dst, in_=src, func=AF.Copy), reads=reads, writes=writes)
        else:
            kb.op("dve", lambda e: e.tensor_copy(out=dst, in_=src), reads=reads, writes=writes)

    def phase1(self, l):
        kb = self.kb
        self.begin_phase()
        self.wst = [self.sbp("wst%d" % i, (128, KC, 512)) for i in range(2)]
        self.wbf = [self.sbp("wbf%d" % i, (128, KC, 512), BF16) for i in range(2)]
        self.ev = [self.sbp("ev%d" % i, (128, 512)) for i in range(4)]
        W = self.I["w_in"][l]
        tm_segs = [(O_OG, O_ZM), (O_RC, O_U)]
        fm_segs = [(O_XM, O_IG), (O_IG, O_FG), (O_FG, O_OG), (O_ZM, O_RC), (O_U, O_ZS), (O_ZS, O_GM), (O_GM, N_IN)]
        for (c0, c1) in tm_segs:
            c = c0
            while c < c1:
                w = min(512, c1 - c)
                wb, kwb = self.load_w(W[:, c:c + w], w)
                for ti, (r0, nr) in enumerate(self.tiles):
                    p = self.rr("pp", 4)
                    pp, kpp = self.pp[p], "pp%d" % p
                    for kc in range(KC):
                        kb.op("pe", lambda e, kc=kc, pp=pp, r0=r0, nr=nr, wb=wb, w=w: e.matmul(
                            pp[:nr, :w], lhsT=self.xT[:, kc, r0:r0 + nr], rhs=wb[:, kc, :w],
                            start=(kc == 0), stop=(kc == KC - 1)),
                            reads=[("xT", ti), kwb], writes=[kpp])
                    v = self.rr("ev", 4)
                    ev, kev = self.ev[v], "ev%d" % v
                    self.evac(ev[:nr, :w], pp[:nr, :w], [kpp], [kev])
                    kb.dma("pool", self.S["z"][r0:r0 + nr, c:c + w], ev[:nr, :w], reads=[kev],
                           writes=[("z", ti)])
                c += w
        for (c0, c1) in fm_segs:
            c = c0
            while c < c1:
                w = min(512, c1 - c)
                wb, kwb = self.load_w(W[:, c:c + w], w)
                for s0 in range(0, w, 128):
                    m = min(128, w - s0)
                    for bi, (t0, n) in enumerate(self.blocks):
                        p = self.rr("pp", 4)
                        pp, kpp = self.pp[p], "pp%d" % p
                        tis = list(range(t0 // 128, (t0 + n + 127) // 128))
                        for kc in range(KC):
                            kb.op("pe", lambda e, kc=kc, pp=pp, t0=t0, n=n, wb=wb, s0=s0, m=m: e.matmul(
                                pp[:m, :n], lhsT=wb[:, kc, s0:s0 + m], rhs=self.xT[:, kc, t0:t0 + n],
                                start=(kc == 0), stop=(kc == KC - 1)),
                                reads=[("xT", t) for t in tis] + [kwb], writes=[kpp])
                        v = self.rr("ev", 4)
                        ev, kev = self.ev[v], "ev%d" % v
                        self.evac(ev[:m, :n], pp[:m, :n], [kpp], [kev])
                        kb.dma("pool", self.S["zT"][c + s0:c + s0 + m, t0:t0 + n], ev[:m, :n], reads=[kev],
                               writes=[("zT", c + s0, bi)])
                c += w
        self.end_phase()

    def easy_states(self, l):
        kb = self.kb
        I, S, O = self.I, self.S, self.O
        T, NS = self.T, self.NS
        nb = len(self.blocks)
        zt_all = [("zT", r, b) for r in range(0, 1024, 128) for b in range(nb)]
        z_all = [("z", ti) for ti in range(len(self.tiles))]
        kb.dma("pool", O["conv_prompt"][l].rearrange("j c -> c j"), S["zT"][0:D, T - 3:T], reads=zt_all, writes=[],
               is_output=True, allow_slow_non_contiguous=True)
        kb.dma("pool", O["shift_prompt"][l:l + 1, :], S["z"][T - 1:T, O_RC:O_RC + R_SHIFT_W], reads=z_all, writes=[], is_output=True)
        if NS:
            kb.dma("pool", O["conv_sample"][l][:, 0:2, :], I["state_mlstm_conv"][l][:, 1:3, :], writes=[], is_output=True)
            for hh in range(4):
                kb.dma("pool", O["conv_sample"][l][:, 2, hh * 256:(hh + 1) * 256].rearrange("b c -> c b"),
                       S["zT"][hh * 256:(hh + 1) * 256, T:T + NS], reads=zt_all, writes=[],
                       is_output=True, allow_slow_non_contiguous=True)
            kb.dma("pool", O["shift_sample"][l], S["z"][T:T + NS, O_RC:O_RC + R_SHIFT_W], reads=z_all, writes=[], is_output=True)

    def mlstm_phase(self, l):
        kb = self.kb
        I, S, O = self.I, self.S, self.O
        T, NS, NT = self.T, self.NS, self.NT
        self.begin_phase()
        sbp = self.sbp
        Wst = sbp("Wst", (128, 4, 2, 256))
        Wq = sbp("Wq", (128, 4, 2, 256), BF16); Wk = sbp("Wk", (128, 4, 2, 256), BF16); Wv = sbp("Wv", (128, 4, 2, 256), BF16)
        cw = sbp("cw", (128, 4, 8)); cb = sbp("cb", (128, 8)); mg = sbp("mg", (128, 8)); msk = sbp("msk", (128, 8))
        gb = sbp("gb", (4, 2))
        igA = sbp("igA", (4, NT)); lfA = sbp("lfA", (4, NT))
        selh = sbp("selh", (4, 4, 128)); causneg = sbp("causneg", (128, 128)); ones4 = sbp("ones4", (4, 128))
        Cst = [[sbp("C%d_%d" % (s_, h), (128, 2, 257)) for h in range(4)] for s_ in range(2)]
        mprev = sbp("mprev", (4, 2))
        xext = [sbp("xext%d" % i, (128, 8, 131)) for i in range(2)]
        ctmp = sbp("cvtmp", (128, 8, 128)); cacc = sbp("cvacc", (128, 8, 128))
        xcT = sbp("xcT", (128, 8, 128)); xcb = sbp("xcb", (128, 8, 128), BF16); xmb = sbp("xmb", (128, 8, 128), BF16)
        qTb = sbp("qTb", (128, 4, 2, 128)); kTb = sbp("kTb", (128, 4, 2, 128))
        kw = sbp("kw", (128, 256)); vaug = sbp("vaug", (128, 257))
        G = sbp("G", (4, 8, 128)); gsm = sbp("gsm", (4, 8)); dg = sbp("dg", (4, 4))
        gc = sbp("gc", (128, 16)); dbc = sbp("dbc", (128, 4))
        DT = sbp("DT", (128, 128)); Stl = sbp("Stl", (128, 128)); mmb = sbp("mmb", (128, 4, 128))
        Asb = sbp("Asb", (128, 257)); nd = sbp("nd", (128, 257)); dsm = sbp("dsm", (128, 4))
        sog = [sbp("sog%d" % i, (128, D)) for i in range(2)]
        hm = sbp("hm", (128, D)); hst = sbp("hst", (128, 16))
        hT = sbp("hT", (128, 8, 128)); zmt = [sbp("zmt%d" % i, (128, 8, 128)) for i in range(2)]
        aob = [sbp("aob%d" % i, (128, 8, 128), BF16) for i in range(2)]
        cvst = sbp("cvst", (48, D)); convT = sbp("convT", (128, 8, 48))

        for (nm, dst, sc_) in (("m_wq", Wq, 1.0 / 16.0), ("m_wk", Wk, 1.0), ("m_wv", Wv, 1.0)):
            kb.dma("sp", Wst[:], I[nm][l].rearrange("h (c p) e -> p h c e", p=128), writes=["Wst"])
            kb.op("act", lambda e, dst=dst, sc_=sc_: e.activation(out=dst[:], in_=Wst[:], func=AF.Copy, scale=sc_),
                  reads=["Wst"], writes=[nm])
        sl = dict(allow_slow_non_contiguous=True)
        kb.dma("sp", cw[:], I["m_conv_w"][l].rearrange("j (k p) -> p j k", p=128), writes=["cw"], **sl)
        kb.dma("sp", cb[:], I["m_conv_b"][l].rearrange("(k p) -> p k", p=128), writes=["cb"], **sl)
        kb.dma("sp", mg[:], I["m_norm_g"][l].rearrange("(k p) -> p k", p=128), writes=["mg"], **sl)
        kb.dma("sp", msk[:], I["m_skip"][l].rearrange("(k p) -> p k", p=128), writes=["msk"], **sl)
        kb.dma("sp", gb[:, 0:1], I["m_ig_b"][l].rearrange("(h o) -> h o", o=1), writes=["gb"], **sl)
        kb.dma("sp", gb[:, 1:2], I["m_fg_b"][l].rearrange("(h o) -> h o", o=1), writes=["gb"], **sl)
        kb.dma("sp", selh[:], I["selh"], writes=["selh"])
        kb.dma("sp", causneg[:], I["causneg"], writes=["causneg"])
        kb.dma("sp", ones4[:], I["ones4"], writes=["ones4"])
        nb = len(self.blocks)
        kb.dma("sp", igA[:], S["zT"][O_IG:O_IG + 4, :], reads=[("zT", O_IG, b) for b in range(nb)], writes=["igA"])
        kb.dma("sp", lfA[:], S["zT"][O_FG:O_FG + 4, :], reads=[("zT", O_FG, b) for b in range(nb)], writes=["lfA"])
        kb.op("dve", lambda e: e.tensor_scalar(out=igA[:], in0=igA[:], scalar1=gb[:, 0:1], scalar2=None, op0=ALU.add),
              reads=["igA", "gb"], writes=["igA"])
        kb.op("dve", lambda e: e.tensor_scalar(out=lfA[:], in0=lfA[:], scalar1=gb[:, 1:2], scalar2=-1.0, op0=ALU.add, op1=ALU.mult),
              reads=["lfA", "gb"], writes=["lfA"])
        kb.op("act", lambda e: e.activation(out=lfA[:], in_=lfA[:], func=AF.Exp), reads=["lfA"], writes=["lfA"])
        kb.op("dve", lambda e: e.tensor_scalar(out=lfA[:], in0=lfA[:], scalar1=1.0, scalar2=None, op0=ALU.add), reads=["lfA"], writes=["lfA"])
        kb.op("act", lambda e: e.activation(out=lfA[:], in_=lfA[:], func=AF.Ln), reads=["lfA"], writes=["lfA"])
        kb.op("dve", lambda e: e.tensor_scalar(out=lfA[:], in0=lfA[:], scalar1=-1.0, scalar2=None, op0=ALU.mult), reads=["lfA"], writes=["lfA"])
        if NS:
            kb.dma("sp", cvst[:3 * NS, :], I["state_mlstm_conv"][l].rearrange("b j c -> (b j) c"), writes=["cvst"])
            for half in range(2):
                p = self.rr("ptr", 2)
                pt, kpt = self.ptr[p], "ptr%d" % p
                for k4 in range(4):
                    kc = half * 4 + k4
                    kb.op("pe", lambda e, pt=pt, k4=k4, kc=kc: e.transpose(out=pt[:, k4 * 48:k4 * 48 + 3 * NS], in_=cvst[:3 * NS, kc * 128:(kc + 1) * 128],
                                                                       identity=self.ident[:3 * NS, :3 * NS]),
                          reads=["cvst", "ident"], writes=[kpt])
                kb.op("dve", lambda e, pt=pt, half=half: e.tensor_copy(out=convT[:, half * 4:half * 4 + 4, :3 * NS],
                                                                     in_=pt[:, 0:192].rearrange("p (k c) -> p k c", k=4)[:, :, :3 * NS]),
                      reads=[kpt], writes=["convT"])

        zt_x = lambda bi: [("zT", r, bi) for r in range(0, 1024, 128)]
        zt_z = lambda bi: [("zT", O_ZM + r, bi) for r in range(0, 1024, 128)]
        blk_of = lambda t: next(i for i, (b0, bn) in enumerate(self.blocks) if b0 <= t < b0 + bn)

        chunks = [(c * 128, 128, None) for c in range(T // 128)] + [(T + b, 1, b) for b in range(NS)]
        for ci, (t0, L, sb_) in enumerate(chunks):
            bi = blk_of(t0)
            ti = t0 // 128
            cs = self.rr("Cset", 2) if sb_ is not None else 0
            C = Cst[cs]
            kC = [("C", cs, h) for h in range(4)]
            if ci == 0:
                for h in range(4):
                    kb.op("pool", lambda e, h=h: e.memset(C[h][:], 0.0), writes=[kC[h]])
                kb.op("dve", lambda e: e.memset(mprev[:, 0:1], NEG), writes=["mprev"])
            if sb_ is not None:
                for h in range(4):
                    kb.dma("sp", C[h][:, :, 0:256], I["state_mlstm_c"][l, sb_, h].rearrange("(c p) v -> p c v", p=128), writes=[kC[h]])
                    kb.dma("sp", C[h][:, :, 256:257], I["state_mlstm_n"][l, sb_, h].rearrange("(c p o) -> p c o", p=128, o=1), writes=[kC[h]], **sl)
                kb.dma("sp", mprev[:, 0:1], I["state_mlstm_m"][l, sb_].rearrange("(h o) -> h o", o=1), writes=["mprev"], **sl)
            xj = self.rr("xext", 2)
            xe, kxe = xext[xj], "xext%d" % xj
            if sb_ is None:
                if t0 == 0:
                    kb.op("pool", lambda e, xe=xe: e.memset(xe[:, :, 0:3], 0.0), writes=[kxe])
                    kb.dma("sp", xe[:, :, 3:3 + L], S["zT"][0:D, t0:t0 + L].rearrange("(k p) t -> p k t", p=128), reads=zt_x(bi), writes=[kxe])
                else:
                    rd = zt_x(bi) + (zt_x(blk_of(t0 - 3)) if blk_of(t0 - 3) != bi else [])
                    kb.dma("sp", xe[:, :, 0:3 + L], S["zT"][0:D, t0 - 3:t0 + L].rearrange("(k p) t -> p k t", p=128), reads=rd, writes=[kxe])
            else:
                kb.op("pool", lambda e, xe=xe, sb_=sb_: e.tensor_copy(out=xe[:, :, 0:3], in_=convT[:, :, 3 * sb_:3 * sb_ + 3]), reads=["convT"], writes=[kxe])
                kb.dma("sp", xe[:, :, 3:4], S["zT"][0:D, t0:t0 + 1].rearrange("(k p) t -> p k t", p=128), reads=zt_x(bi), writes=[kxe], **sl)
            V = lambda x: x[:, :, :L]
            for j in range(4):
                dst = cacc if j == 0 else ctmp
                kd = "cvacc" if j == 0 else "cvtmp"
                kb.op("dve", lambda e, j=j, dst=dst, xe=xe: e.tensor_tensor(out=V(dst), in0=xe[:, :, j:j + L],
                                                                         in1=cw[:, j, :, None].to_broadcast([128, 8, L]), op=ALU.mult),
                      reads=[kxe, "cw"], writes=[kd])
                if j > 0:
                    kb.op("pool", lambda e: e.tensor_tensor(out=V(cacc), in0=V(cacc), in1=V(ctmp), op=ALU.add), reads=["cvacc", "cvtmp"], writes=["cvacc"])
            kb.op("dve", lambda e: e.tensor_tensor(out=V(cacc), in0=V(cacc), in1=cb[:, :, None].to_broadcast([128, 8, L]), op=ALU.add),
                  reads=["cvacc", "cb"], writes=["cvacc"])
            kb.op("act", lambda e: e.activation(out=V(xcT), in_=V(cacc), func=AF.Silu), reads=["cvacc"], writes=["xcT"])
            kb.op("pool", lambda e: e.tensor_copy(out=V(xcb), in_=V(xcT)), reads=["xcT"], writes=["xcb"])
            kb.op("pool", lambda e, xe=xe: e.tensor_copy(out=V(xmb), in_=xe[:, :, 3:3 + L]), reads=[kxe], writes=["xmb"])
            Gr = lambda i: G[:, i, :L]
            kb.op("dve", lambda e: e.tensor_tensor_scan(out=Gr(0), data0=ones4[:, :L], data1=lfA[:, t0:t0 + L], initial=0.0, op0=ALU.mult, op1=ALU.add),
                  reads=["ones4", "lfA"], writes=["G0"])
            kb.op("dve", lambda e: e.tensor_tensor(out=Gr(3), in0=igA[:, t0:t0 + L], in1=Gr(0), op=ALU.subtract), reads=["igA", "G0"], writes=["G3"])
            kb.op("dve", lambda e: e.tensor_tensor_scan(out=Gr(1), data0=Gr(3), data1=Gr(3), initial=-3.0e38, op0=ALU.max, op1=ALU.max),
                  reads=["G3"], writes=["G1"])
            kb.op("dve", lambda e: e.tensor_scalar(out=Gr(1), in0=Gr(1), scalar1=mprev[:, 0:1], scalar2=None, op0=ALU.max), reads=["G1", "mprev"], writes=["G1"])
            kb.op("dve", lambda e: e.tensor_scalar(out=gsm[:, 0:1], in0=G[:, 1, L - 1:L], scalar1=-1.0, scalar2=None, op0=ALU.mult), reads=["G1"], writes=["gsm"])
            kb.op("act", lambda e: e.activation(out=Gr(4), in_=Gr(1), func=AF.Exp, scale=-1.0, bias=mprev[:, 0:1]), reads=["G1", "mprev"], writes=["G4"])
            kb.op("dve", lambda e: e.tensor_tensor(out=Gr(5), in0=Gr(0), in1=Gr(1), op=ALU.add), reads=["G0", "G1"], writes=["G5"])
            kb.op("act", lambda e: e.activation(out=Gr(5), in_=Gr(5), func=AF.Exp, scale=-1.0), reads=["G5"], writes=["G5"])
            kb.op("act", lambda e: e.activation(out=Gr(6), in_=Gr(3), func=AF.Exp, bias=gsm[:, 0:1]), reads=["G3", "gsm"], writes=["G6"])
            kb.op("dve", lambda e: e.tensor_tensor(out=mprev[:, 1:2], in0=G[:, 0, L - 1:L], in1=G[:, 1, L - 1:L], op=ALU.add), reads=["G0", "G1"], writes=["mnew"])
            p = self.rr("ptr", 2)
            pt, kpt = self.ptr[p], "ptr%d" % p
            for ki, gi in enumerate((4, 5, 3, 6)):
                kb.op("pe", lambda e, ki=ki, gi=gi, pt=pt: e.transpose(out=pt[:L, ki * 4:ki * 4 + 4], in_=G[:, gi, :L], identity=self.ident[:4, :4]),
                      reads=["G%d" % gi, "ident"], writes=[kpt])
            kb.op("dve", lambda e, pt=pt: e.tensor_copy(out=gc[:L, :], in_=pt[:L, 0:16]), reads=[kpt], writes=["gc"])
            kb.op("dve", lambda e: e.tensor_scalar(out=dg[:, :], in0=self.ident[:4, :4], scalar1=G[:, 4, L - 1:L], scalar2=None, op0=ALU.mult),
                  reads=["G4", "ident"], writes=["dg"])
            p2 = self.rr("ptr", 2)
            pt2, kpt2 = self.ptr[p2], "ptr%d" % p2
            kb.op("pe", lambda e, pt2=pt2: e.matmul(pt2[:, 0:4], lhsT=ones4[:, :], rhs=dg[:, :], start=True, stop=True), reads=["ones4", "dg"], writes=[kpt2])
            kb.op("dve", lambda e, pt2=pt2: e.tensor_copy(out=dbc[:, :], in_=pt2[:, 0:4]), reads=[kpt2], writes=["dbc"])
            pnn = self.rr("pp", 4)
            pn, kpn = self.pp[pnn], "pp%d" % pnn
            for h in range(4):
                kb.op("pe", lambda e, h=h, pn=pn: e.matmul(pn[:L, h * 128:h * 128 + L], lhsT=selh[:, h, :L], rhs=G[:, 1, :L], start=True, stop=True),
                      reads=["selh", "G1"], writes=[kpn])
            kb.op("act", lambda e, pn=pn: e.activation(out=mmb[:L, :, :L], in_=pn[:L, :].rearrange("p (h t) -> p h t", h=4)[:, :, :L], func=AF.Copy),
                  reads=[kpn], writes=["mmb"])
            sj = self.rr("sog", 2)
            so, kso = sog[sj], "sog%d" % sj
            kb.dma("sp", so[:L, :], S["z"][t0:t0 + L, O_OG:O_OG + D], reads=[("z", ti)], writes=[kso])
            kb.op("act", lambda e, so=so: e.activation(out=so[:L, :], in_=so[:L, :], func=AF.Sigmoid), reads=[kso], writes=[kso])
            for h in range(4):
                for (Wt, wn, dstT, kd) in ((Wq, "m_wq", qTb, "qTb"), (Wk, "m_wk", kTb, "kTb")):
                    pq = self.rr("pp", 4)
                    ppq, kpq = self.pp[pq], "pp%d" % pq
                    for ec in range(2):
                        for dc in range(2):
                            kb.op("pe", lambda e, h=h, ec=ec, dc=dc, Wt=Wt, ppq=ppq: e.matmul(
                                ppq[:, ec * 128:ec * 128 + L], lhsT=Wt[:, h, dc, ec * 128:(ec + 1) * 128], rhs=xcb[:, 2 * h + dc, :L],
                                start=(dc == 0), stop=(dc == 1)), reads=[wn, "xcb"], writes=[kpq])
                    self.evac(dstT[:, h, :, :L], ppq[:, 0:256].rearrange("p (c t) -> p c t", c=2)[:, :, :L], [kpq], [(kd, h)])
            for h in range(4):
                pk = self.rr("pp", 4)
                ppk, kpk = self.pp[pk], "pp%d" % pk
                for dc in range(2):
                    kb.op("pe", lambda e, h=h, dc=dc, ppk=ppk: e.matmul(ppk[:L, 0:256], lhsT=xcb[:, 2 * h + dc, :L], rhs=Wk[:, h, dc, :],
                                                                      start=(dc == 0), stop=(dc == 1)), reads=["m_wk", "xcb"], writes=[kpk])
                kb.op("act", lambda e, h=h, ppk=ppk: e.activation(out=kw[:L, :], in_=ppk[:L, 0:256], func=AF.Copy, scale=gc[:L, 12 + h:13 + h]),
                      reads=[kpk, "gc"], writes=["kw"])
                pv = self.rr("pp", 4)
                ppv, kpv = self.pp[pv], "pp%d" % pv
                for dc in range(2):
                    kb.op("pe", lambda e, h=h, dc=dc, ppv=ppv: e.matmul(ppv[:L, 0:256], lhsT=xmb[:, 2 * h + dc, :L], rhs=Wv[:, h, dc, :],
                                                                      start=(dc == 0), stop=(dc == 1)), reads=["m_wv", "xmb"], writes=[kpv])
                kb.op("dve", lambda e, ppv=ppv: e.tensor_copy(out=vaug[:L, 0:256], in_=ppv[:L, 0:256]), reads=[kpv], writes=["vaug"])
                kb.op("dve", lambda e: e.memset(vaug[:L, 256:257], 1.0), writes=["vaug"])
                kb.op("dve", lambda e, h=h: e.tensor_tensor(out=DT[:L, :L], in0=mmb[:L, h, :L], in1=causneg[:L, :L], op=ALU.add),
                      reads=["mmb", "causneg"], writes=["DT"])
                kb.op("act", lambda e, h=h: e.activation(out=DT[:L, :L], in_=DT[:L, :L], func=AF.Exp, scale=-1.0, bias=gc[:L, 8 + h:9 + h]),
                      reads=["DT", "gc"], writes=["DT"])
                ps_ = self.rr("pp", 4)
                pps, kps = self.pp[ps_], "pp%d" % ps_
                for ec in range(2):
                    kb.op("pe", lambda e, h=h, ec=ec, pps=pps: e.matmul(pps[:L, :L], lhsT=kTb[:, h, ec, :L], rhs=qTb[:, h, ec, :L],
                                                                      start=(ec == 0), stop=(ec == 1)), reads=[("kTb", h), ("qTb", h)], writes=[kps])
                kb.op("dve", lambda e, pps=pps: e.tensor_tensor(out=Stl[:L, :L], in0=pps[:L, :L], in1=DT[:L, :L], op=ALU.mult), reads=[kps, "DT"], writes=["Stl"])
                pa = self.rr("pp", 4)
                ppa, kpa = self.pp[pa], "pp%d" % pa
                kb.op("pe", lambda e, ppa=ppa: e.matmul(ppa[:L, 0:257], lhsT=Stl[:L, :L], rhs=vaug[:L, :], start=True, stop=True), reads=["Stl", "vaug"], writes=[kpa])
                pb = self.rr("ptr", 2)
                ppb, kpb = self.ptr[pb], "ptr%d" % pb
                for ec in range(2):
                    kb.op("pe", lambda e, h=h, ec=ec, ppb=ppb, C=C: e.matmul(ppb[:L, 0:257], lhsT=qTb[:, h, ec, :L], rhs=C[h][:, ec, :],
                                                                      start=(ec == 0), stop=(ec == 1)), reads=[("qTb", h), kC[h]], writes=[kpb])
                kb.op("act", lambda e, ppa=ppa: e.activation(out=Asb[:L, :], in_=ppa[:L, 0:257], func=AF.Copy), reads=[kpa], writes=["Asb"])
                kb.op("dve", lambda e, h=h, ppb=ppb: e.scalar_tensor_tensor(out=nd[:L, :], in0=ppb[:L, 0:257], scalar=gc[:L, h:h + 1], in1=Asb[:L, :],
                                                                        op0=ALU.mult, op1=ALU.add), reads=[kpb, "gc", "Asb"], writes=["nd"])
                kb.op("act", lambda e: e.activation(out=dsm[:L, 2:3], in_=nd[:L, 256:257], func=AF.Abs), reads=["nd"], writes=["dsm"])
                kb.op("dve", lambda e, h=h: e.tensor_scalar(out=dsm[:L, 0:1], in0=dsm[:L, 2:3], scalar1=gc[:L, 4 + h:5 + h], scalar2=None,
                                                            op0=ALU.max), reads=["dsm", "gc"], writes=["dsm"])
                kb.op("dve", lambda e: e.reciprocal(out=dsm[:L, 1:2], in_=dsm[:L, 0:1]), reads=["dsm"], writes=["dsm"])
                kb.op("dve", lambda e, h=h, so=so: e.scalar_tensor_tensor(out=hm[:L, h * 256:(h + 1) * 256], in0=nd[:L, 0:256], scalar=dsm[:L, 1:2],
                                                                      in1=so[:L, h * 256:(h + 1) * 256], op0=ALU.mult, op1=ALU.mult),
                      reads=["nd", "dsm", kso], writes=[("hm", h)])
                for ec in range(2):
                    pu = self.rr("pp", 4)
                    ppu, kpu = self.pp[pu], "pp%d" % pu
                    kb.op("pe", lambda e, ec=ec, ppu=ppu: e.matmul(ppu[:, 0:257], lhsT=kw[:L, ec * 128:(ec + 1) * 128], rhs=vaug[:L, :], start=True, stop=True),
                          reads=["kw", "vaug"], writes=[kpu])
                    kb.op("dve", lambda e, h=h, ec=ec, ppu=ppu, C=C: e.scalar_tensor_tensor(out=C[h][:, ec, :], in0=C[h][:, ec, :], scalar=dbc[:, h:h + 1],
                                                                                      in1=ppu[:, 0:257], op0=ALU.mult, op1=ALU.add),
                          reads=[kC[h], "dbc", kpu], writes=[kC[h]])
            kb.op("dve", lambda e: e.tensor_copy(out=mprev[:, 0:1], in_=mprev[:, 1:2]), reads=["mnew"], writes=["mprev"])
            hmk = [("hm", h) for h in range(4)]
            hv = hm[:L, :].rearrange("t (h d) -> t h d", h=4)
            kb.op("dve", lambda e: e.tensor_reduce(out=hst[:L, 0:4], in_=hv, axis=AX.X, op=ALU.add), reads=hmk, writes=["hst"])
            kb.op("dve", lambda e: e.tensor_scalar(out=hst[:L, 0:4], in0=hst[:L, 0:4], scalar1=-1.0 / 256.0, scalar2=None, op0=ALU.mult), reads=["hst"], writes=["hst"])
            kb.op("dve", lambda e: e.tensor_tensor(out=hv, in0=hv, in1=hst[:L, 0:4].unsqueeze(2).to_broadcast([L, 4, 256]), op=ALU.add),
                  reads=hmk + ["hst"], writes=hmk)
            kb.op("pool", lambda e, so=so: e.tensor_tensor(out=so[:L, :], in0=hm[:L, :], in1=hm[:L, :], op=ALU.mult), reads=hmk, writes=[kso])
            kb.op("dve", lambda e, so=so: e.tensor_reduce(out=hst[:L, 4:8], in_=so[:L, :].rearrange("t (h d) -> t h d", h=4), axis=AX.X, op=ALU.add),
                  reads=[kso], writes=["hst"])
            kb.op("act", lambda e: e.activation(out=hst[:L, 8:12], in_=hst[:L, 4:8], func=AF.Sqrt, scale=1.0 / 256.0, bias=self.epsc[:L, 0:1]),
                  reads=["hst", "epsc"], writes=["hst"])
            kb.op("dve", lambda e: e.reciprocal(out=hst[:L, 12:16], in_=hst[:L, 8:12]), reads=["hst"], writes=["hst"])
            kb.op("dve", lambda e: e.tensor_tensor(out=hv, in0=hv, in1=hst[:L, 12:16].unsqueeze(2).to_broadcast([L, 4, 256]), op=ALU.mult),
                  reads=hmk + ["hst"], writes=hmk)
            zj = self.rr("zmt", 2)
            zm_, kzm = zmt[zj], "zmt%d" % zj
            kb.dma("sp", zm_[:, :, :L], S["zT"][O_ZM:O_ZM + D, t0:t0 + L].rearrange("(k p) t -> p k t", p=128), reads=zt_z(bi), writes=[kzm],
                   **(sl if L == 1 else {}))
            kb.op("act", lambda e, zm_=zm_: e.activation(out=zm_[:, :, :L], in_=zm_[:, :, :L], func=AF.Silu), reads=[kzm], writes=[kzm])
            for half in range(2):
                p = self.rr("ptr", 2)
                pt, kpt = self.ptr[p], "ptr%d" % p
                for k4 in range(4):
                    kc = half * 4 + k4
                    kb.op("pe", lambda e, k4=k4, kc=kc, pt=pt: e.transpose(out=pt[:, k4 * 128:k4 * 128 + L], in_=hm[:L, kc * 128:(kc + 1) * 128],
                                                                       identity=self.ident[:L, :L]), reads=hmk + ["ident"], writes=[kpt])
                kb.op("dve", lambda e, half=half, pt=pt: e.tensor_tensor(out=hT[:, half * 4:half * 4 + 4, :L],
                                                                       in0=pt[:, :].rearrange("p (k t) -> p k t", k=4)[:, :, :L],
                                                                       in1=mg[:, half * 4:half * 4 + 4, None].to_broadcast([128, 4, L]), op=ALU.mult),
                      reads=[kpt, "mg"], writes=[("hT", half)])
            kb.op("pool", lambda e: e.tensor_tensor(out=V(ctmp), in0=V(xcT), in1=msk[:, :, None].to_broadcast([128, 8, L]), op=ALU.mult),
                  reads=["xcT", "msk"], writes=["cvtmp"])
            kb.op("pool", lambda e: e.tensor_tensor(out=V(hT), in0=V(hT), in1=V(ctmp), op=ALU.add), reads=["cvtmp", ("hT", 0), ("hT", 1)], writes=[("hT", 0), ("hT", 1)])
            aj = self.rr("aob", 2)
            kb.op("dve", lambda e, aj=aj, zm_=zm_: e.tensor_tensor(out=aob[aj][:, :, :L], in0=V(hT), in1=zm_[:, :, :L], op=ALU.mult),
                  reads=[("hT", 0), ("hT", 1), kzm], writes=["aob%d" % aj])
            kb.dma("pool", S["act_m"][:, t0:t0 + L].rearrange("(k p) t -> p k t", p=128), aob[aj][:, :, :L], reads=["aob%d" % aj],
                   writes=[("act", "m", bi)], **(sl if L == 1 else {}))
            if sb_ is not None or ci == T // 128 - 1:
                if sb_ is None:
                    oc_, on_, om_ = O["c_prompt"][l], O["n_prompt"][l], O["m_prompt"][l]
                else:
                    oc_, on_, om_ = O["c_sample"][l, sb_], O["n_sample"][l, sb_], O["m_sample"][l, sb_]
                for h in range(4):
                    kb.dma("pool", oc_[h].rearrange("(c p) v -> p c v", p=128), C[h][:, :, 0:256], reads=[kC[h]], writes=[], is_output=True)
                    kb.dma("pool", on_[h].rearrange("(c p o) -> p c o", p=128, o=1), C[h][:, :, 256:257], reads=[kC[h]], writes=[], is_output=True, **sl)
                kb.dma("pool", om_.rearrange("(h o) -> h o", o=1), mprev[:, 0:1], reads=["mprev"], writes=[], is_output=True, **sl)
        self.end_phase()

    def rwkv_phase(self, l):
        kb = self.kb
        I, S, O = self.I, self.S, self.O
        T, NS, NT = self.T, self.NS, self.NT
        self.begin_phase()
        sbp = self.sbp
        sl = dict(allow_slow_non_contiguous=True)
        NPI = 2
        EW = -math.exp(-0.5)
        P_ = {}
        for nm in ("r_w0", "r_a0", "r_k_k", "r_k_a", "r_ln_g", "r_ln_b"):
            P_[nm] = sbp("bc_" + nm, (128, D))
            kb.dma("sp", P_[nm][:], I[nm][l].partition_broadcast(128), writes=[nm])
        P_["r_r_k"] = sbp("bc_rk", (128, D))
        kb.dma("sp", P_["r_r_k"][:], I["r_r_k"][l].rearrange("h j -> (h j)").partition_broadcast(128), writes=["r_r_k"])
        mu = sbp("bc_mu", (128, R_SHIFT_W))
        kb.dma("sp", mu[:], I["r_mu"][l].partition_broadcast(128), writes=["mu"])
        w2 = sbp("w2", (64, 2, D))
        kb.dma("sp", w2[:, 0, :], I["r_w2"][l], writes=["w2"])
        kb.dma("sp", w2[:, 1, :], I["r_a2"][l], writes=["w2"])
        mks = sbp("mks", (128, 384))
        kb.dma("sp", mks[:], I["rmasks"], writes=["mks"])
        e12 = sbp("e12", (128, 1))
        kb.op("dve", lambda e: e.memset(e12[:], RWKV_GN_EPS), writes=["e12"])
        xr = sbp("xr", (128, R_SHIFT_W)); rp = sbp("rp", (128, R_SHIFT_W))
        lt = sbp("lt", (64, 2, 128))
        tv = {n: sbp("tv_" + n, (128, D)) for n in ("lw", "a", "an", "b", "k")}
        ssq = sbp("ssq", (128, 64))
        fm = {n: sbp("fm_" + n, (128, 8, 128)) for n in ("at", "rt", "bh", "bc", "kh", "kc", "cum", "et")}
        wc = sbp("wc", (128, 8, 16))
        ST = sbp("STt", (128, 8, 64))
        Vp = sbp("Vp", (128, 8, 64)); Ych = sbp("Ych", (128, 8, 64))
        nat = sbp("rnat", (128, 8, 64)); natx = sbp("rnatx", (128, 128)); nato = sbp("rnato", (128, 8, 64))
        U_ = []
        for i in range(NPI):
            d = {}
            d["ar"] = sbp("u%d_ar" % i, (128, 256))
            for n in ("bbd", "kbd", "Bcf", "Kcf", "Btm", "Ktm", "Mb", "X0", "X1", "Xt0", "Xt1", "P0", "P1"):
                d[n] = sbp("u%d_%s" % (i, n), (128, 128))
            d["MNb"] = sbp("u%d_MNb" % i, (128, 256)); d["MNk"] = sbp("u%d_MNk" % i, (128, 256))
            d["RHS"] = sbp("u%d_RHS" % i, (128, 64)); d["U"] = sbp("u%d_U" % i, (128, 64))
            U_.append(d)
        ytm = rp[:, D:2 * D]; zrt = rp[:, 0:D]; yst = sbp("yst", (128, 64))
        aob = [sbp("raob%d" % i, (128, 8, 128), BF16) for i in range(2)]

        def zero_units():
            for i in range(NPI):
                for n in ("ar", "bbd", "kbd", "Bcf", "Kcf"):
                    kb.op("pool", lambda e, i=i, n=n: e.memset(U_[i][n][:], 0.0), writes=[("u", i, n)])
        zero_units()
        kb.op("pool", lambda e: e.memset(ST[:], 0.0), writes=[("ST", h) for h in range(8)])

        z_all = lambda ti: [("z", ti)]

        def prep(t0, L, ti, is_s):
            kb.dma("sp", xr[:L, :], S["z"][t0:t0 + L, O_RC:O_RC + R_SHIFT_W], reads=z_all(ti), writes=["xr"])
            if is_s:
                kb.dma("sp", rp[:L, :], I["state_rwkv_shift"][l], writes=["rp"])
            elif t0 == 0:
                kb.op("pool", lambda e: e.memset(rp[0:1, :], 0.0), writes=["rp"])
                kb.dma("sp", rp[1:L, :], S["z"][0:L - 1, O_RC:O_RC + R_SHIFT_W], reads=z_all(ti), writes=["rp"])
            else:
                kb.dma("sp", rp[:L, :], S["z"][t0 - 1:t0 + L - 1, O_RC:O_RC + R_SHIFT_W], reads=z_all(ti) + z_all(ti - 1), writes=["rp"])
            kb.op("pool", lambda e: e.tensor_tensor(out=rp[:L, :], in0=rp[:L, :], in1=xr[:L, :], op=ALU.subtract), reads=["rp", "xr"], writes=["rp"])
            kb.op("dve", lambda e: e.tensor_tensor(out=rp[:L, :], in0=rp[:L, :], in1=mu[:L, :], op=ALU.mult), reads=["rp", "mu"], writes=["rp"])
            kb.op("pool", lambda e: e.tensor_tensor(out=xr[:L, :], in0=xr[:L, :], in1=rp[:L, :], op=ALU.add), reads=["rp", "xr"], writes=["xr"])
            r_ = xr[:L, 0:D]; kr = xr[:L, D:2 * D]; vr = xr[:L, 2 * D:3 * D]
            kb.dma("pool", S["vs"][t0:t0 + L, :], vr, reads=["xr"], writes=[("vs", ti)])
            kb.op("act", lambda e: e.activation(out=xr[:L, 3 * D:3 * D + 64], in_=xr[:L, 3 * D:3 * D + 64], func=AF.Tanh), reads=["xr"], writes=["xr"])
            p = self.rr("ptr", 2)
            pt, kpt = self.ptr[p], "ptr%d" % p
            for i2 in range(2):
                kb.op("pe", lambda e, i2=i2, pt=pt: e.transpose(out=pt[:64, i2 * 128:i2 * 128 + L], in_=xr[:L, 3 * D + 64 * i2:3 * D + 64 * i2 + 64],
                                                             identity=self.ident[:L, :L]), reads=["xr", "ident"], writes=[kpt])
            kb.op("dve", lambda e, pt=pt: e.tensor_copy(out=lt[:, :, :L], in_=pt[:64, 0:256].rearrange("p (a t) -> p a t", a=2)[:, :, :L]), reads=[kpt], writes=["lt"])
            for i2, (dst, pb, sc_) in enumerate((("lw", "r_w0", EW), ("a", "r_a0", 1.0))):
                for hf in range(2):
                    pq = self.rr("pp", 4)
                    pp, kpp = self.pp[pq], "pp%d" % pq
                    kb.op("pe", lambda e, i2=i2, hf=hf, pp=pp: e.matmul(pp[:L, :], lhsT=lt[:, i2, :L], rhs=w2[:, i2, hf * 512:(hf + 1) * 512], start=True, stop=True),
                          reads=["lt", "w2"], writes=[kpp])
                    kb.op("dve", lambda e, dst=dst, pb=pb, hf=hf, pp=pp: e.tensor_tensor(out=tv[dst][:L, hf * 512:(hf + 1) * 512], in0=pp[:L, :],
                                                                                   in1=P_[pb][:L, hf * 512:(hf + 1) * 512], op=ALU.add),
                          reads=[kpp, pb], writes=[("tv", dst)])
                kb.op("act", lambda e, dst=dst: e.activation(out=tv[dst][:L, :], in_=tv[dst][:L, :], func=AF.Sigmoid), reads=[("tv", dst)], writes=[("tv", dst)])
            kb.op("dve", lambda e: e.tensor_scalar(out=tv["lw"][:L, :], in0=tv["lw"][:L, :], scalar1=EW, scalar2=None, op0=ALU.mult), reads=[("tv", "lw")], writes=[("tv", "lw")])
            kb.op("dve", lambda e: e.tensor_tensor(out=tv["an"][:L, :], in0=kr, in1=P_["r_k_k"][:L, :], op=ALU.mult), reads=["xr", "r_k_k"], writes=[("tv", "an")])
            kb.op("pool", lambda e: e.tensor_tensor(out=tv["b"][:L, :], in0=tv["an"][:L, :], in1=tv["an"][:L, :], op=ALU.mult), reads=[("tv", "an")], writes=[("tv", "b")])
            kb.op("dve", lambda e: e.tensor_reduce(out=ssq[:L, 0:16], in_=tv["b"][:L, :].rearrange("t (h j) -> t h j", h=16), axis=AX.X, op=ALU.add),
                  reads=[("tv", "b")], writes=["ssq"])
            kb.op("act", lambda e: e.activation(out=ssq[:L, 0:16], in_=ssq[:L, 0:16], func=AF.Sqrt), reads=["ssq"], writes=["ssq"])
            kb.op("dve", lambda e: e.tensor_scalar(out=ssq[:L, 0:16], in0=ssq[:L, 0:16], scalar1=1e-12, scalar2=None, op0=ALU.max), reads=["ssq"], writes=["ssq"])
            kb.op("dve", lambda e: e.reciprocal(out=ssq[:L, 16:32], in_=ssq[:L, 0:16]), reads=["ssq"], writes=["ssq"])
            hv = lambda x: x[:L, :].rearrange("t (h j) -> t h j", h=16)
            kb.op("dve", lambda e: e.tensor_tensor(out=hv(tv["an"]), in0=hv(tv["an"]), in1=ssq[:L, 16:32].unsqueeze(2).to_broadcast([L, 16, 64]), op=ALU.mult),
                  reads=[("tv", "an"), "ssq"], writes=[("tv", "an")])
            kb.op("pool", lambda e: e.tensor_tensor(out=tv["b"][:L, :], in0=tv["an"][:L, :], in1=tv["a"][:L, :], op=ALU.mult),
                  reads=[("tv", "an"), ("tv", "a")], writes=[("tv", "b")])
            kb.op("dve", lambda e: e.tensor_scalar(out=tv["an"][:L, :], in0=tv["an"][:L, :], scalar1=-1.0, scalar2=None, op0=ALU.mult),
                  reads=[("tv", "an"), ("tv", "b")], writes=[("tv", "an")])
            kb.op("dve", lambda e: e.scalar_tensor_tensor(out=tv["k"][:L, :], in0=tv["a"][:L, :], scalar=-1.0, in1=P_["r_k_a"][:L, :], op0=ALU.add, op1=ALU.mult),
                  reads=[("tv", "a"), "r_k_a"], writes=[("tv", "k")])
            kb.op("dve", lambda e: e.scalar_tensor_tensor(out=tv["k"][:L, :], in0=tv["k"][:L, :], scalar=1.0, in1=kr, op0=ALU.add, op1=ALU.mult),
                  reads=[("tv", "k"), "xr"], writes=[("tv", "k")])
            kb.op("pool", lambda e: e.tensor_tensor(out=tv["a"][:L, :], in0=tv["k"][:L, :], in1=P_["r_r_k"][:L, :], op=ALU.mult),
                  reads=[("tv", "k"), "r_r_k", ("tv", "b")], writes=[("tv", "a")])
            kb.op("pool", lambda e: e.tensor_tensor(out=tv["a"][:L, :], in0=tv["a"][:L, :], in1=r_, op=ALU.mult), reads=[("tv", "a"), "xr"], writes=[("tv", "a")])
            kb.op("dve", lambda e: e.tensor_reduce(out=ssq[:L, 32:48], in_=hv(tv["a"]), axis=AX.X, op=ALU.add), reads=[("tv", "a")], writes=["ssq2"])
            for (dst, src, ksrc) in (("at", tv["an"][:L, :], ("tv", "an")), ("rt", r_, "xr"), ("bh", tv["b"][:L, :], ("tv", "b")),
                                     ("kh", tv["k"][:L, :], ("tv", "k")), ("cum", tv["lw"][:L, :], ("tv", "lw"))):
                for half in range(2):
                    p = self.rr("ptr", 2)
                    pt, kpt = self.ptr[p], "ptr%d" % p
                    for k4 in range(4):
                        kc = half * 4 + k4
                        kb.op("pe", lambda e, k4=k4, kc=kc, pt=pt, src=src: e.transpose(out=pt[:, k4 * 128:k4 * 128 + L], in_=src[:, kc * 128:(kc + 1) * 128],
                                                                                  identity=self.ident[:L, :L]), reads=[ksrc, "ident"], writes=[kpt])
                    self.evac(fm[dst][:, half * 4:half * 4 + 4, :L], pt[:, :].rearrange("p (k t) -> p k t", k=4)[:, :, :L], [kpt], [("fm", dst)])
            Lc = 1 if is_s else 64
            nch = L // Lc
            lwT = fm["cum"]
            kb.op("pool", lambda e: e.tensor_copy(out=fm["et"][:, :, :L], in_=lwT[:, :, :L]), reads=[("fm", "cum")], writes=[("fm", "et")])
            if Lc > 1:
                for hh in range(8):
                    for c in range(nch):
                        kb.op("dve", lambda e, hh=hh, c=c: e.tensor_tensor_scan(out=fm["cum"][:, hh, c * Lc:(c + 1) * Lc], data0=self.onesf[:, :Lc],
                                                                             data1=fm["et"][:, hh, c * Lc:(c + 1) * Lc], initial=0.0, op0=ALU.mult, op1=ALU.add),
                              reads=[("fm", "et"), "onesf"], writes=[("fm", "cum")])
            F = lambda n: fm[n][:, :, :L]
            C4 = lambda n: fm[n][:, :, :L].rearrange("p k (c t) -> p k c t", t=Lc)
            kb.op("dve", lambda e: e.tensor_tensor(out=F("et"), in0=F("cum"), in1=F("et"), op=ALU.subtract), reads=[("fm", "cum"), ("fm", "et")], writes=[("fm", "et")])
            kb.op("act", lambda e: e.activation(out=F("et"), in_=F("et"), func=AF.Exp), reads=[("fm", "et")], writes=[("fm", "et")])
            kb.op("dve", lambda e: e.tensor_tensor(out=F("at"), in0=F("at"), in1=F("et"), op=ALU.mult), reads=[("fm", "at"), ("fm", "et")], writes=[("fm", "at")])
            kb.op("act", lambda e: e.activation(out=F("et"), in_=F("cum"), func=AF.Exp), reads=[("fm", "cum"), ("fm", "at")], writes=[("fm", "et")])
            kb.op("dve", lambda e: e.tensor_tensor(out=F("rt"), in0=F("rt"), in1=F("et"), op=ALU.mult), reads=[("fm", "rt"), ("fm", "et")], writes=[("fm", "rt")])
            kb.op("pool", lambda e: e.tensor_copy(out=wc[:, :, :nch], in_=C4("et")[:, :, :, Lc - 1]), reads=[("fm", "et")], writes=["wc"])
            kb.op("dve", lambda e: e.tensor_tensor(out=C4("et"), in0=C4("cum")[:, :, :, Lc - 1:Lc].to_broadcast([128, 8, nch, Lc]), in1=C4("cum"), op=ALU.subtract),
                  reads=[("fm", "cum"), ("fm", "rt"), "wc"], writes=[("fm", "et")])
            kb.op("act", lambda e: e.activation(out=F("et"), in_=F("et"), func=AF.Exp), reads=[("fm", "et")], writes=[("fm", "et")])
            kb.op("dve", lambda e: e.tensor_tensor(out=F("bc"), in0=F("bh"), in1=F("et"), op=ALU.mult), reads=[("fm", "bh"), ("fm", "et")], writes=[("fm", "bc")])
            kb.op("pool", lambda e: e.tensor_tensor(out=F("kc"), in0=F("kh"), in1=F("et"), op=ALU.mult), reads=[("fm", "kh"), ("fm", "et")], writes=[("fm", "kc")])
            kb.op("act", lambda e: e.activation(out=F("et"), in_=F("cum"), func=AF.Exp, scale=-1.0), reads=[("fm", "cum"), ("fm", "bc"), ("fm", "kc")], writes=[("fm", "et")])
            kb.op("dve", lambda e: e.tensor_tensor(out=F("bh"), in0=F("bh"), in1=F("et"), op=ALU.mult), reads=[("fm", "bh"), ("fm", "et")], writes=[("fm", "bh")])
            kb.op("pool", lambda e: e.tensor_tensor(out=F("kh"), in0=F("kh"), in1=F("et"), op=ALU.mult), reads=[("fm", "kh"), ("fm", "et")], writes=[("fm", "kh")])

        def core(hhs, col0, Lc, cidx):
            us = [(U_[i], i, hh) for i, hh in enumerate(hhs)]
            fmk = [("fm", n) for n in ("at", "rt", "bh", "bc", "kh", "kc")]
            for (u, i, hh) in us:
                for half in range(2):
                    ps = slice(half * 64, half * 64 + 64)
                    cs_ = slice(col0, col0 + Lc)
                    for (tn, off, src, eng) in (("ar", 0, "at", "dve"), ("ar", 128, "rt", "pool"), ("bbd", 0, "bh", "dve"), ("kbd", 0, "kh", "pool"),
                                                ("Bcf", 0, "bc", "dve"), ("Kcf", 0, "kc", "pool")):
                        kb.op(eng, lambda e, u=u, tn=tn, off=off, src=src, ps=ps, half=half, hh=hh, cs_=cs_: e.tensor_copy(
                            out=u[tn][ps, off + half * 64:off + half * 64 + Lc], in_=fm[src][ps, hh, cs_]),
                            reads=[("fm", src)], writes=[("u", i, tn)])
            for (u, i, hh) in us:
                for (src, dst) in (("Bcf", "Btm"), ("Kcf", "Ktm")):
                    p = self.rr("ptr", 2)
                    pt, kpt = self.ptr[p], "ptr%d" % p
                    kb.op("pe", lambda e, u=u, src=src, pt=pt: e.transpose(out=pt[:, 0:128], in_=u[src][:, :], identity=self.ident[:, :]),
                          reads=[("u", i, src), "ident"], writes=[kpt])
                    self.evac(u[dst][:, :], pt[:, 0:128], [kpt], [("u", i, dst)])
            for (u, i, hh) in us:
                for (lh, dst) in (("bbd", "MNb"), ("kbd", "MNk")):
                    pq = self.rr("pp", 4)
                    pp, kpp = self.pp[pq], "pp%d" % pq
                    kb.op("pe", lambda e, u=u, lh=lh, pp=pp: e.matmul(pp[:, 0:256], lhsT=u[lh][:, :], rhs=u["ar"][:, :], start=True, stop=True),
                          reads=[("u", i, lh), ("u", i, "ar")], writes=[kpp])
                    kb.op("dve", lambda e, u=u, dst=dst, pp=pp: e.tensor_tensor(out=u[dst][:, :], in0=pp[:, 0:256], in1=mks[:, 0:256], op=ALU.mult),
                          reads=[kpp, "mks"], writes=[("u", i, dst)])
                pq = self.rr("pp", 4)
                pp, kpp = self.pp[pq], "pp%d" % pq
                kb.op("pe", lambda e, u=u, pp=pp: e.matmul(pp[:, 0:128], lhsT=u["ar"][:, 0:128], rhs=u["bbd"][:, :], start=True, stop=True),
                      reads=[("u", i, "bbd"), ("u", i, "ar")], writes=[kpp])
                kb.op("dve", lambda e, u=u, pp=pp: e.tensor_tensor(out=u["Xt0"][:, :], in0=pp[:, 0:128], in1=mks[:, 256:384], op=ALU.mult),
                      reads=[kpp, "mks"], writes=[("u", i, "Xt0")])
                kb.op("pool", lambda e, u=u: e.tensor_copy(out=u["X0"][:, :], in_=u["MNb"][:, 0:128]), reads=[("u", i, "MNb")], writes=[("u", i, "X0")])
                kb.op("pool", lambda e, u=u: e.tensor_tensor(out=u["P0"][:, :], in0=u["MNb"][:, 0:128], in1=self.ident[:, :], op=ALU.add),
                      reads=[("u", i, "MNb"), "ident"], writes=[("u", i, "P0")])
            nlev = 0
            while (1 << (nlev + 1)) < Lc:
                nlev += 1
            cur = 0
            for lev in range(nlev if Lc > 1 else 0):
                nxt = 1 - cur
                lastlev = (lev == nlev - 1)
                for (u, i, hh) in us:
                    X, Xt, Pc = u["X%d" % cur], u["Xt%d" % cur], u["P%d" % cur]
                    Xn, Xtn, Pn = u["X%d" % nxt], u["Xt%d" % nxt], u["P%d" % nxt]
                    kX, kXt, kP = ("u", i, "X%d" % cur), ("u", i, "Xt%d" % cur), ("u", i, "P%d" % cur)
                    kXn, kXtn, kPn = ("u", i, "X%d" % nxt), ("u", i, "Xt%d" % nxt), ("u", i, "P%d" % nxt)
                    if not lastlev:
                        pq = self.rr("pp", 4)
                        pp, kpp = self.pp[pq], "pp%d" % pq
                        kb.op("pe", lambda e, pp=pp, X=X, Xt=Xt: e.matmul(pp[:, 0:128], lhsT=Xt[:, :], rhs=X[:, :], start=True, stop=True), reads=[kX, kXt], writes=[kpp])
                        self.evac(Xn[:, :], pp[:, 0:128], [kpp], [kXn])
                    pq = self.rr("pp", 4)
                    pp, kpp = self.pp[pq], "pp%d" % pq
                    kb.op("pe", lambda e, pp=pp, X=X, Xt=Xt: e.matmul(pp[:, 0:128], lhsT=X[:, :], rhs=Xt[:, :], start=True, stop=True), reads=[kX, kXt], writes=[kpp])
                    self.evac(Xtn[:, :], pp[:, 0:128], [kpp], [kXtn])
                    pq = self.rr("ptr", 2)
                    pp, kpp = self.ptr[pq], "ptr%d" % pq
                    kb.op("pe", lambda e, pp=pp, Xtn=Xtn, Pc=Pc: e.matmul(pp[:, 0:128], lhsT=Xtn[:, :], rhs=Pc[:, :], start=True, stop=True), reads=[kXtn, kP], writes=[kpp])
                    kb.op("dve", lambda e, pp=pp, Pn=Pn, Pc=Pc: e.tensor_tensor(out=Pn[:, :], in0=pp[:, 0:128], in1=Pc[:, :], op=ALU.add), reads=[kpp, kP], writes=[kPn])
                cur = nxt
            for (u, i, hh) in us:
                Pf, kPf = u["P%d" % cur], ("u", i, "P%d" % cur)
                kS = ("ST", hh)
                pq = self.rr("pp", 4)
                pp, kpp = self.pp[pq], "pp%d" % pq
                kb.op("pe", lambda e, u=u, pp=pp, hh=hh: e.matmul(pp[:, 0:64], lhsT=u["ar"][:, 0:128], rhs=ST[:, hh, :], start=True, stop=False),
                      reads=[("u", i, "ar"), kS], writes=[kpp])
                kb.op("pe", lambda e, u=u, pp=pp, hh=hh: e.matmul(pp[:, 0:64], lhsT=u["MNk"][:, 0:128], rhs=Vp[:, hh, :], start=False, stop=True),
                      reads=[("u", i, "MNk"), ("Vp", hh)], writes=[kpp])
                self.evac(u["RHS"][:, :], pp[:, 0:64], [kpp], [("u", i, "RHS")])
                pq = self.rr("pp", 4)
                pp, kpp = self.pp[pq], "pp%d" % pq
                kb.op("pe", lambda e, pp=pp, Pf=Pf, u=u: e.matmul(pp[:, 0:64], lhsT=Pf[:, :], rhs=u["RHS"][:, :], start=True, stop=True), reads=[kPf, ("u", i, "RHS")], writes=[kpp])
                self.evac(u["U"][:, :], pp[:, 0:64], [kpp], [("u", i, "U")])
                pq = self.rr("pp", 4)
                pp, kpp = self.pp[pq], "pp%d" % pq
                kb.op("pe", lambda e, u=u, pp=pp, hh=hh: e.matmul(pp[:, 0:64], lhsT=u["ar"][:, 128:256], rhs=ST[:, hh, :], start=True, stop=False),
                      reads=[("u", i, "ar"), kS], writes=[kpp])
                kb.op("pe", lambda e, u=u, pp=pp: e.matmul(pp[:, 0:64], lhsT=u["MNb"][:, 128:256], rhs=u["U"][:, :], start=False, stop=False),
                      reads=[("u", i, "MNb"), ("u", i, "U")], writes=[kpp])
                kb.op("pe", lambda e, u=u, pp=pp, hh=hh: e.matmul(pp[:, 0:64], lhsT=u["MNk"][:, 128:256], rhs=Vp[:, hh, :], start=False, stop=True),
                      reads=[("u", i, "MNk"), ("Vp", hh)], writes=[kpp])
                self.evac(Ych[:, hh, :], pp[:, 0:64], [kpp], [("Ych", hh)])
                pq = self.rr("ptr", 2)
                pp, kpp = self.ptr[pq], "ptr%d" % pq
                kb.op("pe", lambda e, u=u, pp=pp: e.matmul(pp[:, 0:64], lhsT=u["Btm"][:, :], rhs=u["U"][:, :], start=True, stop=False),
                      reads=[("u", i, "Btm"), ("u", i, "U")], writes=[kpp])
                kb.op("pe", lambda e, u=u, pp=pp, hh=hh: e.matmul(pp[:, 0:64], lhsT=u["Ktm"][:, :], rhs=Vp[:, hh, :], start=False, stop=True),
                      reads=[("u", i, "Ktm"), ("Vp", hh)], writes=[kpp])
                kb.op("dve", lambda e, pp=pp, hh=hh: e.scalar_tensor_tensor(out=ST[:, hh, :], in0=ST[:, hh, :], scalar=wc[:, hh, cidx:cidx + 1], in1=pp[:, 0:64],
                                                                        op0=ALU.mult, op1=ALU.add), reads=[kS, "wc", kpp], writes=[kS])

        def post(t0, L, ti, bi):
            kb.dma("sp", ytm[:L, :], S["ys"][t0:t0 + L, :], reads=[("ys", ti)], writes=["rp"])
            kb.dma("sp", zrt[:L, :], S["z"][t0:t0 + L, O_ZR:O_ZR + D], reads=z_all(ti), writes=["rp"])
            kb.op("act", lambda e: e.activation(out=zrt[:L, :], in_=zrt[:L, :], func=AF.Silu), reads=["rp"], writes=["rp"])
            hv = lambda x: x[:L, :].rearrange("t (h j) -> t h j", h=16)
            kb.op("dve", lambda e: e.tensor_reduce(out=yst[:L, 0:16], in_=hv(ytm), axis=AX.X, op=ALU.add), reads=["rp"], writes=["yst"])
            kb.op("dve", lambda e: e.tensor_scalar(out=yst[:L, 0:16], in0=yst[:L, 0:16], scalar1=-1.0 / 64.0, scalar2=None, op0=ALU.mult), reads=["yst"], writes=["yst"])
            kb.op("dve", lambda e: e.tensor_tensor(out=hv(ytm), in0=hv(ytm), in1=yst[:L, 0:16].unsqueeze(2).to_broadcast([L, 16, 64]), op=ALU.add),
                  reads=["rp", "yst"], writes=["rp"])
            kb.op("pool", lambda e: e.tensor_tensor(out=tv["lw"][:L, :], in0=ytm[:L, :], in1=ytm[:L, :], op=ALU.mult), reads=["rp"], writes=[("tv", "lw")])
            kb.op("dve", lambda e: e.tensor_reduce(out=yst[:L, 16:32], in_=hv(tv["lw"]), axis=AX.X, op=ALU.add), reads=[("tv", "lw")], writes=["yst"])
            kb.op("act", lambda e: e.activation(out=yst[:L, 32:48], in_=yst[:L, 16:32], func=AF.Sqrt, scale=1.0 / 64.0, bias=e12[:L, 0:1]), reads=["yst", "e12"], writes=["yst"])
            kb.op("dve", lambda e: e.reciprocal(out=yst[:L, 48:64], in_=yst[:L, 32:48]), reads=["yst"], writes=["yst"])
            kb.op("dve", lambda e: e.tensor_tensor(out=hv(ytm), in0=hv(ytm), in1=yst[:L, 48:64].unsqueeze(2).to_broadcast([L, 16, 64]), op=ALU.mult),
                  reads=["rp", "yst"], writes=["rp"])
            kb.op("dve", lambda e: e.tensor_tensor(out=ytm[:L, :], in0=ytm[:L, :], in1=P_["r_ln_g"][:L, :], op=ALU.mult), reads=["rp", "r_ln_g"], writes=["rp"])
            kb.op("pool", lambda e: e.tensor_tensor(out=ytm[:L, :], in0=ytm[:L, :], in1=P_["r_ln_b"][:L, :], op=ALU.add), reads=["rp", "r_ln_b"], writes=["rp"])
            kb.op("dve", lambda e: e.tensor_tensor(out=hv(tv["lw"]), in0=xr[:L, 2 * D:3 * D].rearrange("t (h j) -> t h j", h=16),
                                                   in1=ssq[:L, 32:48].unsqueeze(2).to_broadcast([L, 16, 64]), op=ALU.mult), reads=["xr", "ssq2"], writes=[("tv", "lw")])
            kb.op("pool", lambda e: e.tensor_tensor(out=ytm[:L, :], in0=ytm[:L, :], in1=tv["lw"][:L, :], op=ALU.add), reads=["rp", ("tv", "lw")], writes=["rp"])
            kb.op("dve", lambda e: e.tensor_tensor(out=ytm[:L, :], in0=ytm[:L, :], in1=zrt[:L, :], op=ALU.mult), reads=["rp", "rp"], writes=["rp"])
            aj = self.rr("raob", 2)
            for half in range(2):
                p = self.rr("ptr", 2)
                pt, kpt = self.ptr[p], "ptr%d" % p
                for k4 in range(4):
                    kc = half * 4 + k4
                    kb.op("pe", lambda e, k4=k4, kc=kc, pt=pt: e.transpose(out=pt[:, k4 * 128:k4 * 128 + L], in_=ytm[:L, kc * 128:(kc + 1) * 128],
                                                                       identity=self.ident[:L, :L]), reads=["rp", "ident"], writes=[kpt])
                self.evac(aob[aj][:, half * 4:half * 4 + 4, :L], pt[:, :].rearrange("p (k t) -> p k t", k=4)[:, :, :L], [kpt], ["raob%d" % aj])
            kb.dma("pool", S["act_r"][:, t0:t0 + L].rearrange("(k p) t -> p k t", p=128), aob[aj][:, :, :L], reads=["raob%d" % aj], writes=[("act", "r", bi)])

        def state_out(dst):
            for hh in range(8):
                kb.op("pool", lambda e, hh=hh: e.memset(natx[:, :], 0.0), writes=["natx"])
                for half in range(2):
                    ps = slice(half * 64, half * 64 + 64)
                    kb.op("pool", lambda e, hh=hh, ps=ps: e.tensor_copy(out=natx[ps, ps], in_=ST[ps, hh, :]), reads=[("ST", hh)], writes=["natx"])
                p = self.rr("ptr", 2)
                pt, kpt = self.ptr[p], "ptr%d" % p
                kb.op("pe", lambda e, pt=pt: e.transpose(out=pt[:, 0:128], in_=natx[:, :], identity=self.ident[:, :]), reads=["natx", "ident"], writes=[kpt])
                for half in range(2):
                    ps = slice(half * 64, half * 64 + 64)
                    kb.op("dve", lambda e, hh=hh, ps=ps, pt=pt: e.tensor_copy(out=nato[ps, hh, :], in_=pt[ps, ps]), reads=[kpt], writes=["nato"])
            kb.dma("pool", dst.rearrange("(hh hl) i j -> (hl i) hh j", hl=2), nato[:, :, :], reads=["nato"], writes=[], is_output=True)

        def state_in(src):
            kb.dma("sp", nat[:, :, :], src.rearrange("(hh hl) i j -> (hl i) hh j", hl=2), writes=["nat"])
            for hh in range(8):
                kb.op("pool", lambda e: e.memset(natx[:, :], 0.0), writes=["natx"])
                for half in range(2):
                    ps = slice(half * 64, half * 64 + 64)
                    kb.op("pool", lambda e, hh=hh, ps=ps: e.tensor_copy(out=natx[ps, ps], in_=nat[ps, hh, :]), reads=["nat"], writes=["natx"])
                p = self.rr("ptr", 2)
                pt, kpt = self.ptr[p], "ptr%d" % p
                kb.op("pe", lambda e, pt=pt: e.transpose(out=pt[:, 0:128], in_=natx[:, :], identity=self.ident[:, :]), reads=["natx", "ident"], writes=[kpt])
                for half in range(2):
                    ps = slice(half * 64, half * 64 + 64)
                    kb.op("dve", lambda e, hh=hh, ps=ps, pt=pt: e.tensor_copy(out=ST[ps, hh, :], in_=pt[ps, ps]), reads=[kpt], writes=[("ST", hh)])

        blk_of = lambda t: next(i for i, (b0, bn) in enumerate(self.blocks) if b0 <= t < b0 + bn)
        for ti in range(T // 128):
            t0 = ti * 128
            prep(t0, 128, ti, False)
            for c in range(2):
                for half in range(2):
                    kb.dma("sp", Vp[half * 64:half * 64 + 64, :, :],
                           S["vs"][t0 + c * 64:t0 + c * 64 + 64, :].rearrange("t (hh hl i) -> hl t hh i", hl=2, i=64)[half],
                           reads=[("vs", ti)], writes=[("Vp", h) for h in range(8)])
                for g in range(0, 8, NPI):
                    core(list(range(g, g + NPI)), c * 64, 64, c)
                for half in range(2):
                    kb.dma("pool", S["ys"][t0 + c * 64:t0 + c * 64 + 64, :].rearrange("t (hh hl i) -> hl t hh i", hl=2, i=64)[half],
                           Ych[half * 64:half * 64 + 64, :, :], reads=[("Ych", h) for h in range(8)], writes=[("ys", ti)])
            post(t0, 128, ti, blk_of(t0))
        state_out(O["wkv_prompt"][l])
        if NS:
            ti = T // 128
            prep(T, NS, ti, True)
            zero_units()
            kb.op("pool", lambda e: e.memset(Vp[:], 0.0), writes=[("Vp", h) for h in range(8)])
            for b in range(NS):
                state_in(I["state_rwkv_wkv"][l, b])
                for half in range(2):
                    kb.dma("sp", Vp[half * 64:half * 64 + 1, :, :],
                           S["vs"][T + b:T + b + 1, :].rearrange("t (hh hl i) -> hl t hh i", hl=2, i=64)[half],
                           reads=[("vs", ti)], writes=[("Vp", h) for h in range(8)])
                for g in range(0, 8, NPI):
                    core(list(range(g, g + NPI)), b, 1, b)
                for half in range(2):
                    kb.dma("pool", S["ys"][T + b:T + b + 1, :].rearrange("t (hh hl i) -> hl t hh i", hl=2, i=64)[half],
                           Ych[half * 64:half * 64 + 1, :, :], reads=[("Ych", h) for h in range(8)], writes=[("ys", ti)])
                state_out(O["wkv_sample"][l, b])
            post(T, NS, ti, blk_of(T))
        self.end_phase()

    def sin_to(self, tkey, out, ang, tf, ti, tm, key_out, key_ang, shift=0.0):
        kb = self.kb
        TWO_PI = 2.0 * math.pi
        kt = ("sin_tmp", tkey)
        kb.op("dve", lambda e: e.tensor_scalar(out=tm, in0=ang, scalar1=shift, scalar2=None, op0=ALU.add),
              reads=[key_ang], writes=[(kt, "m")])
        kb.op("dve", lambda e: e.tensor_scalar(out=tf, in0=tm, scalar1=1.0 / TWO_PI, scalar2=None, op0=ALU.mult),
              reads=[(kt, "m")], writes=[(kt, "f")])
        kb.op("dve", lambda e: e.tensor_copy(out=ti, in_=tf), reads=[(kt, "f")], writes=[(kt, "i")])
        kb.op("dve", lambda e: e.tensor_copy(out=tf, in_=ti), reads=[(kt, "i")], writes=[(kt, "f")])
        kb.op("dve", lambda e: e.scalar_tensor_tensor(out=tm, in0=tf, scalar=-TWO_PI, in1=tm, op0=ALU.mult, op1=ALU.add),
              reads=[(kt, "f"), (kt, "m")], writes=[(kt, "m")])
        kb.op("dve", lambda e: e.tensor_scalar(out=tf, in0=tm, scalar1=math.pi, scalar2=-TWO_PI, op0=ALU.is_gt, op1=ALU.mult),
              reads=[(kt, "m")], writes=[(kt, "f")])
        kb.op("dve", lambda e: e.tensor_tensor(out=tm, in0=tm, in1=tf, op=ALU.add), reads=[(kt, "f"), (kt, "m")],
              writes=[(kt, "m")])
        kb.op("dve", lambda e: e.tensor_scalar(out=tf, in0=tm, scalar1=-math.pi, scalar2=TWO_PI, op0=ALU.is_lt, op1=ALU.mult),
              reads=[(kt, "m")], writes=[(kt, "f")])
        kb.op("dve", lambda e: e.tensor_tensor(out=tm, in0=tm, in1=tf, op=ALU.add), reads=[(kt, "f"), (kt, "m")],
              writes=[(kt, "m")])
        kb.op("dve", lambda e: e.tensor_scalar(out=tm, in0=tm, scalar1=-3.1415925, scalar2=3.1415925, op0=ALU.max, op1=ALU.min),
              reads=[(kt, "m")], writes=[(kt, "m")])
        kb.op("act", lambda e: e.activation(out=out, in_=tm, func=AF.Sin), reads=[(kt, "m")], writes=[key_out])

    def s5_phase(self, l):
        kb = self.kb
        I, S, O = self.I, self.S, self.O
        T, NS = self.T, self.NS
        self.begin_phase()
        sbp = self.sbp
        nat = sbp("nat", (32, 14, 128))
        nati = sbp("nati", (32, 128), I32)
        PT = sbp("PT", (128, 6, 32))
        BR = sbp("BR", (128, 32, 16)); BI = sbp("BI", (128, 32, 16))
        bbr = sbp("bbr", (128, 32, 16)); bbi = sbp("bbi", (128, 32, 16))
        xin = sbp("xin", (128, 512))
        btmp = xin[:, :].rearrange("p (s c) -> p s c", c=16)
        Bz = [sbp("Bz%d" % i, (128, 32, 128), BF16) for i in range(2)]
        Cn = [sbp("Cn%d" % i, (128, 8, 64)) for i in range(2)]
        Cz = [sbp("Cz%d" % i, (128, 32, 128), BF16) for i in range(2)]
        cosT = sbp("cosT", (128, 32, 128)); sinT = sbp("sinT", (128, 32, 128))
        ang = sbp("ang", (128, 2, 128)); angf = sbp("angf", (128, 2, 128)); angm = sbp("angm", (128, 2, 128))
        angi = sbp("angi", (128, 2, 128), I32)
        car = [sbp("car%d" % i, (128, 32)) for i in range(2)]
        ctmp = sbp("ctmp", (128, 4))
        m3 = sbp("m3", (128, 4, 2)); m4 = sbp("m4", (128, 4, 8)); iota1 = sbp("iota1", (128, 128))
        dcol = sbp("dcol", (128, 8)); gbcol = sbp("gbcol", (128, 8))
        wg = sbp("wglu", (128, KC, D), BF16)
        wstg = [sbp("wstg%d" % i, (128, KC, 128)) for i in range(1)]
        uf = [sbp("uf%d" % i, (128, 256)) for i in range(2)]
        ub = [sbp("ub%d" % i, (128, 256), BF16) for i in range(2)]
        tq = [sbp("tq%d" % i, (128, 256)) for i in range(4)]
        bh = [sbp("bh%d" % i, (128, 256)) for i in range(2)]
        sh = [sbp("sh%d" % i, (128, 256)) for i in range(2)]
        sbf = [sbp("sbf%d" % i, (128, 256), BF16) for i in range(2)]
        ysg = sbp("ysg", (128, KC, 256)); ysb = sbp("ysb", (128, KC, 256), BF16)
        yt = [sbp("yt%d" % i, (128, 256)) for i in range(2)]
        zst = [sbp("zst%d" % i, (128, 256)) for i in range(2)]
        aout = [sbp("aout%d" % i, (128, 256), BF16) for i in range(2)]
        s0T = [sbp("s0T%d" % i, (128, 32, 16)) for i in range(2)]
        snw = [sbp("snw%d" % i, (128, 32, 16)) for i in range(2)]
        sst = sbp("sst", (32, 512))

        kb.dma("sp", m3[:], I["mask3"], writes=["m3"])
        kb.dma("sp", m4[:], I["mask4"], writes=["m4"])
        kb.dma("sp", iota1[:], I["iota1"], writes=["iota1"])
        kb.dma("sp", nat[:, 0, :], I["s_lam_re"][l].rearrange("(s g) p -> s (g p)", g=2), writes=["nat0"])
        kb.dma("sp", nat[:, 1, :], I["s_lam_im"][l].rearrange("(s g) p -> s (g p)", g=2), writes=["nat1"])
        kb.dma("sp", nat[:, 2, 0:2], I["s_log_dt"][l].rearrange("(s g) -> s g", g=2), writes=["nat2"])
        kb.dma("sp", dcol[:], I["s_d"][l].rearrange("(k p) -> p k", p=128), writes=["dcol"], allow_slow_non_contiguous=True)
        kb.dma("sp", gbcol[:], I["s_glu_b"][l].rearrange("(k p) -> p k", p=128), writes=["gbcol"], allow_slow_non_contiguous=True)
        kb.dma("sp", BR[:], I["s_b_re"][l].rearrange("(s g) p c -> (g p) s c", g=2), writes=["BR"])
        kb.dma("sp", BI[:], I["s_b_im"][l].rearrange("(s g) p c -> (g p) s c", g=2), writes=["BI"])
        kb.dma("sp", Cn[0][:], I["s_c_re"][l].rearrange("(o g) c p -> (g c) o p", g=8), writes=["Cn0"])
        kb.dma("sp", Cn[1][:], I["s_c_im"][l].rearrange("(o g) c p -> (g c) o p", g=8), writes=["Cn1"])
        for h in range(8):
            kb.dma("sp", wstg[0][:], I["s_glu_w"][l][:, h * 128:(h + 1) * 128].rearrange("(k p) c -> p k c", p=128),
                   writes=["wstg"])
            kb.op("pool", lambda e, h=h: e.tensor_copy(out=wg[:, :, h * 128:(h + 1) * 128], in_=wstg[0][:]),
                  reads=["wstg"], writes=["wg"])
        N = lambda i: nat[:, i, :]
        kb.op("act", lambda e: e.activation(out=nat[:, 2, 2:4], in_=nat[:, 2, 0:2], func=AF.Exp), reads=["nat2"], writes=["nat2"])
        kb.op("dve", lambda e: e.tensor_copy(out=nat[:, 3, :].rearrange("s (g p) -> s g p", g=2),
                                             in_=nat[:, 2, 2:4].unsqueeze(2).to_broadcast([32, 2, 64])),
              reads=["nat2"], writes=["nat3"])
        kb.op("dve", lambda e: e.tensor_scalar(out=N(0), in0=N(0), scalar1=-1e-4, scalar2=None, op0=ALU.min),
              reads=["nat0"], writes=["nat0"])
        kb.op("dve", lambda e: e.tensor_tensor(out=N(4), in0=N(0), in1=N(3), op=ALU.mult), reads=["nat0", "nat3"], writes=["nat4"])
        kb.op("act", lambda e: e.activation(out=N(4), in_=N(4), func=AF.Exp), reads=["nat4"], writes=["nat4"])
        kb.op("dve", lambda e: e.tensor_tensor(out=N(5), in0=N(1), in1=N(3), op=ALU.mult), reads=["nat1", "nat3"], writes=["nat5"])
        self.sin_to("nat", N(6), N(5), N(12), nati[:, :], N(13), "nat6", "nat5")
        self.sin_to("nat", N(7), N(5), N(12), nati[:, :], N(13), "nat7", "nat5", shift=math.pi / 2)
        kb.op("dve", lambda e: e.tensor_tensor(out=N(8), in0=N(4), in1=N(7), op=ALU.mult), reads=["nat4", "nat7"], writes=["nat8"])
        kb.op("dve", lambda e: e.tensor_tensor(out=N(9), in0=N(4), in1=N(6), op=ALU.mult), reads=["nat4", "nat6"], writes=["nat9"])
        kb.op("dve", lambda e: e.tensor_tensor(out=N(12), in0=N(0), in1=N(0), op=ALU.mult), reads=["nat0"], writes=["nat12"])
        kb.op("dve", lambda e: e.tensor_tensor(out=N(13), in0=N(1), in1=N(1), op=ALU.mult), reads=["nat1"], writes=["nat13"])
        kb.op("dve", lambda e: e.tensor_tensor(out=N(12), in0=N(12), in1=N(13), op=ALU.add), reads=["nat12", "nat13"], writes=["nat12"])
        kb.op("dve", lambda e: e.reciprocal(out=N(12), in_=N(12)), reads=["nat12"], writes=["nat12"])
        kb.op("dve", lambda e: e.tensor_scalar(out=N(13), in0=N(8), scalar1=-1.0, scalar2=None, op0=ALU.add), reads=["nat8"], writes=["nat13"])
        kb.op("dve", lambda e: e.tensor_tensor(out=N(10), in0=N(13), in1=N(0), op=ALU.mult), reads=["nat13", "nat0"], writes=["nat10"])
        kb.op("dve", lambda e: e.tensor_tensor(out=N(11), in0=N(9), in1=N(1), op=ALU.mult), reads=["nat9", "nat1"], writes=["nat11"])
        kb.op("dve", lambda e: e.tensor_tensor(out=N(10), in0=N(10), in1=N(11), op=ALU.add), reads=["nat10", "nat11"], writes=["nat10"])
        kb.op("dve", lambda e: e.tensor_tensor(out=N(10), in0=N(10), in1=N(12), op=ALU.mult), reads=["nat10", "nat12"], writes=["nat10"])
        kb.op("dve", lambda e: e.tensor_tensor(out=N(11), in0=N(9), in1=N(0), op=ALU.mult), reads=["nat9", "nat0"], writes=["nat11"])
        kb.op("dve", lambda e: e.tensor_tensor(out=N(13), in0=N(13), in1=N(1), op=ALU.mult), reads=["nat13", "nat1"], writes=["nat13"])
        kb.op("dve", lambda e: e.tensor_tensor(out=N(11), in0=N(11), in1=N(13), op=ALU.subtract), reads=["nat11", "nat13"], writes=["nat11"])
        kb.op("dve", lambda e: e.tensor_tensor(out=N(11), in0=N(11), in1=N(12), op=ALU.mult), reads=["nat11", "nat12"], writes=["nat11"])
        p = self.rr("ptr", 2)
        pt, kpt = self.ptr[p], "ptr%d" % p
        for si, ni in enumerate((4, 5, 8, 9, 10, 11)):
            kb.op("pe", lambda e, si=si, ni=ni, pt=pt: e.transpose(out=pt[:, si * 32:(si + 1) * 32], in_=nat[:, ni, :],
                                                                  identity=self.ident[:32, :32]),
                  reads=["nat%d" % ni, "ident"], writes=[kpt])
        kb.op("dve", lambda e, pt=pt: e.tensor_copy(out=PT[:, :, :], in_=pt[:, 0:192].rearrange("p (s c) -> p s c", s=6)),
              reads=[kpt], writes=["PT"])
        bc = lambda si: PT[:, si, :].unsqueeze(2).to_broadcast([128, 32, 16])
        kb.op("dve", lambda e: e.tensor_tensor(out=bbr[:], in0=BR[:], in1=bc(4), op=ALU.mult), reads=["BR", "PT"], writes=["bbr"])
        kb.op("dve", lambda e: e.tensor_tensor(out=btmp, in0=BI[:], in1=bc(5), op=ALU.mult), reads=["BI", "PT"], writes=["xin"])
        kb.op("dve", lambda e: e.tensor_tensor(out=bbr[:], in0=bbr[:], in1=btmp, op=ALU.subtract), reads=["bbr", "xin"], writes=["bbr"])
        kb.op("dve", lambda e: e.tensor_tensor(out=bbi[:], in0=BI[:], in1=bc(4), op=ALU.mult), reads=["BI", "PT"], writes=["bbi"])
        kb.op("dve", lambda e: e.tensor_tensor(out=btmp, in0=BR[:], in1=bc(5), op=ALU.mult), reads=["BR", "PT", "bbr"], writes=["xin"])
        kb.op("dve", lambda e: e.tensor_tensor(out=bbi[:], in0=bbi[:], in1=btmp, op=ALU.add), reads=["bbi", "xin"], writes=["bbi"])
        for ri, (bb, kbb) in enumerate(((bbr, "bbr"), (bbi, "bbi"))):
            for oc in range(8):
                kb.op("dve", lambda e, bb=bb, oc=oc: e.tensor_tensor(
                    out=xin[:, :].rearrange("p (q g c) -> p q g c", q=4, g=8),
                    in0=bb[:, oc * 4:oc * 4 + 4, None, :].to_broadcast([128, 4, 8, 16]),
                    in1=m4[:, :, :, None].to_broadcast([128, 4, 8, 16]), op=ALU.mult),
                    reads=[kbb, "m4"], writes=["xin"])
                p = self.rr("ptr", 2)
                pt, kpt = self.ptr[p], "ptr%d" % p
                for q in range(4):
                    kb.op("pe", lambda e, q=q, pt=pt: e.transpose(out=pt[:, q * 128:(q + 1) * 128], in_=xin[:, q * 128:(q + 1) * 128],
                                                                  identity=self.ident[:, :]),
                          reads=["xin", "ident"], writes=[kpt])
                kb.op("act", lambda e, pt=pt, oc=oc, ri=ri: e.activation(
                    out=Bz[ri][:, oc * 4:oc * 4 + 4, :], in_=pt[:, :].rearrange("p (q m) -> p q m", q=4), func=AF.Copy),
                    reads=[kpt], writes=[("Bz", ri)])
        for ri in range(2):
            for oc in range(8):
                kb.op("dve", lambda e, ri=ri, oc=oc: e.tensor_tensor(
                    out=xin[:, :].rearrange("p (q g s) -> p q g s", q=4, g=2),
                    in0=Cn[ri][:, oc, None, None, :].to_broadcast([128, 4, 2, 64]),
                    in1=m3[:, :, :, None].to_broadcast([128, 4, 2, 64]), op=ALU.mult),
                    reads=["Cn%d" % ri, "m3"], writes=["xin"])
                p = self.rr("ptr", 2)
                pt, kpt = self.ptr[p], "ptr%d" % p
                for q in range(4):
                    kb.op("pe", lambda e, q=q, pt=pt: e.transpose(out=pt[:, q * 128:(q + 1) * 128], in_=xin[:, q * 128:(q + 1) * 128],
                                                                  identity=self.ident[:, :]),
                          reads=["xin", "ident"], writes=[kpt])
                kb.op("act", lambda e, pt=pt, oc=oc, ri=ri: e.activation(
                    out=Cz[ri][:, oc * 4:oc * 4 + 4, :], in_=pt[:, :].rearrange("p (q m) -> p q m", q=4), func=AF.Copy,
                    scale=(1.0 if ri == 0 else -1.0)),
                    reads=[kpt], writes=[("Cz", ri)])
        for g in range(16):
            for s8 in range(2):
                sc = g * 2 + s8
                kb.op("dve", lambda e, s8=s8, sc=sc: e.tensor_scalar(out=ang[:, s8, :], in0=iota1[:, :], scalar1=PT[:, 1, sc:sc + 1],
                                                                     scalar2=None, op0=ALU.mult),
                      reads=["iota1", "PT"], writes=["ang"])
            self.sin_to("ang", sinT[:, g * 2:(g + 1) * 2, :], ang[:], angf[:], angi[:], angm[:], ("sinT", g), "ang")
            self.sin_to("ang", cosT[:, g * 2:(g + 1) * 2, :], ang[:], angf[:], angi[:], angm[:], ("cosT", g), "ang",
                        shift=math.pi / 2)
        tabs = [("sinT", g) for g in range(16)] + [("cosT", g) for g in range(16)]
        kb.op("dve", lambda e: e.memset(car[0][:], 0.0), writes=["car0"])
        kb.op("dve", lambda e: e.memset(car[1][:], 0.0), writes=["car1"])

        if NS:
            for ri, nm in enumerate(("state_s5_re", "state_s5_im")):
                p = self.rr("ptr", 2)
                pt, kpt = self.ptr[p], "ptr%d" % p
                for qq in range(8):
                    kb.dma("sp", sst[:NS, :], I[nm][l].rearrange("b g p -> b (g p)")[:, qq * 512:(qq + 1) * 512], writes=["sst"])
                    for s8 in range(4):
                        sc = qq * 4 + s8
                        kb.op("pe", lambda e, sc=sc, s8=s8, pt=pt: e.transpose(out=pt[:, sc * NS:(sc + 1) * NS], in_=sst[:NS, s8 * 128:(s8 + 1) * 128],
                                                                      identity=self.ident[:NS, :NS]),
                              reads=["sst", "ident"], writes=[kpt])
                kb.op("dve", lambda e, pt=pt, ri=ri: e.tensor_copy(out=s0T[ri][:, :, :NS],
                                                                  in_=pt[:, :32 * NS].rearrange("p (s b) -> p s b", s=32)),
                      reads=[kpt], writes=[("s0T", ri)])

        subblocks = []
        for bi, (t0b, nb) in enumerate(self.blocks):
            for t0 in range(t0b, t0b + nb, 256):
                subblocks.append((bi, t0, min(256, t0b + nb - t0)))
        for (bi, t0, n) in subblocks:
            is_s = (t0 >= T)
            nsub = n // 128
            for oc in range(8):
                j = self.rr("uf", 2)
                kuf, kub = "uf%d" % j, "ub%d" % j
                row = O_U + oc * 128
                kb.dma("sp", uf[j][:, :n], S["zT"][row:row + 128, t0:t0 + n], reads=[("zT", row, bi)], writes=[kuf])
                kb.op("pool", lambda e, j=j: e.tensor_copy(out=ub[j][:, :n], in_=uf[j][:, :n]), reads=[kuf], writes=[kub])
                py = self.rr("ptr", 2)
                pys, kpys = self.ptr[py], "ptr%d" % py
                for q in range(4):
                    sc = oc * 4 + q
                    pa = self.rr("pp", 4); pb = self.rr("pp", 4)
                    A, B = self.pp[pa], self.pp[pb]
                    kA, kB = "pp%d" % pa, "pp%d" % pb
                    kb.op("pe", lambda e, A=A, sc=sc, j=j: e.matmul(A[:, :n], lhsT=Bz[0][:, sc, :], rhs=ub[j][:, :n], start=True, stop=True),
                          reads=[("Bz", 0), kub], writes=[kA])
                    kb.op("pe", lambda e, B=B, sc=sc, j=j: e.matmul(B[:, :n], lhsT=Bz[1][:, sc, :], rhs=ub[j][:, :n], start=True, stop=True),
                          reads=[("Bz", 1), kub], writes=[kB])
                    if not is_s:
                        v3 = lambda x: x[:, :n].rearrange("p (m j) -> p m j", j=128)
                        cosv = cosT[:, sc, None, :].to_broadcast([128, nsub, 128])
                        sinv = sinT[:, sc, None, :].to_broadcast([128, nsub, 128])
                        kb.op("dve", lambda e, A=A, cosv=cosv: e.tensor_tensor(out=v3(tq[0]), in0=v3(A), in1=cosv, op=ALU.mult),
                              reads=[kA] + tabs, writes=["tq0"])
                        kb.op("dve", lambda e, B=B, sinv=sinv: e.tensor_tensor(out=v3(tq[1]), in0=v3(B), in1=sinv, op=ALU.mult),
                              reads=[kB] + tabs, writes=["tq1"])
                        kb.op("dve", lambda e, B=B, cosv=cosv: e.tensor_tensor(out=v3(tq[2]), in0=v3(B), in1=cosv, op=ALU.mult),
                              reads=[kB] + tabs, writes=["tq2"])
                        kb.op("dve", lambda e, A=A, sinv=sinv: e.tensor_tensor(out=v3(tq[3]), in0=v3(A), in1=sinv, op=ALU.mult),
                              reads=[kA] + tabs, writes=["tq3"])
                        kb.op("pool", lambda e: e.tensor_tensor(out=bh[0][:, :n], in0=tq[0][:, :n], in1=tq[1][:, :n], op=ALU.add),
                              reads=["tq0", "tq1"], writes=["bh0"])
                        kb.op("pool", lambda e: e.tensor_tensor(out=bh[1][:, :n], in0=tq[2][:, :n], in1=tq[3][:, :n], op=ALU.subtract),
                              reads=["tq2", "tq3"], writes=["bh1"])
                        rho = PT[:, 0, sc:sc + 1].to_broadcast([128, 128])
                        c128 = cosT[:, sc, 127:128]
                        s128 = sinT[:, sc, 127:128]
                        for m in range(nsub):
                            sl = slice(m * 128, (m + 1) * 128)
                            for ri in range(2):
                                kb.op("dve", lambda e, ri=ri, sl=sl, sc=sc, rho=rho: e.tensor_tensor_scan(
                                    out=sh[ri][:, sl], data0=rho, data1=bh[ri][:, sl], initial=car[ri][:, sc:sc + 1],
                                    op0=ALU.mult, op1=ALU.add), reads=["bh%d" % ri, "car%d" % ri, "PT"], writes=["sh%d" % ri])
                            lr = sh[0][:, m * 128 + 127:m * 128 + 128]
                            li = sh[1][:, m * 128 + 127:m * 128 + 128]
                            kb.op("dve", lambda e, li=li, s128=s128: e.tensor_scalar(out=ctmp[:, 0:1], in0=li, scalar1=s128, scalar2=None, op0=ALU.mult),
                                  reads=["sh1"] + tabs, writes=["ctmp"])
                            kb.op("dve", lambda e, li=li, c128=c128: e.tensor_scalar(out=ctmp[:, 1:2], in0=li, scalar1=c128, scalar2=None, op0=ALU.mult),
                                  reads=["sh1"] + tabs, writes=["ctmp"])
                            kb.op("dve", lambda e, lr=lr, c128=c128, sc=sc: e.scalar_tensor_tensor(
                                out=car[0][:, sc:sc + 1], in0=lr, scalar=c128, in1=ctmp[:, 0:1], op0=ALU.mult, op1=ALU.subtract),
                                reads=["sh0", "ctmp"] + tabs, writes=["car0"])
                            kb.op("dve", lambda e, lr=lr, s128=s128, sc=sc: e.scalar_tensor_tensor(
                                out=car[1][:, sc:sc + 1], in0=lr, scalar=s128, in1=ctmp[:, 1:2], op0=ALU.mult, op1=ALU.add),
                                reads=["sh0", "ctmp"] + tabs, writes=["car1"])
                        kb.op("pool", lambda e, cosv=cosv: e.tensor_tensor(out=v3(tq[0]), in0=v3(sh[0]), in1=cosv, op=ALU.mult),
                              reads=["sh0"] + tabs, writes=["tq0"])
                        kb.op("pool", lambda e, sinv=sinv: e.tensor_tensor(out=v3(tq[1]), in0=v3(sh[1]), in1=sinv, op=ALU.mult),
                              reads=["sh1"] + tabs, writes=["tq1"])
                        kb.op("dve", lambda e, sinv=sinv: e.tensor_tensor(out=v3(tq[2]), in0=v3(sh[0]), in1=sinv, op=ALU.mult),
                              reads=["sh0"] + tabs, writes=["tq2"])
                        kb.op("dve", lambda e, cosv=cosv: e.tensor_tensor(out=v3(tq[3]), in0=v3(sh[1]), in1=cosv, op=ALU.mult),
                              reads=["sh1"] + tabs, writes=["tq3"])
                        kb.op("pool", lambda e: e.tensor_tensor(out=sbf[0][:, :n], in0=tq[0][:, :n], in1=tq[1][:, :n], op=ALU.subtract),
                              reads=["tq0", "tq1"], writes=["sbf0"])
                        kb.op("dve", lambda e: e.tensor_tensor(out=sbf[1][:, :n], in0=tq[2][:, :n], in1=tq[3][:, :n], op=ALU.add),
                              reads=["tq2", "tq3"], writes=["sbf1"])
                    else:
                        lbre = PT[:, 2, sc:sc + 1]
                        lbim = PT[:, 3, sc:sc + 1]
                        s0r, s0i = s0T[0][:, sc, :n], s0T[1][:, sc, :n]
                        kb.op("dve", lambda e, s0i=s0i, lbim=lbim: e.tensor_scalar(out=tq[0][:, :n], in0=s0i, scalar1=lbim, scalar2=None, op0=ALU.mult),
                              reads=[("s0T", 1), "PT"], writes=["tq0"])
                        kb.op("dve", lambda e, s0r=s0r, lbre=lbre: e.scalar_tensor_tensor(out=tq[0][:, :n], in0=s0r, scalar=lbre, in1=tq[0][:, :n],
                                                                               op0=ALU.mult, op1=ALU.subtract),
                              reads=[("s0T", 0), "PT", "tq0"], writes=["tq0"])
                        kb.op("dve", lambda e, A=A, sc=sc: e.tensor_tensor(out=snw[0][:, sc, :n], in0=A[:, :n], in1=tq[0][:, :n], op=ALU.add),
                              reads=[kA, "tq0"], writes=[("snw", 0)])
                        kb.op("dve", lambda e, s0r=s0r, lbim=lbim: e.tensor_scalar(out=tq[1][:, :n], in0=s0r, scalar1=lbim, scalar2=None, op0=ALU.mult),
                              reads=[("s0T", 0), "PT"], writes=["tq1"])
                        kb.op("dve", lambda e, s0i=s0i, lbre=lbre: e.scalar_tensor_tensor(out=tq[1][:, :n], in0=s0i, scalar=lbre, in1=tq[1][:, :n],
                                                                               op0=ALU.mult, op1=ALU.add),
                              reads=[("s0T", 1), "PT", "tq1"], writes=["tq1"])
                        kb.op("dve", lambda e, B=B, sc=sc: e.tensor_tensor(out=snw[1][:, sc, :n], in0=B[:, :n], in1=tq[1][:, :n], op=ALU.add),
                              reads=[kB, "tq1"], writes=[("snw", 1)])
                        for ri in range(2):
                            kb.op("pool", lambda e, ri=ri, sc=sc: e.tensor_copy(out=sbf[ri][:, :n], in_=snw[ri][:, sc, :n]),
                                  reads=[("snw", ri)], writes=["sbf%d" % ri])
                    for ri in range(2):
                        kb.op("pe", lambda e, ri=ri, sc=sc, q=q, pys=pys: e.matmul(pys[:, :n], lhsT=Cz[ri][:, sc, :], rhs=sbf[ri][:, :n],
                                                                                start=(q == 0 and ri == 0), stop=(q == 3 and ri == 1)),
                              reads=[("Cz", ri), "sbf%d" % ri], writes=[kpys])
                y0, y1 = yt[0], yt[1]
                kb.op("dve", lambda e, j=j, oc=oc, pys=pys: e.scalar_tensor_tensor(out=y0[:, :n], in0=uf[j][:, :n], scalar=dcol[:, oc:oc + 1],
                                                                             in1=pys[:, :n], op0=ALU.mult, op1=ALU.add),
                      reads=[kuf, "dcol", kpys], writes=["yt0"])
                kb.op("act", lambda e: e.activation(out=y1[:, :n], in_=y0[:, :n], func=AF.Square), reads=["yt0"], writes=["yt1"])
                kb.op("dve", lambda e: e.tensor_scalar(out=y1[:, :n], in0=y1[:, :n], scalar1=0.044715, scalar2=1.0, op0=ALU.mult, op1=ALU.add),
                      reads=["yt1"], writes=["yt1"])
                kb.op("dve", lambda e: e.tensor_tensor(out=y1[:, :n], in0=y1[:, :n], in1=y0[:, :n], op=ALU.mult), reads=["yt1", "yt0"], writes=["yt1"])
                kb.op("act", lambda e: e.activation(out=y1[:, :n], in_=y1[:, :n], func=AF.Sigmoid, scale=2.0 * math.sqrt(2.0 / math.pi)),
                      reads=["yt1"], writes=["yt1"])
                kb.op("dve", lambda e, oc=oc: e.tensor_tensor(out=ysg[:, oc, :n], in0=y1[:, :n], in1=y0[:, :n], op=ALU.mult),
                      reads=["yt1", "yt0"], writes=[("ysg", oc)])
                kb.op("pool", lambda e, oc=oc: e.tensor_copy(out=ysb[:, oc, :n], in_=ysg[:, oc, :n]), reads=[("ysg", oc)], writes=[("ysb", oc)])
            for ec in range(8):
                p = self.rr("pp", 4)
                pp, kpp = self.pp[p], "pp%d" % p
                for kc in range(KC):
                    kb.op("pe", lambda e, kc=kc, ec=ec, pp=pp: e.matmul(pp[:, :n], lhsT=wg[:, kc, ec * 128:(ec + 1) * 128], rhs=ysb[:, kc, :n],
                                                                      start=(kc == 0), stop=(kc == KC - 1)),
                          reads=["wg"] + [("ysb", k) for k in range(KC)], writes=[kpp])
                zj = self.rr("zst", 2)
                kz = "zst%d" % zj
                row = O_ZS + ec * 128
                kb.dma("sp", zst[zj][:, :n], S["zT"][row:row + 128, t0:t0 + n], reads=[("zT", row, bi)], writes=[kz])
                kb.op("act", lambda e, zj=zj: e.activation(out=zst[zj][:, :n], in_=zst[zj][:, :n], func=AF.Silu), reads=[kz], writes=[kz])
                kb.op("act", lambda e, pp=pp, ec=ec: e.activation(out=yt[0][:, :n], in_=pp[:, :n], func=AF.Sigmoid, bias=gbcol[:, ec:ec + 1]),
                      reads=[kpp, "gbcol"], writes=["yt0"])
                kb.op("dve", lambda e, ec=ec: e.tensor_tensor(out=yt[0][:, :n], in0=yt[0][:, :n], in1=ysg[:, ec, :n], op=ALU.mult),
                      reads=["yt0", ("ysg", ec)], writes=["yt0"])
                aj = self.rr("aout", 2)
                kb.op("dve", lambda e, aj=aj, zj=zj: e.tensor_tensor(out=aout[aj][:, :n], in0=yt[0][:, :n], in1=zst[zj][:, :n], op=ALU.mult),
                      reads=["yt0", kz], writes=["aout%d" % aj])
                kb.dma("pool", S["act_s"][ec * 128:(ec + 1) * 128, t0:t0 + n], aout[aj][:, :n],
                       reads=["aout%d" % aj], writes=[("act", "s", bi)])
        for ri, nm in enumerate(("s5_re", "s5_im")):
            p = self.rr("ptr", 2)
            pt, kpt = self.ptr[p], "ptr%d" % p
            kb.op("pe", lambda e, pt=pt, ri=ri: e.transpose(out=pt[:32, 0:128], in_=car[ri][:, :], identity=self.ident[:, :]),
                  reads=["car%d" % ri, "ident"], writes=[kpt])
            kb.op("dve", lambda e, pt=pt: e.tensor_copy(out=sst[:32, 0:128], in_=pt[:32, 0:128]), reads=[kpt], writes=["sst"])
            kb.dma("pool", O[nm + "_prompt"][l].rearrange("(s q) -> s q", q=128), sst[:32, 0:128], reads=["sst"], writes=[],
                   is_output=True)
            if NS:
                for g in range(8):
                    p = self.rr("ptr", 2)
                    pt, kpt = self.ptr[p], "ptr%d" % p
                    for s4 in range(4):
                        sc = g * 4 + s4
                        kb.op("pe", lambda e, pt=pt, ri=ri, sc=sc, s4=s4: e.transpose(out=pt[:NS, s4 * 128:(s4 + 1) * 128], in_=snw[ri][:, sc, :NS],
                                                                                   identity=self.ident[:, :]),
                              reads=[("snw", ri), "ident"], writes=[kpt])
                    kb.op("dve", lambda e, pt=pt, g=g: e.tensor_copy(out=sst[:NS, 0:512], in_=pt[:NS, :]), reads=[kpt], writes=["sst"])
                    kb.dma("pool", O[nm + "_sample"][l].rearrange("b g p -> b (g p)")[:, g * 512:(g + 1) * 512], sst[:NS, 0:512],
                           reads=["sst"], writes=[], is_output=True)
        self.end_phase()

    def load_sq(self, name, dram):
        kb = self.kb
        dst = self.wsq[name]
        for h in range(2):
            i = self.rr("w", 2)
            ws, kws = self.wst[i], "wst%d" % i
            kb.dma("sp", ws[:, :, :512], dram[:, h * 512:(h + 1) * 512].rearrange("(k p) c -> p k c", p=128),
                   writes=[kws])
            kb.op("pool", lambda e, ws=ws, h=h: e.tensor_copy(out=dst[:, :, h * 512:(h + 1) * 512], in_=ws[:, :, :512]),
                  reads=[kws], writes=[("wsq", name)])

    def phase3(self, l):
        kb = self.kb
        I, S = self.I, self.S
        last = (l == self.DEPTH - 1)
        self.begin_phase()
        self.wst = [self.sbp("wst%d" % i, (128, KC, 512)) for i in range(2)]
        self.wsq = {n: self.sbp("wsq_" + n, (128, KC, D), BF16) for n in ("bm", "br", "bs", "out")}
        self.actb = [self.sbp("actb%d" % i, (128, KC, 512), BF16) for i in range(2)]
        self.gmt = [self.sbp("gmt%d" % i, (128, 512)) for i in range(2)]
        self.mrg = self.sbp("mrg", (128, KC, 512), BF16)
        self.macc = self.sbp("macc", (128, KC, 512))
        self.ln_alloc()
        for nme in ("bm", "br", "bs", "out"):
            self.load_sq(nme, I["w_" + nme][l])
        self.load_ln_params(I["ln_g"][l], I["ln_b"][l])
        for bi, (t0, n) in enumerate(self.blocks):
            for ec in range(KC):
                for b, bn in enumerate("mrs"):
                    pass
            acts = {}
            for b, bn in enumerate("mrs"):
                if not self.have_branch(bn):
                    continue
                j = self.rr("actb", 2)
                at, kat = self.actb[j], "actb%d" % j
                kb.dma("sp", at[:, :, :n], S["act_" + bn][:, t0:t0 + n].rearrange("(k p) t -> p k t", p=128),
                       reads=[("act", bn, bi)], writes=[kat])
                for ec in range(KC):
                    p = self.rr("pp", 4)
                    pp, kpp = self.pp[p], "pp%d" % p
                    for kc in range(KC):
                        kb.op("pe", lambda e, kc=kc, pp=pp, ec=ec, at=at, bn=bn: e.matmul(
                            pp[:, :n], lhsT=self.wsq["b" + bn][:, kc, ec * 128:(ec + 1) * 128], rhs=at[:, kc, :n],
                            start=(kc == 0), stop=(kc == KC - 1)),
                            reads=[kat, ("wsq", "b" + bn)], writes=[kpp])
                    g = self.rr("gmt", 2)
                    gt, kgt = self.gmt[g], "gmt%d" % g
                    row = O_GM + b * D + ec * 128
                    kb.dma("sp", gt[:, :n], S["zT"][row:row + 128, t0:t0 + n],
                           reads=[("zT", row, bi)], writes=[kgt])
                    kb.op("act", lambda e, gt=gt: e.activation(out=gt[:, :n], in_=gt[:, :n], func=AF.Sigmoid),
                          reads=[kgt], writes=[kgt])
                    first = (bn == self.first_branch())
                    lastb = (bn == self.last_branch())
                    kmf = ("mrgf", ec)
                    acc = self.macc[:, ec, :n]
                    if first:
                        kb.op("dve", lambda e, acc=acc, gt=gt, pp=pp: e.tensor_tensor(out=acc, in0=pp[:, :n], in1=gt[:, :n],
                                                                                 op=ALU.mult),
                              reads=[kpp, kgt], writes=[("macc", ec)])
                    else:
                        kb.op("dve", lambda e, gt=gt, pp=pp: e.tensor_tensor(out=gt[:, :n], in0=pp[:, :n], in1=gt[:, :n],
                                                                        op=ALU.mult),
                              reads=[kpp, kgt], writes=[kgt])
                        kb.op("pool", lambda e, acc=acc, gt=gt: e.tensor_tensor(out=acc, in0=acc, in1=gt[:, :n], op=ALU.add),
                              reads=[kgt, ("macc", ec)], writes=[("macc", ec)])
                    if lastb:
                        kb.op("pool", lambda e, acc=acc, ec=ec: e.tensor_copy(out=self.mrg[:, ec, :n], in_=acc),
                              reads=[("macc", ec)], writes=[("mrg", ec)])
            for tt in range(0, n, 128):
                nr = min(128, n - tt)
                r0 = t0 + tt
                ti = r0 // 128
                j = self.rr("xt", 2)
                xt, kxt = self.xt[j], "xt%d" % j
                kb.dma("sp", xt[:nr, :], S["xs"][r0:r0 + nr, :], reads=[("xs", ti)], writes=[kxt])
                if self.first_branch() is not None:
                    for h in range(2):
                        p = self.rr("pp", 4)
                        pp, kpp = self.pp[p], "pp%d" % p
                        for kc in range(KC):
                            kb.op("pe", lambda e, kc=kc, pp=pp, h=h, tt=tt, nr=nr: e.matmul(
                                pp[:nr, :], lhsT=self.mrg[:, kc, tt:tt + nr], rhs=self.wsq["out"][:, kc, h * 512:(h + 1) * 512],
                                start=(kc == 0), stop=(kc == KC - 1)),
                                reads=[("mrg", k) for k in range(KC)] + [("wsq", "out")], writes=[kpp])
                        kb.op("dve", lambda e, xt=xt, pp=pp, h=h, nr=nr: e.scalar_tensor_tensor(
                            out=xt[:nr, h * 512:(h + 1) * 512], in0=xt[:nr, h * 512:(h + 1) * 512], scalar=ALPHA,
                            in1=pp[:nr, :], op0=ALU.mult, op1=ALU.add), reads=[kxt, kpp], writes=[kxt])
                else:
                    kb.op("dve", lambda e, xt=xt, nr=nr: e.tensor_scalar(out=xt[:nr, :], in0=xt[:nr, :], scalar1=ALPHA,
                                                                         scalar2=None, op0=ALU.mult),
                          reads=[kxt], writes=[kxt])
                fo = None
                if last:
                    fo = self.O["y_prompt"][r0:r0 + nr, :] if r0 < self.T else self.O["y_sample"][:, :]
                self.ln_tile(xt, kxt, r0, nr, ti, S["xs"][r0:r0 + nr, :], final_out=fo)
        self.end_phase()

    branches = ""

    def have_branch(self, bn):
        return bn in self.branches

    def first_branch(self):
        return self.branches[0] if self.branches else None

    def last_branch(self):
        return self.branches[-1] if self.branches else None


def make_in_map(inputs, c, NS, T, consts):
    m = {}
    m["x_prompt"] = np.ascontiguousarray(inputs["x_prompt"][c, :T])
    m["x_sample"] = np.ascontiguousarray(inputs["x_sample"][c * NS:(c + 1) * NS, 0])
    for n in ("ln_in_g", "ln_in_b", "w_in", "w_out", "ln_g", "ln_b", "w_bm", "w_br", "w_bs", "s_lam_re", "s_lam_im",
              "s_log_dt", "s_b_re", "s_b_im", "s_c_re", "s_c_im", "s_d", "s_glu_w", "s_glu_b",
              "m_conv_w", "m_conv_b", "m_wq", "m_wk", "m_wv", "m_ig_b", "m_fg_b", "m_norm_g", "m_skip",
              "r_mu", "r_w0", "r_w2", "r_a0", "r_a2", "r_k_k", "r_k_a", "r_r_k", "r_ln_g", "r_ln_b"):
        m[n] = np.ascontiguousarray(inputs[n])
    for n in ("state_s5_re", "state_s5_im", "state_mlstm_conv", "state_mlstm_c", "state_mlstm_n", "state_mlstm_m",
              "state_rwkv_wkv", "state_rwkv_shift"):
        m[n] = np.ascontiguousarray(inputs[n][:, c * NS:(c + 1) * NS])
    m.update(consts)
    return m


def kernel(**inputs):
    inputs = {k: np.asarray(v) for k, v in inputs.items()}
    T, NS, L = 2048, 16, 4
    Prog.branches = BRANCHES
    prog = Prog(T, NS, L)
    nc = prog.build()
    consts = host_consts()
    in_maps = [make_in_map(inputs, c, NS, T, consts) for c in range(8)]
    res = run_bass_kernel_spmd(nc, in_maps, core_ids=list(range(8)))
    rs = res.results
    f = np.float32

    def pstack(name, shape):
        if name in rs[0]:
            return np.stack([np.asarray(rs[c][name]).reshape((L,) + shape) for c in range(8)], 1).astype(f)
        return np.zeros((L, 8) + shape, f)

    def sstack(name, shape):
        if name in rs[0]:
            return np.concatenate([np.asarray(rs[c][name]).reshape((L, NS) + shape) for c in range(8)], 1).astype(f)
        return np.zeros((L, 8 * NS) + shape, f)

    y_prompt = np.stack([rs[c]["y_prompt"] for c in range(8)], 0).astype(f)
    y_sample = np.concatenate([rs[c]["y_sample"] for c in range(8)], 0)[:, None, :].astype(f)
    return (y_prompt, y_sample,
            pstack("c_prompt", (4, 256, 256)), sstack("c_sample", (4, 256, 256)),
            pstack("n_prompt", (4, 256)), sstack("n_sample", (4, 256)),
            pstack("m_prompt", (4,)), sstack("m_sample", (4,)),
            pstack("conv_prompt", (3, D)), sstack("conv_sample", (3, D)),
            pstack("wkv_prompt", (16, 64, 64)), sstack("wkv_sample", (16, 64, 64)),
            pstack("shift_prompt", (R_SHIFT_W,)), sstack("shift_sample", (R_SHIFT_W,)),
            pstack("s5_re_prompt", (64, 64)), sstack("s5_re_sample", (64, 64)),
            pstack("s5_im_prompt", (64, 64)), sstack("s5_im_sample", (64, 64)))
```

```python
import math
from contextlib import ExitStack
import numpy as np
import concourse.bass as bass
import concourse.mybir as mybir
from concourse.bass_utils import run_bass_kernel_spmd

F32 = mybir.dt.float32
BF16 = mybir.dt.bfloat16
I32 = mybir.dt.int32
ALU = mybir.AluOpType
AF = mybir.ActivationFunctionType
AX = mybir.AxisListType

D = 1024
KC = 8
N_IN = 12424
M_HEADS = 4
M_HD = 256
R_HEADS = 16
R_HD = 64
R_SHIFT_W = 3200
S_GROUPS = 64
S_STATE = 64
DEPTH_FULL = 4
ALPHA = (2.0 * DEPTH_FULL) ** 0.25
LN_EPS = 1e-5
RWKV_GN_EPS = 64e-5
NEG = -1e30

O_XM, O_IG, O_FG, O_OG, O_ZM = 0, 1024, 1028, 1032, 2056
O_RC, O_ZR, O_U, O_ZS, O_GM = 3080, 6280, 7304, 8328, 9352


class KB:
    def __init__(self, nc, es):
        self.nc = nc
        self.eng = dict(pe=nc.tensor, act=nc.scalar, dve=nc.vector, pool=nc.gpsimd, sp=nc.sync)
        self.stream = {e: [] for e in self.eng}
        self.csem = {e: es.enter_context(nc.semaphore("c_" + e)) for e in ("pe", "act", "dve", "pool")}
        self.ccnt = {e: 0 for e in self.csem}
        self.ring = {}
        for q, n in (("sp", 24), ("pool", 12), ("act", 6)):
            self.ring[q] = [[es.enter_context(nc.semaphore("d_%s%d" % (q, i))), 0] for i in range(n)]
        self.rpos = {q: 0 for q in self.ring}
        self.known = {e: {} for e in self.eng}
        self.lastw = {}
        self.readers = {}
        self.out_tokens = []
        self.n_ops = 0

    def _need(self, e, tok, same_ok=False):
        if tok is None:
            return
        sem, val, owner = tok
        if owner == e and (e == "pe" or same_ok):
            return
        k = id(sem)
        if self.known[e].get(k, 0) >= val:
            return
        self.known[e][k] = val
        self.eng[e].wait_ge(sem, val)

    def _deps(self, e, reads, writes):
        for b in reads:
            self._need(e, self.lastw.get(b))
        for b in writes:
            self._need(e, self.lastw.get(b))
            for tok in self.readers.get(b, ()):
                self._need(e, tok)

    def _commit(self, tok, reads, writes):
        for b in writes:
            self.lastw[b] = tok
            self.readers[b] = []
        for b in reads:
            self.readers.setdefault(b, []).append(tok)

    def op(self, e, fn, reads=(), writes=()):
        self._deps(e, reads, writes)
        self.ccnt[e] += 1
        tok = (self.csem[e], self.ccnt[e], e)
        fn(self.eng[e]).then_inc(self.csem[e], 1)
        self._commit(tok, reads, writes)
        self.n_ops += 1

    def dma(self, q, out, in_, reads=(), writes=(), is_output=False, **kw):
        self._deps(q, reads, writes)
        ring = self.ring[q]
        slot = ring[self.rpos[q] % len(ring)]
        self.rpos[q] += 1
        sem = slot[0]
        if slot[1] > 0:
            self._need(q, (sem, slot[1], None))
        slot[1] += 16
        tok = (sem, slot[1], None)
        self.eng[q].dma_start(out=out, in_=in_, **kw).then_inc(sem, 16)
        self._commit(tok, reads, writes)
        if is_output:
            self.out_tokens.append(tok)
        self.n_ops += 1

    def finish(self):
        for q in self.ring:
            for sem, val in self.ring[q]:
                if val > 0:
                    self._need("sp", (sem, val, None))
        for e in self.csem:
            if self.ccnt[e] > 0:
                self._need("sp", (self.csem[e], self.ccnt[e], e))

    def barrier(self):
        for e in self.eng:
            for q in self.ring:
                for sem, val in self.ring[q]:
                    if val > 0:
                        self._need(e, (sem, val, None))
            for e2 in self.csem:
                if self.ccnt[e2] > 0 and e2 != e:
                    self._need(e, (self.csem[e2], self.ccnt[e2], e2))

    def emit(self):
        pass


BRANCHES = "mrs"


def host_consts():
    c = {}
    c["ident"] = np.eye(128, dtype=np.float32)
    m3 = np.zeros((128, 4, 2), np.float32)
    m4 = np.zeros((128, 4, 8), np.float32)
    for p in range(128):
        g8 = p // 16
        gl = p // 64
        for q in range(4):
            for g in range(2):
                if g8 == 2 * q + g:
                    m3[p, q, g] = 1.0
            for gg in range(8):
                if gg == 2 * q + gl:
                    m4[p, q, gg] = 1.0
    c["mask3"], c["mask4"] = m3, m4
    selh = np.zeros((4, 4, 128), np.float32)
    for h in range(4):
        selh[h, h, :] = 1.0
    c["selh"] = selh
    c["ones4"] = np.ones((4, 128), np.float32)
    cn = np.zeros((128, 128), np.float32)
    for s_ in range(128):
        cn[s_, :s_] = 1.0e30
    c["causneg"] = cn
    rm = np.zeros((128, 384), np.float32)
    for p in range(128):
        for q in range(128):
            if p // 64 == q // 64:
                s_, t_ = p % 64, q % 64
                rm[p, q] = 1.0 if s_ < t_ else 0.0
                rm[p, 128 + q] = 1.0 if s_ <= t_ else 0.0
                rm[p, 256 + q] = 1.0 if s_ > t_ else 0.0
    c["rmasks"] = rm
    c["iota1"] = np.tile(np.arange(1, 129, dtype=np.float32)[None, :], (128, 1))
    return c


class Prog:
    def __init__(self, T, NS, DEPTH):
        self.T, self.NS, self.DEPTH = T, NS, DEPTH
        self.NT = T + NS
        assert T % 128 == 0
        self.ntile = T // 128
        self.tiles = [(i * 128, 128) for i in range(self.ntile)] + [(T, NS)]
        self.blocks = []
        t = 0
        while t < T:
            n = min(512, T - t)
            self.blocks.append((t, n))
            t += n
        self.blocks.append((T, NS))

    def build(self):
        nc = bass.Bass("TRN2", target_bir_lowering=False)
        self.nc = nc
        T, NS, L, NT = self.T, self.NS, self.DEPTH, self.NT
        dt = nc.dram_tensor

        def inp(name, shape, dtype=F32):
            return dt(name, list(shape), dtype, kind="ExternalInput").ap()

        def outp(name, shape):
            return dt(name, list(shape), F32, kind="ExternalOutput").ap()

        def scr(name, shape, dtype=F32):
            return dt(name, list(shape), dtype, kind="Internal").ap()

        I = {}
        I["x_prompt"] = inp("x_prompt", (T, D))
        I["x_sample"] = inp("x_sample", (NS, D))
        I["ident"] = inp("ident", (128, 128))
        for n, s in (("ln_in_g", (D,)), ("ln_in_b", (D,)), ("w_in", (L, D, N_IN)), ("w_out", (L, D, D)),
                     ("ln_g", (L, D)), ("ln_b", (L, D)), ("w_bm", (L, D, D)), ("w_br", (L, D, D)),
                     ("w_bs", (L, D, D)), ("s_lam_re", (L, 64, 64)), ("s_lam_im", (L, 64, 64)), ("s_log_dt", (L, 64)),
                     ("s_b_re", (L, 64, 64, 16)), ("s_b_im", (L, 64, 64, 16)), ("s_c_re", (L, 64, 16, 64)),
                     ("s_c_im", (L, 64, 16, 64)), ("s_d", (L, D)), ("s_glu_w", (L, D, D)), ("s_glu_b", (L, D)),
                     ("state_s5_re", (L, NS, 64, 64)), ("state_s5_im", (L, NS, 64, 64)),
                     ("state_mlstm_conv", (L, NS, 3, D)), ("state_mlstm_c", (L, NS, 4, 256, 256)),
                     ("state_mlstm_n", (L, NS, 4, 256)), ("state_mlstm_m", (L, NS, 4)),
                     ("m_conv_w", (L, 4, D)), ("m_conv_b", (L, D)), ("m_wq", (L, 4, 256, 256)), ("m_wk", (L, 4, 256, 256)),
                     ("m_wv", (L, 4, 256, 256)), ("m_ig_b", (L, 4)), ("m_fg_b", (L, 4)), ("m_norm_g", (L, D)), ("m_skip", (L, D)),
                     ("state_rwkv_wkv", (L, NS, 16, 64, 64)), ("state_rwkv_shift", (L, NS, R_SHIFT_W)),
                     ("r_mu", (L, R_SHIFT_W)), ("r_w0", (L, D)), ("r_w2", (L, 64, D)), ("r_a0", (L, D)), ("r_a2", (L, 64, D)),
                     ("r_k_k", (L, D)), ("r_k_a", (L, D)), ("r_r_k", (L, 16, 64)), ("r_ln_g", (L, D)), ("r_ln_b", (L, D)),
                     ("rmasks", (128, 384)),
                     ("selh", (4, 4, 128)), ("causneg", (128, 128)), ("ones4", (4, 128)),
                     ("mask3", (128, 4, 2)), ("mask4", (128, 4, 8)), ("iota1", (128, 128))):
            I[n] = inp(n, s)
        self.I = I
        O = {}
        O["y_prompt"] = outp("y_prompt", (T, D))
        O["y_sample"] = outp("y_sample", (NS, D))
        O["c_prompt"] = outp("c_prompt", (L, 4, 256, 256))
        O["c_sample"] = outp("c_sample", (L, NS, 4, 256, 256))
        O["n_prompt"] = outp("n_prompt", (L, 4, 256))
        O["n_sample"] = outp("n_sample", (L, NS, 4, 256))
        O["m_prompt"] = outp("m_prompt", (L, 4))
        O["m_sample"] = outp("m_sample", (L, NS, 4))
        O["conv_prompt"] = outp("conv_prompt", (L, 3, D))
        O["conv_sample"] = outp("conv_sample", (L, NS, 3, D))
        O["wkv_prompt"] = outp("wkv_prompt", (L, 16, 64, 64))
        O["wkv_sample"] = outp("wkv_sample", (L, NS, 16, 64, 64))
        O["shift_prompt"] = outp("shift_prompt", (L, R_SHIFT_W))
        O["shift_sample"] = outp("shift_sample", (L, NS, R_SHIFT_W))
        for nm in ("s5_re", "s5_im"):
            O[nm + "_prompt"] = outp(nm + "_prompt", (L, 4096))
            O[nm + "_sample"] = outp(nm + "_sample", (L, NS, 64, 64))
        self.O = O
        S = {}
        S["xs"] = scr("xs", (NT, D))
        S["z"] = scr("z", (NT, N_IN))
        S["zT"] = scr("zT", (N_IN, NT))
        for b in "mrs":
            S["act_" + b] = scr("act_" + b, (D, NT), BF16)
        S["vs"] = scr("vs", (NT, D))
        S["ys"] = scr("ys", (NT, D))
        self.S = S

        with ExitStack() as es:
            self.es = es
            kb = KB(nc, es)
            self.kb = kb
            self.alloc()
            self.phase0()
            for l in range(L):
                self.phase1(l)
                self.easy_states(l)
                if self.have_branch("m"):
                    self.mlstm_phase(l)
                if self.have_branch("r"):
                    self.rwkv_phase(l)
                if self.have_branch("s"):
                    self.s5_phase(l)
                self.phase3(l)
            kb.finish()
            kb.emit()
        return nc

    def sb(self, name, shape, dtype=F32):
        return self.es.enter_context(self.nc.sbuf_tensor("sb_" + name, list(shape), dtype))

    def sbp(self, name, shape, dtype=F32):
        self.uid = getattr(self, "uid", 0) + 1
        return self.pes.enter_context(self.nc.sbuf_tensor("sp%d_%s" % (self.uid, name), list(shape), dtype))

    def begin_phase(self):
        self.pes = ExitStack()

    def end_phase(self):
        self.kb.barrier()
        self.pes.close()

    def ps(self, name, shape, dtype=F32):
        return self.es.enter_context(self.nc.psum_tensor("ps_" + name, list(shape), dtype))

    def alloc(self):
        NT = self.NT
        self.ident = self.sb("ident", (128, 128))
        self.xT = self.sb("xT", (128, KC, NT), BF16)
        self.pp = [self.ps("pp%d" % i, (128, 512)) for i in range(4)]
        self.ptr = [self.ps("ptr%d" % i, (128, 512)) for i in range(2)]
        self.cnt = {}
        kb = self.kb
        kb.dma("sp", self.ident[:], self.I["ident"], writes=["ident"])

    def rr(self, key, n):
        v = self.cnt.get(key, 0)
        self.cnt[key] = v + 1
        return v % n

    def ln_alloc(self):
        self.gbc = self.sbp("gbc", (128, D))
        self.bbc = self.sbp("bbc", (128, D))
        self.xt = [self.sbp("xt%d" % i, (128, D)) for i in range(2)]
        self.xc = [self.sbp("xc%d" % i, (128, D)) for i in range(2)]
        self.st = [self.sbp("st%d" % i, (128, 8)) for i in range(2)]

    def ln_tile(self, src, srckey, row0, nr, ti, xs_out, final_out=None):
        kb = self.kb
        i = self.rr("ln", 2)
        xc, st = self.xc[i], self.st[i]
        kxc, kst = "xc%d" % i, "st%d" % i
        kb.op("dve", lambda e: e.tensor_reduce(out=st[:nr, 0:1], in_=src[:nr, :], axis=AX.X, op=ALU.add),
              reads=[srckey], writes=[kst])
        kb.op("dve", lambda e: e.tensor_scalar(out=st[:nr, 1:2], in0=st[:nr, 0:1], scalar1=-1.0 / D, scalar2=None,
                                               op0=ALU.mult), reads=[kst], writes=[kst])
        kb.op("dve", lambda e: e.tensor_scalar(out=xc[:nr, :], in0=src[:nr, :], scalar1=st[:nr, 1:2], scalar2=None,
                                               op0=ALU.add), reads=[srckey, kst], writes=[kxc])
        j = self.rr("tmpsq", 2)
        sq = self.xt[j]
        ksq = "xt%d" % j
        kb.op("act", lambda e: e.activation(out=sq[:nr, :], in_=xc[:nr, :], func=AF.Square, accum_out=st[:nr, 2:3]),
              reads=[kxc], writes=[ksq, kst])
        kb.op("act", lambda e: e.activation(out=st[:nr, 3:4], in_=st[:nr, 2:3], func=AF.Sqrt, scale=1.0 / D,
                                            bias=self.epsc[:nr, 0:1]), reads=[kst, "epsc"], writes=[kst])
        kb.op("dve", lambda e: e.reciprocal(out=st[:nr, 4:5], in_=st[:nr, 3:4]), reads=[kst], writes=[kst])
        kb.op("dve", lambda e: e.scalar_tensor_tensor(out=xc[:nr, :], in0=xc[:nr, :], scalar=st[:nr, 4:5],
                                                      in1=self.gbc[:nr, :], op0=ALU.mult, op1=ALU.mult),
              reads=[kxc, kst, "gbc"], writes=[kxc])
        kb.op("dve", lambda e: e.tensor_tensor(out=xc[:nr, :], in0=xc[:nr, :], in1=self.bbc[:nr, :], op=ALU.add),
              reads=[kxc, "bbc"], writes=[kxc])
        kb.dma("pool", xs_out, xc[:nr, :], reads=[kxc], writes=[("xs", ti)])
        if final_out is not None:
            kb.dma("pool", final_out, xc[:nr, :], reads=[kxc], writes=[], is_output=True)
        for half in range(2):
            p = self.rr("ptr", 2)
            pt, kpt = self.ptr[p], "ptr%d" % p
            for k4 in range(4):
                kc = half * 4 + k4
                kb.op("pe", lambda e, kc=kc, k4=k4, pt=pt: e.transpose(out=pt[:, k4 * 128:k4 * 128 + nr],
                                                                      in_=xc[:nr, kc * 128:(kc + 1) * 128],
                                                                      identity=self.ident[:nr, :nr]),
                      reads=[kxc, "ident"], writes=[kpt])
            dst = self.xT[:, half * 4:half * 4 + 4, row0:row0 + nr]
            srcp = pt[:, :].rearrange("p (k t) -> p k t", k=4)[:, :, :nr]
            eng = "act" if half == 0 else "dve"
            if eng == "act":
                kb.op("act", lambda e, dst=dst, srcp=srcp: e.activation(out=dst, in_=srcp, func=AF.Copy),
                      reads=[kpt], writes=[("xT", ti)])
            else:
                kb.op("dve", lambda e, dst=dst, srcp=srcp: e.tensor_copy(out=dst, in_=srcp),
                      reads=[kpt], writes=[("xT", ti)])

    def load_ln_params(self, g_ap, b_ap):
        kb = self.kb
        kb.dma("sp", self.gbc[:], g_ap.partition_broadcast(128), writes=["gbc"])
        kb.dma("sp", self.bbc[:], b_ap.partition_broadcast(128), writes=["bbc"])

    def phase0(self):
        kb = self.kb
        self.epsc = self.sb("epsc", (128, 1))
        kb.op("dve", lambda e: e.memset(self.epsc[:], LN_EPS), writes=["epsc"])
        self.onesf = self.sb("onesf", (128, 64))
        kb.op("dve", lambda e: e.memset(self.onesf[:], 1.0), writes=["onesf"])
        self.begin_phase()
        self.ln_alloc()
        self.load_ln_params(self.I["ln_in_g"], self.I["ln_in_b"])
        for ti, (r0, nr) in enumerate(self.tiles):
            j = self.rr("xt", 2)
            xt, kxt = self.xt[j], "xt%d" % j
            src = self.I["x_prompt"][r0:r0 + nr, :] if r0 < self.T else self.I["x_sample"][:, :]
            kb.dma("sp", xt[:nr, :], src, writes=[kxt])
            self.ln_tile(xt, kxt, r0, nr, ti, self.S["xs"][r0:r0 + nr, :])
        self.end_phase()

    def load_w(self, dram_cols, width):
        kb = self.kb
        i = self.rr("w", 2)
        ws, wb = self.wst[i], self.wbf[i]
        kws, kwb = "wst%d" % i, "wbf%d" % i
        kb.dma("sp", ws[:, :, :width], dram_cols.rearrange("(k p) c -> p k c", p=128), writes=[kws])
        kb.op("pool", lambda e: e.tensor_copy(out=wb[:, :, :width], in_=ws[:, :, :width]), reads=[kws], writes=[kwb])
        return wb, kwb

    def evac(self, dst, src, reads, writes):
        kb = self.kb
        if self.rr("evac", 2) == 0:
            kb.op("act", lambda e: e.activation(out=dst, in_=src, func=AF.Copy), reads=reads, writes=writes)
        else:
            kb.op("dve", lambda e: e.tensor_copy(out=dst, in_=src), reads=reads, writes=writes)

    def phase1(self, l):
        kb = self.kb
        self.begin_phase()
        self.wst = [self.sbp("wst%d" % i, (128, KC, 512)) for i in range(2)]
        self.wbf = [self.sbp("wbf%d" % i, (128, KC, 512), BF16) for i in range(2)]
        self.ev = [self.sbp("ev%d" % i, (128, 512)) for i in range(4)]
        W = self.I["w_in"][l]
        tm_segs = [(O_OG, O_ZM), (O_RC, O_U)]
        fm_segs = [(O_XM, O_IG), (O_IG, O_FG), (O_FG, O_OG), (O_ZM, O_RC), (O_U, O_ZS), (O_ZS, O_GM), (O_GM, N_IN)]
        for (c0, c1) in tm_segs:
            c = c0
            while c < c1:
                w = min(512, c1 - c)
                wb, kwb = self.load_w(W[:, c:c + w], w)
                for ti, (r0, nr) in enumerate(self.tiles):
                    p = self.rr("pp", 4)
                    pp, kpp = self.pp[p], "pp%d" % p
                    for kc in range(KC):
                        kb.op("pe", lambda e, kc=kc, pp=pp, r0=r0, nr=nr, wb=wb, w=w: e.matmul(
                            pp[:nr, :w], lhsT=self.xT[:, kc, r0:r0 + nr], rhs=wb[:, kc, :w],
                            start=(kc == 0), stop=(kc == KC - 1)),
                            reads=[("xT", ti), kwb], writes=[kpp])
                    v = self.rr("ev", 4)
                    ev, kev = self.ev[v], "ev%d" % v
                    self.evac(ev[:nr, :w], pp[:nr, :w], [kpp], [kev])
                    kb.dma("pool", self.S["z"][r0:r0 + nr, c:c + w], ev[:nr, :w], reads=[kev],
                           writes=[("z", ti)])
                c += w
        for (c0, c1) in fm_segs:
            c = c0
            while c < c1:
                w = min(512, c1 - c)
                wb, kwb = self.load_w(W[:, c:c + w], w)
                for s0 in range(0, w, 128):
                    m = min(128, w - s0)
                    for bi, (t0, n) in enumerate(self.blocks):
                        p = self.rr("pp", 4)
                        pp, kpp = self.pp[p], "pp%d" % p
                        tis = list(range(t0 // 128, (t0 + n + 127) // 128))
                        for kc in range(KC):
                            kb.op("pe", lambda e, kc=kc, pp=pp, t0=t0, n=n, wb=wb, s0=s0, m=m: e.matmul(
                                pp[:m, :n], lhsT=wb[:, kc, s0:s0 + m], rhs=self.xT[:, kc, t0:t0 + n],
                                start=(kc == 0), stop=(kc == KC - 1)),
                                reads=[("xT", t) for t in tis] + [kwb], writes=[kpp])
                        v = self.rr("ev", 4)
                        ev, kev = self.ev[v], "ev%d" % v
                        self.evac(ev[:m, :n], pp[:m, :n], [kpp], [kev])
                        kb.dma("pool", self.S["zT"][c + s0:c + s0 + m, t0:t0 + n], ev[:m, :n], reads=[kev],
                               writes=[("zT", c + s0, bi)])
                c += w
        self.end_phase()

    def easy_states(self, l):
        kb = self.kb
        I, S, O = self.I, self.S, self.O
        T, NS = self.T, self.NS
        nb = len(self.blocks)
        zt_all = [("zT", r, b) for r in range(0, 1024, 128) for b in range(nb)]
        z_all = [("z", ti) for ti in range(len(self.tiles))]
        kb.dma("pool", O["conv_prompt"][l].rearrange("j c -> c j"), S["zT"][0:D, T - 3:T], reads=zt_all, writes=[],
               is_output=True, allow_slow_non_contiguous=True)
        kb.dma("pool", O["shift_prompt"][l:l + 1, :], S["z"][T - 1:T, O_RC:O_RC + R_SHIFT_W], reads=z_all, writes=[], is_output=True)
        if NS:
            kb.dma("pool", O["conv_sample"][l][:, 0:2, :], I["state_mlstm_conv"][l][:, 1:3, :], writes=[], is_output=True)
            for hh in range(4):
                kb.dma("pool", O["conv_sample"][l][:, 2, hh * 256:(hh + 1) * 256].rearrange("b c -> c b"),
                       S["zT"][hh * 256:(hh + 1) * 256, T:T + NS], reads=zt_all, writes=[],
                       is_output=True, allow_slow_non_contiguous=True)
            kb.dma("pool", O["shift_sample"][l], S["z"][T:T + NS, O_RC:O_RC + R_SHIFT_W], reads=z_all, writes=[], is_output=True)

    def mlstm_phase(self, l):
        kb = self.kb
        I, S, O = self.I, self.S, self.O
        T, NS, NT = self.T, self.NS, self.NT
        self.begin_phase()
        sbp = self.sbp
        Wst = sbp("Wst", (128, 4, 2, 256))
        Wq = sbp("Wq", (128, 4, 2, 256), BF16); Wk = sbp("Wk", (128, 4, 2, 256), BF16); Wv = sbp("Wv", (128, 4, 2, 256), BF16)
        cw = sbp("cw", (128, 4, 8)); cb = sbp("cb", (128, 8)); mg = sbp("mg", (128, 8)); msk = sbp("msk", (128, 8))
        gb = sbp("gb", (4, 2))
        igA = sbp("igA", (4, NT)); lfA = sbp("lfA", (4, NT))
        selh = sbp("selh", (4, 4, 128)); causneg = sbp("causneg", (128, 128)); ones4 = sbp("ones4", (4, 128))
        Cst = [[sbp("C%d_%d" % (s_, h), (128, 2, 257)) for h in range(4)] for s_ in range(2)]
        mprev = sbp("mprev", (4, 2))
        xext = [sbp("xext%d" % i, (128, 8, 131)) for i in range(2)]
        ctmp = sbp("cvtmp", (128, 8, 128)); cacc = sbp("cvacc", (128, 8, 128))
        xcT = sbp("xcT", (128, 8, 128)); xcb = sbp("xcb", (128, 8, 128), BF16); xmb = sbp("xmb", (128, 8, 128), BF16)
        qTb = sbp("qTb", (128, 4, 2, 128)); kTb = sbp("kTb", (128, 4, 2, 128))
        kw = sbp("kw", (128, 256)); vaug = sbp("vaug", (128, 257))
        G = sbp("G", (4, 8, 128)); gsm = sbp("gsm", (4, 8)); dg = sbp("dg", (4, 4))
        gc = sbp("gc", (128, 16)); dbc = sbp("dbc", (128, 4))
        DT = sbp("DT", (128, 128)); Stl = sbp("Stl", (128, 128)); mmb = sbp("mmb", (128, 4, 128))
        Asb = sbp("Asb", (128, 257)); nd = sbp("nd", (128, 257)); dsm = sbp("dsm", (128, 4))
        sog = [sbp("sog%d" % i, (128, D)) for i in range(2)]
        hm = sbp("hm", (128, D)); hst = sbp("hst", (128, 16))
        hT = sbp("hT", (128, 8, 128)); zmt = [sbp("zmt%d" % i, (128, 8, 128)) for i in range(2)]
        aob = [sbp("aob%d" % i, (128, 8, 128), BF16) for i in range(2)]
        cvst = sbp("cvst", (48, D)); convT = sbp("convT", (128, 8, 48))

        for (nm, dst, sc_) in (("m_wq", Wq, 1.0 / 16.0), ("m_wk", Wk, 1.0), ("m_wv", Wv, 1.0)):
            kb.dma("sp", Wst[:], I[nm][l].rearrange("h (c p) e -> p h c e", p=128), writes=["Wst"])
            kb.op("act", lambda e, dst=dst, sc_=sc_: e.activation(out=dst[:], in_=Wst[:], func=AF.Copy, scale=sc_),
                  reads=["Wst"], writes=[nm])
        sl = dict(allow_slow_non_contiguous=True)
        kb.dma("sp", cw[:], I["m_conv_w"][l].rearrange("j (k p) -> p j k", p=128), writes=["cw"], **sl)
        kb.dma("sp", cb[:], I["m_conv_b"][l].rearrange("(k p) -> p k", p=128), writes=["cb"], **sl)
        kb.dma("sp", mg[:], I["m_norm_g"][l].rearrange("(k p) -> p k", p=128), writes=["mg"], **sl)
        kb.dma("sp", msk[:], I["m_skip"][l].rearrange("(k p) -> p k", p=128), writes=["msk"], **sl)
        kb.dma("sp", gb[:, 0:1], I["m_ig_b"][l].rearrange("(h o) -> h o", o=1), writes=["gb"], **sl)
        kb.dma("sp", gb[:, 1:2], I["m_fg_b"][l].rearrange("(h o) -> h o", o=1), writes=["gb"], **sl)
        kb.dma("sp", selh[:], I["selh"], writes=["selh"])
        kb.dma("sp", causneg[:], I["causneg"], writes=["causneg"])
        kb.dma("sp", ones4[:], I["ones4"], writes=["ones4"])
        nb = len(self.blocks)
        kb.dma("sp", igA[:], S["zT"][O_IG:O_IG + 4, :], reads=[("zT", O_IG, b) for b in range(nb)], writes=["igA"])
        kb.dma("sp", lfA[:], S["zT"][O_FG:O_FG + 4, :], reads=[("zT", O_FG, b) for b in range(nb)], writes=["lfA"])
        kb.op("dve", lambda e: e.tensor_scalar(out=igA[:], in0=igA[:], scalar1=gb[:, 0:1], scalar2=None, op0=ALU.add),
              reads=["igA", "gb"], writes=["igA"])
        kb.op("dve", lambda e: e.tensor_scalar(out=lfA[:], in0=lfA[:], scalar1=gb[:, 1:2], scalar2=-1.0, op0=ALU.add, op1=ALU.mult),
              reads=["lfA", "gb"], writes=["lfA"])
        kb.op("act", lambda e: e.activation(out=lfA[:], in_=lfA[:], func=AF.Exp), reads=["lfA"], writes=["lfA"])
        kb.op("dve", lambda e: e.tensor_scalar(out=lfA[:], in0=lfA[:], scalar1=1.0, scalar2=None, op0=ALU.add), reads=["lfA"], writes=["lfA"])
        kb.op("act", lambda e: e.activation(out=lfA[:], in_=lfA[:], func=AF.Ln), reads=["lfA"], writes=["lfA"])
        kb.op("dve", lambda e: e.tensor_scalar(out=lfA[:], in0=lfA[:], scalar1=-1.0, scalar2=None, op0=ALU.mult), reads=["lfA"], writes=["lfA"])
        if NS:
            kb.dma("sp", cvst[:3 * NS, :], I["state_mlstm_conv"][l].rearrange("b j c -> (b j) c"), writes=["cvst"])
            for half in range(2):
                p = self.rr("ptr", 2)
                pt, kpt = self.ptr[p], "ptr%d" % p
                for k4 in range(4):
                    kc = half * 4 + k4
                    kb.op("pe", lambda e, pt=pt, k4=k4, kc=kc: e.transpose(out=pt[:, k4 * 48:k4 * 48 + 3 * NS], in_=cvst[:3 * NS, kc * 128:(kc + 1) * 128],
                                                                       identity=self.ident[:3 * NS, :3 * NS]),
                          reads=["cvst", "ident"], writes=[kpt])
                kb.op("dve", lambda e, pt=pt, half=half: e.tensor_copy(out=convT[:, half * 4:half * 4 + 4, :3 * NS],
                                                                     in_=pt[:, 0:192].rearrange("p (k c) -> p k c", k=4)[:, :, :3 * NS]),
                      reads=[kpt], writes=["convT"])

        zt_x = lambda bi: [("zT", r, bi) for r in range(0, 1024, 128)]
        zt_z = lambda bi: [("zT", O_ZM + r, bi) for r in range(0, 1024, 128)]
        blk_of = lambda t: next(i for i, (b0, bn) in enumerate(self.blocks) if b0 <= t < b0 + bn)

        chunks = [(c * 128, 128, None) for c in range(T // 128)] + [(T + b, 1, b) for b in range(NS)]
        for ci, (t0, L, sb_) in enumerate(chunks):
            bi = blk_of(t0)
            ti = t0 // 128
            cs = self.rr("Cset", 2) if sb_ is not None else 0
            C = Cst[cs]
            kC = [("C", cs, h) for h in range(4)]
            if ci == 0:
                for h in range(4):
                    kb.op("pool", lambda e, h=h: e.memset(C[h][:], 0.0), writes=[kC[h]])
                kb.op("dve", lambda e: e.memset(mprev[:, 0:1], NEG), writes=["mprev"])
            if sb_ is not None:
                for h in range(4):
                    kb.dma("sp", C[h][:, :, 0:256], I["state_mlstm_c"][l, sb_, h].rearrange("(c p) v -> p c v", p=128), writes=[kC[h]])
                    kb.dma("sp", C[h][:, :, 256:257], I["state_mlstm_n"][l, sb_, h].rearrange("(c p o) -> p c o", p=128, o=1), writes=[kC[h]], **sl)
                kb.dma("sp", mprev[:, 0:1], I["state_mlstm_m"][l, sb_].rearrange("(h o) -> h o", o=1), writes=["mprev"], **sl)
            xj = self.rr("xext", 2)
            xe, kxe = xext[xj], "xext%d" % xj
            if sb_ is None:
                if t0 == 0:
                    kb.op("pool", lambda e, xe=xe: e.memset(xe[:, :, 0:3], 0.0), writes=[kxe])
                    kb.dma("sp", xe[:, :, 3:3 + L], S["zT"][0:D, t0:t0 + L].rearrange("(k p) t -> p k t", p=128), reads=zt_x(bi), writes=[kxe])
                else:
                    rd = zt_x(bi) + (zt_x(blk_of(t0 - 3)) if blk_of(t0 - 3) != bi else [])
                    kb.dma("sp", xe[:, :, 0:3 + L], S["zT"][0:D, t0 - 3:t0 + L].rearrange("(k p) t -> p k t", p=128), reads=rd, writes=[kxe])
            else:
                kb.op("pool", lambda e, xe=xe, sb_=sb_: e.tensor_copy(out=xe[:, :, 0:3], in_=convT[:, :, 3 * sb_:3 * sb_ + 3]), reads=["convT"], writes=[kxe])
                kb.dma("sp", xe[:, :, 3:4], S["zT"][0:D, t0:t0 + 1].rearrange("(k p) t -> p k t", p=128), reads=zt_x(bi), writes=[kxe], **sl)
            V = lambda x: x[:, :, :L]
            for j in range(4):
                dst = cacc if j == 0 else ctmp
                kd = "cvacc" if j == 0 else "cvtmp"
                kb.op("dve", lambda e, j=j, dst=dst, xe=xe: e.tensor_tensor(out=V(dst), in0=xe[:, :, j:j + L],
                                                                         in1=cw[:, j, :, None].to_broadcast([128, 8, L]), op=ALU.mult),
                      reads=[kxe, "cw"], writes=[kd])
                if j > 0:
                    kb.op("pool", lambda e: e.tensor_tensor(out=V(cacc), in0=V(cacc), in1=V(ctmp), op=ALU.add), reads=["cvacc", "cvtmp"], writes=["cvacc"])
            kb.op("dve", lambda e: e.tensor_tensor(out=V(cacc), in0=V(cacc), in1=cb[:, :, None].to_broadcast([128, 8, L]), op=ALU.add),
                  reads=["cvacc", "cb"], writes=["cvacc"])
            kb.op("act", lambda e: e.activation(out=V(xcT), in_=V(cacc), func=AF.Silu), reads=["cvacc"], writes=["xcT"])
            kb.op("pool", lambda e: e.tensor_copy(out=V(xcb), in_=V(xcT)), reads=["xcT"], writes=["xcb"])
            kb.op("pool", lambda e, xe=xe: e.tensor_copy(out=V(xmb), in_=xe[:, :, 3:3 + L]), reads=[kxe], writes=["xmb"])
            Gr = lambda i: G[:, i, :L]
            kb.op("dve", lambda e: e.tensor_tensor_scan(out=Gr(0), data0=ones4[:, :L], data1=lfA[:, t0:t0 + L], initial=0.0, op0=ALU.mult, op1=ALU.add),
                  reads=["ones4", "lfA"], writes=["G0"])
            kb.op("dve", lambda e: e.tensor_tensor(out=Gr(3), in0=igA[:, t0:t0 + L], in1=Gr(0), op=ALU.subtract), reads=["igA", "G0"], writes=["G3"])
            kb.op("dve", lambda e: e.tensor_tensor_scan(out=Gr(1), data0=Gr(3), data1=Gr(3), initial=-3.0e38, op0=ALU.max, op1=ALU.max),
                  reads=["G3"], writes=["G1"])
            kb.op("dve", lambda e: e.tensor_scalar(out=Gr(1), in0=Gr(1), scalar1=mprev[:, 0:1], scalar2=None, op0=ALU.max), reads=["G1", "mprev"], writes=["G1"])
            kb.op("dve", lambda e: e.tensor_scalar(out=gsm[:, 0:1], in0=G[:, 1, L - 1:L], scalar1=-1.0, scalar2=None, op0=ALU.mult), reads=["G1"], writes=["gsm"])
            kb.op("act", lambda e: e.activation(out=Gr(4), in_=Gr(1), func=AF.Exp, scale=-1.0, bias=mprev[:, 0:1]), reads=["G1", "mprev"], writes=["G4"])
            kb.op("dve", lambda e: e.tensor_tensor(out=Gr(5), in0=Gr(0), in1=Gr(1), op=ALU.add), reads=["G0", "G1"], writes=["G5"])
            kb.op("act", lambda e: e.activation(out=Gr(5), in_=Gr(5), func=AF.Exp, scale=-1.0), reads=["G5"], writes=["G5"])
            kb.op("act", lambda e: e.activation(out=Gr(6), in_=Gr(3), func=AF.Exp, bias=gsm[:, 0:1]), reads=["G3", "gsm"], writes=["G6"])
            kb.op("dve", lambda e: e.tensor_tensor(out=mprev[:, 1:2], in0=G[:, 0, L - 1:L], in1=G[:, 1, L - 1:L], op=ALU.add), reads=["G0", "G1"], writes=["mnew"])
            p = self.rr("ptr", 2)
            pt, kpt = self.ptr[p], "ptr%d" % p
            for ki, gi in enumerate((4, 5, 3, 6)):
                kb.op("pe", lambda e, ki=ki, gi=gi, pt=pt: e.transpose(out=pt[:L, ki * 4:ki * 4 + 4], in_=G[:, gi, :L], identity=self.ident[:4, :4]),
                      reads=["G%d" % gi, "ident"], writes=[kpt])
            kb.op("dve", lambda e, pt=pt: e.tensor_copy(out=gc[:L, :], in_=pt[:L, 0:16]), reads=[kpt], writes=["gc"])
            kb.op("dve", lambda e: e.tensor_scalar(out=dg[:, :], in0=self.ident[:4, :4], scalar1=G[:, 4, L - 1:L], scalar2=None, op0=ALU.mult),
                  reads=["G4", "ident"], writes=["dg"])
            p2 = self.rr("ptr", 2)
            pt2, kpt2 = self.ptr[p2], "ptr%d" % p2
            kb.op("pe", lambda e, pt2=pt2: e.matmul(pt2[:, 0:4], lhsT=ones4[:, :], rhs=dg[:, :], start=True, stop=True), reads=["ones4", "dg"], writes=[kpt2])
            kb.op("dve", lambda e, pt2=pt2: e.tensor_copy(out=dbc[:, :], in_=pt2[:, 0:4]), reads=[kpt2], writes=["dbc"])
            pnn = self.rr("pp", 4)
            pn, kpn = self.pp[pnn], "pp%d" % pnn
            for h in range(4):
                kb.op("pe", lambda e, h=h, pn=pn: e.matmul(pn[:L, h * 128:h * 128 + L], lhsT=selh[:, h, :L], rhs=G[:, 1, :L], start=True, stop=True),
                      reads=["selh", "G1"], writes=[kpn])
            kb.op("act", lambda e, pn=pn: e.activation(out=mmb[:L, :, :L], in_=pn[:L, :].rearrange("p (h t) -> p h t", h=4)[:, :, :L], func=AF.Copy),
                  reads=[kpn], writes=["mmb"])
            sj = self.rr("sog", 2)
            so, kso = sog[sj], "sog%d" % sj
            kb.dma("sp", so[:L, :], S["z"][t0:t0 + L, O_OG:O_OG + D], reads=[("z", ti)], writes=[kso])
            kb.op("act", lambda e, so=so: e.activation(out=so[:L, :], in_=so[:L, :], func=AF.Sigmoid), reads=[kso], writes=[kso])
            for h in range(4):
                for (Wt, wn, dstT, kd) in ((Wq, "m_wq", qTb, "qTb"), (Wk, "m_wk", kTb, "kTb")):
                    pq = self.rr("pp", 4)
                    ppq, kpq = self.pp[pq], "pp%d" % pq
                    for ec in range(2):
                        for dc in range(2):
                            kb.op("pe", lambda e, h=h, ec=ec, dc=dc, Wt=Wt, ppq=ppq: e.matmul(
                                ppq[:, ec * 128:ec * 128 + L], lhsT=Wt[:, h, dc, ec * 128:(ec + 1) * 128], rhs=xcb[:, 2 * h + dc, :L],
                                start=(dc == 0), stop=(dc == 1)), reads=[wn, "xcb"], writes=[kpq])
                    self.evac(dstT[:, h, :, :L], ppq[:, 0:256].rearrange("p (c t) -> p c t", c=2)[:, :, :L], [kpq], [(kd, h)])
            for h in range(4):
                pk = self.rr("pp", 4)
                ppk, kpk = self.pp[pk], "pp%d" % pk
                for dc in range(2):
                    kb.op("pe", lambda e, h=h, dc=dc, ppk=ppk: e.matmul(ppk[:L, 0:256], lhsT=xcb[:, 2 * h + dc, :L], rhs=Wk[:, h, dc, :],
                                                                      start=(dc == 0), stop=(dc == 1)), reads=["m_wk", "xcb"], writes=[kpk])
                kb.op("act", lambda e, h=h, ppk=ppk: e.activation(out=kw[:L, :], in_=ppk[:L, 0:256], func=AF.Copy, scale=gc[:L, 12 + h:13 + h]),
                      reads=[kpk, "gc"], writes=["kw"])
                pv = self.rr("pp", 4)
                ppv, kpv = self.pp[pv], "pp%d" % pv
                for dc in range(2):
                    kb.op("pe", lambda e, h=h, dc=dc, ppv=ppv: e.matmul(ppv[:L, 0:256], lhsT=xmb[:, 2 * h + dc, :L], rhs=Wv[:, h, dc, :],
                                                                      start=(dc == 0), stop=(dc == 1)), reads=["m_wv", "xmb"], writes=[kpv])
                kb.op("dve", lambda e, ppv=ppv: e.tensor_copy(out=vaug[:L, 0:256], in_=ppv[:L, 0:256]), reads=[kpv], writes=["vaug"])
                kb.op("dve", lambda e: e.memset(vaug[:L, 256:257], 1.0), writes=["vaug"])
                kb.op("dve", lambda e, h=h: e.tensor_tensor(out=DT[:L, :L], in0=mmb[:L, h, :L], in1=causneg[:L, :L], op=ALU.add),
                      reads=["mmb", "causneg"], writes=["DT"])
                kb.op("act", lambda e, h=h: e.activation(out=DT[:L, :L], in_=DT[:L, :L], func=AF.Exp, scale=-1.0, bias=gc[:L, 8 + h:9 + h]),
                      reads=["DT", "gc"], writes=["DT"])
                ps_ = self.rr("pp", 4)
                pps, kps = self.pp[ps_], "pp%d" % ps_
                for ec in range(2):
                    kb.op("pe", lambda e, h=h, ec=ec, pps=pps: e.matmul(pps[:L, :L], lhsT=kTb[:, h, ec, :L], rhs=qTb[:, h, ec, :L],
                                                                      start=(ec == 0), stop=(ec == 1)), reads=[("kTb", h), ("qTb", h)], writes=[kps])
                kb.op("dve", lambda e, pps=pps: e.tensor_tensor(out=Stl[:L, :L], in0=pps[:L, :L], in1=DT[:L, :L], op=ALU.mult), reads=[kps, "DT"], writes=["Stl"])
                pa = self.rr("pp", 4)
                ppa, kpa = self.pp[pa], "pp%d" % pa
                kb.op("pe", lambda e, ppa=ppa: e.matmul(ppa[:L, 0:257], lhsT=Stl[:L, :L], rhs=vaug[:L, :], start=True, stop=True), reads=["Stl", "vaug"], writes=[kpa])
                pb = self.rr("ptr", 2)
                ppb, kpb = self.ptr[pb], "ptr%d" % pb
                for ec in range(2):
                    kb.op("pe", lambda e, h=h, ec=ec, ppb=ppb, C=C: e.matmul(ppb[:L, 0:257], lhsT=qTb[:, h, ec, :L], rhs=C[h][:, ec, :],
                                                                      start=(ec == 0), stop=(ec == 1)), reads=[("qTb", h), kC[h]], writes=[kpb])
                kb.op("act", lambda e, ppa=ppa: e.activation(out=Asb[:L, :], in_=ppa[:L, 0:257], func=AF.Copy), reads=[kpa], writes=["Asb"])
                kb.op("dve", lambda e, h=h, ppb=ppb: e.scalar_tensor_tensor(out=nd[:L, :], in0=ppb[:L, 0:257], scalar=gc[:L, h:h + 1], in1=Asb[:L, :],
                                                                        op0=ALU.mult, op1=ALU.add), reads=[kpb, "gc", "Asb"], writes=["nd"])
                kb.op("act", lambda e: e.activation(out=dsm[:L, 2:3], in_=nd[:L, 256:257], func=AF.Abs), reads=["nd"], writes=["dsm"])
                kb.op("dve", lambda e, h=h: e.tensor_scalar(out=dsm[:L, 0:1], in0=dsm[:L, 2:3], scalar1=gc[:L, 4 + h:5 + h], scalar2=None,
                                                            op0=ALU.max), reads=["dsm", "gc"], writes=["dsm"])
                kb.op("dve", lambda e: e.reciprocal(out=dsm[:L, 1:2], in_=dsm[:L, 0:1]), reads=["dsm"], writes=["dsm"])
                kb.op("dve", lambda e, h=h, so=so: e.scalar_tensor_tensor(out=hm[:L, h * 256:(h + 1) * 256], in0=nd[:L, 0:256], scalar=dsm[:L, 1:2],
                                                                      in1=so[:L, h * 256:(h + 1) * 256], op0=ALU.mult, op1=ALU.mult),
                      reads=["nd", "dsm", kso], writes=[("hm", h)])
                for ec in range(2):
                    pu = self.rr("pp", 4)
                    ppu, kpu = self.pp[pu], "pp%d" % pu
                    kb.op("pe", lambda e, ec=ec, ppu=ppu: e.matmul(ppu[:, 0:257], lhsT=kw[:L, ec * 128:(ec + 1) * 128], rhs=vaug[:L, :], start=True, stop=True),
                          reads=["kw", "vaug"], writes=[kpu])
                    kb.op("dve", lambda e, h=h, ec=ec, ppu=ppu, C=C: e.scalar_tensor_tensor(out=C[h][:, ec, :], in0=C[h][:, ec, :], scalar=dbc[:, h:h + 1],
                                                                                      in1=ppu[:, 0:257], op0=ALU.mult, op1=ALU.add),
                          reads=[kC[h], "dbc", kpu], writes=[kC[h]])
            kb.op("dve", lambda e: e.tensor_copy(out=mprev[:, 0:1], in_=mprev[:, 1:2]), reads=["mnew"], writes=["mprev"])
            hmk = [("hm", h) for h in range(4)]
            hv = hm[:L, :].rearrange("t (h d) -> t h d", h=4)
            kb.op("dve", lambda e: e.tensor_reduce(out=hst[:L, 0:4], in_=hv, axis=AX.X, op=ALU.add), reads=hmk, writes=["hst"])
            kb.op("dve", lambda e: e.tensor_scalar(out=hst[:L, 0:4], in0=hst[:L, 0:4], scalar1=-1.0 / 256.0, scalar2=None, op0=ALU.mult), reads=["hst"], writes=["hst"])
            kb.op("dve", lambda e: e.tensor_tensor(out=hv, in0=hv, in1=hst[:L, 0:4].unsqueeze(2).to_broadcast([L, 4, 256]), op=ALU.add),
                  reads=hmk + ["hst"], writes=hmk)
            kb.op("pool", lambda e, so=so: e.tensor_tensor(out=so[:L, :], in0=hm[:L, :], in1=hm[:L, :], op=ALU.mult), reads=hmk, writes=[kso])
            kb.op("dve", lambda e, so=so: e.tensor_reduce(out=hst[:L, 4:8], in_=so[:L, :].rearrange("t (h d) -> t h d", h=4), axis=AX.X, op=ALU.add),
                  reads=[kso], writes=["hst"])
            kb.op("act", lambda e: e.activation(out=hst[:L, 8:12], in_=hst[:L, 4:8], func=AF.Sqrt, scale=1.0 / 256.0, bias=self.epsc[:L, 0:1]),
                  reads=["hst", "epsc"], writes=["hst"])
            kb.op("dve", lambda e: e.reciprocal(out=hst[:L, 12:16], in_=hst[:L, 8:12]), reads=["hst"], writes=["hst"])
            kb.op("dve", lambda e: e.tensor_tensor(out=hv, in0=hv, in1=hst[:L, 12:16].unsqueeze(2).to_broadcast([L, 4, 256]), op=ALU.mult),
                  reads=hmk + ["hst"], writes=hmk)
            zj = self.rr("zmt", 2)
            zm_, kzm = zmt[zj], "zmt%d" % zj
            kb.dma("sp", zm_[:, :, :L], S["zT"][O_ZM:O_ZM + D, t0:t0 + L].rearrange("(k p) t -> p k t", p=128), reads=zt_z(bi), writes=[kzm],
                   **(sl if L == 1 else {}))
            kb.op("act", lambda e, zm_=zm_: e.activation(out=zm_[:, :, :L], in_=zm_[:, :, :L], func=AF.Silu), reads=[kzm], writes=[kzm])
            for half in range(2):
                p = self.rr("ptr", 2)
                pt, kpt = self.ptr[p], "ptr%d" % p
                for k4 in range(4):
                    kc = half * 4 + k4
                    kb.op("pe", lambda e, k4=k4, kc=kc, pt=pt: e.transpose(out=pt[:, k4 * 128:k4 * 128 + L], in_=hm[:L, kc * 128:(kc + 1) * 128],
                                                                       identity=self.ident[:L, :L]), reads=hmk + ["ident"], writes=[kpt])
                kb.op("dve", lambda e, half=half, pt=pt: e.tensor_tensor(out=hT[:, half * 4:half * 4 + 4, :L],
                                                                       in0=pt[:, :].rearrange("p (k t) -> p k t", k=4)[:, :, :L],
                                                                       in1=mg[:, half * 4:half * 4 + 4, None].to_broadcast([128, 4, L]), op=ALU.mult),
                      reads=[kpt, "mg"], writes=[("hT", half)])
            kb.op("pool", lambda e: e.tensor_tensor(out=V(ctmp), in0=V(xcT), in1=msk[:, :, None].to_broadcast([128, 8, L]), op=ALU.mult),
                  reads=["xcT", "msk"], writes=["cvtmp"])
            kb.op("pool", lambda e: e.tensor_tensor(out=V(hT), in0=V(hT), in1=V(ctmp), op=ALU.add), reads=["cvtmp", ("hT", 0), ("hT", 1)], writes=[("hT", 0), ("hT", 1)])
            aj = self.rr("aob", 2)
            kb.op("dve", lambda e, aj=aj, zm_=zm_: e.tensor_tensor(out=aob[aj][:, :, :L], in0=V(hT), in1=zm_[:, :, :L], op=ALU.mult),
                  reads=[("hT", 0), ("hT", 1), kzm], writes=["aob%d" % aj])
            kb.dma("pool", S["act_m"][:, t0:t0 + L].rearrange("(k p) t -> p k t", p=128), aob[aj][:, :, :L], reads=["aob%d" % aj],
                   writes=[("act", "m", bi)], **(sl if L == 1 else {}))
            if sb_ is not None or ci == T // 128 - 1:
                if sb_ is None:
                    oc_, on_, om_ = O["c_prompt"][l], O["n_prompt"][l], O["m_prompt"][l]
                else:
                    oc_, on_, om_ = O["c_sample"][l, sb_], O["n_sample"][l, sb_], O["m_sample"][l, sb_]
                for h in range(4):
                    kb.dma("pool", oc_[h].rearrange("(c p) v -> p c v", p=128), C[h][:, :, 0:256], reads=[kC[h]], writes=[], is_output=True)
                    kb.dma("pool", on_[h].rearrange("(c p o) -> p c o", p=128, o=1), C[h][:, :, 256:257], reads=[kC[h]], writes=[], is_output=True, **sl)
                kb.dma("pool", om_.rearrange("(h o) -> h o", o=1), mprev[:, 0:1], reads=["mprev"], writes=[], is_output=True, **sl)
        self.end_phase()

    def rwkv_phase(self, l):
        kb = self.kb
        I, S, O = self.I, self.S, self.O
        T, NS, NT = self.T, self.NS, self.NT
        self.begin_phase()
        sbp = self.sbp
        sl = dict(allow_slow_non_contiguous=True)
        NPI = self.NPI
        EW = -math.exp(-0.5)
        P_ = {}
        for nm in ("r_w0", "r_a0", "r_k_k", "r_k_a", "r_ln_g", "r_ln_b"):
            P_[nm] = sbp("bc_" + nm, (128, D))
            kb.dma("sp", P_[nm][:], I[nm][l].partition_broadcast(128), writes=[nm])
        P_["r_r_k"] = sbp("bc_rk", (128, D))
        kb.dma("sp", P_["r_r_k"][:], I["r_r_k"][l].rearrange("h j -> (h j)").partition_broadcast(128), writes=["r_r_k"])
        mu = sbp("bc_mu", (128, R_SHIFT_W))
        kb.dma("sp", mu[:], I["r_mu"][l].partition_broadcast(128), writes=["mu"])
        w2 = sbp("w2", (64, 2, D))
        kb.dma("sp", w2[:, 0, :], I["r_w2"][l], writes=["w2"])
        kb.dma("sp", w2[:, 1, :], I["r_a2"][l], writes=["w2"])
        mks = sbp("mks", (128, 384))
        kb.dma("sp", mks[:], I["rmasks"], writes=["mks"])
        e12 = sbp("e12", (128, 1))
        kb.op("dve", lambda e: e.memset(e12[:], RWKV_GN_EPS), writes=["e12"])
        xr = sbp("xr", (128, R_SHIFT_W)); rp = sbp("rp", (128, R_SHIFT_W))
        lt = sbp("lt", (64, 2, 128))
        tv = {n: sbp("tv_" + n, (128, D)) for n in ("lw", "a", "an", "b", "k")}
        ssq = sbp("ssq", (128, 64))
        fm = {n: sbp("fm_" + n, (128, 8, 128)) for n in ("at", "rt", "bh", "bc", "kh", "kc", "cum", "et")}
        wc = sbp("wc", (128, 8, 16))
        ST = sbp("STt", (128, 8, 64))
        Vp = sbp("Vp", (128, 8, 64)); Ych = sbp("Ych", (128, 8, 64))
        nat = sbp("rnat", (128, 8, 64)); natx = sbp("rnatx", (128, 128)); nato = sbp("rnato", (128, 8, 64))
        CDT = BF16 if self.CORE_BF16 else F32
        U_ = []
        for i in range(NPI):
            d = {}
            d["ar"] = sbp("u%d_ar" % i, (128, 256), CDT)
            for n in ("bbd", "kbd", "Bcf", "Kcf", "Btm", "Ktm", "X0", "Xt0", "P0"):
                d[n] = sbp("u%d_%s" % (i, n), (128, 128), CDT)
            d["MNb"] = sbp("u%d_MNb" % i, (128, 256), CDT); d["MNk"] = sbp("u%d_MNk" % i, (128, 256), CDT)
            d["RHS"] = sbp("u%d_RHS" % i, (128, 64), CDT); d["U"] = sbp("u%d_U" % i, (128, 64), CDT)
            U_.append(d)
        if self.CORE_BF16:
            STb = sbp("STb", (128, 8, 64), BF16); Vpb = sbp("Vpb", (128, 8, 64), BF16); identc = sbp("identc", (128, 128), BF16)
            kb.op("pool", lambda e: e.tensor_copy(out=identc[:], in_=self.ident[:]), reads=["ident"], writes=["identc"])
            kb.op("pool", lambda e: e.memset(STb[:], 0.0), writes=[("STb", h) for h in range(8)])
        else:
            STb, Vpb, identc = ST, Vp, self.ident
        kSTb = (lambda hh: ("STb", hh)) if self.CORE_BF16 else (lambda hh: ("ST", hh))
        kVpb = (lambda hh: ("Vpb", hh)) if self.CORE_BF16 else (lambda hh: ("Vp", hh))

        def vp_ready():
            if self.CORE_BF16:
                kb.op("pool", lambda e: e.tensor_copy(out=Vpb[:], in_=Vp[:]), reads=[("Vp", h) for h in range(8)], writes=[("Vpb", h) for h in range(8)])
        ytm = rp[:, D:2 * D]; zrt = rp[:, 0:D]; yst = sbp("yst", (128, 64))
        aob = [sbp("raob%d" % i, (128, 8, 128), BF16) for i in range(2)]

        def zero_units():
            for i in range(NPI):
                for n in ("ar", "bbd", "kbd", "Bcf", "Kcf"):
                    kb.op("pool", lambda e, i=i, n=n: e.memset(U_[i][n][:], 0.0), writes=[("u", i, n)])
        zero_units()
        kb.op("pool", lambda e: e.memset(ST[:], 0.0), writes=[("ST", h) for h in range(8)])

        z_all = lambda ti: [("z", ti)]

        def prep(t0, L, ti, is_s):
            kb.dma("sp", xr[:L, :], S["z"][t0:t0 + L, O_RC:O_RC + R_SHIFT_W], reads=z_all(ti), writes=["xr"])
            if is_s:
                kb.dma("sp", rp[:L, :], I["state_rwkv_shift"][l], writes=["rp"])
            elif t0 == 0:
                kb.op("pool", lambda e: e.memset(rp[0:1, :], 0.0), writes=["rp"])
                kb.dma("sp", rp[1:L, :], S["z"][0:L - 1, O_RC:O_RC + R_SHIFT_W], reads=z_all(ti), writes=["rp"])
            else:
                kb.dma("sp", rp[:L, :], S["z"][t0 - 1:t0 + L - 1, O_RC:O_RC + R_SHIFT_W], reads=z_all(ti) + z_all(ti - 1), writes=["rp"])
            kb.op("pool", lambda e: e.tensor_tensor(out=rp[:L, :], in0=rp[:L, :], in1=xr[:L, :], op=ALU.subtract), reads=["rp", "xr"], writes=["rp"])
            kb.op("dve", lambda e: e.tensor_tensor(out=rp[:L, :], in0=rp[:L, :], in1=mu[:L, :], op=ALU.mult), reads=["rp", "mu"], writes=["rp"])
            kb.op("pool", lambda e: e.tensor_tensor(out=xr[:L, :], in0=xr[:L, :], in1=rp[:L, :], op=ALU.add), reads=["rp", "xr"], writes=["xr"])
            r_ = xr[:L, 0:D]; kr = xr[:L, D:2 * D]; vr = xr[:L, 2 * D:3 * D]
            kb.dma("pool", S["vs"][t0:t0 + L, :], vr, reads=["xr"], writes=[("vs", ti)])
            kb.op("act", lambda e: e.activation(out=xr[:L, 3 * D:3 * D + 64], in_=xr[:L, 3 * D:3 * D + 64], func=AF.Tanh), reads=["xr"], writes=["xr"])
            p = self.rr("ptr", 2)
            pt, kpt = self.ptr[p], "ptr%d" % p
            for i2 in range(2):
                kb.op("pe", lambda e, i2=i2, pt=pt: e.transpose(out=pt[:64, i2 * 128:i2 * 128 + L], in_=xr[:L, 3 * D + 64 * i2:3 * D + 64 * i2 + 64],
                                                             identity=self.ident[:L, :L]), reads=["xr", "ident"], writes=[kpt])
            kb.op("dve", lambda e, pt=pt: e.tensor_copy(out=lt[:, :, :L], in_=pt[:64, 0:256].rearrange("p (a t) -> p a t", a=2)[:, :, :L]), reads=[kpt], writes=["lt"])
            for i2, (dst, pb, sc_) in enumerate((("lw", "r_w0", EW), ("a", "r_a0", 1.0))):
                for hf in range(2):
                    pq = self.rr("pp", 4)
                    pp, kpp = self.pp[pq], "pp%d" % pq
                    kb.op("pe", lambda e, i2=i2, hf=hf, pp=pp: e.matmul(pp[:L, :], lhsT=lt[:, i2, :L], rhs=w2[:, i2, hf * 512:(hf + 1) * 512], start=True, stop=True),
                          reads=["lt", "w2"], writes=[kpp])
                    kb.op("dve", lambda e, dst=dst, pb=pb, hf=hf, pp=pp: e.tensor_tensor(out=tv[dst][:L, hf * 512:(hf + 1) * 512], in0=pp[:L, :],
                                                                                   in1=P_[pb][:L, hf * 512:(hf + 1) * 512], op=ALU.add),
                          reads=[kpp, pb], writes=[("tv", dst)])
                kb.op("act", lambda e, dst=dst: e.activation(out=tv[dst][:L, :], in_=tv[dst][:L, :], func=AF.Sigmoid), reads=[("tv", dst)], writes=[("tv", dst)])
            kb.op("dve", lambda e: e.tensor_scalar(out=tv["lw"][:L, :], in0=tv["lw"][:L, :], scalar1=EW, scalar2=None, op0=ALU.mult), reads=[("tv", "lw")], writes=[("tv", "lw")])
            kb.op("dve", lambda e: e.tensor_tensor(out=tv["an"][:L, :], in0=kr, in1=P_["r_k_k"][:L, :], op=ALU.mult), reads=["xr", "r_k_k"], writes=[("tv", "an")])
            kb.op("pool", lambda e: e.tensor_tensor(out=tv["b"][:L, :], in0=tv["an"][:L, :], in1=tv["an"][:L, :], op=ALU.mult), reads=[("tv", "an")], writes=[("tv", "b")])
            kb.op("dve", lambda e: e.tensor_reduce(out=ssq[:L, 0:16], in_=tv["b"][:L, :].rearrange("t (h j) -> t h j", h=16), axis=AX.X, op=ALU.add),
                  reads=[("tv", "b")], writes=["ssq"])
            kb.op("act", lambda e: e.activation(out=ssq[:L, 0:16], in_=ssq[:L, 0:16], func=AF.Sqrt), reads=["ssq"], writes=["ssq"])
            kb.op("dve", lambda e: e.tensor_scalar(out=ssq[:L, 0:16], in0=ssq[:L, 0:16], scalar1=1e-12, scalar2=None, op0=ALU.max), reads=["ssq"], writes=["ssq"])
            kb.op("dve", lambda e: e.reciprocal(out=ssq[:L, 16:32], in_=ssq[:L, 0:16]), reads=["ssq"], writes=["ssq"])
            hv = lambda x: x[:L, :].rearrange("t (h j) -> t h j", h=16)
            kb.op("dve", lambda e: e.tensor_tensor(out=hv(tv["an"]), in0=hv(tv["an"]), in1=ssq[:L, 16:32].unsqueeze(2).to_broadcast([L, 16, 64]), op=ALU.mult),
                  reads=[("tv", "an"), "ssq"], writes=[("tv", "an")])
            kb.op("pool", lambda e: e.tensor_tensor(out=tv["b"][:L, :], in0=tv["an"][:L, :], in1=tv["a"][:L, :], op=ALU.mult),
                  reads=[("tv", "an"), ("tv", "a")], writes=[("tv", "b")])
            kb.op("dve", lambda e: e.tensor_scalar(out=tv["an"][:L, :], in0=tv["an"][:L, :], scalar1=-1.0, scalar2=None, op0=ALU.mult),
                  reads=[("tv", "an"), ("tv", "b")], writes=[("tv", "an")])
            kb.op("dve", lambda e: e.scalar_tensor_tensor(out=tv["k"][:L, :], in0=tv["a"][:L, :], scalar=-1.0, in1=P_["r_k_a"][:L, :], op0=ALU.add, op1=ALU.mult),
                  reads=[("tv", "a"), "r_k_a"], writes=[("tv", "k")])
            kb.op("dve", lambda e: e.scalar_tensor_tensor(out=tv["k"][:L, :], in0=tv["k"][:L, :], scalar=1.0, in1=kr, op0=ALU.add, op1=ALU.mult),
                  reads=[("tv", "k"), "xr"], writes=[("tv", "k")])
            kb.op("pool", lambda e: e.tensor_tensor(out=tv["a"][:L, :], in0=tv["k"][:L, :], in1=P_["r_r_k"][:L, :], op=ALU.mult),
                  reads=[("tv", "k"), "r_r_k", ("tv", "b")], writes=[("tv", "a")])
            kb.op("pool", lambda e: e.tensor_tensor(out=tv["a"][:L, :], in0=tv["a"][:L, :], in1=r_, op=ALU.mult), reads=[("tv", "a"), "xr"], writes=[("tv", "a")])
            kb.op("dve", lambda e: e.tensor_reduce(out=ssq[:L, 32:48], in_=hv(tv["a"]), axis=AX.X, op=ALU.add), reads=[("tv", "a")], writes=["ssq2"])
            for (dst, src, ksrc) in (("at", tv["an"][:L, :], ("tv", "an")), ("rt", r_, "xr"), ("bh", tv["b"][:L, :], ("tv", "b")),
                                     ("kh", tv["k"][:L, :], ("tv", "k")), ("cum", tv["lw"][:L, :], ("tv", "lw"))):
                for half in range(2):
                    p = self.rr("ptr", 2)
                    pt, kpt = self.ptr[p], "ptr%d" % p
                    for k4 in range(4):
                        kc = half * 4 + k4
                        kb.op("pe", lambda e, k4=k4, kc=kc, pt=pt, src=src: e.transpose(out=pt[:, k4 * 128:k4 * 128 + L], in_=src[:, kc * 128:(kc + 1) * 128],
                                                                                  identity=self.ident[:L, :L]), reads=[ksrc, "ident"], writes=[kpt])
                    self.evac(fm[dst][:, half * 4:half * 4 + 4, :L], pt[:, :].rearrange("p (k t) -> p k t", k=4)[:, :, :L], [kpt], [("fm", dst)])
            Lc = 1 if is_s else 64
            nch = L // Lc
            lwT = fm["cum"]
            kb.op("pool", lambda e: e.tensor_copy(out=fm["et"][:, :, :L], in_=lwT[:, :, :L]), reads=[("fm", "cum")], writes=[("fm", "et")])
            if Lc > 1:
                for hh in range(8):
                    for c in range(nch):
                        kb.op("dve", lambda e, hh=hh, c=c: e.tensor_tensor_scan(out=fm["cum"][:, hh, c * Lc:(c + 1) * Lc], data0=self.onesf[:, :Lc],
                                                                             data1=fm["et"][:, hh, c * Lc:(c + 1) * Lc], initial=0.0, op0=ALU.mult, op1=ALU.add),
                              reads=[("fm", "et"), "onesf"], writes=[("fm", "cum")])
            F = lambda n: fm[n][:, :, :L]
            C4 = lambda n: fm[n][:, :, :L].rearrange("p k (c t) -> p k c t", t=Lc)
            kb.op("dve", lambda e: e.tensor_tensor(out=F("et"), in0=F("cum"), in1=F("et"), op=ALU.subtract), reads=[("fm", "cum"), ("fm", "et")], writes=[("fm", "et")])
            kb.op("act", lambda e: e.activation(out=F("et"), in_=F("et"), func=AF.Exp), reads=[("fm", "et")], writes=[("fm", "et")])
            kb.op("dve", lambda e: e.tensor_tensor(out=F("at"), in0=F("at"), in1=F("et"), op=ALU.mult), reads=[("fm", "at"), ("fm", "et")], writes=[("fm", "at")])
            kb.op("act", lambda e: e.activation(out=F("et"), in_=F("cum"), func=AF.Exp), reads=[("fm", "cum"), ("fm", "at")], writes=[("fm", "et")])
            kb.op("dve", lambda e: e.tensor_tensor(out=F("rt"), in0=F("rt"), in1=F("et"), op=ALU.mult), reads=[("fm", "rt"), ("fm", "et")], writes=[("fm", "rt")])
            kb.op("pool", lambda e: e.tensor_copy(out=wc[:, :, :nch], in_=C4("et")[:, :, :, Lc - 1]), reads=[("fm", "et")], writes=["wc"])
            kb.op("dve", lambda e: e.tensor_tensor(out=C4("et"), in0=C4("cum")[:, :, :, Lc - 1:Lc].to_broadcast([128, 8, nch, Lc]), in1=C4("cum"), op=ALU.subtract),
                  reads=[("fm", "cum"), ("fm", "rt"), "wc"], writes=[("fm", "et")])
            kb.op("act", lambda e: e.activation(out=F("et"), in_=F("et"), func=AF.Exp), reads=[("fm", "et")], writes=[("fm", "et")])
            kb.op("dve", lambda e: e.tensor_tensor(out=F("bc"), in0=F("bh"), in1=F("et"), op=ALU.mult), reads=[("fm", "bh"), ("fm", "et")], writes=[("fm", "bc")])
            kb.op("pool", lambda e: e.tensor_tensor(out=F("kc"), in0=F("kh"), in1=F("et"), op=ALU.mult), reads=[("fm", "kh"), ("fm", "et")], writes=[("fm", "kc")])
            kb.op("act", lambda e: e.activation(out=F("et"), in_=F("cum"), func=AF.Exp, scale=-1.0), reads=[("fm", "cum"), ("fm", "bc"), ("fm", "kc")], writes=[("fm", "et")])
            kb.op("dve", lambda e: e.tensor_tensor(out=F("bh"), in0=F("bh"), in1=F("et"), op=ALU.mult), reads=[("fm", "bh"), ("fm", "et")], writes=[("fm", "bh")])
            kb.op("pool", lambda e: e.tensor_tensor(out=F("kh"), in0=F("kh"), in1=F("et"), op=ALU.mult), reads=[("fm", "kh"), ("fm", "et")], writes=[("fm", "kh")])

        def core(hhs, col0, Lc, cidx):
            us = [(U_[i], i, hh) for i, hh in enumerate(hhs)]
            fmk = [("fm", n) for n in ("at", "rt", "bh", "bc", "kh", "kc")]
            for (u, i, hh) in us:
                for half in range(2):
                    ps = slice(half * 64, half * 64 + 64)
                    cs_ = slice(col0, col0 + Lc)
                    for (tn, off, src, eng) in (("ar", 0, "at", "dve"), ("ar", 128, "rt", "pool"), ("bbd", 0, "bh", "dve"), ("kbd", 0, "kh", "pool"),
                                                ("Bcf", 0, "bc", "dve"), ("Kcf", 0, "kc", "pool")):
                        kb.op(eng, lambda e, u=u, tn=tn, off=off, src=src, ps=ps, half=half, hh=hh, cs_=cs_: e.tensor_copy(
                            out=u[tn][ps, off + half * 64:off + half * 64 + Lc], in_=fm[src][ps, hh, cs_]),
                            reads=[("fm", src)], writes=[("u", i, tn)])
            for (u, i, hh) in us:
                for (src, dst) in (("Bcf", "Btm"), ("Kcf", "Ktm")):
                    p = self.rr("ptr", 2)
                    pt, kpt = self.ptr[p], "ptr%d" % p
                    if self.CORE_BF16:
                        kb.op("pe", lambda e, u=u, src=src, pt=pt: e.matmul(pt[:, 0:128], lhsT=u[src][:, :], rhs=identc[:, :], start=True, stop=True),
                              reads=[("u", i, src), "identc"], writes=[kpt])
                    else:
                        kb.op("pe", lambda e, u=u, src=src, pt=pt: e.transpose(out=pt[:, 0:128], in_=u[src][:, :], identity=self.ident[:, :]),
                              reads=[("u", i, src), "ident"], writes=[kpt])
                    self.evac(u[dst][:, :], pt[:, 0:128], [kpt], [("u", i, dst)])
            for (u, i, hh) in us:
                for (lh, dst) in (("bbd", "MNb"), ("kbd", "MNk")):
                    pq = self.rr("pp", 4)
                    pp, kpp = self.pp[pq], "pp%d" % pq
                    kb.op("pe", lambda e, u=u, lh=lh, pp=pp: e.matmul(pp[:, 0:256], lhsT=u[lh][:, :], rhs=u["ar"][:, :], start=True, stop=True),
                          reads=[("u", i, lh), ("u", i, "ar")], writes=[kpp])
                    kb.op("dve", lambda e, u=u, dst=dst, pp=pp: e.tensor_tensor(out=u[dst][:, :], in0=pp[:, 0:256], in1=mks[:, 0:256], op=ALU.mult),
                          reads=[kpp, "mks"], writes=[("u", i, dst)])
                pq = self.rr("pp", 4)
                pp, kpp = self.pp[pq], "pp%d" % pq
                kb.op("pe", lambda e, u=u, pp=pp: e.matmul(pp[:, 0:128], lhsT=u["ar"][:, 0:128], rhs=u["bbd"][:, :], start=True, stop=True),
                      reads=[("u", i, "bbd"), ("u", i, "ar")], writes=[kpp])
                kb.op("dve", lambda e, u=u, pp=pp: e.tensor_tensor(out=u["Xt0"][:, :], in0=pp[:, 0:128], in1=mks[:, 256:384], op=ALU.mult),
                      reads=[kpp, "mks"], writes=[("u", i, "Xt0")])
                kb.op("pool", lambda e, u=u: e.tensor_copy(out=u["X0"][:, :], in_=u["MNb"][:, 0:128]), reads=[("u", i, "MNb")], writes=[("u", i, "X0")])
                kb.op("pool", lambda e, u=u: e.tensor_tensor(out=u["P0"][:, :], in0=u["MNb"][:, 0:128], in1=self.ident[:, :], op=ALU.add),
                      reads=[("u", i, "MNb"), "ident"], writes=[("u", i, "P0")])
            nlev = 0
            while (1 << (nlev + 1)) < Lc:
                nlev += 1
            cur = 0
            for lev in range(nlev if Lc > 1 else 0):
                nxt = 1 - cur
                lastlev = (lev == nlev - 1)
                for (u, i, hh) in us:
                    X, Xt, Pc = u["X0"], u["Xt0"], u["P0"]
                    Xn, Xtn, Pn = X, Xt, Pc
                    kX, kXt, kP = ("u", i, "X0"), ("u", i, "Xt0"), ("u", i, "P0")
                    kXn, kXtn, kPn = kX, kXt, kP
                    pq1 = pq2 = None
                    if not lastlev:
                        pq1 = self.rr("pp", 4)
                        pp1, kpp1 = self.pp[pq1], "pp%d" % pq1
                        kb.op("pe", lambda e, pp1=pp1, X=X, Xt=Xt: e.matmul(pp1[:, 0:128], lhsT=Xt[:, :], rhs=X[:, :], start=True, stop=True), reads=[kX, kXt], writes=[kpp1])
                    pq2 = self.rr("pp", 4)
                    pp2, kpp2 = self.pp[pq2], "pp%d" % pq2
                    kb.op("pe", lambda e, pp2=pp2, X=X, Xt=Xt: e.matmul(pp2[:, 0:128], lhsT=X[:, :], rhs=Xt[:, :], start=True, stop=True), reads=[kX, kXt], writes=[kpp2])
                    if not lastlev:
                        self.evac(Xn[:, :], pp1[:, 0:128], [kpp1], [kXn])
                    self.evac(Xtn[:, :], pp2[:, 0:128], [kpp2], [kXtn])
                    pq = self.rr("ptr", 2)
                    pp, kpp = self.ptr[pq], "ptr%d" % pq
                    kb.op("pe", lambda e, pp=pp, Xtn=Xtn, Pc=Pc: e.matmul(pp[:, 0:128], lhsT=Xtn[:, :], rhs=Pc[:, :], start=True, stop=True), reads=[kXtn, kP], writes=[kpp])
                    kb.op("dve", lambda e, pp=pp, Pn=Pn, Pc=Pc: e.tensor_tensor(out=Pn[:, :], in0=pp[:, 0:128], in1=Pc[:, :], op=ALU.add), reads=[kpp, kP], writes=[kPn])
                cur = nxt
            for (u, i, hh) in us:
                Pf, kPf = u["P0"], ("u", i, "P0")
                kS = ("ST", hh)
                pq = self.rr("pp", 4)
                pp, kpp = self.pp[pq], "pp%d" % pq
                kb.op("pe", lambda e, u=u, pp=pp, hh=hh: e.matmul(pp[:, 0:64], lhsT=u["ar"][:, 0:128], rhs=STb[:, hh, :], start=True, stop=False),
                      reads=[("u", i, "ar"), kSTb(hh)], writes=[kpp])
                kb.op("pe", lambda e, u=u, pp=pp, hh=hh: e.matmul(pp[:, 0:64], lhsT=u["MNk"][:, 0:128], rhs=Vpb[:, hh, :], start=False, stop=True),
                      reads=[("u", i, "MNk"), kVpb(hh)], writes=[kpp])
                self.evac(u["RHS"][:, :], pp[:, 0:64], [kpp], [("u", i, "RHS")])
                pq = self.rr("pp", 4)
                pp, kpp = self.pp[pq], "pp%d" % pq
                kb.op("pe", lambda e, pp=pp, Pf=Pf, u=u: e.matmul(pp[:, 0:64], lhsT=Pf[:, :], rhs=u["RHS"][:, :], start=True, stop=True), reads=[kPf, ("u", i, "RHS")], writes=[kpp])
                self.evac(u["U"][:, :], pp[:, 0:64], [kpp], [("u", i, "U")])
                pq = self.rr("pp", 4)
                pp, kpp = self.pp[pq], "pp%d" % pq
                kb.op("pe", lambda e, u=u, pp=pp, hh=hh: e.matmul(pp[:, 0:64], lhsT=u["ar"][:, 128:256], rhs=STb[:, hh, :], start=True, stop=False),
                      reads=[("u", i, "ar"), kSTb(hh)], writes=[kpp])
                kb.op("pe", lambda e, u=u, pp=pp: e.matmul(pp[:, 0:64], lhsT=u["MNb"][:, 128:256], rhs=u["U"][:, :], start=False, stop=False),
                      reads=[("u", i, "MNb"), ("u", i, "U")], writes=[kpp])
                kb.op("pe", lambda e, u=u, pp=pp, hh=hh: e.matmul(pp[:, 0:64], lhsT=u["MNk"][:, 128:256], rhs=Vpb[:, hh, :], start=False, stop=True),
                      reads=[("u", i, "MNk"), kVpb(hh)], writes=[kpp])
                self.evac(Ych[:, hh, :], pp[:, 0:64], [kpp], [("Ych", hh)])
                pq = self.rr("ptr", 2)
                pp, kpp = self.ptr[pq], "ptr%d" % pq
                kb.op("pe", lambda e, u=u, pp=pp: e.matmul(pp[:, 0:64], lhsT=u["Btm"][:, :], rhs=u["U"][:, :], start=True, stop=False),
                      reads=[("u", i, "Btm"), ("u", i, "U")], writes=[kpp])
                kb.op("pe", lambda e, u=u, pp=pp, hh=hh: e.matmul(pp[:, 0:64], lhsT=u["Ktm"][:, :], rhs=Vpb[:, hh, :], start=False, stop=True),
                      reads=[("u", i, "Ktm"), kVpb(hh)], writes=[kpp])
                kb.op("dve", lambda e, pp=pp, hh=hh: e.scalar_tensor_tensor(out=ST[:, hh, :], in0=ST[:, hh, :], scalar=wc[:, hh, cidx:cidx + 1], in1=pp[:, 0:64],
                                                                        op0=ALU.mult, op1=ALU.add), reads=[kS, "wc", kpp], writes=[kS])
                if self.CORE_BF16:
                    kb.op("pool", lambda e, hh=hh: e.tensor_copy(out=STb[:, hh, :], in_=ST[:, hh, :]), reads=[kS], writes=[("STb", hh)])

        def post(t0, L, ti, bi):
            kb.dma("sp", ytm[:L, :], S["ys"][t0:t0 + L, :], reads=[("ys", ti)], writes=["rp"])
            kb.dma("sp", zrt[:L, :], S["z"][t0:t0 + L, O_ZR:O_ZR + D], reads=z_all(ti), writes=["rp"])
            kb.op("act", lambda e: e.activation(out=zrt[:L, :], in_=zrt[:L, :], func=AF.Silu), reads=["rp"], writes=["rp"])
            hv = lambda x: x[:L, :].rearrange("t (h j) -> t h j", h=16)
            kb.op("dve", lambda e: e.tensor_reduce(out=yst[:L, 0:16], in_=hv(ytm), axis=AX.X, op=ALU.add), reads=["rp"], writes=["yst"])
            kb.op("dve", lambda e: e.tensor_scalar(out=yst[:L, 0:16], in0=yst[:L, 0:16], scalar1=-1.0 / 64.0, scalar2=None, op0=ALU.mult), reads=["yst"], writes=["yst"])
            kb.op("dve", lambda e: e.tensor_tensor(out=hv(ytm), in0=hv(ytm), in1=yst[:L, 0:16].unsqueeze(2).to_broadcast([L, 16, 64]), op=ALU.add),
                  reads=["rp", "yst"], writes=["rp"])
            kb.op("pool", lambda e: e.tensor_tensor(out=tv["lw"][:L, :], in0=ytm[:L, :], in1=ytm[:L, :], op=ALU.mult), reads=["rp"], writes=[("tv", "lw")])
            kb.op("dve", lambda e: e.tensor_reduce(out=yst[:L, 16:32], in_=hv(tv["lw"]), axis=AX.X, op=ALU.add), reads=[("tv", "lw")], writes=["yst"])
            kb.op("act", lambda e: e.activation(out=yst[:L, 32:48], in_=yst[:L, 16:32], func=AF.Sqrt, scale=1.0 / 64.0, bias=e12[:L, 0:1]), reads=["yst", "e12"], writes=["yst"])
            kb.op("dve", lambda e: e.reciprocal(out=yst[:L, 48:64], in_=yst[:L, 32:48]), reads=["yst"], writes=["yst"])
            kb.op("dve", lambda e: e.tensor_tensor(out=hv(ytm), in0=hv(ytm), in1=yst[:L, 48:64].unsqueeze(2).to_broadcast([L, 16, 64]), op=ALU.mult),
                  reads=["rp", "yst"], writes=["rp"])
            kb.op("dve", lambda e: e.tensor_tensor(out=ytm[:L, :], in0=ytm[:L, :], in1=P_["r_ln_g"][:L, :], op=ALU.mult), reads=["rp", "r_ln_g"], writes=["rp"])
            kb.op("pool", lambda e: e.tensor_tensor(out=ytm[:L, :], in0=ytm[:L, :], in1=P_["r_ln_b"][:L, :], op=ALU.add), reads=["rp", "r_ln_b"], writes=["rp"])
            kb.op("dve", lambda e: e.tensor_tensor(out=hv(tv["lw"]), in0=xr[:L, 2 * D:3 * D].rearrange("t (h j) -> t h j", h=16),
                                                   in1=ssq[:L, 32:48].unsqueeze(2).to_broadcast([L, 16, 64]), op=ALU.mult), reads=["xr", "ssq2"], writes=[("tv", "lw")])
            kb.op("pool", lambda e: e.tensor_tensor(out=ytm[:L, :], in0=ytm[:L, :], in1=tv["lw"][:L, :], op=ALU.add), reads=["rp", ("tv", "lw")], writes=["rp"])
            kb.op("dve", lambda e: e.tensor_tensor(out=ytm[:L, :], in0=ytm[:L, :], in1=zrt[:L, :], op=ALU.mult), reads=["rp", "rp"], writes=["rp"])
            aj = self.rr("raob", 2)
            for half in range(2):
                p = self.rr("ptr", 2)
                pt, kpt = self.ptr[p], "ptr%d" % p
                for k4 in range(4):
                    kc = half * 4 + k4
                    kb.op("pe", lambda e, k4=k4, kc=kc, pt=pt: e.transpose(out=pt[:, k4 * 128:k4 * 128 + L], in_=ytm[:L, kc * 128:(kc + 1) * 128],
                                                                       identity=self.ident[:L, :L]), reads=["rp", "ident"], writes=[kpt])
                self.evac(aob[aj][:, half * 4:half * 4 + 4, :L], pt[:, :].rearrange("p (k t) -> p k t", k=4)[:, :, :L], [kpt], ["raob%d" % aj])
            kb.dma("pool", S["act_r"][:, t0:t0 + L].rearrange("(k p) t -> p k t", p=128), aob[aj][:, :, :L], reads=["raob%d" % aj], writes=[("act", "r", bi)])

        def state_out(dst):
            for hh in range(8):
                kb.op("pool", lambda e, hh=hh: e.memset(natx[:, :], 0.0), writes=["natx"])
                for half in range(2):
                    ps = slice(half * 64, half * 64 + 64)
                    kb.op("pool", lambda e, hh=hh, ps=ps: e.tensor_copy(out=natx[ps, ps], in_=ST[ps, hh, :]), reads=[("ST", hh)], writes=["natx"])
                p = self.rr("ptr", 2)
                pt, kpt = self.ptr[p], "ptr%d" % p
                kb.op("pe", lambda e, pt=pt: e.transpose(out=pt[:, 0:128], in_=natx[:, :], identity=self.ident[:, :]), reads=["natx", "ident"], writes=[kpt])
                for half in range(2):
                    ps = slice(half * 64, half * 64 + 64)
                    kb.op("dve", lambda e, hh=hh, ps=ps, pt=pt: e.tensor_copy(out=nato[ps, hh, :], in_=pt[ps, ps]), reads=[kpt], writes=["nato"])
            kb.dma("pool", dst.rearrange("(hh hl) i j -> (hl i) hh j", hl=2), nato[:, :, :], reads=["nato"], writes=[], is_output=True)

        def state_in(src):
            kb.dma("sp", nat[:, :, :], src.rearrange("(hh hl) i j -> (hl i) hh j", hl=2), writes=["nat"])
            for hh in range(8):
                kb.op("pool", lambda e: e.memset(natx[:, :], 0.0), writes=["natx"])
                for half in range(2):
                    ps = slice(half * 64, half * 64 + 64)
                    kb.op("pool", lambda e, hh=hh, ps=ps: e.tensor_copy(out=natx[ps, ps], in_=nat[ps, hh, :]), reads=["nat"], writes=["natx"])
                p = self.rr("ptr", 2)
                pt, kpt = self.ptr[p], "ptr%d" % p
                kb.op("pe", lambda e, pt=pt: e.transpose(out=pt[:, 0:128], in_=natx[:, :], identity=self.ident[:, :]), reads=["natx", "ident"], writes=[kpt])
                for half in range(2):
                    ps = slice(half * 64, half * 64 + 64)
                    kb.op("dve", lambda e, hh=hh, ps=ps, pt=pt: e.tensor_copy(out=ST[ps, hh, :], in_=pt[ps, ps]), reads=[kpt], writes=[("ST", hh)])
                    if self.CORE_BF16:
                        kb.op("pool", lambda e, hh=hh, ps=ps, pt=pt: e.tensor_copy(out=STb[ps, hh, :], in_=ST[ps, hh, :]), reads=[("ST", hh)], writes=[("STb", hh)])

        blk_of = lambda t: next(i for i, (b0, bn) in enumerate(self.blocks) if b0 <= t < b0 + bn)
        for ti in range(T // 128):
            t0 = ti * 128
            prep(t0, 128, ti, False)
            for c in range(2):
                for half in range(2):
                    kb.dma("sp", Vp[half * 64:half * 64 + 64, :, :],
                           S["vs"][t0 + c * 64:t0 + c * 64 + 64, :].rearrange("t (hh hl i) -> hl t hh i", hl=2, i=64)[half],
                           reads=[("vs", ti)], writes=[("Vp", h) for h in range(8)])
                vp_ready()
                for g in range(0, 8, NPI):
                    core(list(range(g, g + NPI)), c * 64, 64, c)
                for half in range(2):
                    kb.dma("pool", S["ys"][t0 + c * 64:t0 + c * 64 + 64, :].rearrange("t (hh hl i) -> hl t hh i", hl=2, i=64)[half],
                           Ych[half * 64:half * 64 + 64, :, :], reads=[("Ych", h) for h in range(8)], writes=[("ys", ti)])
            post(t0, 128, ti, blk_of(t0))
        state_out(O["wkv_prompt"][l])
        if NS:
            ti = T // 128
            prep(T, NS, ti, True)
            zero_units()
            kb.op("pool", lambda e: e.memset(Vp[:], 0.0), writes=[("Vp", h) for h in range(8)])
            for b in range(NS):
                state_in(I["state_rwkv_wkv"][l, b])
                for half in range(2):
                    kb.dma("sp", Vp[half * 64:half * 64 + 1, :, :],
                           S["vs"][T + b:T + b + 1, :].rearrange("t (hh hl i) -> hl t hh i", hl=2, i=64)[half],
                           reads=[("vs", ti)], writes=[("Vp", h) for h in range(8)])
                vp_ready()
                for g in range(0, 8, NPI):
                    core(list(range(g, g + NPI)), b, 1, b)
                for half in range(2):
                    kb.dma("pool", S["ys"][T + b:T + b + 1, :].rearrange("t (hh hl i) -> hl t hh i", hl=2, i=64)[half],
                           Ych[half * 64:half * 64 + 1, :, :], reads=[("Ych", h) for h in range(8)], writes=[("ys", ti)])
                state_out(O["wkv_sample"][l, b])
            post(T, NS, ti, blk_of(T))
        self.end_phase()

    def sin_to(self, tkey, out, ang, tf, ti, tm, key_out, key_ang, shift=0.0):
        kb = self.kb
        TWO_PI = 2.0 * math.pi
        kt = ("sin_tmp", tkey)
        kb.op("dve", lambda e: e.tensor_scalar(out=tm, in0=ang, scalar1=shift, scalar2=None, op0=ALU.add),
              reads=[key_ang], writes=[(kt, "m")])
        kb.op("dve", lambda e: e.tensor_scalar(out=tf, in0=tm, scalar1=1.0 / TWO_PI, scalar2=None, op0=ALU.mult),
              reads=[(kt, "m")], writes=[(kt, "f")])
        kb.op("dve", lambda e: e.tensor_copy(out=ti, in_=tf), reads=[(kt, "f")], writes=[(kt, "i")])
        kb.op("dve", lambda e: e.tensor_copy(out=tf, in_=ti), reads=[(kt, "i")], writes=[(kt, "f")])
        kb.op("dve", lambda e: e.scalar_tensor_tensor(out=tm, in0=tf, scalar=-TWO_PI, in1=tm, op0=ALU.mult, op1=ALU.add),
              reads=[(kt, "f"), (kt, "m")], writes=[(kt, "m")])
        kb.op("dve", lambda e: e.tensor_scalar(out=tf, in0=tm, scalar1=math.pi, scalar2=-TWO_PI, op0=ALU.is_gt, op1=ALU.mult),
              reads=[(kt, "m")], writes=[(kt, "f")])
        kb.op("dve", lambda e: e.tensor_tensor(out=tm, in0=tm, in1=tf, op=ALU.add), reads=[(kt, "f"), (kt, "m")],
              writes=[(kt, "m")])
        kb.op("dve", lambda e: e.tensor_scalar(out=tf, in0=tm, scalar1=-math.pi, scalar2=TWO_PI, op0=ALU.is_lt, op1=ALU.mult),
              reads=[(kt, "m")], writes=[(kt, "f")])
        kb.op("dve", lambda e: e.tensor_tensor(out=tm, in0=tm, in1=tf, op=ALU.add), reads=[(kt, "f"), (kt, "m")],
              writes=[(kt, "m")])
        kb.op("dve", lambda e: e.tensor_scalar(out=tm, in0=tm, scalar1=-3.1415925, scalar2=3.1415925, op0=ALU.max, op1=ALU.min),
              reads=[(kt, "m")], writes=[(kt, "m")])
        kb.op("act", lambda e: e.activation(out=out, in_=tm, func=AF.Sin), reads=[(kt, "m")], writes=[key_out])

    def s5_phase(self, l):
        kb = self.kb
        I, S, O = self.I, self.S, self.O
        T, NS = self.T, self.NS
        self.begin_phase()
        sbp = self.sbp
        nat = sbp("nat", (32, 14, 128))
        nati = sbp("nati", (32, 128), I32)
        PT = sbp("PT", (128, 6, 32))
        BR = sbp("BR", (128, 32, 16)); BI = sbp("BI", (128, 32, 16))
        bbr = sbp("bbr", (128, 32, 16)); bbi = sbp("bbi", (128, 32, 16))
        xin = sbp("xin", (128, 512))
        btmp = xin[:, :].rearrange("p (s c) -> p s c", c=16)
        Bz = [sbp("Bz%d" % i, (128, 32, 128), BF16) for i in range(2)]
        Cn = [sbp("Cn%d" % i, (128, 8, 64)) for i in range(2)]
        Cz = [sbp("Cz%d" % i, (128, 32, 128), BF16) for i in range(2)]
        cosT = sbp("cosT", (128, 32, 128)); sinT = sbp("sinT", (128, 32, 128))
        ang = sbp("ang", (128, 2, 128)); angf = sbp("angf", (128, 2, 128)); angm = sbp("angm", (128, 2, 128))
        angi = sbp("angi", (128, 2, 128), I32)
        car = [sbp("car%d" % i, (128, 32)) for i in range(2)]
        ctmp = sbp("ctmp", (128, 4))
        m3 = sbp("m3", (128, 4, 2)); m4 = sbp("m4", (128, 4, 8)); iota1 = sbp("iota1", (128, 128))
        dcol = sbp("dcol", (128, 8)); gbcol = sbp("gbcol", (128, 8))
        wg = sbp("wglu", (128, KC, D), BF16)
        wstg = [sbp("wstg%d" % i, (128, KC, 128)) for i in range(1)]
        uf = [sbp("uf%d" % i, (128, 256)) for i in range(2)]
        ub = [sbp("ub%d" % i, (128, 256), BF16) for i in range(2)]
        WS = []
        for w_ in range(2):
            WS.append(([sbp("tq%d_%d" % (w_, i), (128, 256)) for i in range(4)],
                       [sbp("bh%d_%d" % (w_, i), (128, 256)) for i in range(2)],
                       [sbp("sh%d_%d" % (w_, i), (128, 256)) for i in range(2)],
                       [sbp("sbf%d_%d" % (w_, i), (128, 256), BF16) for i in range(2)]))
        ysg = sbp("ysg", (128, KC, 256)); ysb = sbp("ysb", (128, KC, 256), BF16)
        yt = [sbp("yt%d" % i, (128, 256)) for i in range(2)]
        zst = [sbp("zst%d" % i, (128, 256)) for i in range(2)]
        aout = [sbp("aout%d" % i, (128, 256), BF16) for i in range(2)]
        s0T = [sbp("s0T%d" % i, (128, 32, 16)) for i in range(2)]
        snw = [sbp("snw%d" % i, (128, 32, 16)) for i in range(2)]
        sst = sbp("sst", (32, 512))

        kb.dma("sp", m3[:], I["mask3"], writes=["m3"])
        kb.dma("sp", m4[:], I["mask4"], writes=["m4"])
        kb.dma("sp", iota1[:], I["iota1"], writes=["iota1"])
        kb.dma("sp", nat[:, 0, :], I["s_lam_re"][l].rearrange("(s g) p -> s (g p)", g=2), writes=["nat0"])
        kb.dma("sp", nat[:, 1, :], I["s_lam_im"][l].rearrange("(s g) p -> s (g p)", g=2), writes=["nat1"])
        kb.dma("sp", nat[:, 2, 0:2], I["s_log_dt"][l].rearrange("(s g) -> s g", g=2), writes=["nat2"])
        kb.dma("sp", dcol[:], I["s_d"][l].rearrange("(k p) -> p k", p=128), writes=["dcol"], allow_slow_non_contiguous=True)
        kb.dma("sp", gbcol[:], I["s_glu_b"][l].rearrange("(k p) -> p k", p=128), writes=["gbcol"], allow_slow_non_contiguous=True)
        kb.dma("sp", BR[:], I["s_b_re"][l].rearrange("(s g) p c -> (g p) s c", g=2), writes=["BR"])
        kb.dma("sp", BI[:], I["s_b_im"][l].rearrange("(s g) p c -> (g p) s c", g=2), writes=["BI"])
        kb.dma("sp", Cn[0][:], I["s_c_re"][l].rearrange("(o g) c p -> (g c) o p", g=8), writes=["Cn0"])
        kb.dma("sp", Cn[1][:], I["s_c_im"][l].rearrange("(o g) c p -> (g c) o p", g=8), writes=["Cn1"])
        for h in range(8):
            kb.dma("sp", wstg[0][:], I["s_glu_w"][l][:, h * 128:(h + 1) * 128].rearrange("(k p) c -> p k c", p=128),
                   writes=["wstg"])
            kb.op("pool", lambda e, h=h: e.tensor_copy(out=wg[:, :, h * 128:(h + 1) * 128], in_=wstg[0][:]),
                  reads=["wstg"], writes=["wg"])
        N = lambda i: nat[:, i, :]
        kb.op("act", lambda e: e.activation(out=nat[:, 2, 2:4], in_=nat[:, 2, 0:2], func=AF.Exp), reads=["nat2"], writes=["nat2"])
        kb.op("dve", lambda e: e.tensor_copy(out=nat[:, 3, :].rearrange("s (g p) -> s g p", g=2),
                                             in_=nat[:, 2, 2:4].unsqueeze(2).to_broadcast([32, 2, 64])),
              reads=["nat2"], writes=["nat3"])
        kb.op("dve", lambda e: e.tensor_scalar(out=N(0), in0=N(0), scalar1=-1e-4, scalar2=None, op0=ALU.min),
              reads=["nat0"], writes=["nat0"])
        kb.op("dve", lambda e: e.tensor_tensor(out=N(4), in0=N(0), in1=N(3), op=ALU.mult), reads=["nat0", "nat3"], writes=["nat4"])
        kb.op("act", lambda e: e.activation(out=N(4), in_=N(4), func=AF.Exp), reads=["nat4"], writes=["nat4"])
        kb.op("dve", lambda e: e.tensor_tensor(out=N(5), in0=N(1), in1=N(3), op=ALU.mult), reads=["nat1", "nat3"], writes=["nat5"])
        self.sin_to("nat", N(6), N(5), N(12), nati[:, :], N(13), "nat6", "nat5")
        self.sin_to("nat", N(7), N(5), N(12), nati[:, :], N(13), "nat7", "nat5", shift=math.pi / 2)
        kb.op("dve", lambda e: e.tensor_tensor(out=N(8), in0=N(4), in1=N(7), op=ALU.mult), reads=["nat4", "nat7"], writes=["nat8"])
        kb.op("dve", lambda e: e.tensor_tensor(out=N(9), in0=N(4), in1=N(6), op=ALU.mult), reads=["nat4", "nat6"], writes=["nat9"])
        kb.op("dve", lambda e: e.tensor_tensor(out=N(12), in0=N(0), in1=N(0), op=ALU.mult), reads=["nat0"], writes=["nat12"])
        kb.op("dve", lambda e: e.tensor_tensor(out=N(13), in0=N(1), in1=N(1), op=ALU.mult), reads=["nat1"], writes=["nat13"])
        kb.op("dve", lambda e: e.tensor_tensor(out=N(12), in0=N(12), in1=N(13), op=ALU.add), reads=["nat12", "nat13"], writes=["nat12"])
        kb.op("dve", lambda e: e.reciprocal(out=N(12), in_=N(12)), reads=["nat12"], writes=["nat12"])
        kb.op("dve", lambda e: e.tensor_scalar(out=N(13), in0=N(8), scalar1=-1.0, scalar2=None, op0=ALU.add), reads=["nat8"], writes=["nat13"])
        kb.op("dve", lambda e: e.tensor_tensor(out=N(10), in0=N(13), in1=N(0), op=ALU.mult), reads=["nat13", "nat0"], writes=["nat10"])
        kb.op("dve", lambda e: e.tensor_tensor(out=N(11), in0=N(9), in1=N(1), op=ALU.mult), reads=["nat9", "nat1"], writes=["nat11"])
        kb.op("dve", lambda e: e.tensor_tensor(out=N(10), in0=N(10), in1=N(11), op=ALU.add), reads=["nat10", "nat11"], writes=["nat10"])
        kb.op("dve", lambda e: e.tensor_tensor(out=N(10), in0=N(10), in1=N(12), op=ALU.mult), reads=["nat10", "nat12"], writes=["nat10"])
        kb.op("dve", lambda e: e.tensor_tensor(out=N(11), in0=N(9), in1=N(0), op=ALU.mult), reads=["nat9", "nat0"], writes=["nat11"])
        kb.op("dve", lambda e: e.tensor_tensor(out=N(13), in0=N(13), in1=N(1), op=ALU.mult), reads=["nat13", "nat1"], writes=["nat13"])
        kb.op("dve", lambda e: e.tensor_tensor(out=N(11), in0=N(11), in1=N(13), op=ALU.subtract), reads=["nat11", "nat13"], writes=["nat11"])
        kb.op("dve", lambda e: e.tensor_tensor(out=N(11), in0=N(11), in1=N(12), op=ALU.mult), reads=["nat11", "nat12"], writes=["nat11"])
        p = self.rr("ptr", 2)
        pt, kpt = self.ptr[p], "ptr%d" % p
        for si, ni in enumerate((4, 5, 8, 9, 10, 11)):
            kb.op("pe", lambda e, si=si, ni=ni, pt=pt: e.transpose(out=pt[:, si * 32:(si + 1) * 32], in_=nat[:, ni, :],
                                                                  identity=self.ident[:32, :32]),
                  reads=["nat%d" % ni, "ident"], writes=[kpt])
        kb.op("dve", lambda e, pt=pt: e.tensor_copy(out=PT[:, :, :], in_=pt[:, 0:192].rearrange("p (s c) -> p s c", s=6)),
              reads=[kpt], writes=["PT"])
        bc = lambda si: PT[:, si, :].unsqueeze(2).to_broadcast([128, 32, 16])
        kb.op("dve", lambda e: e.tensor_tensor(out=bbr[:], in0=BR[:], in1=bc(4), op=ALU.mult), reads=["BR", "PT"], writes=["bbr"])
        kb.op("dve", lambda e: e.tensor_tensor(out=btmp, in0=BI[:], in1=bc(5), op=ALU.mult), reads=["BI", "PT"], writes=["xin"])
        kb.op("dve", lambda e: e.tensor_tensor(out=bbr[:], in0=bbr[:], in1=btmp, op=ALU.subtract), reads=["bbr", "xin"], writes=["bbr"])
        kb.op("dve", lambda e: e.tensor_tensor(out=bbi[:], in0=BI[:], in1=bc(4), op=ALU.mult), reads=["BI", "PT"], writes=["bbi"])
        kb.op("dve", lambda e: e.tensor_tensor(out=btmp, in0=BR[:], in1=bc(5), op=ALU.mult), reads=["BR", "PT", "bbr"], writes=["xin"])
        kb.op("dve", lambda e: e.tensor_tensor(out=bbi[:], in0=bbi[:], in1=btmp, op=ALU.add), reads=["bbi", "xin"], writes=["bbi"])
        for ri, (bb, kbb) in enumerate(((bbr, "bbr"), (bbi, "bbi"))):
            for oc in range(8):
                kb.op("dve", lambda e, bb=bb, oc=oc: e.tensor_tensor(
                    out=xin[:, :].rearrange("p (q g c) -> p q g c", q=4, g=8),
                    in0=bb[:, oc * 4:oc * 4 + 4, None, :].to_broadcast([128, 4, 8, 16]),
                    in1=m4[:, :, :, None].to_broadcast([128, 4, 8, 16]), op=ALU.mult),
                    reads=[kbb, "m4"], writes=["xin"])
                p = self.rr("ptr", 2)
                pt, kpt = self.ptr[p], "ptr%d" % p
                for q in range(4):
                    kb.op("pe", lambda e, q=q, pt=pt: e.transpose(out=pt[:, q * 128:(q + 1) * 128], in_=xin[:, q * 128:(q + 1) * 128],
                                                                  identity=self.ident[:, :]),
                          reads=["xin", "ident"], writes=[kpt])
                kb.op("act", lambda e, pt=pt, oc=oc, ri=ri: e.activation(
                    out=Bz[ri][:, oc * 4:oc * 4 + 4, :], in_=pt[:, :].rearrange("p (q m) -> p q m", q=4), func=AF.Copy),
                    reads=[kpt], writes=[("Bz", ri)])
        for ri in range(2):
            for oc in range(8):
                kb.op("dve", lambda e, ri=ri, oc=oc: e.tensor_tensor(
                    out=xin[:, :].rearrange("p (q g s) -> p q g s", q=4, g=2),
                    in0=Cn[ri][:, oc, None, None, :].to_broadcast([128, 4, 2, 64]),
                    in1=m3[:, :, :, None].to_broadcast([128, 4, 2, 64]), op=ALU.mult),
                    reads=["Cn%d" % ri, "m3"], writes=["xin"])
                p = self.rr("ptr", 2)
                pt, kpt = self.ptr[p], "ptr%d" % p
                for q in range(4):
                    kb.op("pe", lambda e, q=q, pt=pt: e.transpose(out=pt[:, q * 128:(q + 1) * 128], in_=xin[:, q * 128:(q + 1) * 128],
                                                                  identity=self.ident[:, :]),
                          reads=["xin", "ident"], writes=[kpt])
                kb.op("act", lambda e, pt=pt, oc=oc, ri=ri: e.activation(
                    out=Cz[ri][:, oc * 4:oc * 4 + 4, :], in_=pt[:, :].rearrange("p (q m) -> p q m", q=4), func=AF.Copy,
                    scale=(1.0 if ri == 0 else -1.0)),
                    reads=[kpt], writes=[("Cz", ri)])
        for g in range(16):
            for s8 in range(2):
                sc = g * 2 + s8
                kb.op("dve", lambda e, s8=s8, sc=sc: e.tensor_scalar(out=ang[:, s8, :], in0=iota1[:, :], scalar1=PT[:, 1, sc:sc + 1],
                                                                     scalar2=None, op0=ALU.mult),
                      reads=["iota1", "PT"], writes=["ang"])
            self.sin_to("ang", sinT[:, g * 2:(g + 1) * 2, :], ang[:], angf[:], angi[:], angm[:], ("sinT", g), "ang")
            self.sin_to("ang", cosT[:, g * 2:(g + 1) * 2, :], ang[:], angf[:], angi[:], angm[:], ("cosT", g), "ang",
                        shift=math.pi / 2)
        tabs = [("sinT", g) for g in range(16)] + [("cosT", g) for g in range(16)]
        kb.op("dve", lambda e: e.memset(car[0][:], 0.0), writes=[("car", 0, sc_) for sc_ in range(32)])
        kb.op("dve", lambda e: e.memset(car[1][:], 0.0), writes=[("car", 1, sc_) for sc_ in range(32)])

        if NS:
            for ri, nm in enumerate(("state_s5_re", "state_s5_im")):
                p = self.rr("ptr", 2)
                pt, kpt = self.ptr[p], "ptr%d" % p
                for qq in range(8):
                    kb.dma("sp", sst[:NS, :], I[nm][l].rearrange("b g p -> b (g p)")[:, qq * 512:(qq + 1) * 512], writes=["sst"])
                    for s8 in range(4):
                        sc = qq * 4 + s8
                        kb.op("pe", lambda e, sc=sc, s8=s8, pt=pt: e.transpose(out=pt[:, sc * NS:(sc + 1) * NS], in_=sst[:NS, s8 * 128:(s8 + 1) * 128],
                                                                      identity=self.ident[:NS, :NS]),
                              reads=["sst", "ident"], writes=[kpt])
                kb.op("dve", lambda e, pt=pt, ri=ri: e.tensor_copy(out=s0T[ri][:, :, :NS],
                                                                  in_=pt[:, :32 * NS].rearrange("p (s b) -> p s b", s=32)),
                      reads=[kpt], writes=[("s0T", ri)])

        subblocks = []
        for bi, (t0b, nb) in enumerate(self.blocks):
            for t0 in range(t0b, t0b + nb, 256):
                subblocks.append((bi, t0, min(256, t0b + nb - t0)))
        for (bi, t0, n) in subblocks:
            is_s = (t0 >= T)
            nsub = n // 128
            for oc in range(8):
                j = self.rr("uf", 2)
                kuf, kub = "uf%d" % j, "ub%d" % j
                row = O_U + oc * 128
                kb.dma("sp", uf[j][:, :n], S["zT"][row:row + 128, t0:t0 + n], reads=[("zT", row, bi)], writes=[kuf])
                kb.op("pool", lambda e, j=j: e.tensor_copy(out=ub[j][:, :n], in_=uf[j][:, :n]), reads=[kuf], writes=[kub])
                py = self.rr("ptr", 2)
                pys, kpys = self.ptr[py], "ptr%d" % py
                def sc_gen(q, W):
                    tq, bh, sh, sbf = W
                    wk = lambda n_: (n_, id(W))
                    sc = oc * 4 + q
                    pa = self.rr("pp", 4); pb = self.rr("pp", 4)
                    A, B = self.pp[pa], self.pp[pb]
                    kA, kB = "pp%d" % pa, "pp%d" % pb
                    kb.op("pe", lambda e, A=A, sc=sc, j=j: e.matmul(A[:, :n], lhsT=Bz[0][:, sc, :], rhs=ub[j][:, :n], start=True, stop=True),
                          reads=[("Bz", 0), kub], writes=[kA])
                    kb.op("pe", lambda e, B=B, sc=sc, j=j: e.matmul(B[:, :n], lhsT=Bz[1][:, sc, :], rhs=ub[j][:, :n], start=True, stop=True),
                          reads=[("Bz", 1), kub], writes=[kB])
                    yield
                    if not is_s:
                        v3 = lambda x: x[:, :n].rearrange("p (m j) -> p m j", j=128)
                        cosv = cosT[:, sc, None, :].to_broadcast([128, nsub, 128])
                        sinv = sinT[:, sc, None, :].to_broadcast([128, nsub, 128])
                        kb.op("dve", lambda e, A=A, cosv=cosv: e.tensor_tensor(out=v3(tq[0]), in0=v3(A), in1=cosv, op=ALU.mult),
                              reads=[kA] + tabs, writes=[wk("tq0")])
                        kb.op("dve", lambda e, B=B, sinv=sinv: e.tensor_tensor(out=v3(tq[1]), in0=v3(B), in1=sinv, op=ALU.mult),
                              reads=[kB] + tabs, writes=[wk("tq1")])
                        kb.op("dve", lambda e, B=B, cosv=cosv: e.tensor_tensor(out=v3(tq[2]), in0=v3(B), in1=cosv, op=ALU.mult),
                              reads=[kB] + tabs, writes=[wk("tq2")])
                        kb.op("dve", lambda e, A=A, sinv=sinv: e.tensor_tensor(out=v3(tq[3]), in0=v3(A), in1=sinv, op=ALU.mult),
                              reads=[kA] + tabs, writes=[wk("tq3")])
                        kb.op("pool", lambda e: e.tensor_tensor(out=bh[0][:, :n], in0=tq[0][:, :n], in1=tq[1][:, :n], op=ALU.add),
                              reads=[wk("tq0"), wk("tq1")], writes=[wk("bh0")])
                        kb.op("pool", lambda e: e.tensor_tensor(out=bh[1][:, :n], in0=tq[2][:, :n], in1=tq[3][:, :n], op=ALU.subtract),
                              reads=[wk("tq2"), wk("tq3")], writes=[wk("bh1")])
                        yield
                        rho = PT[:, 0, sc:sc + 1].to_broadcast([128, 128])
                        c128 = cosT[:, sc, 127:128]
                        s128 = sinT[:, sc, 127:128]
                        for m in range(nsub):
                            sl = slice(m * 128, (m + 1) * 128)
                            for ri in range(2):
                                kb.op("dve", lambda e, ri=ri, sl=sl, sc=sc, rho=rho: e.tensor_tensor_scan(
                                    out=sh[ri][:, sl], data0=rho, data1=bh[ri][:, sl], initial=car[ri][:, sc:sc + 1],
                                    op0=ALU.mult, op1=ALU.add), reads=[wk("bh%d" % ri), ("car", ri, sc), "PT"], writes=[wk("sh%d" % ri)])
                            lr = sh[0][:, m * 128 + 127:m * 128 + 128]
                            li = sh[1][:, m * 128 + 127:m * 128 + 128]
                            kb.op("dve", lambda e, li=li, s128=s128: e.tensor_scalar(out=ctmp[:, 2 * (q % 2):2 * (q % 2) + 1], in0=li, scalar1=s128, scalar2=None, op0=ALU.mult),
                                  reads=[wk("sh1")] + tabs, writes=[wk("ctmp")])
                            kb.op("dve", lambda e, li=li, c128=c128: e.tensor_scalar(out=ctmp[:, 2 * (q % 2) + 1:2 * (q % 2) + 2], in0=li, scalar1=c128, scalar2=None, op0=ALU.mult),
                                  reads=[wk("sh1")] + tabs, writes=[wk("ctmp")])
                            kb.op("dve", lambda e, lr=lr, c128=c128, sc=sc: e.scalar_tensor_tensor(
                                out=car[0][:, sc:sc + 1], in0=lr, scalar=c128, in1=ctmp[:, 2 * (q % 2):2 * (q % 2) + 1], op0=ALU.mult, op1=ALU.subtract),
                                reads=[wk("sh0"), wk("ctmp")] + tabs, writes=[("car", 0, sc)])
                            kb.op("dve", lambda e, lr=lr, s128=s128, sc=sc: e.scalar_tensor_tensor(
                                out=car[1][:, sc:sc + 1], in0=lr, scalar=s128, in1=ctmp[:, 2 * (q % 2) + 1:2 * (q % 2) + 2], op0=ALU.mult, op1=ALU.add),
                                reads=[wk("sh0"), wk("ctmp")] + tabs, writes=[("car", 1, sc)])
                        yield
                        kb.op("pool", lambda e, cosv=cosv: e.tensor_tensor(out=v3(tq[0]), in0=v3(sh[0]), in1=cosv, op=ALU.mult),
                              reads=[wk("sh0")] + tabs, writes=[wk("tq0")])
                        kb.op("pool", lambda e, sinv=sinv: e.tensor_tensor(out=v3(tq[1]), in0=v3(sh[1]), in1=sinv, op=ALU.mult),
                              reads=[wk("sh1")] + tabs, writes=[wk("tq1")])
                        kb.op("dve", lambda e, sinv=sinv: e.tensor_tensor(out=v3(tq[2]), in0=v3(sh[0]), in1=sinv, op=ALU.mult),
                              reads=[wk("sh0")] + tabs, writes=[wk("tq2")])
                        kb.op("dve", lambda e, cosv=cosv: e.tensor_tensor(out=v3(tq[3]), in0=v3(sh[1]), in1=cosv, op=ALU.mult),
                              reads=[wk("sh1")] + tabs, writes=[wk("tq3")])
                        kb.op("pool", lambda e: e.tensor_tensor(out=sbf[0][:, :n], in0=tq[0][:, :n], in1=tq[1][:, :n], op=ALU.subtract),
                              reads=[wk("tq0"), wk("tq1")], writes=[wk("sbf0")])
                        kb.op("dve", lambda e: e.tensor_tensor(out=sbf[1][:, :n], in0=tq[2][:, :n], in1=tq[3][:, :n], op=ALU.add),
                              reads=[wk("tq2"), wk("tq3")], writes=[wk("sbf1")])
                    else:
                        lbre = PT[:, 2, sc:sc + 1]
                        lbim = PT[:, 3, sc:sc + 1]
                        s0r, s0i = s0T[0][:, sc, :n], s0T[1][:, sc, :n]
                        kb.op("dve", lambda e, s0i=s0i, lbim=lbim: e.tensor_scalar(out=tq[0][:, :n], in0=s0i, scalar1=lbim, scalar2=None, op0=ALU.mult),
                              reads=[("s0T", 1), "PT"], writes=[wk("tq0")])
                        kb.op("dve", lambda e, s0r=s0r, lbre=lbre: e.scalar_tensor_tensor(out=tq[0][:, :n], in0=s0r, scalar=lbre, in1=tq[0][:, :n],
                                                                               op0=ALU.mult, op1=ALU.subtract),
                              reads=[("s0T", 0), "PT", wk("tq0")], writes=[wk("tq0")])
                        kb.op("dve", lambda e, A=A, sc=sc: e.tensor_tensor(out=snw[0][:, sc, :n], in0=A[:, :n], in1=tq[0][:, :n], op=ALU.add),
                              reads=[kA, wk("tq0")], writes=[("snw", 0)])
                        kb.op("dve", lambda e, s0r=s0r, lbim=lbim: e.tensor_scalar(out=tq[1][:, :n], in0=s0r, scalar1=lbim, scalar2=None, op0=ALU.mult),
                              reads=[("s0T", 0), "PT"], writes=[wk("tq1")])
                        kb.op("dve", lambda e, s0i=s0i, lbre=lbre: e.scalar_tensor_tensor(out=tq[1][:, :n], in0=s0i, scalar=lbre, in1=tq[1][:, :n],
                                                                               op0=ALU.mult, op1=ALU.add),
                              reads=[("s0T", 1), "PT", wk("tq1")], writes=[wk("tq1")])
                        kb.op("dve", lambda e, B=B, sc=sc: e.tensor_tensor(out=snw[1][:, sc, :n], in0=B[:, :n], in1=tq[1][:, :n], op=ALU.add),
                              reads=[kB, wk("tq1")], writes=[("snw", 1)])
                        for ri in range(2):
                            kb.op("pool", lambda e, ri=ri, sc=sc: e.tensor_copy(out=sbf[ri][:, :n], in_=snw[ri][:, sc, :n]),
                                  reads=[("snw", ri)], writes=[wk("sbf%d" % ri)])
                    yield
                    for ri in range(2):
                        kb.op("pe", lambda e, ri=ri, sc=sc, q=q, pys=pys: e.matmul(pys[:, :n], lhsT=Cz[ri][:, sc, :], rhs=sbf[ri][:, :n],
                                                                                start=(q == 0 and ri == 0), stop=(q == 3 and ri == 1)),
                              reads=[("Cz", ri), wk("sbf%d" % ri)], writes=[kpys])
                for qp in range(2):
                    alive = [sc_gen(2 * qp + w_, WS[w_]) for w_ in range(2)]
                    while alive:
                        for g_ in list(alive):
                            try:
                                next(g_)
                            except StopIteration:
                                alive.remove(g_)
                y0, y1 = yt[0], yt[1]
                kb.op("dve", lambda e, j=j, oc=oc, pys=pys: e.scalar_tensor_tensor(out=y0[:, :n], in0=uf[j][:, :n], scalar=dcol[:, oc:oc + 1],
                                                                             in1=pys[:, :n], op0=ALU.mult, op1=ALU.add),
                      reads=[kuf, "dcol", kpys], writes=["yt0"])
                kb.op("act", lambda e: e.activation(out=y1[:, :n], in_=y0[:, :n], func=AF.Square), reads=["yt0"], writes=["yt1"])
                kb.op("dve", lambda e: e.tensor_scalar(out=y1[:, :n], in0=y1[:, :n], scalar1=0.044715, scalar2=1.0, op0=ALU.mult, op1=ALU.add),
                      reads=["yt1"], writes=["yt1"])
                kb.op("dve", lambda e: e.tensor_tensor(out=y1[:, :n], in0=y1[:, :n], in1=y0[:, :n], op=ALU.mult), reads=["yt1", "yt0"], writes=["yt1"])
                kb.op("act", lambda e: e.activation(out=y1[:, :n], in_=y1[:, :n], func=AF.Sigmoid, scale=2.0 * math.sqrt(2.0 / math.pi)),
                      reads=["yt1"], writes=["yt1"])
                kb.op("dve", lambda e, oc=oc: e.tensor_tensor(out=ysg[:, oc, :n], in0=y1[:, :n], in1=y0[:, :n], op=ALU.mult),
                      reads=["yt1", "yt0"], writes=[("ysg", oc)])
                kb.op("pool", lambda e, oc=oc: e.tensor_copy(out=ysb[:, oc, :n], in_=ysg[:, oc, :n]), reads=[("ysg", oc)], writes=[("ysb", oc)])
            for ec in range(8):
                p = self.rr("pp", 4)
                pp, kpp = self.pp[p], "pp%d" % p
                for kc in range(KC):
                    kb.op("pe", lambda e, kc=kc, ec=ec, pp=pp: e.matmul(pp[:, :n], lhsT=wg[:, kc, ec * 128:(ec + 1) * 128], rhs=ysb[:, kc, :n],
                                                                      start=(kc == 0), stop=(kc == KC - 1)),
                          reads=["wg"] + [("ysb", k) for k in range(KC)], writes=[kpp])
                zj = self.rr("zst", 2)
                kz = "zst%d" % zj
                row = O_ZS + ec * 128
                kb.dma("sp", zst[zj][:, :n], S["zT"][row:row + 128, t0:t0 + n], reads=[("zT", row, bi)], writes=[kz])
                kb.op("act", lambda e, zj=zj: e.activation(out=zst[zj][:, :n], in_=zst[zj][:, :n], func=AF.Silu), reads=[kz], writes=[kz])
                kb.op("act", lambda e, pp=pp, ec=ec: e.activation(out=yt[0][:, :n], in_=pp[:, :n], func=AF.Sigmoid, bias=gbcol[:, ec:ec + 1]),
                      reads=[kpp, "gbcol"], writes=["yt0"])
                kb.op("dve", lambda e, ec=ec: e.tensor_tensor(out=yt[0][:, :n], in0=yt[0][:, :n], in1=ysg[:, ec, :n], op=ALU.mult),
                      reads=["yt0", ("ysg", ec)], writes=["yt0"])
                aj = self.rr("aout", 2)
                kb.op("dve", lambda e, aj=aj, zj=zj: e.tensor_tensor(out=aout[aj][:, :n], in0=yt[0][:, :n], in1=zst[zj][:, :n], op=ALU.mult),
                      reads=["yt0", kz], writes=["aout%d" % aj])
                kb.dma("pool", S["act_s"][ec * 128:(ec + 1) * 128, t0:t0 + n], aout[aj][:, :n],
                       reads=["aout%d" % aj], writes=[("act", "s", bi)])
        for ri, nm in enumerate(("s5_re", "s5_im")):
            p = self.rr("ptr", 2)
            pt, kpt = self.ptr[p], "ptr%d" % p
            kb.op("pe", lambda e, pt=pt, ri=ri: e.transpose(out=pt[:32, 0:128], in_=car[ri][:, :], identity=self.ident[:, :]),
                  reads=[("car", ri, sc_) for sc_ in range(32)] + ["ident"], writes=[kpt])
            kb.op("dve", lambda e, pt=pt: e.tensor_copy(out=sst[:32, 0:128], in_=pt[:32, 0:128]), reads=[kpt], writes=["sst"])
            kb.dma("pool", O[nm + "_prompt"][l].rearrange("(s q) -> s q", q=128), sst[:32, 0:128], reads=["sst"], writes=[],
                   is_output=True)
            if NS:
                for g in range(8):
                    p = self.rr("ptr", 2)
                    pt, kpt = self.ptr[p], "ptr%d" % p
                    for s4 in range(4):
                        sc = g * 4 + s4
                        kb.op("pe", lambda e, pt=pt, ri=ri, sc=sc, s4=s4: e.transpose(out=pt[:NS, s4 * 128:(s4 + 1) * 128], in_=snw[ri][:, sc, :NS],
                                                                                   identity=self.ident[:, :]),
                              reads=[("snw", ri), "ident"], writes=[kpt])
                    kb.op("dve", lambda e, pt=pt, g=g: e.tensor_copy(out=sst[:NS, 0:512], in_=pt[:NS, :]), reads=[kpt], writes=["sst"])
                    kb.dma("pool", O[nm + "_sample"][l].rearrange("b g p -> b (g p)")[:, g * 512:(g + 1) * 512], sst[:NS, 0:512],
                           reads=["sst"], writes=[], is_output=True)
        self.end_phase()

    def load_sq(self, name, dram):
        kb = self.kb
        dst = self.wsq[name]
        for h in range(2):
            i = self.rr("w", 2)
            ws, kws = self.wst[i], "wst%d" % i
            kb.dma("sp", ws[:, :, :512], dram[:, h * 512:(h + 1) * 512].rearrange("(k p) c -> p k c", p=128),
                   writes=[kws])
            kb.op("pool", lambda e, ws=ws, h=h: e.tensor_copy(out=dst[:, :, h * 512:(h + 1) * 512], in_=ws[:, :, :512]),
                  reads=[kws], writes=[("wsq", name)])

    def phase3(self, l):
        kb = self.kb
        I, S = self.I, self.S
        last = (l == self.DEPTH - 1)
        self.begin_phase()
        self.wst = [self.sbp("wst%d" % i, (128, KC, 512)) for i in range(2)]
        self.wsq = {n: self.sbp("wsq_" + n, (128, KC, D), BF16) for n in ("bm", "br", "bs", "out")}
        self.actb = [self.sbp("actb%d" % i, (128, KC, 512), BF16) for i in range(2)]
        self.gmt = [self.sbp("gmt%d" % i, (128, 512)) for i in range(2)]
        self.mrg = self.sbp("mrg", (128, KC, 512), BF16)
        self.macc = self.sbp("macc", (128, KC, 512))
        self.ln_alloc()
        for nme in ("bm", "br", "bs", "out"):
            self.load_sq(nme, I["w_" + nme][l])
        self.load_ln_params(I["ln_g"][l], I["ln_b"][l])
        for bi, (t0, n) in enumerate(self.blocks):
            for ec in range(KC):
                for b, bn in enumerate("mrs"):
                    pass
            acts = {}
            for b, bn in enumerate("mrs"):
                if not self.have_branch(bn):
                    continue
                j = self.rr("actb", 2)
                at, kat = self.actb[j], "actb%d" % j
                kb.dma("sp", at[:, :, :n], S["act_" + bn][:, t0:t0 + n].rearrange("(k p) t -> p k t", p=128),
                       reads=[("act", bn, bi)], writes=[kat])
                for ec in range(KC):
                    p = self.rr("pp", 4)
                    pp, kpp = self.pp[p], "pp%d" % p
                    for kc in range(KC):
                        kb.op("pe", lambda e, kc=kc, pp=pp, ec=ec, at=at, bn=bn: e.matmul(
                            pp[:, :n], lhsT=self.wsq["b" + bn][:, kc, ec * 128:(ec + 1) * 128], rhs=at[:, kc, :n],
                            start=(kc == 0), stop=(kc == KC - 1)),
                            reads=[kat, ("wsq", "b" + bn)], writes=[kpp])
                    g = self.rr("gmt", 2)
                    gt, kgt = self.gmt[g], "gmt%d" % g
                    row = O_GM + b * D + ec * 128
                    kb.dma("sp", gt[:, :n], S["zT"][row:row + 128, t0:t0 + n],
                           reads=[("zT", row, bi)], writes=[kgt])
                    kb.op("act", lambda e, gt=gt: e.activation(out=gt[:, :n], in_=gt[:, :n], func=AF.Sigmoid),
                          reads=[kgt], writes=[kgt])
                    first = (bn == self.first_branch())
                    lastb = (bn == self.last_branch())
                    kmf = ("mrgf", ec)
                    acc = self.macc[:, ec, :n]
                    if first:
                        kb.op("dve", lambda e, acc=acc, gt=gt, pp=pp: e.tensor_tensor(out=acc, in0=pp[:, :n], in1=gt[:, :n],
                                                                                 op=ALU.mult),
                              reads=[kpp, kgt], writes=[("macc", ec)])
                    else:
                        kb.op("dve", lambda e, gt=gt, pp=pp: e.tensor_tensor(out=gt[:, :n], in0=pp[:, :n], in1=gt[:, :n],
                                                                        op=ALU.mult),
                              reads=[kpp, kgt], writes=[kgt])
                        kb.op("pool", lambda e, acc=acc, gt=gt: e.tensor_tensor(out=acc, in0=acc, in1=gt[:, :n], op=ALU.add),
                              reads=[kgt, ("macc", ec)], writes=[("macc", ec)])
                    if lastb:
                        kb.op("pool", lambda e, acc=acc, ec=ec: e.tensor_copy(out=self.mrg[:, ec, :n], in_=acc),
                              reads=[("macc", ec)], writes=[("mrg", ec)])
            for tt in range(0, n, 128):
                nr = min(128, n - tt)
                r0 = t0 + tt
                ti = r0 // 128
                j = self.rr("xt", 2)
                xt, kxt = self.xt[j], "xt%d" % j
                kb.dma("sp", xt[:nr, :], S["xs"][r0:r0 + nr, :], reads=[("xs", ti)], writes=[kxt])
                if self.first_branch() is not None:
                    for h in range(2):
                        p = self.rr("pp", 4)
                        pp, kpp = self.pp[p], "pp%d" % p
                        for kc in range(KC):
                            kb.op("pe", lambda e, kc=kc, pp=pp, h=h, tt=tt, nr=nr: e.matmul(
                                pp[:nr, :], lhsT=self.mrg[:, kc, tt:tt + nr], rhs=self.wsq["out"][:, kc, h * 512:(h + 1) * 512],
                                start=(kc == 0), stop=(kc == KC - 1)),
                                reads=[("mrg", k) for k in range(KC)] + [("wsq", "out")], writes=[kpp])
                        kb.op("dve", lambda e, xt=xt, pp=pp, h=h, nr=nr: e.scalar_tensor_tensor(
                            out=xt[:nr, h * 512:(h + 1) * 512], in0=xt[:nr, h * 512:(h + 1) * 512], scalar=ALPHA,
                            in1=pp[:nr, :], op0=ALU.mult, op1=ALU.add), reads=[kxt, kpp], writes=[kxt])
                else:
                    kb.op("dve", lambda e, xt=xt, nr=nr: e.tensor_scalar(out=xt[:nr, :], in0=xt[:nr, :], scalar1=ALPHA,
                                                                         scalar2=None, op0=ALU.mult),
                          reads=[kxt], writes=[kxt])
                fo = None
                if last:
                    fo = self.O["y_prompt"][r0:r0 + nr, :] if r0 < self.T else self.O["y_sample"][:, :]
                self.ln_tile(xt, kxt, r0, nr, ti, S["xs"][r0:r0 + nr, :], final_out=fo)
        self.end_phase()

    branches = ""
    NPI = 4
    CORE_BF16 = True

    def have_branch(self, bn):
        return bn in self.branches

    def first_branch(self):
        return self.branches[0] if self.branches else None

    def last_branch(self):
        return self.branches[-1] if self.branches else None


def make_in_map(inputs, c, NS, T, consts):
    m = {}
    m["x_prompt"] = np.ascontiguousarray(inputs["x_prompt"][c, :T])
    m["x_sample"] = np.ascontiguousarray(inputs["x_sample"][c * NS:(c + 1) * NS, 0])
    for n in ("ln_in_g", "ln_in_b", "w_in", "w_out", "ln_g", "ln_b", "w_bm", "w_br", "w_bs", "s_lam_re", "s_lam_im",
              "s_log_dt", "s_b_re", "s_b_im", "s_c_re", "s_c_im", "s_d", "s_glu_w", "s_glu_b",
              "m_conv_w", "m_conv_b", "m_wq", "m_wk", "m_wv", "m_ig_b", "m_fg_b", "m_norm_g", "m_skip",
              "r_mu", "r_w0", "r_w2", "r_a0", "r_a2", "r_k_k", "r_k_a", "r_r_k", "r_ln_g", "r_ln_b"):
        m[n] = np.ascontiguousarray(inputs[n])
    for n in ("state_s5_re", "state_s5_im", "state_mlstm_conv", "state_mlstm_c", "state_mlstm_n", "state_mlstm_m",
              "state_rwkv_wkv", "state_rwkv_shift"):
        m[n] = np.ascontiguousarray(inputs[n][:, c * NS:(c + 1) * NS])
    m.update(consts)
    return m


def kernel(**inputs):
    inputs = {k: np.asarray(v) for k, v in inputs.items()}
    T, NS, L = 2048, 16, 4
    Prog.branches = BRANCHES
    prog = Prog(T, NS, L)
    nc = prog.build()
    consts = host_consts()
    in_maps = [make_in_map(inputs, c, NS, T, consts) for c in range(8)]
    res = run_bass_kernel_spmd(nc, in_maps, core_ids=list(range(8)))
    rs = res.results
    f = np.float32

    def pstack(name, shape):
        if name in rs[0]:
            return np.stack([np.asarray(rs[c][name]).reshape((L,) + shape) for c in range(8)], 1).astype(f)
        return np.zeros((L, 8) + shape, f)

    def sstack(name, shape):
        if name in rs[0]:
            return np.concatenate([np.asarray(rs[c][name]).reshape((L, NS) + shape) for c in range(8)], 1).astype(f)
        return np.zeros((L, 8 * NS) + shape, f)

    y_prompt = np.stack([rs[c]["y_prompt"] for c in range(8)], 0).astype(f)
    y_sample = np.concatenate([rs[c]["y_sample"] for c in range(8)], 0)[:, None, :].astype(f)
    return (y_prompt, y_sample,
            pstack("c_prompt", (4, 256, 256)), sstack("c_sample", (4, 256, 256)),
            pstack("n_prompt", (4, 256)), sstack("n_sample", (4, 256)),
            pstack("m_prompt", (4,)), sstack("m_sample", (4,)),
            pstack("conv_prompt", (3, D)), sstack("conv_sample", (3, D)),
            pstack("wkv_prompt", (16, 64, 64)), sstack("wkv_sample", (16, 64, 64)),
            pstack("shift_prompt", (R_SHIFT_W,)), sstack("shift_sample", (R_SHIFT_W,)),
            pstack("s5_re_prompt", (64, 64)), sstack("s5_re_sample", (64, 64)),
            pstack("s5_im_prompt", (64, 64)), sstack("s5_im_sample", (64, 64)))
```

```python
import math
from contextlib import ExitStack
import numpy as np
import concourse.bass as bass
import concourse.mybir as mybir
from concourse.bass_utils import run_bass_kernel_spmd

F32 = mybir.dt.float32
BF16 = mybir.dt.bfloat16
I32 = mybir.dt.int32
ALU = mybir.AluOpType
AF = mybir.ActivationFunctionType
AX = mybir.AxisListType

D = 1024
KC = 8
N_IN = 12424
M_HEADS = 4
M_HD = 256
R_HEADS = 16
R_HD = 64
R_SHIFT_W = 3200
S_GROUPS = 64
S_STATE = 64
DEPTH_FULL = 4
ALPHA = (2.0 * DEPTH_FULL) ** 0.25
LN_EPS = 1e-5
RWKV_GN_EPS = 64e-5
NEG = -1e30

O_XM, O_IG, O_FG, O_OG, O_ZM = 0, 1024, 1028, 1032, 2056
O_RC, O_ZR, O_U, O_ZS, O_GM = 3080, 6280, 7304, 8328, 9352


class KB:
    def __init__(self, nc, es):
        self.nc = nc
        self.eng = dict(pe=nc.tensor, act=nc.scalar, dve=nc.vector, pool=nc.gpsimd, sp=nc.sync)
        self.stream = {e: [] for e in self.eng}
        self.csem = {e: es.enter_context(nc.semaphore("c_" + e)) for e in ("pe", "act", "dve", "pool")}
        self.ccnt = {e: 0 for e in self.csem}
        self.ring = {}
        for q, n in (("sp", 24), ("pool", 12), ("act", 6)):
            self.ring[q] = [[es.enter_context(nc.semaphore("d_%s%d" % (q, i))), 0] for i in range(n)]
        self.rpos = {q: 0 for q in self.ring}
        self.known = {e: {} for e in self.eng}
        self.lastw = {}
        self.readers = {}
        self.out_tokens = []
        self.n_ops = 0

    def _need(self, e, tok, same_ok=False):
        if tok is None:
            return
        sem, val, owner = tok
        if owner == e and (e == "pe" or same_ok):
            return
        k = id(sem)
        if self.known[e].get(k, 0) >= val:
            return
        self.known[e][k] = val
        self.eng[e].wait_ge(sem, val)

    def _deps(self, e, reads, writes):
        for b in reads:
            self._need(e, self.lastw.get(b))
        for b in writes:
            self._need(e, self.lastw.get(b))
            for tok in self.readers.get(b, ()):
                self._need(e, tok)

    def _commit(self, tok, reads, writes):
        for b in writes:
            self.lastw[b] = tok
            self.readers[b] = []
        for b in reads:
            self.readers.setdefault(b, []).append(tok)

    def op(self, e, fn, reads=(), writes=()):
        self._deps(e, reads, writes)
        self.ccnt[e] += 1
        tok = (self.csem[e], self.ccnt[e], e)
        fn(self.eng[e]).then_inc(self.csem[e], 1)
        self._commit(tok, reads, writes)
        self.n_ops += 1

    def dma(self, q, out, in_, reads=(), writes=(), is_output=False, **kw):
        self._deps(q, reads, writes)
        ring = self.ring[q]
        slot = ring[self.rpos[q] % len(ring)]
        self.rpos[q] += 1
        sem = slot[0]
        if slot[1] > 0:
            self._need(q, (sem, slot[1], None))
        slot[1] += 16
        tok = (sem, slot[1], None)
        self.eng[q].dma_start(out=out, in_=in_, **kw).then_inc(sem, 16)
        self._commit(tok, reads, writes)
        if is_output:
            self.out_tokens.append(tok)
        self.n_ops += 1

    def finish(self):
        for q in self.ring:
            for sem, val in self.ring[q]:
                if val > 0:
                    self._need("sp", (sem, val, None))
        for e in self.csem:
            if self.ccnt[e] > 0:
                self._need("sp", (self.csem[e], self.ccnt[e], e))

    def barrier(self):
        for e in self.eng:
            for q in self.ring:
                for sem, val in self.ring[q]:
                    if val > 0:
                        self._need(e, (sem, val, None))
            for e2 in self.csem:
                if self.ccnt[e2] > 0 and e2 != e:
                    self._need(e, (self.csem[e2], self.ccnt[e2], e2))

    def emit(self):
        pass


BRANCHES = "mrs"


def host_consts():
    c = {}
    c["ident"] = np.eye(128, dtype=np.float32)
    m3 = np.zeros((128, 4, 2), np.float32)
    m4 = np.zeros((128, 4, 8), np.float32)
    for p in range(128):
        g8 = p // 16
        gl = p // 64
        for q in range(4):
            for g in range(2):
                if g8 == 2 * q + g:
                    m3[p, q, g] = 1.0
            for gg in range(8):
                if gg == 2 * q + gl:
                    m4[p, q, gg] = 1.0
    c["mask3"], c["mask4"] = m3, m4
    selh = np.zeros((4, 4, 128), np.float32)
    for h in range(4):
        selh[h, h, :] = 1.0
    c["selh"] = selh
    c["ones4"] = np.ones((4, 128), np.float32)
    cn = np.zeros((128, 128), np.float32)
    for s_ in range(128):
        cn[s_, :s_] = 1.0e30
    c["causneg"] = cn
    rm = np.zeros((128, 384), np.float32)
    for p in range(128):
        for q in range(128):
            if p // 64 == q // 64:
                s_, t_ = p % 64, q % 64
                rm[p, q] = 1.0 if s_ < t_ else 0.0
                rm[p, 128 + q] = 1.0 if s_ <= t_ else 0.0
                rm[p, 256 + q] = 1.0 if s_ > t_ else 0.0
    c["rmasks"] = rm
    c["iota1"] = np.tile(np.arange(1, 129, dtype=np.float32)[None, :], (128, 1))
    return c


class Prog:
    def __init__(self, T, NS, DEPTH):
        self.T, self.NS, self.DEPTH = T, NS, DEPTH
        self.NT = T + NS
        assert T % 128 == 0
        self.ntile = T // 128
        self.tiles = [(i * 128, 128) for i in range(self.ntile)] + [(T, NS)]
        self.blocks = []
        t = 0
        while t < T:
            n = min(512, T - t)
            self.blocks.append((t, n))
            t += n
        self.blocks.append((T, NS))

    def build(self):
        nc = bass.Bass("TRN2", target_bir_lowering=False)
        self.nc = nc
        T, NS, L, NT = self.T, self.NS, self.DEPTH, self.NT
        dt = nc.dram_tensor

        def inp(name, shape, dtype=F32):
            return dt(name, list(shape), dtype, kind="ExternalInput").ap()

        def outp(name, shape):
            return dt(name, list(shape), F32, kind="ExternalOutput").ap()

        def scr(name, shape, dtype=F32):
            return dt(name, list(shape), dtype, kind="Internal").ap()

        I = {}
        I["x_prompt"] = inp("x_prompt", (T, D))
        I["x_sample"] = inp("x_sample", (NS, D))
        I["ident"] = inp("ident", (128, 128))
        for n, s in (("ln_in_g", (D,)), ("ln_in_b", (D,)), ("w_in", (L, D, N_IN)), ("w_out", (L, D, D)),
                     ("ln_g", (L, D)), ("ln_b", (L, D)), ("w_bm", (L, D, D)), ("w_br", (L, D, D)),
                     ("w_bs", (L, D, D)), ("s_lam_re", (L, 64, 64)), ("s_lam_im", (L, 64, 64)), ("s_log_dt", (L, 64)),
                     ("s_b_re", (L, 64, 64, 16)), ("s_b_im", (L, 64, 64, 16)), ("s_c_re", (L, 64, 16, 64)),
                     ("s_c_im", (L, 64, 16, 64)), ("s_d", (L, D)), ("s_glu_w", (L, D, D)), ("s_glu_b", (L, D)),
                     ("state_s5_re", (L, NS, 64, 64)), ("state_s5_im", (L, NS, 64, 64)),
                     ("state_mlstm_conv", (L, NS, 3, D)), ("state_mlstm_c", (L, NS, 4, 256, 256)),
                     ("state_mlstm_n", (L, NS, 4, 256)), ("state_mlstm_m", (L, NS, 4)),
                     ("m_conv_w", (L, 4, D)), ("m_conv_b", (L, D)), ("m_wq", (L, 4, 256, 256)), ("m_wk", (L, 4, 256, 256)),
                     ("m_wv", (L, 4, 256, 256)), ("m_ig_b", (L, 4)), ("m_fg_b", (L, 4)), ("m_norm_g", (L, D)), ("m_skip", (L, D)),
                     ("state_rwkv_wkv", (L, NS, 16, 64, 64)), ("state_rwkv_shift", (L, NS, R_SHIFT_W)),
                     ("r_mu", (L, R_SHIFT_W)), ("r_w0", (L, D)), ("r_w2", (L, 64, D)), ("r_a0", (L, D)), ("r_a2", (L, 64, D)),
                     ("r_k_k", (L, D)), ("r_k_a", (L, D)), ("r_r_k", (L, 16, 64)), ("r_ln_g", (L, D)), ("r_ln_b", (L, D)),
                     ("rmasks", (128, 384)),
                     ("selh", (4, 4, 128)), ("causneg", (128, 128)), ("ones4", (4, 128)),
                     ("mask3", (128, 4, 2)), ("mask4", (128, 4, 8)), ("iota1", (128, 128))):
            I[n] = inp(n, s)
        self.I = I
        O = {}
        O["y_prompt"] = outp("y_prompt", (T, D))
        O["y_sample"] = outp("y_sample", (NS, D))
        O["c_prompt"] = outp("c_prompt", (L, 4, 256, 256))
        O["c_sample"] = outp("c_sample", (L, NS, 4, 256, 256))
        O["n_prompt"] = outp("n_prompt", (L, 4, 256))
        O["n_sample"] = outp("n_sample", (L, NS, 4, 256))
        O["m_prompt"] = outp("m_prompt", (L, 4))
        O["m_sample"] = outp("m_sample", (L, NS, 4))
        O["conv_prompt"] = outp("conv_prompt", (L, 3, D))
        O["conv_sample"] = outp("conv_sample", (L, NS, 3, D))
        O["wkv_prompt"] = outp("wkv_prompt", (L, 16, 64, 64))
        O["wkv_sample"] = outp("wkv_sample", (L, NS, 16, 64, 64))
        O["shift_prompt"] = outp("shift_prompt", (L, R_SHIFT_W))
        O["shift_sample"] = outp("shift_sample", (L, NS, R_SHIFT_W))
        for nm in ("s5_re", "s5_im"):
            O[nm + "_prompt"] = outp(nm + "_prompt", (L, 4096))
            O[nm + "_sample"] = outp(nm + "_sample", (L, NS, 64, 64))
        self.O = O
        S = {}
        S["xs"] = scr("xs", (NT, D))
        S["z"] = scr("z", (NT, N_IN))
        S["zT"] = scr("zT", (N_IN, NT))
        for b in "mrs":
            S["act_" + b] = scr("act_" + b, (D, NT), BF16)
        S["vs"] = scr("vs", (NT, D))
        S["ys"] = scr("ys", (NT, D))
        self.S = S

        with ExitStack() as es:
            self.es = es
            kb = KB(nc, es)
            self.kb = kb
            self.alloc()
            self.phase0()
            for l in range(L):
                self.phase1(l)
                self.easy_states(l)
                if self.have_branch("m"):
                    self.mlstm_phase(l)
                if self.have_branch("r"):
                    self.rwkv_phase(l)
                if self.have_branch("s"):
                    self.s5_phase(l)
                self.phase3(l)
            kb.finish()
            kb.emit()
        return nc

    def sb(self, name, shape, dtype=F32):
        return self.es.enter_context(self.nc.sbuf_tensor("sb_" + name, list(shape), dtype))

    def sbp(self, name, shape, dtype=F32):
        self.uid = getattr(self, "uid", 0) + 1
        return self.pes.enter_context(self.nc.sbuf_tensor("sp%d_%s" % (self.uid, name), list(shape), dtype))

    def begin_phase(self):
        self.pes = ExitStack()

    def end_phase(self):
        self.kb.barrier()
        self.pes.close()

    def ps(self, name, shape, dtype=F32):
        return self.es.enter_context(self.nc.psum_tensor("ps_" + name, list(shape), dtype))

    def alloc(self):
        NT = self.NT
        self.ident = self.sb("ident", (128, 128))
        self.xT = self.sb("xT", (128, KC, NT), BF16)
        self.pp = [self.ps("pp%d" % i, (128, 512)) for i in range(4)]
        self.ptr = [self.ps("ptr%d" % i, (128, 512)) for i in range(2)]
        self.pex = [self.ps("pex%d" % i, (128, 512)) for i in range(2)]
        self.cnt = {}
        kb = self.kb
        kb.dma("sp", self.ident[:], self.I["ident"], writes=["ident"])

    def rr(self, key, n):
        v = self.cnt.get(key, 0)
        self.cnt[key] = v + 1
        return v % n

    def ln_alloc(self):
        self.gbc = self.sbp("gbc", (128, D))
        self.bbc = self.sbp("bbc", (128, D))
        self.xt = [self.sbp("xt%d" % i, (128, D)) for i in range(2)]
        self.xc = [self.sbp("xc%d" % i, (128, D)) for i in range(2)]
        self.st = [self.sbp("st%d" % i, (128, 8)) for i in range(2)]

    def ln_tile(self, src, srckey, row0, nr, ti, xs_out, final_out=None):
        kb = self.kb
        i = self.rr("ln", 2)
        xc, st = self.xc[i], self.st[i]
        kxc, kst = "xc%d" % i, "st%d" % i
        kb.op("dve", lambda e: e.tensor_reduce(out=st[:nr, 0:1], in_=src[:nr, :], axis=AX.X, op=ALU.add),
              reads=[srckey], writes=[kst])
        kb.op("dve", lambda e: e.tensor_scalar(out=st[:nr, 1:2], in0=st[:nr, 0:1], scalar1=-1.0 / D, scalar2=None,
                                               op0=ALU.mult), reads=[kst], writes=[kst])
        kb.op("dve", lambda e: e.tensor_scalar(out=xc[:nr, :], in0=src[:nr, :], scalar1=st[:nr, 1:2], scalar2=None,
                                               op0=ALU.add), reads=[srckey, kst], writes=[kxc])
        j = self.rr("tmpsq", 2)
        sq = self.xt[j]
        ksq = "xt%d" % j
        kb.op("act", lambda e: e.activation(out=sq[:nr, :], in_=xc[:nr, :], func=AF.Square, accum_out=st[:nr, 2:3]),
              reads=[kxc], writes=[ksq, kst])
        kb.op("act", lambda e: e.activation(out=st[:nr, 3:4], in_=st[:nr, 2:3], func=AF.Sqrt, scale=1.0 / D,
                                            bias=self.epsc[:nr, 0:1]), reads=[kst, "epsc"], writes=[kst])
        kb.op("dve", lambda e: e.reciprocal(out=st[:nr, 4:5], in_=st[:nr, 3:4]), reads=[kst], writes=[kst])
        kb.op("dve", lambda e: e.scalar_tensor_tensor(out=xc[:nr, :], in0=xc[:nr, :], scalar=st[:nr, 4:5],
                                                      in1=self.gbc[:nr, :], op0=ALU.mult, op1=ALU.mult),
              reads=[kxc, kst, "gbc"], writes=[kxc])
        kb.op("dve", lambda e: e.tensor_tensor(out=xc[:nr, :], in0=xc[:nr, :], in1=self.bbc[:nr, :], op=ALU.add),
              reads=[kxc, "bbc"], writes=[kxc])
        kb.dma("pool", xs_out, xc[:nr, :], reads=[kxc], writes=[("xs", ti)])
        if final_out is not None:
            kb.dma("pool", final_out, xc[:nr, :], reads=[kxc], writes=[], is_output=True)
        for half in range(2):
            p = self.rr("ptr", 2)
            pt, kpt = self.ptr[p], "ptr%d" % p
            for k4 in range(4):
                kc = half * 4 + k4
                kb.op("pe", lambda e, kc=kc, k4=k4, pt=pt: e.transpose(out=pt[:, k4 * 128:k4 * 128 + nr],
                                                                      in_=xc[:nr, kc * 128:(kc + 1) * 128],
                                                                      identity=self.ident[:nr, :nr]),
                      reads=[kxc, "ident"], writes=[kpt])
            dst = self.xT[:, half * 4:half * 4 + 4, row0:row0 + nr]
            srcp = pt[:, :].rearrange("p (k t) -> p k t", k=4)[:, :, :nr]
            eng = "act" if half == 0 else "dve"
            if eng == "act":
                kb.op("act", lambda e, dst=dst, srcp=srcp: e.activation(out=dst, in_=srcp, func=AF.Copy),
                      reads=[kpt], writes=[("xT", ti)])
            else:
                kb.op("dve", lambda e, dst=dst, srcp=srcp: e.tensor_copy(out=dst, in_=srcp),
                      reads=[kpt], writes=[("xT", ti)])

    def load_ln_params(self, g_ap, b_ap):
        kb = self.kb
        kb.dma("sp", self.gbc[:], g_ap.partition_broadcast(128), writes=["gbc"])
        kb.dma("sp", self.bbc[:], b_ap.partition_broadcast(128), writes=["bbc"])

    def phase0(self):
        kb = self.kb
        self.epsc = self.sb("epsc", (128, 1))
        kb.op("dve", lambda e: e.memset(self.epsc[:], LN_EPS), writes=["epsc"])
        self.onesf = self.sb("onesf", (128, 64))
        kb.op("dve", lambda e: e.memset(self.onesf[:], 1.0), writes=["onesf"])
        self.begin_phase()
        self.ln_alloc()
        self.load_ln_params(self.I["ln_in_g"], self.I["ln_in_b"])
        for ti, (r0, nr) in enumerate(self.tiles):
            j = self.rr("xt", 2)
            xt, kxt = self.xt[j], "xt%d" % j
            src = self.I["x_prompt"][r0:r0 + nr, :] if r0 < self.T else self.I["x_sample"][:, :]
            kb.dma("sp", xt[:nr, :], src, writes=[kxt])
            self.ln_tile(xt, kxt, r0, nr, ti, self.S["xs"][r0:r0 + nr, :])
        self.end_phase()

    def load_w(self, dram_cols, width):
        kb = self.kb
        i = self.rr("w", 2)
        ws, wb = self.wst[i], self.wbf[i]
        kws, kwb = "wst%d" % i, "wbf%d" % i
        kb.dma("sp", ws[:, :, :width], dram_cols.rearrange("(k p) c -> p k c", p=128), writes=[kws])
        kb.op("pool", lambda e: e.tensor_copy(out=wb[:, :, :width], in_=ws[:, :, :width]), reads=[kws], writes=[kwb])
        return wb, kwb

    def evac(self, dst, src, reads, writes):
        kb = self.kb
        if self.rr("evac", 2) == 0:
            kb.op("act", lambda e: e.activation(out=dst, in_=src, func=AF.Copy), reads=reads, writes=writes)
        else:
            kb.op("dve", lambda e: e.tensor_copy(out=dst, in_=src), reads=reads, writes=writes)

    def phase1(self, l):
        kb = self.kb
        self.begin_phase()
        self.wst = [self.sbp("wst%d" % i, (128, KC, 512)) for i in range(2)]
        self.wbf = [self.sbp("wbf%d" % i, (128, KC, 512), BF16) for i in range(2)]
        self.ev = [self.sbp("ev%d" % i, (128, 512)) for i in range(4)]
        W = self.I["w_in"][l]
        tm_segs = [(O_OG, O_ZM), (O_RC, O_U)]
        fm_segs = [(O_XM, O_IG), (O_IG, O_FG), (O_FG, O_OG), (O_ZM, O_RC), (O_U, O_ZS), (O_ZS, O_GM), (O_GM, N_IN)]
        for (c0, c1) in tm_segs:
            c = c0
            while c < c1:
                w = min(512, c1 - c)
                wb, kwb = self.load_w(W[:, c:c + w], w)
                for ti, (r0, nr) in enumerate(self.tiles):
                    p = self.rr("pp", 4)
                    pp, kpp = self.pp[p], "pp%d" % p
                    for kc in range(KC):
                        kb.op("pe", lambda e, kc=kc, pp=pp, r0=r0, nr=nr, wb=wb, w=w: e.matmul(
                            pp[:nr, :w], lhsT=self.xT[:, kc, r0:r0 + nr], rhs=wb[:, kc, :w],
                            start=(kc == 0), stop=(kc == KC - 1)),
                            reads=[("xT", ti), kwb], writes=[kpp])
                    v = self.rr("ev", 4)
                    ev, kev = self.ev[v], "ev%d" % v
                    self.evac(ev[:nr, :w], pp[:nr, :w], [kpp], [kev])
                    kb.dma("pool", self.S["z"][r0:r0 + nr, c:c + w], ev[:nr, :w], reads=[kev],
                           writes=[("z", ti)])
                c += w
        for (c0, c1) in fm_segs:
            c = c0
            while c < c1:
                w = min(512, c1 - c)
                wb, kwb = self.load_w(W[:, c:c + w], w)
                for s0 in range(0, w, 128):
                    m = min(128, w - s0)
                    for bi, (t0, n) in enumerate(self.blocks):
                        p = self.rr("pp", 4)
                        pp, kpp = self.pp[p], "pp%d" % p
                        tis = list(range(t0 // 128, (t0 + n + 127) // 128))
                        for kc in range(KC):
                            kb.op("pe", lambda e, kc=kc, pp=pp, t0=t0, n=n, wb=wb, s0=s0, m=m: e.matmul(
                                pp[:m, :n], lhsT=wb[:, kc, s0:s0 + m], rhs=self.xT[:, kc, t0:t0 + n],
                                start=(kc == 0), stop=(kc == KC - 1)),
                                reads=[("xT", t) for t in tis] + [kwb], writes=[kpp])
                        v = self.rr("ev", 4)
                        ev, kev = self.ev[v], "ev%d" % v
                        self.evac(ev[:m, :n], pp[:m, :n], [kpp], [kev])
                        kb.dma("pool", self.S["zT"][c + s0:c + s0 + m, t0:t0 + n], ev[:m, :n], reads=[kev],
                               writes=[("zT", c + s0, bi)])
                c += w
        self.end_phase()

    def easy_states(self, l):
        kb = self.kb
        I, S, O = self.I, self.S, self.O
        T, NS = self.T, self.NS
        nb = len(self.blocks)
        zt_all = [("zT", r, b) for r in range(0, 1024, 128) for b in range(nb)]
        z_all = [("z", ti) for ti in range(len(self.tiles))]
        kb.dma("pool", O["conv_prompt"][l].rearrange("j c -> c j"), S["zT"][0:D, T - 3:T], reads=zt_all, writes=[],
               is_output=True, allow_slow_non_contiguous=True)
        kb.dma("pool", O["shift_prompt"][l:l + 1, :], S["z"][T - 1:T, O_RC:O_RC + R_SHIFT_W], reads=z_all, writes=[], is_output=True)
        if NS:
            kb.dma("pool", O["conv_sample"][l][:, 0:2, :], I["state_mlstm_conv"][l][:, 1:3, :], writes=[], is_output=True)
            for hh in range(4):
                kb.dma("pool", O["conv_sample"][l][:, 2, hh * 256:(hh + 1) * 256].rearrange("b c -> c b"),
                       S["zT"][hh * 256:(hh + 1) * 256, T:T + NS], reads=zt_all, writes=[],
                       is_output=True, allow_slow_non_contiguous=True)
            kb.dma("pool", O["shift_sample"][l], S["z"][T:T + NS, O_RC:O_RC + R_SHIFT_W], reads=z_all, writes=[], is_output=True)

    def mlstm_phase(self, l):
        kb = self.kb
        I, S, O = self.I, self.S, self.O
        T, NS, NT = self.T, self.NS, self.NT
        self.begin_phase()
        sbp = self.sbp
        Wst = sbp("Wst", (128, 4, 2, 256))
        Wq = sbp("Wq", (128, 4, 2, 256), BF16); Wk = sbp("Wk", (128, 4, 2, 256), BF16); Wv = sbp("Wv", (128, 4, 2, 256), BF16)
        cw = sbp("cw", (128, 4, 8)); cb = sbp("cb", (128, 8)); mg = sbp("mg", (128, 8)); msk = sbp("msk", (128, 8))
        gb = sbp("gb", (4, 2))
        igA = sbp("igA", (4, NT)); lfA = sbp("lfA", (4, NT))
        selh = sbp("selh", (4, 4, 128)); causneg = sbp("causneg", (128, 128)); ones4 = sbp("ones4", (4, 128))
        Cst = [[sbp("C%d_%d" % (s_, h), (128, 2, 257)) for h in range(4)] for s_ in range(2)]
        mprev = sbp("mprev", (4, 2))
        xext = [sbp("xext%d" % i, (128, 8, 131)) for i in range(2)]
        ctmp = sbp("cvtmp", (128, 8, 128)); cacc = sbp("cvacc", (128, 8, 128))
        xcT = sbp("xcT", (128, 8, 128)); xcb = sbp("xcb", (128, 8, 128), BF16); xmb = sbp("xmb", (128, 8, 128), BF16)
        qTb = sbp("qTb", (128, 4, 2, 128)); kTb = sbp("kTb", (128, 4, 2, 128))
        kw = sbp("kw", (128, 256)); vaug = sbp("vaug", (128, 257))
        G = sbp("G", (4, 8, 128)); gsm = sbp("gsm", (4, 8)); dg = sbp("dg", (4, 4))
        gc = sbp("gc", (128, 16)); dbc = sbp("dbc", (128, 4))
        DT = sbp("DT", (128, 128)); Stl = sbp("Stl", (128, 128)); mmb = sbp("mmb", (128, 4, 128))
        Asb = sbp("Asb", (128, 257)); nd = sbp("nd", (128, 257)); dsm = sbp("dsm", (128, 4))
        sog = [sbp("sog%d" % i, (128, D)) for i in range(2)]
        hm = sbp("hm", (128, D)); hst = sbp("hst", (128, 16))
        hT = sbp("hT", (128, 8, 128)); zmt = [sbp("zmt%d" % i, (128, 8, 128)) for i in range(2)]
        aob = [sbp("aob%d" % i, (128, 8, 128), BF16) for i in range(2)]
        cvst = sbp("cvst", (48, D)); convT = sbp("convT", (128, 8, 48))

        for (nm, dst, sc_) in (("m_wq", Wq, 1.0 / 16.0), ("m_wk", Wk, 1.0), ("m_wv", Wv, 1.0)):
            kb.dma("sp", Wst[:], I[nm][l].rearrange("h (c p) e -> p h c e", p=128), writes=["Wst"])
            kb.op("act", lambda e, dst=dst, sc_=sc_: e.activation(out=dst[:], in_=Wst[:], func=AF.Copy, scale=sc_),
                  reads=["Wst"], writes=[nm])
        sl = dict(allow_slow_non_contiguous=True)
        kb.dma("sp", cw[:], I["m_conv_w"][l].rearrange("j (k p) -> p j k", p=128), writes=["cw"], **sl)
        kb.dma("sp", cb[:], I["m_conv_b"][l].rearrange("(k p) -> p k", p=128), writes=["cb"], **sl)
        kb.dma("sp", mg[:], I["m_norm_g"][l].rearrange("(k p) -> p k", p=128), writes=["mg"], **sl)
        kb.dma("sp", msk[:], I["m_skip"][l].rearrange("(k p) -> p k", p=128), writes=["msk"], **sl)
        kb.dma("sp", gb[:, 0:1], I["m_ig_b"][l].rearrange("(h o) -> h o", o=1), writes=["gb"], **sl)
        kb.dma("sp", gb[:, 1:2], I["m_fg_b"][l].rearrange("(h o) -> h o", o=1), writes=["gb"], **sl)
        kb.dma("sp", selh[:], I["selh"], writes=["selh"])
        kb.dma("sp", causneg[:], I["causneg"], writes=["causneg"])
        kb.dma("sp", ones4[:], I["ones4"], writes=["ones4"])
        nb = len(self.blocks)
        kb.dma("sp", igA[:], S["zT"][O_IG:O_IG + 4, :], reads=[("zT", O_IG, b) for b in range(nb)], writes=["igA"])
        kb.dma("sp", lfA[:], S["zT"][O_FG:O_FG + 4, :], reads=[("zT", O_FG, b) for b in range(nb)], writes=["lfA"])
        kb.op("dve", lambda e: e.tensor_scalar(out=igA[:], in0=igA[:], scalar1=gb[:, 0:1], scalar2=None, op0=ALU.add),
              reads=["igA", "gb"], writes=["igA"])
        kb.op("dve", lambda e: e.tensor_scalar(out=lfA[:], in0=lfA[:], scalar1=gb[:, 1:2], scalar2=-1.0, op0=ALU.add, op1=ALU.mult),
              reads=["lfA", "gb"], writes=["lfA"])
        kb.op("act", lambda e: e.activation(out=lfA[:], in_=lfA[:], func=AF.Exp), reads=["lfA"], writes=["lfA"])
        kb.op("dve", lambda e: e.tensor_scalar(out=lfA[:], in0=lfA[:], scalar1=1.0, scalar2=None, op0=ALU.add), reads=["lfA"], writes=["lfA"])
        kb.op("act", lambda e: e.activation(out=lfA[:], in_=lfA[:], func=AF.Ln), reads=["lfA"], writes=["lfA"])
        kb.op("dve", lambda e: e.tensor_scalar(out=lfA[:], in0=lfA[:], scalar1=-1.0, scalar2=None, op0=ALU.mult), reads=["lfA"], writes=["lfA"])
        if NS:
            kb.dma("sp", cvst[:3 * NS, :], I["state_mlstm_conv"][l].rearrange("b j c -> (b j) c"), writes=["cvst"])
            for half in range(2):
                p = self.rr("ptr", 2)
                pt, kpt = self.ptr[p], "ptr%d" % p
                for k4 in range(4):
                    kc = half * 4 + k4
                    kb.op("pe", lambda e, pt=pt, k4=k4, kc=kc: e.transpose(out=pt[:, k4 * 48:k4 * 48 + 3 * NS], in_=cvst[:3 * NS, kc * 128:(kc + 1) * 128],
                                                                       identity=self.ident[:3 * NS, :3 * NS]),
                          reads=["cvst", "ident"], writes=[kpt])
                kb.op("dve", lambda e, pt=pt, half=half: e.tensor_copy(out=convT[:, half * 4:half * 4 + 4, :3 * NS],
                                                                     in_=pt[:, 0:192].rearrange("p (k c) -> p k c", k=4)[:, :, :3 * NS]),
                      reads=[kpt], writes=["convT"])

        zt_x = lambda bi: [("zT", r, bi) for r in range(0, 1024, 128)]
        zt_z = lambda bi: [("zT", O_ZM + r, bi) for r in range(0, 1024, 128)]
        blk_of = lambda t: next(i for i, (b0, bn) in enumerate(self.blocks) if b0 <= t < b0 + bn)

        chunks = [(c * 128, 128, None) for c in range(T // 128)] + [(T + b, 1, b) for b in range(NS)]
        for ci, (t0, L, sb_) in enumerate(chunks):
            bi = blk_of(t0)
            ti = t0 // 128
            cs = self.rr("Cset", 2) if sb_ is not None else 0
            C = Cst[cs]
            kC = [("C", cs, h) for h in range(4)]
            if ci == 0:
                for h in range(4):
                    kb.op("pool", lambda e, h=h: e.memset(C[h][:], 0.0), writes=[kC[h]])
                kb.op("dve", lambda e: e.memset(mprev[:, 0:1], NEG), writes=["mprev"])
            if sb_ is not None:
                for h in range(4):
                    kb.dma("sp", C[h][:, :, 0:256], I["state_mlstm_c"][l, sb_, h].rearrange("(c p) v -> p c v", p=128), writes=[kC[h]])
                    kb.dma("sp", C[h][:, :, 256:257], I["state_mlstm_n"][l, sb_, h].rearrange("(c p o) -> p c o", p=128, o=1), writes=[kC[h]], **sl)
                kb.dma("sp", mprev[:, 0:1], I["state_mlstm_m"][l, sb_].rearrange("(h o) -> h o", o=1), writes=["mprev"], **sl)
            xj = self.rr("xext", 2)
            xe, kxe = xext[xj], "xext%d" % xj
            if sb_ is None:
                if t0 == 0:
                    kb.op("pool", lambda e, xe=xe: e.memset(xe[:, :, 0:3], 0.0), writes=[kxe])
                    kb.dma("sp", xe[:, :, 3:3 + L], S["zT"][0:D, t0:t0 + L].rearrange("(k p) t -> p k t", p=128), reads=zt_x(bi), writes=[kxe])
                else:
                    rd = zt_x(bi) + (zt_x(blk_of(t0 - 3)) if blk_of(t0 - 3) != bi else [])
                    kb.dma("sp", xe[:, :, 0:3 + L], S["zT"][0:D, t0 - 3:t0 + L].rearrange("(k p) t -> p k t", p=128), reads=rd, writes=[kxe])
            else:
                kb.op("pool", lambda e, xe=xe, sb_=sb_: e.tensor_copy(out=xe[:, :, 0:3], in_=convT[:, :, 3 * sb_:3 * sb_ + 3]), reads=["convT"], writes=[kxe])
                kb.dma("sp", xe[:, :, 3:4], S["zT"][0:D, t0:t0 + 1].rearrange("(k p) t -> p k t", p=128), reads=zt_x(bi), writes=[kxe], **sl)
            V = lambda x: x[:, :, :L]
            for j in range(4):
                dst = cacc if j == 0 else ctmp
                kd = "cvacc" if j == 0 else "cvtmp"
                kb.op("dve", lambda e, j=j, dst=dst, xe=xe: e.tensor_tensor(out=V(dst), in0=xe[:, :, j:j + L],
                                                                         in1=cw[:, j, :, None].to_broadcast([128, 8, L]), op=ALU.mult),
                      reads=[kxe, "cw"], writes=[kd])
                if j > 0:
                    kb.op("pool", lambda e: e.tensor_tensor(out=V(cacc), in0=V(cacc), in1=V(ctmp), op=ALU.add), reads=["cvacc", "cvtmp"], writes=["cvacc"])
            kb.op("dve", lambda e: e.tensor_tensor(out=V(cacc), in0=V(cacc), in1=cb[:, :, None].to_broadcast([128, 8, L]), op=ALU.add),
                  reads=["cvacc", "cb"], writes=["cvacc"])
            kb.op("act", lambda e: e.activation(out=V(xcT), in_=V(cacc), func=AF.Silu), reads=["cvacc"], writes=["xcT"])
            kb.op("pool", lambda e: e.tensor_copy(out=V(xcb), in_=V(xcT)), reads=["xcT"], writes=["xcb"])
            kb.op("pool", lambda e, xe=xe: e.tensor_copy(out=V(xmb), in_=xe[:, :, 3:3 + L]), reads=[kxe], writes=["xmb"])
            Gr = lambda i: G[:, i, :L]
            kb.op("dve", lambda e: e.tensor_tensor_scan(out=Gr(0), data0=ones4[:, :L], data1=lfA[:, t0:t0 + L], initial=0.0, op0=ALU.mult, op1=ALU.add),
                  reads=["ones4", "lfA"], writes=["G0"])
            kb.op("dve", lambda e: e.tensor_tensor(out=Gr(3), in0=igA[:, t0:t0 + L], in1=Gr(0), op=ALU.subtract), reads=["igA", "G0"], writes=["G3"])
            kb.op("dve", lambda e: e.tensor_tensor_scan(out=Gr(1), data0=Gr(3), data1=Gr(3), initial=-3.0e38, op0=ALU.max, op1=ALU.max),
                  reads=["G3"], writes=["G1"])
            kb.op("dve", lambda e: e.tensor_scalar(out=Gr(1), in0=Gr(1), scalar1=mprev[:, 0:1], scalar2=None, op0=ALU.max), reads=["G1", "mprev"], writes=["G1"])
            kb.op("dve", lambda e: e.tensor_scalar(out=gsm[:, 0:1], in0=G[:, 1, L - 1:L], scalar1=-1.0, scalar2=None, op0=ALU.mult), reads=["G1"], writes=["gsm"])
            kb.op("act", lambda e: e.activation(out=Gr(4), in_=Gr(1), func=AF.Exp, scale=-1.0, bias=mprev[:, 0:1]), reads=["G1", "mprev"], writes=["G4"])
            kb.op("dve", lambda e: e.tensor_tensor(out=Gr(5), in0=Gr(0), in1=Gr(1), op=ALU.add), reads=["G0", "G1"], writes=["G5"])
            kb.op("act", lambda e: e.activation(out=Gr(5), in_=Gr(5), func=AF.Exp, scale=-1.0), reads=["G5"], writes=["G5"])
            kb.op("act", lambda e: e.activation(out=Gr(6), in_=Gr(3), func=AF.Exp, bias=gsm[:, 0:1]), reads=["G3", "gsm"], writes=["G6"])
            kb.op("dve", lambda e: e.tensor_tensor(out=mprev[:, 1:2], in0=G[:, 0, L - 1:L], in1=G[:, 1, L - 1:L], op=ALU.add), reads=["G0", "G1"], writes=["mnew"])
            p = self.rr("ptr", 2)
            pt, kpt = self.ptr[p], "ptr%d" % p
            for ki, gi in enumerate((4, 5, 3, 6)):
                kb.op("pe", lambda e, ki=ki, gi=gi, pt=pt: e.transpose(out=pt[:L, ki * 4:ki * 4 + 4], in_=G[:, gi, :L], identity=self.ident[:4, :4]),
                      reads=["G%d" % gi, "ident"], writes=[kpt])
            kb.op("dve", lambda e, pt=pt: e.tensor_copy(out=gc[:L, :], in_=pt[:L, 0:16]), reads=[kpt], writes=["gc"])
            kb.op("dve", lambda e: e.tensor_scalar(out=dg[:, :], in0=self.ident[:4, :4], scalar1=G[:, 4, L - 1:L], scalar2=None, op0=ALU.mult),
                  reads=["G4", "ident"], writes=["dg"])
            p2 = self.rr("ptr", 2)
            pt2, kpt2 = self.ptr[p2], "ptr%d" % p2
            kb.op("pe", lambda e, pt2=pt2: e.matmul(pt2[:, 0:4], lhsT=ones4[:, :], rhs=dg[:, :], start=True, stop=True), reads=["ones4", "dg"], writes=[kpt2])
            kb.op("dve", lambda e, pt2=pt2: e.tensor_copy(out=dbc[:, :], in_=pt2[:, 0:4]), reads=[kpt2], writes=["dbc"])
            pnn = self.rr("pp", 4)
            pn, kpn = self.pp[pnn], "pp%d" % pnn
            for h in range(4):
                kb.op("pe", lambda e, h=h, pn=pn: e.matmul(pn[:L, h * 128:h * 128 + L], lhsT=selh[:, h, :L], rhs=G[:, 1, :L], start=True, stop=True),
                      reads=["selh", "G1"], writes=[kpn])
            kb.op("act", lambda e, pn=pn: e.activation(out=mmb[:L, :, :L], in_=pn[:L, :].rearrange("p (h t) -> p h t", h=4)[:, :, :L], func=AF.Copy),
                  reads=[kpn], writes=["mmb"])
            sj = self.rr("sog", 2)
            so, kso = sog[sj], "sog%d" % sj
            kb.dma("sp", so[:L, :], S["z"][t0:t0 + L, O_OG:O_OG + D], reads=[("z", ti)], writes=[kso])
            kb.op("act", lambda e, so=so: e.activation(out=so[:L, :], in_=so[:L, :], func=AF.Sigmoid), reads=[kso], writes=[kso])
            for h in range(4):
                for (Wt, wn, dstT, kd) in ((Wq, "m_wq", qTb, "qTb"), (Wk, "m_wk", kTb, "kTb")):
                    pq = self.rr("pp", 4)
                    ppq, kpq = self.pp[pq], "pp%d" % pq
                    for ec in range(2):
                        for dc in range(2):
                            kb.op("pe", lambda e, h=h, ec=ec, dc=dc, Wt=Wt, ppq=ppq: e.matmul(
                                ppq[:, ec * 128:ec * 128 + L], lhsT=Wt[:, h, dc, ec * 128:(ec + 1) * 128], rhs=xcb[:, 2 * h + dc, :L],
                                start=(dc == 0), stop=(dc == 1)), reads=[wn, "xcb"], writes=[kpq])
                    self.evac(dstT[:, h, :, :L], ppq[:, 0:256].rearrange("p (c t) -> p c t", c=2)[:, :, :L], [kpq], [(kd, h)])
            for h in range(4):
                pk = self.rr("pp", 4)
                ppk, kpk = self.pp[pk], "pp%d" % pk
                for dc in range(2):
                    kb.op("pe", lambda e, h=h, dc=dc, ppk=ppk: e.matmul(ppk[:L, 0:256], lhsT=xcb[:, 2 * h + dc, :L], rhs=Wk[:, h, dc, :],
                                                                      start=(dc == 0), stop=(dc == 1)), reads=["m_wk", "xcb"], writes=[kpk])
                kb.op("act", lambda e, h=h, ppk=ppk: e.activation(out=kw[:L, :], in_=ppk[:L, 0:256], func=AF.Copy, scale=gc[:L, 12 + h:13 + h]),
                      reads=[kpk, "gc"], writes=["kw"])
                pv = self.rr("pp", 4)
                ppv, kpv = self.pp[pv], "pp%d" % pv
                for dc in range(2):
                    kb.op("pe", lambda e, h=h, dc=dc, ppv=ppv: e.matmul(ppv[:L, 0:256], lhsT=xmb[:, 2 * h + dc, :L], rhs=Wv[:, h, dc, :],
                                                                      start=(dc == 0), stop=(dc == 1)), reads=["m_wv", "xmb"], writes=[kpv])
                kb.op("dve", lambda e, ppv=ppv: e.tensor_copy(out=vaug[:L, 0:256], in_=ppv[:L, 0:256]), reads=[kpv], writes=["vaug"])
                kb.op("dve", lambda e: e.memset(vaug[:L, 256:257], 1.0), writes=["vaug"])
                kb.op("dve", lambda e, h=h: e.tensor_tensor(out=DT[:L, :L], in0=mmb[:L, h, :L], in1=causneg[:L, :L], op=ALU.add),
                      reads=["mmb", "causneg"], writes=["DT"])
                kb.op("act", lambda e, h=h: e.activation(out=DT[:L, :L], in_=DT[:L, :L], func=AF.Exp, scale=-1.0, bias=gc[:L, 8 + h:9 + h]),
                      reads=["DT", "gc"], writes=["DT"])
                ps_ = self.rr("pp", 4)
                pps, kps = self.pp[ps_], "pp%d" % ps_
                for ec in range(2):
                    kb.op("pe", lambda e, h=h, ec=ec, pps=pps: e.matmul(pps[:L, :L], lhsT=kTb[:, h, ec, :L], rhs=qTb[:, h, ec, :L],
                                                                      start=(ec == 0), stop=(ec == 1)), reads=[("kTb", h), ("qTb", h)], writes=[kps])
                kb.op("dve", lambda e, pps=pps: e.tensor_tensor(out=Stl[:L, :L], in0=pps[:L, :L], in1=DT[:L, :L], op=ALU.mult), reads=[kps, "DT"], writes=["Stl"])
                pa = self.rr("pp", 4)
                ppa, kpa = self.pp[pa], "pp%d" % pa
                kb.op("pe", lambda e, ppa=ppa: e.matmul(ppa[:L, 0:257], lhsT=Stl[:L, :L], rhs=vaug[:L, :], start=True, stop=True), reads=["Stl", "vaug"], writes=[kpa])
                pb = self.rr("ptr", 2)
                ppb, kpb = self.ptr[pb], "ptr%d" % pb
                for ec in range(2):
                    kb.op("pe", lambda e, h=h, ec=ec, ppb=ppb, C=C: e.matmul(ppb[:L, 0:257], lhsT=qTb[:, h, ec, :L], rhs=C[h][:, ec, :],
                                                                      start=(ec == 0), stop=(ec == 1)), reads=[("qTb", h), kC[h]], writes=[kpb])
                kb.op("act", lambda e, ppa=ppa: e.activation(out=Asb[:L, :], in_=ppa[:L, 0:257], func=AF.Copy), reads=[kpa], writes=["Asb"])
                kb.op("dve", lambda e, h=h, ppb=ppb: e.scalar_tensor_tensor(out=nd[:L, :], in0=ppb[:L, 0:257], scalar=gc[:L, h:h + 1], in1=Asb[:L, :],
                                                                        op0=ALU.mult, op1=ALU.add), reads=[kpb, "gc", "Asb"], writes=["nd"])
                kb.op("act", lambda e: e.activation(out=dsm[:L, 2:3], in_=nd[:L, 256:257], func=AF.Abs), reads=["nd"], writes=["dsm"])
                kb.op("dve", lambda e, h=h: e.tensor_scalar(out=dsm[:L, 0:1], in0=dsm[:L, 2:3], scalar1=gc[:L, 4 + h:5 + h], scalar2=None,
                                                            op0=ALU.max), reads=["dsm", "gc"], writes=["dsm"])
                kb.op("dve", lambda e: e.reciprocal(out=dsm[:L, 1:2], in_=dsm[:L, 0:1]), reads=["dsm"], writes=["dsm"])
                kb.op("dve", lambda e, h=h, so=so: e.scalar_tensor_tensor(out=hm[:L, h * 256:(h + 1) * 256], in0=nd[:L, 0:256], scalar=dsm[:L, 1:2],
                                                                      in1=so[:L, h * 256:(h + 1) * 256], op0=ALU.mult, op1=ALU.mult),
                      reads=["nd", "dsm", kso], writes=[("hm", h)])
                for ec in range(2):
                    pu = self.rr("pp", 4)
                    ppu, kpu = self.pp[pu], "pp%d" % pu
                    kb.op("pe", lambda e, ec=ec, ppu=ppu: e.matmul(ppu[:, 0:257], lhsT=kw[:L, ec * 128:(ec + 1) * 128], rhs=vaug[:L, :], start=True, stop=True),
                          reads=["kw", "vaug"], writes=[kpu])
                    kb.op("dve", lambda e, h=h, ec=ec, ppu=ppu, C=C: e.scalar_tensor_tensor(out=C[h][:, ec, :], in0=C[h][:, ec, :], scalar=dbc[:, h:h + 1],
                                                                                      in1=ppu[:, 0:257], op0=ALU.mult, op1=ALU.add),
                          reads=[kC[h], "dbc", kpu], writes=[kC[h]])
            kb.op("dve", lambda e: e.tensor_copy(out=mprev[:, 0:1], in_=mprev[:, 1:2]), reads=["mnew"], writes=["mprev"])
            hmk = [("hm", h) for h in range(4)]
            hv = hm[:L, :].rearrange("t (h d) -> t h d", h=4)
            kb.op("dve", lambda e: e.tensor_reduce(out=hst[:L, 0:4], in_=hv, axis=AX.X, op=ALU.add), reads=hmk, writes=["hst"])
            kb.op("dve", lambda e: e.tensor_scalar(out=hst[:L, 0:4], in0=hst[:L, 0:4], scalar1=-1.0 / 256.0, scalar2=None, op0=ALU.mult), reads=["hst"], writes=["hst"])
            kb.op("dve", lambda e: e.tensor_tensor(out=hv, in0=hv, in1=hst[:L, 0:4].unsqueeze(2).to_broadcast([L, 4, 256]), op=ALU.add),
                  reads=hmk + ["hst"], writes=hmk)
            kb.op("pool", lambda e, so=so: e.tensor_tensor(out=so[:L, :], in0=hm[:L, :], in1=hm[:L, :], op=ALU.mult), reads=hmk, writes=[kso])
            kb.op("dve", lambda e, so=so: e.tensor_reduce(out=hst[:L, 4:8], in_=so[:L, :].rearrange("t (h d) -> t h d", h=4), axis=AX.X, op=ALU.add),
                  reads=[kso], writes=["hst"])
            kb.op("act", lambda e: e.activation(out=hst[:L, 8:12], in_=hst[:L, 4:8], func=AF.Sqrt, scale=1.0 / 256.0, bias=self.epsc[:L, 0:1]),
                  reads=["hst", "epsc"], writes=["hst"])
            kb.op("dve", lambda e: e.reciprocal(out=hst[:L, 12:16], in_=hst[:L, 8:12]), reads=["hst"], writes=["hst"])
            kb.op("dve", lambda e: e.tensor_tensor(out=hv, in0=hv, in1=hst[:L, 12:16].unsqueeze(2).to_broadcast([L, 4, 256]), op=ALU.mult),
                  reads=hmk + ["hst"], writes=hmk)
            zj = self.rr("zmt", 2)
            zm_, kzm = zmt[zj], "zmt%d" % zj
            kb.dma("sp", zm_[:, :, :L], S["zT"][O_ZM:O_ZM + D, t0:t0 + L].rearrange("(k p) t -> p k t", p=128), reads=zt_z(bi), writes=[kzm],
                   **(sl if L == 1 else {}))
            kb.op("act", lambda e, zm_=zm_: e.activation(out=zm_[:, :, :L], in_=zm_[:, :, :L], func=AF.Silu), reads=[kzm], writes=[kzm])
            for half in range(2):
                p = self.rr("ptr", 2)
                pt, kpt = self.ptr[p], "ptr%d" % p
                for k4 in range(4):
                    kc = half * 4 + k4
                    kb.op("pe", lambda e, k4=k4, kc=kc, pt=pt: e.transpose(out=pt[:, k4 * 128:k4 * 128 + L], in_=hm[:L, kc * 128:(kc + 1) * 128],
                                                                       identity=self.ident[:L, :L]), reads=hmk + ["ident"], writes=[kpt])
                kb.op("dve", lambda e, half=half, pt=pt: e.tensor_tensor(out=hT[:, half * 4:half * 4 + 4, :L],
                                                                       in0=pt[:, :].rearrange("p (k t) -> p k t", k=4)[:, :, :L],
                                                                       in1=mg[:, half * 4:half * 4 + 4, None].to_broadcast([128, 4, L]), op=ALU.mult),
                      reads=[kpt, "mg"], writes=[("hT", half)])
            kb.op("pool", lambda e: e.tensor_tensor(out=V(ctmp), in0=V(xcT), in1=msk[:, :, None].to_broadcast([128, 8, L]), op=ALU.mult),
                  reads=["xcT", "msk"], writes=["cvtmp"])
            kb.op("pool", lambda e: e.tensor_tensor(out=V(hT), in0=V(hT), in1=V(ctmp), op=ALU.add), reads=["cvtmp", ("hT", 0), ("hT", 1)], writes=[("hT", 0), ("hT", 1)])
            aj = self.rr("aob", 2)
            kb.op("dve", lambda e, aj=aj, zm_=zm_: e.tensor_tensor(out=aob[aj][:, :, :L], in0=V(hT), in1=zm_[:, :, :L], op=ALU.mult),
                  reads=[("hT", 0), ("hT", 1), kzm], writes=["aob%d" % aj])
            kb.dma("pool", S["act_m"][:, t0:t0 + L].rearrange("(k p) t -> p k t", p=128), aob[aj][:, :, :L], reads=["aob%d" % aj],
                   writes=[("act", "m", bi)], **(sl if L == 1 else {}))
            if sb_ is not None or ci == T // 128 - 1:
                if sb_ is None:
                    oc_, on_, om_ = O["c_prompt"][l], O["n_prompt"][l], O["m_prompt"][l]
                else:
                    oc_, on_, om_ = O["c_sample"][l, sb_], O["n_sample"][l, sb_], O["m_sample"][l, sb_]
                for h in range(4):
                    kb.dma("pool", oc_[h].rearrange("(c p) v -> p c v", p=128), C[h][:, :, 0:256], reads=[kC[h]], writes=[], is_output=True)
                    kb.dma("pool", on_[h].rearrange("(c p o) -> p c o", p=128, o=1), C[h][:, :, 256:257], reads=[kC[h]], writes=[], is_output=True, **sl)
                kb.dma("pool", om_.rearrange("(h o) -> h o", o=1), mprev[:, 0:1], reads=["mprev"], writes=[], is_output=True, **sl)
        self.end_phase()

    def rwkv_phase(self, l):
        kb = self.kb
        I, S, O = self.I, self.S, self.O
        T, NS, NT = self.T, self.NS, self.NT
        self.begin_phase()
        sbp = self.sbp
        sl = dict(allow_slow_non_contiguous=True)
        NPI = self.NPI
        EW = -math.exp(-0.5)
        P_ = {}
        for nm in ("r_k_k", "r_k_a", "r_ln_g", "r_ln_b"):
            P_[nm] = sbp("bc_" + nm, (128, D))
            kb.dma("sp", P_[nm][:], I[nm][l].partition_broadcast(128), writes=[nm])
        P_["r_r_k"] = sbp("bc_rk", (128, D))
        kb.dma("sp", P_["r_r_k"][:], I["r_r_k"][l].rearrange("h j -> (h j)").partition_broadcast(128), writes=["r_r_k"])
        mu = sbp("bc_mu", (128, R_SHIFT_W))
        kb.dma("sp", mu[:], I["r_mu"][l].partition_broadcast(128), writes=["mu"])
        w2 = sbp("w2", (65, 2, D))
        kb.dma("sp", w2[0:64, 0, :], I["r_w2"][l], writes=["w2"])
        kb.dma("sp", w2[0:64, 1, :], I["r_a2"][l], writes=["w2"])
        kb.dma("sp", w2[64:65, 0, :], I["r_w0"][l:l + 1, :], writes=["w2"])
        kb.dma("sp", w2[64:65, 1, :], I["r_a0"][l:l + 1, :], writes=["w2"])
        mks = sbp("mks", (128, 384))
        kb.dma("sp", mks[:], I["rmasks"], writes=["mks"])
        e12 = sbp("e12", (128, 1))
        kb.op("dve", lambda e: e.memset(e12[:], RWKV_GN_EPS), writes=["e12"])
        xr = sbp("xr", (128, R_SHIFT_W)); rp = sbp("rp", (128, R_SHIFT_W))
        lt = sbp("lt", (65, 2, 128))
        kb.op("pool", lambda e: e.memset(lt[64:65, :, :], 1.0), writes=["lt1"])
        tv = {n: sbp("tv_" + n, (128, D)) for n in ("lw", "a", "an", "b", "k")}
        ssq = sbp("ssq", (128, 64))
        fmall = sbp("fmall", (128, 8, 6, 128))
        fm = {n: fmall[:, :, i_, :] for i_, n in enumerate(("at", "rt", "bh", "kh", "bc", "kc"))}
        fm["cum"] = sbp("fm_cum", (128, 8, 128)); fm["et"] = sbp("fm_et", (128, 8, 128))
        wc = sbp("wc", (128, 8, 16))
        ST = sbp("STt", (128, 8, 64))
        Vp = sbp("Vp", (128, 8, 64)); Ych = sbp("Ych", (128, 8, 64))
        nat = sbp("rnat", (128, 8, 64)); natx = sbp("rnatx", (128, 128)); nato = sbp("rnato", (128, 8, 64))
        CDT = BF16 if self.CORE_BF16 else F32
        NG = 2
        G_ = []
        BK = sbp("g_BK", (128, 4, 2, 128), CDT)
        for gi in range(NG):
            d = {}
            d["UBD"] = sbp("g%d_UBD" % gi, (128, 4, 4, 128), CDT)
            d["Btm"] = sbp("g%d_Btm" % gi, (128, 4, 128), CDT); d["Ktm"] = sbp("g%d_Ktm" % gi, (128, 4, 128), CDT)
            d["MNb"] = sbp("g%d_MNb" % gi, (128, 4, 256), CDT); d["MNk"] = sbp("g%d_MNk" % gi, (128, 4, 256), CDT)
            d["X"] = sbp("g%d_X" % gi, (128, 4, 128), CDT); d["Xt"] = sbp("g%d_Xt" % gi, (128, 4, 128), CDT); d["P"] = sbp("g%d_P" % gi, (128, 4, 128), CDT)
            d["RHS"] = sbp("g%d_RHS" % gi, (128, 4, 64), CDT); d["U"] = sbp("g%d_U" % gi, (128, 4, 64), CDT)
            G_.append(d)
        self.bank8 = list(self.pp) + list(self.ptr) + list(self.pex)
        self.bank8k = ["pp%d" % i for i in range(4)] + ["ptr%d" % i for i in range(2)] + ["pex%d" % i for i in range(2)]
        if self.CORE_BF16:
            STb = sbp("STb", (128, 8, 64), BF16); Vpb = sbp("Vpb", (128, 8, 64), BF16); identc = sbp("identc", (128, 128), BF16)
            kb.op("pool", lambda e: e.tensor_copy(out=identc[:], in_=self.ident[:]), reads=["ident"], writes=["identc"])
            kb.op("pool", lambda e: e.memset(STb[:], 0.0), writes=[("STb", h) for h in range(8)])
        else:
            STb, Vpb, identc = ST, Vp, self.ident
        kSTb = (lambda hh: ("STb", hh)) if self.CORE_BF16 else (lambda hh: ("ST", hh))
        kVpb = (lambda hh: ("Vpb", hh)) if self.CORE_BF16 else (lambda hh: ("Vp", hh))

        def vp_ready():
            if self.CORE_BF16:
                kb.op("pool", lambda e: e.tensor_copy(out=Vpb[:], in_=Vp[:]), reads=[("Vp", h) for h in range(8)], writes=[("Vpb", h) for h in range(8)])
        ytm = rp[:, D:2 * D]; zrt = rp[:, 0:D]; yst = sbp("yst", (128, 64))
        aob = [sbp("raob%d" % i, (128, 8, 128), BF16) for i in range(2)]

        def zero_units():
            for gi in range(NG):
                kb.op("pool", lambda e, gi=gi: e.memset(G_[gi]["UBD"][:], 0.0), writes=[("g", gi, "UBD")])
            kb.op("pool", lambda e: e.memset(BK[:], 0.0), writes=["BK"])
        zero_units()
        kb.op("pool", lambda e: e.memset(ST[:], 0.0), writes=[("ST", h) for h in range(8)])

        z_all = lambda ti: [("z", ti)]

        def prep(t0, L, ti, is_s):
            kb.dma("sp", xr[:L, :], S["z"][t0:t0 + L, O_RC:O_RC + R_SHIFT_W], reads=z_all(ti), writes=["xr"])
            if is_s:
                kb.dma("sp", rp[:L, :], I["state_rwkv_shift"][l], writes=["rp"])
            elif t0 == 0:
                kb.op("pool", lambda e: e.memset(rp[0:1, :], 0.0), writes=["rp"])
                kb.dma("sp", rp[1:L, :], S["z"][0:L - 1, O_RC:O_RC + R_SHIFT_W], reads=z_all(ti), writes=["rp"])
            else:
                kb.dma("sp", rp[:L, :], S["z"][t0 - 1:t0 + L - 1, O_RC:O_RC + R_SHIFT_W], reads=z_all(ti) + z_all(ti - 1), writes=["rp"])
            kb.op("pool", lambda e: e.tensor_tensor(out=rp[:L, :], in0=rp[:L, :], in1=xr[:L, :], op=ALU.subtract), reads=["rp", "xr"], writes=["rp"])
            kb.op("dve", lambda e: e.tensor_tensor(out=rp[:L, :], in0=rp[:L, :], in1=mu[:L, :], op=ALU.mult), reads=["rp", "mu"], writes=["rp"])
            kb.op("pool", lambda e: e.tensor_tensor(out=xr[:L, :], in0=xr[:L, :], in1=rp[:L, :], op=ALU.add), reads=["rp", "xr"], writes=["xr"])
            r_ = xr[:L, 0:D]; kr = xr[:L, D:2 * D]; vr = xr[:L, 2 * D:3 * D]
            kb.dma("pool", S["vs"][t0:t0 + L, :], vr, reads=["xr"], writes=[("vs", ti)])
            kb.op("act", lambda e: e.activation(out=xr[:L, 3 * D:3 * D + 64], in_=xr[:L, 3 * D:3 * D + 64], func=AF.Tanh), reads=["xr"], writes=["xr"])
            p = self.rr("ptr", 2)
            pt, kpt = self.ptr[p], "ptr%d" % p
            for i2 in range(2):
                kb.op("pe", lambda e, i2=i2, pt=pt: e.transpose(out=pt[:64, i2 * 128:i2 * 128 + L], in_=xr[:L, 3 * D + 64 * i2:3 * D + 64 * i2 + 64],
                                                             identity=self.ident[:L, :L]), reads=["xr", "ident"], writes=[kpt])
            kb.op("dve", lambda e, pt=pt: e.tensor_copy(out=lt[0:64, :, :L], in_=pt[:64, 0:256].rearrange("p (a t) -> p a t", a=2)[:, :, :L]), reads=[kpt], writes=["lt"])
            for i2, dst in enumerate(("lw", "a")):
                for hf in range(2):
                    pq = self.rr("pp", 4)
                    pp, kpp = self.pp[pq], "pp%d" % pq
                    kb.op("pe", lambda e, i2=i2, hf=hf, pp=pp: e.matmul(pp[:L, :], lhsT=lt[:, i2, :L], rhs=w2[:, i2, hf * 512:(hf + 1) * 512], start=True, stop=True),
                          reads=["lt", "lt1", "w2"], writes=[kpp])
                    kb.op("act", lambda e, dst=dst, hf=hf, pp=pp: e.activation(out=tv[dst][:L, hf * 512:(hf + 1) * 512], in_=pp[:L, :], func=AF.Sigmoid),
                          reads=[kpp], writes=[("tv", dst)])
            kb.op("dve", lambda e: e.tensor_scalar(out=tv["lw"][:L, :], in0=tv["lw"][:L, :], scalar1=EW, scalar2=None, op0=ALU.mult), reads=[("tv", "lw")], writes=[("tv", "lw")])
            kb.op("dve", lambda e: e.tensor_tensor(out=tv["an"][:L, :], in0=kr, in1=P_["r_k_k"][:L, :], op=ALU.mult), reads=["xr", "r_k_k"], writes=[("tv", "an")])
            kb.op("pool", lambda e: e.tensor_tensor(out=tv["b"][:L, :], in0=tv["an"][:L, :], in1=tv["an"][:L, :], op=ALU.mult), reads=[("tv", "an")], writes=[("tv", "b")])
            kb.op("dve", lambda e: e.tensor_reduce(out=ssq[:L, 0:16], in_=tv["b"][:L, :].rearrange("t (h j) -> t h j", h=16), axis=AX.X, op=ALU.add),
                  reads=[("tv", "b")], writes=["ssq"])
            kb.op("act", lambda e: e.activation(out=ssq[:L, 0:16], in_=ssq[:L, 0:16], func=AF.Sqrt), reads=["ssq"], writes=["ssq"])
            kb.op("dve", lambda e: e.tensor_scalar(out=ssq[:L, 0:16], in0=ssq[:L, 0:16], scalar1=1e-12, scalar2=None, op0=ALU.max), reads=["ssq"], writes=["ssq"])
            kb.op("dve", lambda e: e.reciprocal(out=ssq[:L, 16:32], in_=ssq[:L, 0:16]), reads=["ssq"], writes=["ssq"])
            hv = lambda x: x[:L, :].rearrange("t (h j) -> t h j", h=16)
            kb.op("dve", lambda e: e.tensor_tensor(out=hv(tv["an"]), in0=hv(tv["an"]), in1=ssq[:L, 16:32].unsqueeze(2).to_broadcast([L, 16, 64]), op=ALU.mult),
                  reads=[("tv", "an"), "ssq"], writes=[("tv", "an")])
            kb.op("pool", lambda e: e.tensor_tensor(out=tv["b"][:L, :], in0=tv["an"][:L, :], in1=tv["a"][:L, :], op=ALU.mult),
                  reads=[("tv", "an"), ("tv", "a")], writes=[("tv", "b")])
            kb.op("dve", lambda e: e.tensor_scalar(out=tv["an"][:L, :], in0=tv["an"][:L, :], scalar1=-1.0, scalar2=None, op0=ALU.mult),
                  reads=[("tv", "an"), ("tv", "b")], writes=[("tv", "an")])
            kb.op("dve", lambda e: e.scalar_tensor_tensor(out=tv["k"][:L, :], in0=tv["a"][:L, :], scalar=-1.0, in1=P_["r_k_a"][:L, :], op0=ALU.add, op1=ALU.mult),
                  reads=[("tv", "a"), "r_k_a"], writes=[("tv", "k")])
            kb.op("dve", lambda e: e.scalar_tensor_tensor(out=tv["k"][:L, :], in0=tv["k"][:L, :], scalar=1.0, in1=kr, op0=ALU.add, op1=ALU.mult),
                  reads=[("tv", "k"), "xr"], writes=[("tv", "k")])
            kb.op("pool", lambda e: e.tensor_tensor(out=tv["a"][:L, :], in0=tv["k"][:L, :], in1=P_["r_r_k"][:L, :], op=ALU.mult),
                  reads=[("tv", "k"), "r_r_k", ("tv", "b")], writes=[("tv", "a")])
            kb.op("pool", lambda e: e.tensor_tensor(out=tv["a"][:L, :], in0=tv["a"][:L, :], in1=r_, op=ALU.mult), reads=[("tv", "a"), "xr"], writes=[("tv", "a")])
            kb.op("dve", lambda e: e.tensor_reduce(out=ssq[:L, 32:48], in_=hv(tv["a"]), axis=AX.X, op=ALU.add), reads=[("tv", "a")], writes=["ssq2"])
            for (dst, src, ksrc) in (("at", tv["an"][:L, :], ("tv", "an")), ("rt", r_, "xr"), ("bh", tv["b"][:L, :], ("tv", "b")),
                                     ("kh", tv["k"][:L, :], ("tv", "k")), ("cum", tv["lw"][:L, :], ("tv", "lw"))):
                for half in range(2):
                    p = self.rr("ptr", 2)
                    pt, kpt = self.ptr[p], "ptr%d" % p
                    for k4 in range(4):
                        kc = half * 4 + k4
                        kb.op("pe", lambda e, k4=k4, kc=kc, pt=pt, src=src: e.transpose(out=pt[:, k4 * 128:k4 * 128 + L], in_=src[:, kc * 128:(kc + 1) * 128],
                                                                                  identity=self.ident[:L, :L]), reads=[ksrc, "ident"], writes=[kpt])
                    self.evac(fm[dst][:, half * 4:half * 4 + 4, :L], pt[:, :].rearrange("p (k t) -> p k t", k=4)[:, :, :L], [kpt], [("fm", dst)])
            Lc = 1 if is_s else 64
            nch = L // Lc
            lwT = fm["cum"]
            kb.op("pool", lambda e: e.tensor_copy(out=fm["et"][:, :, :L], in_=lwT[:, :, :L]), reads=[("fm", "cum")], writes=[("fm", "et")])
            if Lc > 1:
                for hh in range(8):
                    for c in range(nch):
                        kb.op("dve", lambda e, hh=hh, c=c: e.tensor_tensor_scan(out=fm["cum"][:, hh, c * Lc:(c + 1) * Lc], data0=self.onesf[:, :Lc],
                                                                             data1=fm["et"][:, hh, c * Lc:(c + 1) * Lc], initial=0.0, op0=ALU.mult, op1=ALU.add),
                              reads=[("fm", "et"), "onesf"], writes=[("fm", "cum")])
            F = lambda n: fm[n][:, :, :L]
            C4 = lambda n: fm[n][:, :, :L].rearrange("p k (c t) -> p k c t", t=Lc)
            kb.op("dve", lambda e: e.tensor_tensor(out=F("et"), in0=F("cum"), in1=F("et"), op=ALU.subtract), reads=[("fm", "cum"), ("fm", "et")], writes=[("fm", "et")])
            kb.op("act", lambda e: e.activation(out=F("et"), in_=F("et"), func=AF.Exp), reads=[("fm", "et")], writes=[("fm", "et")])
            kb.op("dve", lambda e: e.tensor_tensor(out=F("at"), in0=F("at"), in1=F("et"), op=ALU.mult), reads=[("fm", "at"), ("fm", "et")], writes=[("fm", "at")])
            kb.op("act", lambda e: e.activation(out=F("et"), in_=F("cum"), func=AF.Exp), reads=[("fm", "cum"), ("fm", "at")], writes=[("fm", "et")])
            kb.op("dve", lambda e: e.tensor_tensor(out=F("rt"), in0=F("rt"), in1=F("et"), op=ALU.mult), reads=[("fm", "rt"), ("fm", "et")], writes=[("fm", "rt")])
            kb.op("pool", lambda e: e.tensor_copy(out=wc[:, :, :nch], in_=C4("et")[:, :, :, Lc - 1]), reads=[("fm", "et")], writes=["wc"])
            kb.op("dve", lambda e: e.tensor_tensor(out=C4("et"), in0=C4("cum")[:, :, :, Lc - 1:Lc].to_broadcast([128, 8, nch, Lc]), in1=C4("cum"), op=ALU.subtract),
                  reads=[("fm", "cum"), ("fm", "rt"), "wc"], writes=[("fm", "et")])
            kb.op("act", lambda e: e.activation(out=F("et"), in_=F("et"), func=AF.Exp), reads=[("fm", "et")], writes=[("fm", "et")])
            kb.op("dve", lambda e: e.tensor_tensor(out=F("bc"), in0=F("bh"), in1=F("et"), op=ALU.mult), reads=[("fm", "bh"), ("fm", "et")], writes=[("fm", "bc")])
            kb.op("pool", lambda e: e.tensor_tensor(out=F("kc"), in0=F("kh"), in1=F("et"), op=ALU.mult), reads=[("fm", "kh"), ("fm", "et")], writes=[("fm", "kc")])
            kb.op("act", lambda e: e.activation(out=F("et"), in_=F("cum"), func=AF.Exp, scale=-1.0), reads=[("fm", "cum"), ("fm", "bc"), ("fm", "kc")], writes=[("fm", "et")])
            kb.op("dve", lambda e: e.tensor_tensor(out=F("bh"), in0=F("bh"), in1=F("et"), op=ALU.mult), reads=[("fm", "bh"), ("fm", "et")], writes=[("fm", "bh")])
            kb.op("pool", lambda e: e.tensor_tensor(out=F("kh"), in0=F("kh"), in1=F("et"), op=ALU.mult), reads=[("fm", "kh"), ("fm", "et")], writes=[("fm", "kh")])

        def bank():
            q_ = self.rr("bank8", 8)
            return self.bank8[q_], self.bank8k[q_]

        def core(gi, col0, Lc, cidx):
            g = G_[gi]
            K_ = lambda n: ("g", gi, n)
            hs = slice(4 * gi, 4 * gi + 4)
            cs_ = slice(col0, col0 + Lc)
            for half in range(2):
                ps = slice(half * 64, half * 64 + 64)
                kb.op("dve" if half == 0 else "pool", lambda e, ps=ps, half=half: e.tensor_copy(
                    out=g["UBD"][ps, :, :, half * 64:half * 64 + Lc], in_=fmall[ps, hs, 0:4, cs_]),
                    reads=[("fm", n) for n in ("at", "rt", "bh", "kh")], writes=[K_("UBD")])
            yield
            for half in range(2):
                ps = slice(half * 64, half * 64 + 64)
                kb.op("pool" if half == 0 else "dve", lambda e, ps=ps, half=half: e.tensor_copy(
                    out=BK[ps, :, :, half * 64:half * 64 + Lc], in_=fmall[ps, hs, 4:6, cs_]),
                    reads=[("fm", "bc"), ("fm", "kc")], writes=["BK"])
            for vi, dst in enumerate(("Btm", "Ktm")):
                pb_, kpb = bank()
                for u in range(4):
                    kb.op("pe", lambda e, u=u, vi=vi, pb_=pb_: e.matmul(pb_[:, u * 128:(u + 1) * 128], lhsT=BK[:, u, vi, :], rhs=identc[:, :], start=True, stop=True),
                          reads=["BK", "identc"], writes=[kpb])
                self.evac(g[dst][:, :, :], pb_[:, :].rearrange("p (u m) -> p u m", u=4), [kpb], [K_(dst)])
            yield
            for (vec, dst) in ((2, "MNb"), (3, "MNk")):
                for pr in range(2):
                    pb_, kpb = bank()
                    for u2 in range(2):
                        u = pr * 2 + u2
                        kb.op("pe", lambda e, u=u, u2=u2, vec=vec, pb_=pb_: e.matmul(pb_[:, u2 * 256:(u2 + 1) * 256], lhsT=g["UBD"][:, u, vec, :],
                                                                              rhs=g["UBD"][:, u, 0:2, :].rearrange("p a m -> p (a m)"), start=True, stop=True),
                              reads=[K_("UBD")], writes=[kpb])
                    kb.op("dve", lambda e, pr=pr, dst=dst, pb_=pb_: e.tensor_tensor(out=g[dst][:, pr * 2:pr * 2 + 2, :], in0=pb_[:, :].rearrange("p (u m) -> p u m", u=2),
                                                                           in1=mks[:, None, 0:256].to_broadcast([128, 2, 256]), op=ALU.mult),
                          reads=[kpb, "mks"], writes=[K_(dst)])
            pb_, kpb = bank()
            for u in range(4):
                kb.op("pe", lambda e, u=u, pb_=pb_: e.matmul(pb_[:, u * 128:(u + 1) * 128], lhsT=g["UBD"][:, u, 0, :], rhs=g["UBD"][:, u, 2, :], start=True, stop=True),
                      reads=[K_("UBD")], writes=[kpb])
            kb.op("dve", lambda e, pb_=pb_: e.tensor_tensor(out=g["Xt"][:, :, :], in0=pb_[:, :].rearrange("p (u m) -> p u m", u=4),
                                                       in1=mks[:, None, 256:384].to_broadcast([128, 4, 128]), op=ALU.mult),
                  reads=[kpb, "mks"], writes=[K_("Xt")])
            kb.op("pool", lambda e: e.tensor_copy(out=g["X"][:, :, :], in_=g["MNb"][:, :, 0:128]), reads=[K_("MNb")], writes=[K_("X")])
            kb.op("pool", lambda e: e.tensor_tensor(out=g["P"][:, :, :], in0=g["MNb"][:, :, 0:128], in1=identc[:, None, :].to_broadcast([128, 4, 128]), op=ALU.add),
                  reads=[K_("MNb"), "identc"], writes=[K_("P")])
            yield
            nlev = 0
            while (1 << (nlev + 1)) < Lc:
                nlev += 1
            for lev in range(nlev if Lc > 1 else 0):
                lastlev = (lev == nlev - 1)
                pb1 = kp1 = None
                if not lastlev:
                    pb1, kp1 = bank()
                    for u in range(4):
                        kb.op("pe", lambda e, u=u, pb1=pb1: e.matmul(pb1[:, u * 128:(u + 1) * 128], lhsT=g["Xt"][:, u, :], rhs=g["X"][:, u, :], start=True, stop=True),
                              reads=[K_("X"), K_("Xt")], writes=[kp1])
                pb2, kp2 = bank()
                for u in range(4):
                    kb.op("pe", lambda e, u=u, pb2=pb2: e.matmul(pb2[:, u * 128:(u + 1) * 128], lhsT=g["X"][:, u, :], rhs=g["Xt"][:, u, :], start=True, stop=True),
                          reads=[K_("X"), K_("Xt")], writes=[kp2])
                if not lastlev:
                    self.evac(g["X"][:, :, :], pb1[:, :].rearrange("p (u m) -> p u m", u=4), [kp1], [K_("X")])
                self.evac(g["Xt"][:, :, :], pb2[:, :].rearrange("p (u m) -> p u m", u=4), [kp2], [K_("Xt")])
                pb3, kp3 = bank()
                for u in range(4):
                    kb.op("pe", lambda e, u=u, pb3=pb3: e.matmul(pb3[:, u * 128:(u + 1) * 128], lhsT=g["Xt"][:, u, :], rhs=g["P"][:, u, :], start=True, stop=True),
                          reads=[K_("Xt"), K_("P")], writes=[kp3])
                kb.op("dve", lambda e, pb3=pb3: e.tensor_tensor(out=g["P"][:, :, :], in0=pb3[:, :].rearrange("p (u m) -> p u m", u=4), in1=g["P"][:, :, :], op=ALU.add),
                      reads=[kp3, K_("P")], writes=[K_("P")])
                yield
            kS = [("ST", 4 * gi + u) for u in range(4)]
            kSb = [kSTb(4 * gi + u) for u in range(4)]
            kVb = [kVpb(4 * gi + u) for u in range(4)]
            pb_, kpb = bank()
            for u in range(4):
                hh = 4 * gi + u
                kb.op("pe", lambda e, u=u, hh=hh, pb_=pb_: e.matmul(pb_[:, u * 64:(u + 1) * 64], lhsT=g["UBD"][:, u, 0, :], rhs=STb[:, hh, :], start=True, stop=False),
                      reads=[K_("UBD"), kSb[u]], writes=[kpb])
                kb.op("pe", lambda e, u=u, hh=hh, pb_=pb_: e.matmul(pb_[:, u * 64:(u + 1) * 64], lhsT=g["MNk"][:, u, 0:128], rhs=Vpb[:, hh, :], start=False, stop=True),
                      reads=[K_("MNk"), kVb[u]], writes=[kpb])
            self.evac(g["RHS"][:, :, :], pb_[:, 0:256].rearrange("p (u m) -> p u m", u=4), [kpb], [K_("RHS")])
            pb_, kpb = bank()
            for u in range(4):
                kb.op("pe", lambda e, u=u, pb_=pb_: e.matmul(pb_[:, u * 64:(u + 1) * 64], lhsT=g["P"][:, u, :], rhs=g["RHS"][:, u, :], start=True, stop=True),
                      reads=[K_("P"), K_("RHS")], writes=[kpb])
            self.evac(g["U"][:, :, :], pb_[:, 0:256].rearrange("p (u m) -> p u m", u=4), [kpb], [K_("U")])
            yield
            pb_, kpb = bank()
            for u in range(4):
                hh = 4 * gi + u
                kb.op("pe", lambda e, u=u, hh=hh, pb_=pb_: e.matmul(pb_[:, u * 64:(u + 1) * 64], lhsT=g["UBD"][:, u, 1, :], rhs=STb[:, hh, :], start=True, stop=False),
                      reads=[K_("UBD"), kSb[u]], writes=[kpb])
                kb.op("pe", lambda e, u=u, pb_=pb_: e.matmul(pb_[:, u * 64:(u + 1) * 64], lhsT=g["MNb"][:, u, 128:256], rhs=g["U"][:, u, :], start=False, stop=False),
                      reads=[K_("MNb"), K_("U")], writes=[kpb])
                kb.op("pe", lambda e, u=u, hh=hh, pb_=pb_: e.matmul(pb_[:, u * 64:(u + 1) * 64], lhsT=g["MNk"][:, u, 128:256], rhs=Vpb[:, hh, :], start=False, stop=True),
                      reads=[K_("MNk"), kVb[u]], writes=[kpb])
            self.evac(Ych[:, hs, :], pb_[:, 0:256].rearrange("p (u m) -> p u m", u=4), [kpb], [("Ych", 4 * gi + u) for u in range(4)])
            pb_, kpb = bank()
            for u in range(4):
                hh = 4 * gi + u
                kb.op("pe", lambda e, u=u, pb_=pb_: e.matmul(pb_[:, u * 64:(u + 1) * 64], lhsT=g["Btm"][:, u, :], rhs=g["U"][:, u, :], start=True, stop=False),
                      reads=[K_("Btm"), K_("U")], writes=[kpb])
                kb.op("pe", lambda e, u=u, hh=hh, pb_=pb_: e.matmul(pb_[:, u * 64:(u + 1) * 64], lhsT=g["Ktm"][:, u, :], rhs=Vpb[:, hh, :], start=False, stop=True),
                      reads=[K_("Ktm"), kVb[u]], writes=[kpb])
            kb.op("pool", lambda e: e.tensor_tensor(out=ST[:, hs, :], in0=ST[:, hs, :], in1=wc[:, hs, cidx:cidx + 1].to_broadcast([128, 4, 64]), op=ALU.mult),
                  reads=kS + ["wc"], writes=kS)
            kb.op("dve", lambda e, pb_=pb_: e.tensor_tensor(out=ST[:, hs, :], in0=ST[:, hs, :], in1=pb_[:, 0:256].rearrange("p (u m) -> p u m", u=4), op=ALU.add),
                  reads=kS + [kpb], writes=kS)
            if self.CORE_BF16:
                kb.op("act", lambda e: e.activation(out=STb[:, hs, :], in_=ST[:, hs, :], func=AF.Copy), reads=kS, writes=kSb)
            yield

        def run_core(col0, Lc, cidx):
            alive = [core(gi, col0, Lc, cidx) for gi in range(NG)]
            while alive:
                for g_ in list(alive):
                    try:
                        next(g_)
                    except StopIteration:
                        alive.remove(g_)

        def post(t0, L, ti, bi):
            kb.dma("sp", ytm[:L, :], S["ys"][t0:t0 + L, :], reads=[("ys", ti)], writes=["rp"])
            kb.dma("sp", zrt[:L, :], S["z"][t0:t0 + L, O_ZR:O_ZR + D], reads=z_all(ti), writes=["rp"])
            kb.op("act", lambda e: e.activation(out=zrt[:L, :], in_=zrt[:L, :], func=AF.Silu), reads=["rp"], writes=["rp"])
            hv = lambda x: x[:L, :].rearrange("t (h j) -> t h j", h=16)
            kb.op("dve", lambda e: e.tensor_reduce(out=yst[:L, 0:16], in_=hv(ytm), axis=AX.X, op=ALU.add), reads=["rp"], writes=["yst"])
            kb.op("dve", lambda e: e.tensor_scalar(out=yst[:L, 0:16], in0=yst[:L, 0:16], scalar1=-1.0 / 64.0, scalar2=None, op0=ALU.mult), reads=["yst"], writes=["yst"])
            kb.op("dve", lambda e: e.tensor_tensor(out=hv(ytm), in0=hv(ytm), in1=yst[:L, 0:16].unsqueeze(2).to_broadcast([L, 16, 64]), op=ALU.add),
                  reads=["rp", "yst"], writes=["rp"])
            kb.op("pool", lambda e: e.tensor_tensor(out=tv["lw"][:L, :], in0=ytm[:L, :], in1=ytm[:L, :], op=ALU.mult), reads=["rp"], writes=[("tv", "lw")])
            kb.op("dve", lambda e: e.tensor_reduce(out=yst[:L, 16:32], in_=hv(tv["lw"]), axis=AX.X, op=ALU.add), reads=[("tv", "lw")], writes=["yst"])
            kb.op("act", lambda e: e.activation(out=yst[:L, 32:48], in_=yst[:L, 16:32], func=AF.Sqrt, scale=1.0 / 64.0, bias=e12[:L, 0:1]), reads=["yst", "e12"], writes=["yst"])
            kb.op("dve", lambda e: e.reciprocal(out=yst[:L, 48:64], in_=yst[:L, 32:48]), reads=["yst"], writes=["yst"])
            kb.op("dve", lambda e: e.tensor_tensor(out=hv(ytm), in0=hv(ytm), in1=yst[:L, 48:64].unsqueeze(2).to_broadcast([L, 16, 64]), op=ALU.mult),
                  reads=["rp", "yst"], writes=["rp"])
            kb.op("dve", lambda e: e.tensor_tensor(out=ytm[:L, :], in0=ytm[:L, :], in1=P_["r_ln_g"][:L, :], op=ALU.mult), reads=["rp", "r_ln_g"], writes=["rp"])
            kb.op("pool", lambda e: e.tensor_tensor(out=ytm[:L, :], in0=ytm[:L, :], in1=P_["r_ln_b"][:L, :], op=ALU.add), reads=["rp", "r_ln_b"], writes=["rp"])
            kb.op("dve", lambda e: e.tensor_tensor(out=hv(tv["lw"]), in0=xr[:L, 2 * D:3 * D].rearrange("t (h j) -> t h j", h=16),
                                                   in1=ssq[:L, 32:48].unsqueeze(2).to_broadcast([L, 16, 64]), op=ALU.mult), reads=["xr", "ssq2"], writes=[("tv", "lw")])
            kb.op("pool", lambda e: e.tensor_tensor(out=ytm[:L, :], in0=ytm[:L, :], in1=tv["lw"][:L, :], op=ALU.add), reads=["rp", ("tv", "lw")], writes=["rp"])
            kb.op("dve", lambda e: e.tensor_tensor(out=ytm[:L, :], in0=ytm[:L, :], in1=zrt[:L, :], op=ALU.mult), reads=["rp", "rp"], writes=["rp"])
            aj = self.rr("raob", 2)
            for half in range(2):
                p = self.rr("ptr", 2)
                pt, kpt = self.ptr[p], "ptr%d" % p
                for k4 in range(4):
                    kc = half * 4 + k4
                    kb.op("pe", lambda e, k4=k4, kc=kc, pt=pt: e.transpose(out=pt[:, k4 * 128:k4 * 128 + L], in_=ytm[:L, kc * 128:(kc + 1) * 128],
                                                                       identity=self.ident[:L, :L]), reads=["rp", "ident"], writes=[kpt])
                self.evac(aob[aj][:, half * 4:half * 4 + 4, :L], pt[:, :].rearrange("p (k t) -> p k t", k=4)[:, :, :L], [kpt], ["raob%d" % aj])
            kb.dma("pool", S["act_r"][:, t0:t0 + L].rearrange("(k p) t -> p k t", p=128), aob[aj][:, :, :L], reads=["raob%d" % aj], writes=[("act", "r", bi)])

        def state_out(dst):
            for hh in range(8):
                kb.op("pool", lambda e, hh=hh: e.memset(natx[:, :], 0.0), writes=["natx"])
                for half in range(2):
                    ps = slice(half * 64, half * 64 + 64)
                    kb.op("pool", lambda e, hh=hh, ps=ps: e.tensor_copy(out=natx[ps, ps], in_=ST[ps, hh, :]), reads=[("ST", hh)], writes=["natx"])
                p = self.rr("ptr", 2)
                pt, kpt = self.ptr[p], "ptr%d" % p
                kb.op("pe", lambda e, pt=pt: e.transpose(out=pt[:, 0:128], in_=natx[:, :], identity=self.ident[:, :]), reads=["natx", "ident"], writes=[kpt])
                for half in range(2):
                    ps = slice(half * 64, half * 64 + 64)
                    kb.op("dve", lambda e, hh=hh, ps=ps, pt=pt: e.tensor_copy(out=nato[ps, hh, :], in_=pt[ps, ps]), reads=[kpt], writes=["nato"])
            kb.dma("pool", dst.rearrange("(hh hl) i j -> (hl i) hh j", hl=2), nato[:, :, :], reads=["nato"], writes=[], is_output=True)

        def state_in(src):
            kb.dma("sp", nat[:, :, :], src.rearrange("(hh hl) i j -> (hl i) hh j", hl=2), writes=["nat"])
            for hh in range(8):
                kb.op("pool", lambda e: e.memset(natx[:, :], 0.0), writes=["natx"])
                for half in range(2):
                    ps = slice(half * 64, half * 64 + 64)
                    kb.op("pool", lambda e, hh=hh, ps=ps: e.tensor_copy(out=natx[ps, ps], in_=nat[ps, hh, :]), reads=["nat"], writes=["natx"])
                p = self.rr("ptr", 2)
                pt, kpt = self.ptr[p], "ptr%d" % p
                kb.op("pe", lambda e, pt=pt: e.transpose(out=pt[:, 0:128], in_=natx[:, :], identity=self.ident[:, :]), reads=["natx", "ident"], writes=[kpt])
                for half in range(2):
                    ps = slice(half * 64, half * 64 + 64)
                    kb.op("dve", lambda e, hh=hh, ps=ps, pt=pt: e.tensor_copy(out=ST[ps, hh, :], in_=pt[ps, ps]), reads=[kpt], writes=[("ST", hh)])
                    if self.CORE_BF16:
                        kb.op("pool", lambda e, hh=hh, ps=ps, pt=pt: e.tensor_copy(out=STb[ps, hh, :], in_=ST[ps, hh, :]), reads=[("ST", hh)], writes=[("STb", hh)])

        blk_of = lambda t: next(i for i, (b0, bn) in enumerate(self.blocks) if b0 <= t < b0 + bn)
        for ti in range(T // 128):
            t0 = ti * 128
            prep(t0, 128, ti, False)
            for c in range(2):
                for half in range(2):
                    kb.dma("sp", Vp[half * 64:half * 64 + 64, :, :],
                           S["vs"][t0 + c * 64:t0 + c * 64 + 64, :].rearrange("t (hh hl i) -> hl t hh i", hl=2, i=64)[half],
                           reads=[("vs", ti)], writes=[("Vp", h) for h in range(8)])
                vp_ready()
                run_core(c * 64, 64, c)
                for half in range(2):
                    kb.dma("pool", S["ys"][t0 + c * 64:t0 + c * 64 + 64, :].rearrange("t (hh hl i) -> hl t hh i", hl=2, i=64)[half],
                           Ych[half * 64:half * 64 + 64, :, :], reads=[("Ych", h) for h in range(8)], writes=[("ys", ti)])
            post(t0, 128, ti, blk_of(t0))
        state_out(O["wkv_prompt"][l])
        if NS:
            ti = T // 128
            prep(T, NS, ti, True)
            zero_units()
            kb.op("pool", lambda e: e.memset(Vp[:], 0.0), writes=[("Vp", h) for h in range(8)])
            for b in range(NS):
                state_in(I["state_rwkv_wkv"][l, b])
                for half in range(2):
                    kb.dma("sp", Vp[half * 64:half * 64 + 1, :, :],
                           S["vs"][T + b:T + b + 1, :].rearrange("t (hh hl i) -> hl t hh i", hl=2, i=64)[half],
                           reads=[("vs", ti)], writes=[("Vp", h) for h in range(8)])
                vp_ready()
                run_core(b, 1, b)
                for half in range(2):
                    kb.dma("pool", S["ys"][T + b:T + b + 1, :].rearrange("t (hh hl i) -> hl t hh i", hl=2, i=64)[half],
                           Ych[half * 64:half * 64 + 1, :, :], reads=[("Ych", h) for h in range(8)], writes=[("ys", ti)])
                state_out(O["wkv_sample"][l, b])
            post(T, NS, ti, blk_of(T))
        self.end_phase()

    def sin_to(self, tkey, out, ang, tf, ti, tm, key_out, key_ang, shift=0.0):
        kb = self.kb
        TWO_PI = 2.0 * math.pi
        kt = ("sin_tmp", tkey)
        kb.op("dve", lambda e: e.tensor_scalar(out=tm, in0=ang, scalar1=shift, scalar2=None, op0=ALU.add),
              reads=[key_ang], writes=[(kt, "m")])
        kb.op("dve", lambda e: e.tensor_scalar(out=tf, in0=tm, scalar1=1.0 / TWO_PI, scalar2=None, op0=ALU.mult),
              reads=[(kt, "m")], writes=[(kt, "f")])
        kb.op("dve", lambda e: e.tensor_copy(out=ti, in_=tf), reads=[(kt, "f")], writes=[(kt, "i")])
        kb.op("dve", lambda e: e.tensor_copy(out=tf, in_=ti), reads=[(kt, "i")], writes=[(kt, "f")])
        kb.op("dve", lambda e: e.scalar_tensor_tensor(out=tm, in0=tf, scalar=-TWO_PI, in1=tm, op0=ALU.mult, op1=ALU.add),
              reads=[(kt, "f"), (kt, "m")], writes=[(kt, "m")])
        kb.op("dve", lambda e: e.tensor_scalar(out=tf, in0=tm, scalar1=math.pi, scalar2=-TWO_PI, op0=ALU.is_gt, op1=ALU.mult),
              reads=[(kt, "m")], writes=[(kt, "f")])
        kb.op("dve", lambda e: e.tensor_tensor(out=tm, in0=tm, in1=tf, op=ALU.add), reads=[(kt, "f"), (kt, "m")],
              writes=[(kt, "m")])
        kb.op("dve", lambda e: e.tensor_scalar(out=tf, in0=tm, scalar1=-math.pi, scalar2=TWO_PI, op0=ALU.is_lt, op1=ALU.mult),
              reads=[(kt, "m")], writes=[(kt, "f")])
        kb.op("dve", lambda e: e.tensor_tensor(out=tm, in0=tm, in1=tf, op=ALU.add), reads=[(kt, "f"), (kt, "m")],
              writes=[(kt, "m")])
        kb.op("dve", lambda e: e.tensor_scalar(out=tm, in0=tm, scalar1=-3.1415925, scalar2=3.1415925, op0=ALU.max, op1=ALU.min),
              reads=[(kt, "m")], writes=[(kt, "m")])
        kb.op("act", lambda e: e.activation(out=out, in_=tm, func=AF.Sin), reads=[(kt, "m")], writes=[key_out])

    def s5_phase(self, l):
        kb = self.kb
        I, S, O = self.I, self.S, self.O
        T, NS = self.T, self.NS
        self.begin_phase()
        sbp = self.sbp
        nat = sbp("nat", (32, 14, 128))
        nati = sbp("nati", (32, 128), I32)
        PT = sbp("PT", (128, 6, 32))
        BR = sbp("BR", (128, 32, 16)); BI = sbp("BI", (128, 32, 16))
        bbr = sbp("bbr", (128, 32, 16)); bbi = sbp("bbi", (128, 32, 16))
        xin = sbp("xin", (128, 512))
        btmp = xin[:, :].rearrange("p (s c) -> p s c", c=16)
        Bz = [sbp("Bz%d" % i, (128, 32, 128), BF16) for i in range(2)]
        Cn = [sbp("Cn%d" % i, (128, 8, 64)) for i in range(2)]
        Cz = [sbp("Cz%d" % i, (128, 32, 128), BF16) for i in range(2)]
        cosT = sbp("cosT", (128, 32, 128)); sinT = sbp("sinT", (128, 32, 128))
        ang = sbp("ang", (128, 2, 128)); angf = sbp("angf", (128, 2, 128)); angm = sbp("angm", (128, 2, 128))
        angi = sbp("angi", (128, 2, 128), I32)
        car = [sbp("car%d" % i, (128, 32)) for i in range(2)]
        ctmp = sbp("ctmp", (128, 4))
        m3 = sbp("m3", (128, 4, 2)); m4 = sbp("m4", (128, 4, 8)); iota1 = sbp("iota1", (128, 128))
        dcol = sbp("dcol", (128, 8)); gbcol = sbp("gbcol", (128, 8))
        wg = sbp("wglu", (128, KC, D), BF16)
        wstg = [sbp("wstg%d" % i, (128, KC, 128)) for i in range(1)]
        uf = [sbp("uf%d" % i, (128, 256)) for i in range(2)]
        ub = [sbp("ub%d" % i, (128, 256), BF16) for i in range(2)]
        WS = []
        for w_ in range(2):
            WS.append(([sbp("tq%d_%d" % (w_, i), (128, 256)) for i in range(4)],
                       [sbp("bh%d_%d" % (w_, i), (128, 256)) for i in range(2)],
                       [sbp("sh%d_%d" % (w_, i), (128, 256)) for i in range(2)],
                       [sbp("sbf%d_%d" % (w_, i), (128, 256), BF16) for i in range(2)]))
        ysg = sbp("ysg", (128, KC, 256)); ysb = sbp("ysb", (128, KC, 256), BF16)
        yt = [sbp("yt%d" % i, (128, 256)) for i in range(2)]
        zst = [sbp("zst%d" % i, (128, 256)) for i in range(2)]
        aout = [sbp("aout%d" % i, (128, 256), BF16) for i in range(2)]
        s0T = [sbp("s0T%d" % i, (128, 32, 16)) for i in range(2)]
        snw = [sbp("snw%d" % i, (128, 32, 16)) for i in range(2)]
        sst = sbp("sst", (32, 512))

        kb.dma("sp", m3[:], I["mask3"], writes=["m3"])
        kb.dma("sp", m4[:], I["mask4"], writes=["m4"])
        kb.dma("sp", iota1[:], I["iota1"], writes=["iota1"])
        kb.dma("sp", nat[:, 0, :], I["s_lam_re"][l].rearrange("(s g) p -> s (g p)", g=2), writes=["nat0"])
        kb.dma("sp", nat[:, 1, :], I["s_lam_im"][l].rearrange("(s g) p -> s (g p)", g=2), writes=["nat1"])
        kb.dma("sp", nat[:, 2, 0:2], I["s_log_dt"][l].rearrange("(s g) -> s g", g=2), writes=["nat2"])
        kb.dma("sp", dcol[:], I["s_d"][l].rearrange("(k p) -> p k", p=128), writes=["dcol"], allow_slow_non_contiguous=True)
        kb.dma("sp", gbcol[:], I["s_glu_b"][l].rearrange("(k p) -> p k", p=128), writes=["gbcol"], allow_slow_non_contiguous=True)
        kb.dma("sp", BR[:], I["s_b_re"][l].rearrange("(s g) p c -> (g p) s c", g=2), writes=["BR"])
        kb.dma("sp", BI[:], I["s_b_im"][l].rearrange("(s g) p c -> (g p) s c", g=2), writes=["BI"])
        kb.dma("sp", Cn[0][:], I["s_c_re"][l].rearrange("(o g) c p -> (g c) o p", g=8), writes=["Cn0"])
        kb.dma("sp", Cn[1][:], I["s_c_im"][l].rearrange("(o g) c p -> (g c) o p", g=8), writes=["Cn1"])
        for h in range(8):
            kb.dma("sp", wstg[0][:], I["s_glu_w"][l][:, h * 128:(h + 1) * 128].rearrange("(k p) c -> p k c", p=128),
                   writes=["wstg"])
            kb.op("pool", lambda e, h=h: e.tensor_copy(out=wg[:, :, h * 128:(h + 1) * 128], in_=wstg[0][:]),
                  reads=["wstg"], writes=["wg"])
        N = lambda i: nat[:, i, :]
        kb.op("act", lambda e: e.activation(out=nat[:, 2, 2:4], in_=nat[:, 2, 0:2], func=AF.Exp), reads=["nat2"], writes=["nat2"])
        kb.op("dve", lambda e: e.tensor_copy(out=nat[:, 3, :].rearrange("s (g p) -> s g p", g=2),
                                             in_=nat[:, 2, 2:4].unsqueeze(2).to_broadcast([32, 2, 64])),
              reads=["nat2"], writes=["nat3"])
        kb.op("dve", lambda e: e.tensor_scalar(out=N(0), in0=N(0), scalar1=-1e-4, scalar2=None, op0=ALU.min),
              reads=["nat0"], writes=["nat0"])
        kb.op("dve", lambda e: e.tensor_tensor(out=N(4), in0=N(0), in1=N(3), op=ALU.mult), reads=["nat0", "nat3"], writes=["nat4"])
        kb.op("act", lambda e: e.activation(out=N(4), in_=N(4), func=AF.Exp), reads=["nat4"], writes=["nat4"])
        kb.op("dve", lambda e: e.tensor_tensor(out=N(5), in0=N(1), in1=N(3), op=ALU.mult), reads=["nat1", "nat3"], writes=["nat5"])
        self.sin_to("nat", N(6), N(5), N(12), nati[:, :], N(13), "nat6", "nat5")
        self.sin_to("nat", N(7), N(5), N(12), nati[:, :], N(13), "nat7", "nat5", shift=math.pi / 2)
        kb.op("dve", lambda e: e.tensor_tensor(out=N(8), in0=N(4), in1=N(7), op=ALU.mult), reads=["nat4", "nat7"], writes=["nat8"])
        kb.op("dve", lambda e: e.tensor_tensor(out=N(9), in0=N(4), in1=N(6), op=ALU.mult), reads=["nat4", "nat6"], writes=["nat9"])
        kb.op("dve", lambda e: e.tensor_tensor(out=N(12), in0=N(0), in1=N(0), op=ALU.mult), reads=["nat0"], writes=["nat12"])
        kb.op("dve", lambda e: e.tensor_tensor(out=N(13), in0=N(1), in1=N(1), op=ALU.mult), reads=["nat1"], writes=["nat13"])
        kb.op("dve", lambda e: e.tensor_tensor(out=N(12), in0=N(12), in1=N(13), op=ALU.add), reads=["nat12", "nat13"], writes=["nat12"])
        kb.op("dve", lambda e: e.reciprocal(out=N(12), in_=N(12)), reads=["nat12"], writes=["nat12"])
        kb.op("dve", lambda e: e.tensor_scalar(out=N(13), in0=N(8), scalar1=-1.0, scalar2=None, op0=ALU.add), reads=["nat8"], writes=["nat13"])
        kb.op("dve", lambda e: e.tensor_tensor(out=N(10), in0=N(13), in1=N(0), op=ALU.mult), reads=["nat13", "nat0"], writes=["nat10"])
        kb.op("dve", lambda e: e.tensor_tensor(out=N(11), in0=N(9), in1=N(1), op=ALU.mult), reads=["nat9", "nat1"], writes=["nat11"])
        kb.op("dve", lambda e: e.tensor_tensor(out=N(10), in0=N(10), in1=N(11), op=ALU.add), reads=["nat10", "nat11"], writes=["nat10"])
        kb.op("dve", lambda e: e.tensor_tensor(out=N(10), in0=N(10), in1=N(12), op=ALU.mult), reads=["nat10", "nat12"], writes=["nat10"])
        kb.op("dve", lambda e: e.tensor_tensor(out=N(11), in0=N(9), in1=N(0), op=ALU.mult), reads=["nat9", "nat0"], writes=["nat11"])
        kb.op("dve", lambda e: e.tensor_tensor(out=N(13), in0=N(13), in1=N(1), op=ALU.mult), reads=["nat13", "nat1"], writes=["nat13"])
        kb.op("dve", lambda e: e.tensor_tensor(out=N(11), in0=N(11), in1=N(13), op=ALU.subtract), reads=["nat11", "nat13"], writes=["nat11"])
        kb.op("dve", lambda e: e.tensor_tensor(out=N(11), in0=N(11), in1=N(12), op=ALU.mult), reads=["nat11", "nat12"], writes=["nat11"])
        p = self.rr("ptr", 2)
        pt, kpt = self.ptr[p], "ptr%d" % p
        for si, ni in enumerate((4, 5, 8, 9, 10, 11)):
            kb.op("pe", lambda e, si=si, ni=ni, pt=pt: e.transpose(out=pt[:, si * 32:(si + 1) * 32], in_=nat[:, ni, :],
                                                                  identity=self.ident[:32, :32]),
                  reads=["nat%d" % ni, "ident"], writes=[kpt])
        kb.op("dve", lambda e, pt=pt: e.tensor_copy(out=PT[:, :, :], in_=pt[:, 0:192].rearrange("p (s c) -> p s c", s=6)),
              reads=[kpt], writes=["PT"])
        bc = lambda si: PT[:, si, :].unsqueeze(2).to_broadcast([128, 32, 16])
        kb.op("dve", lambda e: e.tensor_tensor(out=bbr[:], in0=BR[:], in1=bc(4), op=ALU.mult), reads=["BR", "PT"], writes=["bbr"])
        kb.op("dve", lambda e: e.tensor_tensor(out=btmp, in0=BI[:], in1=bc(5), op=ALU.mult), reads=["BI", "PT"], writes=["xin"])
        kb.op("dve", lambda e: e.tensor_tensor(out=bbr[:], in0=bbr[:], in1=btmp, op=ALU.subtract), reads=["bbr", "xin"], writes=["bbr"])
        kb.op("dve", lambda e: e.tensor_tensor(out=bbi[:], in0=BI[:], in1=bc(4), op=ALU.mult), reads=["BI", "PT"], writes=["bbi"])
        kb.op("dve", lambda e: e.tensor_tensor(out=btmp, in0=BR[:], in1=bc(5), op=ALU.mult), reads=["BR", "PT", "bbr"], writes=["xin"])
        kb.op("dve", lambda e: e.tensor_tensor(out=bbi[:], in0=bbi[:], in1=btmp, op=ALU.add), reads=["bbi", "xin"], writes=["bbi"])
        for ri, (bb, kbb) in enumerate(((bbr, "bbr"), (bbi, "bbi"))):
            for oc in range(8):
                kb.op("dve", lambda e, bb=bb, oc=oc: e.tensor_tensor(
                    out=xin[:, :].rearrange("p (q g c) -> p q g c", q=4, g=8),
                    in0=bb[:, oc * 4:oc * 4 + 4, None, :].to_broadcast([128, 4, 8, 16]),
                    in1=m4[:, :, :, None].to_broadcast([128, 4, 8, 16]), op=ALU.mult),
                    reads=[kbb, "m4"], writes=["xin"])
                p = self.rr("ptr", 2)
                pt, kpt = self.ptr[p], "ptr%d" % p
                for q in range(4):
                    kb.op("pe", lambda e, q=q, pt=pt: e.transpose(out=pt[:, q * 128:(q + 1) * 128], in_=xin[:, q * 128:(q + 1) * 128],
                                                                  identity=self.ident[:, :]),
                          reads=["xin", "ident"], writes=[kpt])
                kb.op("act", lambda e, pt=pt, oc=oc, ri=ri: e.activation(
                    out=Bz[ri][:, oc * 4:oc * 4 + 4, :], in_=pt[:, :].rearrange("p (q m) -> p q m", q=4), func=AF.Copy),
                    reads=[kpt], writes=[("Bz", ri)])
        for ri in range(2):
            for oc in range(8):
                kb.op("dve", lambda e, ri=ri, oc=oc: e.tensor_tensor(
                    out=xin[:, :].rearrange("p (q g s) -> p q g s", q=4, g=2),
                    in0=Cn[ri][:, oc, None, None, :].to_broadcast([128, 4, 2, 64]),
                    in1=m3[:, :, :, None].to_broadcast([128, 4, 2, 64]), op=ALU.mult),
                    reads=["Cn%d" % ri, "m3"], writes=["xin"])
                p = self.rr("ptr", 2)
                pt, kpt = self.ptr[p], "ptr%d" % p
                for q in range(4):
                    kb.op("pe", lambda e, q=q, pt=pt: e.transpose(out=pt[:, q * 128:(q + 1) * 128], in_=xin[:, q * 128:(q + 1) * 128],
                                                                  identity=self.ident[:, :]),
                          reads=["xin", "ident"], writes=[kpt])
                kb.op("act", lambda e, pt=pt, oc=oc, ri=ri: e.activation(
                    out=Cz[ri][:, oc * 4:oc * 4 + 4, :], in_=pt[:, :].rearrange("p (q m) -> p q m", q=4), func=AF.Copy,
                    scale=(1.0 if ri == 0 else -1.0)),
                    reads=[kpt], writes=[("Cz", ri)])
        for g in range(16):
            for s8 in range(2):
                sc = g * 2 + s8
                kb.op("dve", lambda e, s8=s8, sc=sc: e.tensor_scalar(out=ang[:, s8, :], in0=iota1[:, :], scalar1=PT[:, 1, sc:sc + 1],
                                                                     scalar2=None, op0=ALU.mult),
                      reads=["iota1", "PT"], writes=["ang"])
            self.sin_to("ang", sinT[:, g * 2:(g + 1) * 2, :], ang[:], angf[:], angi[:], angm[:], ("sinT", g), "ang")
            self.sin_to("ang", cosT[:, g * 2:(g + 1) * 2, :], ang[:], angf[:], angi[:], angm[:], ("cosT", g), "ang",
                        shift=math.pi / 2)
        tabs = [("sinT", g) for g in range(16)] + [("cosT", g) for g in range(16)]
        kb.op("dve", lambda e: e.memset(car[0][:], 0.0), writes=[("car", 0, sc_) for sc_ in range(32)])
        kb.op("dve", lambda e: e.memset(car[1][:], 0.0), writes=[("car", 1, sc_) for sc_ in range(32)])

        if NS:
            for ri, nm in enumerate(("state_s5_re", "state_s5_im")):
                p = self.rr("ptr", 2)
                pt, kpt = self.ptr[p], "ptr%d" % p
                for qq in range(8):
                    kb.dma("sp", sst[:NS, :], I[nm][l].rearrange("b g p -> b (g p)")[:, qq * 512:(qq + 1) * 512], writes=["sst"])
                    for s8 in range(4):
                        sc = qq * 4 + s8
                        kb.op("pe", lambda e, sc=sc, s8=s8, pt=pt: e.transpose(out=pt[:, sc * NS:(sc + 1) * NS], in_=sst[:NS, s8 * 128:(s8 + 1) * 128],
                                                                      identity=self.ident[:NS, :NS]),
                              reads=["sst", "ident"], writes=[kpt])
                kb.op("dve", lambda e, pt=pt, ri=ri: e.tensor_copy(out=s0T[ri][:, :, :NS],
                                                                  in_=pt[:, :32 * NS].rearrange("p (s b) -> p s b", s=32)),
                      reads=[kpt], writes=[("s0T", ri)])

        subblocks = []
        for bi, (t0b, nb) in enumerate(self.blocks):
            for t0 in range(t0b, t0b + nb, 256):
                subblocks.append((bi, t0, min(256, t0b + nb - t0)))
        for (bi, t0, n) in subblocks:
            is_s = (t0 >= T)
            nsub = n // 128
            for oc in range(8):
                j = self.rr("uf", 2)
                kuf, kub = "uf%d" % j, "ub%d" % j
                row = O_U + oc * 128
                kb.dma("sp", uf[j][:, :n], S["zT"][row:row + 128, t0:t0 + n], reads=[("zT", row, bi)], writes=[kuf])
                kb.op("pool", lambda e, j=j: e.tensor_copy(out=ub[j][:, :n], in_=uf[j][:, :n]), reads=[kuf], writes=[kub])
                py = self.rr("ptr", 2)
                pys, kpys = self.ptr[py], "ptr%d" % py
                def sc_gen(q, W):
                    tq, bh, sh, sbf = W
                    wk = lambda n_: (n_, id(W))
                    sc = oc * 4 + q
                    pa = self.rr("pp", 4); pb = self.rr("pp", 4)
                    A, B = self.pp[pa], self.pp[pb]
                    kA, kB = "pp%d" % pa, "pp%d" % pb
                    kb.op("pe", lambda e, A=A, sc=sc, j=j: e.matmul(A[:, :n], lhsT=Bz[0][:, sc, :], rhs=ub[j][:, :n], start=True, stop=True),
                          reads=[("Bz", 0), kub], writes=[kA])
                    kb.op("pe", lambda e, B=B, sc=sc, j=j: e.matmul(B[:, :n], lhsT=Bz[1][:, sc, :], rhs=ub[j][:, :n], start=True, stop=True),
                          reads=[("Bz", 1), kub], writes=[kB])
                    yield
                    if not is_s:
                        v3 = lambda x: x[:, :n].rearrange("p (m j) -> p m j", j=128)
                        cosv = cosT[:, sc, None, :].to_broadcast([128, nsub, 128])
                        sinv = sinT[:, sc, None, :].to_broadcast([128, nsub, 128])
                        kb.op("dve", lambda e, A=A, cosv=cosv: e.tensor_tensor(out=v3(tq[0]), in0=v3(A), in1=cosv, op=ALU.mult),
                              reads=[kA] + tabs, writes=[wk("tq0")])
                        kb.op("dve", lambda e, B=B, sinv=sinv: e.tensor_tensor(out=v3(tq[1]), in0=v3(B), in1=sinv, op=ALU.mult),
                              reads=[kB] + tabs, writes=[wk("tq1")])
                        kb.op("dve", lambda e, B=B, cosv=cosv: e.tensor_tensor(out=v3(tq[2]), in0=v3(B), in1=cosv, op=ALU.mult),
                              reads=[kB] + tabs, writes=[wk("tq2")])
                        kb.op("dve", lambda e, A=A, sinv=sinv: e.tensor_tensor(out=v3(tq[3]), in0=v3(A), in1=sinv, op=ALU.mult),
                              reads=[kA] + tabs, writes=[wk("tq3")])
                        kb.op("pool", lambda e: e.tensor_tensor(out=bh[0][:, :n], in0=tq[0][:, :n], in1=tq[1][:, :n], op=ALU.add),
                              reads=[wk("tq0"), wk("tq1")], writes=[wk("bh0")])
                        kb.op("pool", lambda e: e.tensor_tensor(out=bh[1][:, :n], in0=tq[2][:, :n], in1=tq[3][:, :n], op=ALU.subtract),
                              reads=[wk("tq2"), wk("tq3")], writes=[wk("bh1")])
                        yield
                        rho = PT[:, 0, sc:sc + 1].to_broadcast([128, 128])
                        c128 = cosT[:, sc, 127:128]
                        s128 = sinT[:, sc, 127:128]
                        for m in range(nsub):
                            sl = slice(m * 128, (m + 1) * 128)
                            for ri in range(2):
                                kb.op("dve", lambda e, ri=ri, sl=sl, sc=sc, rho=rho: e.tensor_tensor_scan(
                                    out=sh[ri][:, sl], data0=rho, data1=bh[ri][:, sl], initial=car[ri][:, sc:sc + 1],
                                    op0=ALU.mult, op1=ALU.add), reads=[wk("bh%d" % ri), ("car", ri, sc), "PT"], writes=[wk("sh%d" % ri)])
                            lr = sh[0][:, m * 128 + 127:m * 128 + 128]
                            li = sh[1][:, m * 128 + 127:m * 128 + 128]
                            kb.op("dve", lambda e, li=li, s128=s128: e.tensor_scalar(out=ctmp[:, 2 * (q % 2):2 * (q % 2) + 1], in0=li, scalar1=s128, scalar2=None, op0=ALU.mult),
                                  reads=[wk("sh1")] + tabs, writes=[wk("ctmp")])
                            kb.op("dve", lambda e, li=li, c128=c128: e.tensor_scalar(out=ctmp[:, 2 * (q % 2) + 1:2 * (q % 2) + 2], in0=li, scalar1=c128, scalar2=None, op0=ALU.mult),
                                  reads=[wk("sh1")] + tabs, writes=[wk("ctmp")])
                            kb.op("dve", lambda e, lr=lr, c128=c128, sc=sc: e.scalar_tensor_tensor(
                                out=car[0][:, sc:sc + 1], in0=lr, scalar=c128, in1=ctmp[:, 2 * (q % 2):2 * (q % 2) + 1], op0=ALU.mult, op1=ALU.subtract),
                                reads=[wk("sh0"), wk("ctmp")] + tabs, writes=[("car", 0, sc)])
                            kb.op("dve", lambda e, lr=lr, s128=s128, sc=sc: e.scalar_tensor_tensor(
                                out=car[1][:, sc:sc + 1], in0=lr, scalar=s128, in1=ctmp[:, 2 * (q % 2) + 1:2 * (q % 2) + 2], op0=ALU.mult, op1=ALU.add),
                                reads=[wk("sh0"), wk("ctmp")] + tabs, writes=[("car", 1, sc)])
                        yield
                        kb.op("pool", lambda e, cosv=cosv: e.tensor_tensor(out=v3(tq[0]), in0=v3(sh[0]), in1=cosv, op=ALU.mult),
                              reads=[wk("sh0")] + tabs, writes=[wk("tq0")])
                        kb.op("pool", lambda e, sinv=sinv: e.tensor_tensor(out=v3(tq[1]), in0=v3(sh[1]), in1=sinv, op=ALU.mult),
                              reads=[wk("sh1")] + tabs, writes=[wk("tq1")])
                        kb.op("dve", lambda e, sinv=sinv: e.tensor_tensor(out=v3(tq[2]), in0=v3(sh[0]), in1=sinv, op=ALU.mult),
                              reads=[wk("sh0")] + tabs, writes=[wk("tq2")])
                        kb.op("dve", lambda e, cosv=cosv: e.tensor_tensor(out=v3(tq[3]), in0=v3(sh[1]), in1=cosv, op=ALU.mult),
                              reads=[wk("sh1")] + tabs, writes=[wk("tq3")])
                        kb.op("pool", lambda e: e.tensor_tensor(out=sbf[0][:, :n], in0=tq[0][:, :n], in1=tq[1][:, :n], op=ALU.subtract),
                              reads=[wk("tq0"), wk("tq1")], writes=[wk("sbf0")])
                        kb.op("dve", lambda e: e.tensor_tensor(out=sbf[1][:, :n], in0=tq[2][:, :n], in1=tq[3][:, :n], op=ALU.add),
                              reads=[wk("tq2"), wk("tq3")], writes=[wk("sbf1")])
                    else:
                        lbre = PT[:, 2, sc:sc + 1]
                        lbim = PT[:, 3, sc:sc + 1]
                        s0r, s0i = s0T[0][:, sc, :n], s0T[1][:, sc, :n]
                        kb.op("dve", lambda e, s0i=s0i, lbim=lbim: e.tensor_scalar(out=tq[0][:, :n], in0=s0i, scalar1=lbim, scalar2=None, op0=ALU.mult),
                              reads=[("s0T", 1), "PT"], writes=[wk("tq0")])
                        kb.op("dve", lambda e, s0r=s0r, lbre=lbre: e.scalar_tensor_tensor(out=tq[0][:, :n], in0=s0r, scalar=lbre, in1=tq[0][:, :n],
                                                                               op0=ALU.mult, op1=ALU.subtract),
                              reads=[("s0T", 0), "PT", wk("tq0")], writes=[wk("tq0")])
                        kb.op("dve", lambda e, A=A, sc=sc: e.tensor_tensor(out=snw[0][:, sc, :n], in0=A[:, :n], in1=tq[0][:, :n], op=ALU.add),
                              reads=[kA, wk("tq0")], writes=[("snw", 0)])
                        kb.op("dve", lambda e, s0r=s0r, lbim=lbim: e.tensor_scalar(out=tq[1][:, :n], in0=s0r, scalar1=lbim, scalar2=None, op0=ALU.mult),
                              reads=[("s0T", 0), "PT"], writes=[wk("tq1")])
                        kb.op("dve", lambda e, s0i=s0i, lbre=lbre: e.scalar_tensor_tensor(out=tq[1][:, :n], in0=s0i, scalar=lbre, in1=tq[1][:, :n],
                                                                               op0=ALU.mult, op1=ALU.add),
                              reads=[("s0T", 1), "PT", wk("tq1")], writes=[wk("tq1")])
                        kb.op("dve", lambda e, B=B, sc=sc: e.tensor_tensor(out=snw[1][:, sc, :n], in0=B[:, :n], in1=tq[1][:, :n], op=ALU.add),
                              reads=[kB, wk("tq1")], writes=[("snw", 1)])
                        for ri in range(2):
                            kb.op("pool", lambda e, ri=ri, sc=sc: e.tensor_copy(out=sbf[ri][:, :n], in_=snw[ri][:, sc, :n]),
                                  reads=[("snw", ri)], writes=[wk("sbf%d" % ri)])
                    yield
                    for ri in range(2):
                        kb.op("pe", lambda e, ri=ri, sc=sc, q=q, pys=pys: e.matmul(pys[:, :n], lhsT=Cz[ri][:, sc, :], rhs=sbf[ri][:, :n],
                                                                                start=(q == 0 and ri == 0), stop=(q == 3 and ri == 1)),
                              reads=[("Cz", ri), wk("sbf%d" % ri)], writes=[kpys])
                for qp in range(2):
                    alive = [sc_gen(2 * qp + w_, WS[w_]) for w_ in range(2)]
                    while alive:
                        for g_ in list(alive):
                            try:
                                next(g_)
                            except StopIteration:
                                alive.remove(g_)
                y0, y1 = yt[0], yt[1]
                kb.op("dve", lambda e, j=j, oc=oc, pys=pys: e.scalar_tensor_tensor(out=y0[:, :n], in0=uf[j][:, :n], scalar=dcol[:, oc:oc + 1],
                                                                             in1=pys[:, :n], op0=ALU.mult, op1=ALU.add),
                      reads=[kuf, "dcol", kpys], writes=["yt0"])
                kb.op("act", lambda e: e.activation(out=y1[:, :n], in_=y0[:, :n], func=AF.Square), reads=["yt0"], writes=["yt1"])
                kb.op("dve", lambda e: e.tensor_scalar(out=y1[:, :n], in0=y1[:, :n], scalar1=0.044715, scalar2=1.0, op0=ALU.mult, op1=ALU.add),
                      reads=["yt1"], writes=["yt1"])
                kb.op("dve", lambda e: e.tensor_tensor(out=y1[:, :n], in0=y1[:, :n], in1=y0[:, :n], op=ALU.mult), reads=["yt1", "yt0"], writes=["yt1"])
                kb.op("act", lambda e: e.activation(out=y1[:, :n], in_=y1[:, :n], func=AF.Sigmoid, scale=2.0 * math.sqrt(2.0 / math.pi)),
                      reads=["yt1"], writes=["yt1"])
                kb.op("dve", lambda e, oc=oc: e.tensor_tensor(out=ysg[:, oc, :n], in0=y1[:, :n], in1=y0[:, :n], op=ALU.mult),
                      reads=["yt1", "yt0"], writes=[("ysg", oc)])
                kb.op("pool", lambda e, oc=oc: e.tensor_copy(out=ysb[:, oc, :n], in_=ysg[:, oc, :n]), reads=[("ysg", oc)], writes=[("ysb", oc)])
            for ec in range(8):
                p = self.rr("pp", 4)
                pp, kpp = self.pp[p], "pp%d" % p
                for kc in range(KC):
                    kb.op("pe", lambda e, kc=kc, ec=ec, pp=pp: e.matmul(pp[:, :n], lhsT=wg[:, kc, ec * 128:(ec + 1) * 128], rhs=ysb[:, kc, :n],
                                                                      start=(kc == 0), stop=(kc == KC - 1)),
                          reads=["wg"] + [("ysb", k) for k in range(KC)], writes=[kpp])
                zj = self.rr("zst", 2)
                kz = "zst%d" % zj
                row = O_ZS + ec * 128
                kb.dma("sp", zst[zj][:, :n], S["zT"][row:row + 128, t0:t0 + n], reads=[("zT", row, bi)], writes=[kz])
                kb.op("act", lambda e, zj=zj: e.activation(out=zst[zj][:, :n], in_=zst[zj][:, :n], func=AF.Silu), reads=[kz], writes=[kz])
                kb.op("act", lambda e, pp=pp, ec=ec: e.activation(out=yt[0][:, :n], in_=pp[:, :n], func=AF.Sigmoid, bias=gbcol[:, ec:ec + 1]),
                      reads=[kpp, "gbcol"], writes=["yt0"])
                kb.op("dve", lambda e, ec=ec: e.tensor_tensor(out=yt[0][:, :n], in0=yt[0][:, :n], in1=ysg[:, ec, :n], op=ALU.mult),
                      reads=["yt0", ("ysg", ec)], writes=["yt0"])
                aj = self.rr("aout", 2)
                kb.op("dve", lambda e, aj=aj, zj=zj: e.tensor_tensor(out=aout[aj][:, :n], in0=yt[0][:, :n], in1=zst[zj][:, :n], op=ALU.mult),
                      reads=["yt0", kz], writes=["aout%d" % aj])
                kb.dma("pool", S["act_s"][ec * 128:(ec + 1) * 128, t0:t0 + n], aout[aj][:, :n],
                       reads=["aout%d" % aj], writes=[("act", "s", bi)])
        for ri, nm in enumerate(("s5_re", "s5_im")):
            p = self.rr("ptr", 2)
            pt, kpt = self.ptr[p], "ptr%d" % p
            kb.op("pe", lambda e, pt=pt, ri=ri: e.transpose(out=pt[:32, 0:128], in_=car[ri][:, :], identity=self.ident[:, :]),
                  reads=[("car", ri, sc_) for sc_ in range(32)] + ["ident"], writes=[kpt])
            kb.op("dve", lambda e, pt=pt: e.tensor_copy(out=sst[:32, 0:128], in_=pt[:32, 0:128]), reads=[kpt], writes=["sst"])
            kb.dma("pool", O[nm + "_prompt"][l].rearrange("(s q) -> s q", q=128), sst[:32, 0:128], reads=["sst"], writes=[],
                   is_output=True)
            if NS:
                for g in range(8):
                    p = self.rr("ptr", 2)
                    pt, kpt = self.ptr[p], "ptr%d" % p
                    for s4 in range(4):
                        sc = g * 4 + s4
                        kb.op("pe", lambda e, pt=pt, ri=ri, sc=sc, s4=s4: e.transpose(out=pt[:NS, s4 * 128:(s4 + 1) * 128], in_=snw[ri][:, sc, :NS],
                                                                                   identity=self.ident[:, :]),
                              reads=[("snw", ri), "ident"], writes=[kpt])
                    kb.op("dve", lambda e, pt=pt, g=g: e.tensor_copy(out=sst[:NS, 0:512], in_=pt[:NS, :]), reads=[kpt], writes=["sst"])
                    kb.dma("pool", O[nm + "_sample"][l].rearrange("b g p -> b (g p)")[:, g * 512:(g + 1) * 512], sst[:NS, 0:512],
                           reads=["sst"], writes=[], is_output=True)
        self.end_phase()

    def load_sq(self, name, dram):
        kb = self.kb
        dst = self.wsq[name]
        for h in range(2):
            i = self.rr("w", 2)
            ws, kws = self.wst[i], "wst%d" % i
            kb.dma("sp", ws[:, :, :512], dram[:, h * 512:(h + 1) * 512].rearrange("(k p) c -> p k c", p=128),
                   writes=[kws])
            kb.op("pool", lambda e, ws=ws, h=h: e.tensor_copy(out=dst[:, :, h * 512:(h + 1) * 512], in_=ws[:, :, :512]),
                  reads=[kws], writes=[("wsq", name)])

    def phase3(self, l):
        kb = self.kb
        I, S = self.I, self.S
        last = (l == self.DEPTH - 1)
        self.begin_phase()
        self.wst = [self.sbp("wst%d" % i, (128, KC, 512)) for i in range(2)]
        self.wsq = {n: self.sbp("wsq_" + n, (128, KC, D), BF16) for n in ("bm", "br", "bs", "out")}
        self.actb = [self.sbp("actb%d" % i, (128, KC, 512), BF16) for i in range(2)]
        self.gmt = [self.sbp("gmt%d" % i, (128, 512)) for i in range(2)]
        self.mrg = self.sbp("mrg", (128, KC, 512), BF16)
        self.macc = self.sbp("macc", (128, KC, 512))
        self.ln_alloc()
        for nme in ("bm", "br", "bs", "out"):
            self.load_sq(nme, I["w_" + nme][l])
        self.load_ln_params(I["ln_g"][l], I["ln_b"][l])
        for bi, (t0, n) in enumerate(self.blocks):
            for ec in range(KC):
                for b, bn in enumerate("mrs"):
                    pass
            acts = {}
            for b, bn in enumerate("mrs"):
                if not self.have_branch(bn):
                    continue
                j = self.rr("actb", 2)
                at, kat = self.actb[j], "actb%d" % j
                kb.dma("sp", at[:, :, :n], S["act_" + bn][:, t0:t0 + n].rearrange("(k p) t -> p k t", p=128),
                       reads=[("act", bn, bi)], writes=[kat])
                for ec in range(KC):
                    p = self.rr("pp", 4)
                    pp, kpp = self.pp[p], "pp%d" % p
                    for kc in range(KC):
                        kb.op("pe", lambda e, kc=kc, pp=pp, ec=ec, at=at, bn=bn: e.matmul(
                            pp[:, :n], lhsT=self.wsq["b" + bn][:, kc, ec * 128:(ec + 1) * 128], rhs=at[:, kc, :n],
                            start=(kc == 0), stop=(kc == KC - 1)),
                            reads=[kat, ("wsq", "b" + bn)], writes=[kpp])
                    g = self.rr("gmt", 2)
                    gt, kgt = self.gmt[g], "gmt%d" % g
                    row = O_GM + b * D + ec * 128
                    kb.dma("sp", gt[:, :n], S["zT"][row:row + 128, t0:t0 + n],
                           reads=[("zT", row, bi)], writes=[kgt])
                    kb.op("act", lambda e, gt=gt: e.activation(out=gt[:, :n], in_=gt[:, :n], func=AF.Sigmoid),
                          reads=[kgt], writes=[kgt])
                    first = (bn == self.first_branch())
                    lastb = (bn == self.last_branch())
                    kmf = ("mrgf", ec)
                    acc = self.macc[:, ec, :n]
                    if first:
                        kb.op("dve", lambda e, acc=acc, gt=gt, pp=pp: e.tensor_tensor(out=acc, in0=pp[:, :n], in1=gt[:, :n],
                                                                                 op=ALU.mult),
                              reads=[kpp, kgt], writes=[("macc", ec)])
                    else:
                        kb.op("dve", lambda e, gt=gt, pp=pp: e.tensor_tensor(out=gt[:, :n], in0=pp[:, :n], in1=gt[:, :n],
                                                                        op=ALU.mult),
                              reads=[kpp, kgt], writes=[kgt])
                        kb.op("pool", lambda e, acc=acc, gt=gt: e.tensor_tensor(out=acc, in0=acc, in1=gt[:, :n], op=ALU.add),
                              reads=[kgt, ("macc", ec)], writes=[("macc", ec)])
                    if lastb:
                        kb.op("pool", lambda e, acc=acc, ec=ec: e.tensor_copy(out=self.mrg[:, ec, :n], in_=acc),
                              reads=[("macc", ec)], writes=[("mrg", ec)])
            for tt in range(0, n, 128):
                nr = min(128, n - tt)
                r0 = t0 + tt
                ti = r0 // 128
                j = self.rr("xt", 2)
                xt, kxt = self.xt[j], "xt%d" % j
                kb.dma("sp", xt[:nr, :], S["xs"][r0:r0 + nr, :], reads=[("xs", ti)], writes=[kxt])
                if self.first_branch() is not None:
                    for h in range(2):
                        p = self.rr("pp", 4)
                        pp, kpp = self.pp[p], "pp%d" % p
                        for kc in range(KC):
                            kb.op("pe", lambda e, kc=kc, pp=pp, h=h, tt=tt, nr=nr: e.matmul(
                                pp[:nr, :], lhsT=self.mrg[:, kc, tt:tt + nr], rhs=self.wsq["out"][:, kc, h * 512:(h + 1) * 512],
                                start=(kc == 0), stop=(kc == KC - 1)),
                                reads=[("mrg", k) for k in range(KC)] + [("wsq", "out")], writes=[kpp])
                        kb.op("dve", lambda e, xt=xt, pp=pp, h=h, nr=nr: e.scalar_tensor_tensor(
                            out=xt[:nr, h * 512:(h + 1) * 512], in0=xt[:nr, h * 512:(h + 1) * 512], scalar=ALPHA,
                            in1=pp[:nr, :], op0=ALU.mult, op1=ALU.add), reads=[kxt, kpp], writes=[kxt])
                else:
                    kb.op("dve", lambda e, xt=xt, nr=nr: e.tensor_scalar(out=xt[:nr, :], in0=xt[:nr, :], scalar1=ALPHA,
                                                                         scalar2=None, op0=ALU.mult),
                          reads=[kxt], writes=[kxt])
                fo = None
                if last:
                    fo = self.O["y_prompt"][r0:r0 + nr, :] if r0 < self.T else self.O["y_sample"][:, :]
                self.ln_tile(xt, kxt, r0, nr, ti, S["xs"][r0:r0 + nr, :], final_out=fo)
        self.end_phase()

    branches = ""
    NPI = 4
    CORE_BF16 = True

    def have_branch(self, bn):
        return bn in self.branches

    def first_branch(self):
        return self.branches[0] if self.branches else None

    def last_branch(self):
        return self.branches[-1] if self.branches else None


def make_in_map(inputs, c, NS, T, consts):
    m = {}
    m["x_prompt"] = np.ascontiguousarray(inputs["x_prompt"][c, :T])
    m["x_sample"] = np.ascontiguousarray(inputs["x_sample"][c * NS:(c + 1) * NS, 0])
    for n in ("ln_in_g", "ln_in_b", "w_in", "w_out", "ln_g", "ln_b", "w_bm", "w_br", "w_bs", "s_lam_re", "s_lam_im",
              "s_log_dt", "s_b_re", "s_b_im", "s_c_re", "s_c_im", "s_d", "s_glu_w", "s_glu_b",
              "m_conv_w", "m_conv_b", "m_wq", "m_wk", "m_wv", "m_ig_b", "m_fg_b", "m_norm_g", "m_skip",
              "r_mu", "r_w0", "r_w2", "r_a0", "r_a2", "r_k_k", "r_k_a", "r_r_k", "r_ln_g", "r_ln_b"):
        m[n] = np.ascontiguousarray(inputs[n])
    for n in ("state_s5_re", "state_s5_im", "state_mlstm_conv", "state_mlstm_c", "state_mlstm_n", "state_mlstm_m",
              "state_rwkv_wkv", "state_rwkv_shift"):
        m[n] = np.ascontiguousarray(inputs[n][:, c * NS:(c + 1) * NS])
    m.update(consts)
    return m


def kernel(**inputs):
    inputs = {k: np.asarray(v) for k, v in inputs.items()}
    T, NS, L = 2048, 16, 4
    Prog.branches = BRANCHES
    prog = Prog(T, NS, L)
    nc = prog.build()
    consts = host_consts()
    in_maps = [make_in_map(inputs, c, NS, T, consts) for c in range(8)]
    res = run_bass_kernel_spmd(nc, in_maps, core_ids=list(range(8)))
    rs = res.results
    f = np.float32

    def pstack(name, shape):
        if name in rs[0]:
            return np.stack([np.asarray(rs[c][name]).reshape((L,) + shape) for c in range(8)], 1).astype(f)
        return np.zeros((L, 8) + shape, f)

    def sstack(name, shape):
        if name in rs[0]:
            return np.concatenate([np.asarray(rs[c][name]).reshape((L, NS) + shape) for c in range(8)], 1).astype(f)
        return np.zeros((L, 8 * NS) + shape, f)

    y_prompt = np.stack([rs[c]["y_prompt"] for c in range(8)], 0).astype(f)
    y_sample = np.concatenate([rs[c]["y_sample"] for c in range(8)], 0)[:, None, :].astype(f)
    return (y_prompt, y_sample,
            pstack("c_prompt", (4, 256, 256)), sstack("c_sample", (4, 256, 256)),
            pstack("n_prompt", (4, 256)), sstack("n_sample", (4, 256)),
            pstack("m_prompt", (4,)), sstack("m_sample", (4,)),
            pstack("conv_prompt", (3, D)), sstack("conv_sample", (3, D)),
            pstack("wkv_prompt", (16, 64, 64)), sstack("wkv_sample", (16, 64, 64)),
            pstack("shift_prompt", (R_SHIFT_W,)), sstack("shift_sample", (R_SHIFT_W,)),
            pstack("s5_re_prompt", (64, 64)), sstack("s5_re_sample", (64, 64)),
            pstack("s5_im_prompt", (64, 64)), sstack("s5_im_sample", (64, 64)))
```

```python
import math
from contextlib import ExitStack
import numpy as np
import concourse.bass as bass
import concourse.mybir as mybir
from concourse.bass_utils import run_bass_kernel_spmd

F32 = mybir.dt.float32
BF16 = mybir.dt.bfloat16
I32 = mybir.dt.int32
ALU = mybir.AluOpType
AF = mybir.ActivationFunctionType
AX = mybir.AxisListType

D = 1024
KC = 8
N_IN = 12424
M_HEADS = 4
M_HD = 256
R_HEADS = 16
R_HD = 64
R_SHIFT_W = 3200
S_GROUPS = 64
S_STATE = 64
DEPTH_FULL = 4
ALPHA = (2.0 * DEPTH_FULL) ** 0.25
LN_EPS = 1e-5
RWKV_GN_EPS = 64e-5
NEG = -1e30

O_XM, O_IG, O_FG, O_OG, O_ZM = 0, 1024, 1028, 1032, 2056
O_RC, O_ZR, O_U, O_ZS, O_GM = 3080, 6280, 7304, 8328, 9352


class KB:
    def __init__(self, nc, es):
        self.nc = nc
        self.eng = dict(pe=nc.tensor, act=nc.scalar, dve=nc.vector, pool=nc.gpsimd, sp=nc.sync)
        self.stream = {e: [] for e in self.eng}
        self.csem = {e: es.enter_context(nc.semaphore("c_" + e)) for e in ("pe", "act", "dve", "pool")}
        self.ccnt = {e: 0 for e in self.csem}
        self.ring = {}
        for q, n in (("sp", 24), ("pool", 12), ("act", 6)):
            self.ring[q] = [[es.enter_context(nc.semaphore("d_%s%d" % (q, i))), 0] for i in range(n)]
        self.rpos = {q: 0 for q in self.ring}
        self.known = {e: {} for e in self.eng}
        self.lastw = {}
        self.readers = {}
        self.out_tokens = []
        self.n_ops = 0

    def _need(self, e, tok, same_ok=False):
        if tok is None:
            return
        sem, val, owner = tok
        if owner == e and (e == "pe" or same_ok):
            return
        k = id(sem)
        if self.known[e].get(k, 0) >= val:
            return
        self.known[e][k] = val
        self.eng[e].wait_ge(sem, val)

    ns = None
    local = frozenset()

    def _k(self, b):
        if self.ns is None:
            return b
        base = b[0] if isinstance(b, tuple) else b
        return (self.ns, b) if base in self.local else b

    def _deps(self, e, reads, writes):
        reads = [self._k(b) for b in reads]
        writes = [self._k(b) for b in writes]
        for b in reads:
            self._need(e, self.lastw.get(b))
        for b in writes:
            self._need(e, self.lastw.get(b))
            for tok in self.readers.get(b, ()):
                self._need(e, tok)

    def _commit(self, tok, reads, writes):
        reads = [self._k(b) for b in reads]
        writes = [self._k(b) for b in writes]
        for b in writes:
            self.lastw[b] = tok
            self.readers[b] = []
        for b in reads:
            self.readers.setdefault(b, []).append(tok)

    def op(self, e, fn, reads=(), writes=()):
        self._deps(e, reads, writes)
        self.ccnt[e] += 1
        tok = (self.csem[e], self.ccnt[e], e)
        fn(self.eng[e]).then_inc(self.csem[e], 1)
        self._commit(tok, reads, writes)
        self.n_ops += 1

    def dma(self, q, out, in_, reads=(), writes=(), is_output=False, **kw):
        self._deps(q, reads, writes)
        ring = self.ring[q]
        slot = ring[self.rpos[q] % len(ring)]
        self.rpos[q] += 1
        sem = slot[0]
        if slot[1] > 0:
            self._need(q, (sem, slot[1], None))
        slot[1] += 16
        tok = (sem, slot[1], None)
        self.eng[q].dma_start(out=out, in_=in_, **kw).then_inc(sem, 16)
        self._commit(tok, reads, writes)
        if is_output:
            self.out_tokens.append(tok)
        self.n_ops += 1

    def finish(self):
        for q in self.ring:
            for sem, val in self.ring[q]:
                if val > 0:
                    self._need("sp", (sem, val, None))
        for e in self.csem:
            if self.ccnt[e] > 0:
                self._need("sp", (self.csem[e], self.ccnt[e], e))

    def barrier(self):
        for e in self.eng:
            for q in self.ring:
                for sem, val in self.ring[q]:
                    if val > 0:
                        self._need(e, (sem, val, None))
            for e2 in self.csem:
                if self.ccnt[e2] > 0 and e2 != e:
                    self._need(e, (self.csem[e2], self.ccnt[e2], e2))

    def emit(self):
        pass


BRANCHES = "mrs"


def host_consts():
    c = {}
    c["ident"] = np.eye(128, dtype=np.float32)
    m3 = np.zeros((128, 4, 2), np.float32)
    m4 = np.zeros((128, 4, 8), np.float32)
    for p in range(128):
        g8 = p // 16
        gl = p // 64
        for q in range(4):
            for g in range(2):
                if g8 == 2 * q + g:
                    m3[p, q, g] = 1.0
            for gg in range(8):
                if gg == 2 * q + gl:
                    m4[p, q, gg] = 1.0
    c["mask3"], c["mask4"] = m3, m4
    selh = np.zeros((4, 4, 128), np.float32)
    for h in range(4):
        selh[h, h, :] = 1.0
    c["selh"] = selh
    c["ones4"] = np.ones((4, 128), np.float32)
    cn = np.zeros((128, 128), np.float32)
    for s_ in range(128):
        cn[s_, :s_] = 1.0e30
    c["causneg"] = cn
    rm = np.zeros((128, 384), np.float32)
    for p in range(128):
        for q in range(128):
            if p // 64 == q // 64:
                s_, t_ = p % 64, q % 64
                rm[p, q] = 1.0 if s_ < t_ else 0.0
                rm[p, 128 + q] = 1.0 if s_ <= t_ else 0.0
                rm[p, 256 + q] = 1.0 if s_ > t_ else 0.0
    c["rmasks"] = rm
    c["iota1"] = np.tile(np.arange(1, 129, dtype=np.float32)[None, :], (128, 1))
    return c


class Prog:
    def __init__(self, T, NS, DEPTH):
        self.T, self.NS, self.DEPTH = T, NS, DEPTH
        self.NT = T + NS
        assert T % 128 == 0
        self.ntile = T // 128
        self.tiles = [(i * 128, 128) for i in range(self.ntile)] + [(T, NS)]
        self.blocks = []
        t = 0
        while t < T:
            n = min(512, T - t)
            self.blocks.append((t, n))
            t += n
        self.blocks.append((T, NS))

    def build(self):
        nc = bass.Bass("TRN2", target_bir_lowering=False)
        self.nc = nc
        T, NS, L, NT = self.T, self.NS, self.DEPTH, self.NT
        dt = nc.dram_tensor

        def inp(name, shape, dtype=F32):
            return dt(name, list(shape), dtype, kind="ExternalInput").ap()

        def outp(name, shape):
            return dt(name, list(shape), F32, kind="ExternalOutput").ap()

        def scr(name, shape, dtype=F32):
            return dt(name, list(shape), dtype, kind="Internal").ap()

        I = {}
        I["x_prompt"] = inp("x_prompt", (T, D))
        I["x_sample"] = inp("x_sample", (NS, D))
        I["ident"] = inp("ident", (128, 128))
        for n, s in (("ln_in_g", (D,)), ("ln_in_b", (D,)), ("w_in", (L, D, N_IN)), ("w_out", (L, D, D)),
                     ("ln_g", (L, D)), ("ln_b", (L, D)), ("w_bm", (L, D, D)), ("w_br", (L, D, D)),
                     ("w_bs", (L, D, D)), ("s_lam_re", (L, 64, 64)), ("s_lam_im", (L, 64, 64)), ("s_log_dt", (L, 64)),
                     ("s_b_re", (L, 64, 64, 16)), ("s_b_im", (L, 64, 64, 16)), ("s_c_re", (L, 64, 16, 64)),
                     ("s_c_im", (L, 64, 16, 64)), ("s_d", (L, D)), ("s_glu_w", (L, D, D)), ("s_glu_b", (L, D)),
                     ("state_s5_re", (L, NS, 64, 64)), ("state_s5_im", (L, NS, 64, 64)),
                     ("state_mlstm_conv", (L, NS, 3, D)), ("state_mlstm_c", (L, NS, 4, 256, 256)),
                     ("state_mlstm_n", (L, NS, 4, 256)), ("state_mlstm_m", (L, NS, 4)),
                     ("m_conv_w", (L, 4, D)), ("m_conv_b", (L, D)), ("m_wq", (L, 4, 256, 256)), ("m_wk", (L, 4, 256, 256)),
                     ("m_wv", (L, 4, 256, 256)), ("m_ig_b", (L, 4)), ("m_fg_b", (L, 4)), ("m_norm_g", (L, D)), ("m_skip", (L, D)),
                     ("state_rwkv_wkv", (L, NS, 16, 64, 64)), ("state_rwkv_shift", (L, NS, R_SHIFT_W)),
                     ("r_mu", (L, R_SHIFT_W)), ("r_w0", (L, D)), ("r_w2", (L, 64, D)), ("r_a0", (L, D)), ("r_a2", (L, 64, D)),
                     ("r_k_k", (L, D)), ("r_k_a", (L, D)), ("r_r_k", (L, 16, 64)), ("r_ln_g", (L, D)), ("r_ln_b", (L, D)),
                     ("rmasks", (128, 384)),
                     ("selh", (4, 4, 128)), ("causneg", (128, 128)), ("ones4", (4, 128)),
                     ("mask3", (128, 4, 2)), ("mask4", (128, 4, 8)), ("iota1", (128, 128))):
            I[n] = inp(n, s)
        self.I = I
        O = {}
        O["y_prompt"] = outp("y_prompt", (T, D))
        O["y_sample"] = outp("y_sample", (NS, D))
        O["c_prompt"] = outp("c_prompt", (L, 4, 256, 256))
        O["c_sample"] = outp("c_sample", (L, NS, 4, 256, 256))
        O["n_prompt"] = outp("n_prompt", (L, 4, 256))
        O["n_sample"] = outp("n_sample", (L, NS, 4, 256))
        O["m_prompt"] = outp("m_prompt", (L, 4))
        O["m_sample"] = outp("m_sample", (L, NS, 4))
        O["conv_prompt"] = outp("conv_prompt", (L, 3, D))
        O["conv_sample"] = outp("conv_sample", (L, NS, 3, D))
        O["wkv_prompt"] = outp("wkv_prompt", (L, 16, 64, 64))
        O["wkv_sample"] = outp("wkv_sample", (L, NS, 16, 64, 64))
        O["shift_prompt"] = outp("shift_prompt", (L, R_SHIFT_W))
        O["shift_sample"] = outp("shift_sample", (L, NS, R_SHIFT_W))
        for nm in ("s5_re", "s5_im"):
            O[nm + "_prompt"] = outp(nm + "_prompt", (L, 4096))
            O[nm + "_sample"] = outp(nm + "_sample", (L, NS, 64, 64))
        self.O = O
        S = {}
        S["xs"] = scr("xs", (NT, D))
        S["z"] = scr("z", (NT, N_IN))
        S["zT"] = scr("zT", (N_IN, NT))
        for b in "mrs":
            S["act_" + b] = scr("act_" + b, (D, NT), BF16)
        S["vs"] = scr("vs", (NT, D))
        S["ys"] = scr("ys", (NT, D))
        self.S = S

        with ExitStack() as es:
            self.es = es
            kb = KB(nc, es)
            self.kb = kb
            self.alloc()
            self.phase0()
            for l in range(L):
                self.phase1(l)
                self.easy_states(l)
                if self.have_branch("m"):
                    self.mlstm_phase(l)
                if self.have_branch("r"):
                    self.rwkv_phase(l)
                if self.have_branch("s"):
                    self.s5_phase(l)
                self.phase3(l)
            kb.finish()
            kb.emit()
        return nc

    def sb(self, name, shape, dtype=F32):
        return self.es.enter_context(self.nc.sbuf_tensor("sb_" + name, list(shape), dtype))

    def sbp(self, name, shape, dtype=F32):
        self.uid = getattr(self, "uid", 0) + 1
        return self.pes.enter_context(self.nc.sbuf_tensor("sp%d_%s" % (self.uid, name), list(shape), dtype))

    def begin_phase(self):
        self.pes = ExitStack()

    def end_phase(self):
        self.kb.barrier()
        self.pes.close()

    def ps(self, name, shape, dtype=F32):
        return self.es.enter_context(self.nc.psum_tensor("ps_" + name, list(shape), dtype))

    def alloc(self):
        NT = self.NT
        self.ident = self.sb("ident", (128, 128))
        self.xT = self.sb("xT", (128, KC, NT), BF16)
        self.pp = [self.ps("pp%d" % i, (128, 512)) for i in range(4)]
        self.ptr = [self.ps("ptr%d" % i, (128, 512)) for i in range(2)]
        self.pex = [self.ps("pex%d" % i, (128, 512)) for i in range(2)]
        self.cnt = {}
        kb = self.kb
        kb.dma("sp", self.ident[:], self.I["ident"], writes=["ident"])

    def rr(self, key, n):
        v = self.cnt.get(key, 0)
        self.cnt[key] = v + 1
        return v % n

    def ln_alloc(self):
        self.gbc = self.sbp("gbc", (128, D))
        self.bbc = self.sbp("bbc", (128, D))
        self.xt = [self.sbp("xt%d" % i, (128, D)) for i in range(2)]
        self.xc = [self.sbp("xc%d" % i, (128, D)) for i in range(2)]
        self.st = [self.sbp("st%d" % i, (128, 8)) for i in range(2)]

    def ln_tile(self, src, srckey, row0, nr, ti, xs_out, final_out=None):
        kb = self.kb
        i = self.rr("ln", 2)
        xc, st = self.xc[i], self.st[i]
        kxc, kst = "xc%d" % i, "st%d" % i
        kb.op("dve", lambda e: e.tensor_reduce(out=st[:nr, 0:1], in_=src[:nr, :], axis=AX.X, op=ALU.add),
              reads=[srckey], writes=[kst])
        kb.op("dve", lambda e: e.tensor_scalar(out=st[:nr, 1:2], in0=st[:nr, 0:1], scalar1=-1.0 / D, scalar2=None,
                                               op0=ALU.mult), reads=[kst], writes=[kst])
        kb.op("dve", lambda e: e.tensor_scalar(out=xc[:nr, :], in0=src[:nr, :], scalar1=st[:nr, 1:2], scalar2=None,
                                               op0=ALU.add), reads=[srckey, kst], writes=[kxc])
        j = self.rr("tmpsq", 2)
        sq = self.xt[j]
        ksq = "xt%d" % j
        kb.op("act", lambda e: e.activation(out=sq[:nr, :], in_=xc[:nr, :], func=AF.Square, accum_out=st[:nr, 2:3]),
              reads=[kxc], writes=[ksq, kst])
        kb.op("act", lambda e: e.activation(out=st[:nr, 3:4], in_=st[:nr, 2:3], func=AF.Sqrt, scale=1.0 / D,
                                            bias=self.epsc[:nr, 0:1]), reads=[kst, "epsc"], writes=[kst])
        kb.op("dve", lambda e: e.reciprocal(out=st[:nr, 4:5], in_=st[:nr, 3:4]), reads=[kst], writes=[kst])
        kb.op("dve", lambda e: e.scalar_tensor_tensor(out=xc[:nr, :], in0=xc[:nr, :], scalar=st[:nr, 4:5],
                                                      in1=self.gbc[:nr, :], op0=ALU.mult, op1=ALU.mult),
              reads=[kxc, kst, "gbc"], writes=[kxc])
        kb.op("dve", lambda e: e.tensor_tensor(out=xc[:nr, :], in0=xc[:nr, :], in1=self.bbc[:nr, :], op=ALU.add),
              reads=[kxc, "bbc"], writes=[kxc])
        kb.dma("pool", xs_out, xc[:nr, :], reads=[kxc], writes=[("xs", ti)])
        if final_out is not None:
            kb.dma("pool", final_out, xc[:nr, :], reads=[kxc], writes=[], is_output=True)
        for half in range(2):
            p = self.rr("ptr", 2)
            pt, kpt = self.ptr[p], "ptr%d" % p
            for k4 in range(4):
                kc = half * 4 + k4
                kb.op("pe", lambda e, kc=kc, k4=k4, pt=pt: e.transpose(out=pt[:, k4 * 128:k4 * 128 + nr],
                                                                      in_=xc[:nr, kc * 128:(kc + 1) * 128],
                                                                      identity=self.ident[:nr, :nr]),
                      reads=[kxc, "ident"], writes=[kpt])
            dst = self.xT[:, half * 4:half * 4 + 4, row0:row0 + nr]
            srcp = pt[:, :].rearrange("p (k t) -> p k t", k=4)[:, :, :nr]
            eng = "act" if half == 0 else "dve"
            if eng == "act":
                kb.op("act", lambda e, dst=dst, srcp=srcp: e.activation(out=dst, in_=srcp, func=AF.Copy),
                      reads=[kpt], writes=[("xT", ti)])
            else:
                kb.op("dve", lambda e, dst=dst, srcp=srcp: e.tensor_copy(out=dst, in_=srcp),
                      reads=[kpt], writes=[("xT", ti)])

    def load_ln_params(self, g_ap, b_ap):
        kb = self.kb
        kb.dma("sp", self.gbc[:], g_ap.partition_broadcast(128), writes=["gbc"])
        kb.dma("sp", self.bbc[:], b_ap.partition_broadcast(128), writes=["bbc"])

    def phase0(self):
        kb = self.kb
        self.epsc = self.sb("epsc", (128, 1))
        kb.op("dve", lambda e: e.memset(self.epsc[:], LN_EPS), writes=["epsc"])
        self.onesf = self.sb("onesf", (128, 64))
        kb.op("dve", lambda e: e.memset(self.onesf[:], 1.0), writes=["onesf"])
        self.begin_phase()
        self.ln_alloc()
        self.load_ln_params(self.I["ln_in_g"], self.I["ln_in_b"])
        for ti, (r0, nr) in enumerate(self.tiles):
            j = self.rr("xt", 2)
            xt, kxt = self.xt[j], "xt%d" % j
            src = self.I["x_prompt"][r0:r0 + nr, :] if r0 < self.T else self.I["x_sample"][:, :]
            kb.dma("sp", xt[:nr, :], src, writes=[kxt])
            self.ln_tile(xt, kxt, r0, nr, ti, self.S["xs"][r0:r0 + nr, :])
        self.end_phase()

    def load_w(self, dram_cols, width):
        kb = self.kb
        i = self.rr("w", 2)
        ws, wb = self.wst[i], self.wbf[i]
        kws, kwb = "wst%d" % i, "wbf%d" % i
        kb.dma("sp", ws[:, :, :width], dram_cols.rearrange("(k p) c -> p k c", p=128), writes=[kws])
        if self.rr("wcast", 2) == 0:
            kb.op("dve", lambda e: e.tensor_copy(out=wb[:, :, :width], in_=ws[:, :, :width]), reads=[kws], writes=[kwb])
        else:
            kb.op("act", lambda e: e.activation(out=wb[:, :, :width], in_=ws[:, :, :width], func=AF.Copy), reads=[kws], writes=[kwb])
        return wb, kwb

    def evac(self, dst, src, reads, writes):
        kb = self.kb
        if self.rr("evac", 2) == 0:
            kb.op("act", lambda e: e.activation(out=dst, in_=src, func=AF.Copy), reads=reads, writes=writes)
        else:
            kb.op("dve", lambda e: e.tensor_copy(out=dst, in_=src), reads=reads, writes=writes)

    def phase1(self, l):
        kb = self.kb
        self.begin_phase()
        self.wst = [self.sbp("wst%d" % i, (128, KC, 512)) for i in range(2)]
        self.wbf = [self.sbp("wbf%d" % i, (128, KC, 512), BF16) for i in range(2)]
        self.ev = [self.sbp("ev%d" % i, (128, 512)) for i in range(4)]
        W = self.I["w_in"][l]
        tm_segs = [(O_OG, O_ZM), (O_RC, O_U)]
        fm_segs = [(O_XM, O_IG), (O_IG, O_FG), (O_FG, O_OG), (O_ZM, O_RC), (O_U, O_ZS), (O_ZS, O_GM), (O_GM, N_IN)]
        for (c0, c1) in tm_segs:
            c = c0
            while c < c1:
                w = min(512, c1 - c)
                wb, kwb = self.load_w(W[:, c:c + w], w)
                for ti, (r0, nr) in enumerate(self.tiles):
                    p = self.rr("pp", 4)
                    pp, kpp = self.pp[p], "pp%d" % p
                    for kc in range(KC):
                        kb.op("pe", lambda e, kc=kc, pp=pp, r0=r0, nr=nr, wb=wb, w=w: e.matmul(
                            pp[:nr, :w], lhsT=self.xT[:, kc, r0:r0 + nr], rhs=wb[:, kc, :w],
                            start=(kc == 0), stop=(kc == KC - 1)),
                            reads=[("xT", ti), kwb], writes=[kpp])
                    v = self.rr("ev", 4)
                    ev, kev = self.ev[v], "ev%d" % v
                    self.evac(ev[:nr, :w], pp[:nr, :w], [kpp], [kev])
                    kb.dma("pool", self.S["z"][r0:r0 + nr, c:c + w], ev[:nr, :w], reads=[kev],
                           writes=[("z", ti)])
                c += w
        for (c0, c1) in fm_segs:
            c = c0
            while c < c1:
                w = min(512, c1 - c)
                wb, kwb = self.load_w(W[:, c:c + w], w)
                for s0 in range(0, w, 128):
                    m = min(128, w - s0)
                    for bi, (t0, n) in enumerate(self.blocks):
                        p = self.rr("pp", 4)
                        pp, kpp = self.pp[p], "pp%d" % p
                        tis = list(range(t0 // 128, (t0 + n + 127) // 128))
                        for kc in range(KC):
                            kb.op("pe", lambda e, kc=kc, pp=pp, t0=t0, n=n, wb=wb, s0=s0, m=m: e.matmul(
                                pp[:m, :n], lhsT=wb[:, kc, s0:s0 + m], rhs=self.xT[:, kc, t0:t0 + n],
                                start=(kc == 0), stop=(kc == KC - 1)),
                                reads=[("xT", t) for t in tis] + [kwb], writes=[kpp])
                        v = self.rr("ev", 4)
                        ev, kev = self.ev[v], "ev%d" % v
                        self.evac(ev[:m, :n], pp[:m, :n], [kpp], [kev])
                        kb.dma("pool", self.S["zT"][c + s0:c + s0 + m, t0:t0 + n], ev[:m, :n], reads=[kev],
                               writes=[("zT", c + s0, bi)])
                c += w
        self.end_phase()

    def easy_states(self, l):
        kb = self.kb
        I, S, O = self.I, self.S, self.O
        T, NS = self.T, self.NS
        nb = len(self.blocks)
        zt_all = [("zT", r, b) for r in range(0, 1024, 128) for b in range(nb)]
        z_all = [("z", ti) for ti in range(len(self.tiles))]
        kb.dma("pool", O["conv_prompt"][l].rearrange("j c -> c j"), S["zT"][0:D, T - 3:T], reads=zt_all, writes=[],
               is_output=True, allow_slow_non_contiguous=True)
        kb.dma("pool", O["shift_prompt"][l:l + 1, :], S["z"][T - 1:T, O_RC:O_RC + R_SHIFT_W], reads=z_all, writes=[], is_output=True)
        if NS:
            kb.dma("pool", O["conv_sample"][l][:, 0:2, :], I["state_mlstm_conv"][l][:, 1:3, :], writes=[], is_output=True)
            for hh in range(4):
                kb.dma("pool", O["conv_sample"][l][:, 2, hh * 256:(hh + 1) * 256].rearrange("b c -> c b"),
                       S["zT"][hh * 256:(hh + 1) * 256, T:T + NS], reads=zt_all, writes=[],
                       is_output=True, allow_slow_non_contiguous=True)
            kb.dma("pool", O["shift_sample"][l], S["z"][T:T + NS, O_RC:O_RC + R_SHIFT_W], reads=z_all, writes=[], is_output=True)

    def mlstm_phase(self, l):
        kb = self.kb
        I, S, O = self.I, self.S, self.O
        T, NS, NT = self.T, self.NS, self.NT
        self.begin_phase()
        sbp = self.sbp
        Wst = sbp("Wst", (128, 4, 2, 256))
        Wq = sbp("Wq", (128, 4, 2, 256), BF16); Wk = sbp("Wk", (128, 4, 2, 256), BF16); Wv = sbp("Wv", (128, 4, 2, 256), BF16)
        cw = sbp("cw", (128, 4, 8)); cb = sbp("cb", (128, 8)); mg = sbp("mg", (128, 8)); msk = sbp("msk", (128, 8))
        gb = sbp("gb", (4, 2))
        igA = sbp("igA", (4, NT)); lfA = sbp("lfA", (4, NT))
        selh = sbp("selh", (4, 4, 128)); causneg = sbp("causneg", (128, 128)); ones4 = sbp("ones4", (4, 128))
        Cst = [[sbp("C%d_%d" % (s_, h), (128, 2, 257)) for h in range(4)] for s_ in range(3)]

        def mk_set(tag, Lm):
            W = {"tag": tag}
            W["mprev"] = sbp(tag + "mprev", (4, 2))
            W["xext"] = [sbp(tag + "xext%d" % i, (128, 8, Lm + 3)) for i in range(2)]
            W["ctmp"] = sbp(tag + "cvtmp", (128, 8, Lm)); W["cacc"] = sbp(tag + "cvacc", (128, 8, Lm))
            W["xcT"] = sbp(tag + "xcT", (128, 8, Lm)); W["xcb"] = sbp(tag + "xcb", (128, 8, Lm), BF16); W["xmb"] = sbp(tag + "xmb", (128, 8, Lm), BF16)
            W["qTb"] = sbp(tag + "qTb", (128, 4, 2, Lm)); W["kTb"] = sbp(tag + "kTb", (128, 4, 2, Lm))
            W["kw"] = sbp(tag + "kw", (128, 256)); W["vaug"] = sbp(tag + "vaug", (128, 257))
            W["G"] = sbp(tag + "G", (4, 8, Lm)); W["gsm"] = sbp(tag + "gsm", (4, 8)); W["dg"] = sbp(tag + "dg", (4, 4))
            W["gc"] = sbp(tag + "gc", (128, 16)); W["dbc"] = sbp(tag + "dbc", (128, 4))
            W["DT"] = sbp(tag + "DT", (128, Lm)); W["Stl"] = sbp(tag + "Stl", (128, Lm)); W["mmb"] = sbp(tag + "mmb", (128, 4, Lm))
            W["Asb"] = sbp(tag + "Asb", (128, 257)); W["nd"] = sbp(tag + "nd", (128, 257)); W["dsm"] = sbp(tag + "dsm", (128, 4))
            W["sog"] = [sbp(tag + "sog%d" % i, (128, D)) for i in range(2 if Lm > 1 else 1)]
            W["hm"] = sbp(tag + "hm", (128, D)); W["hst"] = sbp(tag + "hst", (128, 16))
            W["hT"] = sbp(tag + "hT", (128, 8, Lm)); W["zmt"] = [sbp(tag + "zmt%d" % i, (128, 8, Lm)) for i in range(2)]
            W["aob"] = [sbp(tag + "aob%d" % i, (128, 8, Lm), BF16) for i in range(2)]
            return W
        WP = mk_set("P", 128)
        WS_ = mk_set("S", 1) if NS else None
        kb.local = frozenset(["mprev", "mnew", "xext0", "xext1", "cvacc", "cvtmp", "xcT", "xcb", "xmb", "G0", "G1", "G3", "G4", "G5", "G6", "gsm", "gc", "dg",
                              "dbc", "mmb", "DT", "Stl", "kw", "vaug", "Asb", "nd", "dsm", "sog0", "sog1", "hst", "zmt0", "zmt1", "aob0", "aob1",
                              "hm", "hT", "qTb", "kTb"])
        cvst = sbp("cvst", (48, D)); convT = sbp("convT", (128, 8, 48))

        for (nm, dst, sc_) in (("m_wq", Wq, 1.0 / 16.0), ("m_wk", Wk, 1.0), ("m_wv", Wv, 1.0)):
            kb.dma("sp", Wst[:], I[nm][l].rearrange("h (c p) e -> p h c e", p=128), writes=["Wst"])
            kb.op("act", lambda e, dst=dst, sc_=sc_: e.activation(out=dst[:], in_=Wst[:], func=AF.Copy, scale=sc_),
                  reads=["Wst"], writes=[nm])
        sl = dict(allow_slow_non_contiguous=True)
        kb.dma("sp", cw[:], I["m_conv_w"][l].rearrange("j (k p) -> p j k", p=128), writes=["cw"], **sl)
        kb.dma("sp", cb[:], I["m_conv_b"][l].rearrange("(k p) -> p k", p=128), writes=["cb"], **sl)
        kb.dma("sp", mg[:], I["m_norm_g"][l].rearrange("(k p) -> p k", p=128), writes=["mg"], **sl)
        kb.dma("sp", msk[:], I["m_skip"][l].rearrange("(k p) -> p k", p=128), writes=["msk"], **sl)
        kb.dma("sp", gb[:, 0:1], I["m_ig_b"][l].rearrange("(h o) -> h o", o=1), writes=["gb"], **sl)
        kb.dma("sp", gb[:, 1:2], I["m_fg_b"][l].rearrange("(h o) -> h o", o=1), writes=["gb"], **sl)
        kb.dma("sp", selh[:], I["selh"], writes=["selh"])
        kb.dma("sp", causneg[:], I["causneg"], writes=["causneg"])
        kb.dma("sp", ones4[:], I["ones4"], writes=["ones4"])
        nb = len(self.blocks)
        kb.dma("sp", igA[:], S["zT"][O_IG:O_IG + 4, :], reads=[("zT", O_IG, b) for b in range(nb)], writes=["igA"])
        kb.dma("sp", lfA[:], S["zT"][O_FG:O_FG + 4, :], reads=[("zT", O_FG, b) for b in range(nb)], writes=["lfA"])
        kb.op("dve", lambda e: e.tensor_scalar(out=igA[:], in0=igA[:], scalar1=gb[:, 0:1], scalar2=None, op0=ALU.add),
              reads=["igA", "gb"], writes=["igA"])
        kb.op("dve", lambda e: e.tensor_scalar(out=lfA[:], in0=lfA[:], scalar1=gb[:, 1:2], scalar2=-1.0, op0=ALU.add, op1=ALU.mult),
              reads=["lfA", "gb"], writes=["lfA"])
        kb.op("act", lambda e: e.activation(out=lfA[:], in_=lfA[:], func=AF.Exp), reads=["lfA"], writes=["lfA"])
        kb.op("dve", lambda e: e.tensor_scalar(out=lfA[:], in0=lfA[:], scalar1=1.0, scalar2=None, op0=ALU.add), reads=["lfA"], writes=["lfA"])
        kb.op("act", lambda e: e.activation(out=lfA[:], in_=lfA[:], func=AF.Ln), reads=["lfA"], writes=["lfA"])
        kb.op("dve", lambda e: e.tensor_scalar(out=lfA[:], in0=lfA[:], scalar1=-1.0, scalar2=None, op0=ALU.mult), reads=["lfA"], writes=["lfA"])
        if NS:
            kb.dma("sp", cvst[:3 * NS, :], I["state_mlstm_conv"][l].rearrange("b j c -> (b j) c"), writes=["cvst"])
            for half in range(2):
                p = self.rr("ptr", 2)
                pt, kpt = self.ptr[p], "ptr%d" % p
                for k4 in range(4):
                    kc = half * 4 + k4
                    kb.op("pe", lambda e, pt=pt, k4=k4, kc=kc: e.transpose(out=pt[:, k4 * 48:k4 * 48 + 3 * NS], in_=cvst[:3 * NS, kc * 128:(kc + 1) * 128],
                                                                       identity=self.ident[:3 * NS, :3 * NS]),
                          reads=["cvst", "ident"], writes=[kpt])
                kb.op("dve", lambda e, pt=pt, half=half: e.tensor_copy(out=convT[:, half * 4:half * 4 + 4, :3 * NS],
                                                                     in_=pt[:, 0:192].rearrange("p (k c) -> p k c", k=4)[:, :, :3 * NS]),
                      reads=[kpt], writes=["convT"])

        zt_x = lambda bi: [("zT", r, bi) for r in range(0, 1024, 128)]
        zt_z = lambda bi: [("zT", O_ZM + r, bi) for r in range(0, 1024, 128)]
        blk_of = lambda t: next(i for i, (b0, bn) in enumerate(self.blocks) if b0 <= t < b0 + bn)

        chunks = [(c * 128, 128, None) for c in range(T // 128)] + [(T + b, 1, b) for b in range(NS)]
        def chunk_gen(ci, t0, L, sb_, W):
            mprev = W["mprev"]; xext = W["xext"]; ctmp = W["ctmp"]; cacc = W["cacc"]; xcT = W["xcT"]; xcb = W["xcb"]; xmb = W["xmb"]
            qTb = W["qTb"]; kTb = W["kTb"]; kw = W["kw"]; vaug = W["vaug"]; G = W["G"]; gsm = W["gsm"]; dg = W["dg"]; gc = W["gc"]; dbc = W["dbc"]
            DT = W["DT"]; Stl = W["Stl"]; mmb = W["mmb"]; Asb = W["Asb"]; nd = W["nd"]; dsm = W["dsm"]; sog = W["sog"]; hm = W["hm"]; hst = W["hst"]
            hT = W["hT"]; zmt = W["zmt"]; aob = W["aob"]
            bi = blk_of(t0)
            ti = t0 // 128
            cs = (1 + self.rr("Cset", 2)) if sb_ is not None else 0
            C = Cst[cs]
            kC = [("C", cs, h) for h in range(4)]
            if ci == 0:
                for h in range(4):
                    kb.op("pool", lambda e, h=h: e.memset(C[h][:], 0.0), writes=[kC[h]])
                kb.op("dve", lambda e: e.memset(mprev[:, 0:1], NEG), writes=["mprev"])
            if sb_ is not None:
                for h in range(4):
                    kb.dma("sp", C[h][:, :, 0:256], I["state_mlstm_c"][l, sb_, h].rearrange("(c p) v -> p c v", p=128), writes=[kC[h]])
                    kb.dma("sp", C[h][:, :, 256:257], I["state_mlstm_n"][l, sb_, h].rearrange("(c p o) -> p c o", p=128, o=1), writes=[kC[h]], **sl)
                kb.dma("sp", mprev[:, 0:1], I["state_mlstm_m"][l, sb_].rearrange("(h o) -> h o", o=1), writes=["mprev"], **sl)
            xj = self.rr("xext" + W["tag"], 2)
            xe, kxe = xext[xj], "xext%d" % xj
            if sb_ is None:
                if t0 == 0:
                    kb.op("pool", lambda e, xe=xe: e.memset(xe[:, :, 0:3], 0.0), writes=[kxe])
                    kb.dma("sp", xe[:, :, 3:3 + L], S["zT"][0:D, t0:t0 + L].rearrange("(k p) t -> p k t", p=128), reads=zt_x(bi), writes=[kxe])
                else:
                    rd = zt_x(bi) + (zt_x(blk_of(t0 - 3)) if blk_of(t0 - 3) != bi else [])
                    kb.dma("sp", xe[:, :, 0:3 + L], S["zT"][0:D, t0 - 3:t0 + L].rearrange("(k p) t -> p k t", p=128), reads=rd, writes=[kxe])
            else:
                kb.op("pool", lambda e, xe=xe, sb_=sb_: e.tensor_copy(out=xe[:, :, 0:3], in_=convT[:, :, 3 * sb_:3 * sb_ + 3]), reads=["convT"], writes=[kxe])
                kb.dma("sp", xe[:, :, 3:4], S["zT"][0:D, t0:t0 + 1].rearrange("(k p) t -> p k t", p=128), reads=zt_x(bi), writes=[kxe], **sl)
            V = lambda x: x[:, :, :L]
            for j in range(4):
                dst = cacc if j == 0 else ctmp
                kd = "cvacc" if j == 0 else "cvtmp"
                kb.op("dve", lambda e, j=j, dst=dst, xe=xe: e.tensor_tensor(out=V(dst), in0=xe[:, :, j:j + L],
                                                                         in1=cw[:, j, :, None].to_broadcast([128, 8, L]), op=ALU.mult),
                      reads=[kxe, "cw"], writes=[kd])
                if j > 0:
                    kb.op("dve", lambda e: e.tensor_tensor(out=V(cacc), in0=V(cacc), in1=V(ctmp), op=ALU.add), reads=["cvacc", "cvtmp"], writes=["cvacc"])
            kb.op("dve", lambda e: e.tensor_tensor(out=V(cacc), in0=V(cacc), in1=cb[:, :, None].to_broadcast([128, 8, L]), op=ALU.add),
                  reads=["cvacc", "cb"], writes=["cvacc"])
            kb.op("act", lambda e: e.activation(out=V(xcT), in_=V(cacc), func=AF.Silu), reads=["cvacc"], writes=["xcT"])
            kb.op("dve", lambda e: e.tensor_copy(out=V(xcb), in_=V(xcT)), reads=["xcT"], writes=["xcb"])
            kb.op("dve", lambda e, xe=xe: e.tensor_copy(out=V(xmb), in_=xe[:, :, 3:3 + L]), reads=[kxe], writes=["xmb"])
            yield
            Gr = lambda i: G[:, i, :L]
            kb.op("dve", lambda e: e.tensor_tensor_scan(out=Gr(0), data0=ones4[:, :L], data1=lfA[:, t0:t0 + L], initial=0.0, op0=ALU.mult, op1=ALU.add),
                  reads=["ones4", "lfA"], writes=["G0"])
            kb.op("dve", lambda e: e.tensor_tensor(out=Gr(3), in0=igA[:, t0:t0 + L], in1=Gr(0), op=ALU.subtract), reads=["igA", "G0"], writes=["G3"])
            kb.op("dve", lambda e: e.tensor_tensor_scan(out=Gr(1), data0=Gr(3), data1=Gr(3), initial=-3.0e38, op0=ALU.max, op1=ALU.max),
                  reads=["G3"], writes=["G1"])
            kb.op("dve", lambda e: e.tensor_scalar(out=Gr(1), in0=Gr(1), scalar1=mprev[:, 0:1], scalar2=None, op0=ALU.max), reads=["G1", "mprev"], writes=["G1"])
            kb.op("dve", lambda e: e.tensor_scalar(out=gsm[:, 0:1], in0=G[:, 1, L - 1:L], scalar1=-1.0, scalar2=None, op0=ALU.mult), reads=["G1"], writes=["gsm"])
            kb.op("act", lambda e: e.activation(out=Gr(4), in_=Gr(1), func=AF.Exp, scale=-1.0, bias=mprev[:, 0:1]), reads=["G1", "mprev"], writes=["G4"])
            kb.op("dve", lambda e: e.tensor_tensor(out=Gr(5), in0=Gr(0), in1=Gr(1), op=ALU.add), reads=["G0", "G1"], writes=["G5"])
            kb.op("act", lambda e: e.activation(out=Gr(5), in_=Gr(5), func=AF.Exp, scale=-1.0), reads=["G5"], writes=["G5"])
            kb.op("act", lambda e: e.activation(out=Gr(6), in_=Gr(3), func=AF.Exp, bias=gsm[:, 0:1]), reads=["G3", "gsm"], writes=["G6"])
            kb.op("dve", lambda e: e.tensor_tensor(out=mprev[:, 1:2], in0=G[:, 0, L - 1:L], in1=G[:, 1, L - 1:L], op=ALU.add), reads=["G0", "G1"], writes=["mnew"])
            p = self.rr("ptr", 2)
            pt, kpt = self.ptr[p], "ptr%d" % p
            for ki, gi in enumerate((4, 5, 3, 6)):
                kb.op("pe", lambda e, ki=ki, gi=gi, pt=pt: e.transpose(out=pt[:L, ki * 4:ki * 4 + 4], in_=G[:, gi, :L], identity=self.ident[:4, :4]),
                      reads=["G%d" % gi, "ident"], writes=[kpt])
            kb.op("dve", lambda e, pt=pt: e.tensor_copy(out=gc[:L, :], in_=pt[:L, 0:16]), reads=[kpt], writes=["gc"])
            kb.op("dve", lambda e: e.tensor_scalar(out=dg[:, :], in0=self.ident[:4, :4], scalar1=G[:, 4, L - 1:L], scalar2=None, op0=ALU.mult),
                  reads=["G4", "ident"], writes=["dg"])
            p2 = self.rr("ptr", 2)
            pt2, kpt2 = self.ptr[p2], "ptr%d" % p2
            kb.op("pe", lambda e, pt2=pt2: e.matmul(pt2[:, 0:4], lhsT=ones4[:, :], rhs=dg[:, :], start=True, stop=True), reads=["ones4", "dg"], writes=[kpt2])
            kb.op("dve", lambda e, pt2=pt2: e.tensor_copy(out=dbc[:, :], in_=pt2[:, 0:4]), reads=[kpt2], writes=["dbc"])
            pnn = self.rr("pp", 4)
            pn, kpn = self.pp[pnn], "pp%d" % pnn
            for h in range(4):
                kb.op("pe", lambda e, h=h, pn=pn: e.matmul(pn[:L, h * 128:h * 128 + L], lhsT=selh[:, h, :L], rhs=G[:, 1, :L], start=True, stop=True),
                      reads=["selh", "G1"], writes=[kpn])
            kb.op("act", lambda e, pn=pn: e.activation(out=mmb[:L, :, :L], in_=pn[:L, :].rearrange("p (h t) -> p h t", h=4)[:, :, :L], func=AF.Copy),
                  reads=[kpn], writes=["mmb"])
            sj = self.rr("sog" + W["tag"], len(sog))
            so, kso = sog[sj], "sog%d" % sj
            kb.dma("sp", so[:L, :], S["z"][t0:t0 + L, O_OG:O_OG + D], reads=[("z", ti)], writes=[kso])
            kb.op("act", lambda e, so=so: e.activation(out=so[:L, :], in_=so[:L, :], func=AF.Sigmoid), reads=[kso], writes=[kso])
            yield
            for h in range(4):
                for (Wt, wn, dstT, kd) in ((Wq, "m_wq", qTb, "qTb"), (Wk, "m_wk", kTb, "kTb")):
                    pq = self.rr("pp", 4)
                    ppq, kpq = self.pp[pq], "pp%d" % pq
                    for ec in range(2):
                        for dc in range(2):
                            kb.op("pe", lambda e, h=h, ec=ec, dc=dc, Wt=Wt, ppq=ppq: e.matmul(
                                ppq[:, ec * 128:ec * 128 + L], lhsT=Wt[:, h, dc, ec * 128:(ec + 1) * 128], rhs=xcb[:, 2 * h + dc, :L],
                                start=(dc == 0), stop=(dc == 1)), reads=[wn, "xcb"], writes=[kpq])
                    self.evac(dstT[:, h, :, :L], ppq[:, 0:256].rearrange("p (c t) -> p c t", c=2)[:, :, :L], [kpq], [(kd, h)])
            for h in range(4):
                yield
                pk = self.rr("pp", 4)
                ppk, kpk = self.pp[pk], "pp%d" % pk
                for dc in range(2):
                    kb.op("pe", lambda e, h=h, dc=dc, ppk=ppk: e.matmul(ppk[:L, 0:256], lhsT=xcb[:, 2 * h + dc, :L], rhs=Wk[:, h, dc, :],
                                                                      start=(dc == 0), stop=(dc == 1)), reads=["m_wk", "xcb"], writes=[kpk])
                kb.op("act", lambda e, h=h, ppk=ppk: e.activation(out=kw[:L, :], in_=ppk[:L, 0:256], func=AF.Copy, scale=gc[:L, 12 + h:13 + h]),
                      reads=[kpk, "gc"], writes=["kw"])
                pv = self.rr("pp", 4)
                ppv, kpv = self.pp[pv], "pp%d" % pv
                for dc in range(2):
                    kb.op("pe", lambda e, h=h, dc=dc, ppv=ppv: e.matmul(ppv[:L, 0:256], lhsT=xmb[:, 2 * h + dc, :L], rhs=Wv[:, h, dc, :],
                                                                      start=(dc == 0), stop=(dc == 1)), reads=["m_wv", "xmb"], writes=[kpv])
                kb.op("dve", lambda e, ppv=ppv: e.tensor_copy(out=vaug[:L, 0:256], in_=ppv[:L, 0:256]), reads=[kpv], writes=["vaug"])
                kb.op("dve", lambda e: e.memset(vaug[:L, 256:257], 1.0), writes=["vaug"])
                kb.op("dve", lambda e, h=h: e.tensor_tensor(out=DT[:L, :L], in0=mmb[:L, h, :L], in1=causneg[:L, :L], op=ALU.add),
                      reads=["mmb", "causneg"], writes=["DT"])
                kb.op("act", lambda e, h=h: e.activation(out=DT[:L, :L], in_=DT[:L, :L], func=AF.Exp, scale=-1.0, bias=gc[:L, 8 + h:9 + h]),
                      reads=["DT", "gc"], writes=["DT"])
                ps_ = self.rr("pp", 4)
                pps, kps = self.pp[ps_], "pp%d" % ps_
                for ec in range(2):
                    kb.op("pe", lambda e, h=h, ec=ec, pps=pps: e.matmul(pps[:L, :L], lhsT=kTb[:, h, ec, :L], rhs=qTb[:, h, ec, :L],
                                                                      start=(ec == 0), stop=(ec == 1)), reads=[("kTb", h), ("qTb", h)], writes=[kps])
                kb.op("dve", lambda e, pps=pps: e.tensor_tensor(out=Stl[:L, :L], in0=pps[:L, :L], in1=DT[:L, :L], op=ALU.mult), reads=[kps, "DT"], writes=["Stl"])
                pa = self.rr("pp", 4)
                ppa, kpa = self.pp[pa], "pp%d" % pa
                kb.op("pe", lambda e, ppa=ppa: e.matmul(ppa[:L, 0:257], lhsT=Stl[:L, :L], rhs=vaug[:L, :], start=True, stop=True), reads=["Stl", "vaug"], writes=[kpa])
                pb = self.rr("ptr", 2)
                ppb, kpb = self.ptr[pb], "ptr%d" % pb
                for ec in range(2):
                    kb.op("pe", lambda e, h=h, ec=ec, ppb=ppb, C=C: e.matmul(ppb[:L, 0:257], lhsT=qTb[:, h, ec, :L], rhs=C[h][:, ec, :],
                                                                      start=(ec == 0), stop=(ec == 1)), reads=[("qTb", h), kC[h]], writes=[kpb])
                kb.op("act", lambda e, ppa=ppa: e.activation(out=Asb[:L, :], in_=ppa[:L, 0:257], func=AF.Copy), reads=[kpa], writes=["Asb"])
                kb.op("dve", lambda e, h=h, ppb=ppb: e.scalar_tensor_tensor(out=nd[:L, :], in0=ppb[:L, 0:257], scalar=gc[:L, h:h + 1], in1=Asb[:L, :],
                                                                        op0=ALU.mult, op1=ALU.add), reads=[kpb, "gc", "Asb"], writes=["nd"])
                kb.op("act", lambda e: e.activation(out=dsm[:L, 2:3], in_=nd[:L, 256:257], func=AF.Abs), reads=["nd"], writes=["dsm"])
                kb.op("dve", lambda e, h=h: e.tensor_scalar(out=dsm[:L, 0:1], in0=dsm[:L, 2:3], scalar1=gc[:L, 4 + h:5 + h], scalar2=None,
                                                            op0=ALU.max), reads=["dsm", "gc"], writes=["dsm"])
                kb.op("dve", lambda e: e.reciprocal(out=dsm[:L, 1:2], in_=dsm[:L, 0:1]), reads=["dsm"], writes=["dsm"])
                kb.op("dve", lambda e, h=h, so=so: e.scalar_tensor_tensor(out=hm[:L, h * 256:(h + 1) * 256], in0=nd[:L, 0:256], scalar=dsm[:L, 1:2],
                                                                      in1=so[:L, h * 256:(h + 1) * 256], op0=ALU.mult, op1=ALU.mult),
                      reads=["nd", "dsm", kso], writes=[("hm", h)])
                for ec in range(2):
                    pu = self.rr("pp", 4)
                    ppu, kpu = self.pp[pu], "pp%d" % pu
                    kb.op("pe", lambda e, ec=ec, ppu=ppu: e.matmul(ppu[:, 0:257], lhsT=kw[:L, ec * 128:(ec + 1) * 128], rhs=vaug[:L, :], start=True, stop=True),
                          reads=["kw", "vaug"], writes=[kpu])
                    kb.op("dve", lambda e, h=h, ec=ec, ppu=ppu, C=C: e.scalar_tensor_tensor(out=C[h][:, ec, :], in0=C[h][:, ec, :], scalar=dbc[:, h:h + 1],
                                                                                      in1=ppu[:, 0:257], op0=ALU.mult, op1=ALU.add),
                          reads=[kC[h], "dbc", kpu], writes=[kC[h]])
            kb.op("dve", lambda e: e.tensor_copy(out=mprev[:, 0:1], in_=mprev[:, 1:2]), reads=["mnew"], writes=["mprev"])
            yield
            hmk = [("hm", h) for h in range(4)]
            hv = hm[:L, :].rearrange("t (h d) -> t h d", h=4)
            kb.op("dve", lambda e: e.tensor_reduce(out=hst[:L, 0:4], in_=hv, axis=AX.X, op=ALU.add), reads=hmk, writes=["hst"])
            kb.op("dve", lambda e: e.tensor_scalar(out=hst[:L, 0:4], in0=hst[:L, 0:4], scalar1=-1.0 / 256.0, scalar2=None, op0=ALU.mult), reads=["hst"], writes=["hst"])
            kb.op("dve", lambda e: e.tensor_tensor(out=hv, in0=hv, in1=hst[:L, 0:4].unsqueeze(2).to_broadcast([L, 4, 256]), op=ALU.add),
                  reads=hmk + ["hst"], writes=hmk)
            kb.op("dve", lambda e, so=so: e.tensor_tensor(out=so[:L, :], in0=hm[:L, :], in1=hm[:L, :], op=ALU.mult), reads=hmk, writes=[kso])
            kb.op("dve", lambda e, so=so: e.tensor_reduce(out=hst[:L, 4:8], in_=so[:L, :].rearrange("t (h d) -> t h d", h=4), axis=AX.X, op=ALU.add),
                  reads=[kso], writes=["hst"])
            kb.op("act", lambda e: e.activation(out=hst[:L, 8:12], in_=hst[:L, 4:8], func=AF.Sqrt, scale=1.0 / 256.0, bias=self.epsc[:L, 0:1]),
                  reads=["hst", "epsc"], writes=["hst"])
            kb.op("dve", lambda e: e.reciprocal(out=hst[:L, 12:16], in_=hst[:L, 8:12]), reads=["hst"], writes=["hst"])
            kb.op("dve", lambda e: e.tensor_tensor(out=hv, in0=hv, in1=hst[:L, 12:16].unsqueeze(2).to_broadcast([L, 4, 256]), op=ALU.mult),
                  reads=hmk + ["hst"], writes=hmk)
            zj = self.rr("zmt" + W["tag"], 2)
            zm_, kzm = zmt[zj], "zmt%d" % zj
            kb.dma("sp", zm_[:, :, :L], S["zT"][O_ZM:O_ZM + D, t0:t0 + L].rearrange("(k p) t -> p k t", p=128), reads=zt_z(bi), writes=[kzm],
                   **(sl if L == 1 else {}))
            kb.op("act", lambda e, zm_=zm_: e.activation(out=zm_[:, :, :L], in_=zm_[:, :, :L], func=AF.Silu), reads=[kzm], writes=[kzm])
            for half in range(2):
                p = self.rr("ptr", 2)
                pt, kpt = self.ptr[p], "ptr%d" % p
                for k4 in range(4):
                    kc = half * 4 + k4
                    kb.op("pe", lambda e, k4=k4, kc=kc, pt=pt: e.transpose(out=pt[:, k4 * 128:k4 * 128 + L], in_=hm[:L, kc * 128:(kc + 1) * 128],
                                                                       identity=self.ident[:L, :L]), reads=hmk + ["ident"], writes=[kpt])
                kb.op("dve", lambda e, half=half, pt=pt: e.tensor_tensor(out=hT[:, half * 4:half * 4 + 4, :L],
                                                                       in0=pt[:, :].rearrange("p (k t) -> p k t", k=4)[:, :, :L],
                                                                       in1=mg[:, half * 4:half * 4 + 4, None].to_broadcast([128, 4, L]), op=ALU.mult),
                      reads=[kpt, "mg"], writes=[("hT", half)])
            kb.op("dve", lambda e: e.tensor_tensor(out=V(ctmp), in0=V(xcT), in1=msk[:, :, None].to_broadcast([128, 8, L]), op=ALU.mult),
                  reads=["xcT", "msk"], writes=["cvtmp"])
            kb.op("dve", lambda e: e.tensor_tensor(out=V(hT), in0=V(hT), in1=V(ctmp), op=ALU.add), reads=["cvtmp", ("hT", 0), ("hT", 1)], writes=[("hT", 0), ("hT", 1)])
            aj = self.rr("aob" + W["tag"], 2)
            kb.op("dve", lambda e, aj=aj, zm_=zm_: e.tensor_tensor(out=aob[aj][:, :, :L], in0=V(hT), in1=zm_[:, :, :L], op=ALU.mult),
                  reads=[("hT", 0), ("hT", 1), kzm], writes=["aob%d" % aj])
            kb.dma("pool", S["act_m"][:, t0:t0 + L].rearrange("(k p) t -> p k t", p=128), aob[aj][:, :, :L], reads=["aob%d" % aj],
                   writes=[("act", "m", bi)], **(sl if L == 1 else {}))
            if sb_ is not None or ci == T // 128 - 1:
                if sb_ is None:
                    oc_, on_, om_ = O["c_prompt"][l], O["n_prompt"][l], O["m_prompt"][l]
                else:
                    oc_, on_, om_ = O["c_sample"][l, sb_], O["n_sample"][l, sb_], O["m_sample"][l, sb_]
                for h in range(4):
                    kb.dma("pool", oc_[h].rearrange("(c p) v -> p c v", p=128), C[h][:, :, 0:256], reads=[kC[h]], writes=[], is_output=True)
                    kb.dma("pool", on_[h].rearrange("(c p o) -> p c o", p=128, o=1), C[h][:, :, 256:257], reads=[kC[h]], writes=[], is_output=True, **sl)
                kb.dma("pool", om_.rearrange("(h o) -> h o", o=1), mprev[:, 0:1], reads=["mprev"], writes=[], is_output=True, **sl)
        pq_ = [(ci, t0, L, sb_) for ci, (t0, L, sb_) in enumerate(chunks) if sb_ is None]
        sq_ = [(ci, t0, L, sb_) for ci, (t0, L, sb_) in enumerate(chunks) if sb_ is not None]
        streams = [[pq_, WP, None], [sq_, WS_, None]]
        while any(st[0] or st[2] is not None for st in streams):
            for st in streams:
                if st[2] is None and st[0]:
                    args = st[0].pop(0)
                    st[2] = chunk_gen(*args, st[1])
                if st[2] is not None:
                    kb.ns = st[1]["tag"]
                    try:
                        next(st[2])
                    except StopIteration:
                        st[2] = None
                    kb.ns = None
        self.end_phase()

    def rwkv_phase(self, l):
        kb = self.kb
        I, S, O = self.I, self.S, self.O
        T, NS, NT = self.T, self.NS, self.NT
        self.begin_phase()
        sbp = self.sbp
        sl = dict(allow_slow_non_contiguous=True)
        NPI = self.NPI
        EW = -math.exp(-0.5)
        P_ = {}
        for nm in ("r_k_k", "r_k_a", "r_ln_g", "r_ln_b"):
            P_[nm] = sbp("bc_" + nm, (128, D))
            kb.dma("sp", P_[nm][:], I[nm][l].partition_broadcast(128), writes=[nm])
        P_["r_r_k"] = sbp("bc_rk", (128, D))
        kb.dma("sp", P_["r_r_k"][:], I["r_r_k"][l].rearrange("h j -> (h j)").partition_broadcast(128), writes=["r_r_k"])
        mu = sbp("bc_mu", (128, R_SHIFT_W))
        kb.dma("sp", mu[:], I["r_mu"][l].partition_broadcast(128), writes=["mu"])
        w2 = sbp("w2", (65, 2, D))
        kb.dma("sp", w2[0:64, 0, :], I["r_w2"][l], writes=["w2"])
        kb.dma("sp", w2[0:64, 1, :], I["r_a2"][l], writes=["w2"])
        kb.dma("sp", w2[64:65, 0, :], I["r_w0"][l:l + 1, :], writes=["w2"])
        kb.dma("sp", w2[64:65, 1, :], I["r_a0"][l:l + 1, :], writes=["w2"])
        mks = sbp("mks", (128, 384))
        kb.dma("sp", mks[:], I["rmasks"], writes=["mks"])
        e12 = sbp("e12", (128, 1))
        kb.op("dve", lambda e: e.memset(e12[:], RWKV_GN_EPS), writes=["e12"])
        xr = sbp("xr", (128, R_SHIFT_W)); rp = sbp("rp", (128, R_SHIFT_W))
        lt = sbp("lt", (65, 2, 128))
        kb.op("pool", lambda e: e.memset(lt[64:65, :, :], 1.0), writes=["lt1"])
        tv = {n: sbp("tv_" + n, (128, D)) for n in ("lw", "a", "an", "b", "k")}
        ssq = sbp("ssq", (128, 64))
        fmall = sbp("fmall", (128, 8, 6, 128))
        fm = {n: fmall[:, :, i_, :] for i_, n in enumerate(("at", "rt", "bh", "kh", "bc", "kc"))}
        fm["cum"] = sbp("fm_cum", (128, 8, 128)); fm["et"] = sbp("fm_et", (128, 8, 128))
        wc = sbp("wc", (128, 8, 16))
        ST = sbp("STt", (128, 8, 64))
        Vp = sbp("Vp", (128, 8, 64)); Ych = sbp("Ych", (128, 8, 64))
        nat = sbp("rnat", (128, 8, 64)); nato = sbp("rnato", (128, 8, 64))
        CDT = BF16 if self.CORE_BF16 else F32
        NG = 2
        G_ = []
        BK = sbp("g_BK", (128, 4, 2, 128), CDT)
        for gi in range(NG):
            d = {}
            d["UBD"] = sbp("g%d_UBD" % gi, (128, 4, 4, 128), CDT)
            d["Btm"] = sbp("g%d_Btm" % gi, (128, 4, 128), CDT); d["Ktm"] = sbp("g%d_Ktm" % gi, (128, 4, 128), CDT)
            d["MNb"] = sbp("g%d_MNb" % gi, (128, 4, 256), CDT); d["MNk"] = sbp("g%d_MNk" % gi, (128, 4, 256), CDT)
            d["X"] = sbp("g%d_X" % gi, (128, 4, 128), CDT); d["Xt"] = sbp("g%d_Xt" % gi, (128, 4, 128), CDT); d["P"] = sbp("g%d_P" % gi, (128, 4, 128), CDT)
            d["RHS"] = sbp("g%d_RHS" % gi, (128, 4, 64), CDT); d["U"] = sbp("g%d_U" % gi, (128, 4, 64), CDT)
            G_.append(d)
        self.bank8 = list(self.pp) + list(self.ptr) + list(self.pex)
        self.bank8k = ["pp%d" % i for i in range(4)] + ["ptr%d" % i for i in range(2)] + ["pex%d" % i for i in range(2)]
        if self.CORE_BF16:
            STb = sbp("STb", (128, 8, 64), BF16); Vpb = sbp("Vpb", (128, 8, 64), BF16); identc = sbp("identc", (128, 128), BF16)
            kb.op("pool", lambda e: e.tensor_copy(out=identc[:], in_=self.ident[:]), reads=["ident"], writes=["identc"])
            kb.op("pool", lambda e: e.memset(STb[:], 0.0), writes=[("STb", h) for h in range(8)])
        else:
            STb, Vpb, identc = ST, Vp, self.ident
        kSTb = (lambda hh: ("STb", hh)) if self.CORE_BF16 else (lambda hh: ("ST", hh))
        kVpb = (lambda hh: ("Vpb", hh)) if self.CORE_BF16 else (lambda hh: ("Vp", hh))

        def vp_ready():
            if self.CORE_BF16:
                kb.op("pool", lambda e: e.tensor_copy(out=Vpb[:], in_=Vp[:]), reads=[("Vp", h) for h in range(8)], writes=[("Vpb", h) for h in range(8)])
        ytm = rp[:, D:2 * D]; zrt = rp[:, 0:D]; yst = sbp("yst", (128, 64))
        aob = [sbp("raob%d" % i, (128, 8, 128), BF16) for i in range(2)]

        def zero_units():
            for gi in range(NG):
                kb.op("pool", lambda e, gi=gi: e.memset(G_[gi]["UBD"][:], 0.0), writes=[("g", gi, "UBD")])
            kb.op("pool", lambda e: e.memset(BK[:], 0.0), writes=["BK"])
        zero_units()
        kb.op("pool", lambda e: e.memset(ST[:], 0.0), writes=[("ST", h) for h in range(8)])

        z_all = lambda ti: [("z", ti)]

        def prep(t0, L, ti, is_s):
            kb.dma("sp", xr[:L, :], S["z"][t0:t0 + L, O_RC:O_RC + R_SHIFT_W], reads=z_all(ti), writes=["xr"])
            if is_s:
                kb.dma("sp", rp[:L, :], I["state_rwkv_shift"][l], writes=["rp"])
            elif t0 == 0:
                kb.op("pool", lambda e: e.memset(rp[0:1, :], 0.0), writes=["rp"])
                kb.dma("sp", rp[1:L, :], S["z"][0:L - 1, O_RC:O_RC + R_SHIFT_W], reads=z_all(ti), writes=["rp"])
            else:
                kb.dma("sp", rp[:L, :], S["z"][t0 - 1:t0 + L - 1, O_RC:O_RC + R_SHIFT_W], reads=z_all(ti) + z_all(ti - 1), writes=["rp"])
            kb.op("dve", lambda e: e.tensor_tensor(out=rp[:L, :], in0=rp[:L, :], in1=xr[:L, :], op=ALU.subtract), reads=["rp", "xr"], writes=["rp"])
            kb.op("dve", lambda e: e.tensor_tensor(out=rp[:L, :], in0=rp[:L, :], in1=mu[:L, :], op=ALU.mult), reads=["rp", "mu"], writes=["rp"])
            kb.op("dve", lambda e: e.tensor_tensor(out=xr[:L, :], in0=xr[:L, :], in1=rp[:L, :], op=ALU.add), reads=["rp", "xr"], writes=["xr"])
            r_ = xr[:L, 0:D]; kr = xr[:L, D:2 * D]; vr = xr[:L, 2 * D:3 * D]
            kb.dma("pool", S["vs"][t0:t0 + L, :], vr, reads=["xr"], writes=[("vs", ti)])
            kb.op("act", lambda e: e.activation(out=xr[:L, 3 * D:3 * D + 64], in_=xr[:L, 3 * D:3 * D + 64], func=AF.Tanh), reads=["xr"], writes=["xr"])
            p = self.rr("ptr", 2)
            pt, kpt = self.ptr[p], "ptr%d" % p
            for i2 in range(2):
                kb.op("pe", lambda e, i2=i2, pt=pt: e.transpose(out=pt[:64, i2 * 128:i2 * 128 + L], in_=xr[:L, 3 * D + 64 * i2:3 * D + 64 * i2 + 64],
                                                             identity=self.ident[:L, :L]), reads=["xr", "ident"], writes=[kpt])
            kb.op("dve", lambda e, pt=pt: e.tensor_copy(out=lt[0:64, :, :L], in_=pt[:64, 0:256].rearrange("p (a t) -> p a t", a=2)[:, :, :L]), reads=[kpt], writes=["lt"])
            for i2, dst in enumerate(("lw", "a")):
                for hf in range(2):
                    pq = self.rr("pp", 4)
                    pp, kpp = self.pp[pq], "pp%d" % pq
                    kb.op("pe", lambda e, i2=i2, hf=hf, pp=pp: e.matmul(pp[:L, :], lhsT=lt[:, i2, :L], rhs=w2[:, i2, hf * 512:(hf + 1) * 512], start=True, stop=True),
                          reads=["lt", "lt1", "w2"], writes=[kpp])
                    kb.op("act", lambda e, dst=dst, hf=hf, pp=pp: e.activation(out=tv[dst][:L, hf * 512:(hf + 1) * 512], in_=pp[:L, :], func=AF.Sigmoid),
                          reads=[kpp], writes=[("tv", dst)])
            kb.op("dve", lambda e: e.tensor_scalar(out=tv["lw"][:L, :], in0=tv["lw"][:L, :], scalar1=EW, scalar2=None, op0=ALU.mult), reads=[("tv", "lw")], writes=[("tv", "lw")])
            kb.op("dve", lambda e: e.tensor_tensor(out=tv["an"][:L, :], in0=kr, in1=P_["r_k_k"][:L, :], op=ALU.mult), reads=["xr", "r_k_k"], writes=[("tv", "an")])
            kb.op("dve", lambda e: e.tensor_tensor(out=tv["b"][:L, :], in0=tv["an"][:L, :], in1=tv["an"][:L, :], op=ALU.mult), reads=[("tv", "an")], writes=[("tv", "b")])
            kb.op("dve", lambda e: e.tensor_reduce(out=ssq[:L, 0:16], in_=tv["b"][:L, :].rearrange("t (h j) -> t h j", h=16), axis=AX.X, op=ALU.add),
                  reads=[("tv", "b")], writes=["ssq"])
            kb.op("act", lambda e: e.activation(out=ssq[:L, 0:16], in_=ssq[:L, 0:16], func=AF.Sqrt), reads=["ssq"], writes=["ssq"])
            kb.op("dve", lambda e: e.tensor_scalar(out=ssq[:L, 0:16], in0=ssq[:L, 0:16], scalar1=1e-12, scalar2=None, op0=ALU.max), reads=["ssq"], writes=["ssq"])
            kb.op("dve", lambda e: e.reciprocal(out=ssq[:L, 16:32], in_=ssq[:L, 0:16]), reads=["ssq"], writes=["ssq"])
            hv = lambda x: x[:L, :].rearrange("t (h j) -> t h j", h=16)
            kb.op("dve", lambda e: e.tensor_tensor(out=hv(tv["an"]), in0=hv(tv["an"]), in1=ssq[:L, 16:32].unsqueeze(2).to_broadcast([L, 16, 64]), op=ALU.mult),
                  reads=[("tv", "an"), "ssq"], writes=[("tv", "an")])
            kb.op("dve", lambda e: e.tensor_tensor(out=tv["b"][:L, :], in0=tv["an"][:L, :], in1=tv["a"][:L, :], op=ALU.mult),
                  reads=[("tv", "an"), ("tv", "a")], writes=[("tv", "b")])
            kb.op("dve", lambda e: e.tensor_scalar(out=tv["an"][:L, :], in0=tv["an"][:L, :], scalar1=-1.0, scalar2=None, op0=ALU.mult),
                  reads=[("tv", "an"), ("tv", "b")], writes=[("tv", "an")])
            kb.op("dve", lambda e: e.scalar_tensor_tensor(out=tv["k"][:L, :], in0=tv["a"][:L, :], scalar=-1.0, in1=P_["r_k_a"][:L, :], op0=ALU.add, op1=ALU.mult),
                  reads=[("tv", "a"), "r_k_a"], writes=[("tv", "k")])
            kb.op("dve", lambda e: e.scalar_tensor_tensor(out=tv["k"][:L, :], in0=tv["k"][:L, :], scalar=1.0, in1=kr, op0=ALU.add, op1=ALU.mult),
                  reads=[("tv", "k"), "xr"], writes=[("tv", "k")])
            kb.op("dve", lambda e: e.tensor_tensor(out=tv["a"][:L, :], in0=tv["k"][:L, :], in1=P_["r_r_k"][:L, :], op=ALU.mult),
                  reads=[("tv", "k"), "r_r_k", ("tv", "b")], writes=[("tv", "a")])
            kb.op("dve", lambda e: e.tensor_tensor(out=tv["a"][:L, :], in0=tv["a"][:L, :], in1=r_, op=ALU.mult), reads=[("tv", "a"), "xr"], writes=[("tv", "a")])
            kb.op("dve", lambda e: e.tensor_reduce(out=ssq[:L, 32:48], in_=hv(tv["a"]), axis=AX.X, op=ALU.add), reads=[("tv", "a")], writes=["ssq2"])
            for (dst, src, ksrc) in (("at", tv["an"][:L, :], ("tv", "an")), ("rt", r_, "xr"), ("bh", tv["b"][:L, :], ("tv", "b")),
                                     ("kh", tv["k"][:L, :], ("tv", "k")), ("cum", tv["lw"][:L, :], ("tv", "lw"))):
                for half in range(2):
                    p = self.rr("ptr", 2)
                    pt, kpt = self.ptr[p], "ptr%d" % p
                    for k4 in range(4):
                        kc = half * 4 + k4
                        kb.op("pe", lambda e, k4=k4, kc=kc, pt=pt, src=src: e.transpose(out=pt[:, k4 * 128:k4 * 128 + L], in_=src[:, kc * 128:(kc + 1) * 128],
                                                                                  identity=self.ident[:L, :L]), reads=[ksrc, "ident"], writes=[kpt])
                    self.evac(fm[dst][:, half * 4:half * 4 + 4, :L], pt[:, :].rearrange("p (k t) -> p k t", k=4)[:, :, :L], [kpt], [("fm", dst)])
            Lc = 1 if is_s else 64
            nch = L // Lc
            lwT = fm["cum"]
            kb.op("dve", lambda e: e.tensor_copy(out=fm["et"][:, :, :L], in_=lwT[:, :, :L]), reads=[("fm", "cum")], writes=[("fm", "et")])
            if Lc > 1:
                for hh in range(8):
                    for c in range(nch):
                        kb.op("dve", lambda e, hh=hh, c=c: e.tensor_tensor_scan(out=fm["cum"][:, hh, c * Lc:(c + 1) * Lc], data0=self.onesf[:, :Lc],
                                                                             data1=fm["et"][:, hh, c * Lc:(c + 1) * Lc], initial=0.0, op0=ALU.mult, op1=ALU.add),
                              reads=[("fm", "et"), "onesf"], writes=[("fm", "cum")])
            F = lambda n: fm[n][:, :, :L]
            C4 = lambda n: fm[n][:, :, :L].rearrange("p k (c t) -> p k c t", t=Lc)
            kb.op("dve", lambda e: e.tensor_tensor(out=F("et"), in0=F("cum"), in1=F("et"), op=ALU.subtract), reads=[("fm", "cum"), ("fm", "et")], writes=[("fm", "et")])
            kb.op("act", lambda e: e.activation(out=F("et"), in_=F("et"), func=AF.Exp), reads=[("fm", "et")], writes=[("fm", "et")])
            kb.op("dve", lambda e: e.tensor_tensor(out=F("at"), in0=F("at"), in1=F("et"), op=ALU.mult), reads=[("fm", "at"), ("fm", "et")], writes=[("fm", "at")])
            kb.op("act", lambda e: e.activation(out=F("et"), in_=F("cum"), func=AF.Exp), reads=[("fm", "cum"), ("fm", "at")], writes=[("fm", "et")])
            kb.op("dve", lambda e: e.tensor_tensor(out=F("rt"), in0=F("rt"), in1=F("et"), op=ALU.mult), reads=[("fm", "rt"), ("fm", "et")], writes=[("fm", "rt")])
            kb.op("pool", lambda e: e.tensor_copy(out=wc[:, :, :nch], in_=C4("et")[:, :, :, Lc - 1]), reads=[("fm", "et")], writes=["wc"])
            kb.op("dve", lambda e: e.tensor_tensor(out=C4("et"), in0=C4("cum")[:, :, :, Lc - 1:Lc].to_broadcast([128, 8, nch, Lc]), in1=C4("cum"), op=ALU.subtract),
                  reads=[("fm", "cum"), ("fm", "rt"), "wc"], writes=[("fm", "et")])
            kb.op("act", lambda e: e.activation(out=F("et"), in_=F("et"), func=AF.Exp), reads=[("fm", "et")], writes=[("fm", "et")])
            kb.op("dve", lambda e: e.tensor_tensor(out=F("bc"), in0=F("bh"), in1=F("et"), op=ALU.mult), reads=[("fm", "bh"), ("fm", "et")], writes=[("fm", "bc")])
            kb.op("dve", lambda e: e.tensor_tensor(out=F("kc"), in0=F("kh"), in1=F("et"), op=ALU.mult), reads=[("fm", "kh"), ("fm", "et")], writes=[("fm", "kc")])
            kb.op("act", lambda e: e.activation(out=F("et"), in_=F("cum"), func=AF.Exp, scale=-1.0), reads=[("fm", "cum"), ("fm", "bc"), ("fm", "kc")], writes=[("fm", "et")])
            kb.op("dve", lambda e: e.tensor_tensor(out=F("bh"), in0=F("bh"), in1=F("et"), op=ALU.mult), reads=[("fm", "bh"), ("fm", "et")], writes=[("fm", "bh")])
            kb.op("dve", lambda e: e.tensor_tensor(out=F("kh"), in0=F("kh"), in1=F("et"), op=ALU.mult), reads=[("fm", "kh"), ("fm", "et")], writes=[("fm", "kh")])

        def bank():
            q_ = self.rr("bank8", 8)
            return self.bank8[q_], self.bank8k[q_]

        def core(gi, col0, Lc, cidx):
            g = G_[gi]
            K_ = lambda n: ("g", gi, n)
            hs = slice(4 * gi, 4 * gi + 4)
            cs_ = slice(col0, col0 + Lc)
            for half in range(2):
                ps = slice(half * 64, half * 64 + 64)
                kb.op("dve" if half == 0 else "pool", lambda e, ps=ps, half=half: e.tensor_copy(
                    out=g["UBD"][ps, :, :, half * 64:half * 64 + Lc], in_=fmall[ps, hs, 0:4, cs_]),
                    reads=[("fm", n) for n in ("at", "rt", "bh", "kh")], writes=[K_("UBD")])
            yield
            for half in range(2):
                ps = slice(half * 64, half * 64 + 64)
                kb.op("pool" if half == 0 else "dve", lambda e, ps=ps, half=half: e.tensor_copy(
                    out=BK[ps, :, :, half * 64:half * 64 + Lc], in_=fmall[ps, hs, 4:6, cs_]),
                    reads=[("fm", "bc"), ("fm", "kc")], writes=["BK"])
            for vi, dst in enumerate(("Btm", "Ktm")):
                pb_, kpb = bank()
                for u in range(4):
                    kb.op("pe", lambda e, u=u, vi=vi, pb_=pb_: e.matmul(pb_[:, u * 128:(u + 1) * 128], lhsT=BK[:, u, vi, :], rhs=identc[:, :], start=True, stop=True),
                          reads=["BK", "identc"], writes=[kpb])
                self.evac(g[dst][:, :, :], pb_[:, :].rearrange("p (u m) -> p u m", u=4), [kpb], [K_(dst)])
            yield
            for (vec, dst) in ((2, "MNb"), (3, "MNk")):
                for pr in range(2):
                    pb_, kpb = bank()
                    for u2 in range(2):
                        u = pr * 2 + u2
                        kb.op("pe", lambda e, u=u, u2=u2, vec=vec, pb_=pb_: e.matmul(pb_[:, u2 * 256:(u2 + 1) * 256], lhsT=g["UBD"][:, u, vec, :],
                                                                              rhs=g["UBD"][:, u, 0:2, :].rearrange("p a m -> p (a m)"), start=True, stop=True),
                              reads=[K_("UBD")], writes=[kpb])
                    kb.op("dve", lambda e, pr=pr, dst=dst, pb_=pb_: e.tensor_tensor(out=g[dst][:, pr * 2:pr * 2 + 2, :], in0=pb_[:, :].rearrange("p (u m) -> p u m", u=2),
                                                                           in1=mks[:, None, 0:256].to_broadcast([128, 2, 256]), op=ALU.mult),
                          reads=[kpb, "mks"], writes=[K_(dst)])
            pb_, kpb = bank()
            for u in range(4):
                kb.op("pe", lambda e, u=u, pb_=pb_: e.matmul(pb_[:, u * 128:(u + 1) * 128], lhsT=g["UBD"][:, u, 0, :], rhs=g["UBD"][:, u, 2, :], start=True, stop=True),
                      reads=[K_("UBD")], writes=[kpb])
            kb.op("dve", lambda e, pb_=pb_: e.tensor_tensor(out=g["Xt"][:, :, :], in0=pb_[:, :].rearrange("p (u m) -> p u m", u=4),
                                                       in1=mks[:, None, 256:384].to_broadcast([128, 4, 128]), op=ALU.mult),
                  reads=[kpb, "mks"], writes=[K_("Xt")])
            kb.op("act", lambda e: e.activation(out=g["X"][:, :, :], in_=g["MNb"][:, :, 0:128], func=AF.Copy), reads=[K_("MNb")], writes=[K_("X")])
            kb.op("dve", lambda e: e.tensor_tensor(out=g["P"][:, :, :], in0=g["MNb"][:, :, 0:128], in1=identc[:, None, :].to_broadcast([128, 4, 128]), op=ALU.add),
                  reads=[K_("MNb"), "identc"], writes=[K_("P")])
            yield
            nlev = 0
            while (1 << (nlev + 1)) < Lc:
                nlev += 1
            for lev in range(nlev if Lc > 1 else 0):
                lastlev = (lev == nlev - 1)
                pb1 = kp1 = None
                if not lastlev:
                    pb1, kp1 = bank()
                    for u in range(4):
                        kb.op("pe", lambda e, u=u, pb1=pb1: e.matmul(pb1[:, u * 128:(u + 1) * 128], lhsT=g["Xt"][:, u, :], rhs=g["X"][:, u, :], start=True, stop=True),
                              reads=[K_("X"), K_("Xt")], writes=[kp1])
                pb2, kp2 = bank()
                for u in range(4):
                    kb.op("pe", lambda e, u=u, pb2=pb2: e.matmul(pb2[:, u * 128:(u + 1) * 128], lhsT=g["X"][:, u, :], rhs=g["Xt"][:, u, :], start=True, stop=True),
                          reads=[K_("X"), K_("Xt")], writes=[kp2])
                if not lastlev:
                    self.evac(g["X"][:, :, :], pb1[:, :].rearrange("p (u m) -> p u m", u=4), [kp1], [K_("X")])
                self.evac(g["Xt"][:, :, :], pb2[:, :].rearrange("p (u m) -> p u m", u=4), [kp2], [K_("Xt")])
                pb3, kp3 = bank()
                for u in range(4):
                    kb.op("pe", lambda e, u=u, pb3=pb3: e.matmul(pb3[:, u * 128:(u + 1) * 128], lhsT=g["Xt"][:, u, :], rhs=g["P"][:, u, :], start=True, stop=True),
                          reads=[K_("Xt"), K_("P")], writes=[kp3])
                kb.op("dve", lambda e, pb3=pb3: e.tensor_tensor(out=g["P"][:, :, :], in0=pb3[:, :].rearrange("p (u m) -> p u m", u=4), in1=g["P"][:, :, :], op=ALU.add),
                      reads=[kp3, K_("P")], writes=[K_("P")])
                yield
            kS = [("ST", 4 * gi + u) for u in range(4)]
            kSb = [kSTb(4 * gi + u) for u in range(4)]
            kVb = [kVpb(4 * gi + u) for u in range(4)]
            pb_, kpb = bank()
            for u in range(4):
                hh = 4 * gi + u
                kb.op("pe", lambda e, u=u, hh=hh, pb_=pb_: e.matmul(pb_[:, u * 64:(u + 1) * 64], lhsT=g["UBD"][:, u, 0, :], rhs=STb[:, hh, :], start=True, stop=False),
                      reads=[K_("UBD"), kSb[u]], writes=[kpb])
                kb.op("pe", lambda e, u=u, hh=hh, pb_=pb_: e.matmul(pb_[:, u * 64:(u + 1) * 64], lhsT=g["MNk"][:, u, 0:128], rhs=Vpb[:, hh, :], start=False, stop=True),
                      reads=[K_("MNk"), kVb[u]], writes=[kpb])
            self.evac(g["RHS"][:, :, :], pb_[:, 0:256].rearrange("p (u m) -> p u m", u=4), [kpb], [K_("RHS")])
            pb_, kpb = bank()
            for u in range(4):
                kb.op("pe", lambda e, u=u, pb_=pb_: e.matmul(pb_[:, u * 64:(u + 1) * 64], lhsT=g["P"][:, u, :], rhs=g["RHS"][:, u, :], start=True, stop=True),
                      reads=[K_("P"), K_("RHS")], writes=[kpb])
            self.evac(g["U"][:, :, :], pb_[:, 0:256].rearrange("p (u m) -> p u m", u=4), [kpb], [K_("U")])
            yield
            pb_, kpb = bank()
            for u in range(4):
                hh = 4 * gi + u
                kb.op("pe", lambda e, u=u, hh=hh, pb_=pb_: e.matmul(pb_[:, u * 64:(u + 1) * 64], lhsT=g["UBD"][:, u, 1, :], rhs=STb[:, hh, :], start=True, stop=False),
                      reads=[K_("UBD"), kSb[u]], writes=[kpb])
                kb.op("pe", lambda e, u=u, pb_=pb_: e.matmul(pb_[:, u * 64:(u + 1) * 64], lhsT=g["MNb"][:, u, 128:256], rhs=g["U"][:, u, :], start=False, stop=False),
                      reads=[K_("MNb"), K_("U")], writes=[kpb])
                kb.op("pe", lambda e, u=u, hh=hh, pb_=pb_: e.matmul(pb_[:, u * 64:(u + 1) * 64], lhsT=g["MNk"][:, u, 128:256], rhs=Vpb[:, hh, :], start=False, stop=True),
                      reads=[K_("MNk"), kVb[u]], writes=[kpb])
            self.evac(Ych[:, hs, :], pb_[:, 0:256].rearrange("p (u m) -> p u m", u=4), [kpb], [("Ych", 4 * gi + u) for u in range(4)])
            pb_, kpb = bank()
            for u in range(4):
                hh = 4 * gi + u
                kb.op("pe", lambda e, u=u, pb_=pb_: e.matmul(pb_[:, u * 64:(u + 1) * 64], lhsT=g["Btm"][:, u, :], rhs=g["U"][:, u, :], start=True, stop=False),
                      reads=[K_("Btm"), K_("U")], writes=[kpb])
                kb.op("pe", lambda e, u=u, hh=hh, pb_=pb_: e.matmul(pb_[:, u * 64:(u + 1) * 64], lhsT=g["Ktm"][:, u, :], rhs=Vpb[:, hh, :], start=False, stop=True),
                      reads=[K_("Ktm"), kVb[u]], writes=[kpb])
            kb.op("dve", lambda e: e.tensor_tensor(out=ST[:, hs, :], in0=ST[:, hs, :], in1=wc[:, hs, cidx:cidx + 1].to_broadcast([128, 4, 64]), op=ALU.mult),
                  reads=kS + ["wc"], writes=kS)
            kb.op("dve", lambda e, pb_=pb_: e.tensor_tensor(out=ST[:, hs, :], in0=ST[:, hs, :], in1=pb_[:, 0:256].rearrange("p (u m) -> p u m", u=4), op=ALU.add),
                  reads=kS + [kpb], writes=kS)
            if self.CORE_BF16:
                kb.op("act", lambda e: e.activation(out=STb[:, hs, :], in_=ST[:, hs, :], func=AF.Copy), reads=kS, writes=kSb)
            yield

        def run_core(col0, Lc, cidx):
            alive = [core(gi, col0, Lc, cidx) for gi in range(NG)]
            while alive:
                for g_ in list(alive):
                    try:
                        next(g_)
                    except StopIteration:
                        alive.remove(g_)

        def post(t0, L, ti, bi):
            kb.dma("sp", ytm[:L, :], S["ys"][t0:t0 + L, :], reads=[("ys", ti)], writes=["rp"])
            kb.dma("sp", zrt[:L, :], S["z"][t0:t0 + L, O_ZR:O_ZR + D], reads=z_all(ti), writes=["rp"])
            kb.op("act", lambda e: e.activation(out=zrt[:L, :], in_=zrt[:L, :], func=AF.Silu), reads=["rp"], writes=["rp"])
            hv = lambda x: x[:L, :].rearrange("t (h j) -> t h j", h=16)
            kb.op("dve", lambda e: e.tensor_reduce(out=yst[:L, 0:16], in_=hv(ytm), axis=AX.X, op=ALU.add), reads=["rp"], writes=["yst"])
            kb.op("dve", lambda e: e.tensor_scalar(out=yst[:L, 0:16], in0=yst[:L, 0:16], scalar1=-1.0 / 64.0, scalar2=None, op0=ALU.mult), reads=["yst"], writes=["yst"])
            kb.op("dve", lambda e: e.tensor_tensor(out=hv(ytm), in0=hv(ytm), in1=yst[:L, 0:16].unsqueeze(2).to_broadcast([L, 16, 64]), op=ALU.add),
                  reads=["rp", "yst"], writes=["rp"])
            kb.op("dve", lambda e: e.tensor_tensor(out=tv["lw"][:L, :], in0=ytm[:L, :], in1=ytm[:L, :], op=ALU.mult), reads=["rp"], writes=[("tv", "lw")])
            kb.op("dve", lambda e: e.tensor_reduce(out=yst[:L, 16:32], in_=hv(tv["lw"]), axis=AX.X, op=ALU.add), reads=[("tv", "lw")], writes=["yst"])
            kb.op("act", lambda e: e.activation(out=yst[:L, 32:48], in_=yst[:L, 16:32], func=AF.Sqrt, scale=1.0 / 64.0, bias=e12[:L, 0:1]), reads=["yst", "e12"], writes=["yst"])
            kb.op("dve", lambda e: e.reciprocal(out=yst[:L, 48:64], in_=yst[:L, 32:48]), reads=["yst"], writes=["yst"])
            kb.op("dve", lambda e: e.tensor_tensor(out=hv(ytm), in0=hv(ytm), in1=yst[:L, 48:64].unsqueeze(2).to_broadcast([L, 16, 64]), op=ALU.mult),
                  reads=["rp", "yst"], writes=["rp"])
            kb.op("dve", lambda e: e.tensor_tensor(out=ytm[:L, :], in0=ytm[:L, :], in1=P_["r_ln_g"][:L, :], op=ALU.mult), reads=["rp", "r_ln_g"], writes=["rp"])
            kb.op("dve", lambda e: e.tensor_tensor(out=ytm[:L, :], in0=ytm[:L, :], in1=P_["r_ln_b"][:L, :], op=ALU.add), reads=["rp", "r_ln_b"], writes=["rp"])
            kb.op("dve", lambda e: e.tensor_tensor(out=hv(tv["lw"]), in0=xr[:L, 2 * D:3 * D].rearrange("t (h j) -> t h j", h=16),
                                                   in1=ssq[:L, 32:48].unsqueeze(2).to_broadcast([L, 16, 64]), op=ALU.mult), reads=["xr", "ssq2"], writes=[("tv", "lw")])
            kb.op("dve", lambda e: e.tensor_tensor(out=ytm[:L, :], in0=ytm[:L, :], in1=tv["lw"][:L, :], op=ALU.add), reads=["rp", ("tv", "lw")], writes=["rp"])
            kb.op("dve", lambda e: e.tensor_tensor(out=ytm[:L, :], in0=ytm[:L, :], in1=zrt[:L, :], op=ALU.mult), reads=["rp", "rp"], writes=["rp"])
            aj = self.rr("raob", 2)
            for half in range(2):
                p = self.rr("ptr", 2)
                pt, kpt = self.ptr[p], "ptr%d" % p
                for k4 in range(4):
                    kc = half * 4 + k4
                    kb.op("pe", lambda e, k4=k4, kc=kc, pt=pt: e.transpose(out=pt[:, k4 * 128:k4 * 128 + L], in_=ytm[:L, kc * 128:(kc + 1) * 128],
                                                                       identity=self.ident[:L, :L]), reads=["rp", "ident"], writes=[kpt])
                self.evac(aob[aj][:, half * 4:half * 4 + 4, :L], pt[:, :].rearrange("p (k t) -> p k t", k=4)[:, :, :L], [kpt], ["raob%d" % aj])
            kb.dma("pool", S["act_r"][:, t0:t0 + L].rearrange("(k p) t -> p k t", p=128), aob[aj][:, :, :L], reads=["raob%d" % aj], writes=[("act", "r", bi)])

        natx8 = sbp("rnatx8", (128, 8, 128))
        kb.op("pool", lambda e: e.memset(natx8[:], 0.0), writes=["natx8"])

        def bd_transpose(src, ksrc, dst, kdst, also_b=None):
            for half in range(2):
                ps = slice(half * 64, half * 64 + 64)
                kb.op("pool" if half else "dve", lambda e, ps=ps: e.tensor_copy(out=natx8[ps, :, ps], in_=src[ps, :, :]), reads=ksrc, writes=["natx8"])
            for q4 in range(2):
                pb_, kpb = bank()
                for u in range(4):
                    kb.op("pe", lambda e, u=u, q4=q4, pb_=pb_: e.transpose(out=pb_[:, u * 128:(u + 1) * 128], in_=natx8[:, q4 * 4 + u, :], identity=self.ident[:, :]),
                          reads=["natx8", "ident"], writes=[kpb])
                for half in range(2):
                    ps = slice(half * 64, half * 64 + 64)
                    kb.op("dve" if half else "act", (lambda e, ps=ps, q4=q4, pb_=pb_: e.tensor_copy(out=dst[ps, q4 * 4:q4 * 4 + 4, :], in_=pb_[:, :].rearrange("p (u m) -> p u m", u=4)[ps, :, ps]))
                          if half else (lambda e, ps=ps, q4=q4, pb_=pb_: e.activation(out=dst[ps, q4 * 4:q4 * 4 + 4, :], in_=pb_[:, :].rearrange("p (u m) -> p u m", u=4)[ps, :, ps], func=AF.Copy)),
                          reads=[kpb], writes=kdst)

        def state_out(dst):
            bd_transpose(ST, [("ST", h) for h in range(8)], nato, ["nato"])
            kb.dma("pool", dst.rearrange("(hh hl) i j -> (hl i) hh j", hl=2), nato[:, :, :], reads=["nato"], writes=[], is_output=True)

        def state_in(src):
            kb.dma("sp", nat[:, :, :], src.rearrange("(hh hl) i j -> (hl i) hh j", hl=2), writes=["nat"])
            bd_transpose(nat, ["nat"], ST, [("ST", h) for h in range(8)])
            if self.CORE_BF16:
                kb.op("pool", lambda e: e.tensor_copy(out=STb[:, :, :], in_=ST[:, :, :]), reads=[("ST", h) for h in range(8)], writes=[("STb", h) for h in range(8)])

        blk_of = lambda t: next(i for i, (b0, bn) in enumerate(self.blocks) if b0 <= t < b0 + bn)
        for ti in range(T // 128):
            t0 = ti * 128
            prep(t0, 128, ti, False)
            for c in range(2):
                for half in range(2):
                    kb.dma("sp", Vp[half * 64:half * 64 + 64, :, :],
                           S["vs"][t0 + c * 64:t0 + c * 64 + 64, :].rearrange("t (hh hl i) -> hl t hh i", hl=2, i=64)[half],
                           reads=[("vs", ti)], writes=[("Vp", h) for h in range(8)])
                vp_ready()
                run_core(c * 64, 64, c)
                for half in range(2):
                    kb.dma("pool", S["ys"][t0 + c * 64:t0 + c * 64 + 64, :].rearrange("t (hh hl i) -> hl t hh i", hl=2, i=64)[half],
                           Ych[half * 64:half * 64 + 64, :, :], reads=[("Ych", h) for h in range(8)], writes=[("ys", ti)])
            post(t0, 128, ti, blk_of(t0))
        state_out(O["wkv_prompt"][l])
        if NS:
            ti = T // 128
            prep(T, NS, ti, True)
            zero_units()
            kb.op("pool", lambda e: e.memset(Vp[:], 0.0), writes=[("Vp", h) for h in range(8)])
            for b in range(NS):
                state_in(I["state_rwkv_wkv"][l, b])
                for half in range(2):
                    kb.dma("sp", Vp[half * 64:half * 64 + 1, :, :],
                           S["vs"][T + b:T + b + 1, :].rearrange("t (hh hl i) -> hl t hh i", hl=2, i=64)[half],
                           reads=[("vs", ti)], writes=[("Vp", h) for h in range(8)])
                vp_ready()
                run_core(b, 1, b)
                for half in range(2):
                    kb.dma("pool", S["ys"][T + b:T + b + 1, :].rearrange("t (hh hl i) -> hl t hh i", hl=2, i=64)[half],
                           Ych[half * 64:half * 64 + 1, :, :], reads=[("Ych", h) for h in range(8)], writes=[("ys", ti)])
                state_out(O["wkv_sample"][l, b])
            post(T, NS, ti, blk_of(T))
        self.end_phase()

    def sin_to(self, tkey, out, ang, tf, ti, tm, key_out, key_ang, shift=0.0):
        kb = self.kb
        TWO_PI = 2.0 * math.pi
        kt = ("sin_tmp", tkey)
        kb.op("dve", lambda e: e.tensor_scalar(out=tm, in0=ang, scalar1=shift, scalar2=None, op0=ALU.add),
              reads=[key_ang], writes=[(kt, "m")])
        kb.op("dve", lambda e: e.tensor_scalar(out=tf, in0=tm, scalar1=1.0 / TWO_PI, scalar2=None, op0=ALU.mult),
              reads=[(kt, "m")], writes=[(kt, "f")])
        kb.op("dve", lambda e: e.tensor_copy(out=ti, in_=tf), reads=[(kt, "f")], writes=[(kt, "i")])
        kb.op("dve", lambda e: e.tensor_copy(out=tf, in_=ti), reads=[(kt, "i")], writes=[(kt, "f")])
        kb.op("dve", lambda e: e.scalar_tensor_tensor(out=tm, in0=tf, scalar=-TWO_PI, in1=tm, op0=ALU.mult, op1=ALU.add),
              reads=[(kt, "f"), (kt, "m")], writes=[(kt, "m")])
        kb.op("dve", lambda e: e.tensor_scalar(out=tf, in0=tm, scalar1=math.pi, scalar2=-TWO_PI, op0=ALU.is_gt, op1=ALU.mult),
              reads=[(kt, "m")], writes=[(kt, "f")])
        kb.op("dve", lambda e: e.tensor_tensor(out=tm, in0=tm, in1=tf, op=ALU.add), reads=[(kt, "f"), (kt, "m")],
              writes=[(kt, "m")])
        kb.op("dve", lambda e: e.tensor_scalar(out=tf, in0=tm, scalar1=-math.pi, scalar2=TWO_PI, op0=ALU.is_lt, op1=ALU.mult),
              reads=[(kt, "m")], writes=[(kt, "f")])
        kb.op("dve", lambda e: e.tensor_tensor(out=tm, in0=tm, in1=tf, op=ALU.add), reads=[(kt, "f"), (kt, "m")],
              writes=[(kt, "m")])
        kb.op("dve", lambda e: e.tensor_scalar(out=tm, in0=tm, scalar1=-3.1415925, scalar2=3.1415925, op0=ALU.max, op1=ALU.min),
              reads=[(kt, "m")], writes=[(kt, "m")])
        kb.op("act", lambda e: e.activation(out=out, in_=tm, func=AF.Sin), reads=[(kt, "m")], writes=[key_out])

    def s5_phase(self, l):
        kb = self.kb
        I, S, O = self.I, self.S, self.O
        T, NS = self.T, self.NS
        self.begin_phase()
        sbp = self.sbp
        nat = sbp("nat", (32, 14, 128))
        nati = sbp("nati", (32, 128), I32)
        PT = sbp("PT", (128, 6, 32))
        BR = sbp("BR", (128, 32, 16)); BI = sbp("BI", (128, 32, 16))
        bbr = sbp("bbr", (128, 32, 16)); bbi = sbp("bbi", (128, 32, 16))
        xin = sbp("xin", (128, 512))
        btmp = xin[:, :].rearrange("p (s c) -> p s c", c=16)
        Bz = [sbp("Bz%d" % i, (128, 32, 128), BF16) for i in range(2)]
        Cn = [sbp("Cn%d" % i, (128, 8, 64)) for i in range(2)]
        Cz = [sbp("Cz%d" % i, (128, 32, 128), BF16) for i in range(2)]
        cosT = sbp("cosT", (128, 32, 128)); sinT = sbp("sinT", (128, 32, 128))
        ang = sbp("ang", (128, 2, 128)); angf = sbp("angf", (128, 2, 128)); angm = sbp("angm", (128, 2, 128))
        angi = sbp("angi", (128, 2, 128), I32)
        car = [sbp("car%d" % i, (128, 32)) for i in range(2)]
        ctmp = sbp("ctmp", (128, 4))
        m3 = sbp("m3", (128, 4, 2)); m4 = sbp("m4", (128, 4, 8)); iota1 = sbp("iota1", (128, 128))
        dcol = sbp("dcol", (128, 8)); gbcol = sbp("gbcol", (128, 8))
        wg = sbp("wglu", (128, KC, D), BF16)
        wstg = [sbp("wstg%d" % i, (128, KC, 128)) for i in range(1)]
        uf = [sbp("uf%d" % i, (128, 256)) for i in range(2)]
        ub = [sbp("ub%d" % i, (128, 256), BF16) for i in range(2)]
        WS = []
        for w_ in range(2):
            WS.append(([sbp("tq%d_%d" % (w_, i), (128, 256)) for i in range(4)],
                       [sbp("bh%d_%d" % (w_, i), (128, 256)) for i in range(2)],
                       [sbp("sh%d_%d" % (w_, i), (128, 256)) for i in range(2)],
                       [sbp("sbf%d_%d" % (w_, i), (128, 256), BF16) for i in range(2)]))
        ysg = sbp("ysg", (128, KC, 256)); ysb = sbp("ysb", (128, KC, 256), BF16)
        yt = [sbp("yt%d" % i, (128, 256)) for i in range(2)]
        zst = [sbp("zst%d" % i, (128, 256)) for i in range(2)]
        aout = [sbp("aout%d" % i, (128, 256), BF16) for i in range(2)]
        s0T = [sbp("s0T%d" % i, (128, 32, 16)) for i in range(2)]
        snw = [sbp("snw%d" % i, (128, 32, 16)) for i in range(2)]
        sst = sbp("sst", (32, 512))

        kb.dma("sp", m3[:], I["mask3"], writes=["m3"])
        kb.dma("sp", m4[:], I["mask4"], writes=["m4"])
        kb.dma("sp", iota1[:], I["iota1"], writes=["iota1"])
        kb.dma("sp", nat[:, 0, :], I["s_lam_re"][l].rearrange("(s g) p -> s (g p)", g=2), writes=["nat0"])
        kb.dma("sp", nat[:, 1, :], I["s_lam_im"][l].rearrange("(s g) p -> s (g p)", g=2), writes=["nat1"])
        kb.dma("sp", nat[:, 2, 0:2], I["s_log_dt"][l].rearrange("(s g) -> s g", g=2), writes=["nat2"])
        kb.dma("sp", dcol[:], I["s_d"][l].rearrange("(k p) -> p k", p=128), writes=["dcol"], allow_slow_non_contiguous=True)
        kb.dma("sp", gbcol[:], I["s_glu_b"][l].rearrange("(k p) -> p k", p=128), writes=["gbcol"], allow_slow_non_contiguous=True)
        kb.dma("sp", BR[:], I["s_b_re"][l].rearrange("(s g) p c -> (g p) s c", g=2), writes=["BR"])
        kb.dma("sp", BI[:], I["s_b_im"][l].rearrange("(s g) p c -> (g p) s c", g=2), writes=["BI"])
        kb.dma("sp", Cn[0][:], I["s_c_re"][l].rearrange("(o g) c p -> (g c) o p", g=8), writes=["Cn0"])
        kb.dma("sp", Cn[1][:], I["s_c_im"][l].rearrange("(o g) c p -> (g c) o p", g=8), writes=["Cn1"])
        for h in range(8):
            kb.dma("sp", wstg[0][:], I["s_glu_w"][l][:, h * 128:(h + 1) * 128].rearrange("(k p) c -> p k c", p=128),
                   writes=["wstg"])
            kb.op("dve", lambda e, h=h: e.tensor_copy(out=wg[:, :, h * 128:(h + 1) * 128], in_=wstg[0][:]),
                  reads=["wstg"], writes=["wg"])
        N = lambda i: nat[:, i, :]
        kb.op("act", lambda e: e.activation(out=nat[:, 2, 2:4], in_=nat[:, 2, 0:2], func=AF.Exp), reads=["nat2"], writes=["nat2"])
        kb.op("dve", lambda e: e.tensor_copy(out=nat[:, 3, :].rearrange("s (g p) -> s g p", g=2),
                                             in_=nat[:, 2, 2:4].unsqueeze(2).to_broadcast([32, 2, 64])),
              reads=["nat2"], writes=["nat3"])
        kb.op("dve", lambda e: e.tensor_scalar(out=N(0), in0=N(0), scalar1=-1e-4, scalar2=None, op0=ALU.min),
              reads=["nat0"], writes=["nat0"])
        kb.op("dve", lambda e: e.tensor_tensor(out=N(4), in0=N(0), in1=N(3), op=ALU.mult), reads=["nat0", "nat3"], writes=["nat4"])
        kb.op("act", lambda e: e.activation(out=N(4), in_=N(4), func=AF.Exp), reads=["nat4"], writes=["nat4"])
        kb.op("dve", lambda e: e.tensor_tensor(out=N(5), in0=N(1), in1=N(3), op=ALU.mult), reads=["nat1", "nat3"], writes=["nat5"])
        self.sin_to("nat", N(6), N(5), N(12), nati[:, :], N(13), "nat6", "nat5")
        self.sin_to("nat", N(7), N(5), N(12), nati[:, :], N(13), "nat7", "nat5", shift=math.pi / 2)
        kb.op("dve", lambda e: e.tensor_tensor(out=N(8), in0=N(4), in1=N(7), op=ALU.mult), reads=["nat4", "nat7"], writes=["nat8"])
        kb.op("dve", lambda e: e.tensor_tensor(out=N(9), in0=N(4), in1=N(6), op=ALU.mult), reads=["nat4", "nat6"], writes=["nat9"])
        kb.op("dve", lambda e: e.tensor_tensor(out=N(12), in0=N(0), in1=N(0), op=ALU.mult), reads=["nat0"], writes=["nat12"])
        kb.op("dve", lambda e: e.tensor_tensor(out=N(13), in0=N(1), in1=N(1), op=ALU.mult), reads=["nat1"], writes=["nat13"])
        kb.op("dve", lambda e: e.tensor_tensor(out=N(12), in0=N(12), in1=N(13), op=ALU.add), reads=["nat12", "nat13"], writes=["nat12"])
        kb.op("dve", lambda e: e.reciprocal(out=N(12), in_=N(12)), reads=["nat12"], writes=["nat12"])
        kb.op("dve", lambda e: e.tensor_scalar(out=N(13), in0=N(8), scalar1=-1.0, scalar2=None, op0=ALU.add), reads=["nat8"], writes=["nat13"])
        kb.op("dve", lambda e: e.tensor_tensor(out=N(10), in0=N(13), in1=N(0), op=ALU.mult), reads=["nat13", "nat0"], writes=["nat10"])
        kb.op("dve", lambda e: e.tensor_tensor(out=N(11), in0=N(9), in1=N(1), op=ALU.mult), reads=["nat9", "nat1"], writes=["nat11"])
        kb.op("dve", lambda e: e.tensor_tensor(out=N(10), in0=N(10), in1=N(11), op=ALU.add), reads=["nat10", "nat11"], writes=["nat10"])
        kb.op("dve", lambda e: e.tensor_tensor(out=N(10), in0=N(10), in1=N(12), op=ALU.mult), reads=["nat10", "nat12"], writes=["nat10"])
        kb.op("dve", lambda e: e.tensor_tensor(out=N(11), in0=N(9), in1=N(0), op=ALU.mult), reads=["nat9", "nat0"], writes=["nat11"])
        kb.op("dve", lambda e: e.tensor_tensor(out=N(13), in0=N(13), in1=N(1), op=ALU.mult), reads=["nat13", "nat1"], writes=["nat13"])
        kb.op("dve", lambda e: e.tensor_tensor(out=N(11), in0=N(11), in1=N(13), op=ALU.subtract), reads=["nat11", "nat13"], writes=["nat11"])
        kb.op("dve", lambda e: e.tensor_tensor(out=N(11), in0=N(11), in1=N(12), op=ALU.mult), reads=["nat11", "nat12"], writes=["nat11"])
        p = self.rr("ptr", 2)
        pt, kpt = self.ptr[p], "ptr%d" % p
        for si, ni in enumerate((4, 5, 8, 9, 10, 11)):
            kb.op("pe", lambda e, si=si, ni=ni, pt=pt: e.transpose(out=pt[:, si * 32:(si + 1) * 32], in_=nat[:, ni, :],
                                                                  identity=self.ident[:32, :32]),
                  reads=["nat%d" % ni, "ident"], writes=[kpt])
        kb.op("dve", lambda e, pt=pt: e.tensor_copy(out=PT[:, :, :], in_=pt[:, 0:192].rearrange("p (s c) -> p s c", s=6)),
              reads=[kpt], writes=["PT"])
        bc = lambda si: PT[:, si, :].unsqueeze(2).to_broadcast([128, 32, 16])
        kb.op("dve", lambda e: e.tensor_tensor(out=bbr[:], in0=BR[:], in1=bc(4), op=ALU.mult), reads=["BR", "PT"], writes=["bbr"])
        kb.op("dve", lambda e: e.tensor_tensor(out=btmp, in0=BI[:], in1=bc(5), op=ALU.mult), reads=["BI", "PT"], writes=["xin"])
        kb.op("dve", lambda e: e.tensor_tensor(out=bbr[:], in0=bbr[:], in1=btmp, op=ALU.subtract), reads=["bbr", "xin"], writes=["bbr"])
        kb.op("dve", lambda e: e.tensor_tensor(out=bbi[:], in0=BI[:], in1=bc(4), op=ALU.mult), reads=["BI", "PT"], writes=["bbi"])
        kb.op("dve", lambda e: e.tensor_tensor(out=btmp, in0=BR[:], in1=bc(5), op=ALU.mult), reads=["BR", "PT", "bbr"], writes=["xin"])
        kb.op("dve", lambda e: e.tensor_tensor(out=bbi[:], in0=bbi[:], in1=btmp, op=ALU.add), reads=["bbi", "xin"], writes=["bbi"])
        for ri, (bb, kbb) in enumerate(((bbr, "bbr"), (bbi, "bbi"))):
            for oc in range(8):
                kb.op("dve", lambda e, bb=bb, oc=oc: e.tensor_tensor(
                    out=xin[:, :].rearrange("p (q g c) -> p q g c", q=4, g=8),
                    in0=bb[:, oc * 4:oc * 4 + 4, None, :].to_broadcast([128, 4, 8, 16]),
                    in1=m4[:, :, :, None].to_broadcast([128, 4, 8, 16]), op=ALU.mult),
                    reads=[kbb, "m4"], writes=["xin"])
                p = self.rr("ptr", 2)
                pt, kpt = self.ptr[p], "ptr%d" % p
                for q in range(4):
                    kb.op("pe", lambda e, q=q, pt=pt: e.transpose(out=pt[:, q * 128:(q + 1) * 128], in_=xin[:, q * 128:(q + 1) * 128],
                                                                  identity=self.ident[:, :]),
                          reads=["xin", "ident"], writes=[kpt])
                kb.op("act", lambda e, pt=pt, oc=oc, ri=ri: e.activation(
                    out=Bz[ri][:, oc * 4:oc * 4 + 4, :], in_=pt[:, :].rearrange("p (q m) -> p q m", q=4), func=AF.Copy),
                    reads=[kpt], writes=[("Bz", ri)])
        for ri in range(2):
            for oc in range(8):
                kb.op("dve", lambda e, ri=ri, oc=oc: e.tensor_tensor(
                    out=xin[:, :].rearrange("p (q g s) -> p q g s", q=4, g=2),
                    in0=Cn[ri][:, oc, None, None, :].to_broadcast([128, 4, 2, 64]),
                    in1=m3[:, :, :, None].to_broadcast([128, 4, 2, 64]), op=ALU.mult),
                    reads=["Cn%d" % ri, "m3"], writes=["xin"])
                p = self.rr("ptr", 2)
                pt, kpt = self.ptr[p], "ptr%d" % p
                for q in range(4):
                    kb.op("pe", lambda e, q=q, pt=pt: e.transpose(out=pt[:, q * 128:(q + 1) * 128], in_=xin[:, q * 128:(q + 1) * 128],
                                                                  identity=self.ident[:, :]),
                          reads=["xin", "ident"], writes=[kpt])
                kb.op("act", lambda e, pt=pt, oc=oc, ri=ri: e.activation(
                    out=Cz[ri][:, oc * 4:oc * 4 + 4, :], in_=pt[:, :].rearrange("p (q m) -> p q m", q=4), func=AF.Copy,
                    scale=(1.0 if ri == 0 else -1.0)),
                    reads=[kpt], writes=[("Cz", ri)])
        for g in range(16):
            for s8 in range(2):
                sc = g * 2 + s8
                kb.op("dve", lambda e, s8=s8, sc=sc: e.tensor_scalar(out=ang[:, s8, :], in0=iota1[:, :], scalar1=PT[:, 1, sc:sc + 1],
                                                                     scalar2=None, op0=ALU.mult),
                      reads=["iota1", "PT"], writes=["ang"])
            self.sin_to("ang", sinT[:, g * 2:(g + 1) * 2, :], ang[:], angf[:], angi[:], angm[:], ("sinT", g), "ang")
            self.sin_to("ang", cosT[:, g * 2:(g + 1) * 2, :], ang[:], angf[:], angi[:], angm[:], ("cosT", g), "ang",
                        shift=math.pi / 2)
        tabs = [("sinT", g) for g in range(16)] + [("cosT", g) for g in range(16)]
        kb.op("dve", lambda e: e.memset(car[0][:], 0.0), writes=[("car", 0, sc_) for sc_ in range(32)])
        kb.op("dve", lambda e: e.memset(car[1][:], 0.0), writes=[("car", 1, sc_) for sc_ in range(32)])

        if NS:
            for ri, nm in enumerate(("state_s5_re", "state_s5_im")):
                p = self.rr("ptr", 2)
                pt, kpt = self.ptr[p], "ptr%d" % p
                for qq in range(8):
                    kb.dma("sp", sst[:NS, :], I[nm][l].rearrange("b g p -> b (g p)")[:, qq * 512:(qq + 1) * 512], writes=["sst"])
                    for s8 in range(4):
                        sc = qq * 4 + s8
                        kb.op("pe", lambda e, sc=sc, s8=s8, pt=pt: e.transpose(out=pt[:, sc * NS:(sc + 1) * NS], in_=sst[:NS, s8 * 128:(s8 + 1) * 128],
                                                                      identity=self.ident[:NS, :NS]),
                              reads=["sst", "ident"], writes=[kpt])
                kb.op("dve", lambda e, pt=pt, ri=ri: e.tensor_copy(out=s0T[ri][:, :, :NS],
                                                                  in_=pt[:, :32 * NS].rearrange("p (s b) -> p s b", s=32)),
                      reads=[kpt], writes=[("s0T", ri)])

        subblocks = []
        for bi, (t0b, nb) in enumerate(self.blocks):
            for t0 in range(t0b, t0b + nb, 256):
                subblocks.append((bi, t0, min(256, t0b + nb - t0)))
        for (bi, t0, n) in subblocks:
            is_s = (t0 >= T)
            nsub = n // 128
            for oc in range(8):
                j = self.rr("uf", 2)
                kuf, kub = "uf%d" % j, "ub%d" % j
                row = O_U + oc * 128
                kb.dma("sp", uf[j][:, :n], S["zT"][row:row + 128, t0:t0 + n], reads=[("zT", row, bi)], writes=[kuf])
                kb.op("pool", lambda e, j=j: e.tensor_copy(out=ub[j][:, :n], in_=uf[j][:, :n]), reads=[kuf], writes=[kub])
                py = self.rr("ptr", 2)
                pys, kpys = self.ptr[py], "ptr%d" % py
                def sc_gen(q, W):
                    tq, bh, sh, sbf = W
                    wk = lambda n_: (n_, id(W))
                    sc = oc * 4 + q
                    pa = self.rr("pp", 4); pb = self.rr("pp", 4)
                    A, B = self.pp[pa], self.pp[pb]
                    kA, kB = "pp%d" % pa, "pp%d" % pb
                    kb.op("pe", lambda e, A=A, sc=sc, j=j: e.matmul(A[:, :n], lhsT=Bz[0][:, sc, :], rhs=ub[j][:, :n], start=True, stop=True),
                          reads=[("Bz", 0), kub], writes=[kA])
                    kb.op("pe", lambda e, B=B, sc=sc, j=j: e.matmul(B[:, :n], lhsT=Bz[1][:, sc, :], rhs=ub[j][:, :n], start=True, stop=True),
                          reads=[("Bz", 1), kub], writes=[kB])
                    yield
                    if not is_s:
                        v3 = lambda x: x[:, :n].rearrange("p (m j) -> p m j", j=128)
                        cosv = cosT[:, sc, None, :].to_broadcast([128, nsub, 128])
                        sinv = sinT[:, sc, None, :].to_broadcast([128, nsub, 128])
                        kb.op("dve", lambda e, A=A, cosv=cosv: e.tensor_tensor(out=v3(tq[0]), in0=v3(A), in1=cosv, op=ALU.mult),
                              reads=[kA] + tabs, writes=[wk("tq0")])
                        kb.op("dve", lambda e, B=B, sinv=sinv: e.tensor_tensor(out=v3(tq[1]), in0=v3(B), in1=sinv, op=ALU.mult),
                              reads=[kB] + tabs, writes=[wk("tq1")])
                        kb.op("dve", lambda e, B=B, cosv=cosv: e.tensor_tensor(out=v3(tq[2]), in0=v3(B), in1=cosv, op=ALU.mult),
                              reads=[kB] + tabs, writes=[wk("tq2")])
                        kb.op("dve", lambda e, A=A, sinv=sinv: e.tensor_tensor(out=v3(tq[3]), in0=v3(A), in1=sinv, op=ALU.mult),
                              reads=[kA] + tabs, writes=[wk("tq3")])
                        kb.op("pool", lambda e: e.tensor_tensor(out=bh[0][:, :n], in0=tq[0][:, :n], in1=tq[1][:, :n], op=ALU.add),
                              reads=[wk("tq0"), wk("tq1")], writes=[wk("bh0")])
                        kb.op("pool", lambda e: e.tensor_tensor(out=bh[1][:, :n], in0=tq[2][:, :n], in1=tq[3][:, :n], op=ALU.subtract),
                              reads=[wk("tq2"), wk("tq3")], writes=[wk("bh1")])
                        yield
                        rho = PT[:, 0, sc:sc + 1].to_broadcast([128, 128])
                        c128 = cosT[:, sc, 127:128]
                        s128 = sinT[:, sc, 127:128]
                        for m in range(nsub):
                            sl = slice(m * 128, (m + 1) * 128)
                            for ri in range(2):
                                kb.op("dve", lambda e, ri=ri, sl=sl, sc=sc, rho=rho: e.tensor_tensor_scan(
                                    out=sh[ri][:, sl], data0=rho, data1=bh[ri][:, sl], initial=car[ri][:, sc:sc + 1],
                                    op0=ALU.mult, op1=ALU.add), reads=[wk("bh%d" % ri), ("car", ri, sc), "PT"], writes=[wk("sh%d" % ri)])
                            lr = sh[0][:, m * 128 + 127:m * 128 + 128]
                            li = sh[1][:, m * 128 + 127:m * 128 + 128]
                            kb.op("dve", lambda e, li=li, s128=s128: e.tensor_scalar(out=ctmp[:, 2 * (q % 2):2 * (q % 2) + 1], in0=li, scalar1=s128, scalar2=None, op0=ALU.mult),
                                  reads=[wk("sh1")] + tabs, writes=[wk("ctmp")])
                            kb.op("dve", lambda e, li=li, c128=c128: e.tensor_scalar(out=ctmp[:, 2 * (q % 2) + 1:2 * (q % 2) + 2], in0=li, scalar1=c128, scalar2=None, op0=ALU.mult),
                                  reads=[wk("sh1")] + tabs, writes=[wk("ctmp")])
                            kb.op("dve", lambda e, lr=lr, c128=c128, sc=sc: e.scalar_tensor_tensor(
                                out=car[0][:, sc:sc + 1], in0=lr, scalar=c128, in1=ctmp[:, 2 * (q % 2):2 * (q % 2) + 1], op0=ALU.mult, op1=ALU.subtract),
                                reads=[wk("sh0"), wk("ctmp")] + tabs, writes=[("car", 0, sc)])
                            kb.op("dve", lambda e, lr=lr, s128=s128, sc=sc: e.scalar_tensor_tensor(
                                out=car[1][:, sc:sc + 1], in0=lr, scalar=s128, in1=ctmp[:, 2 * (q % 2) + 1:2 * (q % 2) + 2], op0=ALU.mult, op1=ALU.add),
                                reads=[wk("sh0"), wk("ctmp")] + tabs, writes=[("car", 1, sc)])
                        yield
                        kb.op("pool", lambda e, cosv=cosv: e.tensor_tensor(out=v3(tq[0]), in0=v3(sh[0]), in1=cosv, op=ALU.mult),
                              reads=[wk("sh0")] + tabs, writes=[wk("tq0")])
                        kb.op("pool", lambda e, sinv=sinv: e.tensor_tensor(out=v3(tq[1]), in0=v3(sh[1]), in1=sinv, op=ALU.mult),
                              reads=[wk("sh1")] + tabs, writes=[wk("tq1")])
                        kb.op("dve", lambda e, sinv=sinv: e.tensor_tensor(out=v3(tq[2]), in0=v3(sh[0]), in1=sinv, op=ALU.mult),
                              reads=[wk("sh0")] + tabs, writes=[wk("tq2")])
                        kb.op("dve", lambda e, cosv=cosv: e.tensor_tensor(out=v3(tq[3]), in0=v3(sh[1]), in1=cosv, op=ALU.mult),
                              reads=[wk("sh1")] + tabs, writes=[wk("tq3")])
                        kb.op("pool", lambda e: e.tensor_tensor(out=sbf[0][:, :n], in0=tq[0][:, :n], in1=tq[1][:, :n], op=ALU.subtract),
                              reads=[wk("tq0"), wk("tq1")], writes=[wk("sbf0")])
                        kb.op("dve", lambda e: e.tensor_tensor(out=sbf[1][:, :n], in0=tq[2][:, :n], in1=tq[3][:, :n], op=ALU.add),
                              reads=[wk("tq2"), wk("tq3")], writes=[wk("sbf1")])
                    else:
                        lbre = PT[:, 2, sc:sc + 1]
                        lbim = PT[:, 3, sc:sc + 1]
                        s0r, s0i = s0T[0][:, sc, :n], s0T[1][:, sc, :n]
                        kb.op("dve", lambda e, s0i=s0i, lbim=lbim: e.tensor_scalar(out=tq[0][:, :n], in0=s0i, scalar1=lbim, scalar2=None, op0=ALU.mult),
                              reads=[("s0T", 1), "PT"], writes=[wk("tq0")])
                        kb.op("dve", lambda e, s0r=s0r, lbre=lbre: e.scalar_tensor_tensor(out=tq[0][:, :n], in0=s0r, scalar=lbre, in1=tq[0][:, :n],
                                                                               op0=ALU.mult, op1=ALU.subtract),
                              reads=[("s0T", 0), "PT", wk("tq0")], writes=[wk("tq0")])
                        kb.op("dve", lambda e, A=A, sc=sc: e.tensor_tensor(out=snw[0][:, sc, :n], in0=A[:, :n], in1=tq[0][:, :n], op=ALU.add),
                              reads=[kA, wk("tq0")], writes=[("snw", 0)])
                        kb.op("dve", lambda e, s0r=s0r, lbim=lbim: e.tensor_scalar(out=tq[1][:, :n], in0=s0r, scalar1=lbim, scalar2=None, op0=ALU.mult),
                              reads=[("s0T", 0), "PT"], writes=[wk("tq1")])
                        kb.op("dve", lambda e, s0i=s0i, lbre=lbre: e.scalar_tensor_tensor(out=tq[1][:, :n], in0=s0i, scalar=lbre, in1=tq[1][:, :n],
                                                                               op0=ALU.mult, op1=ALU.add),
                              reads=[("s0T", 1), "PT", wk("tq1")], writes=[wk("tq1")])
                        kb.op("dve", lambda e, B=B, sc=sc: e.tensor_tensor(out=snw[1][:, sc, :n], in0=B[:, :n], in1=tq[1][:, :n], op=ALU.add),
                              reads=[kB, wk("tq1")], writes=[("snw", 1)])
                        for ri in range(2):
                            kb.op("pool", lambda e, ri=ri, sc=sc: e.tensor_copy(out=sbf[ri][:, :n], in_=snw[ri][:, sc, :n]),
                                  reads=[("snw", ri)], writes=[wk("sbf%d" % ri)])
                    yield
                    for ri in range(2):
                        kb.op("pe", lambda e, ri=ri, sc=sc, q=q, pys=pys: e.matmul(pys[:, :n], lhsT=Cz[ri][:, sc, :], rhs=sbf[ri][:, :n],
                                                                                start=(q == 0 and ri == 0), stop=(q == 3 and ri == 1)),
                              reads=[("Cz", ri), wk("sbf%d" % ri)], writes=[kpys])
                for qp in range(2):
                    alive = [sc_gen(2 * qp + w_, WS[w_]) for w_ in range(2)]
                    while alive:
                        for g_ in list(alive):
                            try:
                                next(g_)
                            except StopIteration:
                                alive.remove(g_)
                y0, y1 = yt[0], yt[1]
                kb.op("dve", lambda e, j=j, oc=oc, pys=pys: e.scalar_tensor_tensor(out=y0[:, :n], in0=uf[j][:, :n], scalar=dcol[:, oc:oc + 1],
                                                                             in1=pys[:, :n], op0=ALU.mult, op1=ALU.add),
                      reads=[kuf, "dcol", kpys], writes=["yt0"])
                kb.op("act", lambda e: e.activation(out=y1[:, :n], in_=y0[:, :n], func=AF.Square), reads=["yt0"], writes=["yt1"])
                kb.op("dve", lambda e: e.tensor_scalar(out=y1[:, :n], in0=y1[:, :n], scalar1=0.044715, scalar2=1.0, op0=ALU.mult, op1=ALU.add),
                      reads=["yt1"], writes=["yt1"])
                kb.op("dve", lambda e: e.tensor_tensor(out=y1[:, :n], in0=y1[:, :n], in1=y0[:, :n], op=ALU.mult), reads=["yt1", "yt0"], writes=["yt1"])
                kb.op("act", lambda e: e.activation(out=y1[:, :n], in_=y1[:, :n], func=AF.Sigmoid, scale=2.0 * math.sqrt(2.0 / math.pi)),
                      reads=["yt1"], writes=["yt1"])
                kb.op("dve", lambda e, oc=oc: e.tensor_tensor(out=ysg[:, oc, :n], in0=y1[:, :n], in1=y0[:, :n], op=ALU.mult),
                      reads=["yt1", "yt0"], writes=[("ysg", oc)])
                kb.op("pool", lambda e, oc=oc: e.tensor_copy(out=ysb[:, oc, :n], in_=ysg[:, oc, :n]), reads=[("ysg", oc)], writes=[("ysb", oc)])
            for ec in range(8):
                p = self.rr("pp", 4)
                pp, kpp = self.pp[p], "pp%d" % p
                for kc in range(KC):
                    kb.op("pe", lambda e, kc=kc, ec=ec, pp=pp: e.matmul(pp[:, :n], lhsT=wg[:, kc, ec * 128:(ec + 1) * 128], rhs=ysb[:, kc, :n],
                                                                      start=(kc == 0), stop=(kc == KC - 1)),
                          reads=["wg"] + [("ysb", k) for k in range(KC)], writes=[kpp])
                zj = self.rr("zst", 2)
                kz = "zst%d" % zj
                row = O_ZS + ec * 128
                kb.dma("sp", zst[zj][:, :n], S["zT"][row:row + 128, t0:t0 + n], reads=[("zT", row, bi)], writes=[kz])
                kb.op("act", lambda e, zj=zj: e.activation(out=zst[zj][:, :n], in_=zst[zj][:, :n], func=AF.Silu), reads=[kz], writes=[kz])
                kb.op("act", lambda e, pp=pp, ec=ec: e.activation(out=yt[0][:, :n], in_=pp[:, :n], func=AF.Sigmoid, bias=gbcol[:, ec:ec + 1]),
                      reads=[kpp, "gbcol"], writes=["yt0"])
                kb.op("dve", lambda e, ec=ec: e.tensor_tensor(out=yt[0][:, :n], in0=yt[0][:, :n], in1=ysg[:, ec, :n], op=ALU.mult),
                      reads=["yt0", ("ysg", ec)], writes=["yt0"])
                aj = self.rr("aout", 2)
                kb.op("dve", lambda e, aj=aj, zj=zj: e.tensor_tensor(out=aout[aj][:, :n], in0=yt[0][:, :n], in1=zst[zj][:, :n], op=ALU.mult),
                      reads=["yt0", kz], writes=["aout%d" % aj])
                kb.dma("pool", S["act_s"][ec * 128:(ec + 1) * 128, t0:t0 + n], aout[aj][:, :n],
                       reads=["aout%d" % aj], writes=[("act", "s", bi)])
        for ri, nm in enumerate(("s5_re", "s5_im")):
            p = self.rr("ptr", 2)
            pt, kpt = self.ptr[p], "ptr%d" % p
            kb.op("pe", lambda e, pt=pt, ri=ri: e.transpose(out=pt[:32, 0:128], in_=car[ri][:, :], identity=self.ident[:, :]),
                  reads=[("car", ri, sc_) for sc_ in range(32)] + ["ident"], writes=[kpt])
            kb.op("dve", lambda e, pt=pt: e.tensor_copy(out=sst[:32, 0:128], in_=pt[:32, 0:128]), reads=[kpt], writes=["sst"])
            kb.dma("pool", O[nm + "_prompt"][l].rearrange("(s q) -> s q", q=128), sst[:32, 0:128], reads=["sst"], writes=[],
                   is_output=True)
            if NS:
                for g in range(8):
                    p = self.rr("ptr", 2)
                    pt, kpt = self.ptr[p], "ptr%d" % p
                    for s4 in range(4):
                        sc = g * 4 + s4
                        kb.op("pe", lambda e, pt=pt, ri=ri, sc=sc, s4=s4: e.transpose(out=pt[:NS, s4 * 128:(s4 + 1) * 128], in_=snw[ri][:, sc, :NS],
                                                                                   identity=self.ident[:, :]),
                              reads=[("snw", ri), "ident"], writes=[kpt])
                    kb.op("dve", lambda e, pt=pt, g=g: e.tensor_copy(out=sst[:NS, 0:512], in_=pt[:NS, :]), reads=[kpt], writes=["sst"])
                    kb.dma("pool", O[nm + "_sample"][l].rearrange("b g p -> b (g p)")[:, g * 512:(g + 1) * 512], sst[:NS, 0:512],
                           reads=["sst"], writes=[], is_output=True)
        self.end_phase()

    def load_sq(self, name, dram):
        kb = self.kb
        dst = self.wsq[name]
        for h in range(2):
            i = self.rr("w", 2)
            ws, kws = self.wst[i], "wst%d" % i
            kb.dma("sp", ws[:, :, :512], dram[:, h * 512:(h + 1) * 512].rearrange("(k p) c -> p k c", p=128),
                   writes=[kws])
            kb.op("dve", lambda e, ws=ws, h=h: e.tensor_copy(out=dst[:, :, h * 512:(h + 1) * 512], in_=ws[:, :, :512]),
                  reads=[kws], writes=[("wsq", name)])

    def phase3(self, l):
        kb = self.kb
        I, S = self.I, self.S
        last = (l == self.DEPTH - 1)
        self.begin_phase()
        self.wst = [self.sbp("wst%d" % i, (128, KC, 512)) for i in range(2)]
        self.wsq = {n: self.sbp("wsq_" + n, (128, KC, D), BF16) for n in ("bm", "br", "bs", "out")}
        self.actb = [self.sbp("actb%d" % i, (128, KC, 512), BF16) for i in range(2)]
        self.gmt = [self.sbp("gmt%d" % i, (128, 512)) for i in range(2)]
        self.mrg = self.sbp("mrg", (128, KC, 512), BF16)
        self.macc = self.sbp("macc", (128, KC, 512))
        self.ln_alloc()
        for nme in ("bm", "br", "bs", "out"):
            self.load_sq(nme, I["w_" + nme][l])
        self.load_ln_params(I["ln_g"][l], I["ln_b"][l])
        for bi, (t0, n) in enumerate(self.blocks):
            for ec in range(KC):
                for b, bn in enumerate("mrs"):
                    pass
            acts = {}
            for b, bn in enumerate("mrs"):
                if not self.have_branch(bn):
                    continue
                j = self.rr("actb", 2)
                at, kat = self.actb[j], "actb%d" % j
                kb.dma("sp", at[:, :, :n], S["act_" + bn][:, t0:t0 + n].rearrange("(k p) t -> p k t", p=128),
                       reads=[("act", bn, bi)], writes=[kat])
                for ec in range(KC):
                    p = self.rr("pp", 4)
                    pp, kpp = self.pp[p], "pp%d" % p
                    for kc in range(KC):
                        kb.op("pe", lambda e, kc=kc, pp=pp, ec=ec, at=at, bn=bn: e.matmul(
                            pp[:, :n], lhsT=self.wsq["b" + bn][:, kc, ec * 128:(ec + 1) * 128], rhs=at[:, kc, :n],
                            start=(kc == 0), stop=(kc == KC - 1)),
                            reads=[kat, ("wsq", "b" + bn)], writes=[kpp])
                    g = self.rr("gmt", 2)
                    gt, kgt = self.gmt[g], "gmt%d" % g
                    row = O_GM + b * D + ec * 128
                    kb.dma("sp", gt[:, :n], S["zT"][row:row + 128, t0:t0 + n],
                           reads=[("zT", row, bi)], writes=[kgt])
                    kb.op("act", lambda e, gt=gt: e.activation(out=gt[:, :n], in_=gt[:, :n], func=AF.Sigmoid),
                          reads=[kgt], writes=[kgt])
                    first = (bn == self.first_branch())
                    lastb = (bn == self.last_branch())
                    kmf = ("mrgf", ec)
                    acc = self.macc[:, ec, :n]
                    if first:
                        kb.op("dve", lambda e, acc=acc, gt=gt, pp=pp: e.tensor_tensor(out=acc, in0=pp[:, :n], in1=gt[:, :n],
                                                                                 op=ALU.mult),
                              reads=[kpp, kgt], writes=[("macc", ec)])
                    else:
                        kb.op("dve", lambda e, gt=gt, pp=pp: e.tensor_tensor(out=gt[:, :n], in0=pp[:, :n], in1=gt[:, :n],
                                                                        op=ALU.mult),
                              reads=[kpp, kgt], writes=[kgt])
                        kb.op("dve", lambda e, acc=acc, gt=gt: e.tensor_tensor(out=acc, in0=acc, in1=gt[:, :n], op=ALU.add),
                              reads=[kgt, ("macc", ec)], writes=[("macc", ec)])
                    if lastb:
                        kb.op("dve", lambda e, acc=acc, ec=ec: e.tensor_copy(out=self.mrg[:, ec, :n], in_=acc),
                              reads=[("macc", ec)], writes=[("mrg", ec)])
            for tt in range(0, n, 128):
                nr = min(128, n - tt)
                r0 = t0 + tt
                ti = r0 // 128
                j = self.rr("xt", 2)
                xt, kxt = self.xt[j], "xt%d" % j
                kb.dma("sp", xt[:nr, :], S["xs"][r0:r0 + nr, :], reads=[("xs", ti)], writes=[kxt])
                if self.first_branch() is not None:
                    for h in range(2):
                        p = self.rr("pp", 4)
                        pp, kpp = self.pp[p], "pp%d" % p
                        for kc in range(KC):
                            kb.op("pe", lambda e, kc=kc, pp=pp, h=h, tt=tt, nr=nr: e.matmul(
                                pp[:nr, :], lhsT=self.mrg[:, kc, tt:tt + nr], rhs=self.wsq["out"][:, kc, h * 512:(h + 1) * 512],
                                start=(kc == 0), stop=(kc == KC - 1)),
                                reads=[("mrg", k) for k in range(KC)] + [("wsq", "out")], writes=[kpp])
                        kb.op("dve", lambda e, xt=xt, pp=pp, h=h, nr=nr: e.scalar_tensor_tensor(
                            out=xt[:nr, h * 512:(h + 1) * 512], in0=xt[:nr, h * 512:(h + 1) * 512], scalar=ALPHA,
                            in1=pp[:nr, :], op0=ALU.mult, op1=ALU.add), reads=[kxt, kpp], writes=[kxt])
                else:
                    kb.op("dve", lambda e, xt=xt, nr=nr: e.tensor_scalar(out=xt[:nr, :], in0=xt[:nr, :], scalar1=ALPHA,
                                                                         scalar2=None, op0=ALU.mult),
                          reads=[kxt], writes=[kxt])
                fo = None
                if last:
                    fo = self.O["y_prompt"][r0:r0 + nr, :] if r0 < self.T else self.O["y_sample"][:, :]
                self.ln_tile(xt, kxt, r0, nr, ti, S["xs"][r0:r0 + nr, :], final_out=fo)
        self.end_phase()

    branches = ""
    NPI = 4
    CORE_BF16 = True

    def have_branch(self, bn):
        return bn in self.branches

    def first_branch(self):
        return self.branches[0] if self.branches else None

    def last_branch(self):
        return self.branches[-1] if self.branches else None


def make_in_map(inputs, c, NS, T, consts):
    m = {}
    m["x_prompt"] = np.ascontiguousarray(inputs["x_prompt"][c, :T])
    m["x_sample"] = np.ascontiguousarray(inputs["x_sample"][c * NS:(c + 1) * NS, 0])
    for n in ("ln_in_g", "ln_in_b", "w_in", "w_out", "ln_g", "ln_b", "w_bm", "w_br", "w_bs", "s_lam_re", "s_lam_im",
              "s_log_dt", "s_b_re", "s_b_im", "s_c_re", "s_c_im", "s_d", "s_glu_w", "s_glu_b",
              "m_conv_w", "m_conv_b", "m_wq", "m_wk", "m_wv", "m_ig_b", "m_fg_b", "m_norm_g", "m_skip",
              "r_mu", "r_w0", "r_w2", "r_a0", "r_a2", "r_k_k", "r_k_a", "r_r_k", "r_ln_g", "r_ln_b"):
        m[n] = np.ascontiguousarray(inputs[n])
    for n in ("state_s5_re", "state_s5_im", "state_mlstm_conv", "state_mlstm_c", "state_mlstm_n", "state_mlstm_m",
              "state_rwkv_wkv", "state_rwkv_shift"):
        m[n] = np.ascontiguousarray(inputs[n][:, c * NS:(c + 1) * NS])
    m.update(consts)
    return m


def kernel(**inputs):
    inputs = {k: np.asarray(v) for k, v in inputs.items()}
    T, NS, L = 2048, 16, 4
    Prog.branches = BRANCHES
    prog = Prog(T, NS, L)
    nc = prog.build()
    consts = host_consts()
    in_maps = [make_in_map(inputs, c, NS, T, consts) for c in range(8)]
    res = run_bass_kernel_spmd(nc, in_maps, core_ids=list(range(8)))
    rs = res.results
    f = np.float32

    def pstack(name, shape):
        if name in rs[0]:
            return np.stack([np.asarray(rs[c][name]).reshape((L,) + shape) for c in range(8)], 1).astype(f)
        return np.zeros((L, 8) + shape, f)

    def sstack(name, shape):
        if name in rs[0]:
            return np.concatenate([np.asarray(rs[c][name]).reshape((L, NS) + shape) for c in range(8)], 1).astype(f)
        return np.zeros((L, 8 * NS) + shape, f)

    y_prompt = np.stack([rs[c]["y_prompt"] for c in range(8)], 0).astype(f)
    y_sample = np.concatenate([rs[c]["y_sample"] for c in range(8)], 0)[:, None, :].astype(f)
    return (y_prompt, y_sample,
            pstack("c_prompt", (4, 256, 256)), sstack("c_sample", (4, 256, 256)),
            pstack("n_prompt", (4, 256)), sstack("n_sample", (4, 256)),
            pstack("m_prompt", (4,)), sstack("m_sample", (4,)),
            pstack("conv_prompt", (3, D)), sstack("conv_sample", (3, D)),
            pstack("wkv_prompt", (16, 64, 64)), sstack("wkv_sample", (16, 64, 64)),
            pstack("shift_prompt", (R_SHIFT_W,)), sstack("shift_sample", (R_SHIFT_W,)),
            pstack("s5_re_prompt", (64, 64)), sstack("s5_re_sample", (64, 64)),
            pstack("s5_im_prompt", (64, 64)), sstack("s5_im_sample", (64, 64)))
```

```python
import math
from contextlib import ExitStack
import numpy as np
import concourse.bass as bass
import concourse.mybir as mybir
from concourse.bass_utils import run_bass_kernel_spmd

F32 = mybir.dt.float32
BF16 = mybir.dt.bfloat16
I32 = mybir.dt.int32
ALU = mybir.AluOpType
AF = mybir.ActivationFunctionType
AX = mybir.AxisListType

D = 1024
KC = 8
N_IN = 12424
M_HEADS = 4
M_HD = 256
R_HEADS = 16
R_HD = 64
R_SHIFT_W = 3200
S_GROUPS = 64
S_STATE = 64
DEPTH_FULL = 4
ALPHA = (2.0 * DEPTH_FULL) ** 0.25
LN_EPS = 1e-5
RWKV_GN_EPS = 64e-5
NEG = -1e30

O_XM, O_IG, O_FG, O_OG, O_ZM = 0, 1024, 1028, 1032, 2056
O_RC, O_ZR, O_U, O_ZS, O_GM = 3080, 6280, 7304, 8328, 9352


class KB:
    def __init__(self, nc, es):
        self.nc = nc
        self.eng = dict(pe=nc.tensor, act=nc.scalar, dve=nc.vector, pool=nc.gpsimd, sp=nc.sync)
        self.stream = {e: [] for e in self.eng}
        self.csem = {e: es.enter_context(nc.semaphore("c_" + e)) for e in ("pe", "act", "dve", "pool")}
        self.ccnt = {e: 0 for e in self.csem}
        self.ring = {}
        for q, n in (("sp", 24), ("pool", 12), ("act", 6)):
            self.ring[q] = [[es.enter_context(nc.semaphore("d_%s%d" % (q, i))), 0] for i in range(n)]
        self.rpos = {q: 0 for q in self.ring}
        self.known = {e: {} for e in self.eng}
        self.lastw = {}
        self.readers = {}
        self.out_tokens = []
        self.n_ops = 0

    def _need(self, e, tok, same_ok=False):
        if tok is None:
            return
        sem, val, owner = tok
        if owner == e and (e == "pe" or same_ok):
            return
        k = id(sem)
        if self.known[e].get(k, 0) >= val:
            return
        self.known[e][k] = val
        self.eng[e].wait_ge(sem, val)

    ns = None
    local = frozenset()

    def _k(self, b):
        if self.ns is None:
            return b
        base = b[0] if isinstance(b, tuple) else b
        return (self.ns, b) if base in self.local else b

    def _deps(self, e, reads, writes):
        reads = [self._k(b) for b in reads]
        writes = [self._k(b) for b in writes]
        for b in reads:
            self._need(e, self.lastw.get(b))
        for b in writes:
            self._need(e, self.lastw.get(b))
            for tok in self.readers.get(b, ()):
                self._need(e, tok)

    def _commit(self, tok, reads, writes):
        reads = [self._k(b) for b in reads]
        writes = [self._k(b) for b in writes]
        for b in writes:
            self.lastw[b] = tok
            self.readers[b] = []
        for b in reads:
            self.readers.setdefault(b, []).append(tok)

    def op(self, e, fn, reads=(), writes=()):
        self._deps(e, reads, writes)
        self.ccnt[e] += 1
        tok = (self.csem[e], self.ccnt[e], e)
        fn(self.eng[e]).then_inc(self.csem[e], 1)
        self._commit(tok, reads, writes)
        self.n_ops += 1

    def dma(self, q, out, in_, reads=(), writes=(), is_output=False, **kw):
        self._deps(q, reads, writes)
        ring = self.ring[q]
        slot = ring[self.rpos[q] % len(ring)]
        self.rpos[q] += 1
        sem = slot[0]
        if slot[1] > 0:
            self._need(q, (sem, slot[1], None))
        slot[1] += 16
        tok = (sem, slot[1], None)
        self.eng[q].dma_start(out=out, in_=in_, **kw).then_inc(sem, 16)
        self._commit(tok, reads, writes)
        if is_output:
            self.out_tokens.append(tok)
        self.n_ops += 1

    def finish(self):
        for q in self.ring:
            for sem, val in self.ring[q]:
                if val > 0:
                    self._need("sp", (sem, val, None))
        for e in self.csem:
            if self.ccnt[e] > 0:
                self._need("sp", (self.csem[e], self.ccnt[e], e))

    def barrier(self):
        for e in self.eng:
            for q in self.ring:
                for sem, val in self.ring[q]:
                    if val > 0:
                        self._need(e, (sem, val, None))
            for e2 in self.csem:
                if self.ccnt[e2] > 0 and e2 != e:
                    self._need(e, (self.csem[e2], self.ccnt[e2], e2))

    def emit(self):
        pass


BRANCHES = "mrs"


def host_consts():
    c = {}
    c["ident"] = np.eye(128, dtype=np.float32)
    m3 = np.zeros((128, 4, 2), np.float32)
    m4 = np.zeros((128, 4, 8), np.float32)
    for p in range(128):
        g8 = p // 16
        gl = p // 64
        for q in range(4):
            for g in range(2):
                if g8 == 2 * q + g:
                    m3[p, q, g] = 1.0
            for gg in range(8):
                if gg == 2 * q + gl:
                    m4[p, q, gg] = 1.0
    c["mask3"], c["mask4"] = m3, m4
    selh = np.zeros((4, 4, 128), np.float32)
    for h in range(4):
        selh[h, h, :] = 1.0
    c["selh"] = selh
    c["ones4"] = np.ones((4, 128), np.float32)
    cn = np.zeros((128, 128), np.float32)
    for s_ in range(128):
        cn[s_, :s_] = 1.0e30
    c["causneg"] = cn
    rm = np.zeros((128, 384), np.float32)
    for p in range(128):
        for q in range(128):
            if p // 64 == q // 64:
                s_, t_ = p % 64, q % 64
                rm[p, q] = 1.0 if s_ < t_ else 0.0
                rm[p, 128 + q] = 1.0 if s_ <= t_ else 0.0
                rm[p, 256 + q] = 1.0 if s_ > t_ else 0.0
    c["rmasks"] = rm
    c["iota1"] = np.tile(np.arange(1, 257, dtype=np.float32)[None, :], (128, 1))
    return c


class Prog:
    def __init__(self, T, NS, DEPTH):
        self.T, self.NS, self.DEPTH = T, NS, DEPTH
        self.NT = T + NS
        assert T % 128 == 0
        self.ntile = T // 128
        self.tiles = [(i * 128, 128) for i in range(self.ntile)] + [(T, NS)]
        self.blocks = []
        t = 0
        while t < T:
            n = min(512, T - t)
            self.blocks.append((t, n))
            t += n
        self.blocks.append((T, NS))

    def build(self):
        nc = bass.Bass("TRN2", target_bir_lowering=False)
        self.nc = nc
        T, NS, L, NT = self.T, self.NS, self.DEPTH, self.NT
        dt = nc.dram_tensor

        def inp(name, shape, dtype=F32):
            return dt(name, list(shape), dtype, kind="ExternalInput").ap()

        def outp(name, shape):
            return dt(name, list(shape), F32, kind="ExternalOutput").ap()

        def scr(name, shape, dtype=F32):
            return dt(name, list(shape), dtype, kind="Internal").ap()

        I = {}
        I["x_prompt"] = inp("x_prompt", (T, D))
        I["x_sample"] = inp("x_sample", (NS, D))
        I["ident"] = inp("ident", (128, 128))
        for n, s in (("ln_in_g", (D,)), ("ln_in_b", (D,)), ("w_in", (L, D, N_IN)), ("w_out", (L, D, D)),
                     ("ln_g", (L, D)), ("ln_b", (L, D)), ("w_bm", (L, D, D)), ("w_br", (L, D, D)),
                     ("w_bs", (L, D, D)), ("s_lam_re", (L, 64, 64)), ("s_lam_im", (L, 64, 64)), ("s_log_dt", (L, 64)),
                     ("s_b_re", (L, 64, 64, 16)), ("s_b_im", (L, 64, 64, 16)), ("s_c_re", (L, 64, 16, 64)),
                     ("s_c_im", (L, 64, 16, 64)), ("s_d", (L, D)), ("s_glu_w", (L, D, D)), ("s_glu_b", (L, D)),
                     ("state_s5_re", (L, NS, 64, 64)), ("state_s5_im", (L, NS, 64, 64)),
                     ("state_mlstm_conv", (L, NS, 3, D)), ("state_mlstm_c", (L, NS, 4, 256, 256)),
                     ("state_mlstm_n", (L, NS, 4, 256)), ("state_mlstm_m", (L, NS, 4)),
                     ("m_conv_w", (L, 4, D)), ("m_conv_b", (L, D)), ("m_wq", (L, 4, 256, 256)), ("m_wk", (L, 4, 256, 256)),
                     ("m_wv", (L, 4, 256, 256)), ("m_ig_b", (L, 4)), ("m_fg_b", (L, 4)), ("m_norm_g", (L, D)), ("m_skip", (L, D)),
                     ("state_rwkv_wkv", (L, NS, 16, 64, 64)), ("state_rwkv_shift", (L, NS, R_SHIFT_W)),
                     ("r_mu", (L, R_SHIFT_W)), ("r_w0", (L, D)), ("r_w2", (L, 64, D)), ("r_a0", (L, D)), ("r_a2", (L, 64, D)),
                     ("r_k_k", (L, D)), ("r_k_a", (L, D)), ("r_r_k", (L, 16, 64)), ("r_ln_g", (L, D)), ("r_ln_b", (L, D)),
                     ("rmasks", (128, 384)),
                     ("selh", (4, 4, 128)), ("causneg", (128, 128)), ("ones4", (4, 128)),
                     ("mask3", (128, 4, 2)), ("mask4", (128, 4, 8)), ("iota1", (128, 256))):
            I[n] = inp(n, s)
        self.I = I
        O = {}
        O["y_prompt"] = outp("y_prompt", (T, D))
        O["y_sample"] = outp("y_sample", (NS, D))
        O["c_prompt"] = outp("c_prompt", (L, 4, 256, 256))
        O["c_sample"] = outp("c_sample", (L, NS, 4, 256, 256))
        O["n_prompt"] = outp("n_prompt", (L, 4, 256))
        O["n_sample"] = outp("n_sample", (L, NS, 4, 256))
        O["m_prompt"] = outp("m_prompt", (L, 4))
        O["m_sample"] = outp("m_sample", (L, NS, 4))
        O["conv_prompt"] = outp("conv_prompt", (L, 3, D))
        O["conv_sample"] = outp("conv_sample", (L, NS, 3, D))
        O["wkv_prompt"] = outp("wkv_prompt", (L, 16, 64, 64))
        O["wkv_sample"] = outp("wkv_sample", (L, NS, 16, 64, 64))
        O["shift_prompt"] = outp("shift_prompt", (L, R_SHIFT_W))
        O["shift_sample"] = outp("shift_sample", (L, NS, R_SHIFT_W))
        for nm in ("s5_re", "s5_im"):
            O[nm + "_prompt"] = outp(nm + "_prompt", (L, 4096))
            O[nm + "_sample"] = outp(nm + "_sample", (L, NS, 64, 64))
        self.O = O
        S = {}
        S["xs"] = scr("xs", (NT, D))
        S["z"] = scr("z", (NT, N_IN))
        S["zT"] = scr("zT", (N_IN, NT))
        for b in "mrs":
            S["act_" + b] = scr("act_" + b, (D, NT), BF16)
        S["vs"] = scr("vs", (NT, D))
        S["ys"] = scr("ys", (NT, D))
        self.S = S

        with ExitStack() as es:
            self.es = es
            kb = KB(nc, es)
            self.kb = kb
            self.alloc()
            self.phase0()
            for l in range(L):
                self.phase1(l)
                self.easy_states(l)
                if self.have_branch("m"):
                    self.mlstm_phase(l)
                if self.have_branch("r"):
                    self.rwkv_phase(l)
                if self.have_branch("s"):
                    self.s5_phase(l)
                self.phase3(l)
            kb.finish()
            kb.emit()
        return nc

    def sb(self, name, shape, dtype=F32):
        return self.es.enter_context(self.nc.sbuf_tensor("sb_" + name, list(shape), dtype))

    def sbp(self, name, shape, dtype=F32):
        self.uid = getattr(self, "uid", 0) + 1
        return self.pes.enter_context(self.nc.sbuf_tensor("sp%d_%s" % (self.uid, name), list(shape), dtype))

    def begin_phase(self):
        self.pes = ExitStack()

    def end_phase(self):
        self.kb.barrier()
        self.pes.close()

    def ps(self, name, shape, dtype=F32):
        return self.es.enter_context(self.nc.psum_tensor("ps_" + name, list(shape), dtype))

    def alloc(self):
        NT = self.NT
        self.ident = self.sb("ident", (128, 128))
        self.xT = self.sb("xT", (128, KC, NT), BF16)
        self.pp = [self.ps("pp%d" % i, (128, 512)) for i in range(4)]
        self.ptr = [self.ps("ptr%d" % i, (128, 512)) for i in range(2)]
        self.pex = [self.ps("pex%d" % i, (128, 512)) for i in range(2)]
        self.cnt = {}
        kb = self.kb
        kb.dma("sp", self.ident[:], self.I["ident"], writes=["ident"])

    def rr(self, key, n):
        v = self.cnt.get(key, 0)
        self.cnt[key] = v + 1
        return v % n

    def ln_alloc(self):
        self.gbc = self.sbp("gbc", (128, D))
        self.bbc = self.sbp("bbc", (128, D))
        self.xt = [self.sbp("xt%d" % i, (128, D)) for i in range(2)]
        self.xc = [self.sbp("xc%d" % i, (128, D)) for i in range(2)]
        self.st = [self.sbp("st%d" % i, (128, 8)) for i in range(2)]

    def ln_tile(self, src, srckey, row0, nr, ti, xs_out, final_out=None):
        kb = self.kb
        i = self.rr("ln", 2)
        xc, st = self.xc[i], self.st[i]
        kxc, kst = "xc%d" % i, "st%d" % i
        kb.op("dve", lambda e: e.tensor_reduce(out=st[:nr, 0:1], in_=src[:nr, :], axis=AX.X, op=ALU.add),
              reads=[srckey], writes=[kst])
        kb.op("dve", lambda e: e.tensor_scalar(out=st[:nr, 1:2], in0=st[:nr, 0:1], scalar1=-1.0 / D, scalar2=None,
                                               op0=ALU.mult), reads=[kst], writes=[kst])
        kb.op("dve", lambda e: e.tensor_scalar(out=xc[:nr, :], in0=src[:nr, :], scalar1=st[:nr, 1:2], scalar2=None,
                                               op0=ALU.add), reads=[srckey, kst], writes=[kxc])
        j = self.rr("tmpsq", 2)
        sq = self.xt[j]
        ksq = "xt%d" % j
        kb.op("act", lambda e: e.activation(out=sq[:nr, :], in_=xc[:nr, :], func=AF.Square, accum_out=st[:nr, 2:3]),
              reads=[kxc], writes=[ksq, kst])
        kb.op("act", lambda e: e.activation(out=st[:nr, 3:4], in_=st[:nr, 2:3], func=AF.Sqrt, scale=1.0 / D,
                                            bias=self.epsc[:nr, 0:1]), reads=[kst, "epsc"], writes=[kst])
        kb.op("dve", lambda e: e.reciprocal(out=st[:nr, 4:5], in_=st[:nr, 3:4]), reads=[kst], writes=[kst])
        kb.op("dve", lambda e: e.scalar_tensor_tensor(out=xc[:nr, :], in0=xc[:nr, :], scalar=st[:nr, 4:5],
                                                      in1=self.gbc[:nr, :], op0=ALU.mult, op1=ALU.mult),
              reads=[kxc, kst, "gbc"], writes=[kxc])
        kb.op("dve", lambda e: e.tensor_tensor(out=xc[:nr, :], in0=xc[:nr, :], in1=self.bbc[:nr, :], op=ALU.add),
              reads=[kxc, "bbc"], writes=[kxc])
        kb.dma("pool", xs_out, xc[:nr, :], reads=[kxc], writes=[("xs", ti)])
        if final_out is not None:
            kb.dma("pool", final_out, xc[:nr, :], reads=[kxc], writes=[], is_output=True)
        for half in range(2):
            p = self.rr("ptr", 2)
            pt, kpt = self.ptr[p], "ptr%d" % p
            for k4 in range(4):
                kc = half * 4 + k4
                kb.op("pe", lambda e, kc=kc, k4=k4, pt=pt: e.transpose(out=pt[:, k4 * 128:k4 * 128 + nr],
                                                                      in_=xc[:nr, kc * 128:(kc + 1) * 128],
                                                                      identity=self.ident[:nr, :nr]),
                      reads=[kxc, "ident"], writes=[kpt])
            dst = self.xT[:, half * 4:half * 4 + 4, row0:row0 + nr]
            srcp = pt[:, :].rearrange("p (k t) -> p k t", k=4)[:, :, :nr]
            eng = "act" if half == 0 else "dve"
            if eng == "act":
                kb.op("act", lambda e, dst=dst, srcp=srcp: e.activation(out=dst, in_=srcp, func=AF.Copy),
                      reads=[kpt], writes=[("xT", ti)])
            else:
                kb.op("dve", lambda e, dst=dst, srcp=srcp: e.tensor_copy(out=dst, in_=srcp),
                      reads=[kpt], writes=[("xT", ti)])

    def load_ln_params(self, g_ap, b_ap):
        kb = self.kb
        kb.dma("sp", self.gbc[:], g_ap.partition_broadcast(128), writes=["gbc"])
        kb.dma("sp", self.bbc[:], b_ap.partition_broadcast(128), writes=["bbc"])

    def phase0(self):
        kb = self.kb
        self.epsc = self.sb("epsc", (128, 1))
        kb.op("dve", lambda e: e.memset(self.epsc[:], LN_EPS), writes=["epsc"])
        self.onesf = self.sb("onesf", (128, 64))
        kb.op("dve", lambda e: e.memset(self.onesf[:], 1.0), writes=["onesf"])
        self.begin_phase()
        self.ln_alloc()
        self.load_ln_params(self.I["ln_in_g"], self.I["ln_in_b"])
        for ti, (r0, nr) in enumerate(self.tiles):
            j = self.rr("xt", 2)
            xt, kxt = self.xt[j], "xt%d" % j
            src = self.I["x_prompt"][r0:r0 + nr, :] if r0 < self.T else self.I["x_sample"][:, :]
            kb.dma("sp", xt[:nr, :], src, writes=[kxt])
            self.ln_tile(xt, kxt, r0, nr, ti, self.S["xs"][r0:r0 + nr, :])
        self.end_phase()

    def load_w(self, dram_cols, width):
        kb = self.kb
        i = self.rr("w", 2)
        ws, wb = self.wst[i], self.wbf[i]
        kws, kwb = "wst%d" % i, "wbf%d" % i
        kb.dma("sp", ws[:, :, :width], dram_cols.rearrange("(k p) c -> p k c", p=128), writes=[kws])
        if self.rr("wcast", 2) == 0:
            kb.op("dve", lambda e: e.tensor_copy(out=wb[:, :, :width], in_=ws[:, :, :width]), reads=[kws], writes=[kwb])
        else:
            kb.op("act", lambda e: e.activation(out=wb[:, :, :width], in_=ws[:, :, :width], func=AF.Copy), reads=[kws], writes=[kwb])
        return wb, kwb

    def evac(self, dst, src, reads, writes):
        kb = self.kb
        if self.rr("evac", 2) == 0:
            kb.op("act", lambda e: e.activation(out=dst, in_=src, func=AF.Copy), reads=reads, writes=writes)
        else:
            kb.op("dve", lambda e: e.tensor_copy(out=dst, in_=src), reads=reads, writes=writes)

    def phase1(self, l):
        kb = self.kb
        self.begin_phase()
        self.wst = [self.sbp("wst%d" % i, (128, KC, 512)) for i in range(2)]
        self.wbf = [self.sbp("wbf%d" % i, (128, KC, 512), BF16) for i in range(2)]
        self.ev = [self.sbp("ev%d" % i, (128, 512)) for i in range(4)]
        W = self.I["w_in"][l]
        tm_segs = [(O_OG, O_ZM), (O_RC, O_U)]
        fm_segs = [(O_XM, O_IG), (O_IG, O_FG), (O_FG, O_OG), (O_ZM, O_RC), (O_U, O_ZS), (O_ZS, O_GM), (O_GM, N_IN)]
        for (c0, c1) in tm_segs:
            c = c0
            while c < c1:
                w = min(512, c1 - c)
                wb, kwb = self.load_w(W[:, c:c + w], w)
                for ti, (r0, nr) in enumerate(self.tiles):
                    p = self.rr("pp", 4)
                    pp, kpp = self.pp[p], "pp%d" % p
                    for kc in range(KC):
                        kb.op("pe", lambda e, kc=kc, pp=pp, r0=r0, nr=nr, wb=wb, w=w: e.matmul(
                            pp[:nr, :w], lhsT=self.xT[:, kc, r0:r0 + nr], rhs=wb[:, kc, :w],
                            start=(kc == 0), stop=(kc == KC - 1)),
                            reads=[("xT", ti), kwb], writes=[kpp])
                    v = self.rr("ev", 4)
                    ev, kev = self.ev[v], "ev%d" % v
                    self.evac(ev[:nr, :w], pp[:nr, :w], [kpp], [kev])
                    kb.dma("pool", self.S["z"][r0:r0 + nr, c:c + w], ev[:nr, :w], reads=[kev],
                           writes=[("z", ti)])
                c += w
        for (c0, c1) in fm_segs:
            c = c0
            while c < c1:
                w = min(512, c1 - c)
                wb, kwb = self.load_w(W[:, c:c + w], w)
                for s0 in range(0, w, 128):
                    m = min(128, w - s0)
                    for bi, (t0, n) in enumerate(self.blocks):
                        p = self.rr("pp", 4)
                        pp, kpp = self.pp[p], "pp%d" % p
                        tis = list(range(t0 // 128, (t0 + n + 127) // 128))
                        for kc in range(KC):
                            kb.op("pe", lambda e, kc=kc, pp=pp, t0=t0, n=n, wb=wb, s0=s0, m=m: e.matmul(
                                pp[:m, :n], lhsT=wb[:, kc, s0:s0 + m], rhs=self.xT[:, kc, t0:t0 + n],
                                start=(kc == 0), stop=(kc == KC - 1)),
                                reads=[("xT", t) for t in tis] + [kwb], writes=[kpp])
                        v = self.rr("ev", 4)
                        ev, kev = self.ev[v], "ev%d" % v
                        self.evac(ev[:m, :n], pp[:m, :n], [kpp], [kev])
                        kb.dma("pool", self.S["zT"][c + s0:c + s0 + m, t0:t0 + n], ev[:m, :n], reads=[kev],
                               writes=[("zT", c + s0, bi)])
                c += w
        self.end_phase()

    def easy_states(self, l):
        kb = self.kb
        I, S, O = self.I, self.S, self.O
        T, NS = self.T, self.NS
        nb = len(self.blocks)
        zt_all = [("zT", r, b) for r in range(0, 1024, 128) for b in range(nb)]
        z_all = [("z", ti) for ti in range(len(self.tiles))]
        kb.dma("pool", O["conv_prompt"][l].rearrange("j c -> c j"), S["zT"][0:D, T - 3:T], reads=zt_all, writes=[],
               is_output=True, allow_slow_non_contiguous=True)
        kb.dma("pool", O["shift_prompt"][l:l + 1, :], S["z"][T - 1:T, O_RC:O_RC + R_SHIFT_W], reads=z_all, writes=[], is_output=True)
        if NS:
            kb.dma("pool", O["conv_sample"][l][:, 0:2, :], I["state_mlstm_conv"][l][:, 1:3, :], writes=[], is_output=True)
            for hh in range(4):
                kb.dma("pool", O["conv_sample"][l][:, 2, hh * 256:(hh + 1) * 256].rearrange("b c -> c b"),
                       S["zT"][hh * 256:(hh + 1) * 256, T:T + NS], reads=zt_all, writes=[],
                       is_output=True, allow_slow_non_contiguous=True)
            kb.dma("pool", O["shift_sample"][l], S["z"][T:T + NS, O_RC:O_RC + R_SHIFT_W], reads=z_all, writes=[], is_output=True)

    def mlstm_phase(self, l):
        kb = self.kb
        I, S, O = self.I, self.S, self.O
        T, NS, NT = self.T, self.NS, self.NT
        self.begin_phase()
        sbp = self.sbp
        Wst = sbp("Wst", (128, 4, 2, 256))
        Wq = sbp("Wq", (128, 4, 2, 256), BF16); Wk = sbp("Wk", (128, 4, 2, 256), BF16); Wv = sbp("Wv", (128, 4, 2, 256), BF16)
        cw = sbp("cw", (128, 4, 8)); cb = sbp("cb", (128, 8)); mg = sbp("mg", (128, 8)); msk = sbp("msk", (128, 8))
        gb = sbp("gb", (4, 2))
        igA = sbp("igA", (4, NT)); lfA = sbp("lfA", (4, NT))
        selh = sbp("selh", (4, 4, 128)); causneg = sbp("causneg", (128, 128)); ones4 = sbp("ones4", (4, 128))
        Cst = [[sbp("C%d_%d" % (s_, h), (128, 2, 257)) for h in range(4)] for s_ in range(3)]

        def mk_set(tag, Lm):
            W = {"tag": tag}
            W["mprev"] = sbp(tag + "mprev", (4, 2))
            W["xext"] = [sbp(tag + "xext%d" % i, (128, 8, Lm + 3)) for i in range(2)]
            W["ctmp"] = sbp(tag + "cvtmp", (128, 8, Lm)); W["cacc"] = sbp(tag + "cvacc", (128, 8, Lm))
            W["xcT"] = sbp(tag + "xcT", (128, 8, Lm)); W["xcb"] = sbp(tag + "xcb", (128, 8, Lm), BF16); W["xmb"] = sbp(tag + "xmb", (128, 8, Lm), BF16)
            W["qTb"] = sbp(tag + "qTb", (128, 4, 2, Lm)); W["kTb"] = sbp(tag + "kTb", (128, 4, 2, Lm))
            W["kw"] = sbp(tag + "kw", (128, 256)); W["vaug"] = sbp(tag + "vaug", (128, 257))
            W["G"] = sbp(tag + "G", (4, 8, Lm)); W["gsm"] = sbp(tag + "gsm", (4, 8)); W["dg"] = sbp(tag + "dg", (4, 4))
            W["gc"] = sbp(tag + "gc", (128, 16)); W["dbc"] = sbp(tag + "dbc", (128, 4))
            W["DT"] = sbp(tag + "DT", (128, Lm)); W["Stl"] = sbp(tag + "Stl", (128, Lm)); W["mmb"] = sbp(tag + "mmb", (128, 4, Lm))
            W["Asb"] = sbp(tag + "Asb", (128, 257)); W["nd"] = sbp(tag + "nd", (128, 257)); W["dsm"] = sbp(tag + "dsm", (128, 4))
            W["sog"] = [sbp(tag + "sog%d" % i, (128, D)) for i in range(2 if Lm > 1 else 1)]
            W["hm"] = sbp(tag + "hm", (128, D)); W["hst"] = sbp(tag + "hst", (128, 16))
            W["hT"] = sbp(tag + "hT", (128, 8, Lm)); W["zmt"] = [sbp(tag + "zmt%d" % i, (128, 8, Lm)) for i in range(2)]
            W["aob"] = [sbp(tag + "aob%d" % i, (128, 8, Lm), BF16) for i in range(2)]
            return W
        WP = mk_set("P", 128)
        WS_ = mk_set("S", 1) if NS else None
        kb.local = frozenset(["mprev", "mnew", "xext0", "xext1", "cvacc", "cvtmp", "xcT", "xcb", "xmb", "G0", "G1", "G3", "G4", "G5", "G6", "gsm", "gc", "dg",
                              "dbc", "mmb", "DT", "Stl", "kw", "vaug", "Asb", "nd", "dsm", "sog0", "sog1", "hst", "zmt0", "zmt1", "aob0", "aob1",
                              "hm", "hT", "qTb", "kTb"])
        cvst = sbp("cvst", (48, D)); convT = sbp("convT", (128, 8, 48))

        for (nm, dst, sc_) in (("m_wq", Wq, 1.0 / 16.0), ("m_wk", Wk, 1.0), ("m_wv", Wv, 1.0)):
            kb.dma("sp", Wst[:], I[nm][l].rearrange("h (c p) e -> p h c e", p=128), writes=["Wst"])
            kb.op("act", lambda e, dst=dst, sc_=sc_: e.activation(out=dst[:], in_=Wst[:], func=AF.Copy, scale=sc_),
                  reads=["Wst"], writes=[nm])
        sl = dict(allow_slow_non_contiguous=True)
        kb.dma("sp", cw[:], I["m_conv_w"][l].rearrange("j (k p) -> p j k", p=128), writes=["cw"], **sl)
        kb.dma("sp", cb[:], I["m_conv_b"][l].rearrange("(k p) -> p k", p=128), writes=["cb"], **sl)
        kb.dma("sp", mg[:], I["m_norm_g"][l].rearrange("(k p) -> p k", p=128), writes=["mg"], **sl)
        kb.dma("sp", msk[:], I["m_skip"][l].rearrange("(k p) -> p k", p=128), writes=["msk"], **sl)
        kb.dma("sp", gb[:, 0:1], I["m_ig_b"][l].rearrange("(h o) -> h o", o=1), writes=["gb"], **sl)
        kb.dma("sp", gb[:, 1:2], I["m_fg_b"][l].rearrange("(h o) -> h o", o=1), writes=["gb"], **sl)
        kb.dma("sp", selh[:], I["selh"], writes=["selh"])
        kb.dma("sp", causneg[:], I["causneg"], writes=["causneg"])
        kb.dma("sp", ones4[:], I["ones4"], writes=["ones4"])
        nb = len(self.blocks)
        kb.dma("sp", igA[:], S["zT"][O_IG:O_IG + 4, :], reads=[("zT", O_IG, b) for b in range(nb)], writes=["igA"])
        kb.dma("sp", lfA[:], S["zT"][O_FG:O_FG + 4, :], reads=[("zT", O_FG, b) for b in range(nb)], writes=["lfA"])
        kb.op("dve", lambda e: e.tensor_scalar(out=igA[:], in0=igA[:], scalar1=gb[:, 0:1], scalar2=None, op0=ALU.add),
              reads=["igA", "gb"], writes=["igA"])
        kb.op("dve", lambda e: e.tensor_scalar(out=lfA[:], in0=lfA[:], scalar1=gb[:, 1:2], scalar2=-1.0, op0=ALU.add, op1=ALU.mult),
              reads=["lfA", "gb"], writes=["lfA"])
        kb.op("act", lambda e: e.activation(out=lfA[:], in_=lfA[:], func=AF.Exp), reads=["lfA"], writes=["lfA"])
        kb.op("dve", lambda e: e.tensor_scalar(out=lfA[:], in0=lfA[:], scalar1=1.0, scalar2=None, op0=ALU.add), reads=["lfA"], writes=["lfA"])
        kb.op("act", lambda e: e.activation(out=lfA[:], in_=lfA[:], func=AF.Ln), reads=["lfA"], writes=["lfA"])
        kb.op("dve", lambda e: e.tensor_scalar(out=lfA[:], in0=lfA[:], scalar1=-1.0, scalar2=None, op0=ALU.mult), reads=["lfA"], writes=["lfA"])
        if NS:
            kb.dma("sp", cvst[:3 * NS, :], I["state_mlstm_conv"][l].rearrange("b j c -> (b j) c"), writes=["cvst"])
            for half in range(2):
                p = self.rr("ptr", 2)
                pt, kpt = self.ptr[p], "ptr%d" % p
                for k4 in range(4):
                    kc = half * 4 + k4
                    kb.op("pe", lambda e, pt=pt, k4=k4, kc=kc: e.transpose(out=pt[:, k4 * 48:k4 * 48 + 3 * NS], in_=cvst[:3 * NS, kc * 128:(kc + 1) * 128],
                                                                       identity=self.ident[:3 * NS, :3 * NS]),
                          reads=["cvst", "ident"], writes=[kpt])
                kb.op("dve", lambda e, pt=pt, half=half: e.tensor_copy(out=convT[:, half * 4:half * 4 + 4, :3 * NS],
                                                                     in_=pt[:, 0:192].rearrange("p (k c) -> p k c", k=4)[:, :, :3 * NS]),
                      reads=[kpt], writes=["convT"])

        zt_x = lambda bi: [("zT", r, bi) for r in range(0, 1024, 128)]
        zt_z = lambda bi: [("zT", O_ZM + r, bi) for r in range(0, 1024, 128)]
        blk_of = lambda t: next(i for i, (b0, bn) in enumerate(self.blocks) if b0 <= t < b0 + bn)

        chunks = [(c * 128, 128, None) for c in range(T // 128)] + [(T + b, 1, b) for b in range(NS)]
        def chunk_gen(ci, t0, L, sb_, W):
            mprev = W["mprev"]; xext = W["xext"]; ctmp = W["ctmp"]; cacc = W["cacc"]; xcT = W["xcT"]; xcb = W["xcb"]; xmb = W["xmb"]
            qTb = W["qTb"]; kTb = W["kTb"]; kw = W["kw"]; vaug = W["vaug"]; G = W["G"]; gsm = W["gsm"]; dg = W["dg"]; gc = W["gc"]; dbc = W["dbc"]
            DT = W["DT"]; Stl = W["Stl"]; mmb = W["mmb"]; Asb = W["Asb"]; nd = W["nd"]; dsm = W["dsm"]; sog = W["sog"]; hm = W["hm"]; hst = W["hst"]
            hT = W["hT"]; zmt = W["zmt"]; aob = W["aob"]
            bi = blk_of(t0)
            ti = t0 // 128
            cs = (1 + self.rr("Cset", 2)) if sb_ is not None else 0
            C = Cst[cs]
            kC = [("C", cs, h) for h in range(4)]
            if ci == 0:
                for h in range(4):
                    kb.op("pool", lambda e, h=h: e.memset(C[h][:], 0.0), writes=[kC[h]])
                kb.op("dve", lambda e: e.memset(mprev[:, 0:1], NEG), writes=["mprev"])
            if sb_ is not None:
                for h in range(4):
                    kb.dma("sp", C[h][:, :, 0:256], I["state_mlstm_c"][l, sb_, h].rearrange("(c p) v -> p c v", p=128), writes=[kC[h]])
                    kb.dma("sp", C[h][:, :, 256:257], I["state_mlstm_n"][l, sb_, h].rearrange("(c p o) -> p c o", p=128, o=1), writes=[kC[h]], **sl)
                kb.dma("sp", mprev[:, 0:1], I["state_mlstm_m"][l, sb_].rearrange("(h o) -> h o", o=1), writes=["mprev"], **sl)
            xj = self.rr("xext" + W["tag"], 2)
            xe, kxe = xext[xj], "xext%d" % xj
            if sb_ is None:
                if t0 == 0:
                    kb.op("pool", lambda e, xe=xe: e.memset(xe[:, :, 0:3], 0.0), writes=[kxe])
                    kb.dma("sp", xe[:, :, 3:3 + L], S["zT"][0:D, t0:t0 + L].rearrange("(k p) t -> p k t", p=128), reads=zt_x(bi), writes=[kxe])
                else:
                    rd = zt_x(bi) + (zt_x(blk_of(t0 - 3)) if blk_of(t0 - 3) != bi else [])
                    kb.dma("sp", xe[:, :, 0:3 + L], S["zT"][0:D, t0 - 3:t0 + L].rearrange("(k p) t -> p k t", p=128), reads=rd, writes=[kxe])
            else:
                kb.op("pool", lambda e, xe=xe, sb_=sb_: e.tensor_copy(out=xe[:, :, 0:3], in_=convT[:, :, 3 * sb_:3 * sb_ + 3]), reads=["convT"], writes=[kxe])
                kb.dma("sp", xe[:, :, 3:4], S["zT"][0:D, t0:t0 + 1].rearrange("(k p) t -> p k t", p=128), reads=zt_x(bi), writes=[kxe], **sl)
            V = lambda x: x[:, :, :L]
            for j in range(4):
                dst = cacc if j == 0 else ctmp
                kd = "cvacc" if j == 0 else "cvtmp"
                kb.op("dve", lambda e, j=j, dst=dst, xe=xe: e.tensor_tensor(out=V(dst), in0=xe[:, :, j:j + L],
                                                                         in1=cw[:, j, :, None].to_broadcast([128, 8, L]), op=ALU.mult),
                      reads=[kxe, "cw"], writes=[kd])
                if j > 0:
                    kb.op("dve", lambda e: e.tensor_tensor(out=V(cacc), in0=V(cacc), in1=V(ctmp), op=ALU.add), reads=["cvacc", "cvtmp"], writes=["cvacc"])
            kb.op("dve", lambda e: e.tensor_tensor(out=V(cacc), in0=V(cacc), in1=cb[:, :, None].to_broadcast([128, 8, L]), op=ALU.add),
                  reads=["cvacc", "cb"], writes=["cvacc"])
            kb.op("act", lambda e: e.activation(out=V(xcT), in_=V(cacc), func=AF.Silu), reads=["cvacc"], writes=["xcT"])
            kb.op("dve", lambda e: e.tensor_copy(out=V(xcb), in_=V(xcT)), reads=["xcT"], writes=["xcb"])
            kb.op("dve", lambda e, xe=xe: e.tensor_copy(out=V(xmb), in_=xe[:, :, 3:3 + L]), reads=[kxe], writes=["xmb"])
            yield
            Gr = lambda i: G[:, i, :L]
            kb.op("dve", lambda e: e.tensor_tensor_scan(out=Gr(0), data0=ones4[:, :L], data1=lfA[:, t0:t0 + L], initial=0.0, op0=ALU.mult, op1=ALU.add),
                  reads=["ones4", "lfA"], writes=["G0"])
            kb.op("dve", lambda e: e.tensor_tensor(out=Gr(3), in0=igA[:, t0:t0 + L], in1=Gr(0), op=ALU.subtract), reads=["igA", "G0"], writes=["G3"])
            kb.op("dve", lambda e: e.tensor_tensor_scan(out=Gr(1), data0=Gr(3), data1=Gr(3), initial=-3.0e38, op0=ALU.max, op1=ALU.max),
                  reads=["G3"], writes=["G1"])
            kb.op("dve", lambda e: e.tensor_scalar(out=Gr(1), in0=Gr(1), scalar1=mprev[:, 0:1], scalar2=None, op0=ALU.max), reads=["G1", "mprev"], writes=["G1"])
            kb.op("dve", lambda e: e.tensor_scalar(out=gsm[:, 0:1], in0=G[:, 1, L - 1:L], scalar1=-1.0, scalar2=None, op0=ALU.mult), reads=["G1"], writes=["gsm"])
            kb.op("act", lambda e: e.activation(out=Gr(4), in_=Gr(1), func=AF.Exp, scale=-1.0, bias=mprev[:, 0:1]), reads=["G1", "mprev"], writes=["G4"])
            kb.op("dve", lambda e: e.tensor_tensor(out=Gr(5), in0=Gr(0), in1=Gr(1), op=ALU.add), reads=["G0", "G1"], writes=["G5"])
            kb.op("act", lambda e: e.activation(out=Gr(5), in_=Gr(5), func=AF.Exp, scale=-1.0), reads=["G5"], writes=["G5"])
            kb.op("act", lambda e: e.activation(out=Gr(6), in_=Gr(3), func=AF.Exp, bias=gsm[:, 0:1]), reads=["G3", "gsm"], writes=["G6"])
            kb.op("dve", lambda e: e.tensor_tensor(out=mprev[:, 1:2], in0=G[:, 0, L - 1:L], in1=G[:, 1, L - 1:L], op=ALU.add), reads=["G0", "G1"], writes=["mnew"])
            p = self.rr("ptr", 2)
            pt, kpt = self.ptr[p], "ptr%d" % p
            for ki, gi in enumerate((4, 5, 3, 6)):
                kb.op("pe", lambda e, ki=ki, gi=gi, pt=pt: e.transpose(out=pt[:L, ki * 4:ki * 4 + 4], in_=G[:, gi, :L], identity=self.ident[:4, :4]),
                      reads=["G%d" % gi, "ident"], writes=[kpt])
            kb.op("dve", lambda e, pt=pt: e.tensor_copy(out=gc[:L, :], in_=pt[:L, 0:16]), reads=[kpt], writes=["gc"])
            kb.op("dve", lambda e: e.tensor_scalar(out=dg[:, :], in0=self.ident[:4, :4], scalar1=G[:, 4, L - 1:L], scalar2=None, op0=ALU.mult),
                  reads=["G4", "ident"], writes=["dg"])
            p2 = self.rr("ptr", 2)
            pt2, kpt2 = self.ptr[p2], "ptr%d" % p2
            kb.op("pe", lambda e, pt2=pt2: e.matmul(pt2[:, 0:4], lhsT=ones4[:, :], rhs=dg[:, :], start=True, stop=True), reads=["ones4", "dg"], writes=[kpt2])
            kb.op("dve", lambda e, pt2=pt2: e.tensor_copy(out=dbc[:, :], in_=pt2[:, 0:4]), reads=[kpt2], writes=["dbc"])
            pnn = self.rr("pp", 4)
            pn, kpn = self.pp[pnn], "pp%d" % pnn
            for h in range(4):
                kb.op("pe", lambda e, h=h, pn=pn: e.matmul(pn[:L, h * 128:h * 128 + L], lhsT=selh[:, h, :L], rhs=G[:, 1, :L], start=True, stop=True),
                      reads=["selh", "G1"], writes=[kpn])
            kb.op("act", lambda e, pn=pn: e.activation(out=mmb[:L, :, :L], in_=pn[:L, :].rearrange("p (h t) -> p h t", h=4)[:, :, :L], func=AF.Copy),
                  reads=[kpn], writes=["mmb"])
            sj = self.rr("sog" + W["tag"], len(sog))
            so, kso = sog[sj], "sog%d" % sj
            kb.dma("sp", so[:L, :], S["z"][t0:t0 + L, O_OG:O_OG + D], reads=[("z", ti)], writes=[kso])
            kb.op("act", lambda e, so=so: e.activation(out=so[:L, :], in_=so[:L, :], func=AF.Sigmoid), reads=[kso], writes=[kso])
            yield
            for h in range(4):
                for (Wt, wn, dstT, kd) in ((Wq, "m_wq", qTb, "qTb"), (Wk, "m_wk", kTb, "kTb")):
                    pq = self.rr("pp", 4)
                    ppq, kpq = self.pp[pq], "pp%d" % pq
                    for ec in range(2):
                        for dc in range(2):
                            kb.op("pe", lambda e, h=h, ec=ec, dc=dc, Wt=Wt, ppq=ppq: e.matmul(
                                ppq[:, ec * 128:ec * 128 + L], lhsT=Wt[:, h, dc, ec * 128:(ec + 1) * 128], rhs=xcb[:, 2 * h + dc, :L],
                                start=(dc == 0), stop=(dc == 1)), reads=[wn, "xcb"], writes=[kpq])
                    self.evac(dstT[:, h, :, :L], ppq[:, 0:256].rearrange("p (c t) -> p c t", c=2)[:, :, :L], [kpq], [(kd, h)])
            for h in range(4):
                yield
                pk = self.rr("pp", 4)
                ppk, kpk = self.pp[pk], "pp%d" % pk
                for dc in range(2):
                    kb.op("pe", lambda e, h=h, dc=dc, ppk=ppk: e.matmul(ppk[:L, 0:256], lhsT=xcb[:, 2 * h + dc, :L], rhs=Wk[:, h, dc, :],
                                                                      start=(dc == 0), stop=(dc == 1)), reads=["m_wk", "xcb"], writes=[kpk])
                kb.op("act", lambda e, h=h, ppk=ppk: e.activation(out=kw[:L, :], in_=ppk[:L, 0:256], func=AF.Copy, scale=gc[:L, 12 + h:13 + h]),
                      reads=[kpk, "gc"], writes=["kw"])
                pv = self.rr("pp", 4)
                ppv, kpv = self.pp[pv], "pp%d" % pv
                for dc in range(2):
                    kb.op("pe", lambda e, h=h, dc=dc, ppv=ppv: e.matmul(ppv[:L, 0:256], lhsT=xmb[:, 2 * h + dc, :L], rhs=Wv[:, h, dc, :],
                                                                      start=(dc == 0), stop=(dc == 1)), reads=["m_wv", "xmb"], writes=[kpv])
                kb.op("dve", lambda e, ppv=ppv: e.tensor_copy(out=vaug[:L, 0:256], in_=ppv[:L, 0:256]), reads=[kpv], writes=["vaug"])
                kb.op("dve", lambda e: e.memset(vaug[:L, 256:257], 1.0), writes=["vaug"])
                kb.op("dve", lambda e, h=h: e.tensor_tensor(out=DT[:L, :L], in0=mmb[:L, h, :L], in1=causneg[:L, :L], op=ALU.add),
                      reads=["mmb", "causneg"], writes=["DT"])
                kb.op("act", lambda e, h=h: e.activation(out=DT[:L, :L], in_=DT[:L, :L], func=AF.Exp, scale=-1.0, bias=gc[:L, 8 + h:9 + h]),
                      reads=["DT", "gc"], writes=["DT"])
                ps_ = self.rr("pp", 4)
                pps, kps = self.pp[ps_], "pp%d" % ps_
                for ec in range(2):
                    kb.op("pe", lambda e, h=h, ec=ec, pps=pps: e.matmul(pps[:L, :L], lhsT=kTb[:, h, ec, :L], rhs=qTb[:, h, ec, :L],
                                                                      start=(ec == 0), stop=(ec == 1)), reads=[("kTb", h), ("qTb", h)], writes=[kps])
                kb.op("dve", lambda e, pps=pps: e.tensor_tensor(out=Stl[:L, :L], in0=pps[:L, :L], in1=DT[:L, :L], op=ALU.mult), reads=[kps, "DT"], writes=["Stl"])
                pa = self.rr("pp", 4)
                ppa, kpa = self.pp[pa], "pp%d" % pa
                kb.op("pe", lambda e, ppa=ppa: e.matmul(ppa[:L, 0:257], lhsT=Stl[:L, :L], rhs=vaug[:L, :], start=True, stop=True), reads=["Stl", "vaug"], writes=[kpa])
                pb = self.rr("ptr", 2)
                ppb, kpb = self.ptr[pb], "ptr%d" % pb
                for ec in range(2):
                    kb.op("pe", lambda e, h=h, ec=ec, ppb=ppb, C=C: e.matmul(ppb[:L, 0:257], lhsT=qTb[:, h, ec, :L], rhs=C[h][:, ec, :],
                                                                      start=(ec == 0), stop=(ec == 1)), reads=[("qTb", h), kC[h]], writes=[kpb])
                kb.op("act", lambda e, ppa=ppa: e.activation(out=Asb[:L, :], in_=ppa[:L, 0:257], func=AF.Copy), reads=[kpa], writes=["Asb"])
                kb.op("dve", lambda e, h=h, ppb=ppb: e.scalar_tensor_tensor(out=nd[:L, :], in0=ppb[:L, 0:257], scalar=gc[:L, h:h + 1], in1=Asb[:L, :],
                                                                        op0=ALU.mult, op1=ALU.add), reads=[kpb, "gc", "Asb"], writes=["nd"])
                kb.op("act", lambda e: e.activation(out=dsm[:L, 2:3], in_=nd[:L, 256:257], func=AF.Abs), reads=["nd"], writes=["dsm"])
                kb.op("dve", lambda e, h=h: e.tensor_scalar(out=dsm[:L, 0:1], in0=dsm[:L, 2:3], scalar1=gc[:L, 4 + h:5 + h], scalar2=None,
                                                            op0=ALU.max), reads=["dsm", "gc"], writes=["dsm"])
                kb.op("dve", lambda e: e.reciprocal(out=dsm[:L, 1:2], in_=dsm[:L, 0:1]), reads=["dsm"], writes=["dsm"])
                kb.op("dve", lambda e, h=h, so=so: e.scalar_tensor_tensor(out=hm[:L, h * 256:(h + 1) * 256], in0=nd[:L, 0:256], scalar=dsm[:L, 1:2],
                                                                      in1=so[:L, h * 256:(h + 1) * 256], op0=ALU.mult, op1=ALU.mult),
                      reads=["nd", "dsm", kso], writes=[("hm", h)])
                for ec in range(2):
                    pu = self.rr("pp", 4)
                    ppu, kpu = self.pp[pu], "pp%d" % pu
                    kb.op("pe", lambda e, ec=ec, ppu=ppu: e.matmul(ppu[:, 0:257], lhsT=kw[:L, ec * 128:(ec + 1) * 128], rhs=vaug[:L, :], start=True, stop=True),
                          reads=["kw", "vaug"], writes=[kpu])
                    kb.op("dve", lambda e, h=h, ec=ec, ppu=ppu, C=C: e.scalar_tensor_tensor(out=C[h][:, ec, :], in0=C[h][:, ec, :], scalar=dbc[:, h:h + 1],
                                                                                      in1=ppu[:, 0:257], op0=ALU.mult, op1=ALU.add),
                          reads=[kC[h], "dbc", kpu], writes=[kC[h]])
            kb.op("dve", lambda e: e.tensor_copy(out=mprev[:, 0:1], in_=mprev[:, 1:2]), reads=["mnew"], writes=["mprev"])
            yield
            hmk = [("hm", h) for h in range(4)]
            hv = hm[:L, :].rearrange("t (h d) -> t h d", h=4)
            kb.op("dve", lambda e: e.tensor_reduce(out=hst[:L, 0:4], in_=hv, axis=AX.X, op=ALU.add), reads=hmk, writes=["hst"])
            kb.op("dve", lambda e: e.tensor_scalar(out=hst[:L, 0:4], in0=hst[:L, 0:4], scalar1=-1.0 / 256.0, scalar2=None, op0=ALU.mult), reads=["hst"], writes=["hst"])
            kb.op("dve", lambda e: e.tensor_tensor(out=hv, in0=hv, in1=hst[:L, 0:4].unsqueeze(2).to_broadcast([L, 4, 256]), op=ALU.add),
                  reads=hmk + ["hst"], writes=hmk)
            kb.op("dve", lambda e, so=so: e.tensor_tensor(out=so[:L, :], in0=hm[:L, :], in1=hm[:L, :], op=ALU.mult), reads=hmk, writes=[kso])
            kb.op("dve", lambda e, so=so: e.tensor_reduce(out=hst[:L, 4:8], in_=so[:L, :].rearrange("t (h d) -> t h d", h=4), axis=AX.X, op=ALU.add),
                  reads=[kso], writes=["hst"])
            kb.op("act", lambda e: e.activation(out=hst[:L, 8:12], in_=hst[:L, 4:8], func=AF.Sqrt, scale=1.0 / 256.0, bias=self.epsc[:L, 0:1]),
                  reads=["hst", "epsc"], writes=["hst"])
            kb.op("dve", lambda e: e.reciprocal(out=hst[:L, 12:16], in_=hst[:L, 8:12]), reads=["hst"], writes=["hst"])
            kb.op("dve", lambda e: e.tensor_tensor(out=hv, in0=hv, in1=hst[:L, 12:16].unsqueeze(2).to_broadcast([L, 4, 256]), op=ALU.mult),
                  reads=hmk + ["hst"], writes=hmk)
            zj = self.rr("zmt" + W["tag"], 2)
            zm_, kzm = zmt[zj], "zmt%d" % zj
            kb.dma("sp", zm_[:, :, :L], S["zT"][O_ZM:O_ZM + D, t0:t0 + L].rearrange("(k p) t -> p k t", p=128), reads=zt_z(bi), writes=[kzm],
                   **(sl if L == 1 else {}))
            kb.op("act", lambda e, zm_=zm_: e.activation(out=zm_[:, :, :L], in_=zm_[:, :, :L], func=AF.Silu), reads=[kzm], writes=[kzm])
            for half in range(2):
                p = self.rr("ptr", 2)
                pt, kpt = self.ptr[p], "ptr%d" % p
                for k4 in range(4):
                    kc = half * 4 + k4
                    kb.op("pe", lambda e, k4=k4, kc=kc, pt=pt: e.transpose(out=pt[:, k4 * 128:k4 * 128 + L], in_=hm[:L, kc * 128:(kc + 1) * 128],
                                                                       identity=self.ident[:L, :L]), reads=hmk + ["ident"], writes=[kpt])
                kb.op("dve", lambda e, half=half, pt=pt: e.tensor_tensor(out=hT[:, half * 4:half * 4 + 4, :L],
                                                                       in0=pt[:, :].rearrange("p (k t) -> p k t", k=4)[:, :, :L],
                                                                       in1=mg[:, half * 4:half * 4 + 4, None].to_broadcast([128, 4, L]), op=ALU.mult),
                      reads=[kpt, "mg"], writes=[("hT", half)])
            kb.op("dve", lambda e: e.tensor_tensor(out=V(ctmp), in0=V(xcT), in1=msk[:, :, None].to_broadcast([128, 8, L]), op=ALU.mult),
                  reads=["xcT", "msk"], writes=["cvtmp"])
            kb.op("dve", lambda e: e.tensor_tensor(out=V(hT), in0=V(hT), in1=V(ctmp), op=ALU.add), reads=["cvtmp", ("hT", 0), ("hT", 1)], writes=[("hT", 0), ("hT", 1)])
            aj = self.rr("aob" + W["tag"], 2)
            kb.op("dve", lambda e, aj=aj, zm_=zm_: e.tensor_tensor(out=aob[aj][:, :, :L], in0=V(hT), in1=zm_[:, :, :L], op=ALU.mult),
                  reads=[("hT", 0), ("hT", 1), kzm], writes=["aob%d" % aj])
            kb.dma("pool", S["act_m"][:, t0:t0 + L].rearrange("(k p) t -> p k t", p=128), aob[aj][:, :, :L], reads=["aob%d" % aj],
                   writes=[("act", "m", bi)], **(sl if L == 1 else {}))
            if sb_ is not None or ci == T // 128 - 1:
                if sb_ is None:
                    oc_, on_, om_ = O["c_prompt"][l], O["n_prompt"][l], O["m_prompt"][l]
                else:
                    oc_, on_, om_ = O["c_sample"][l, sb_], O["n_sample"][l, sb_], O["m_sample"][l, sb_]
                for h in range(4):
                    kb.dma("pool", oc_[h].rearrange("(c p) v -> p c v", p=128), C[h][:, :, 0:256], reads=[kC[h]], writes=[], is_output=True)
                    kb.dma("pool", on_[h].rearrange("(c p o) -> p c o", p=128, o=1), C[h][:, :, 256:257], reads=[kC[h]], writes=[], is_output=True, **sl)
                kb.dma("pool", om_.rearrange("(h o) -> h o", o=1), mprev[:, 0:1], reads=["mprev"], writes=[], is_output=True, **sl)
        pq_ = [(ci, t0, L, sb_) for ci, (t0, L, sb_) in enumerate(chunks) if sb_ is None]
        sq_ = [(ci, t0, L, sb_) for ci, (t0, L, sb_) in enumerate(chunks) if sb_ is not None]
        streams = [[pq_, WP, None], [sq_, WS_, None]]
        while any(st[0] or st[2] is not None for st in streams):
            for st in streams:
                if st[2] is None and st[0]:
                    args = st[0].pop(0)
                    st[2] = chunk_gen(*args, st[1])
                if st[2] is not None:
                    kb.ns = st[1]["tag"]
                    try:
                        next(st[2])
                    except StopIteration:
                        st[2] = None
                    kb.ns = None
        self.end_phase()

    def rwkv_phase(self, l):
        kb = self.kb
        I, S, O = self.I, self.S, self.O
        T, NS, NT = self.T, self.NS, self.NT
        self.begin_phase()
        sbp = self.sbp
        sl = dict(allow_slow_non_contiguous=True)
        NPI = self.NPI
        EW = -math.exp(-0.5)
        P_ = {}
        for nm in ("r_k_k", "r_k_a", "r_ln_g", "r_ln_b"):
            P_[nm] = sbp("bc_" + nm, (128, D))
            kb.dma("sp", P_[nm][:], I[nm][l].partition_broadcast(128), writes=[nm])
        P_["r_r_k"] = sbp("bc_rk", (128, D))
        kb.dma("sp", P_["r_r_k"][:], I["r_r_k"][l].rearrange("h j -> (h j)").partition_broadcast(128), writes=["r_r_k"])
        mu = sbp("bc_mu", (128, R_SHIFT_W))
        kb.dma("sp", mu[:], I["r_mu"][l].partition_broadcast(128), writes=["mu"])
        w2 = sbp("w2", (65, 2, D))
        kb.dma("sp", w2[0:64, 0, :], I["r_w2"][l], writes=["w2"])
        kb.dma("sp", w2[0:64, 1, :], I["r_a2"][l], writes=["w2"])
        kb.dma("sp", w2[64:65, 0, :], I["r_w0"][l:l + 1, :], writes=["w2"])
        kb.dma("sp", w2[64:65, 1, :], I["r_a0"][l:l + 1, :], writes=["w2"])
        mks = sbp("mks", (128, 384))
        kb.dma("sp", mks[:], I["rmasks"], writes=["mks"])
        e12 = sbp("e12", (128, 1))
        kb.op("dve", lambda e: e.memset(e12[:], RWKV_GN_EPS), writes=["e12"])
        xr = sbp("xr", (128, R_SHIFT_W)); rp = sbp("rp", (128, R_SHIFT_W))
        lt = sbp("lt", (65, 2, 128))
        kb.op("pool", lambda e: e.memset(lt[64:65, :, :], 1.0), writes=["lt1"])
        tv = {n: sbp("tv_" + n, (128, D)) for n in ("lw", "a", "an", "b", "k")}
        ssq = sbp("ssq", (128, 64))
        fmall = sbp("fmall", (128, 8, 6, 128))
        fm = {n: fmall[:, :, i_, :] for i_, n in enumerate(("at", "rt", "bh", "kh", "bc", "kc"))}
        fm["cum"] = sbp("fm_cum", (128, 8, 128)); fm["et"] = sbp("fm_et", (128, 8, 128))
        wc = sbp("wc", (128, 8, 16))
        ST = sbp("STt", (128, 8, 64))
        Vp = sbp("Vp", (128, 8, 64)); Ych = sbp("Ych", (128, 8, 64))
        nat = sbp("rnat", (128, 8, 64)); nato = sbp("rnato", (128, 8, 64))
        CDT = BF16 if self.CORE_BF16 else F32
        NG = 2
        G_ = []
        BK = sbp("g_BK", (128, 4, 2, 128), CDT)
        for gi in range(NG):
            d = {}
            d["UBD"] = sbp("g%d_UBD" % gi, (128, 4, 4, 128), CDT)
            d["Btm"] = sbp("g%d_Btm" % gi, (128, 4, 128), CDT); d["Ktm"] = sbp("g%d_Ktm" % gi, (128, 4, 128), CDT)
            d["MNb"] = sbp("g%d_MNb" % gi, (128, 4, 256), CDT); d["MNk"] = sbp("g%d_MNk" % gi, (128, 4, 256), CDT)
            d["X"] = sbp("g%d_X" % gi, (128, 4, 128), CDT); d["Xt"] = sbp("g%d_Xt" % gi, (128, 4, 128), CDT); d["P"] = sbp("g%d_P" % gi, (128, 4, 128), CDT)
            d["RHS"] = sbp("g%d_RHS" % gi, (128, 4, 64), CDT); d["U"] = sbp("g%d_U" % gi, (128, 4, 64), CDT)
            G_.append(d)
        self.bank8 = list(self.pp) + list(self.ptr) + list(self.pex)
        self.bank8k = ["pp%d" % i for i in range(4)] + ["ptr%d" % i for i in range(2)] + ["pex%d" % i for i in range(2)]
        if self.CORE_BF16:
            STb = sbp("STb", (128, 8, 64), BF16); Vpb = sbp("Vpb", (128, 8, 64), BF16); identc = sbp("identc", (128, 128), BF16)
            kb.op("pool", lambda e: e.tensor_copy(out=identc[:], in_=self.ident[:]), reads=["ident"], writes=["identc"])
            kb.op("pool", lambda e: e.memset(STb[:], 0.0), writes=[("STb", h) for h in range(8)])
        else:
            STb, Vpb, identc = ST, Vp, self.ident
        kSTb = (lambda hh: ("STb", hh)) if self.CORE_BF16 else (lambda hh: ("ST", hh))
        kVpb = (lambda hh: ("Vpb", hh)) if self.CORE_BF16 else (lambda hh: ("Vp", hh))

        def vp_ready():
            if self.CORE_BF16:
                kb.op("pool", lambda e: e.tensor_copy(out=Vpb[:], in_=Vp[:]), reads=[("Vp", h) for h in range(8)], writes=[("Vpb", h) for h in range(8)])
        ytm = rp[:, D:2 * D]; zrt = rp[:, 0:D]; yst = sbp("yst", (128, 64))
        aob = [sbp("raob%d" % i, (128, 8, 128), BF16) for i in range(2)]

        def zero_units():
            for gi in range(NG):
                kb.op("pool", lambda e, gi=gi: e.memset(G_[gi]["UBD"][:], 0.0), writes=[("g", gi, "UBD")])
            kb.op("pool", lambda e: e.memset(BK[:], 0.0), writes=["BK"])
        zero_units()
        kb.op("pool", lambda e: e.memset(ST[:], 0.0), writes=[("ST", h) for h in range(8)])

        z_all = lambda ti: [("z", ti)]

        def prep(t0, L, ti, is_s):
            kb.dma("sp", xr[:L, :], S["z"][t0:t0 + L, O_RC:O_RC + R_SHIFT_W], reads=z_all(ti), writes=["xr"])
            if is_s:
                kb.dma("sp", rp[:L, :], I["state_rwkv_shift"][l], writes=["rp"])
            elif t0 == 0:
                kb.op("pool", lambda e: e.memset(rp[0:1, :], 0.0), writes=["rp"])
                kb.dma("sp", rp[1:L, :], S["z"][0:L - 1, O_RC:O_RC + R_SHIFT_W], reads=z_all(ti), writes=["rp"])
            else:
                kb.dma("sp", rp[:L, :], S["z"][t0 - 1:t0 + L - 1, O_RC:O_RC + R_SHIFT_W], reads=z_all(ti) + z_all(ti - 1), writes=["rp"])
            kb.op("dve", lambda e: e.tensor_tensor(out=rp[:L, :], in0=rp[:L, :], in1=xr[:L, :], op=ALU.subtract), reads=["rp", "xr"], writes=["rp"])
            kb.op("dve", lambda e: e.tensor_tensor(out=rp[:L, :], in0=rp[:L, :], in1=mu[:L, :], op=ALU.mult), reads=["rp", "mu"], writes=["rp"])
            kb.op("dve", lambda e: e.tensor_tensor(out=xr[:L, :], in0=xr[:L, :], in1=rp[:L, :], op=ALU.add), reads=["rp", "xr"], writes=["xr"])
            r_ = xr[:L, 0:D]; kr = xr[:L, D:2 * D]; vr = xr[:L, 2 * D:3 * D]
            kb.dma("pool", S["vs"][t0:t0 + L, :], vr, reads=["xr"], writes=[("vs", ti)])
            kb.op("act", lambda e: e.activation(out=xr[:L, 3 * D:3 * D + 64], in_=xr[:L, 3 * D:3 * D + 64], func=AF.Tanh), reads=["xr"], writes=["xr"])
            p = self.rr("ptr", 2)
            pt, kpt = self.ptr[p], "ptr%d" % p
            for i2 in range(2):
                kb.op("pe", lambda e, i2=i2, pt=pt: e.transpose(out=pt[:64, i2 * 128:i2 * 128 + L], in_=xr[:L, 3 * D + 64 * i2:3 * D + 64 * i2 + 64],
                                                             identity=self.ident[:L, :L]), reads=["xr", "ident"], writes=[kpt])
            kb.op("dve", lambda e, pt=pt: e.tensor_copy(out=lt[0:64, :, :L], in_=pt[:64, 0:256].rearrange("p (a t) -> p a t", a=2)[:, :, :L]), reads=[kpt], writes=["lt"])
            for i2, dst in enumerate(("lw", "a")):
                for hf in range(2):
                    pq = self.rr("pp", 4)
                    pp, kpp = self.pp[pq], "pp%d" % pq
                    kb.op("pe", lambda e, i2=i2, hf=hf, pp=pp: e.matmul(pp[:L, :], lhsT=lt[:, i2, :L], rhs=w2[:, i2, hf * 512:(hf + 1) * 512], start=True, stop=True),
                          reads=["lt", "lt1", "w2"], writes=[kpp])
                    kb.op("act", lambda e, dst=dst, hf=hf, pp=pp: e.activation(out=tv[dst][:L, hf * 512:(hf + 1) * 512], in_=pp[:L, :], func=AF.Sigmoid),
                          reads=[kpp], writes=[("tv", dst)])
            kb.op("dve", lambda e: e.tensor_scalar(out=tv["lw"][:L, :], in0=tv["lw"][:L, :], scalar1=EW, scalar2=None, op0=ALU.mult), reads=[("tv", "lw")], writes=[("tv", "lw")])
            kb.op("dve", lambda e: e.tensor_tensor(out=tv["an"][:L, :], in0=kr, in1=P_["r_k_k"][:L, :], op=ALU.mult), reads=["xr", "r_k_k"], writes=[("tv", "an")])
            kb.op("dve", lambda e: e.tensor_tensor(out=tv["b"][:L, :], in0=tv["an"][:L, :], in1=tv["an"][:L, :], op=ALU.mult), reads=[("tv", "an")], writes=[("tv", "b")])
            kb.op("dve", lambda e: e.tensor_reduce(out=ssq[:L, 0:16], in_=tv["b"][:L, :].rearrange("t (h j) -> t h j", h=16), axis=AX.X, op=ALU.add),
                  reads=[("tv", "b")], writes=["ssq"])
            kb.op("act", lambda e: e.activation(out=ssq[:L, 0:16], in_=ssq[:L, 0:16], func=AF.Sqrt), reads=["ssq"], writes=["ssq"])
            kb.op("dve", lambda e: e.tensor_scalar(out=ssq[:L, 0:16], in0=ssq[:L, 0:16], scalar1=1e-12, scalar2=None, op0=ALU.max), reads=["ssq"], writes=["ssq"])
            kb.op("dve", lambda e: e.reciprocal(out=ssq[:L, 16:32], in_=ssq[:L, 0:16]), reads=["ssq"], writes=["ssq"])
            hv = lambda x: x[:L, :].rearrange("t (h j) -> t h j", h=16)
            kb.op("dve", lambda e: e.tensor_tensor(out=hv(tv["an"]), in0=hv(tv["an"]), in1=ssq[:L, 16:32].unsqueeze(2).to_broadcast([L, 16, 64]), op=ALU.mult),
                  reads=[("tv", "an"), "ssq"], writes=[("tv", "an")])
            kb.op("dve", lambda e: e.tensor_tensor(out=tv["b"][:L, :], in0=tv["an"][:L, :], in1=tv["a"][:L, :], op=ALU.mult),
                  reads=[("tv", "an"), ("tv", "a")], writes=[("tv", "b")])
            kb.op("dve", lambda e: e.tensor_scalar(out=tv["an"][:L, :], in0=tv["an"][:L, :], scalar1=-1.0, scalar2=None, op0=ALU.mult),
                  reads=[("tv", "an"), ("tv", "b")], writes=[("tv", "an")])
            kb.op("dve", lambda e: e.scalar_tensor_tensor(out=tv["k"][:L, :], in0=tv["a"][:L, :], scalar=-1.0, in1=P_["r_k_a"][:L, :], op0=ALU.add, op1=ALU.mult),
                  reads=[("tv", "a"), "r_k_a"], writes=[("tv", "k")])
            kb.op("dve", lambda e: e.scalar_tensor_tensor(out=tv["k"][:L, :], in0=tv["k"][:L, :], scalar=1.0, in1=kr, op0=ALU.add, op1=ALU.mult),
                  reads=[("tv", "k"), "xr"], writes=[("tv", "k")])
            kb.op("dve", lambda e: e.tensor_tensor(out=tv["a"][:L, :], in0=tv["k"][:L, :], in1=P_["r_r_k"][:L, :], op=ALU.mult),
                  reads=[("tv", "k"), "r_r_k", ("tv", "b")], writes=[("tv", "a")])
            kb.op("dve", lambda e: e.tensor_tensor(out=tv["a"][:L, :], in0=tv["a"][:L, :], in1=r_, op=ALU.mult), reads=[("tv", "a"), "xr"], writes=[("tv", "a")])
            kb.op("dve", lambda e: e.tensor_reduce(out=ssq[:L, 32:48], in_=hv(tv["a"]), axis=AX.X, op=ALU.add), reads=[("tv", "a")], writes=["ssq2"])
            for (dst, src, ksrc) in (("at", tv["an"][:L, :], ("tv", "an")), ("rt", r_, "xr"), ("bh", tv["b"][:L, :], ("tv", "b")),
                                     ("kh", tv["k"][:L, :], ("tv", "k")), ("cum", tv["lw"][:L, :], ("tv", "lw"))):
                for half in range(2):
                    p = self.rr("ptr", 2)
                    pt, kpt = self.ptr[p], "ptr%d" % p
                    for k4 in range(4):
                        kc = half * 4 + k4
                        kb.op("pe", lambda e, k4=k4, kc=kc, pt=pt, src=src: e.transpose(out=pt[:, k4 * 128:k4 * 128 + L], in_=src[:, kc * 128:(kc + 1) * 128],
                                                                                  identity=self.ident[:L, :L]), reads=[ksrc, "ident"], writes=[kpt])
                    self.evac(fm[dst][:, half * 4:half * 4 + 4, :L], pt[:, :].rearrange("p (k t) -> p k t", k=4)[:, :, :L], [kpt], [("fm", dst)])
            Lc = 1 if is_s else 64
            nch = L // Lc
            lwT = fm["cum"]
            kb.op("dve", lambda e: e.tensor_copy(out=fm["et"][:, :, :L], in_=lwT[:, :, :L]), reads=[("fm", "cum")], writes=[("fm", "et")])
            if Lc > 1:
                for hh in range(8):
                    for c in range(nch):
                        kb.op("dve", lambda e, hh=hh, c=c: e.tensor_tensor_scan(out=fm["cum"][:, hh, c * Lc:(c + 1) * Lc], data0=self.onesf[:, :Lc],
                                                                             data1=fm["et"][:, hh, c * Lc:(c + 1) * Lc], initial=0.0, op0=ALU.mult, op1=ALU.add),
                              reads=[("fm", "et"), "onesf"], writes=[("fm", "cum")])
            F = lambda n: fm[n][:, :, :L]
            C4 = lambda n: fm[n][:, :, :L].rearrange("p k (c t) -> p k c t", t=Lc)
            kb.op("dve", lambda e: e.tensor_tensor(out=F("et"), in0=F("cum"), in1=F("et"), op=ALU.subtract), reads=[("fm", "cum"), ("fm", "et")], writes=[("fm", "et")])
            kb.op("act", lambda e: e.activation(out=F("et"), in_=F("et"), func=AF.Exp), reads=[("fm", "et")], writes=[("fm", "et")])
            kb.op("dve", lambda e: e.tensor_tensor(out=F("at"), in0=F("at"), in1=F("et"), op=ALU.mult), reads=[("fm", "at"), ("fm", "et")], writes=[("fm", "at")])
            kb.op("act", lambda e: e.activation(out=F("et"), in_=F("cum"), func=AF.Exp), reads=[("fm", "cum"), ("fm", "at")], writes=[("fm", "et")])
            kb.op("dve", lambda e: e.tensor_tensor(out=F("rt"), in0=F("rt"), in1=F("et"), op=ALU.mult), reads=[("fm", "rt"), ("fm", "et")], writes=[("fm", "rt")])
            kb.op("pool", lambda e: e.tensor_copy(out=wc[:, :, :nch], in_=C4("et")[:, :, :, Lc - 1]), reads=[("fm", "et")], writes=["wc"])
            kb.op("dve", lambda e: e.tensor_tensor(out=C4("et"), in0=C4("cum")[:, :, :, Lc - 1:Lc].to_broadcast([128, 8, nch, Lc]), in1=C4("cum"), op=ALU.subtract),
                  reads=[("fm", "cum"), ("fm", "rt"), "wc"], writes=[("fm", "et")])
            kb.op("act", lambda e: e.activation(out=F("et"), in_=F("et"), func=AF.Exp), reads=[("fm", "et")], writes=[("fm", "et")])
            kb.op("dve", lambda e: e.tensor_tensor(out=F("bc"), in0=F("bh"), in1=F("et"), op=ALU.mult), reads=[("fm", "bh"), ("fm", "et")], writes=[("fm", "bc")])
            kb.op("dve", lambda e: e.tensor_tensor(out=F("kc"), in0=F("kh"), in1=F("et"), op=ALU.mult), reads=[("fm", "kh"), ("fm", "et")], writes=[("fm", "kc")])
            kb.op("act", lambda e: e.activation(out=F("et"), in_=F("cum"), func=AF.Exp, scale=-1.0), reads=[("fm", "cum"), ("fm", "bc"), ("fm", "kc")], writes=[("fm", "et")])
            kb.op("dve", lambda e: e.tensor_tensor(out=F("bh"), in0=F("bh"), in1=F("et"), op=ALU.mult), reads=[("fm", "bh"), ("fm", "et")], writes=[("fm", "bh")])
            kb.op("dve", lambda e: e.tensor_tensor(out=F("kh"), in0=F("kh"), in1=F("et"), op=ALU.mult), reads=[("fm", "kh"), ("fm", "et")], writes=[("fm", "kh")])

        def bank():
            q_ = self.rr("bank8", 8)
            return self.bank8[q_], self.bank8k[q_]

        def core(gi, col0, Lc, cidx):
            g = G_[gi]
            K_ = lambda n: ("g", gi, n)
            hs = slice(4 * gi, 4 * gi + 4)
            cs_ = slice(col0, col0 + Lc)
            for half in range(2):
                ps = slice(half * 64, half * 64 + 64)
                kb.op("dve" if half == 0 else "pool", lambda e, ps=ps, half=half: e.tensor_copy(
                    out=g["UBD"][ps, :, :, half * 64:half * 64 + Lc], in_=fmall[ps, hs, 0:4, cs_]),
                    reads=[("fm", n) for n in ("at", "rt", "bh", "kh")], writes=[K_("UBD")])
            yield
            for half in range(2):
                ps = slice(half * 64, half * 64 + 64)
                kb.op("pool" if half == 0 else "dve", lambda e, ps=ps, half=half: e.tensor_copy(
                    out=BK[ps, :, :, half * 64:half * 64 + Lc], in_=fmall[ps, hs, 4:6, cs_]),
                    reads=[("fm", "bc"), ("fm", "kc")], writes=["BK"])
            for vi, dst in enumerate(("Btm", "Ktm")):
                pb_, kpb = bank()
                for u in range(4):
                    kb.op("pe", lambda e, u=u, vi=vi, pb_=pb_: e.matmul(pb_[:, u * 128:(u + 1) * 128], lhsT=BK[:, u, vi, :], rhs=identc[:, :], start=True, stop=True),
                          reads=["BK", "identc"], writes=[kpb])
                self.evac(g[dst][:, :, :], pb_[:, :].rearrange("p (u m) -> p u m", u=4), [kpb], [K_(dst)])
            yield
            for (vec, dst) in ((2, "MNb"), (3, "MNk")):
                for pr in range(2):
                    pb_, kpb = bank()
                    for u2 in range(2):
                        u = pr * 2 + u2
                        kb.op("pe", lambda e, u=u, u2=u2, vec=vec, pb_=pb_: e.matmul(pb_[:, u2 * 256:(u2 + 1) * 256], lhsT=g["UBD"][:, u, vec, :],
                                                                              rhs=g["UBD"][:, u, 0:2, :].rearrange("p a m -> p (a m)"), start=True, stop=True),
                              reads=[K_("UBD")], writes=[kpb])
                    kb.op("dve", lambda e, pr=pr, dst=dst, pb_=pb_: e.tensor_tensor(out=g[dst][:, pr * 2:pr * 2 + 2, :], in0=pb_[:, :].rearrange("p (u m) -> p u m", u=2),
                                                                           in1=mks[:, None, 0:256].to_broadcast([128, 2, 256]), op=ALU.mult),
                          reads=[kpb, "mks"], writes=[K_(dst)])
            pb_, kpb = bank()
            for u in range(4):
                kb.op("pe", lambda e, u=u, pb_=pb_: e.matmul(pb_[:, u * 128:(u + 1) * 128], lhsT=g["UBD"][:, u, 0, :], rhs=g["UBD"][:, u, 2, :], start=True, stop=True),
                      reads=[K_("UBD")], writes=[kpb])
            kb.op("dve", lambda e, pb_=pb_: e.tensor_tensor(out=g["Xt"][:, :, :], in0=pb_[:, :].rearrange("p (u m) -> p u m", u=4),
                                                       in1=mks[:, None, 256:384].to_broadcast([128, 4, 128]), op=ALU.mult),
                  reads=[kpb, "mks"], writes=[K_("Xt")])
            kb.op("act", lambda e: e.activation(out=g["X"][:, :, :], in_=g["MNb"][:, :, 0:128], func=AF.Copy), reads=[K_("MNb")], writes=[K_("X")])
            kb.op("dve", lambda e: e.tensor_tensor(out=g["P"][:, :, :], in0=g["MNb"][:, :, 0:128], in1=identc[:, None, :].to_broadcast([128, 4, 128]), op=ALU.add),
                  reads=[K_("MNb"), "identc"], writes=[K_("P")])
            yield
            nlev = 0
            while (1 << (nlev + 1)) < Lc:
                nlev += 1
            for lev in range(nlev if Lc > 1 else 0):
                lastlev = (lev == nlev - 1)
                pb1 = kp1 = None
                if not lastlev:
                    pb1, kp1 = bank()
                    for u in range(4):
                        kb.op("pe", lambda e, u=u, pb1=pb1: e.matmul(pb1[:, u * 128:(u + 1) * 128], lhsT=g["Xt"][:, u, :], rhs=g["X"][:, u, :], start=True, stop=True),
                              reads=[K_("X"), K_("Xt")], writes=[kp1])
                pb2, kp2 = bank()
                for u in range(4):
                    kb.op("pe", lambda e, u=u, pb2=pb2: e.matmul(pb2[:, u * 128:(u + 1) * 128], lhsT=g["X"][:, u, :], rhs=g["Xt"][:, u, :], start=True, stop=True),
                          reads=[K_("X"), K_("Xt")], writes=[kp2])
                if not lastlev:
                    self.evac(g["X"][:, :, :], pb1[:, :].rearrange("p (u m) -> p u m", u=4), [kp1], [K_("X")])
                self.evac(g["Xt"][:, :, :], pb2[:, :].rearrange("p (u m) -> p u m", u=4), [kp2], [K_("Xt")])
                pb3, kp3 = bank()
                for u in range(4):
                    kb.op("pe", lambda e, u=u, pb3=pb3: e.matmul(pb3[:, u * 128:(u + 1) * 128], lhsT=g["Xt"][:, u, :], rhs=g["P"][:, u, :], start=True, stop=True),
                          reads=[K_("Xt"), K_("P")], writes=[kp3])
                kb.op("dve", lambda e, pb3=pb3: e.tensor_tensor(out=g["P"][:, :, :], in0=pb3[:, :].rearrange("p (u m) -> p u m", u=4), in1=g["P"][:, :, :], op=ALU.add),
                      reads=[kp3, K_("P")], writes=[K_("P")])
                yield
            kS = [("ST", 4 * gi + u) for u in range(4)]
            kSb = [kSTb(4 * gi + u) for u in range(4)]
            kVb = [kVpb(4 * gi + u) for u in range(4)]
            pb_, kpb = bank()
            for u in range(4):
                hh = 4 * gi + u
                kb.op("pe", lambda e, u=u, hh=hh, pb_=pb_: e.matmul(pb_[:, u * 64:(u + 1) * 64], lhsT=g["UBD"][:, u, 0, :], rhs=STb[:, hh, :], start=True, stop=False),
                      reads=[K_("UBD"), kSb[u]], writes=[kpb])
                kb.op("pe", lambda e, u=u, hh=hh, pb_=pb_: e.matmul(pb_[:, u * 64:(u + 1) * 64], lhsT=g["MNk"][:, u, 0:128], rhs=Vpb[:, hh, :], start=False, stop=True),
                      reads=[K_("MNk"), kVb[u]], writes=[kpb])
            self.evac(g["RHS"][:, :, :], pb_[:, 0:256].rearrange("p (u m) -> p u m", u=4), [kpb], [K_("RHS")])
            pb_, kpb = bank()
            for u in range(4):
                kb.op("pe", lambda e, u=u, pb_=pb_: e.matmul(pb_[:, u * 64:(u + 1) * 64], lhsT=g["P"][:, u, :], rhs=g["RHS"][:, u, :], start=True, stop=True),
                      reads=[K_("P"), K_("RHS")], writes=[kpb])
            self.evac(g["U"][:, :, :], pb_[:, 0:256].rearrange("p (u m) -> p u m", u=4), [kpb], [K_("U")])
            yield
            pb_, kpb = bank()
            for u in range(4):
                hh = 4 * gi + u
                kb.op("pe", lambda e, u=u, hh=hh, pb_=pb_: e.matmul(pb_[:, u * 64:(u + 1) * 64], lhsT=g["UBD"][:, u, 1, :], rhs=STb[:, hh, :], start=True, stop=False),
                      reads=[K_("UBD"), kSb[u]], writes=[kpb])
                kb.op("pe", lambda e, u=u, pb_=pb_: e.matmul(pb_[:, u * 64:(u + 1) * 64], lhsT=g["MNb"][:, u, 128:256], rhs=g["U"][:, u, :], start=False, stop=False),
                      reads=[K_("MNb"), K_("U")], writes=[kpb])
                kb.op("pe", lambda e, u=u, hh=hh, pb_=pb_: e.matmul(pb_[:, u * 64:(u + 1) * 64], lhsT=g["MNk"][:, u, 128:256], rhs=Vpb[:, hh, :], start=False, stop=True),
                      reads=[K_("MNk"), kVb[u]], writes=[kpb])
            self.evac(Ych[:, hs, :], pb_[:, 0:256].rearrange("p (u m) -> p u m", u=4), [kpb], [("Ych", 4 * gi + u) for u in range(4)])
            pb_, kpb = bank()
            for u in range(4):
                hh = 4 * gi + u
                kb.op("pe", lambda e, u=u, pb_=pb_: e.matmul(pb_[:, u * 64:(u + 1) * 64], lhsT=g["Btm"][:, u, :], rhs=g["U"][:, u, :], start=True, stop=False),
                      reads=[K_("Btm"), K_("U")], writes=[kpb])
                kb.op("pe", lambda e, u=u, hh=hh, pb_=pb_: e.matmul(pb_[:, u * 64:(u + 1) * 64], lhsT=g["Ktm"][:, u, :], rhs=Vpb[:, hh, :], start=False, stop=True),
                      reads=[K_("Ktm"), kVb[u]], writes=[kpb])
            kb.op("dve", lambda e: e.tensor_tensor(out=ST[:, hs, :], in0=ST[:, hs, :], in1=wc[:, hs, cidx:cidx + 1].to_broadcast([128, 4, 64]), op=ALU.mult),
                  reads=kS + ["wc"], writes=kS)
            kb.op("dve", lambda e, pb_=pb_: e.tensor_tensor(out=ST[:, hs, :], in0=ST[:, hs, :], in1=pb_[:, 0:256].rearrange("p (u m) -> p u m", u=4), op=ALU.add),
                  reads=kS + [kpb], writes=kS)
            if self.CORE_BF16:
                kb.op("act", lambda e: e.activation(out=STb[:, hs, :], in_=ST[:, hs, :], func=AF.Copy), reads=kS, writes=kSb)
            yield

        def run_core(col0, Lc, cidx):
            alive = [core(gi, col0, Lc, cidx) for gi in range(NG)]
            while alive:
                for g_ in list(alive):
                    try:
                        next(g_)
                    except StopIteration:
                        alive.remove(g_)

        def post(t0, L, ti, bi):
            kb.dma("sp", ytm[:L, :], S["ys"][t0:t0 + L, :], reads=[("ys", ti)], writes=["rp"])
            kb.dma("sp", zrt[:L, :], S["z"][t0:t0 + L, O_ZR:O_ZR + D], reads=z_all(ti), writes=["rp"])
            kb.op("act", lambda e: e.activation(out=zrt[:L, :], in_=zrt[:L, :], func=AF.Silu), reads=["rp"], writes=["rp"])
            hv = lambda x: x[:L, :].rearrange("t (h j) -> t h j", h=16)
            kb.op("dve", lambda e: e.tensor_reduce(out=yst[:L, 0:16], in_=hv(ytm), axis=AX.X, op=ALU.add), reads=["rp"], writes=["yst"])
            kb.op("dve", lambda e: e.tensor_scalar(out=yst[:L, 0:16], in0=yst[:L, 0:16], scalar1=-1.0 / 64.0, scalar2=None, op0=ALU.mult), reads=["yst"], writes=["yst"])
            kb.op("dve", lambda e: e.tensor_tensor(out=hv(ytm), in0=hv(ytm), in1=yst[:L, 0:16].unsqueeze(2).to_broadcast([L, 16, 64]), op=ALU.add),
                  reads=["rp", "yst"], writes=["rp"])
            kb.op("dve", lambda e: e.tensor_tensor(out=tv["lw"][:L, :], in0=ytm[:L, :], in1=ytm[:L, :], op=ALU.mult), reads=["rp"], writes=[("tv", "lw")])
            kb.op("dve", lambda e: e.tensor_reduce(out=yst[:L, 16:32], in_=hv(tv["lw"]), axis=AX.X, op=ALU.add), reads=[("tv", "lw")], writes=["yst"])
            kb.op("act", lambda e: e.activation(out=yst[:L, 32:48], in_=yst[:L, 16:32], func=AF.Sqrt, scale=1.0 / 64.0, bias=e12[:L, 0:1]), reads=["yst", "e12"], writes=["yst"])
            kb.op("dve", lambda e: e.reciprocal(out=yst[:L, 48:64], in_=yst[:L, 32:48]), reads=["yst"], writes=["yst"])
            kb.op("dve", lambda e: e.tensor_tensor(out=hv(ytm), in0=hv(ytm), in1=yst[:L, 48:64].unsqueeze(2).to_broadcast([L, 16, 64]), op=ALU.mult),
                  reads=["rp", "yst"], writes=["rp"])
            kb.op("dve", lambda e: e.tensor_tensor(out=ytm[:L, :], in0=ytm[:L, :], in1=P_["r_ln_g"][:L, :], op=ALU.mult), reads=["rp", "r_ln_g"], writes=["rp"])
            kb.op("dve", lambda e: e.tensor_tensor(out=ytm[:L, :], in0=ytm[:L, :], in1=P_["r_ln_b"][:L, :], op=ALU.add), reads=["rp", "r_ln_b"], writes=["rp"])
            kb.op("dve", lambda e: e.tensor_tensor(out=hv(tv["lw"]), in0=xr[:L, 2 * D:3 * D].rearrange("t (h j) -> t h j", h=16),
                                                   in1=ssq[:L, 32:48].unsqueeze(2).to_broadcast([L, 16, 64]), op=ALU.mult), reads=["xr", "ssq2"], writes=[("tv", "lw")])
            kb.op("dve", lambda e: e.tensor_tensor(out=ytm[:L, :], in0=ytm[:L, :], in1=tv["lw"][:L, :], op=ALU.add), reads=["rp", ("tv", "lw")], writes=["rp"])
            kb.op("dve", lambda e: e.tensor_tensor(out=ytm[:L, :], in0=ytm[:L, :], in1=zrt[:L, :], op=ALU.mult), reads=["rp", "rp"], writes=["rp"])
            aj = self.rr("raob", 2)
            for half in range(2):
                p = self.rr("ptr", 2)
                pt, kpt = self.ptr[p], "ptr%d" % p
                for k4 in range(4):
                    kc = half * 4 + k4
                    kb.op("pe", lambda e, k4=k4, kc=kc, pt=pt: e.transpose(out=pt[:, k4 * 128:k4 * 128 + L], in_=ytm[:L, kc * 128:(kc + 1) * 128],
                                                                       identity=self.ident[:L, :L]), reads=["rp", "ident"], writes=[kpt])
                self.evac(aob[aj][:, half * 4:half * 4 + 4, :L], pt[:, :].rearrange("p (k t) -> p k t", k=4)[:, :, :L], [kpt], ["raob%d" % aj])
            kb.dma("pool", S["act_r"][:, t0:t0 + L].rearrange("(k p) t -> p k t", p=128), aob[aj][:, :, :L], reads=["raob%d" % aj], writes=[("act", "r", bi)])

        natx8 = sbp("rnatx8", (128, 8, 128))
        kb.op("pool", lambda e: e.memset(natx8[:], 0.0), writes=["natx8"])

        def bd_transpose(src, ksrc, dst, kdst, also_b=None):
            for half in range(2):
                ps = slice(half * 64, half * 64 + 64)
                kb.op("pool" if half else "dve", lambda e, ps=ps: e.tensor_copy(out=natx8[ps, :, ps], in_=src[ps, :, :]), reads=ksrc, writes=["natx8"])
            for q4 in range(2):
                pb_, kpb = bank()
                for u in range(4):
                    kb.op("pe", lambda e, u=u, q4=q4, pb_=pb_: e.transpose(out=pb_[:, u * 128:(u + 1) * 128], in_=natx8[:, q4 * 4 + u, :], identity=self.ident[:, :]),
                          reads=["natx8", "ident"], writes=[kpb])
                for half in range(2):
                    ps = slice(half * 64, half * 64 + 64)
                    kb.op("dve" if half else "act", (lambda e, ps=ps, q4=q4, pb_=pb_: e.tensor_copy(out=dst[ps, q4 * 4:q4 * 4 + 4, :], in_=pb_[:, :].rearrange("p (u m) -> p u m", u=4)[ps, :, ps]))
                          if half else (lambda e, ps=ps, q4=q4, pb_=pb_: e.activation(out=dst[ps, q4 * 4:q4 * 4 + 4, :], in_=pb_[:, :].rearrange("p (u m) -> p u m", u=4)[ps, :, ps], func=AF.Copy)),
                          reads=[kpb], writes=kdst)

        def state_out(dst):
            bd_transpose(ST, [("ST", h) for h in range(8)], nato, ["nato"])
            kb.dma("pool", dst.rearrange("(hh hl) i j -> (hl i) hh j", hl=2), nato[:, :, :], reads=["nato"], writes=[], is_output=True)

        def state_in(src):
            kb.dma("sp", nat[:, :, :], src.rearrange("(hh hl) i j -> (hl i) hh j", hl=2), writes=["nat"])
            bd_transpose(nat, ["nat"], ST, [("ST", h) for h in range(8)])
            if self.CORE_BF16:
                kb.op("pool", lambda e: e.tensor_copy(out=STb[:, :, :], in_=ST[:, :, :]), reads=[("ST", h) for h in range(8)], writes=[("STb", h) for h in range(8)])

        blk_of = lambda t: next(i for i, (b0, bn) in enumerate(self.blocks) if b0 <= t < b0 + bn)
        for ti in range(T // 128):
            t0 = ti * 128
            prep(t0, 128, ti, False)
            for c in range(2):
                for half in range(2):
                    kb.dma("sp", Vp[half * 64:half * 64 + 64, :, :],
                           S["vs"][t0 + c * 64:t0 + c * 64 + 64, :].rearrange("t (hh hl i) -> hl t hh i", hl=2, i=64)[half],
                           reads=[("vs", ti)], writes=[("Vp", h) for h in range(8)])
                vp_ready()
                run_core(c * 64, 64, c)
                for half in range(2):
                    kb.dma("pool", S["ys"][t0 + c * 64:t0 + c * 64 + 64, :].rearrange("t (hh hl i) -> hl t hh i", hl=2, i=64)[half],
                           Ych[half * 64:half * 64 + 64, :, :], reads=[("Ych", h) for h in range(8)], writes=[("ys", ti)])
            post(t0, 128, ti, blk_of(t0))
        state_out(O["wkv_prompt"][l])
        if NS:
            ti = T // 128
            prep(T, NS, ti, True)
            zero_units()
            kb.op("pool", lambda e: e.memset(Vp[:], 0.0), writes=[("Vp", h) for h in range(8)])
            for b in range(NS):
                state_in(I["state_rwkv_wkv"][l, b])
                for half in range(2):
                    kb.dma("sp", Vp[half * 64:half * 64 + 1, :, :],
                           S["vs"][T + b:T + b + 1, :].rearrange("t (hh hl i) -> hl t hh i", hl=2, i=64)[half],
                           reads=[("vs", ti)], writes=[("Vp", h) for h in range(8)])
                vp_ready()
                run_core(b, 1, b)
                for half in range(2):
                    kb.dma("pool", S["ys"][T + b:T + b + 1, :].rearrange("t (hh hl i) -> hl t hh i", hl=2, i=64)[half],
                           Ych[half * 64:half * 64 + 1, :, :], reads=[("Ych", h) for h in range(8)], writes=[("ys", ti)])
                state_out(O["wkv_sample"][l, b])
            post(T, NS, ti, blk_of(T))
        self.end_phase()

    def sin_to(self, tkey, out, ang, tf, ti, tm, key_out, key_ang, shift=0.0):
        kb = self.kb
        TWO_PI = 2.0 * math.pi
        kt = ("sin_tmp", tkey)
        kb.op("dve", lambda e: e.tensor_scalar(out=tm, in0=ang, scalar1=shift, scalar2=None, op0=ALU.add),
              reads=[key_ang], writes=[(kt, "m")])
        kb.op("dve", lambda e: e.tensor_scalar(out=tf, in0=tm, scalar1=1.0 / TWO_PI, scalar2=None, op0=ALU.mult),
              reads=[(kt, "m")], writes=[(kt, "f")])
        kb.op("dve", lambda e: e.tensor_copy(out=ti, in_=tf), reads=[(kt, "f")], writes=[(kt, "i")])
        kb.op("dve", lambda e: e.tensor_copy(out=tf, in_=ti), reads=[(kt, "i")], writes=[(kt, "f")])
        kb.op("dve", lambda e: e.scalar_tensor_tensor(out=tm, in0=tf, scalar=-TWO_PI, in1=tm, op0=ALU.mult, op1=ALU.add),
              reads=[(kt, "f"), (kt, "m")], writes=[(kt, "m")])
        kb.op("dve", lambda e: e.tensor_scalar(out=tf, in0=tm, scalar1=math.pi, scalar2=-TWO_PI, op0=ALU.is_gt, op1=ALU.mult),
              reads=[(kt, "m")], writes=[(kt, "f")])
        kb.op("dve", lambda e: e.tensor_tensor(out=tm, in0=tm, in1=tf, op=ALU.add), reads=[(kt, "f"), (kt, "m")],
              writes=[(kt, "m")])
        kb.op("dve", lambda e: e.tensor_scalar(out=tf, in0=tm, scalar1=-math.pi, scalar2=TWO_PI, op0=ALU.is_lt, op1=ALU.mult),
              reads=[(kt, "m")], writes=[(kt, "f")])
        kb.op("dve", lambda e: e.tensor_tensor(out=tm, in0=tm, in1=tf, op=ALU.add), reads=[(kt, "f"), (kt, "m")],
              writes=[(kt, "m")])
        kb.op("dve", lambda e: e.tensor_scalar(out=tm, in0=tm, scalar1=-3.1415925, scalar2=3.1415925, op0=ALU.max, op1=ALU.min),
              reads=[(kt, "m")], writes=[(kt, "m")])
        kb.op("act", lambda e: e.activation(out=out, in_=tm, func=AF.Sin), reads=[(kt, "m")], writes=[key_out])

    def s5_phase(self, l):
        kb = self.kb
        I, S, O = self.I, self.S, self.O
        T, NS = self.T, self.NS
        self.begin_phase()
        sbp = self.sbp
        PT = sbp("PT", (128, 6, 32))
        Bz = [sbp("Bz%d" % i, (128, 32, 128), BF16) for i in range(2)]
        Cz = [sbp("Cz%d" % i, (128, 32, 128), BF16) for i in range(2)]
        cosT = sbp("cosT", (128, 32, 256)); sinT = sbp("sinT", (128, 32, 256))
        car = [sbp("car%d" % i, (128, 32)) for i in range(2)]
        ctmp = sbp("ctmp", (128, 4))
        dcol = sbp("dcol", (128, 8)); gbcol = sbp("gbcol", (128, 8))
        wg = sbp("wglu", (128, KC, D), BF16)
        s0T = [sbp("s0T%d" % i, (128, 32, 16)) for i in range(2)]
        snw = [sbp("snw%d" % i, (128, 32, 16)) for i in range(2)]
        sst = sbp("sst", (32, 512))
        outer_pes = self.pes
        self.pes = ExitStack()
        nat = sbp("nat", (32, 14, 128))
        nati = sbp("nati", (32, 128), I32)
        BR = sbp("BR", (128, 32, 16)); BI = sbp("BI", (128, 32, 16))
        bbr = sbp("bbr", (128, 32, 16)); bbi = sbp("bbi", (128, 32, 16))
        xin = sbp("xin", (128, 512))
        btmp = xin[:, :].rearrange("p (s c) -> p s c", c=16)
        Cn = [sbp("Cn%d" % i, (128, 8, 64)) for i in range(2)]
        ang = sbp("ang", (128, 2, 256)); angf = sbp("angf", (128, 2, 256)); angm = sbp("angm", (128, 2, 256))
        angi = sbp("angi", (128, 2, 256), I32)
        m3 = sbp("m3", (128, 4, 2)); m4 = sbp("m4", (128, 4, 8)); iota1 = sbp("iota1", (128, 256))
        wstg = [sbp("wstg%d" % i, (128, KC, 128)) for i in range(1)]

        kb.dma("sp", m3[:], I["mask3"], writes=["m3"])
        kb.dma("sp", m4[:], I["mask4"], writes=["m4"])
        kb.dma("sp", iota1[:], I["iota1"], writes=["iota1"])
        kb.dma("sp", nat[:, 0, :], I["s_lam_re"][l].rearrange("(s g) p -> s (g p)", g=2), writes=["nat0"])
        kb.dma("sp", nat[:, 1, :], I["s_lam_im"][l].rearrange("(s g) p -> s (g p)", g=2), writes=["nat1"])
        kb.dma("sp", nat[:, 2, 0:2], I["s_log_dt"][l].rearrange("(s g) -> s g", g=2), writes=["nat2"])
        kb.dma("sp", dcol[:], I["s_d"][l].rearrange("(k p) -> p k", p=128), writes=["dcol"], allow_slow_non_contiguous=True)
        kb.dma("sp", gbcol[:], I["s_glu_b"][l].rearrange("(k p) -> p k", p=128), writes=["gbcol"], allow_slow_non_contiguous=True)
        kb.dma("sp", BR[:], I["s_b_re"][l].rearrange("(s g) p c -> (g p) s c", g=2), writes=["BR"])
        kb.dma("sp", BI[:], I["s_b_im"][l].rearrange("(s g) p c -> (g p) s c", g=2), writes=["BI"])
        kb.dma("sp", Cn[0][:], I["s_c_re"][l].rearrange("(o g) c p -> (g c) o p", g=8), writes=["Cn0"])
        kb.dma("sp", Cn[1][:], I["s_c_im"][l].rearrange("(o g) c p -> (g c) o p", g=8), writes=["Cn1"])
        for h in range(8):
            kb.dma("sp", wstg[0][:], I["s_glu_w"][l][:, h * 128:(h + 1) * 128].rearrange("(k p) c -> p k c", p=128),
                   writes=["wstg"])
            kb.op("dve", lambda e, h=h: e.tensor_copy(out=wg[:, :, h * 128:(h + 1) * 128], in_=wstg[0][:]),
                  reads=["wstg"], writes=["wg"])
        N = lambda i: nat[:, i, :]
        kb.op("act", lambda e: e.activation(out=nat[:, 2, 2:4], in_=nat[:, 2, 0:2], func=AF.Exp), reads=["nat2"], writes=["nat2"])
        kb.op("dve", lambda e: e.tensor_copy(out=nat[:, 3, :].rearrange("s (g p) -> s g p", g=2),
                                             in_=nat[:, 2, 2:4].unsqueeze(2).to_broadcast([32, 2, 64])),
              reads=["nat2"], writes=["nat3"])
        kb.op("dve", lambda e: e.tensor_scalar(out=N(0), in0=N(0), scalar1=-1e-4, scalar2=None, op0=ALU.min),
              reads=["nat0"], writes=["nat0"])
        kb.op("dve", lambda e: e.tensor_tensor(out=N(4), in0=N(0), in1=N(3), op=ALU.mult), reads=["nat0", "nat3"], writes=["nat4"])
        kb.op("act", lambda e: e.activation(out=N(4), in_=N(4), func=AF.Exp), reads=["nat4"], writes=["nat4"])
        kb.op("dve", lambda e: e.tensor_tensor(out=N(5), in0=N(1), in1=N(3), op=ALU.mult), reads=["nat1", "nat3"], writes=["nat5"])
        self.sin_to("nat", N(6), N(5), N(12), nati[:, :], N(13), "nat6", "nat5")
        self.sin_to("nat", N(7), N(5), N(12), nati[:, :], N(13), "nat7", "nat5", shift=math.pi / 2)
        kb.op("dve", lambda e: e.tensor_tensor(out=N(8), in0=N(4), in1=N(7), op=ALU.mult), reads=["nat4", "nat7"], writes=["nat8"])
        kb.op("dve", lambda e: e.tensor_tensor(out=N(9), in0=N(4), in1=N(6), op=ALU.mult), reads=["nat4", "nat6"], writes=["nat9"])
        kb.op("dve", lambda e: e.tensor_tensor(out=N(12), in0=N(0), in1=N(0), op=ALU.mult), reads=["nat0"], writes=["nat12"])
        kb.op("dve", lambda e: e.tensor_tensor(out=N(13), in0=N(1), in1=N(1), op=ALU.mult), reads=["nat1"], writes=["nat13"])
        kb.op("dve", lambda e: e.tensor_tensor(out=N(12), in0=N(12), in1=N(13), op=ALU.add), reads=["nat12", "nat13"], writes=["nat12"])
        kb.op("dve", lambda e: e.reciprocal(out=N(12), in_=N(12)), reads=["nat12"], writes=["nat12"])
        kb.op("dve", lambda e: e.tensor_scalar(out=N(13), in0=N(8), scalar1=-1.0, scalar2=None, op0=ALU.add), reads=["nat8"], writes=["nat13"])
        kb.op("dve", lambda e: e.tensor_tensor(out=N(10), in0=N(13), in1=N(0), op=ALU.mult), reads=["nat13", "nat0"], writes=["nat10"])
        kb.op("dve", lambda e: e.tensor_tensor(out=N(11), in0=N(9), in1=N(1), op=ALU.mult), reads=["nat9", "nat1"], writes=["nat11"])
        kb.op("dve", lambda e: e.tensor_tensor(out=N(10), in0=N(10), in1=N(11), op=ALU.add), reads=["nat10", "nat11"], writes=["nat10"])
        kb.op("dve", lambda e: e.tensor_tensor(out=N(10), in0=N(10), in1=N(12), op=ALU.mult), reads=["nat10", "nat12"], writes=["nat10"])
        kb.op("dve", lambda e: e.tensor_tensor(out=N(11), in0=N(9), in1=N(0), op=ALU.mult), reads=["nat9", "nat0"], writes=["nat11"])
        kb.op("dve", lambda e: e.tensor_tensor(out=N(13), in0=N(13), in1=N(1), op=ALU.mult), reads=["nat13", "nat1"], writes=["nat13"])
        kb.op("dve", lambda e: e.tensor_tensor(out=N(11), in0=N(11), in1=N(13), op=ALU.subtract), reads=["nat11", "nat13"], writes=["nat11"])
        kb.op("dve", lambda e: e.tensor_tensor(out=N(11), in0=N(11), in1=N(12), op=ALU.mult), reads=["nat11", "nat12"], writes=["nat11"])
        p = self.rr("ptr", 2)
        pt, kpt = self.ptr[p], "ptr%d" % p
        for si, ni in enumerate((4, 5, 8, 9, 10, 11)):
            kb.op("pe", lambda e, si=si, ni=ni, pt=pt: e.transpose(out=pt[:, si * 32:(si + 1) * 32], in_=nat[:, ni, :],
                                                                  identity=self.ident[:32, :32]),
                  reads=["nat%d" % ni, "ident"], writes=[kpt])
        kb.op("dve", lambda e, pt=pt: e.tensor_copy(out=PT[:, :, :], in_=pt[:, 0:192].rearrange("p (s c) -> p s c", s=6)),
              reads=[kpt], writes=["PT"])
        bc = lambda si: PT[:, si, :].unsqueeze(2).to_broadcast([128, 32, 16])
        kb.op("dve", lambda e: e.tensor_tensor(out=bbr[:], in0=BR[:], in1=bc(4), op=ALU.mult), reads=["BR", "PT"], writes=["bbr"])
        kb.op("dve", lambda e: e.tensor_tensor(out=btmp, in0=BI[:], in1=bc(5), op=ALU.mult), reads=["BI", "PT"], writes=["xin"])
        kb.op("dve", lambda e: e.tensor_tensor(out=bbr[:], in0=bbr[:], in1=btmp, op=ALU.subtract), reads=["bbr", "xin"], writes=["bbr"])
        kb.op("dve", lambda e: e.tensor_tensor(out=bbi[:], in0=BI[:], in1=bc(4), op=ALU.mult), reads=["BI", "PT"], writes=["bbi"])
        kb.op("dve", lambda e: e.tensor_tensor(out=btmp, in0=BR[:], in1=bc(5), op=ALU.mult), reads=["BR", "PT", "bbr"], writes=["xin"])
        kb.op("dve", lambda e: e.tensor_tensor(out=bbi[:], in0=bbi[:], in1=btmp, op=ALU.add), reads=["bbi", "xin"], writes=["bbi"])
        for ri, (bb, kbb) in enumerate(((bbr, "bbr"), (bbi, "bbi"))):
            for oc in range(8):
                kb.op("dve", lambda e, bb=bb, oc=oc: e.tensor_tensor(
                    out=xin[:, :].rearrange("p (q g c) -> p q g c", q=4, g=8),
                    in0=bb[:, oc * 4:oc * 4 + 4, None, :].to_broadcast([128, 4, 8, 16]),
                    in1=m4[:, :, :, None].to_broadcast([128, 4, 8, 16]), op=ALU.mult),
                    reads=[kbb, "m4"], writes=["xin"])
                p = self.rr("ptr", 2)
                pt, kpt = self.ptr[p], "ptr%d" % p
                for q in range(4):
                    kb.op("pe", lambda e, q=q, pt=pt: e.transpose(out=pt[:, q * 128:(q + 1) * 128], in_=xin[:, q * 128:(q + 1) * 128],
                                                                  identity=self.ident[:, :]),
                          reads=["xin", "ident"], writes=[kpt])
                kb.op("act", lambda e, pt=pt, oc=oc, ri=ri: e.activation(
                    out=Bz[ri][:, oc * 4:oc * 4 + 4, :], in_=pt[:, :].rearrange("p (q m) -> p q m", q=4), func=AF.Copy),
                    reads=[kpt], writes=[("Bz", ri)])
        for ri in range(2):
            for oc in range(8):
                kb.op("dve", lambda e, ri=ri, oc=oc: e.tensor_tensor(
                    out=xin[:, :].rearrange("p (q g s) -> p q g s", q=4, g=2),
                    in0=Cn[ri][:, oc, None, None, :].to_broadcast([128, 4, 2, 64]),
                    in1=m3[:, :, :, None].to_broadcast([128, 4, 2, 64]), op=ALU.mult),
                    reads=["Cn%d" % ri, "m3"], writes=["xin"])
                p = self.rr("ptr", 2)
                pt, kpt = self.ptr[p], "ptr%d" % p
                for q in range(4):
                    kb.op("pe", lambda e, q=q, pt=pt: e.transpose(out=pt[:, q * 128:(q + 1) * 128], in_=xin[:, q * 128:(q + 1) * 128],
                                                                  identity=self.ident[:, :]),
                          reads=["xin", "ident"], writes=[kpt])
                kb.op("act", lambda e, pt=pt, oc=oc, ri=ri: e.activation(
                    out=Cz[ri][:, oc * 4:oc * 4 + 4, :], in_=pt[:, :].rearrange("p (q m) -> p q m", q=4), func=AF.Copy,
                    scale=(1.0 if ri == 0 else -1.0)),
                    reads=[kpt], writes=[("Cz", ri)])
        for g in range(16):
            for s8 in range(2):
                sc = g * 2 + s8
                kb.op("dve", lambda e, s8=s8, sc=sc: e.tensor_scalar(out=ang[:, s8, :], in0=iota1[:, :], scalar1=PT[:, 1, sc:sc + 1],
                                                                     scalar2=None, op0=ALU.mult),
                      reads=["iota1", "PT"], writes=["ang"])
            self.sin_to("ang", sinT[:, g * 2:(g + 1) * 2, :], ang[:], angf[:], angi[:], angm[:], ("sinT", g), "ang")
            self.sin_to("ang", cosT[:, g * 2:(g + 1) * 2, :], ang[:], angf[:], angi[:], angm[:], ("cosT", g), "ang",
                        shift=math.pi / 2)
        tabs = [("sinT", g) for g in range(16)] + [("cosT", g) for g in range(16)]
        kb.op("dve", lambda e: e.memset(car[0][:], 0.0), writes=[("car", 0, sc_) for sc_ in range(32)])
        kb.op("dve", lambda e: e.memset(car[1][:], 0.0), writes=[("car", 1, sc_) for sc_ in range(32)])

        if NS:
            for ri, nm in enumerate(("state_s5_re", "state_s5_im")):
                p = self.rr("ptr", 2)
                pt, kpt = self.ptr[p], "ptr%d" % p
                for qq in range(8):
                    kb.dma("sp", sst[:NS, :], I[nm][l].rearrange("b g p -> b (g p)")[:, qq * 512:(qq + 1) * 512], writes=["sst"])
                    for s8 in range(4):
                        sc = qq * 4 + s8
                        kb.op("pe", lambda e, sc=sc, s8=s8, pt=pt: e.transpose(out=pt[:, sc * NS:(sc + 1) * NS], in_=sst[:NS, s8 * 128:(s8 + 1) * 128],
                                                                      identity=self.ident[:NS, :NS]),
                              reads=["sst", "ident"], writes=[kpt])
                kb.op("dve", lambda e, pt=pt, ri=ri: e.tensor_copy(out=s0T[ri][:, :, :NS],
                                                                  in_=pt[:, :32 * NS].rearrange("p (s b) -> p s b", s=32)),
                      reads=[kpt], writes=[("s0T", ri)])

        kb.barrier()
        self.pes.close()
        self.pes = outer_pes
        uf = [sbp("uf%d" % i, (128, 256)) for i in range(2)]
        ub = [sbp("ub%d" % i, (128, 256), BF16) for i in range(2)]
        WS = []
        for w_ in range(2):
            WS.append(([sbp("tq%d_%d" % (w_, i), (128, 256)) for i in range(4)],
                       [sbp("bh%d_%d" % (w_, i), (128, 256)) for i in range(2)],
                       [sbp("sh%d_%d" % (w_, i), (128, 256)) for i in range(2)],
                       [sbp("sbf%d_%d" % (w_, i), (128, 256), BF16) for i in range(2)]))
        ysg = sbp("ysg", (128, KC, 256)); ysb = sbp("ysb", (128, KC, 256), BF16)
        yt = [sbp("yt%d" % i, (128, 256)) for i in range(2)]
        zst = [sbp("zst%d" % i, (128, 256)) for i in range(2)]
        aout = [sbp("aout%d" % i, (128, 256), BF16) for i in range(2)]
        subblocks = []
        for bi, (t0b, nb) in enumerate(self.blocks):
            for t0 in range(t0b, t0b + nb, 256):
                subblocks.append((bi, t0, min(256, t0b + nb - t0)))
        for (bi, t0, n) in subblocks:
            is_s = (t0 >= T)
            nsub = n // 256
            for oc in range(8):
                j = self.rr("uf", 2)
                kuf, kub = "uf%d" % j, "ub%d" % j
                row = O_U + oc * 128
                kb.dma("sp", uf[j][:, :n], S["zT"][row:row + 128, t0:t0 + n], reads=[("zT", row, bi)], writes=[kuf])
                kb.op("pool", lambda e, j=j: e.tensor_copy(out=ub[j][:, :n], in_=uf[j][:, :n]), reads=[kuf], writes=[kub])
                py = self.rr("ptr", 2)
                pys, kpys = self.ptr[py], "ptr%d" % py
                def sc_gen(q, W):
                    tq, bh, sh, sbf = W
                    wk = lambda n_: (n_, id(W))
                    sc = oc * 4 + q
                    pa = self.rr("pp", 4); pb = self.rr("pp", 4)
                    A, B = self.pp[pa], self.pp[pb]
                    kA, kB = "pp%d" % pa, "pp%d" % pb
                    kb.op("pe", lambda e, A=A, sc=sc, j=j: e.matmul(A[:, :n], lhsT=Bz[0][:, sc, :], rhs=ub[j][:, :n], start=True, stop=True),
                          reads=[("Bz", 0), kub], writes=[kA])
                    kb.op("pe", lambda e, B=B, sc=sc, j=j: e.matmul(B[:, :n], lhsT=Bz[1][:, sc, :], rhs=ub[j][:, :n], start=True, stop=True),
                          reads=[("Bz", 1), kub], writes=[kB])
                    yield
                    if not is_s:
                        v3 = lambda x: x[:, :n].rearrange("p (m j) -> p m j", j=256)
                        cosv = cosT[:, sc, None, :].to_broadcast([128, nsub, 256])
                        sinv = sinT[:, sc, None, :].to_broadcast([128, nsub, 256])
                        kb.op("dve", lambda e, A=A, cosv=cosv: e.tensor_tensor(out=v3(tq[0]), in0=v3(A), in1=cosv, op=ALU.mult),
                              reads=[kA] + tabs, writes=[wk("tq0")])
                        kb.op("dve", lambda e, B=B, sinv=sinv: e.tensor_tensor(out=v3(tq[1]), in0=v3(B), in1=sinv, op=ALU.mult),
                              reads=[kB] + tabs, writes=[wk("tq1")])
                        kb.op("dve", lambda e, B=B, cosv=cosv: e.tensor_tensor(out=v3(tq[2]), in0=v3(B), in1=cosv, op=ALU.mult),
                              reads=[kB] + tabs, writes=[wk("tq2")])
                        kb.op("dve", lambda e, A=A, sinv=sinv: e.tensor_tensor(out=v3(tq[3]), in0=v3(A), in1=sinv, op=ALU.mult),
                              reads=[kA] + tabs, writes=[wk("tq3")])
                        kb.op("pool", lambda e: e.tensor_tensor(out=bh[0][:, :n], in0=tq[0][:, :n], in1=tq[1][:, :n], op=ALU.add),
                              reads=[wk("tq0"), wk("tq1")], writes=[wk("bh0")])
                        kb.op("pool", lambda e: e.tensor_tensor(out=bh[1][:, :n], in0=tq[2][:, :n], in1=tq[3][:, :n], op=ALU.subtract),
                              reads=[wk("tq2"), wk("tq3")], writes=[wk("bh1")])
                        yield
                        rho = PT[:, 0, sc:sc + 1].to_broadcast([128, 256])
                        c128 = cosT[:, sc, 255:256]
                        s128 = sinT[:, sc, 255:256]
                        for m in range(nsub):
                            sl = slice(m * 256, (m + 1) * 256)
                            for ri in range(2):
                                kb.op("dve", lambda e, ri=ri, sl=sl, sc=sc, rho=rho: e.tensor_tensor_scan(
                                    out=sh[ri][:, sl], data0=rho, data1=bh[ri][:, sl], initial=car[ri][:, sc:sc + 1],
                                    op0=ALU.mult, op1=ALU.add), reads=[wk("bh%d" % ri), ("car", ri, sc), "PT"], writes=[wk("sh%d" % ri)])
                            lr = sh[0][:, m * 256 + 255:m * 256 + 256]
                            li = sh[1][:, m * 256 + 255:m * 256 + 256]
                            kb.op("dve", lambda e, li=li, s128=s128: e.tensor_scalar(out=ctmp[:, 2 * (q % 2):2 * (q % 2) + 1], in0=li, scalar1=s128, scalar2=None, op0=ALU.mult),
                                  reads=[wk("sh1")] + tabs, writes=[wk("ctmp")])
                            kb.op("dve", lambda e, li=li, c128=c128: e.tensor_scalar(out=ctmp[:, 2 * (q % 2) + 1:2 * (q % 2) + 2], in0=li, scalar1=c128, scalar2=None, op0=ALU.mult),
                                  reads=[wk("sh1")] + tabs, writes=[wk("ctmp")])
                            kb.op("dve", lambda e, lr=lr, c128=c128, sc=sc: e.scalar_tensor_tensor(
                                out=car[0][:, sc:sc + 1], in0=lr, scalar=c128, in1=ctmp[:, 2 * (q % 2):2 * (q % 2) + 1], op0=ALU.mult, op1=ALU.subtract),
                                reads=[wk("sh0"), wk("ctmp")] + tabs, writes=[("car", 0, sc)])
                            kb.op("dve", lambda e, lr=lr, s128=s128, sc=sc: e.scalar_tensor_tensor(
                                out=car[1][:, sc:sc + 1], in0=lr, scalar=s128, in1=ctmp[:, 2 * (q % 2) + 1:2 * (q % 2) + 2], op0=ALU.mult, op1=ALU.add),
                                reads=[wk("sh0"), wk("ctmp")] + tabs, writes=[("car", 1, sc)])
                        yield
                        kb.op("pool", lambda e, cosv=cosv: e.tensor_tensor(out=v3(tq[0]), in0=v3(sh[0]), in1=cosv, op=ALU.mult),
                              reads=[wk("sh0")] + tabs, writes=[wk("tq0")])
                        kb.op("pool", lambda e, sinv=sinv: e.tensor_tensor(out=v3(tq[1]), in0=v3(sh[1]), in1=sinv, op=ALU.mult),
                              reads=[wk("sh1")] + tabs, writes=[wk("tq1")])
                        kb.op("dve", lambda e, sinv=sinv: e.tensor_tensor(out=v3(tq[2]), in0=v3(sh[0]), in1=sinv, op=ALU.mult),
                              reads=[wk("sh0")] + tabs, writes=[wk("tq2")])
                        kb.op("dve", lambda e, cosv=cosv: e.tensor_tensor(out=v3(tq[3]), in0=v3(sh[1]), in1=cosv, op=ALU.mult),
                              reads=[wk("sh1")] + tabs, writes=[wk("tq3")])
                        kb.op("pool", lambda e: e.tensor_tensor(out=sbf[0][:, :n], in0=tq[0][:, :n], in1=tq[1][:, :n], op=ALU.subtract),
                              reads=[wk("tq0"), wk("tq1")], writes=[wk("sbf0")])
                        kb.op("dve", lambda e: e.tensor_tensor(out=sbf[1][:, :n], in0=tq[2][:, :n], in1=tq[3][:, :n], op=ALU.add),
                              reads=[wk("tq2"), wk("tq3")], writes=[wk("sbf1")])
                    else:
                        lbre = PT[:, 2, sc:sc + 1]
                        lbim = PT[:, 3, sc:sc + 1]
                        s0r, s0i = s0T[0][:, sc, :n], s0T[1][:, sc, :n]
                        kb.op("dve", lambda e, s0i=s0i, lbim=lbim: e.tensor_scalar(out=tq[0][:, :n], in0=s0i, scalar1=lbim, scalar2=None, op0=ALU.mult),
                              reads=[("s0T", 1), "PT"], writes=[wk("tq0")])
                        kb.op("dve", lambda e, s0r=s0r, lbre=lbre: e.scalar_tensor_tensor(out=tq[0][:, :n], in0=s0r, scalar=lbre, in1=tq[0][:, :n],
                                                                               op0=ALU.mult, op1=ALU.subtract),
                              reads=[("s0T", 0), "PT", wk("tq0")], writes=[wk("tq0")])
                        kb.op("dve", lambda e, A=A, sc=sc: e.tensor_tensor(out=snw[0][:, sc, :n], in0=A[:, :n], in1=tq[0][:, :n], op=ALU.add),
                              reads=[kA, wk("tq0")], writes=[("snw", 0)])
                        kb.op("dve", lambda e, s0r=s0r, lbim=lbim: e.tensor_scalar(out=tq[1][:, :n], in0=s0r, scalar1=lbim, scalar2=None, op0=ALU.mult),
                              reads=[("s0T", 0), "PT"], writes=[wk("tq1")])
                        kb.op("dve", lambda e, s0i=s0i, lbre=lbre: e.scalar_tensor_tensor(out=tq[1][:, :n], in0=s0i, scalar=lbre, in1=tq[1][:, :n],
                                                                               op0=ALU.mult, op1=ALU.add),
                              reads=[("s0T", 1), "PT", wk("tq1")], writes=[wk("tq1")])
                        kb.op("dve", lambda e, B=B, sc=sc: e.tensor_tensor(out=snw[1][:, sc, :n], in0=B[:, :n], in1=tq[1][:, :n], op=ALU.add),
                              reads=[kB, wk("tq1")], writes=[("snw", 1)])
                        for ri in range(2):
                            kb.op("pool", lambda e, ri=ri, sc=sc: e.tensor_copy(out=sbf[ri][:, :n], in_=snw[ri][:, sc, :n]),
                                  reads=[("snw", ri)], writes=[wk("sbf%d" % ri)])
                    yield
                    for ri in range(2):
                        kb.op("pe", lambda e, ri=ri, sc=sc, q=q, pys=pys: e.matmul(pys[:, :n], lhsT=Cz[ri][:, sc, :], rhs=sbf[ri][:, :n],
                                                                                start=(q == 0 and ri == 0), stop=(q == 3 and ri == 1)),
                              reads=[("Cz", ri), wk("sbf%d" % ri)], writes=[kpys])
                for qp in range(2):
                    alive = [sc_gen(2 * qp + w_, WS[w_]) for w_ in range(2)]
                    while alive:
                        for g_ in list(alive):
                            try:
                                next(g_)
                            except StopIteration:
                                alive.remove(g_)
                y0, y1 = yt[0], yt[1]
                kb.op("dve", lambda e, j=j, oc=oc, pys=pys: e.scalar_tensor_tensor(out=y0[:, :n], in0=uf[j][:, :n], scalar=dcol[:, oc:oc + 1],
                                                                             in1=pys[:, :n], op0=ALU.mult, op1=ALU.add),
                      reads=[kuf, "dcol", kpys], writes=["yt0"])
                kb.op("act", lambda e: e.activation(out=y1[:, :n], in_=y0[:, :n], func=AF.Square), reads=["yt0"], writes=["yt1"])
                kb.op("dve", lambda e: e.tensor_scalar(out=y1[:, :n], in0=y1[:, :n], scalar1=0.044715, scalar2=1.0, op0=ALU.mult, op1=ALU.add),
                      reads=["yt1"], writes=["yt1"])
                kb.op("dve", lambda e: e.tensor_tensor(out=y1[:, :n], in0=y1[:, :n], in1=y0[:, :n], op=ALU.mult), reads=["yt1", "yt0"], writes=["yt1"])
                kb.op("act", lambda e: e.activation(out=y1[:, :n], in_=y1[:, :n], func=AF.Sigmoid, scale=2.0 * math.sqrt(2.0 / math.pi)),
                      reads=["yt1"], writes=["yt1"])
                kb.op("dve", lambda e, oc=oc: e.tensor_tensor(out=ysg[:, oc, :n], in0=y1[:, :n], in1=y0[:, :n], op=ALU.mult),
                      reads=["yt1", "yt0"], writes=[("ysg", oc)])
                kb.op("pool", lambda e, oc=oc: e.tensor_copy(out=ysb[:, oc, :n], in_=ysg[:, oc, :n]), reads=[("ysg", oc)], writes=[("ysb", oc)])
            for ec in range(8):
                p = self.rr("pp", 4)
                pp, kpp = self.pp[p], "pp%d" % p
                for kc in range(KC):
                    kb.op("pe", lambda e, kc=kc, ec=ec, pp=pp: e.matmul(pp[:, :n], lhsT=wg[:, kc, ec * 128:(ec + 1) * 128], rhs=ysb[:, kc, :n],
                                                                      start=(kc == 0), stop=(kc == KC - 1)),
                          reads=["wg"] + [("ysb", k) for k in range(KC)], writes=[kpp])
                zj = self.rr("zst", 2)
                kz = "zst%d" % zj
                row = O_ZS + ec * 128
                kb.dma("sp", zst[zj][:, :n], S["zT"][row:row + 128, t0:t0 + n], reads=[("zT", row, bi)], writes=[kz])
                kb.op("act", lambda e, zj=zj: e.activation(out=zst[zj][:, :n], in_=zst[zj][:, :n], func=AF.Silu), reads=[kz], writes=[kz])
                kb.op("act", lambda e, pp=pp, ec=ec: e.activation(out=yt[0][:, :n], in_=pp[:, :n], func=AF.Sigmoid, bias=gbcol[:, ec:ec + 1]),
                      reads=[kpp, "gbcol"], writes=["yt0"])
                kb.op("dve", lambda e, ec=ec: e.tensor_tensor(out=yt[0][:, :n], in0=yt[0][:, :n], in1=ysg[:, ec, :n], op=ALU.mult),
                      reads=["yt0", ("ysg", ec)], writes=["yt0"])
                aj = self.rr("aout", 2)
                kb.op("dve", lambda e, aj=aj, zj=zj: e.tensor_tensor(out=aout[aj][:, :n], in0=yt[0][:, :n], in1=zst[zj][:, :n], op=ALU.mult),
                      reads=["yt0", kz], writes=["aout%d" % aj])
                kb.dma("pool", S["act_s"][ec * 128:(ec + 1) * 128, t0:t0 + n], aout[aj][:, :n],
                       reads=["aout%d" % aj], writes=[("act", "s", bi)])
        for ri, nm in enumerate(("s5_re", "s5_im")):
            p = self.rr("ptr", 2)
            pt, kpt = self.ptr[p], "ptr%d" % p
            kb.op("pe", lambda e, pt=pt, ri=ri: e.transpose(out=pt[:32, 0:128], in_=car[ri][:, :], identity=self.ident[:, :]),
                  reads=[("car", ri, sc_) for sc_ in range(32)] + ["ident"], writes=[kpt])
            kb.op("dve", lambda e, pt=pt: e.tensor_copy(out=sst[:32, 0:128], in_=pt[:32, 0:128]), reads=[kpt], writes=["sst"])
            kb.dma("pool", O[nm + "_prompt"][l].rearrange("(s q) -> s q", q=128), sst[:32, 0:128], reads=["sst"], writes=[],
                   is_output=True)
            if NS:
                for g in range(8):
                    p = self.rr("ptr", 2)
                    pt, kpt = self.ptr[p], "ptr%d" % p
                    for s4 in range(4):
                        sc = g * 4 + s4
                        kb.op("pe", lambda e, pt=pt, ri=ri, sc=sc, s4=s4: e.transpose(out=pt[:NS, s4 * 128:(s4 + 1) * 128], in_=snw[ri][:, sc, :NS],
                                                                                   identity=self.ident[:, :]),
                              reads=[("snw", ri), "ident"], writes=[kpt])
                    kb.op("dve", lambda e, pt=pt, g=g: e.tensor_copy(out=sst[:NS, 0:512], in_=pt[:NS, :]), reads=[kpt], writes=["sst"])
                    kb.dma("pool", O[nm + "_sample"][l].rearrange("b g p -> b (g p)")[:, g * 512:(g + 1) * 512], sst[:NS, 0:512],
                           reads=["sst"], writes=[], is_output=True)
        self.end_phase()

    def load_sq(self, name, dram):
        kb = self.kb
        dst = self.wsq[name]
        for h in range(2):
            i = self.rr("w", 2)
            ws, kws = self.wst[i], "wst%d" % i
            kb.dma("sp", ws[:, :, :512], dram[:, h * 512:(h + 1) * 512].rearrange("(k p) c -> p k c", p=128),
                   writes=[kws])
            kb.op("dve", lambda e, ws=ws, h=h: e.tensor_copy(out=dst[:, :, h * 512:(h + 1) * 512], in_=ws[:, :, :512]),
                  reads=[kws], writes=[("wsq", name)])

    def phase3(self, l):
        kb = self.kb
        I, S = self.I, self.S
        last = (l == self.DEPTH - 1)
        self.begin_phase()
        self.wst = [self.sbp("wst%d" % i, (128, KC, 512)) for i in range(2)]
        self.wsq = {n: self.sbp("wsq_" + n, (128, KC, D), BF16) for n in ("bm", "br", "bs", "out")}
        self.actb = [self.sbp("actb%d" % i, (128, KC, 512), BF16) for i in range(2)]
        self.gmt = [self.sbp("gmt%d" % i, (128, 512)) for i in range(2)]
        self.mrg = self.sbp("mrg", (128, KC, 512), BF16)
        self.macc = self.sbp("macc", (128, KC, 512))
        self.ln_alloc()
        for nme in ("bm", "br", "bs", "out"):
            self.load_sq(nme, I["w_" + nme][l])
        self.load_ln_params(I["ln_g"][l], I["ln_b"][l])
        for bi, (t0, n) in enumerate(self.blocks):
            for ec in range(KC):
                for b, bn in enumerate("mrs"):
                    pass
            acts = {}
            for b, bn in enumerate("mrs"):
                if not self.have_branch(bn):
                    continue
                j = self.rr("actb", 2)
                at, kat = self.actb[j], "actb%d" % j
                kb.dma("sp", at[:, :, :n], S["act_" + bn][:, t0:t0 + n].rearrange("(k p) t -> p k t", p=128),
                       reads=[("act", bn, bi)], writes=[kat])
                for ec in range(KC):
                    p = self.rr("pp", 4)
                    pp, kpp = self.pp[p], "pp%d" % p
                    for kc in range(KC):
                        kb.op("pe", lambda e, kc=kc, pp=pp, ec=ec, at=at, bn=bn: e.matmul(
                            pp[:, :n], lhsT=self.wsq["b" + bn][:, kc, ec * 128:(ec + 1) * 128], rhs=at[:, kc, :n],
                            start=(kc == 0), stop=(kc == KC - 1)),
                            reads=[kat, ("wsq", "b" + bn)], writes=[kpp])
                    g = self.rr("gmt", 2)
                    gt, kgt = self.gmt[g], "gmt%d" % g
                    row = O_GM + b * D + ec * 128
                    kb.dma("sp", gt[:, :n], S["zT"][row:row + 128, t0:t0 + n],
                           reads=[("zT", row, bi)], writes=[kgt])
                    kb.op("act", lambda e, gt=gt: e.activation(out=gt[:, :n], in_=gt[:, :n], func=AF.Sigmoid),
                          reads=[kgt], writes=[kgt])
                    first = (bn == self.first_branch())
                    lastb = (bn == self.last_branch())
                    kmf = ("mrgf", ec)
                    acc = self.macc[:, ec, :n]
                    if first:
                        kb.op("dve", lambda e, acc=acc, gt=gt, pp=pp: e.tensor_tensor(out=acc, in0=pp[:, :n], in1=gt[:, :n],
                                                                                 op=ALU.mult),
                              reads=[kpp, kgt], writes=[("macc", ec)])
                    else:
                        kb.op("dve", lambda e, gt=gt, pp=pp: e.tensor_tensor(out=gt[:, :n], in0=pp[:, :n], in1=gt[:, :n],
                                                                        op=ALU.mult),
                              reads=[kpp, kgt], writes=[kgt])
                        kb.op("dve", lambda e, acc=acc, gt=gt: e.tensor_tensor(out=acc, in0=acc, in1=gt[:, :n], op=ALU.add),
                              reads=[kgt, ("macc", ec)], writes=[("macc", ec)])
                    if lastb:
                        kb.op("dve", lambda e, acc=acc, ec=ec: e.tensor_copy(out=self.mrg[:, ec, :n], in_=acc),
                              reads=[("macc", ec)], writes=[("mrg", ec)])
            for tt in range(0, n, 128):
                nr = min(128, n - tt)
                r0 = t0 + tt
                ti = r0 // 128
                j = self.rr("xt", 2)
                xt, kxt = self.xt[j], "xt%d" % j
                kb.dma("sp", xt[:nr, :], S["xs"][r0:r0 + nr, :], reads=[("xs", ti)], writes=[kxt])
                if self.first_branch() is not None:
                    for h in range(2):
                        p = self.rr("pp", 4)
                        pp, kpp = self.pp[p], "pp%d" % p
                        for kc in range(KC):
                            kb.op("pe", lambda e, kc=kc, pp=pp, h=h, tt=tt, nr=nr: e.matmul(
                                pp[:nr, :], lhsT=self.mrg[:, kc, tt:tt + nr], rhs=self.wsq["out"][:, kc, h * 512:(h + 1) * 512],
                                start=(kc == 0), stop=(kc == KC - 1)),
                                reads=[("mrg", k) for k in range(KC)] + [("wsq", "out")], writes=[kpp])
                        kb.op("dve", lambda e, xt=xt, pp=pp, h=h, nr=nr: e.scalar_tensor_tensor(
                            out=xt[:nr, h * 512:(h + 1) * 512], in0=xt[:nr, h * 512:(h + 1) * 512], scalar=ALPHA,
                            in1=pp[:nr, :], op0=ALU.mult, op1=ALU.add), reads=[kxt, kpp], writes=[kxt])
                else:
                    kb.op("dve", lambda e, xt=xt, nr=nr: e.tensor_scalar(out=xt[:nr, :], in0=xt[:nr, :], scalar1=ALPHA,
                                                                         scalar2=None, op0=ALU.mult),
                          reads=[kxt], writes=[kxt])
                fo = None
                if last:
                    fo = self.O["y_prompt"][r0:r0 + nr, :] if r0 < self.T else self.O["y_sample"][:, :]
                self.ln_tile(xt, kxt, r0, nr, ti, S["xs"][r0:r0 + nr, :], final_out=fo)
        self.end_phase()

    branches = ""
    NPI = 4
    CORE_BF16 = True

    def have_branch(self, bn):
        return bn in self.branches

    def first_branch(self):
        return self.branches[0] if self.branches else None

    def last_branch(self):
        return self.branches[-1] if self.branches else None


def make_in_map(inputs, c, NS, T, consts):
    m = {}
    m["x_prompt"] = np.ascontiguousarray(inputs["x_prompt"][c, :T])
    m["x_sample"] = np.ascontiguousarray(inputs["x_sample"][c * NS:(c + 1) * NS, 0])
    for n in ("ln_in_g", "ln_in_b", "w_in", "w_out", "ln_g", "ln_b", "w_bm", "w_br", "w_bs", "s_lam_re", "s_lam_im",
              "s_log_dt", "s_b_re", "s_b_im", "s_c_re", "s_c_im", "s_d", "s_glu_w", "s_glu_b",
              "m_conv_w", "m_conv_b", "m_wq", "m_wk", "m_wv", "m_ig_b", "m_fg_b", "m_norm_g", "m_skip",
              "r_mu", "r_w0", "r_w2", "r_a0", "r_a2", "r_k_k", "r_k_a", "r_r_k", "r_ln_g", "r_ln_b"):
        m[n] = np.ascontiguousarray(inputs[n])
    for n in ("state_s5_re", "state_s5_im", "state_mlstm_conv", "state_mlstm_c", "state_mlstm_n", "state_mlstm_m",
              "state_rwkv_wkv", "state_rwkv_shift"):
        m[n] = np.ascontiguousarray(inputs[n][:, c * NS:(c + 1) * NS])
    m.update(consts)
    return m


def kernel(**inputs):
    inputs = {k: np.asarray(v) for k, v in inputs.items()}
    T, NS, L = 2048, 16, 4
    Prog.branches = BRANCHES
    prog = Prog(T, NS, L)
    nc = prog.build()
    consts = host_consts()
    in_maps = [make_in_map(inputs, c, NS, T, consts) for c in range(8)]
    res = run_bass_kernel_spmd(nc, in_maps, core_ids=list(range(8)))
    rs = res.results
    f = np.float32

    def pstack(name, shape):
        if name in rs[0]:
            return np.stack([np.asarray(rs[c][name]).reshape((L,) + shape) for c in range(8)], 1).astype(f)
        return np.zeros((L, 8) + shape, f)

    def sstack(name, shape):
        if name in rs[0]:
            return np.concatenate([np.asarray(rs[c][name]).reshape((L, NS) + shape) for c in range(8)], 1).astype(f)
        return np.zeros((L, 8 * NS) + shape, f)

    y_prompt = np.stack([rs[c]["y_prompt"] for c in range(8)], 0).astype(f)
    y_sample = np.concatenate([rs[c]["y_sample"] for c in range(8)], 0)[:, None, :].astype(f)
    return (y_prompt, y_sample,
            pstack("c_prompt", (4, 256, 256)), sstack("c_sample", (4, 256, 256)),
            pstack("n_prompt", (4, 256)), sstack("n_sample", (4, 256)),
            pstack("m_prompt", (4,)), sstack("m_sample", (4,)),
            pstack("conv_prompt", (3, D)), sstack("conv_sample", (3, D)),
            pstack("wkv_prompt", (16, 64, 64)), sstack("wkv_sample", (16, 64, 64)),
            pstack("shift_prompt", (R_SHIFT_W,)), sstack("shift_sample", (R_SHIFT_W,)),
            pstack("s5_re_prompt", (64, 64)), sstack("s5_re_sample", (64, 64)),
            pstack("s5_im_prompt", (64, 64)), sstack("s5_im_sample", (64, 64)))
```

```python
import math
from contextlib import ExitStack
import numpy as np
import concourse.bass as bass
import concourse.mybir as mybir
from concourse.bass_utils import run_bass_kernel_spmd

F32 = mybir.dt.float32
BF16 = mybir.dt.bfloat16
I32 = mybir.dt.int32
ALU = mybir.AluOpType
AF = mybir.ActivationFunctionType
AX = mybir.AxisListType

D = 1024
KC = 8
N_IN = 12424
M_HEADS = 4
M_HD = 256
R_HEADS = 16
R_HD = 64
R_SHIFT_W = 3200
S_GROUPS = 64
S_STATE = 64
DEPTH_FULL = 4
ALPHA = (2.0 * DEPTH_FULL) ** 0.25
LN_EPS = 1e-5
RWKV_GN_EPS = 64e-5
NEG = -1e30

O_XM, O_IG, O_FG, O_OG, O_ZM = 0, 1024, 1028, 1032, 2056
O_RC, O_ZR, O_U, O_ZS, O_GM = 3080, 6280, 7304, 8328, 9352


class KB:
    def __init__(self, nc, es):
        self.nc = nc
        self.eng = dict(pe=nc.tensor, act=nc.scalar, dve=nc.vector, pool=nc.gpsimd, sp=nc.sync)
        self.stream = {e: [] for e in self.eng}
        self.csem = {e: es.enter_context(nc.semaphore("c_" + e)) for e in ("pe", "act", "dve", "pool")}
        self.ccnt = {e: 0 for e in self.csem}
        self.ring = {}
        for q, n in (("sp", 24), ("pool", 12), ("act", 6)):
            self.ring[q] = [[es.enter_context(nc.semaphore("d_%s%d" % (q, i))), 0] for i in range(n)]
        self.rpos = {q: 0 for q in self.ring}
        self.known = {e: {} for e in self.eng}
        self.lastw = {}
        self.readers = {}
        self.out_tokens = []
        self.n_ops = 0

    def _need(self, e, tok, same_ok=False):
        if tok is None:
            return
        sem, val, owner = tok
        if owner == e and (e == "pe" or same_ok):
            return
        k = id(sem)
        if self.known[e].get(k, 0) >= val:
            return
        self.known[e][k] = val
        self.eng[e].wait_ge(sem, val)

    ns = None
    local = frozenset()

    def _k(self, b):
        if self.ns is None:
            return b
        base = b[0] if isinstance(b, tuple) else b
        return (self.ns, b) if base in self.local else b

    def _deps(self, e, reads, writes):
        reads = [self._k(b) for b in reads]
        writes = [self._k(b) for b in writes]
        for b in reads:
            self._need(e, self.lastw.get(b))
        for b in writes:
            self._need(e, self.lastw.get(b))
            for tok in self.readers.get(b, ()):
                self._need(e, tok)

    def _commit(self, tok, reads, writes):
        reads = [self._k(b) for b in reads]
        writes = [self._k(b) for b in writes]
        for b in writes:
            self.lastw[b] = tok
            self.readers[b] = []
        for b in reads:
            self.readers.setdefault(b, []).append(tok)

    def op(self, e, fn, reads=(), writes=()):
        self._deps(e, reads, writes)
        self.ccnt[e] += 1
        tok = (self.csem[e], self.ccnt[e], e)
        fn(self.eng[e]).then_inc(self.csem[e], 1)
        self._commit(tok, reads, writes)
        self.n_ops += 1

    def dma(self, q, out, in_, reads=(), writes=(), is_output=False, **kw):
        self._deps(q, reads, writes)
        ring = self.ring[q]
        slot = ring[self.rpos[q] % len(ring)]
        self.rpos[q] += 1
        sem = slot[0]
        if slot[1] > 0:
            self._need(q, (sem, slot[1], None))
        slot[1] += 16
        tok = (sem, slot[1], None)
        self.eng[q].dma_start(out=out, in_=in_, **kw).then_inc(sem, 16)
        self._commit(tok, reads, writes)
        if is_output:
            self.out_tokens.append(tok)
        self.n_ops += 1

    def finish(self):
        for q in self.ring:
            for sem, val in self.ring[q]:
                if val > 0:
                    self._need("sp", (sem, val, None))
        for e in self.csem:
            if self.ccnt[e] > 0:
                self._need("sp", (self.csem[e], self.ccnt[e], e))

    def barrier(self):
        for e in self.eng:
            for q in self.ring:
                for sem, val in self.ring[q]:
                    if val > 0:
                        self._need(e, (sem, val, None))
            for e2 in self.csem:
                if self.ccnt[e2] > 0 and e2 != e:
                    self._need(e, (self.csem[e2], self.ccnt[e2], e2))

    def emit(self):
        pass


BRANCHES = "mrs"


def host_consts():
    c = {}
    c["ident"] = np.eye(128, dtype=np.float32)
    m3 = np.zeros((128, 4, 2), np.float32)
    m4 = np.zeros((128, 4, 8), np.float32)
    for p in range(128):
        g8 = p // 16
        gl = p // 64
        for q in range(4):
            for g in range(2):
                if g8 == 2 * q + g:
                    m3[p, q, g] = 1.0
            for gg in range(8):
                if gg == 2 * q + gl:
                    m4[p, q, gg] = 1.0
    c["mask3"], c["mask4"] = m3, m4
    selh = np.zeros((4, 4, 128), np.float32)
    for h in range(4):
        selh[h, h, :] = 1.0
    c["selh"] = selh
    c["ones4"] = np.ones((4, 128), np.float32)
    cn = np.zeros((128, 128), np.float32)
    for s_ in range(128):
        cn[s_, :s_] = 1.0e30
    c["causneg"] = cn
    rm = np.zeros((128, 384), np.float32)
    for p in range(128):
        for q in range(128):
            if p // 64 == q // 64:
                s_, t_ = p % 64, q % 64
                rm[p, q] = 1.0 if s_ < t_ else 0.0
                rm[p, 128 + q] = 1.0 if s_ <= t_ else 0.0
                rm[p, 256 + q] = 1.0 if s_ > t_ else 0.0
    c["rmasks"] = rm
    c["iota1"] = np.tile(np.arange(1, 257, dtype=np.float32)[None, :], (128, 1))
    return c


class Prog:
    def __init__(self, T, NS, DEPTH):
        self.T, self.NS, self.DEPTH = T, NS, DEPTH
        self.NT = T + NS
        assert T % 128 == 0
        self.ntile = T // 128
        self.tiles = [(i * 128, 128) for i in range(self.ntile)] + [(T, NS)]
        self.blocks = []
        t = 0
        while t < T:
            n = min(512, T - t)
            self.blocks.append((t, n))
            t += n
        self.blocks.append((T, NS))

    def build(self):
        nc = bass.Bass("TRN2", target_bir_lowering=False)
        self.nc = nc
        T, NS, L, NT = self.T, self.NS, self.DEPTH, self.NT
        dt = nc.dram_tensor

        def inp(name, shape, dtype=F32):
            return dt(name, list(shape), dtype, kind="ExternalInput").ap()

        def outp(name, shape):
            return dt(name, list(shape), F32, kind="ExternalOutput").ap()

        def scr(name, shape, dtype=F32):
            return dt(name, list(shape), dtype, kind="Internal").ap()

        I = {}
        I["x_prompt"] = inp("x_prompt", (T, D))
        I["x_sample"] = inp("x_sample", (NS, D))
        I["ident"] = inp("ident", (128, 128))
        for n, s in (("ln_in_g", (D,)), ("ln_in_b", (D,)), ("w_in", (L, D, N_IN)), ("w_out", (L, D, D)),
                     ("ln_g", (L, D)), ("ln_b", (L, D)), ("w_bm", (L, D, D)), ("w_br", (L, D, D)),
                     ("w_bs", (L, D, D)), ("s_lam_re", (L, 64, 64)), ("s_lam_im", (L, 64, 64)), ("s_log_dt", (L, 64)),
                     ("s_b_re", (L, 64, 64, 16)), ("s_b_im", (L, 64, 64, 16)), ("s_c_re", (L, 64, 16, 64)),
                     ("s_c_im", (L, 64, 16, 64)), ("s_d", (L, D)), ("s_glu_w", (L, D, D)), ("s_glu_b", (L, D)),
                     ("state_s5_re", (L, NS, 64, 64)), ("state_s5_im", (L, NS, 64, 64)),
                     ("state_mlstm_conv", (L, NS, 3, D)), ("state_mlstm_c", (L, NS, 4, 256, 256)),
                     ("state_mlstm_n", (L, NS, 4, 256)), ("state_mlstm_m", (L, NS, 4)),
                     ("m_conv_w", (L, 4, D)), ("m_conv_b", (L, D)), ("m_wq", (L, 4, 256, 256)), ("m_wk", (L, 4, 256, 256)),
                     ("m_wv", (L, 4, 256, 256)), ("m_ig_b", (L, 4)), ("m_fg_b", (L, 4)), ("m_norm_g", (L, D)), ("m_skip", (L, D)),
                     ("state_rwkv_wkv", (L, NS, 16, 64, 64)), ("state_rwkv_shift", (L, NS, R_SHIFT_W)),
                     ("r_mu", (L, R_SHIFT_W)), ("r_w0", (L, D)), ("r_w2", (L, 64, D)), ("r_a0", (L, D)), ("r_a2", (L, 64, D)),
                     ("r_k_k", (L, D)), ("r_k_a", (L, D)), ("r_r_k", (L, 16, 64)), ("r_ln_g", (L, D)), ("r_ln_b", (L, D)),
                     ("rmasks", (128, 384)),
                     ("selh", (4, 4, 128)), ("causneg", (128, 128)), ("ones4", (4, 128)),
                     ("mask3", (128, 4, 2)), ("mask4", (128, 4, 8)), ("iota1", (128, 256))):
            I[n] = inp(n, s)
        self.I = I
        O = {}
        O["y_prompt"] = outp("y_prompt", (T, D))
        O["y_sample"] = outp("y_sample", (NS, D))
        O["c_prompt"] = outp("c_prompt", (L, 4, 256, 256))
        O["c_sample"] = outp("c_sample", (L, NS, 4, 256, 256))
        O["n_prompt"] = outp("n_prompt", (L, 4, 256))
        O["n_sample"] = outp("n_sample", (L, NS, 4, 256))
        O["m_prompt"] = outp("m_prompt", (L, 4))
        O["m_sample"] = outp("m_sample", (L, NS, 4))
        O["conv_prompt"] = outp("conv_prompt", (L, 3, D))
        O["conv_sample"] = outp("conv_sample", (L, NS, 3, D))
        O["wkv_prompt"] = outp("wkv_prompt", (L, 16, 64, 64))
        O["wkv_sample"] = outp("wkv_sample", (L, NS, 16, 64, 64))
        O["shift_prompt"] = outp("shift_prompt", (L, R_SHIFT_W))
        O["shift_sample"] = outp("shift_sample", (L, NS, R_SHIFT_W))
        for nm in ("s5_re", "s5_im"):
            O[nm + "_prompt"] = outp(nm + "_prompt", (L, 4096))
            O[nm + "_sample"] = outp(nm + "_sample", (L, NS, 64, 64))
        self.O = O
        S = {}
        S["xs"] = scr("xs", (NT, D))
        S["z"] = scr("z", (NT, N_IN))
        S["zT"] = scr("zT", (N_IN, NT))
        for b in "mrs":
            S["act_" + b] = scr("act_" + b, (D, NT), BF16)
        S["vs"] = scr("vs", (NT, D))
        S["ys"] = scr("ys", (NT, D))
        self.S = S

        with ExitStack() as es:
            self.es = es
            kb = KB(nc, es)
            self.kb = kb
            self.alloc()
            self.phase0()
            for l in range(L):
                self.phase1(l)
                self.easy_states(l)
                if self.have_branch("m"):
                    self.mlstm_phase(l)
                if self.have_branch("r"):
                    self.rwkv_phase(l)
                if self.have_branch("s"):
                    self.s5_phase(l)
                self.phase3(l)
            kb.finish()
            kb.emit()
        return nc

    def sb(self, name, shape, dtype=F32):
        return self.es.enter_context(self.nc.sbuf_tensor("sb_" + name, list(shape), dtype))

    def sbp(self, name, shape, dtype=F32):
        self.uid = getattr(self, "uid", 0) + 1
        return self.pes.enter_context(self.nc.sbuf_tensor("sp%d_%s" % (self.uid, name), list(shape), dtype))

    def begin_phase(self):
        self.pes = ExitStack()

    def end_phase(self):
        self.kb.barrier()
        self.pes.close()

    def ps(self, name, shape, dtype=F32):
        return self.es.enter_context(self.nc.psum_tensor("ps_" + name, list(shape), dtype))

    def alloc(self):
        NT = self.NT
        self.ident = self.sb("ident", (128, 128))
        self.xT = self.sb("xT", (128, KC, NT), BF16)
        self.pp = [self.ps("pp%d" % i, (128, 512)) for i in range(4)]
        self.ptr = [self.ps("ptr%d" % i, (128, 512)) for i in range(2)]
        self.pex = [self.ps("pex%d" % i, (128, 512)) for i in range(2)]
        self.cnt = {}
        kb = self.kb
        kb.dma("sp", self.ident[:], self.I["ident"], writes=["ident"])

    def rr(self, key, n):
        v = self.cnt.get(key, 0)
        self.cnt[key] = v + 1
        return v % n

    def ln_alloc(self):
        self.gbc = self.sbp("gbc", (128, D))
        self.bbc = self.sbp("bbc", (128, D))
        self.xt = [self.sbp("xt%d" % i, (128, D)) for i in range(2)]
        self.xc = [self.sbp("xc%d" % i, (128, D)) for i in range(2)]
        self.st = [self.sbp("st%d" % i, (128, 8)) for i in range(2)]

    def ln_tile(self, src, srckey, row0, nr, ti, xs_out, final_out=None):
        kb = self.kb
        i = self.rr("ln", 2)
        xc, st = self.xc[i], self.st[i]
        kxc, kst = "xc%d" % i, "st%d" % i
        kb.op("dve", lambda e: e.tensor_reduce(out=st[:nr, 0:1], in_=src[:nr, :], axis=AX.X, op=ALU.add),
              reads=[srckey], writes=[kst])
        kb.op("dve", lambda e: e.tensor_scalar(out=st[:nr, 1:2], in0=st[:nr, 0:1], scalar1=-1.0 / D, scalar2=None,
                                               op0=ALU.mult), reads=[kst], writes=[kst])
        kb.op("dve", lambda e: e.tensor_scalar(out=xc[:nr, :], in0=src[:nr, :], scalar1=st[:nr, 1:2], scalar2=None,
                                               op0=ALU.add), reads=[srckey, kst], writes=[kxc])
        j = self.rr("tmpsq", 2)
        sq = self.xt[j]
        ksq = "xt%d" % j
        kb.op("act", lambda e: e.activation(out=sq[:nr, :], in_=xc[:nr, :], func=AF.Square, accum_out=st[:nr, 2:3]),
              reads=[kxc], writes=[ksq, kst])
        kb.op("act", lambda e: e.activation(out=st[:nr, 3:4], in_=st[:nr, 2:3], func=AF.Sqrt, scale=1.0 / D,
                                            bias=self.epsc[:nr, 0:1]), reads=[kst, "epsc"], writes=[kst])
        kb.op("dve", lambda e: e.reciprocal(out=st[:nr, 4:5], in_=st[:nr, 3:4]), reads=[kst], writes=[kst])
        kb.op("dve", lambda e: e.scalar_tensor_tensor(out=xc[:nr, :], in0=xc[:nr, :], scalar=st[:nr, 4:5],
                                                      in1=self.gbc[:nr, :], op0=ALU.mult, op1=ALU.mult),
              reads=[kxc, kst, "gbc"], writes=[kxc])
        kb.op("dve", lambda e: e.tensor_tensor(out=xc[:nr, :], in0=xc[:nr, :], in1=self.bbc[:nr, :], op=ALU.add),
              reads=[kxc, "bbc"], writes=[kxc])
        kb.dma("pool", xs_out, xc[:nr, :], reads=[kxc], writes=[("xs", ti)])
        if final_out is not None:
            kb.dma("pool", final_out, xc[:nr, :], reads=[kxc], writes=[], is_output=True)
        for half in range(2):
            p = self.rr("ptr", 2)
            pt, kpt = self.ptr[p], "ptr%d" % p
            for k4 in range(4):
                kc = half * 4 + k4
                kb.op("pe", lambda e, kc=kc, k4=k4, pt=pt: e.transpose(out=pt[:, k4 * 128:k4 * 128 + nr],
                                                                      in_=xc[:nr, kc * 128:(kc + 1) * 128],
                                                                      identity=self.ident[:nr, :nr]),
                      reads=[kxc, "ident"], writes=[kpt])
            dst = self.xT[:, half * 4:half * 4 + 4, row0:row0 + nr]
            srcp = pt[:, :].rearrange("p (k t) -> p k t", k=4)[:, :, :nr]
            eng = "act" if half == 0 else "dve"
            if eng == "act":
                kb.op("act", lambda e, dst=dst, srcp=srcp: e.activation(out=dst, in_=srcp, func=AF.Copy),
                      reads=[kpt], writes=[("xT", ti)])
            else:
                kb.op("dve", lambda e, dst=dst, srcp=srcp: e.tensor_copy(out=dst, in_=srcp),
                      reads=[kpt], writes=[("xT", ti)])

    def load_ln_params(self, g_ap, b_ap):
        kb = self.kb
        kb.dma("sp", self.gbc[:], g_ap.partition_broadcast(128), writes=["gbc"])
        kb.dma("sp", self.bbc[:], b_ap.partition_broadcast(128), writes=["bbc"])

    def phase0(self):
        kb = self.kb
        self.epsc = self.sb("epsc", (128, 1))
        kb.op("dve", lambda e: e.memset(self.epsc[:], LN_EPS), writes=["epsc"])
        self.onesf = self.sb("onesf", (128, 64))
        kb.op("dve", lambda e: e.memset(self.onesf[:], 1.0), writes=["onesf"])
        self.begin_phase()
        self.ln_alloc()
        self.load_ln_params(self.I["ln_in_g"], self.I["ln_in_b"])
        for ti, (r0, nr) in enumerate(self.tiles):
            j = self.rr("xt", 2)
            xt, kxt = self.xt[j], "xt%d" % j
            src = self.I["x_prompt"][r0:r0 + nr, :] if r0 < self.T else self.I["x_sample"][:, :]
            kb.dma("sp", xt[:nr, :], src, writes=[kxt])
            self.ln_tile(xt, kxt, r0, nr, ti, self.S["xs"][r0:r0 + nr, :])
        self.end_phase()

    def load_w(self, dram_cols, width):
        kb = self.kb
        i = self.rr("w", 2)
        ws, wb = self.wst[i], self.wbf[i]
        kws, kwb = "wst%d" % i, "wbf%d" % i
        kb.dma("sp", ws[:, :, :width], dram_cols.rearrange("(k p) c -> p k c", p=128), writes=[kws])
        if self.rr("wcast", 2) == 0:
            kb.op("dve", lambda e: e.tensor_copy(out=wb[:, :, :width], in_=ws[:, :, :width]), reads=[kws], writes=[kwb])
        else:
            kb.op("act", lambda e: e.activation(out=wb[:, :, :width], in_=ws[:, :, :width], func=AF.Copy), reads=[kws], writes=[kwb])
        return wb, kwb

    def evac(self, dst, src, reads, writes):
        kb = self.kb
        if self.rr("evac", 2) == 0:
            kb.op("act", lambda e: e.activation(out=dst, in_=src, func=AF.Copy), reads=reads, writes=writes)
        else:
            kb.op("dve", lambda e: e.tensor_copy(out=dst, in_=src), reads=reads, writes=writes)

    def phase1(self, l):
        kb = self.kb
        self.begin_phase()
        self.wst = [self.sbp("wst%d" % i, (128, KC, 512)) for i in range(2)]
        self.wbf = [self.sbp("wbf%d" % i, (128, KC, 512), BF16) for i in range(2)]
        self.ev = [self.sbp("ev%d" % i, (128, 512)) for i in range(4)]
        W = self.I["w_in"][l]
        tm_segs = [(O_OG, O_ZM), (O_RC, O_U)]
        fm_segs = [(O_XM, O_IG), (O_IG, O_FG), (O_FG, O_OG), (O_ZM, O_RC), (O_U, O_ZS), (O_ZS, O_GM), (O_GM, N_IN)]
        for (c0, c1) in tm_segs:
            c = c0
            while c < c1:
                w = min(512, c1 - c)
                wb, kwb = self.load_w(W[:, c:c + w], w)
                for ti, (r0, nr) in enumerate(self.tiles):
                    p = self.rr("pp", 4)
                    pp, kpp = self.pp[p], "pp%d" % p
                    for kc in range(KC):
                        kb.op("pe", lambda e, kc=kc, pp=pp, r0=r0, nr=nr, wb=wb, w=w: e.matmul(
                            pp[:nr, :w], lhsT=self.xT[:, kc, r0:r0 + nr], rhs=wb[:, kc, :w],
                            start=(kc == 0), stop=(kc == KC - 1)),
                            reads=[("xT", ti), kwb], writes=[kpp])
                    v = self.rr("ev", 4)
                    ev, kev = self.ev[v], "ev%d" % v
                    self.evac(ev[:nr, :w], pp[:nr, :w], [kpp], [kev])
                    kb.dma("pool", self.S["z"][r0:r0 + nr, c:c + w], ev[:nr, :w], reads=[kev],
                           writes=[("z", ti)])
                c += w
        for (c0, c1) in fm_segs:
            c = c0
            while c < c1:
                w = min(512, c1 - c)
                wb, kwb = self.load_w(W[:, c:c + w], w)
                for s0 in range(0, w, 128):
                    m = min(128, w - s0)
                    for bi, (t0, n) in enumerate(self.blocks):
                        p = self.rr("pp", 4)
                        pp, kpp = self.pp[p], "pp%d" % p
                        tis = list(range(t0 // 128, (t0 + n + 127) // 128))
                        for kc in range(KC):
                            kb.op("pe", lambda e, kc=kc, pp=pp, t0=t0, n=n, wb=wb, s0=s0, m=m: e.matmul(
                                pp[:m, :n], lhsT=wb[:, kc, s0:s0 + m], rhs=self.xT[:, kc, t0:t0 + n],
                                start=(kc == 0), stop=(kc == KC - 1)),
                                reads=[("xT", t) for t in tis] + [kwb], writes=[kpp])
                        v = self.rr("ev", 4)
                        ev, kev = self.ev[v], "ev%d" % v
                        self.evac(ev[:m, :n], pp[:m, :n], [kpp], [kev])
                        kb.dma("pool", self.S["zT"][c + s0:c + s0 + m, t0:t0 + n], ev[:m, :n], reads=[kev],
                               writes=[("zT", c + s0, bi)])
                c += w
        self.end_phase()

    def easy_states(self, l):
        kb = self.kb
        I, S, O = self.I, self.S, self.O
        T, NS = self.T, self.NS
        nb = len(self.blocks)
        zt_all = [("zT", r, b) for r in range(0, 1024, 128) for b in range(nb)]
        z_all = [("z", ti) for ti in range(len(self.tiles))]
        kb.dma("pool", O["conv_prompt"][l].rearrange("j c -> c j"), S["zT"][0:D, T - 3:T], reads=zt_all, writes=[],
               is_output=True, allow_slow_non_contiguous=True)
        kb.dma("pool", O["shift_prompt"][l:l + 1, :], S["z"][T - 1:T, O_RC:O_RC + R_SHIFT_W], reads=z_all, writes=[], is_output=True)
        if NS:
            kb.dma("pool", O["conv_sample"][l][:, 0:2, :], I["state_mlstm_conv"][l][:, 1:3, :], writes=[], is_output=True)
            for hh in range(4):
                kb.dma("pool", O["conv_sample"][l][:, 2, hh * 256:(hh + 1) * 256].rearrange("b c -> c b"),
                       S["zT"][hh * 256:(hh + 1) * 256, T:T + NS], reads=zt_all, writes=[],
                       is_output=True, allow_slow_non_contiguous=True)
            kb.dma("pool", O["shift_sample"][l], S["z"][T:T + NS, O_RC:O_RC + R_SHIFT_W], reads=z_all, writes=[], is_output=True)

    def mlstm_phase(self, l):
        kb = self.kb
        I, S, O = self.I, self.S, self.O
        T, NS, NT = self.T, self.NS, self.NT
        self.begin_phase()
        sbp = self.sbp
        Wst = sbp("Wst", (128, 4, 2, 256))
        Wq = sbp("Wq", (128, 4, 2, 256), BF16); Wk = sbp("Wk", (128, 4, 2, 256), BF16); Wv = sbp("Wv", (128, 4, 2, 256), BF16)
        cw = sbp("cw", (128, 4, 8)); cb = sbp("cb", (128, 8)); mg = sbp("mg", (128, 8)); msk = sbp("msk", (128, 8))
        gb = sbp("gb", (4, 2))
        igA = sbp("igA", (4, NT)); lfA = sbp("lfA", (4, NT))
        selh = sbp("selh", (4, 4, 128)); causneg = sbp("causneg", (128, 128)); ones4 = sbp("ones4", (4, 128))
        Cst = [[sbp("C%d_%d" % (s_, h), (128, 2, 257)) for h in range(4)] for s_ in range(3)]

        def mk_set(tag, Lm):
            W = {"tag": tag}
            W["mprev"] = sbp(tag + "mprev", (4, 2))
            W["xext"] = [sbp(tag + "xext%d" % i, (128, 8, Lm + 3)) for i in range(2)]
            W["ctmp"] = sbp(tag + "cvtmp", (128, 8, Lm)); W["cacc"] = sbp(tag + "cvacc", (128, 8, Lm))
            W["xcT"] = sbp(tag + "xcT", (128, 8, Lm)); W["xcb"] = sbp(tag + "xcb", (128, 8, Lm), BF16); W["xmb"] = sbp(tag + "xmb", (128, 8, Lm), BF16)
            W["qTb"] = sbp(tag + "qTb", (128, 4, 2, Lm), BF16); W["kTb"] = sbp(tag + "kTb", (128, 4, 2, Lm), BF16)
            W["Cb"] = [sbp(tag + "Cb%d" % h, (128, 2, 257), BF16) for h in range(4)]
            W["kw"] = sbp(tag + "kw", (128, 256), BF16); W["vaug"] = sbp(tag + "vaug", (128, 257), BF16)
            W["G"] = sbp(tag + "G", (4, 8, Lm)); W["gsm"] = sbp(tag + "gsm", (4, 8)); W["dg"] = sbp(tag + "dg", (4, 4))
            W["gc"] = sbp(tag + "gc", (128, 16)); W["dbc"] = sbp(tag + "dbc", (128, 4))
            W["DT"] = sbp(tag + "DT", (128, Lm)); W["Stl"] = sbp(tag + "Stl", (128, Lm), BF16); W["mmb"] = sbp(tag + "mmb", (128, 4, Lm))
            W["Asb"] = sbp(tag + "Asb", (128, 257)); W["nd"] = sbp(tag + "nd", (128, 257)); W["dsm"] = sbp(tag + "dsm", (128, 4))
            W["sog"] = [sbp(tag + "sog%d" % i, (128, D)) for i in range(2 if Lm > 1 else 1)]
            W["hm"] = sbp(tag + "hm", (128, D)); W["hst"] = sbp(tag + "hst", (128, 16))
            W["hT"] = sbp(tag + "hT", (128, 8, Lm)); W["zmt"] = [sbp(tag + "zmt%d" % i, (128, 8, Lm)) for i in range(2)]
            W["aob"] = [sbp(tag + "aob%d" % i, (128, 8, Lm), BF16) for i in range(2)]
            return W
        WP = mk_set("P", 128)
        WS_ = mk_set("S", 1) if NS else None
        kb.local = frozenset(["mprev", "mnew", "xext0", "xext1", "cvacc", "cvtmp", "xcT", "xcb", "xmb", "G0", "G1", "G3", "G4", "G5", "G6", "gsm", "gc", "dg",
                              "dbc", "mmb", "DT", "Stl", "kw", "vaug", "Asb", "nd", "dsm", "sog0", "sog1", "hst", "zmt0", "zmt1", "aob0", "aob1",
                              "hm", "hT", "qTb", "kTb", "Cb"])
        cvst = sbp("cvst", (48, D)); convT = sbp("convT", (128, 8, 48))

        for (nm, dst, sc_) in (("m_wq", Wq, 1.0 / 16.0), ("m_wk", Wk, 1.0), ("m_wv", Wv, 1.0)):
            kb.dma("sp", Wst[:], I[nm][l].rearrange("h (c p) e -> p h c e", p=128), writes=["Wst"])
            kb.op("act", lambda e, dst=dst, sc_=sc_: e.activation(out=dst[:], in_=Wst[:], func=AF.Copy, scale=sc_),
                  reads=["Wst"], writes=[nm])
        sl = dict(allow_slow_non_contiguous=True)
        kb.dma("sp", cw[:], I["m_conv_w"][l].rearrange("j (k p) -> p j k", p=128), writes=["cw"], **sl)
        kb.dma("sp", cb[:], I["m_conv_b"][l].rearrange("(k p) -> p k", p=128), writes=["cb"], **sl)
        kb.dma("sp", mg[:], I["m_norm_g"][l].rearrange("(k p) -> p k", p=128), writes=["mg"], **sl)
        kb.dma("sp", msk[:], I["m_skip"][l].rearrange("(k p) -> p k", p=128), writes=["msk"], **sl)
        kb.dma("sp", gb[:, 0:1], I["m_ig_b"][l].rearrange("(h o) -> h o", o=1), writes=["gb"], **sl)
        kb.dma("sp", gb[:, 1:2], I["m_fg_b"][l].rearrange("(h o) -> h o", o=1), writes=["gb"], **sl)
        kb.dma("sp", selh[:], I["selh"], writes=["selh"])
        kb.dma("sp", causneg[:], I["causneg"], writes=["causneg"])
        kb.dma("sp", ones4[:], I["ones4"], writes=["ones4"])
        nb = len(self.blocks)
        kb.dma("sp", igA[:], S["zT"][O_IG:O_IG + 4, :], reads=[("zT", O_IG, b) for b in range(nb)], writes=["igA"])
        kb.dma("sp", lfA[:], S["zT"][O_FG:O_FG + 4, :], reads=[("zT", O_FG, b) for b in range(nb)], writes=["lfA"])
        kb.op("dve", lambda e: e.tensor_scalar(out=igA[:], in0=igA[:], scalar1=gb[:, 0:1], scalar2=None, op0=ALU.add),
              reads=["igA", "gb"], writes=["igA"])
        kb.op("dve", lambda e: e.tensor_scalar(out=lfA[:], in0=lfA[:], scalar1=gb[:, 1:2], scalar2=-1.0, op0=ALU.add, op1=ALU.mult),
              reads=["lfA", "gb"], writes=["lfA"])
        kb.op("act", lambda e: e.activation(out=lfA[:], in_=lfA[:], func=AF.Exp), reads=["lfA"], writes=["lfA"])
        kb.op("dve", lambda e: e.tensor_scalar(out=lfA[:], in0=lfA[:], scalar1=1.0, scalar2=None, op0=ALU.add), reads=["lfA"], writes=["lfA"])
        kb.op("act", lambda e: e.activation(out=lfA[:], in_=lfA[:], func=AF.Ln), reads=["lfA"], writes=["lfA"])
        kb.op("dve", lambda e: e.tensor_scalar(out=lfA[:], in0=lfA[:], scalar1=-1.0, scalar2=None, op0=ALU.mult), reads=["lfA"], writes=["lfA"])
        if NS:
            kb.dma("sp", cvst[:3 * NS, :], I["state_mlstm_conv"][l].rearrange("b j c -> (b j) c"), writes=["cvst"])
            for half in range(2):
                p = self.rr("ptr", 2)
                pt, kpt = self.ptr[p], "ptr%d" % p
                for k4 in range(4):
                    kc = half * 4 + k4
                    kb.op("pe", lambda e, pt=pt, k4=k4, kc=kc: e.transpose(out=pt[:, k4 * 48:k4 * 48 + 3 * NS], in_=cvst[:3 * NS, kc * 128:(kc + 1) * 128],
                                                                       identity=self.ident[:3 * NS, :3 * NS]),
                          reads=["cvst", "ident"], writes=[kpt])
                kb.op("dve", lambda e, pt=pt, half=half: e.tensor_copy(out=convT[:, half * 4:half * 4 + 4, :3 * NS],
                                                                     in_=pt[:, 0:192].rearrange("p (k c) -> p k c", k=4)[:, :, :3 * NS]),
                      reads=[kpt], writes=["convT"])

        zt_x = lambda bi: [("zT", r, bi) for r in range(0, 1024, 128)]
        zt_z = lambda bi: [("zT", O_ZM + r, bi) for r in range(0, 1024, 128)]
        blk_of = lambda t: next(i for i, (b0, bn) in enumerate(self.blocks) if b0 <= t < b0 + bn)

        chunks = [(c * 128, 128, None) for c in range(T // 128)] + [(T + b, 1, b) for b in range(NS)]
        def chunk_gen(ci, t0, L, sb_, W):
            mprev = W["mprev"]; xext = W["xext"]; ctmp = W["ctmp"]; cacc = W["cacc"]; xcT = W["xcT"]; xcb = W["xcb"]; xmb = W["xmb"]
            qTb = W["qTb"]; kTb = W["kTb"]; kw = W["kw"]; vaug = W["vaug"]; G = W["G"]; gsm = W["gsm"]; dg = W["dg"]; gc = W["gc"]; dbc = W["dbc"]
            DT = W["DT"]; Stl = W["Stl"]; mmb = W["mmb"]; Asb = W["Asb"]; nd = W["nd"]; dsm = W["dsm"]; sog = W["sog"]; hm = W["hm"]; hst = W["hst"]
            hT = W["hT"]; zmt = W["zmt"]; aob = W["aob"]; Cb = W["Cb"]
            bi = blk_of(t0)
            ti = t0 // 128
            cs = (1 + self.rr("Cset", 2)) if sb_ is not None else 0
            C = Cst[cs]
            kC = [("C", cs, h) for h in range(4)]
            if ci == 0:
                for h in range(4):
                    kb.op("pool", lambda e, h=h: e.memset(C[h][:], 0.0), writes=[kC[h]])
                kb.op("dve", lambda e: e.memset(mprev[:, 0:1], NEG), writes=["mprev"])
            if sb_ is not None:
                for h in range(4):
                    kb.dma("sp", C[h][:, :, 0:256], I["state_mlstm_c"][l, sb_, h].rearrange("(c p) v -> p c v", p=128), writes=[kC[h]])
                    kb.dma("sp", C[h][:, :, 256:257], I["state_mlstm_n"][l, sb_, h].rearrange("(c p o) -> p c o", p=128, o=1), writes=[kC[h]], **sl)
                kb.dma("sp", mprev[:, 0:1], I["state_mlstm_m"][l, sb_].rearrange("(h o) -> h o", o=1), writes=["mprev"], **sl)
            for h in range(4):
                kb.op("act", lambda e, h=h: e.activation(out=Cb[h][:], in_=C[h][:], func=AF.Copy), reads=[kC[h]], writes=[("Cb", h)])
            xj = self.rr("xext" + W["tag"], 2)
            xe, kxe = xext[xj], "xext%d" % xj
            if sb_ is None:
                if t0 == 0:
                    kb.op("pool", lambda e, xe=xe: e.memset(xe[:, :, 0:3], 0.0), writes=[kxe])
                    kb.dma("sp", xe[:, :, 3:3 + L], S["zT"][0:D, t0:t0 + L].rearrange("(k p) t -> p k t", p=128), reads=zt_x(bi), writes=[kxe])
                else:
                    rd = zt_x(bi) + (zt_x(blk_of(t0 - 3)) if blk_of(t0 - 3) != bi else [])
                    kb.dma("sp", xe[:, :, 0:3 + L], S["zT"][0:D, t0 - 3:t0 + L].rearrange("(k p) t -> p k t", p=128), reads=rd, writes=[kxe])
            else:
                kb.op("pool", lambda e, xe=xe, sb_=sb_: e.tensor_copy(out=xe[:, :, 0:3], in_=convT[:, :, 3 * sb_:3 * sb_ + 3]), reads=["convT"], writes=[kxe])
                kb.dma("sp", xe[:, :, 3:4], S["zT"][0:D, t0:t0 + 1].rearrange("(k p) t -> p k t", p=128), reads=zt_x(bi), writes=[kxe], **sl)
            V = lambda x: x[:, :, :L]
            for j in range(4):
                dst = cacc if j == 0 else ctmp
                kd = "cvacc" if j == 0 else "cvtmp"
                kb.op("dve", lambda e, j=j, dst=dst, xe=xe: e.tensor_tensor(out=V(dst), in0=xe[:, :, j:j + L],
                                                                         in1=cw[:, j, :, None].to_broadcast([128, 8, L]), op=ALU.mult),
                      reads=[kxe, "cw"], writes=[kd])
                if j > 0:
                    kb.op("dve", lambda e: e.tensor_tensor(out=V(cacc), in0=V(cacc), in1=V(ctmp), op=ALU.add), reads=["cvacc", "cvtmp"], writes=["cvacc"])
            kb.op("dve", lambda e: e.tensor_tensor(out=V(cacc), in0=V(cacc), in1=cb[:, :, None].to_broadcast([128, 8, L]), op=ALU.add),
                  reads=["cvacc", "cb"], writes=["cvacc"])
            kb.op("act", lambda e: e.activation(out=V(xcT), in_=V(cacc), func=AF.Silu), reads=["cvacc"], writes=["xcT"])
            kb.op("dve", lambda e: e.tensor_copy(out=V(xcb), in_=V(xcT)), reads=["xcT"], writes=["xcb"])
            kb.op("dve", lambda e, xe=xe: e.tensor_copy(out=V(xmb), in_=xe[:, :, 3:3 + L]), reads=[kxe], writes=["xmb"])
            yield
            Gr = lambda i: G[:, i, :L]
            kb.op("dve", lambda e: e.tensor_tensor_scan(out=Gr(0), data0=ones4[:, :L], data1=lfA[:, t0:t0 + L], initial=0.0, op0=ALU.mult, op1=ALU.add),
                  reads=["ones4", "lfA"], writes=["G0"])
            kb.op("dve", lambda e: e.tensor_tensor(out=Gr(3), in0=igA[:, t0:t0 + L], in1=Gr(0), op=ALU.subtract), reads=["igA", "G0"], writes=["G3"])
            kb.op("dve", lambda e: e.tensor_tensor_scan(out=Gr(1), data0=Gr(3), data1=Gr(3), initial=-3.0e38, op0=ALU.max, op1=ALU.max),
                  reads=["G3"], writes=["G1"])
            kb.op("dve", lambda e: e.tensor_scalar(out=Gr(1), in0=Gr(1), scalar1=mprev[:, 0:1], scalar2=None, op0=ALU.max), reads=["G1", "mprev"], writes=["G1"])
            kb.op("dve", lambda e: e.tensor_scalar(out=gsm[:, 0:1], in0=G[:, 1, L - 1:L], scalar1=-1.0, scalar2=None, op0=ALU.mult), reads=["G1"], writes=["gsm"])
            kb.op("act", lambda e: e.activation(out=Gr(4), in_=Gr(1), func=AF.Exp, scale=-1.0, bias=mprev[:, 0:1]), reads=["G1", "mprev"], writes=["G4"])
            kb.op("dve", lambda e: e.tensor_tensor(out=Gr(5), in0=Gr(0), in1=Gr(1), op=ALU.add), reads=["G0", "G1"], writes=["G5"])
            kb.op("act", lambda e: e.activation(out=Gr(5), in_=Gr(5), func=AF.Exp, scale=-1.0), reads=["G5"], writes=["G5"])
            kb.op("act", lambda e: e.activation(out=Gr(6), in_=Gr(3), func=AF.Exp, bias=gsm[:, 0:1]), reads=["G3", "gsm"], writes=["G6"])
            kb.op("dve", lambda e: e.tensor_tensor(out=mprev[:, 1:2], in0=G[:, 0, L - 1:L], in1=G[:, 1, L - 1:L], op=ALU.add), reads=["G0", "G1"], writes=["mnew"])
            p = self.rr("ptr", 2)
            pt, kpt = self.ptr[p], "ptr%d" % p
            for ki, gi in enumerate((4, 5, 3, 6)):
                kb.op("pe", lambda e, ki=ki, gi=gi, pt=pt: e.transpose(out=pt[:L, ki * 4:ki * 4 + 4], in_=G[:, gi, :L], identity=self.ident[:4, :4]),
                      reads=["G%d" % gi, "ident"], writes=[kpt])
            kb.op("dve", lambda e, pt=pt: e.tensor_copy(out=gc[:L, :], in_=pt[:L, 0:16]), reads=[kpt], writes=["gc"])
            kb.op("dve", lambda e: e.tensor_scalar(out=dg[:, :], in0=self.ident[:4, :4], scalar1=G[:, 4, L - 1:L], scalar2=None, op0=ALU.mult),
                  reads=["G4", "ident"], writes=["dg"])
            p2 = self.rr("ptr", 2)
            pt2, kpt2 = self.ptr[p2], "ptr%d" % p2
            kb.op("pe", lambda e, pt2=pt2: e.matmul(pt2[:, 0:4], lhsT=ones4[:, :], rhs=dg[:, :], start=True, stop=True), reads=["ones4", "dg"], writes=[kpt2])
            kb.op("dve", lambda e, pt2=pt2: e.tensor_copy(out=dbc[:, :], in_=pt2[:, 0:4]), reads=[kpt2], writes=["dbc"])
            pnn = self.rr("pp", 4)
            pn, kpn = self.pp[pnn], "pp%d" % pnn
            for h in range(4):
                kb.op("pe", lambda e, h=h, pn=pn: e.matmul(pn[:L, h * 128:h * 128 + L], lhsT=selh[:, h, :L], rhs=G[:, 1, :L], start=True, stop=True),
                      reads=["selh", "G1"], writes=[kpn])
            kb.op("act", lambda e, pn=pn: e.activation(out=mmb[:L, :, :L], in_=pn[:L, :].rearrange("p (h t) -> p h t", h=4)[:, :, :L], func=AF.Copy),
                  reads=[kpn], writes=["mmb"])
            sj = self.rr("sog" + W["tag"], len(sog))
            so, kso = sog[sj], "sog%d" % sj
            kb.dma("sp", so[:L, :], S["z"][t0:t0 + L, O_OG:O_OG + D], reads=[("z", ti)], writes=[kso])
            kb.op("act", lambda e, so=so: e.activation(out=so[:L, :], in_=so[:L, :], func=AF.Sigmoid), reads=[kso], writes=[kso])
            yield
            for h in range(4):
                for (Wt, wn, dstT, kd) in ((Wq, "m_wq", qTb, "qTb"), (Wk, "m_wk", kTb, "kTb")):
                    pq = self.rr("pp", 4)
                    ppq, kpq = self.pp[pq], "pp%d" % pq
                    for ec in range(2):
                        for dc in range(2):
                            kb.op("pe", lambda e, h=h, ec=ec, dc=dc, Wt=Wt, ppq=ppq: e.matmul(
                                ppq[:, ec * 128:ec * 128 + L], lhsT=Wt[:, h, dc, ec * 128:(ec + 1) * 128], rhs=xcb[:, 2 * h + dc, :L],
                                start=(dc == 0), stop=(dc == 1)), reads=[wn, "xcb"], writes=[kpq])
                    self.evac(dstT[:, h, :, :L], ppq[:, 0:256].rearrange("p (c t) -> p c t", c=2)[:, :, :L], [kpq], [(kd, h)])
            for h in range(4):
                yield
                pk = self.rr("pp", 4)
                ppk, kpk = self.pp[pk], "pp%d" % pk
                for dc in range(2):
                    kb.op("pe", lambda e, h=h, dc=dc, ppk=ppk: e.matmul(ppk[:L, 0:256], lhsT=xcb[:, 2 * h + dc, :L], rhs=Wk[:, h, dc, :],
                                                                      start=(dc == 0), stop=(dc == 1)), reads=["m_wk", "xcb"], writes=[kpk])
                kb.op("act", lambda e, h=h, ppk=ppk: e.activation(out=kw[:L, :], in_=ppk[:L, 0:256], func=AF.Copy, scale=gc[:L, 12 + h:13 + h]),
                      reads=[kpk, "gc"], writes=["kw"])
                pv = self.rr("pp", 4)
                ppv, kpv = self.pp[pv], "pp%d" % pv
                for dc in range(2):
                    kb.op("pe", lambda e, h=h, dc=dc, ppv=ppv: e.matmul(ppv[:L, 0:256], lhsT=xmb[:, 2 * h + dc, :L], rhs=Wv[:, h, dc, :],
                                                                      start=(dc == 0), stop=(dc == 1)), reads=["m_wv", "xmb"], writes=[kpv])
                kb.op("dve", lambda e, ppv=ppv: e.tensor_copy(out=vaug[:L, 0:256], in_=ppv[:L, 0:256]), reads=[kpv], writes=["vaug"])
                kb.op("dve", lambda e: e.memset(vaug[:L, 256:257], 1.0), writes=["vaug"])
                kb.op("dve", lambda e, h=h: e.tensor_tensor(out=DT[:L, :L], in0=mmb[:L, h, :L], in1=causneg[:L, :L], op=ALU.add),
                      reads=["mmb", "causneg"], writes=["DT"])
                kb.op("act", lambda e, h=h: e.activation(out=DT[:L, :L], in_=DT[:L, :L], func=AF.Exp, scale=-1.0, bias=gc[:L, 8 + h:9 + h]),
                      reads=["DT", "gc"], writes=["DT"])
                ps_ = self.rr("pp", 4)
                pps, kps = self.pp[ps_], "pp%d" % ps_
                for ec in range(2):
                    kb.op("pe", lambda e, h=h, ec=ec, pps=pps: e.matmul(pps[:L, :L], lhsT=kTb[:, h, ec, :L], rhs=qTb[:, h, ec, :L],
                                                                      start=(ec == 0), stop=(ec == 1)), reads=[("kTb", h), ("qTb", h)], writes=[kps])
                kb.op("dve", lambda e, pps=pps: e.tensor_tensor(out=Stl[:L, :L], in0=pps[:L, :L], in1=DT[:L, :L], op=ALU.mult), reads=[kps, "DT"], writes=["Stl"])
                pa = self.rr("pp", 4)
                ppa, kpa = self.pp[pa], "pp%d" % pa
                kb.op("pe", lambda e, ppa=ppa: e.matmul(ppa[:L, 0:257], lhsT=Stl[:L, :L], rhs=vaug[:L, :], start=True, stop=True), reads=["Stl", "vaug"], writes=[kpa])
                pb = self.rr("ptr", 2)
                ppb, kpb = self.ptr[pb], "ptr%d" % pb
                for ec in range(2):
                    kb.op("pe", lambda e, h=h, ec=ec, ppb=ppb, C=C: e.matmul(ppb[:L, 0:257], lhsT=qTb[:, h, ec, :L], rhs=Cb[h][:, ec, :],
                                                                      start=(ec == 0), stop=(ec == 1)), reads=[("qTb", h), ("Cb", h)], writes=[kpb])
                kb.op("act", lambda e, ppa=ppa: e.activation(out=Asb[:L, :], in_=ppa[:L, 0:257], func=AF.Copy), reads=[kpa], writes=["Asb"])
                kb.op("dve", lambda e, h=h, ppb=ppb: e.scalar_tensor_tensor(out=nd[:L, :], in0=ppb[:L, 0:257], scalar=gc[:L, h:h + 1], in1=Asb[:L, :],
                                                                        op0=ALU.mult, op1=ALU.add), reads=[kpb, "gc", "Asb"], writes=["nd"])
                kb.op("act", lambda e: e.activation(out=dsm[:L, 2:3], in_=nd[:L, 256:257], func=AF.Abs), reads=["nd"], writes=["dsm"])
                kb.op("dve", lambda e, h=h: e.tensor_scalar(out=dsm[:L, 0:1], in0=dsm[:L, 2:3], scalar1=gc[:L, 4 + h:5 + h], scalar2=None,
                                                            op0=ALU.max), reads=["dsm", "gc"], writes=["dsm"])
                kb.op("dve", lambda e: e.reciprocal(out=dsm[:L, 1:2], in_=dsm[:L, 0:1]), reads=["dsm"], writes=["dsm"])
                kb.op("dve", lambda e, h=h, so=so: e.scalar_tensor_tensor(out=hm[:L, h * 256:(h + 1) * 256], in0=nd[:L, 0:256], scalar=dsm[:L, 1:2],
                                                                      in1=so[:L, h * 256:(h + 1) * 256], op0=ALU.mult, op1=ALU.mult),
                      reads=["nd", "dsm", kso], writes=[("hm", h)])
                for ec in range(2):
                    pu = self.rr("pp", 4)
                    ppu, kpu = self.pp[pu], "pp%d" % pu
                    kb.op("pe", lambda e, ec=ec, ppu=ppu: e.matmul(ppu[:, 0:257], lhsT=kw[:L, ec * 128:(ec + 1) * 128], rhs=vaug[:L, :], start=True, stop=True),
                          reads=["kw", "vaug"], writes=[kpu])
                    kb.op("dve", lambda e, h=h, ec=ec, ppu=ppu, C=C: e.scalar_tensor_tensor(out=C[h][:, ec, :], in0=C[h][:, ec, :], scalar=dbc[:, h:h + 1],
                                                                                      in1=ppu[:, 0:257], op0=ALU.mult, op1=ALU.add),
                          reads=[kC[h], "dbc", kpu], writes=[kC[h]])
            kb.op("dve", lambda e: e.tensor_copy(out=mprev[:, 0:1], in_=mprev[:, 1:2]), reads=["mnew"], writes=["mprev"])
            yield
            hmk = [("hm", h) for h in range(4)]
            hv = hm[:L, :].rearrange("t (h d) -> t h d", h=4)
            kb.op("dve", lambda e: e.tensor_reduce(out=hst[:L, 0:4], in_=hv, axis=AX.X, op=ALU.add), reads=hmk, writes=["hst"])
            kb.op("dve", lambda e: e.tensor_scalar(out=hst[:L, 0:4], in0=hst[:L, 0:4], scalar1=-1.0 / 256.0, scalar2=None, op0=ALU.mult), reads=["hst"], writes=["hst"])
            kb.op("dve", lambda e: e.tensor_tensor(out=hv, in0=hv, in1=hst[:L, 0:4].unsqueeze(2).to_broadcast([L, 4, 256]), op=ALU.add),
                  reads=hmk + ["hst"], writes=hmk)
            kb.op("dve", lambda e, so=so: e.tensor_tensor(out=so[:L, :], in0=hm[:L, :], in1=hm[:L, :], op=ALU.mult), reads=hmk, writes=[kso])
            kb.op("dve", lambda e, so=so: e.tensor_reduce(out=hst[:L, 4:8], in_=so[:L, :].rearrange("t (h d) -> t h d", h=4), axis=AX.X, op=ALU.add),
                  reads=[kso], writes=["hst"])
            kb.op("act", lambda e: e.activation(out=hst[:L, 8:12], in_=hst[:L, 4:8], func=AF.Sqrt, scale=1.0 / 256.0, bias=self.epsc[:L, 0:1]),
                  reads=["hst", "epsc"], writes=["hst"])
            kb.op("dve", lambda e: e.reciprocal(out=hst[:L, 12:16], in_=hst[:L, 8:12]), reads=["hst"], writes=["hst"])
            kb.op("dve", lambda e: e.tensor_tensor(out=hv, in0=hv, in1=hst[:L, 12:16].unsqueeze(2).to_broadcast([L, 4, 256]), op=ALU.mult),
                  reads=hmk + ["hst"], writes=hmk)
            zj = self.rr("zmt" + W["tag"], 2)
            zm_, kzm = zmt[zj], "zmt%d" % zj
            kb.dma("sp", zm_[:, :, :L], S["zT"][O_ZM:O_ZM + D, t0:t0 + L].rearrange("(k p) t -> p k t", p=128), reads=zt_z(bi), writes=[kzm],
                   **(sl if L == 1 else {}))
            kb.op("act", lambda e, zm_=zm_: e.activation(out=zm_[:, :, :L], in_=zm_[:, :, :L], func=AF.Silu), reads=[kzm], writes=[kzm])
            for half in range(2):
                p = self.rr("ptr", 2)
                pt, kpt = self.ptr[p], "ptr%d" % p
                for k4 in range(4):
                    kc = half * 4 + k4
                    kb.op("pe", lambda e, k4=k4, kc=kc, pt=pt: e.transpose(out=pt[:, k4 * 128:k4 * 128 + L], in_=hm[:L, kc * 128:(kc + 1) * 128],
                                                                       identity=self.ident[:L, :L]), reads=hmk + ["ident"], writes=[kpt])
                kb.op("dve", lambda e, half=half, pt=pt: e.tensor_tensor(out=hT[:, half * 4:half * 4 + 4, :L],
                                                                       in0=pt[:, :].rearrange("p (k t) -> p k t", k=4)[:, :, :L],
                                                                       in1=mg[:, half * 4:half * 4 + 4, None].to_broadcast([128, 4, L]), op=ALU.mult),
                      reads=[kpt, "mg"], writes=[("hT", half)])
            kb.op("dve", lambda e: e.tensor_tensor(out=V(ctmp), in0=V(xcT), in1=msk[:, :, None].to_broadcast([128, 8, L]), op=ALU.mult),
                  reads=["xcT", "msk"], writes=["cvtmp"])
            kb.op("dve", lambda e: e.tensor_tensor(out=V(hT), in0=V(hT), in1=V(ctmp), op=ALU.add), reads=["cvtmp", ("hT", 0), ("hT", 1)], writes=[("hT", 0), ("hT", 1)])
            aj = self.rr("aob" + W["tag"], 2)
            kb.op("dve", lambda e, aj=aj, zm_=zm_: e.tensor_tensor(out=aob[aj][:, :, :L], in0=V(hT), in1=zm_[:, :, :L], op=ALU.mult),
                  reads=[("hT", 0), ("hT", 1), kzm], writes=["aob%d" % aj])
            kb.dma("pool", S["act_m"][:, t0:t0 + L].rearrange("(k p) t -> p k t", p=128), aob[aj][:, :, :L], reads=["aob%d" % aj],
                   writes=[("act", "m", bi)], **(sl if L == 1 else {}))
            if sb_ is not None or ci == T // 128 - 1:
                if sb_ is None:
                    oc_, on_, om_ = O["c_prompt"][l], O["n_prompt"][l], O["m_prompt"][l]
                else:
                    oc_, on_, om_ = O["c_sample"][l, sb_], O["n_sample"][l, sb_], O["m_sample"][l, sb_]
                for h in range(4):
                    kb.dma("pool", oc_[h].rearrange("(c p) v -> p c v", p=128), C[h][:, :, 0:256], reads=[kC[h]], writes=[], is_output=True)
                    kb.dma("pool", on_[h].rearrange("(c p o) -> p c o", p=128, o=1), C[h][:, :, 256:257], reads=[kC[h]], writes=[], is_output=True, **sl)
                kb.dma("pool", om_.rearrange("(h o) -> h o", o=1), mprev[:, 0:1], reads=["mprev"], writes=[], is_output=True, **sl)
        pq_ = [(ci, t0, L, sb_) for ci, (t0, L, sb_) in enumerate(chunks) if sb_ is None]
        sq_ = [(ci, t0, L, sb_) for ci, (t0, L, sb_) in enumerate(chunks) if sb_ is not None]
        streams = [[pq_, WP, None], [sq_, WS_, None]]
        while any(st[0] or st[2] is not None for st in streams):
            for st in streams:
                if st[2] is None and st[0]:
                    args = st[0].pop(0)
                    st[2] = chunk_gen(*args, st[1])
                if st[2] is not None:
                    kb.ns = st[1]["tag"]
                    try:
                        next(st[2])
                    except StopIteration:
                        st[2] = None
                    kb.ns = None
        self.end_phase()

    def rwkv_phase(self, l):
        kb = self.kb
        I, S, O = self.I, self.S, self.O
        T, NS, NT = self.T, self.NS, self.NT
        self.begin_phase()
        sbp = self.sbp
        sl = dict(allow_slow_non_contiguous=True)
        NPI = self.NPI
        EW = -math.exp(-0.5)
        P_ = {}
        for nm in ("r_k_k", "r_k_a", "r_ln_g", "r_ln_b"):
            P_[nm] = sbp("bc_" + nm, (128, D))
            kb.dma("sp", P_[nm][:], I[nm][l].partition_broadcast(128), writes=[nm])
        P_["r_r_k"] = sbp("bc_rk", (128, D))
        kb.dma("sp", P_["r_r_k"][:], I["r_r_k"][l].rearrange("h j -> (h j)").partition_broadcast(128), writes=["r_r_k"])
        mu = sbp("bc_mu", (128, R_SHIFT_W))
        kb.dma("sp", mu[:], I["r_mu"][l].partition_broadcast(128), writes=["mu"])
        w2 = sbp("w2", (65, 2, D))
        kb.dma("sp", w2[0:64, 0, :], I["r_w2"][l], writes=["w2"])
        kb.dma("sp", w2[0:64, 1, :], I["r_a2"][l], writes=["w2"])
        kb.dma("sp", w2[64:65, 0, :], I["r_w0"][l:l + 1, :], writes=["w2"])
        kb.dma("sp", w2[64:65, 1, :], I["r_a0"][l:l + 1, :], writes=["w2"])
        mks = sbp("mks", (128, 384))
        kb.dma("sp", mks[:], I["rmasks"], writes=["mks"])
        e12 = sbp("e12", (128, 1))
        kb.op("dve", lambda e: e.memset(e12[:], RWKV_GN_EPS), writes=["e12"])
        xr = sbp("xr", (128, R_SHIFT_W)); rp = sbp("rp", (128, R_SHIFT_W))
        lt = sbp("lt", (65, 2, 128))
        kb.op("pool", lambda e: e.memset(lt[64:65, :, :], 1.0), writes=["lt1"])
        tv = {n: sbp("tv_" + n, (128, D)) for n in ("lw", "a", "an", "b", "k")}
        ssq = sbp("ssq", (128, 64))
        fmall = sbp("fmall", (128, 8, 6, 128))
        fm = {n: fmall[:, :, i_, :] for i_, n in enumerate(("at", "rt", "bh", "kh", "bc", "kc"))}
        fm["cum"] = sbp("fm_cum", (128, 8, 128)); fm["et"] = sbp("fm_et", (128, 8, 128))
        wc = sbp("wc", (128, 8, 16))
        ST = sbp("STt", (128, 8, 64))
        Vp = sbp("Vp", (128, 8, 64)); Ych = sbp("Ych", (128, 8, 64))
        nat = sbp("rnat", (128, 8, 64)); nato = sbp("rnato", (128, 8, 64))
        CDT = BF16 if self.CORE_BF16 else F32
        NG = 2
        G_ = []
        BK = sbp("g_BK", (128, 4, 2, 128), CDT)
        for gi in range(NG):
            d = {}
            d["UBD"] = sbp("g%d_UBD" % gi, (128, 4, 4, 128), CDT)
            d["Btm"] = sbp("g%d_Btm" % gi, (128, 4, 128), CDT); d["Ktm"] = sbp("g%d_Ktm" % gi, (128, 4, 128), CDT)
            d["MNb"] = sbp("g%d_MNb" % gi, (128, 4, 256), CDT); d["MNk"] = sbp("g%d_MNk" % gi, (128, 4, 256), CDT)
            d["X"] = sbp("g%d_X" % gi, (128, 4, 128), CDT); d["Xt"] = sbp("g%d_Xt" % gi, (128, 4, 128), CDT); d["P"] = sbp("g%d_P" % gi, (128, 4, 128), CDT)
            d["RHS"] = sbp("g%d_RHS" % gi, (128, 4, 64), CDT); d["U"] = sbp("g%d_U" % gi, (128, 4, 64), CDT)
            G_.append(d)
        self.bank8 = list(self.pp) + list(self.ptr) + list(self.pex)
        self.bank8k = ["pp%d" % i for i in range(4)] + ["ptr%d" % i for i in range(2)] + ["pex%d" % i for i in range(2)]
        if self.CORE_BF16:
            STb = sbp("STb", (128, 8, 64), BF16); Vpb = sbp("Vpb", (128, 8, 64), BF16); identc = sbp("identc", (128, 128), BF16)
            kb.op("pool", lambda e: e.tensor_copy(out=identc[:], in_=self.ident[:]), reads=["ident"], writes=["identc"])
            kb.op("pool", lambda e: e.memset(STb[:], 0.0), writes=[("STb", h) for h in range(8)])
        else:
            STb, Vpb, identc = ST, Vp, self.ident
        kSTb = (lambda hh: ("STb", hh)) if self.CORE_BF16 else (lambda hh: ("ST", hh))
        kVpb = (lambda hh: ("Vpb", hh)) if self.CORE_BF16 else (lambda hh: ("Vp", hh))

        def vp_ready():
            if self.CORE_BF16:
                kb.op("pool", lambda e: e.tensor_copy(out=Vpb[:], in_=Vp[:]), reads=[("Vp", h) for h in range(8)], writes=[("Vpb", h) for h in range(8)])
        ytm = rp[:, D:2 * D]; zrt = rp[:, 0:D]; yst = sbp("yst", (128, 64))
        aob = [sbp("raob%d" % i, (128, 8, 128), BF16) for i in range(2)]

        def zero_units():
            for gi in range(NG):
                kb.op("pool", lambda e, gi=gi: e.memset(G_[gi]["UBD"][:], 0.0), writes=[("g", gi, "UBD")])
            kb.op("pool", lambda e: e.memset(BK[:], 0.0), writes=["BK"])
        zero_units()
        kb.op("pool", lambda e: e.memset(ST[:], 0.0), writes=[("ST", h) for h in range(8)])

        z_all = lambda ti: [("z", ti)]

        def prep(t0, L, ti, is_s):
            kb.dma("sp", xr[:L, :], S["z"][t0:t0 + L, O_RC:O_RC + R_SHIFT_W], reads=z_all(ti), writes=["xr"])
            if is_s:
                kb.dma("sp", rp[:L, :], I["state_rwkv_shift"][l], writes=["rp"])
            elif t0 == 0:
                kb.op("pool", lambda e: e.memset(rp[0:1, :], 0.0), writes=["rp"])
                kb.dma("sp", rp[1:L, :], S["z"][0:L - 1, O_RC:O_RC + R_SHIFT_W], reads=z_all(ti), writes=["rp"])
            else:
                kb.dma("sp", rp[:L, :], S["z"][t0 - 1:t0 + L - 1, O_RC:O_RC + R_SHIFT_W], reads=z_all(ti) + z_all(ti - 1), writes=["rp"])
            kb.op("dve", lambda e: e.tensor_tensor(out=rp[:L, :], in0=rp[:L, :], in1=xr[:L, :], op=ALU.subtract), reads=["rp", "xr"], writes=["rp"])
            kb.op("dve", lambda e: e.tensor_tensor(out=rp[:L, :], in0=rp[:L, :], in1=mu[:L, :], op=ALU.mult), reads=["rp", "mu"], writes=["rp"])
            kb.op("dve", lambda e: e.tensor_tensor(out=xr[:L, :], in0=xr[:L, :], in1=rp[:L, :], op=ALU.add), reads=["rp", "xr"], writes=["xr"])
            r_ = xr[:L, 0:D]; kr = xr[:L, D:2 * D]; vr = xr[:L, 2 * D:3 * D]
            kb.dma("pool", S["vs"][t0:t0 + L, :], vr, reads=["xr"], writes=[("vs", ti)])
            kb.op("act", lambda e: e.activation(out=xr[:L, 3 * D:3 * D + 64], in_=xr[:L, 3 * D:3 * D + 64], func=AF.Tanh), reads=["xr"], writes=["xr"])
            p = self.rr("ptr", 2)
            pt, kpt = self.ptr[p], "ptr%d" % p
            for i2 in range(2):
                kb.op("pe", lambda e, i2=i2, pt=pt: e.transpose(out=pt[:64, i2 * 128:i2 * 128 + L], in_=xr[:L, 3 * D + 64 * i2:3 * D + 64 * i2 + 64],
                                                             identity=self.ident[:L, :L]), reads=["xr", "ident"], writes=[kpt])
            kb.op("dve", lambda e, pt=pt: e.tensor_copy(out=lt[0:64, :, :L], in_=pt[:64, 0:256].rearrange("p (a t) -> p a t", a=2)[:, :, :L]), reads=[kpt], writes=["lt"])
            for i2, dst in enumerate(("lw", "a")):
                for hf in range(2):
                    pq = self.rr("pp", 4)
                    pp, kpp = self.pp[pq], "pp%d" % pq
                    kb.op("pe", lambda e, i2=i2, hf=hf, pp=pp: e.matmul(pp[:L, :], lhsT=lt[:, i2, :L], rhs=w2[:, i2, hf * 512:(hf + 1) * 512], start=True, stop=True),
                          reads=["lt", "lt1", "w2"], writes=[kpp])
                    kb.op("act", lambda e, dst=dst, hf=hf, pp=pp: e.activation(out=tv[dst][:L, hf * 512:(hf + 1) * 512], in_=pp[:L, :], func=AF.Sigmoid),
                          reads=[kpp], writes=[("tv", dst)])
            kb.op("dve", lambda e: e.tensor_scalar(out=tv["lw"][:L, :], in0=tv["lw"][:L, :], scalar1=EW, scalar2=None, op0=ALU.mult), reads=[("tv", "lw")], writes=[("tv", "lw")])
            kb.op("dve", lambda e: e.tensor_tensor(out=tv["an"][:L, :], in0=kr, in1=P_["r_k_k"][:L, :], op=ALU.mult), reads=["xr", "r_k_k"], writes=[("tv", "an")])
            kb.op("dve", lambda e: e.tensor_tensor(out=tv["b"][:L, :], in0=tv["an"][:L, :], in1=tv["an"][:L, :], op=ALU.mult), reads=[("tv", "an")], writes=[("tv", "b")])
            kb.op("dve", lambda e: e.tensor_reduce(out=ssq[:L, 0:16], in_=tv["b"][:L, :].rearrange("t (h j) -> t h j", h=16), axis=AX.X, op=ALU.add),
                  reads=[("tv", "b")], writes=["ssq"])
            kb.op("act", lambda e: e.activation(out=ssq[:L, 0:16], in_=ssq[:L, 0:16], func=AF.Sqrt), reads=["ssq"], writes=["ssq"])
            kb.op("dve", lambda e: e.tensor_scalar(out=ssq[:L, 0:16], in0=ssq[:L, 0:16], scalar1=1e-12, scalar2=None, op0=ALU.max), reads=["ssq"], writes=["ssq"])
            kb.op("dve", lambda e: e.reciprocal(out=ssq[:L, 16:32], in_=ssq[:L, 0:16]), reads=["ssq"], writes=["ssq"])
            hv = lambda x: x[:L, :].rearrange("t (h j) -> t h j", h=16)
            kb.op("dve", lambda e: e.tensor_tensor(out=hv(tv["an"]), in0=hv(tv["an"]), in1=ssq[:L, 16:32].unsqueeze(2).to_broadcast([L, 16, 64]), op=ALU.mult),
                  reads=[("tv", "an"), "ssq"], writes=[("tv", "an")])
            kb.op("dve", lambda e: e.tensor_tensor(out=tv["b"][:L, :], in0=tv["an"][:L, :], in1=tv["a"][:L, :], op=ALU.mult),
                  reads=[("tv", "an"), ("tv", "a")], writes=[("tv", "b")])
            kb.op("dve", lambda e: e.tensor_scalar(out=tv["an"][:L, :], in0=tv["an"][:L, :], scalar1=-1.0, scalar2=None, op0=ALU.mult),
                  reads=[("tv", "an"), ("tv", "b")], writes=[("tv", "an")])
            kb.op("dve", lambda e: e.scalar_tensor_tensor(out=tv["k"][:L, :], in0=tv["a"][:L, :], scalar=-1.0, in1=P_["r_k_a"][:L, :], op0=ALU.add, op1=ALU.mult),
                  reads=[("tv", "a"), "r_k_a"], writes=[("tv", "k")])
            kb.op("dve", lambda e: e.scalar_tensor_tensor(out=tv["k"][:L, :], in0=tv["k"][:L, :], scalar=1.0, in1=kr, op0=ALU.add, op1=ALU.mult),
                  reads=[("tv", "k"), "xr"], writes=[("tv", "k")])
            kb.op("dve", lambda e: e.tensor_tensor(out=tv["a"][:L, :], in0=tv["k"][:L, :], in1=P_["r_r_k"][:L, :], op=ALU.mult),
                  reads=[("tv", "k"), "r_r_k", ("tv", "b")], writes=[("tv", "a")])
            kb.op("dve", lambda e: e.tensor_tensor(out=tv["a"][:L, :], in0=tv["a"][:L, :], in1=r_, op=ALU.mult), reads=[("tv", "a"), "xr"], writes=[("tv", "a")])
            kb.op("dve", lambda e: e.tensor_reduce(out=ssq[:L, 32:48], in_=hv(tv["a"]), axis=AX.X, op=ALU.add), reads=[("tv", "a")], writes=["ssq2"])
            for (dst, src, ksrc) in (("at", tv["an"][:L, :], ("tv", "an")), ("rt", r_, "xr"), ("bh", tv["b"][:L, :], ("tv", "b")),
                                     ("kh", tv["k"][:L, :], ("tv", "k")), ("cum", tv["lw"][:L, :], ("tv", "lw"))):
                for half in range(2):
                    p = self.rr("ptr", 2)
                    pt, kpt = self.ptr[p], "ptr%d" % p
                    for k4 in range(4):
                        kc = half * 4 + k4
                        kb.op("pe", lambda e, k4=k4, kc=kc, pt=pt, src=src: e.transpose(out=pt[:, k4 * 128:k4 * 128 + L], in_=src[:, kc * 128:(kc + 1) * 128],
                                                                                  identity=self.ident[:L, :L]), reads=[ksrc, "ident"], writes=[kpt])
                    self.evac(fm[dst][:, half * 4:half * 4 + 4, :L], pt[:, :].rearrange("p (k t) -> p k t", k=4)[:, :, :L], [kpt], [("fm", dst)])
            Lc = 1 if is_s else 64
            nch = L // Lc
            lwT = fm["cum"]
            kb.op("dve", lambda e: e.tensor_copy(out=fm["et"][:, :, :L], in_=lwT[:, :, :L]), reads=[("fm", "cum")], writes=[("fm", "et")])
            if Lc > 1:
                for hh in range(8):
                    for c in range(nch):
                        kb.op("dve", lambda e, hh=hh, c=c: e.tensor_tensor_scan(out=fm["cum"][:, hh, c * Lc:(c + 1) * Lc], data0=self.onesf[:, :Lc],
                                                                             data1=fm["et"][:, hh, c * Lc:(c + 1) * Lc], initial=0.0, op0=ALU.mult, op1=ALU.add),
                              reads=[("fm", "et"), "onesf"], writes=[("fm", "cum")])
            F = lambda n: fm[n][:, :, :L]
            C4 = lambda n: fm[n][:, :, :L].rearrange("p k (c t) -> p k c t", t=Lc)
            kb.op("dve", lambda e: e.tensor_tensor(out=F("et"), in0=F("cum"), in1=F("et"), op=ALU.subtract), reads=[("fm", "cum"), ("fm", "et")], writes=[("fm", "et")])
            kb.op("act", lambda e: e.activation(out=F("et"), in_=F("et"), func=AF.Exp), reads=[("fm", "et")], writes=[("fm", "et")])
            kb.op("dve", lambda e: e.tensor_tensor(out=F("at"), in0=F("at"), in1=F("et"), op=ALU.mult), reads=[("fm", "at"), ("fm", "et")], writes=[("fm", "at")])
            kb.op("act", lambda e: e.activation(out=F("et"), in_=F("cum"), func=AF.Exp), reads=[("fm", "cum"), ("fm", "at")], writes=[("fm", "et")])
            kb.op("dve", lambda e: e.tensor_tensor(out=F("rt"), in0=F("rt"), in1=F("et"), op=ALU.mult), reads=[("fm", "rt"), ("fm", "et")], writes=[("fm", "rt")])
            kb.op("pool", lambda e: e.tensor_copy(out=wc[:, :, :nch], in_=C4("et")[:, :, :, Lc - 1]), reads=[("fm", "et")], writes=["wc"])
            kb.op("dve", lambda e: e.tensor_tensor(out=C4("et"), in0=C4("cum")[:, :, :, Lc - 1:Lc].to_broadcast([128, 8, nch, Lc]), in1=C4("cum"), op=ALU.subtract),
                  reads=[("fm", "cum"), ("fm", "rt"), "wc"], writes=[("fm", "et")])
            kb.op("act", lambda e: e.activation(out=F("et"), in_=F("et"), func=AF.Exp), reads=[("fm", "et")], writes=[("fm", "et")])
            kb.op("dve", lambda e: e.tensor_tensor(out=F("bc"), in0=F("bh"), in1=F("et"), op=ALU.mult), reads=[("fm", "bh"), ("fm", "et")], writes=[("fm", "bc")])
            kb.op("dve", lambda e: e.tensor_tensor(out=F("kc"), in0=F("kh"), in1=F("et"), op=ALU.mult), reads=[("fm", "kh"), ("fm", "et")], writes=[("fm", "kc")])
            kb.op("act", lambda e: e.activation(out=F("et"), in_=F("cum"), func=AF.Exp, scale=-1.0), reads=[("fm", "cum"), ("fm", "bc"), ("fm", "kc")], writes=[("fm", "et")])
            kb.op("dve", lambda e: e.tensor_tensor(out=F("bh"), in0=F("bh"), in1=F("et"), op=ALU.mult), reads=[("fm", "bh"), ("fm", "et")], writes=[("fm", "bh")])
            kb.op("dve", lambda e: e.tensor_tensor(out=F("kh"), in0=F("kh"), in1=F("et"), op=ALU.mult), reads=[("fm", "kh"), ("fm", "et")], writes=[("fm", "kh")])

        def bank():
            q_ = self.rr("bank8", 8)
            return self.bank8[q_], self.bank8k[q_]

        def core(gi, col0, Lc, cidx):
            g = G_[gi]
            K_ = lambda n: ("g", gi, n)
            hs = slice(4 * gi, 4 * gi + 4)
            cs_ = slice(col0, col0 + Lc)
            for half in range(2):
                ps = slice(half * 64, half * 64 + 64)
                kb.op("dve" if half == 0 else "pool", lambda e, ps=ps, half=half: e.tensor_copy(
                    out=g["UBD"][ps, :, :, half * 64:half * 64 + Lc], in_=fmall[ps, hs, 0:4, cs_]),
                    reads=[("fm", n) for n in ("at", "rt", "bh", "kh")], writes=[K_("UBD")])
            yield
            for half in range(2):
                ps = slice(half * 64, half * 64 + 64)
                kb.op("pool" if half == 0 else "dve", lambda e, ps=ps, half=half: e.tensor_copy(
                    out=BK[ps, :, :, half * 64:half * 64 + Lc], in_=fmall[ps, hs, 4:6, cs_]),
                    reads=[("fm", "bc"), ("fm", "kc")], writes=["BK"])
            for vi, dst in enumerate(("Btm", "Ktm")):
                pb_, kpb = bank()
                for u in range(4):
                    kb.op("pe", lambda e, u=u, vi=vi, pb_=pb_: e.matmul(pb_[:, u * 128:(u + 1) * 128], lhsT=BK[:, u, vi, :], rhs=identc[:, :], start=True, stop=True),
                          reads=["BK", "identc"], writes=[kpb])
                self.evac(g[dst][:, :, :], pb_[:, :].rearrange("p (u m) -> p u m", u=4), [kpb], [K_(dst)])
            yield
            for (vec, dst) in ((2, "MNb"), (3, "MNk")):
                for pr in range(2):
                    pb_, kpb = bank()
                    for u2 in range(2):
                        u = pr * 2 + u2
                        kb.op("pe", lambda e, u=u, u2=u2, vec=vec, pb_=pb_: e.matmul(pb_[:, u2 * 256:(u2 + 1) * 256], lhsT=g["UBD"][:, u, vec, :],
                                                                              rhs=g["UBD"][:, u, 0:2, :].rearrange("p a m -> p (a m)"), start=True, stop=True),
                              reads=[K_("UBD")], writes=[kpb])
                    kb.op("dve", lambda e, pr=pr, dst=dst, pb_=pb_: e.tensor_tensor(out=g[dst][:, pr * 2:pr * 2 + 2, :], in0=pb_[:, :].rearrange("p (u m) -> p u m", u=2),
                                                                           in1=mks[:, None, 0:256].to_broadcast([128, 2, 256]), op=ALU.mult),
                          reads=[kpb, "mks"], writes=[K_(dst)])
            pb_, kpb = bank()
            for u in range(4):
                kb.op("pe", lambda e, u=u, pb_=pb_: e.matmul(pb_[:, u * 128:(u + 1) * 128], lhsT=g["UBD"][:, u, 0, :], rhs=g["UBD"][:, u, 2, :], start=True, stop=True),
                      reads=[K_("UBD")], writes=[kpb])
            kb.op("dve", lambda e, pb_=pb_: e.tensor_tensor(out=g["Xt"][:, :, :], in0=pb_[:, :].rearrange("p (u m) -> p u m", u=4),
                                                       in1=mks[:, None, 256:384].to_broadcast([128, 4, 128]), op=ALU.mult),
                  reads=[kpb, "mks"], writes=[K_("Xt")])
            kb.op("act", lambda e: e.activation(out=g["X"][:, :, :], in_=g["MNb"][:, :, 0:128], func=AF.Copy), reads=[K_("MNb")], writes=[K_("X")])
            kb.op("dve", lambda e: e.tensor_tensor(out=g["P"][:, :, :], in0=g["MNb"][:, :, 0:128], in1=identc[:, None, :].to_broadcast([128, 4, 128]), op=ALU.add),
                  reads=[K_("MNb"), "identc"], writes=[K_("P")])
            yield
            nlev = 0
            while (1 << (nlev + 1)) < Lc:
                nlev += 1
            for lev in range(nlev if Lc > 1 else 0):
                lastlev = (lev == nlev - 1)
                pb1 = kp1 = None
                if not lastlev:
                    pb1, kp1 = bank()
                    for u in range(4):
                        kb.op("pe", lambda e, u=u, pb1=pb1: e.matmul(pb1[:, u * 128:(u + 1) * 128], lhsT=g["Xt"][:, u, :], rhs=g["X"][:, u, :], start=True, stop=True),
                              reads=[K_("X"), K_("Xt")], writes=[kp1])
                pb2, kp2 = bank()
                for u in range(4):
                    kb.op("pe", lambda e, u=u, pb2=pb2: e.matmul(pb2[:, u * 128:(u + 1) * 128], lhsT=g["X"][:, u, :], rhs=g["Xt"][:, u, :], start=True, stop=True),
                          reads=[K_("X"), K_("Xt")], writes=[kp2])
                if not lastlev:
                    self.evac(g["X"][:, :, :], pb1[:, :].rearrange("p (u m) -> p u m", u=4), [kp1], [K_("X")])
                self.evac(g["Xt"][:, :, :], pb2[:, :].rearrange("p (u m) -> p u m", u=4), [kp2], [K_("Xt")])
                pb3, kp3 = bank()
                for u in range(4):
                    kb.op("pe", lambda e, u=u, pb3=pb3: e.matmul(pb3[:, u * 128:(u + 1) * 128], lhsT=g["Xt"][:, u, :], rhs=g["P"][:, u, :], start=True, stop=True),
                          reads=[K_("Xt"), K_("P")], writes=[kp3])
                kb.op("dve", lambda e, pb3=pb3: e.tensor_tensor(out=g["P"][:, :, :], in0=pb3[:, :].rearrange("p (u m) -> p u m", u=4), in1=g["P"][:, :, :], op=ALU.add),
                      reads=[kp3, K_("P")], writes=[K_("P")])
                yield
            kS = [("ST", 4 * gi + u) for u in range(4)]
            kSb = [kSTb(4 * gi + u) for u in range(4)]
            kVb = [kVpb(4 * gi + u) for u in range(4)]
            pb_, kpb = bank()
            for u in range(4):
                hh = 4 * gi + u
                kb.op("pe", lambda e, u=u, hh=hh, pb_=pb_: e.matmul(pb_[:, u * 64:(u + 1) * 64], lhsT=g["UBD"][:, u, 0, :], rhs=STb[:, hh, :], start=True, stop=False),
                      reads=[K_("UBD"), kSb[u]], writes=[kpb])
                kb.op("pe", lambda e, u=u, hh=hh, pb_=pb_: e.matmul(pb_[:, u * 64:(u + 1) * 64], lhsT=g["MNk"][:, u, 0:128], rhs=Vpb[:, hh, :], start=False, stop=True),
                      reads=[K_("MNk"), kVb[u]], writes=[kpb])
            self.evac(g["RHS"][:, :, :], pb_[:, 0:256].rearrange("p (u m) -> p u m", u=4), [kpb], [K_("RHS")])
            pb_, kpb = bank()
            for u in range(4):
                kb.op("pe", lambda e, u=u, pb_=pb_: e.matmul(pb_[:, u * 64:(u + 1) * 64], lhsT=g["P"][:, u, :], rhs=g["RHS"][:, u, :], start=True, stop=True),
                      reads=[K_("P"), K_("RHS")], writes=[kpb])
            self.evac(g["U"][:, :, :], pb_[:, 0:256].rearrange("p (u m) -> p u m", u=4), [kpb], [K_("U")])
            yield
            pb_, kpb = bank()
            for u in range(4):
                hh = 4 * gi + u
                kb.op("pe", lambda e, u=u, hh=hh, pb_=pb_: e.matmul(pb_[:, u * 64:(u + 1) * 64], lhsT=g["UBD"][:, u, 1, :], rhs=STb[:, hh, :], start=True, stop=False),
                      reads=[K_("UBD"), kSb[u]], writes=[kpb])
                kb.op("pe", lambda e, u=u, pb_=pb_: e.matmul(pb_[:, u * 64:(u + 1) * 64], lhsT=g["MNb"][:, u, 128:256], rhs=g["U"][:, u, :], start=False, stop=False),
                      reads=[K_("MNb"), K_("U")], writes=[kpb])
                kb.op("pe", lambda e, u=u, hh=hh, pb_=pb_: e.matmul(pb_[:, u * 64:(u + 1) * 64], lhsT=g["MNk"][:, u, 128:256], rhs=Vpb[:, hh, :], start=False, stop=True),
                      reads=[K_("MNk"), kVb[u]], writes=[kpb])
            self.evac(Ych[:, hs, :], pb_[:, 0:256].rearrange("p (u m) -> p u m", u=4), [kpb], [("Ych", 4 * gi + u) for u in range(4)])
            pb_, kpb = bank()
            for u in range(4):
                hh = 4 * gi + u
                kb.op("pe", lambda e, u=u, pb_=pb_: e.matmul(pb_[:, u * 64:(u + 1) * 64], lhsT=g["Btm"][:, u, :], rhs=g["U"][:, u, :], start=True, stop=False),
                      reads=[K_("Btm"), K_("U")], writes=[kpb])
                kb.op("pe", lambda e, u=u, hh=hh, pb_=pb_: e.matmul(pb_[:, u * 64:(u + 1) * 64], lhsT=g["Ktm"][:, u, :], rhs=Vpb[:, hh, :], start=False, stop=True),
                      reads=[K_("Ktm"), kVb[u]], writes=[kpb])
            kb.op("dve", lambda e: e.tensor_tensor(out=ST[:, hs, :], in0=ST[:, hs, :], in1=wc[:, hs, cidx:cidx + 1].to_broadcast([128, 4, 64]), op=ALU.mult),
                  reads=kS + ["wc"], writes=kS)
            kb.op("dve", lambda e, pb_=pb_: e.tensor_tensor(out=ST[:, hs, :], in0=ST[:, hs, :], in1=pb_[:, 0:256].rearrange("p (u m) -> p u m", u=4), op=ALU.add),
                  reads=kS + [kpb], writes=kS)
            if self.CORE_BF16:
                kb.op("act", lambda e: e.activation(out=STb[:, hs, :], in_=ST[:, hs, :], func=AF.Copy), reads=kS, writes=kSb)
            yield

        def run_core(col0, Lc, cidx):
            alive = [core(gi, col0, Lc, cidx) for gi in range(NG)]
            while alive:
                for g_ in list(alive):
                    try:
                        next(g_)
                    except StopIteration:
                        alive.remove(g_)

        def post(t0, L, ti, bi):
            kb.dma("sp", ytm[:L, :], S["ys"][t0:t0 + L, :], reads=[("ys", ti)], writes=["rp"])
            kb.dma("sp", zrt[:L, :], S["z"][t0:t0 + L, O_ZR:O_ZR + D], reads=z_all(ti), writes=["rp"])
            kb.op("act", lambda e: e.activation(out=zrt[:L, :], in_=zrt[:L, :], func=AF.Silu), reads=["rp"], writes=["rp"])
            hv = lambda x: x[:L, :].rearrange("t (h j) -> t h j", h=16)
            kb.op("dve", lambda e: e.tensor_reduce(out=yst[:L, 0:16], in_=hv(ytm), axis=AX.X, op=ALU.add), reads=["rp"], writes=["yst"])
            kb.op("dve", lambda e: e.tensor_scalar(out=yst[:L, 0:16], in0=yst[:L, 0:16], scalar1=-1.0 / 64.0, scalar2=None, op0=ALU.mult), reads=["yst"], writes=["yst"])
            kb.op("dve", lambda e: e.tensor_tensor(out=hv(ytm), in0=hv(ytm), in1=yst[:L, 0:16].unsqueeze(2).to_broadcast([L, 16, 64]), op=ALU.add),
                  reads=["rp", "yst"], writes=["rp"])
            kb.op("dve", lambda e: e.tensor_tensor(out=tv["lw"][:L, :], in0=ytm[:L, :], in1=ytm[:L, :], op=ALU.mult), reads=["rp"], writes=[("tv", "lw")])
            kb.op("dve", lambda e: e.tensor_reduce(out=yst[:L, 16:32], in_=hv(tv["lw"]), axis=AX.X, op=ALU.add), reads=[("tv", "lw")], writes=["yst"])
            kb.op("act", lambda e: e.activation(out=yst[:L, 32:48], in_=yst[:L, 16:32], func=AF.Sqrt, scale=1.0 / 64.0, bias=e12[:L, 0:1]), reads=["yst", "e12"], writes=["yst"])
            kb.op("dve", lambda e: e.reciprocal(out=yst[:L, 48:64], in_=yst[:L, 32:48]), reads=["yst"], writes=["yst"])
            kb.op("dve", lambda e: e.tensor_tensor(out=hv(ytm), in0=hv(ytm), in1=yst[:L, 48:64].unsqueeze(2).to_broadcast([L, 16, 64]), op=ALU.mult),
                  reads=["rp", "yst"], writes=["rp"])
            kb.op("dve", lambda e: e.tensor_tensor(out=ytm[:L, :], in0=ytm[:L, :], in1=P_["r_ln_g"][:L, :], op=ALU.mult), reads=["rp", "r_ln_g"], writes=["rp"])
            kb.op("dve", lambda e: e.tensor_tensor(out=ytm[:L, :], in0=ytm[:L, :], in1=P_["r_ln_b"][:L, :], op=ALU.add), reads=["rp", "r_ln_b"], writes=["rp"])
            kb.op("dve", lambda e: e.tensor_tensor(out=hv(tv["lw"]), in0=xr[:L, 2 * D:3 * D].rearrange("t (h j) -> t h j", h=16),
                                                   in1=ssq[:L, 32:48].unsqueeze(2).to_broadcast([L, 16, 64]), op=ALU.mult), reads=["xr", "ssq2"], writes=[("tv", "lw")])
            kb.op("dve", lambda e: e.tensor_tensor(out=ytm[:L, :], in0=ytm[:L, :], in1=tv["lw"][:L, :], op=ALU.add), reads=["rp", ("tv", "lw")], writes=["rp"])
            kb.op("dve", lambda e: e.tensor_tensor(out=ytm[:L, :], in0=ytm[:L, :], in1=zrt[:L, :], op=ALU.mult), reads=["rp", "rp"], writes=["rp"])
            aj = self.rr("raob", 2)
            for half in range(2):
                p = self.rr("ptr", 2)
                pt, kpt = self.ptr[p], "ptr%d" % p
                for k4 in range(4):
                    kc = half * 4 + k4
                    kb.op("pe", lambda e, k4=k4, kc=kc, pt=pt: e.transpose(out=pt[:, k4 * 128:k4 * 128 + L], in_=ytm[:L, kc * 128:(kc + 1) * 128],
                                                                       identity=self.ident[:L, :L]), reads=["rp", "ident"], writes=[kpt])
                self.evac(aob[aj][:, half * 4:half * 4 + 4, :L], pt[:, :].rearrange("p (k t) -> p k t", k=4)[:, :, :L], [kpt], ["raob%d" % aj])
            kb.dma("pool", S["act_r"][:, t0:t0 + L].rearrange("(k p) t -> p k t", p=128), aob[aj][:, :, :L], reads=["raob%d" % aj], writes=[("act", "r", bi)])

        natx8 = sbp("rnatx8", (128, 8, 128))
        kb.op("pool", lambda e: e.memset(natx8[:], 0.0), writes=["natx8"])

        def bd_transpose(src, ksrc, dst, kdst, also_b=None):
            for half in range(2):
                ps = slice(half * 64, half * 64 + 64)
                kb.op("pool" if half else "dve", lambda e, ps=ps: e.tensor_copy(out=natx8[ps, :, ps], in_=src[ps, :, :]), reads=ksrc, writes=["natx8"])
            for q4 in range(2):
                pb_, kpb = bank()
                for u in range(4):
                    kb.op("pe", lambda e, u=u, q4=q4, pb_=pb_: e.transpose(out=pb_[:, u * 128:(u + 1) * 128], in_=natx8[:, q4 * 4 + u, :], identity=self.ident[:, :]),
                          reads=["natx8", "ident"], writes=[kpb])
                for half in range(2):
                    ps = slice(half * 64, half * 64 + 64)
                    kb.op("dve" if half else "act", (lambda e, ps=ps, q4=q4, pb_=pb_: e.tensor_copy(out=dst[ps, q4 * 4:q4 * 4 + 4, :], in_=pb_[:, :].rearrange("p (u m) -> p u m", u=4)[ps, :, ps]))
                          if half else (lambda e, ps=ps, q4=q4, pb_=pb_: e.activation(out=dst[ps, q4 * 4:q4 * 4 + 4, :], in_=pb_[:, :].rearrange("p (u m) -> p u m", u=4)[ps, :, ps], func=AF.Copy)),
                          reads=[kpb], writes=kdst)

        def state_out(dst):
            bd_transpose(ST, [("ST", h) for h in range(8)], nato, ["nato"])
            kb.dma("pool", dst.rearrange("(hh hl) i j -> (hl i) hh j", hl=2), nato[:, :, :], reads=["nato"], writes=[], is_output=True)

        def state_in(src):
            kb.dma("sp", nat[:, :, :], src.rearrange("(hh hl) i j -> (hl i) hh j", hl=2), writes=["nat"])
            bd_transpose(nat, ["nat"], ST, [("ST", h) for h in range(8)])
            if self.CORE_BF16:
                kb.op("pool", lambda e: e.tensor_copy(out=STb[:, :, :], in_=ST[:, :, :]), reads=[("ST", h) for h in range(8)], writes=[("STb", h) for h in range(8)])

        blk_of = lambda t: next(i for i, (b0, bn) in enumerate(self.blocks) if b0 <= t < b0 + bn)
        for ti in range(T // 128):
            t0 = ti * 128
            prep(t0, 128, ti, False)
            for c in range(2):
                for half in range(2):
                    kb.dma("sp", Vp[half * 64:half * 64 + 64, :, :],
                           S["vs"][t0 + c * 64:t0 + c * 64 + 64, :].rearrange("t (hh hl i) -> hl t hh i", hl=2, i=64)[half],
                           reads=[("vs", ti)], writes=[("Vp", h) for h in range(8)])
                vp_ready()
                run_core(c * 64, 64, c)
                for half in range(2):
                    kb.dma("pool", S["ys"][t0 + c * 64:t0 + c * 64 + 64, :].rearrange("t (hh hl i) -> hl t hh i", hl=2, i=64)[half],
                           Ych[half * 64:half * 64 + 64, :, :], reads=[("Ych", h) for h in range(8)], writes=[("ys", ti)])
            post(t0, 128, ti, blk_of(t0))
        state_out(O["wkv_prompt"][l])
        if NS:
            ti = T // 128
            prep(T, NS, ti, True)
            zero_units()
            kb.op("pool", lambda e: e.memset(Vp[:], 0.0), writes=[("Vp", h) for h in range(8)])
            for b in range(NS):
                state_in(I["state_rwkv_wkv"][l, b])
                for half in range(2):
                    kb.dma("sp", Vp[half * 64:half * 64 + 1, :, :],
                           S["vs"][T + b:T + b + 1, :].rearrange("t (hh hl i) -> hl t hh i", hl=2, i=64)[half],
                           reads=[("vs", ti)], writes=[("Vp", h) for h in range(8)])
                vp_ready()
                run_core(b, 1, b)
                for half in range(2):
                    kb.dma("pool", S["ys"][T + b:T + b + 1, :].rearrange("t (hh hl i) -> hl t hh i", hl=2, i=64)[half],
                           Ych[half * 64:half * 64 + 1, :, :], reads=[("Ych", h) for h in range(8)], writes=[("ys", ti)])
                state_out(O["wkv_sample"][l, b])
            post(T, NS, ti, blk_of(T))
        self.end_phase()

    def sin_to(self, tkey, out, ang, tf, ti, tm, key_out, key_ang, shift=0.0):
        kb = self.kb
        TWO_PI = 2.0 * math.pi
        kt = ("sin_tmp", tkey)
        kb.op("dve", lambda e: e.tensor_scalar(out=tm, in0=ang, scalar1=shift, scalar2=None, op0=ALU.add),
              reads=[key_ang], writes=[(kt, "m")])
        kb.op("dve", lambda e: e.tensor_scalar(out=tf, in0=tm, scalar1=1.0 / TWO_PI, scalar2=None, op0=ALU.mult),
              reads=[(kt, "m")], writes=[(kt, "f")])
        kb.op("dve", lambda e: e.tensor_copy(out=ti, in_=tf), reads=[(kt, "f")], writes=[(kt, "i")])
        kb.op("dve", lambda e: e.tensor_copy(out=tf, in_=ti), reads=[(kt, "i")], writes=[(kt, "f")])
        kb.op("dve", lambda e: e.scalar_tensor_tensor(out=tm, in0=tf, scalar=-TWO_PI, in1=tm, op0=ALU.mult, op1=ALU.add),
              reads=[(kt, "f"), (kt, "m")], writes=[(kt, "m")])
        kb.op("dve", lambda e: e.tensor_scalar(out=tf, in0=tm, scalar1=math.pi, scalar2=-TWO_PI, op0=ALU.is_gt, op1=ALU.mult),
              reads=[(kt, "m")], writes=[(kt, "f")])
        kb.op("dve", lambda e: e.tensor_tensor(out=tm, in0=tm, in1=tf, op=ALU.add), reads=[(kt, "f"), (kt, "m")],
              writes=[(kt, "m")])
        kb.op("dve", lambda e: e.tensor_scalar(out=tf, in0=tm, scalar1=-math.pi, scalar2=TWO_PI, op0=ALU.is_lt, op1=ALU.mult),
              reads=[(kt, "m")], writes=[(kt, "f")])
        kb.op("dve", lambda e: e.tensor_tensor(out=tm, in0=tm, in1=tf, op=ALU.add), reads=[(kt, "f"), (kt, "m")],
              writes=[(kt, "m")])
        kb.op("dve", lambda e: e.tensor_scalar(out=tm, in0=tm, scalar1=-3.1415925, scalar2=3.1415925, op0=ALU.max, op1=ALU.min),
              reads=[(kt, "m")], writes=[(kt, "m")])
        kb.op("act", lambda e: e.activation(out=out, in_=tm, func=AF.Sin), reads=[(kt, "m")], writes=[key_out])

    def s5_phase(self, l):
        kb = self.kb
        I, S, O = self.I, self.S, self.O
        T, NS = self.T, self.NS
        self.begin_phase()
        sbp = self.sbp
        PT = sbp("PT", (128, 6, 32))
        Bz = [sbp("Bz%d" % i, (128, 32, 128), BF16) for i in range(2)]
        Cz = [sbp("Cz%d" % i, (128, 32, 128), BF16) for i in range(2)]
        cosT = sbp("cosT", (128, 32, 256)); sinT = sbp("sinT", (128, 32, 256))
        car = [sbp("car%d" % i, (128, 32)) for i in range(2)]
        ctmp = sbp("ctmp", (128, 4))
        dcol = sbp("dcol", (128, 8)); gbcol = sbp("gbcol", (128, 8))
        wg = sbp("wglu", (128, KC, D), BF16)
        s0T = [sbp("s0T%d" % i, (128, 32, 16)) for i in range(2)]
        snw = [sbp("snw%d" % i, (128, 32, 16)) for i in range(2)]
        sst = sbp("sst", (32, 512))
        outer_pes = self.pes
        self.pes = ExitStack()
        nat = sbp("nat", (32, 14, 128))
        nati = sbp("nati", (32, 128), I32)
        BR = sbp("BR", (128, 32, 16)); BI = sbp("BI", (128, 32, 16))
        bbr = sbp("bbr", (128, 32, 16)); bbi = sbp("bbi", (128, 32, 16))
        xin = sbp("xin", (128, 512))
        btmp = xin[:, :].rearrange("p (s c) -> p s c", c=16)
        Cn = [sbp("Cn%d" % i, (128, 8, 64)) for i in range(2)]
        ang = sbp("ang", (128, 2, 256)); angf = sbp("angf", (128, 2, 256)); angm = sbp("angm", (128, 2, 256))
        angi = sbp("angi", (128, 2, 256), I32)
        m3 = sbp("m3", (128, 4, 2)); m4 = sbp("m4", (128, 4, 8)); iota1 = sbp("iota1", (128, 256))
        wstg = [sbp("wstg%d" % i, (128, KC, 128)) for i in range(1)]

        kb.dma("sp", m3[:], I["mask3"], writes=["m3"])
        kb.dma("sp", m4[:], I["mask4"], writes=["m4"])
        kb.dma("sp", iota1[:], I["iota1"], writes=["iota1"])
        kb.dma("sp", nat[:, 0, :], I["s_lam_re"][l].rearrange("(s g) p -> s (g p)", g=2), writes=["nat0"])
        kb.dma("sp", nat[:, 1, :], I["s_lam_im"][l].rearrange("(s g) p -> s (g p)", g=2), writes=["nat1"])
        kb.dma("sp", nat[:, 2, 0:2], I["s_log_dt"][l].rearrange("(s g) -> s g", g=2), writes=["nat2"])
        kb.dma("sp", dcol[:], I["s_d"][l].rearrange("(k p) -> p k", p=128), writes=["dcol"], allow_slow_non_contiguous=True)
        kb.dma("sp", gbcol[:], I["s_glu_b"][l].rearrange("(k p) -> p k", p=128), writes=["gbcol"], allow_slow_non_contiguous=True)
        kb.dma("sp", BR[:], I["s_b_re"][l].rearrange("(s g) p c -> (g p) s c", g=2), writes=["BR"])
        kb.dma("sp", BI[:], I["s_b_im"][l].rearrange("(s g) p c -> (g p) s c", g=2), writes=["BI"])
        kb.dma("sp", Cn[0][:], I["s_c_re"][l].rearrange("(o g) c p -> (g c) o p", g=8), writes=["Cn0"])
        kb.dma("sp", Cn[1][:], I["s_c_im"][l].rearrange("(o g) c p -> (g c) o p", g=8), writes=["Cn1"])
        for h in range(8):
            kb.dma("sp", wstg[0][:], I["s_glu_w"][l][:, h * 128:(h + 1) * 128].rearrange("(k p) c -> p k c", p=128),
                   writes=["wstg"])
            kb.op("dve", lambda e, h=h: e.tensor_copy(out=wg[:, :, h * 128:(h + 1) * 128], in_=wstg[0][:]),
                  reads=["wstg"], writes=["wg"])
        N = lambda i: nat[:, i, :]
        kb.op("act", lambda e: e.activation(out=nat[:, 2, 2:4], in_=nat[:, 2, 0:2], func=AF.Exp), reads=["nat2"], writes=["nat2"])
        kb.op("dve", lambda e: e.tensor_copy(out=nat[:, 3, :].rearrange("s (g p) -> s g p", g=2),
                                             in_=nat[:, 2, 2:4].unsqueeze(2).to_broadcast([32, 2, 64])),
              reads=["nat2"], writes=["nat3"])
        kb.op("dve", lambda e: e.tensor_scalar(out=N(0), in0=N(0), scalar1=-1e-4, scalar2=None, op0=ALU.min),
              reads=["nat0"], writes=["nat0"])
        kb.op("dve", lambda e: e.tensor_tensor(out=N(4), in0=N(0), in1=N(3), op=ALU.mult), reads=["nat0", "nat3"], writes=["nat4"])
        kb.op("act", lambda e: e.activation(out=N(4), in_=N(4), func=AF.Exp), reads=["nat4"], writes=["nat4"])
        kb.op("dve", lambda e: e.tensor_tensor(out=N(5), in0=N(1), in1=N(3), op=ALU.mult), reads=["nat1", "nat3"], writes=["nat5"])
        self.sin_to("nat", N(6), N(5), N(12), nati[:, :], N(13), "nat6", "nat5")
        self.sin_to("nat", N(7), N(5), N(12), nati[:, :], N(13), "nat7", "nat5", shift=math.pi / 2)
        kb.op("dve", lambda e: e.tensor_tensor(out=N(8), in0=N(4), in1=N(7), op=ALU.mult), reads=["nat4", "nat7"], writes=["nat8"])
        kb.op("dve", lambda e: e.tensor_tensor(out=N(9), in0=N(4), in1=N(6), op=ALU.mult), reads=["nat4", "nat6"], writes=["nat9"])
        kb.op("dve", lambda e: e.tensor_tensor(out=N(12), in0=N(0), in1=N(0), op=ALU.mult), reads=["nat0"], writes=["nat12"])
        kb.op("dve", lambda e: e.tensor_tensor(out=N(13), in0=N(1), in1=N(1), op=ALU.mult), reads=["nat1"], writes=["nat13"])
        kb.op("dve", lambda e: e.tensor_tensor(out=N(12), in0=N(12), in1=N(13), op=ALU.add), reads=["nat12", "nat13"], writes=["nat12"])
        kb.op("dve", lambda e: e.reciprocal(out=N(12), in_=N(12)), reads=["nat12"], writes=["nat12"])
        kb.op("dve", lambda e: e.tensor_scalar(out=N(13), in0=N(8), scalar1=-1.0, scalar2=None, op0=ALU.add), reads=["nat8"], writes=["nat13"])
        kb.op("dve", lambda e: e.tensor_tensor(out=N(10), in0=N(13), in1=N(0), op=ALU.mult), reads=["nat13", "nat0"], writes=["nat10"])
        kb.op("dve", lambda e: e.tensor_tensor(out=N(11), in0=N(9), in1=N(1), op=ALU.mult), reads=["nat9", "nat1"], writes=["nat11"])
        kb.op("dve", lambda e: e.tensor_tensor(out=N(10), in0=N(10), in1=N(11), op=ALU.add), reads=["nat10", "nat11"], writes=["nat10"])
        kb.op("dve", lambda e: e.tensor_tensor(out=N(10), in0=N(10), in1=N(12), op=ALU.mult), reads=["nat10", "nat12"], writes=["nat10"])
        kb.op("dve", lambda e: e.tensor_tensor(out=N(11), in0=N(9), in1=N(0), op=ALU.mult), reads=["nat9", "nat0"], writes=["nat11"])
        kb.op("dve", lambda e: e.tensor_tensor(out=N(13), in0=N(13), in1=N(1), op=ALU.mult), reads=["nat13", "nat1"], writes=["nat13"])
        kb.op("dve", lambda e: e.tensor_tensor(out=N(11), in0=N(11), in1=N(13), op=ALU.subtract), reads=["nat11", "nat13"], writes=["nat11"])
        kb.op("dve", lambda e: e.tensor_tensor(out=N(11), in0=N(11), in1=N(12), op=ALU.mult), reads=["nat11", "nat12"], writes=["nat11"])
        p = self.rr("ptr", 2)
        pt, kpt = self.ptr[p], "ptr%d" % p
        for si, ni in enumerate((4, 5, 8, 9, 10, 11)):
            kb.op("pe", lambda e, si=si, ni=ni, pt=pt: e.transpose(out=pt[:, si * 32:(si + 1) * 32], in_=nat[:, ni, :],
                                                                  identity=self.ident[:32, :32]),
                  reads=["nat%d" % ni, "ident"], writes=[kpt])
        kb.op("dve", lambda e, pt=pt: e.tensor_copy(out=PT[:, :, :], in_=pt[:, 0:192].rearrange("p (s c) -> p s c", s=6)),
              reads=[kpt], writes=["PT"])
        bc = lambda si: PT[:, si, :].unsqueeze(2).to_broadcast([128, 32, 16])
        kb.op("dve", lambda e: e.tensor_tensor(out=bbr[:], in0=BR[:], in1=bc(4), op=ALU.mult), reads=["BR", "PT"], writes=["bbr"])
        kb.op("dve", lambda e: e.tensor_tensor(out=btmp, in0=BI[:], in1=bc(5), op=ALU.mult), reads=["BI", "PT"], writes=["xin"])
        kb.op("dve", lambda e: e.tensor_tensor(out=bbr[:], in0=bbr[:], in1=btmp, op=ALU.subtract), reads=["bbr", "xin"], writes=["bbr"])
        kb.op("dve", lambda e: e.tensor_tensor(out=bbi[:], in0=BI[:], in1=bc(4), op=ALU.mult), reads=["BI", "PT"], writes=["bbi"])
        kb.op("dve", lambda e: e.tensor_tensor(out=btmp, in0=BR[:], in1=bc(5), op=ALU.mult), reads=["BR", "PT", "bbr"], writes=["xin"])
        kb.op("dve", lambda e: e.tensor_tensor(out=bbi[:], in0=bbi[:], in1=btmp, op=ALU.add), reads=["bbi", "xin"], writes=["bbi"])
        for ri, (bb, kbb) in enumerate(((bbr, "bbr"), (bbi, "bbi"))):
            for oc in range(8):
                kb.op("dve", lambda e, bb=bb, oc=oc: e.tensor_tensor(
                    out=xin[:, :].rearrange("p (q g c) -> p q g c", q=4, g=8),
                    in0=bb[:, oc * 4:oc * 4 + 4, None, :].to_broadcast([128, 4, 8, 16]),
                    in1=m4[:, :, :, None].to_broadcast([128, 4, 8, 16]), op=ALU.mult),
                    reads=[kbb, "m4"], writes=["xin"])
                p = self.rr("ptr", 2)
                pt, kpt = self.ptr[p], "ptr%d" % p
                for q in range(4):
                    kb.op("pe", lambda e, q=q, pt=pt: e.transpose(out=pt[:, q * 128:(q + 1) * 128], in_=xin[:, q * 128:(q + 1) * 128],
                                                                  identity=self.ident[:, :]),
                          reads=["xin", "ident"], writes=[kpt])
                kb.op("act", lambda e, pt=pt, oc=oc, ri=ri: e.activation(
                    out=Bz[ri][:, oc * 4:oc * 4 + 4, :], in_=pt[:, :].rearrange("p (q m) -> p q m", q=4), func=AF.Copy),
                    reads=[kpt], writes=[("Bz", ri)])
        for ri in range(2):
            for oc in range(8):
                kb.op("dve", lambda e, ri=ri, oc=oc: e.tensor_tensor(
                    out=xin[:, :].rearrange("p (q g s) -> p q g s", q=4, g=2),
                    in0=Cn[ri][:, oc, None, None, :].to_broadcast([128, 4, 2, 64]),
                    in1=m3[:, :, :, None].to_broadcast([128, 4, 2, 64]), op=ALU.mult),
                    reads=["Cn%d" % ri, "m3"], writes=["xin"])
                p = self.rr("ptr", 2)
                pt, kpt = self.ptr[p], "ptr%d" % p
                for q in range(4):
                    kb.op("pe", lambda e, q=q, pt=pt: e.transpose(out=pt[:, q * 128:(q + 1) * 128], in_=xin[:, q * 128:(q + 1) * 128],
                                                                  identity=self.ident[:, :]),
                          reads=["xin", "ident"], writes=[kpt])
                kb.op("act", lambda e, pt=pt, oc=oc, ri=ri: e.activation(
                    out=Cz[ri][:, oc * 4:oc * 4 + 4, :], in_=pt[:, :].rearrange("p (q m) -> p q m", q=4), func=AF.Copy,
                    scale=(1.0 if ri == 0 else -1.0)),
                    reads=[kpt], writes=[("Cz", ri)])
        for g in range(16):
            for s8 in range(2):
                sc = g * 2 + s8
                kb.op("dve", lambda e, s8=s8, sc=sc: e.tensor_scalar(out=ang[:, s8, :], in0=iota1[:, :], scalar1=PT[:, 1, sc:sc + 1],
                                                                     scalar2=None, op0=ALU.mult),
                      reads=["iota1", "PT"], writes=["ang"])
            self.sin_to("ang", sinT[:, g * 2:(g + 1) * 2, :], ang[:], angf[:], angi[:], angm[:], ("sinT", g), "ang")
            self.sin_to("ang", cosT[:, g * 2:(g + 1) * 2, :], ang[:], angf[:], angi[:], angm[:], ("cosT", g), "ang",
                        shift=math.pi / 2)
        tabs = [("sinT", g) for g in range(16)] + [("cosT", g) for g in range(16)]
        kb.op("dve", lambda e: e.memset(car[0][:], 0.0), writes=[("car", 0, sc_) for sc_ in range(32)])
        kb.op("dve", lambda e: e.memset(car[1][:], 0.0), writes=[("car", 1, sc_) for sc_ in range(32)])

        if NS:
            for ri, nm in enumerate(("state_s5_re", "state_s5_im")):
                p = self.rr("ptr", 2)
                pt, kpt = self.ptr[p], "ptr%d" % p
                for qq in range(8):
                    kb.dma("sp", sst[:NS, :], I[nm][l].rearrange("b g p -> b (g p)")[:, qq * 512:(qq + 1) * 512], writes=["sst"])
                    for s8 in range(4):
                        sc = qq * 4 + s8
                        kb.op("pe", lambda e, sc=sc, s8=s8, pt=pt: e.transpose(out=pt[:, sc * NS:(sc + 1) * NS], in_=sst[:NS, s8 * 128:(s8 + 1) * 128],
                                                                      identity=self.ident[:NS, :NS]),
                              reads=["sst", "ident"], writes=[kpt])
                kb.op("dve", lambda e, pt=pt, ri=ri: e.tensor_copy(out=s0T[ri][:, :, :NS],
                                                                  in_=pt[:, :32 * NS].rearrange("p (s b) -> p s b", s=32)),
                      reads=[kpt], writes=[("s0T", ri)])

        kb.barrier()
        self.pes.close()
        self.pes = outer_pes
        uf = [sbp("uf%d" % i, (128, 256)) for i in range(2)]
        ub = [sbp("ub%d" % i, (128, 256), BF16) for i in range(2)]
        WS = []
        for w_ in range(2):
            WS.append(([sbp("tq%d_%d" % (w_, i), (128, 256)) for i in range(4)],
                       [sbp("bh%d_%d" % (w_, i), (128, 256)) for i in range(2)],
                       [sbp("sh%d_%d" % (w_, i), (128, 256)) for i in range(2)],
                       [sbp("sbf%d_%d" % (w_, i), (128, 256), BF16) for i in range(2)]))
        ysg = sbp("ysg", (128, KC, 256)); ysb = sbp("ysb", (128, KC, 256), BF16)
        yt = [sbp("yt%d" % i, (128, 256)) for i in range(2)]
        zst = [sbp("zst%d" % i, (128, 256)) for i in range(2)]
        aout = [sbp("aout%d" % i, (128, 256), BF16) for i in range(2)]
        subblocks = []
        for bi, (t0b, nb) in enumerate(self.blocks):
            for t0 in range(t0b, t0b + nb, 256):
                subblocks.append((bi, t0, min(256, t0b + nb - t0)))
        for (bi, t0, n) in subblocks:
            is_s = (t0 >= T)
            nsub = n // 256
            for oc in range(8):
                j = self.rr("uf", 2)
                kuf, kub = "uf%d" % j, "ub%d" % j
                row = O_U + oc * 128
                kb.dma("sp", uf[j][:, :n], S["zT"][row:row + 128, t0:t0 + n], reads=[("zT", row, bi)], writes=[kuf])
                kb.op("pool", lambda e, j=j: e.tensor_copy(out=ub[j][:, :n], in_=uf[j][:, :n]), reads=[kuf], writes=[kub])
                py = self.rr("ptr", 2)
                pys, kpys = self.ptr[py], "ptr%d" % py
                def sc_gen(q, W):
                    tq, bh, sh, sbf = W
                    wk = lambda n_: (n_, id(W))
                    sc = oc * 4 + q
                    pa = self.rr("pp", 4); pb = self.rr("pp", 4)
                    A, B = self.pp[pa], self.pp[pb]
                    kA, kB = "pp%d" % pa, "pp%d" % pb
                    kb.op("pe", lambda e, A=A, sc=sc, j=j: e.matmul(A[:, :n], lhsT=Bz[0][:, sc, :], rhs=ub[j][:, :n], start=True, stop=True),
                          reads=[("Bz", 0), kub], writes=[kA])
                    kb.op("pe", lambda e, B=B, sc=sc, j=j: e.matmul(B[:, :n], lhsT=Bz[1][:, sc, :], rhs=ub[j][:, :n], start=True, stop=True),
                          reads=[("Bz", 1), kub], writes=[kB])
                    yield
                    if not is_s:
                        v3 = lambda x: x[:, :n].rearrange("p (m j) -> p m j", j=256)
                        cosv = cosT[:, sc, None, :].to_broadcast([128, nsub, 256])
                        sinv = sinT[:, sc, None, :].to_broadcast([128, nsub, 256])
                        kb.op("dve", lambda e, A=A, cosv=cosv: e.tensor_tensor(out=v3(tq[0]), in0=v3(A), in1=cosv, op=ALU.mult),
                              reads=[kA] + tabs, writes=[wk("tq0")])
                        kb.op("dve", lambda e, B=B, sinv=sinv: e.tensor_tensor(out=v3(tq[1]), in0=v3(B), in1=sinv, op=ALU.mult),
                              reads=[kB] + tabs, writes=[wk("tq1")])
                        kb.op("dve", lambda e, B=B, cosv=cosv: e.tensor_tensor(out=v3(tq[2]), in0=v3(B), in1=cosv, op=ALU.mult),
                              reads=[kB] + tabs, writes=[wk("tq2")])
                        kb.op("dve", lambda e, A=A, sinv=sinv: e.tensor_tensor(out=v3(tq[3]), in0=v3(A), in1=sinv, op=ALU.mult),
                              reads=[kA] + tabs, writes=[wk("tq3")])
                        kb.op("pool", lambda e: e.tensor_tensor(out=bh[0][:, :n], in0=tq[0][:, :n], in1=tq[1][:, :n], op=ALU.add),
                              reads=[wk("tq0"), wk("tq1")], writes=[wk("bh0")])
                        kb.op("pool", lambda e: e.tensor_tensor(out=bh[1][:, :n], in0=tq[2][:, :n], in1=tq[3][:, :n], op=ALU.subtract),
                              reads=[wk("tq2"), wk("tq3")], writes=[wk("bh1")])
                        yield
                        rho = PT[:, 0, sc:sc + 1].to_broadcast([128, 256])
                        c128 = cosT[:, sc, 255:256]
                        s128 = sinT[:, sc, 255:256]
                        for m in range(nsub):
                            sl = slice(m * 256, (m + 1) * 256)
                            for ri in range(2):
                                kb.op("dve", lambda e, ri=ri, sl=sl, sc=sc, rho=rho: e.tensor_tensor_scan(
                                    out=sh[ri][:, sl], data0=rho, data1=bh[ri][:, sl], initial=car[ri][:, sc:sc + 1],
                                    op0=ALU.mult, op1=ALU.add), reads=[wk("bh%d" % ri), ("car", ri, sc), "PT"], writes=[wk("sh%d" % ri)])
                            lr = sh[0][:, m * 256 + 255:m * 256 + 256]
                            li = sh[1][:, m * 256 + 255:m * 256 + 256]
                            kb.op("dve", lambda e, li=li, s128=s128: e.tensor_scalar(out=ctmp[:, 2 * (q % 2):2 * (q % 2) + 1], in0=li, scalar1=s128, scalar2=None, op0=ALU.mult),
                                  reads=[wk("sh1")] + tabs, writes=[wk("ctmp")])
                            kb.op("dve", lambda e, li=li, c128=c128: e.tensor_scalar(out=ctmp[:, 2 * (q % 2) + 1:2 * (q % 2) + 2], in0=li, scalar1=c128, scalar2=None, op0=ALU.mult),
                                  reads=[wk("sh1")] + tabs, writes=[wk("ctmp")])
                            kb.op("dve", lambda e, lr=lr, c128=c128, sc=sc: e.scalar_tensor_tensor(
                                out=car[0][:, sc:sc + 1], in0=lr, scalar=c128, in1=ctmp[:, 2 * (q % 2):2 * (q % 2) + 1], op0=ALU.mult, op1=ALU.subtract),
                                reads=[wk("sh0"), wk("ctmp")] + tabs, writes=[("car", 0, sc)])
                            kb.op("dve", lambda e, lr=lr, s128=s128, sc=sc: e.scalar_tensor_tensor(
                                out=car[1][:, sc:sc + 1], in0=lr, scalar=s128, in1=ctmp[:, 2 * (q % 2) + 1:2 * (q % 2) + 2], op0=ALU.mult, op1=ALU.add),
                                reads=[wk("sh0"), wk("ctmp")] + tabs, writes=[("car", 1, sc)])
                        yield
                        kb.op("pool", lambda e, cosv=cosv: e.tensor_tensor(out=v3(tq[0]), in0=v3(sh[0]), in1=cosv, op=ALU.mult),
                              reads=[wk("sh0")] + tabs, writes=[wk("tq0")])
                        kb.op("pool", lambda e, sinv=sinv: e.tensor_tensor(out=v3(tq[1]), in0=v3(sh[1]), in1=sinv, op=ALU.mult),
                              reads=[wk("sh1")] + tabs, writes=[wk("tq1")])
                        kb.op("dve", lambda e, sinv=sinv: e.tensor_tensor(out=v3(tq[2]), in0=v3(sh[0]), in1=sinv, op=ALU.mult),
                              reads=[wk("sh0")] + tabs, writes=[wk("tq2")])
                        kb.op("dve", lambda e, cosv=cosv: e.tensor_tensor(out=v3(tq[3]), in0=v3(sh[1]), in1=cosv, op=ALU.mult),
                              reads=[wk("sh1")] + tabs, writes=[wk("tq3")])
                        kb.op("pool", lambda e: e.tensor_tensor(out=sbf[0][:, :n], in0=tq[0][:, :n], in1=tq[1][:, :n], op=ALU.subtract),
                              reads=[wk("tq0"), wk("tq1")], writes=[wk("sbf0")])
                        kb.op("dve", lambda e: e.tensor_tensor(out=sbf[1][:, :n], in0=tq[2][:, :n], in1=tq[3][:, :n], op=ALU.add),
                              reads=[wk("tq2"), wk("tq3")], writes=[wk("sbf1")])
                    else:
                        lbre = PT[:, 2, sc:sc + 1]
                        lbim = PT[:, 3, sc:sc + 1]
                        s0r, s0i = s0T[0][:, sc, :n], s0T[1][:, sc, :n]
                        kb.op("dve", lambda e, s0i=s0i, lbim=lbim: e.tensor_scalar(out=tq[0][:, :n], in0=s0i, scalar1=lbim, scalar2=None, op0=ALU.mult),
                              reads=[("s0T", 1), "PT"], writes=[wk("tq0")])
                        kb.op("dve", lambda e, s0r=s0r, lbre=lbre: e.scalar_tensor_tensor(out=tq[0][:, :n], in0=s0r, scalar=lbre, in1=tq[0][:, :n],
                                                                               op0=ALU.mult, op1=ALU.subtract),
                              reads=[("s0T", 0), "PT", wk("tq0")], writes=[wk("tq0")])
                        kb.op("dve", lambda e, A=A, sc=sc: e.tensor_tensor(out=snw[0][:, sc, :n], in0=A[:, :n], in1=tq[0][:, :n], op=ALU.add),
                              reads=[kA, wk("tq0")], writes=[("snw", 0)])
                        kb.op("dve", lambda e, s0r=s0r, lbim=lbim: e.tensor_scalar(out=tq[1][:, :n], in0=s0r, scalar1=lbim, scalar2=None, op0=ALU.mult),
                              reads=[("s0T", 0), "PT"], writes=[wk("tq1")])
                        kb.op("dve", lambda e, s0i=s0i, lbre=lbre: e.scalar_tensor_tensor(out=tq[1][:, :n], in0=s0i, scalar=lbre, in1=tq[1][:, :n],
                                                                               op0=ALU.mult, op1=ALU.add),
                              reads=[("s0T", 1), "PT", wk("tq1")], writes=[wk("tq1")])
                        kb.op("dve", lambda e, B=B, sc=sc: e.tensor_tensor(out=snw[1][:, sc, :n], in0=B[:, :n], in1=tq[1][:, :n], op=ALU.add),
                              reads=[kB, wk("tq1")], writes=[("snw", 1)])
                        for ri in range(2):
                            kb.op("pool", lambda e, ri=ri, sc=sc: e.tensor_copy(out=sbf[ri][:, :n], in_=snw[ri][:, sc, :n]),
                                  reads=[("snw", ri)], writes=[wk("sbf%d" % ri)])
                    yield
                    for ri in range(2):
                        kb.op("pe", lambda e, ri=ri, sc=sc, q=q, pys=pys: e.matmul(pys[:, :n], lhsT=Cz[ri][:, sc, :], rhs=sbf[ri][:, :n],
                                                                                start=(q == 0 and ri == 0), stop=(q == 3 and ri == 1)),
                              reads=[("Cz", ri), wk("sbf%d" % ri)], writes=[kpys])
                for qp in range(2):
                    alive = [sc_gen(2 * qp + w_, WS[w_]) for w_ in range(2)]
                    while alive:
                        for g_ in list(alive):
                            try:
                                next(g_)
                            except StopIteration:
                                alive.remove(g_)
                y0, y1 = yt[0], yt[1]
                kb.op("dve", lambda e, j=j, oc=oc, pys=pys: e.scalar_tensor_tensor(out=y0[:, :n], in0=uf[j][:, :n], scalar=dcol[:, oc:oc + 1],
                                                                             in1=pys[:, :n], op0=ALU.mult, op1=ALU.add),
                      reads=[kuf, "dcol", kpys], writes=["yt0"])
                kb.op("act", lambda e: e.activation(out=y1[:, :n], in_=y0[:, :n], func=AF.Square), reads=["yt0"], writes=["yt1"])
                kb.op("dve", lambda e: e.tensor_scalar(out=y1[:, :n], in0=y1[:, :n], scalar1=0.044715, scalar2=1.0, op0=ALU.mult, op1=ALU.add),
                      reads=["yt1"], writes=["yt1"])
                kb.op("dve", lambda e: e.tensor_tensor(out=y1[:, :n], in0=y1[:, :n], in1=y0[:, :n], op=ALU.mult), reads=["yt1", "yt0"], writes=["yt1"])
                kb.op("act", lambda e: e.activation(out=y1[:, :n], in_=y1[:, :n], func=AF.Sigmoid, scale=2.0 * math.sqrt(2.0 / math.pi)),
                      reads=["yt1"], writes=["yt1"])
                kb.op("dve", lambda e, oc=oc: e.tensor_tensor(out=ysg[:, oc, :n], in0=y1[:, :n], in1=y0[:, :n], op=ALU.mult),
                      reads=["yt1", "yt0"], writes=[("ysg", oc)])
                kb.op("pool", lambda e, oc=oc: e.tensor_copy(out=ysb[:, oc, :n], in_=ysg[:, oc, :n]), reads=[("ysg", oc)], writes=[("ysb", oc)])
            for ec in range(8):
                p = self.rr("pp", 4)
                pp, kpp = self.pp[p], "pp%d" % p
                for kc in range(KC):
                    kb.op("pe", lambda e, kc=kc, ec=ec, pp=pp: e.matmul(pp[:, :n], lhsT=wg[:, kc, ec * 128:(ec + 1) * 128], rhs=ysb[:, kc, :n],
                                                                      start=(kc == 0), stop=(kc == KC - 1)),
                          reads=["wg"] + [("ysb", k) for k in range(KC)], writes=[kpp])
                zj = self.rr("zst", 2)
                kz = "zst%d" % zj
                row = O_ZS + ec * 128
                kb.dma("sp", zst[zj][:, :n], S["zT"][row:row + 128, t0:t0 + n], reads=[("zT", row, bi)], writes=[kz])
                kb.op("act", lambda e, zj=zj: e.activation(out=zst[zj][:, :n], in_=zst[zj][:, :n], func=AF.Silu), reads=[kz], writes=[kz])
                kb.op("act", lambda e, pp=pp, ec=ec: e.activation(out=yt[0][:, :n], in_=pp[:, :n], func=AF.Sigmoid, bias=gbcol[:, ec:ec + 1]),
                      reads=[kpp, "gbcol"], writes=["yt0"])
                kb.op("dve", lambda e, ec=ec: e.tensor_tensor(out=yt[0][:, :n], in0=yt[0][:, :n], in1=ysg[:, ec, :n], op=ALU.mult),
                      reads=["yt0", ("ysg", ec)], writes=["yt0"])
                aj = self.rr("aout", 2)
                kb.op("dve", lambda e, aj=aj, zj=zj: e.tensor_tensor(out=aout[aj][:, :n], in0=yt[0][:, :n], in1=zst[zj][:, :n], op=ALU.mult),
                      reads=["yt0", kz], writes=["aout%d" % aj])
                kb.dma("pool", S["act_s"][ec * 128:(ec + 1) * 128, t0:t0 + n], aout[aj][:, :n],
                       reads=["aout%d" % aj], writes=[("act", "s", bi)])
        for ri, nm in enumerate(("s5_re", "s5_im")):
            p = self.rr("ptr", 2)
            pt, kpt = self.ptr[p], "ptr%d" % p
            kb.op("pe", lambda e, pt=pt, ri=ri: e.transpose(out=pt[:32, 0:128], in_=car[ri][:, :], identity=self.ident[:, :]),
                  reads=[("car", ri, sc_) for sc_ in range(32)] + ["ident"], writes=[kpt])
            kb.op("dve", lambda e, pt=pt: e.tensor_copy(out=sst[:32, 0:128], in_=pt[:32, 0:128]), reads=[kpt], writes=["sst"])
            kb.dma("pool", O[nm + "_prompt"][l].rearrange("(s q) -> s q", q=128), sst[:32, 0:128], reads=["sst"], writes=[],
                   is_output=True)
            if NS:
                for g in range(8):
                    p = self.rr("ptr", 2)
                    pt, kpt = self.ptr[p], "ptr%d" % p
                    for s4 in range(4):
                        sc = g * 4 + s4
                        kb.op("pe", lambda e, pt=pt, ri=ri, sc=sc, s4=s4: e.transpose(out=pt[:NS, s4 * 128:(s4 + 1) * 128], in_=snw[ri][:, sc, :NS],
                                                                                   identity=self.ident[:, :]),
                              reads=[("snw", ri), "ident"], writes=[kpt])
                    kb.op("dve", lambda e, pt=pt, g=g: e.tensor_copy(out=sst[:NS, 0:512], in_=pt[:NS, :]), reads=[kpt], writes=["sst"])
                    kb.dma("pool", O[nm + "_sample"][l].rearrange("b g p -> b (g p)")[:, g * 512:(g + 1) * 512], sst[:NS, 0:512],
                           reads=["sst"], writes=[], is_output=True)
        self.end_phase()

    def load_sq(self, name, dram):
        kb = self.kb
        dst = self.wsq[name]
        for h in range(2):
            i = self.rr("w", 2)
            ws, kws = self.wst[i], "wst%d" % i
            kb.dma("sp", ws[:, :, :512], dram[:, h * 512:(h + 1) * 512].rearrange("(k p) c -> p k c", p=128),
                   writes=[kws])
            kb.op("dve", lambda e, ws=ws, h=h: e.tensor_copy(out=dst[:, :, h * 512:(h + 1) * 512], in_=ws[:, :, :512]),
                  reads=[kws], writes=[("wsq", name)])

    def phase3(self, l):
        kb = self.kb
        I, S = self.I, self.S
        last = (l == self.DEPTH - 1)
        self.begin_phase()
        self.wst = [self.sbp("wst%d" % i, (128, KC, 512)) for i in range(2)]
        self.wsq = {n: self.sbp("wsq_" + n, (128, KC, D), BF16) for n in ("bm", "br", "bs", "out")}
        self.actb = [self.sbp("actb%d" % i, (128, KC, 512), BF16) for i in range(2)]
        self.gmt = [self.sbp("gmt%d" % i, (128, 512)) for i in range(2)]
        self.mrg = self.sbp("mrg", (128, KC, 512), BF16)
        self.macc = self.sbp("macc", (128, KC, 512))
        self.ln_alloc()
        for nme in ("bm", "br", "bs", "out"):
            self.load_sq(nme, I["w_" + nme][l])
        self.load_ln_params(I["ln_g"][l], I["ln_b"][l])
        for bi, (t0, n) in enumerate(self.blocks):
            for ec in range(KC):
                for b, bn in enumerate("mrs"):
                    pass
            acts = {}
            for b, bn in enumerate("mrs"):
                if not self.have_branch(bn):
                    continue
                j = self.rr("actb", 2)
                at, kat = self.actb[j], "actb%d" % j
                kb.dma("sp", at[:, :, :n], S["act_" + bn][:, t0:t0 + n].rearrange("(k p) t -> p k t", p=128),
                       reads=[("act", bn, bi)], writes=[kat])
                for ec in range(KC):
                    p = self.rr("pp", 4)
                    pp, kpp = self.pp[p], "pp%d" % p
                    for kc in range(KC):
                        kb.op("pe", lambda e, kc=kc, pp=pp, ec=ec, at=at, bn=bn: e.matmul(
                            pp[:, :n], lhsT=self.wsq["b" + bn][:, kc, ec * 128:(ec + 1) * 128], rhs=at[:, kc, :n],
                            start=(kc == 0), stop=(kc == KC - 1)),
                            reads=[kat, ("wsq", "b" + bn)], writes=[kpp])
                    g = self.rr("gmt", 2)
                    gt, kgt = self.gmt[g], "gmt%d" % g
                    row = O_GM + b * D + ec * 128
                    kb.dma("sp", gt[:, :n], S["zT"][row:row + 128, t0:t0 + n],
                           reads=[("zT", row, bi)], writes=[kgt])
                    kb.op("act", lambda e, gt=gt: e.activation(out=gt[:, :n], in_=gt[:, :n], func=AF.Sigmoid),
                          reads=[kgt], writes=[kgt])
                    first = (bn == self.first_branch())
                    lastb = (bn == self.last_branch())
                    kmf = ("mrgf", ec)
                    acc = self.macc[:, ec, :n]
                    if first:
                        kb.op("dve", lambda e, acc=acc, gt=gt, pp=pp: e.tensor_tensor(out=acc, in0=pp[:, :n], in1=gt[:, :n],
                                                                                 op=ALU.mult),
                              reads=[kpp, kgt], writes=[("macc", ec)])
                    else:
                        kb.op("dve", lambda e, gt=gt, pp=pp: e.tensor_tensor(out=gt[:, :n], in0=pp[:, :n], in1=gt[:, :n],
                                                                        op=ALU.mult),
                              reads=[kpp, kgt], writes=[kgt])
                        kb.op("dve", lambda e, acc=acc, gt=gt: e.tensor_tensor(out=acc, in0=acc, in1=gt[:, :n], op=ALU.add),
                              reads=[kgt, ("macc", ec)], writes=[("macc", ec)])
                    if lastb:
                        kb.op("dve", lambda e, acc=acc, ec=ec: e.tensor_copy(out=self.mrg[:, ec, :n], in_=acc),
                              reads=[("macc", ec)], writes=[("mrg", ec)])
            for tt in range(0, n, 128):
                nr = min(128, n - tt)
                r0 = t0 + tt
                ti = r0 // 128
                j = self.rr("xt", 2)
                xt, kxt = self.xt[j], "xt%d" % j
                kb.dma("sp", xt[:nr, :], S["xs"][r0:r0 + nr, :], reads=[("xs", ti)], writes=[kxt])
                if self.first_branch() is not None:
                    for h in range(2):
                        p = self.rr("pp", 4)
                        pp, kpp = self.pp[p], "pp%d" % p
                        for kc in range(KC):
                            kb.op("pe", lambda e, kc=kc, pp=pp, h=h, tt=tt, nr=nr: e.matmul(
                                pp[:nr, :], lhsT=self.mrg[:, kc, tt:tt + nr], rhs=self.wsq["out"][:, kc, h * 512:(h + 1) * 512],
                                start=(kc == 0), stop=(kc == KC - 1)),
                                reads=[("mrg", k) for k in range(KC)] + [("wsq", "out")], writes=[kpp])
                        kb.op("dve", lambda e, xt=xt, pp=pp, h=h, nr=nr: e.scalar_tensor_tensor(
                            out=xt[:nr, h * 512:(h + 1) * 512], in0=xt[:nr, h * 512:(h + 1) * 512], scalar=ALPHA,
                            in1=pp[:nr, :], op0=ALU.mult, op1=ALU.add), reads=[kxt, kpp], writes=[kxt])
                else:
                    kb.op("dve", lambda e, xt=xt, nr=nr: e.tensor_scalar(out=xt[:nr, :], in0=xt[:nr, :], scalar1=ALPHA,
                                                                         scalar2=None, op0=ALU.mult),
                          reads=[kxt], writes=[kxt])
                fo = None
                if last:
                    fo = self.O["y_prompt"][r0:r0 + nr, :] if r0 < self.T else self.O["y_sample"][:, :]
                self.ln_tile(xt, kxt, r0, nr, ti, S["xs"][r0:r0 + nr, :], final_out=fo)
        self.end_phase()

    branches = ""
    NPI = 4
    CORE_BF16 = True

    def have_branch(self, bn):
        return bn in self.branches

    def first_branch(self):
        return self.branches[0] if self.branches else None

    def last_branch(self):
        return self.branches[-1] if self.branches else None


def make_in_map(inputs, c, NS, T, consts):
    m = {}
    m["x_prompt"] = np.ascontiguousarray(inputs["x_prompt"][c, :T])
    m["x_sample"] = np.ascontiguousarray(inputs["x_sample"][c * NS:(c + 1) * NS, 0])
    for n in ("ln_in_g", "ln_in_b", "w_in", "w_out", "ln_g", "ln_b", "w_bm", "w_br", "w_bs", "s_lam_re", "s_lam_im",
              "s_log_dt", "s_b_re", "s_b_im", "s_c_re", "s_c_im", "s_d", "s_glu_w", "s_glu_b",
              "m_conv_w", "m_conv_b", "m_wq", "m_wk", "m_wv", "m_ig_b", "m_fg_b", "m_norm_g", "m_skip",
              "r_mu", "r_w0", "r_w2", "r_a0", "r_a2", "r_k_k", "r_k_a", "r_r_k", "r_ln_g", "r_ln_b"):
        m[n] = np.ascontiguousarray(inputs[n])
    for n in ("state_s5_re", "state_s5_im", "state_mlstm_conv", "state_mlstm_c", "state_mlstm_n", "state_mlstm_m",
              "state_rwkv_wkv", "state_rwkv_shift"):
        m[n] = np.ascontiguousarray(inputs[n][:, c * NS:(c + 1) * NS])
    m.update(consts)
    return m


def kernel(**inputs):
    inputs = {k: np.asarray(v) for k, v in inputs.items()}
    T, NS, L = 2048, 16, 4
    Prog.branches = BRANCHES
    prog = Prog(T, NS, L)
    nc = prog.build()
    consts = host_consts()
    in_maps = [make_in_map(inputs, c, NS, T, consts) for c in range(8)]
    res = run_bass_kernel_spmd(nc, in_maps, core_ids=list(range(8)))
    rs = res.results
    f = np.float32

    def pstack(name, shape):
        if name in rs[0]:
            return np.stack([np.asarray(rs[c][name]).reshape((L,) + shape) for c in range(8)], 1).astype(f)
        return np.zeros((L, 8) + shape, f)

    def sstack(name, shape):
        if name in rs[0]:
            return np.concatenate([np.asarray(rs[c][name]).reshape((L, NS) + shape) for c in range(8)], 1).astype(f)
        return np.zeros((L, 8 * NS) + shape, f)

    y_prompt = np.stack([rs[c]["y_prompt"] for c in range(8)], 0).astype(f)
    y_sample = np.concatenate([rs[c]["y_sample"] for c in range(8)], 0)[:, None, :].astype(f)
    return (y_prompt, y_sample,
            pstack("c_prompt", (4, 256, 256)), sstack("c_sample", (4, 256, 256)),
            pstack("n_prompt", (4, 256)), sstack("n_sample", (4, 256)),
            pstack("m_prompt", (4,)), sstack("m_sample", (4,)),
            pstack("conv_prompt", (3, D)), sstack("conv_sample", (3, D)),
            pstack("wkv_prompt", (16, 64, 64)), sstack("wkv_sample", (16, 64, 64)),
            pstack("shift_prompt", (R_SHIFT_W,)), sstack("shift_sample", (R_SHIFT_W,)),
            pstack("s5_re_prompt", (64, 64)), sstack("s5_re_sample", (64, 64)),
            pstack("s5_im_prompt", (64, 64)), sstack("s5_im_sample", (64, 64)))
```

```python
import math
from contextlib import ExitStack
import numpy as np
import concourse.bass as bass
import concourse.mybir as mybir
from concourse.bass_utils import run_bass_kernel_spmd

F32 = mybir.dt.float32
BF16 = mybir.dt.bfloat16
I32 = mybir.dt.int32
ALU = mybir.AluOpType
AF = mybir.ActivationFunctionType
AX = mybir.AxisListType

D = 1024
KC = 8
N_IN = 12424
M_HEADS = 4
M_HD = 256
R_HEADS = 16
R_HD = 64
R_SHIFT_W = 3200
S_GROUPS = 64
S_STATE = 64
DEPTH_FULL = 4
ALPHA = (2.0 * DEPTH_FULL) ** 0.25
LN_EPS = 1e-5
RWKV_GN_EPS = 64e-5
NEG = -1e30

O_XM, O_IG, O_FG, O_OG, O_ZM = 0, 1024, 1028, 1032, 2056
O_RC, O_ZR, O_U, O_ZS, O_GM = 3080, 6280, 7304, 8328, 9352


class KB:
    def __init__(self, nc, es):
        self.nc = nc
        self.eng = dict(pe=nc.tensor, act=nc.scalar, dve=nc.vector, pool=nc.gpsimd, sp=nc.sync)
        self.stream = {e: [] for e in self.eng}
        self.csem = {e: es.enter_context(nc.semaphore("c_" + e)) for e in ("pe", "act", "dve", "pool")}
        self.ccnt = {e: 0 for e in self.csem}
        self.ring = {}
        for q, n in (("sp", 24), ("pool", 12), ("act", 6)):
            self.ring[q] = [[es.enter_context(nc.semaphore("d_%s%d" % (q, i))), 0] for i in range(n)]
        self.rpos = {q: 0 for q in self.ring}
        self.known = {e: {} for e in self.eng}
        self.lastw = {}
        self.readers = {}
        self.out_tokens = []
        self.n_ops = 0

    def _need(self, e, tok, same_ok=False):
        if tok is None:
            return
        sem, val, owner = tok
        if owner == e and (e == "pe" or same_ok):
            return
        k = id(sem)
        if self.known[e].get(k, 0) >= val:
            return
        self.known[e][k] = val
        self.eng[e].wait_ge(sem, val)

    ns = None
    local = frozenset()

    def _k(self, b):
        if self.ns is None:
            return b
        base = b[0] if isinstance(b, tuple) else b
        return (self.ns, b) if base in self.local else b

    def _deps(self, e, reads, writes):
        reads = [self._k(b) for b in reads]
        writes = [self._k(b) for b in writes]
        for b in reads:
            self._need(e, self.lastw.get(b))
        for b in writes:
            self._need(e, self.lastw.get(b))
            for tok in self.readers.get(b, ()):
                self._need(e, tok)

    def _commit(self, tok, reads, writes):
        reads = [self._k(b) for b in reads]
        writes = [self._k(b) for b in writes]
        for b in writes:
            self.lastw[b] = tok
            self.readers[b] = []
        for b in reads:
            self.readers.setdefault(b, []).append(tok)

    def op(self, e, fn, reads=(), writes=()):
        self._deps(e, reads, writes)
        self.ccnt[e] += 1
        tok = (self.csem[e], self.ccnt[e], e)
        fn(self.eng[e]).then_inc(self.csem[e], 1)
        self._commit(tok, reads, writes)
        self.n_ops += 1

    def dma(self, q, out, in_, reads=(), writes=(), is_output=False, **kw):
        self._deps(q, reads, writes)
        ring = self.ring[q]
        slot = ring[self.rpos[q] % len(ring)]
        self.rpos[q] += 1
        sem = slot[0]
        if slot[1] > 0:
            self._need(q, (sem, slot[1], None))
        slot[1] += 16
        tok = (sem, slot[1], None)
        self.eng[q].dma_start(out=out, in_=in_, **kw).then_inc(sem, 16)
        self._commit(tok, reads, writes)
        if is_output:
            self.out_tokens.append(tok)
        self.n_ops += 1

    def finish(self):
        for q in self.ring:
            for sem, val in self.ring[q]:
                if val > 0:
                    self._need("sp", (sem, val, None))
        for e in self.csem:
            if self.ccnt[e] > 0:
                self._need("sp", (self.csem[e], self.ccnt[e], e))

    def barrier(self):
        for e in self.eng:
            for q in self.ring:
                for sem, val in self.ring[q]:
                    if val > 0:
                        self._need(e, (sem, val, None))
            for e2 in self.csem:
                if self.ccnt[e2] > 0 and e2 != e:
                    self._need(e, (self.csem[e2], self.ccnt[e2], e2))

    def emit(self):
        pass


BRANCHES = "mrs"


def host_consts():
    c = {}
    c["ident"] = np.eye(128, dtype=np.float32)
    m3 = np.zeros((128, 4, 2), np.float32)
    m4 = np.zeros((128, 4, 8), np.float32)
    for p in range(128):
        g8 = p // 16
        gl = p // 64
        for q in range(4):
            for g in range(2):
                if g8 == 2 * q + g:
                    m3[p, q, g] = 1.0
            for gg in range(8):
                if gg == 2 * q + gl:
                    m4[p, q, gg] = 1.0
    c["mask3"], c["mask4"] = m3, m4
    selh = np.zeros((4, 4, 128), np.float32)
    for h in range(4):
        selh[h, h, :] = 1.0
    c["selh"] = selh
    c["ones4"] = np.ones((4, 128), np.float32)
    cn = np.zeros((128, 128), np.float32)
    for s_ in range(128):
        cn[s_, :s_] = 1.0e30
    c["causneg"] = cn
    rm = np.zeros((128, 384), np.float32)
    for p in range(128):
        for q in range(128):
            if p // 64 == q // 64:
                s_, t_ = p % 64, q % 64
                rm[p, q] = 1.0 if s_ < t_ else 0.0
                rm[p, 128 + q] = 1.0 if s_ <= t_ else 0.0
                rm[p, 256 + q] = 1.0 if s_ > t_ else 0.0
    c["rmasks"] = rm
    c["iota1"] = np.tile(np.arange(1, 257, dtype=np.float32)[None, :], (128, 1))
    return c


class Prog:
    def __init__(self, T, NS, DEPTH):
        self.T, self.NS, self.DEPTH = T, NS, DEPTH
        self.NT = T + NS
        assert T % 128 == 0
        self.ntile = T // 128
        self.tiles = [(i * 128, 128) for i in range(self.ntile)] + [(T, NS)]
        self.blocks = []
        t = 0
        while t < T:
            n = min(512, T - t)
            self.blocks.append((t, n))
            t += n
        self.blocks.append((T, NS))

    def build(self):
        nc = bass.Bass("TRN2", target_bir_lowering=False)
        self.nc = nc
        T, NS, L, NT = self.T, self.NS, self.DEPTH, self.NT
        dt = nc.dram_tensor

        def inp(name, shape, dtype=F32):
            return dt(name, list(shape), dtype, kind="ExternalInput").ap()

        def outp(name, shape):
            return dt(name, list(shape), F32, kind="ExternalOutput").ap()

        def scr(name, shape, dtype=F32):
            return dt(name, list(shape), dtype, kind="Internal").ap()

        I = {}
        I["x_prompt"] = inp("x_prompt", (T, D))
        I["x_sample"] = inp("x_sample", (NS, D))
        I["ident"] = inp("ident", (128, 128))
        for n, s in (("ln_in_g", (D,)), ("ln_in_b", (D,)), ("w_in", (L, D, N_IN)), ("w_out", (L, D, D)),
                     ("ln_g", (L, D)), ("ln_b", (L, D)), ("w_bm", (L, D, D)), ("w_br", (L, D, D)),
                     ("w_bs", (L, D, D)), ("s_lam_re", (L, 64, 64)), ("s_lam_im", (L, 64, 64)), ("s_log_dt", (L, 64)),
                     ("s_b_re", (L, 64, 64, 16)), ("s_b_im", (L, 64, 64, 16)), ("s_c_re", (L, 64, 16, 64)),
                     ("s_c_im", (L, 64, 16, 64)), ("s_d", (L, D)), ("s_glu_w", (L, D, D)), ("s_glu_b", (L, D)),
                     ("state_s5_re", (L, NS, 64, 64)), ("state_s5_im", (L, NS, 64, 64)),
                     ("state_mlstm_conv", (L, NS, 3, D)), ("state_mlstm_c", (L, NS, 4, 256, 256)),
                     ("state_mlstm_n", (L, NS, 4, 256)), ("state_mlstm_m", (L, NS, 4)),
                     ("m_conv_w", (L, 4, D)), ("m_conv_b", (L, D)), ("m_wq", (L, 4, 256, 256)), ("m_wk", (L, 4, 256, 256)),
                     ("m_wv", (L, 4, 256, 256)), ("m_ig_b", (L, 4)), ("m_fg_b", (L, 4)), ("m_norm_g", (L, D)), ("m_skip", (L, D)),
                     ("state_rwkv_wkv", (L, NS, 16, 64, 64)), ("state_rwkv_shift", (L, NS, R_SHIFT_W)),
                     ("r_mu", (L, R_SHIFT_W)), ("r_w0", (L, D)), ("r_w2", (L, 64, D)), ("r_a0", (L, D)), ("r_a2", (L, 64, D)),
                     ("r_k_k", (L, D)), ("r_k_a", (L, D)), ("r_r_k", (L, 16, 64)), ("r_ln_g", (L, D)), ("r_ln_b", (L, D)),
                     ("rmasks", (128, 384)),
                     ("selh", (4, 4, 128)), ("causneg", (128, 128)), ("ones4", (4, 128)),
                     ("mask3", (128, 4, 2)), ("mask4", (128, 4, 8)), ("iota1", (128, 256))):
            I[n] = inp(n, s)
        self.I = I
        O = {}
        O["y_prompt"] = outp("y_prompt", (T, D))
        O["y_sample"] = outp("y_sample", (NS, D))
        O["c_prompt"] = outp("c_prompt", (L, 4, 256, 256))
        O["c_sample"] = outp("c_sample", (L, NS, 4, 256, 256))
        O["n_prompt"] = outp("n_prompt", (L, 4, 256))
        O["n_sample"] = outp("n_sample", (L, NS, 4, 256))
        O["m_prompt"] = outp("m_prompt", (L, 4))
        O["m_sample"] = outp("m_sample", (L, NS, 4))
        O["conv_prompt"] = outp("conv_prompt", (L, 3, D))
        O["conv_sample"] = outp("conv_sample", (L, NS, 3, D))
        O["wkv_prompt"] = outp("wkv_prompt", (L, 16, 64, 64))
        O["wkv_sample"] = outp("wkv_sample", (L, NS, 16, 64, 64))
        O["shift_prompt"] = outp("shift_prompt", (L, R_SHIFT_W))
        O["shift_sample"] = outp("shift_sample", (L, NS, R_SHIFT_W))
        for nm in ("s5_re", "s5_im"):
            O[nm + "_prompt"] = outp(nm + "_prompt", (L, 4096))
            O[nm + "_sample"] = outp(nm + "_sample", (L, NS, 64, 64))
        self.O = O
        S = {}
        S["xs"] = scr("xs", (NT, D))
        S["z"] = scr("z", (NT, N_IN))
        S["zT"] = scr("zT", (N_IN, NT))
        for b in "mrs":
            S["act_" + b] = scr("act_" + b, (D, NT), BF16)
        S["vs"] = scr("vs", (NT, D))
        S["ys"] = scr("ys", (NT, D))
        self.S = S

        with ExitStack() as es:
            self.es = es
            kb = KB(nc, es)
            self.kb = kb
            self.alloc()
            self.phase0()
            for l in range(L):
                self.phase1(l)
                self.easy_states(l)
                if self.have_branch("m"):
                    self.mlstm_phase(l)
                if self.have_branch("r"):
                    self.rwkv_phase(l)
                if self.have_branch("s"):
                    self.s5_phase(l)
                self.phase3(l)
            kb.finish()
            kb.emit()
        return nc

    def sb(self, name, shape, dtype=F32):
        return self.es.enter_context(self.nc.sbuf_tensor("sb_" + name, list(shape), dtype))

    def sbp(self, name, shape, dtype=F32):
        self.uid = getattr(self, "uid", 0) + 1
        return self.pes.enter_context(self.nc.sbuf_tensor("sp%d_%s" % (self.uid, name), list(shape), dtype))

    def begin_phase(self):
        self.pes = ExitStack()

    def end_phase(self):
        self.kb.barrier()
        self.pes.close()

    def ps(self, name, shape, dtype=F32):
        return self.es.enter_context(self.nc.psum_tensor("ps_" + name, list(shape), dtype))

    def alloc(self):
        NT = self.NT
        self.ident = self.sb("ident", (128, 128))
        self.xT = self.sb("xT", (128, KC, NT), BF16)
        self.pp = [self.ps("pp%d" % i, (128, 512)) for i in range(4)]
        self.ptr = [self.ps("ptr%d" % i, (128, 512)) for i in range(2)]
        self.pex = [self.ps("pex%d" % i, (128, 512)) for i in range(2)]
        self.cnt = {}
        kb = self.kb
        kb.dma("sp", self.ident[:], self.I["ident"], writes=["ident"])

    def rr(self, key, n):
        v = self.cnt.get(key, 0)
        self.cnt[key] = v + 1
        return v % n

    def ln_alloc(self):
        self.gbc = self.sbp("gbc", (128, D))
        self.bbc = self.sbp("bbc", (128, D))
        self.xt = [self.sbp("xt%d" % i, (128, D)) for i in range(2)]
        self.xc = [self.sbp("xc%d" % i, (128, D)) for i in range(2)]
        self.st = [self.sbp("st%d" % i, (128, 8)) for i in range(2)]

    def ln_tile(self, src, srckey, row0, nr, ti, xs_out, final_out=None):
        kb = self.kb
        i = self.rr("ln", 2)
        xc, st = self.xc[i], self.st[i]
        kxc, kst = "xc%d" % i, "st%d" % i
        kb.op("dve", lambda e: e.tensor_reduce(out=st[:nr, 0:1], in_=src[:nr, :], axis=AX.X, op=ALU.add),
              reads=[srckey], writes=[kst])
        kb.op("dve", lambda e: e.tensor_scalar(out=st[:nr, 1:2], in0=st[:nr, 0:1], scalar1=-1.0 / D, scalar2=None,
                                               op0=ALU.mult), reads=[kst], writes=[kst])
        kb.op("dve", lambda e: e.tensor_scalar(out=xc[:nr, :], in0=src[:nr, :], scalar1=st[:nr, 1:2], scalar2=None,
                                               op0=ALU.add), reads=[srckey, kst], writes=[kxc])
        j = self.rr("tmpsq", 2)
        sq = self.xt[j]
        ksq = "xt%d" % j
        kb.op("act", lambda e: e.activation(out=sq[:nr, :], in_=xc[:nr, :], func=AF.Square, accum_out=st[:nr, 2:3]),
              reads=[kxc], writes=[ksq, kst])
        kb.op("act", lambda e: e.activation(out=st[:nr, 3:4], in_=st[:nr, 2:3], func=AF.Sqrt, scale=1.0 / D,
                                            bias=self.epsc[:nr, 0:1]), reads=[kst, "epsc"], writes=[kst])
        kb.op("dve", lambda e: e.reciprocal(out=st[:nr, 4:5], in_=st[:nr, 3:4]), reads=[kst], writes=[kst])
        kb.op("dve", lambda e: e.scalar_tensor_tensor(out=xc[:nr, :], in0=xc[:nr, :], scalar=st[:nr, 4:5],
                                                      in1=self.gbc[:nr, :], op0=ALU.mult, op1=ALU.mult),
              reads=[kxc, kst, "gbc"], writes=[kxc])
        kb.op("dve", lambda e: e.tensor_tensor(out=xc[:nr, :], in0=xc[:nr, :], in1=self.bbc[:nr, :], op=ALU.add),
              reads=[kxc, "bbc"], writes=[kxc])
        kb.dma("pool", xs_out, xc[:nr, :], reads=[kxc], writes=[("xs", ti)])
        if final_out is not None:
            kb.dma("pool", final_out, xc[:nr, :], reads=[kxc], writes=[], is_output=True)
        for half in range(2):
            p = self.rr("ptr", 2)
            pt, kpt = self.ptr[p], "ptr%d" % p
            for k4 in range(4):
                kc = half * 4 + k4
                kb.op("pe", lambda e, kc=kc, k4=k4, pt=pt: e.transpose(out=pt[:, k4 * 128:k4 * 128 + nr],
                                                                      in_=xc[:nr, kc * 128:(kc + 1) * 128],
                                                                      identity=self.ident[:nr, :nr]),
                      reads=[kxc, "ident"], writes=[kpt])
            dst = self.xT[:, half * 4:half * 4 + 4, row0:row0 + nr]
            srcp = pt[:, :].rearrange("p (k t) -> p k t", k=4)[:, :, :nr]
            eng = "act" if half == 0 else "dve"
            if eng == "act":
                kb.op("act", lambda e, dst=dst, srcp=srcp: e.activation(out=dst, in_=srcp, func=AF.Copy),
                      reads=[kpt], writes=[("xT", ti)])
            else:
                kb.op("dve", lambda e, dst=dst, srcp=srcp: e.tensor_copy(out=dst, in_=srcp),
                      reads=[kpt], writes=[("xT", ti)])

    def load_ln_params(self, g_ap, b_ap):
        kb = self.kb
        kb.dma("sp", self.gbc[:], g_ap.partition_broadcast(128), writes=["gbc"])
        kb.dma("sp", self.bbc[:], b_ap.partition_broadcast(128), writes=["bbc"])

    def phase0(self):
        kb = self.kb
        self.epsc = self.sb("epsc", (128, 1))
        kb.op("dve", lambda e: e.memset(self.epsc[:], LN_EPS), writes=["epsc"])
        self.onesf = self.sb("onesf", (128, 64))
        kb.op("dve", lambda e: e.memset(self.onesf[:], 1.0), writes=["onesf"])
        self.begin_phase()
        self.ln_alloc()
        self.load_ln_params(self.I["ln_in_g"], self.I["ln_in_b"])
        for ti, (r0, nr) in enumerate(self.tiles):
            j = self.rr("xt", 2)
            xt, kxt = self.xt[j], "xt%d" % j
            src = self.I["x_prompt"][r0:r0 + nr, :] if r0 < self.T else self.I["x_sample"][:, :]
            kb.dma("sp", xt[:nr, :], src, writes=[kxt])
            self.ln_tile(xt, kxt, r0, nr, ti, self.S["xs"][r0:r0 + nr, :])
        self.end_phase()

    def load_w(self, dram_cols, width):
        kb = self.kb
        i = self.rr("w", 2)
        ws, wb = self.wst[i], self.wbf[i]
        kws, kwb = "wst%d" % i, "wbf%d" % i
        kb.dma("sp", ws[:, :, :width], dram_cols.rearrange("(k p) c -> p k c", p=128), writes=[kws])
        if self.rr("wcast", 2) == 0:
            kb.op("dve", lambda e: e.tensor_copy(out=wb[:, :, :width], in_=ws[:, :, :width]), reads=[kws], writes=[kwb])
        else:
            kb.op("act", lambda e: e.activation(out=wb[:, :, :width], in_=ws[:, :, :width], func=AF.Copy), reads=[kws], writes=[kwb])
        return wb, kwb

    def evac(self, dst, src, reads, writes):
        kb = self.kb
        if self.rr("evac", 2) == 0:
            kb.op("act", lambda e: e.activation(out=dst, in_=src, func=AF.Copy), reads=reads, writes=writes)
        else:
            kb.op("dve", lambda e: e.tensor_copy(out=dst, in_=src), reads=reads, writes=writes)

    def phase1(self, l):
        kb = self.kb
        self.begin_phase()
        self.wst = [self.sbp("wst%d" % i, (128, KC, 512)) for i in range(2)]
        self.wbf = [self.sbp("wbf%d" % i, (128, KC, 512), BF16) for i in range(2)]
        self.ev = [self.sbp("ev%d" % i, (128, 512)) for i in range(4)]
        W = self.I["w_in"][l]
        tm_segs = [(O_OG, O_ZM), (O_RC, O_U)]
        fm_segs = [(O_XM, O_IG), (O_IG, O_FG), (O_FG, O_OG), (O_ZM, O_RC), (O_U, O_ZS), (O_ZS, O_GM), (O_GM, N_IN)]
        for (c0, c1) in tm_segs:
            c = c0
            while c < c1:
                w = min(512, c1 - c)
                wb, kwb = self.load_w(W[:, c:c + w], w)
                for ti, (r0, nr) in enumerate(self.tiles):
                    p = self.rr("pp", 4)
                    pp, kpp = self.pp[p], "pp%d" % p
                    for kc in range(KC):
                        kb.op("pe", lambda e, kc=kc, pp=pp, r0=r0, nr=nr, wb=wb, w=w: e.matmul(
                            pp[:nr, :w], lhsT=self.xT[:, kc, r0:r0 + nr], rhs=wb[:, kc, :w],
                            start=(kc == 0), stop=(kc == KC - 1)),
                            reads=[("xT", ti), kwb], writes=[kpp])
                    v = self.rr("ev", 4)
                    ev, kev = self.ev[v], "ev%d" % v
                    self.evac(ev[:nr, :w], pp[:nr, :w], [kpp], [kev])
                    kb.dma("pool", self.S["z"][r0:r0 + nr, c:c + w], ev[:nr, :w], reads=[kev],
                           writes=[("z", ti)])
                c += w
        for (c0, c1) in fm_segs:
            c = c0
            while c < c1:
                w = min(512, c1 - c)
                wb, kwb = self.load_w(W[:, c:c + w], w)
                for s0 in range(0, w, 128):
                    m = min(128, w - s0)
                    for bi, (t0, n) in enumerate(self.blocks):
                        p = self.rr("pp", 4)
                        pp, kpp = self.pp[p], "pp%d" % p
                        tis = list(range(t0 // 128, (t0 + n + 127) // 128))
                        for kc in range(KC):
                            kb.op("pe", lambda e, kc=kc, pp=pp, t0=t0, n=n, wb=wb, s0=s0, m=m: e.matmul(
                                pp[:m, :n], lhsT=wb[:, kc, s0:s0 + m], rhs=self.xT[:, kc, t0:t0 + n],
                                start=(kc == 0), stop=(kc == KC - 1)),
                                reads=[("xT", t) for t in tis] + [kwb], writes=[kpp])
                        v = self.rr("ev", 4)
                        ev, kev = self.ev[v], "ev%d" % v
                        self.evac(ev[:m, :n], pp[:m, :n], [kpp], [kev])
                        kb.dma("pool", self.S["zT"][c + s0:c + s0 + m, t0:t0 + n], ev[:m, :n], reads=[kev],
                               writes=[("zT", c + s0, bi)])
                c += w
        self.end_phase()

    def easy_states(self, l):
        kb = self.kb
        I, S, O = self.I, self.S, self.O
        T, NS = self.T, self.NS
        nb = len(self.blocks)
        zt_all = [("zT", r, b) for r in range(0, 1024, 128) for b in range(nb)]
        z_all = [("z", ti) for ti in range(len(self.tiles))]
        kb.dma("pool", O["conv_prompt"][l].rearrange("j c -> c j"), S["zT"][0:D, T - 3:T], reads=zt_all, writes=[],
               is_output=True, allow_slow_non_contiguous=True)
        kb.dma("pool", O["shift_prompt"][l:l + 1, :], S["z"][T - 1:T, O_RC:O_RC + R_SHIFT_W], reads=z_all, writes=[], is_output=True)
        if NS:
            kb.dma("pool", O["conv_sample"][l][:, 0:2, :], I["state_mlstm_conv"][l][:, 1:3, :], writes=[], is_output=True)
            for hh in range(4):
                kb.dma("pool", O["conv_sample"][l][:, 2, hh * 256:(hh + 1) * 256].rearrange("b c -> c b"),
                       S["zT"][hh * 256:(hh + 1) * 256, T:T + NS], reads=zt_all, writes=[],
                       is_output=True, allow_slow_non_contiguous=True)
            kb.dma("pool", O["shift_sample"][l], S["z"][T:T + NS, O_RC:O_RC + R_SHIFT_W], reads=z_all, writes=[], is_output=True)

    def mlstm_phase(self, l):
        kb = self.kb
        I, S, O = self.I, self.S, self.O
        T, NS, NT = self.T, self.NS, self.NT
        self.begin_phase()
        sbp = self.sbp
        Wst = sbp("Wst", (128, 4, 2, 256))
        Wq = sbp("Wq", (128, 4, 2, 256), BF16); Wk = sbp("Wk", (128, 4, 2, 256), BF16); Wv = sbp("Wv", (128, 4, 2, 256), BF16)
        cw = sbp("cw", (128, 4, 8)); cb = sbp("cb", (128, 8)); mg = sbp("mg", (128, 8)); msk = sbp("msk", (128, 8))
        gb = sbp("gb", (4, 2))
        igA = sbp("igA", (4, NT)); lfA = sbp("lfA", (4, NT))
        selh = sbp("selh", (4, 4, 128)); causneg = sbp("causneg", (128, 128)); ones4 = sbp("ones4", (4, 128))
        Cst = [[sbp("C%d_%d" % (s_, h), (128, 2, 257)) for h in range(4)] for s_ in range(3)]

        def mk_set(tag, Lm):
            W = {"tag": tag}
            W["mprev"] = sbp(tag + "mprev", (4, 2))
            W["xext"] = [sbp(tag + "xext%d" % i, (128, 8, Lm + 3)) for i in range(2)]
            W["ctmp"] = sbp(tag + "cvtmp", (128, 8, Lm)); W["cacc"] = sbp(tag + "cvacc", (128, 8, Lm))
            W["xcT"] = sbp(tag + "xcT", (128, 8, Lm)); W["xcb"] = sbp(tag + "xcb", (128, 8, Lm), BF16); W["xmb"] = sbp(tag + "xmb", (128, 8, Lm), BF16)
            W["qTb"] = sbp(tag + "qTb", (128, 4, 2, Lm), BF16); W["kTb"] = sbp(tag + "kTb", (128, 4, 2, Lm), BF16)
            W["Cb"] = [sbp(tag + "Cb%d" % h, (128, 2, 257), BF16) for h in range(4)]
            W["kw"] = sbp(tag + "kw", (128, 256), BF16); W["vaug"] = sbp(tag + "vaug", (128, 257), BF16)
            W["G"] = sbp(tag + "G", (4, 8, Lm)); W["gsm"] = sbp(tag + "gsm", (4, 8)); W["dg"] = sbp(tag + "dg", (4, 4))
            W["gc"] = sbp(tag + "gc", (128, 16)); W["dbc"] = sbp(tag + "dbc", (128, 4))
            W["DT"] = sbp(tag + "DT", (128, Lm)); W["Stl"] = sbp(tag + "Stl", (128, Lm), BF16); W["mmb"] = sbp(tag + "mmb", (128, 4, Lm))
            W["Asb"] = sbp(tag + "Asb", (128, 257)); W["nd"] = sbp(tag + "nd", (128, 257)); W["dsm"] = sbp(tag + "dsm", (128, 4))
            W["sog"] = [sbp(tag + "sog%d" % i, (128, D)) for i in range(2 if Lm > 1 else 1)]
            W["hm"] = sbp(tag + "hm", (128, D)); W["hst"] = sbp(tag + "hst", (128, 16))
            W["hT"] = sbp(tag + "hT", (128, 8, Lm)); W["zmt"] = [sbp(tag + "zmt%d" % i, (128, 8, Lm)) for i in range(2)]
            W["aob"] = [sbp(tag + "aob%d" % i, (128, 8, Lm), BF16) for i in range(2)]
            return W
        WP = mk_set("P", 128)
        WS_ = mk_set("S", 1) if NS else None
        kb.local = frozenset(["mprev", "mnew", "xext0", "xext1", "cvacc", "cvtmp", "xcT", "xcb", "xmb", "G0", "G1", "G3", "G4", "G5", "G6", "gsm", "gc", "dg",
                              "dbc", "mmb", "DT", "Stl", "kw", "vaug", "Asb", "nd", "dsm", "sog0", "sog1", "hst", "zmt0", "zmt1", "aob0", "aob1",
                              "hm", "hT", "qTb", "kTb", "Cb"])
        cvst = sbp("cvst", (48, D)); convT = sbp("convT", (128, 8, 48))

        for (nm, dst, sc_) in (("m_wq", Wq, 1.0 / 16.0), ("m_wk", Wk, 1.0), ("m_wv", Wv, 1.0)):
            kb.dma("sp", Wst[:], I[nm][l].rearrange("h (c p) e -> p h c e", p=128), writes=["Wst"])
            kb.op("act", lambda e, dst=dst, sc_=sc_: e.activation(out=dst[:], in_=Wst[:], func=AF.Copy, scale=sc_),
                  reads=["Wst"], writes=[nm])
        sl = dict(allow_slow_non_contiguous=True)
        kb.dma("sp", cw[:], I["m_conv_w"][l].rearrange("j (k p) -> p j k", p=128), writes=["cw"], **sl)
        kb.dma("sp", cb[:], I["m_conv_b"][l].rearrange("(k p) -> p k", p=128), writes=["cb"], **sl)
        kb.dma("sp", mg[:], I["m_norm_g"][l].rearrange("(k p) -> p k", p=128), writes=["mg"], **sl)
        kb.dma("sp", msk[:], I["m_skip"][l].rearrange("(k p) -> p k", p=128), writes=["msk"], **sl)
        kb.dma("sp", gb[:, 0:1], I["m_ig_b"][l].rearrange("(h o) -> h o", o=1), writes=["gb"], **sl)
        kb.dma("sp", gb[:, 1:2], I["m_fg_b"][l].rearrange("(h o) -> h o", o=1), writes=["gb"], **sl)
        kb.dma("sp", selh[:], I["selh"], writes=["selh"])
        kb.dma("sp", causneg[:], I["causneg"], writes=["causneg"])
        kb.dma("sp", ones4[:], I["ones4"], writes=["ones4"])
        nb = len(self.blocks)
        kb.dma("sp", igA[:], S["zT"][O_IG:O_IG + 4, :], reads=[("zT", O_IG, b) for b in range(nb)], writes=["igA"])
        kb.dma("sp", lfA[:], S["zT"][O_FG:O_FG + 4, :], reads=[("zT", O_FG, b) for b in range(nb)], writes=["lfA"])
        kb.op("dve", lambda e: e.tensor_scalar(out=igA[:], in0=igA[:], scalar1=gb[:, 0:1], scalar2=None, op0=ALU.add),
              reads=["igA", "gb"], writes=["igA"])
        kb.op("dve", lambda e: e.tensor_scalar(out=lfA[:], in0=lfA[:], scalar1=gb[:, 1:2], scalar2=-1.0, op0=ALU.add, op1=ALU.mult),
              reads=["lfA", "gb"], writes=["lfA"])
        kb.op("act", lambda e: e.activation(out=lfA[:], in_=lfA[:], func=AF.Exp), reads=["lfA"], writes=["lfA"])
        kb.op("dve", lambda e: e.tensor_scalar(out=lfA[:], in0=lfA[:], scalar1=1.0, scalar2=None, op0=ALU.add), reads=["lfA"], writes=["lfA"])
        kb.op("act", lambda e: e.activation(out=lfA[:], in_=lfA[:], func=AF.Ln), reads=["lfA"], writes=["lfA"])
        kb.op("dve", lambda e: e.tensor_scalar(out=lfA[:], in0=lfA[:], scalar1=-1.0, scalar2=None, op0=ALU.mult), reads=["lfA"], writes=["lfA"])
        if NS:
            kb.dma("sp", cvst[:3 * NS, :], I["state_mlstm_conv"][l].rearrange("b j c -> (b j) c"), writes=["cvst"])
            for half in range(2):
                p = self.rr("ptr", 2)
                pt, kpt = self.ptr[p], "ptr%d" % p
                for k4 in range(4):
                    kc = half * 4 + k4
                    kb.op("pe", lambda e, pt=pt, k4=k4, kc=kc: e.transpose(out=pt[:, k4 * 48:k4 * 48 + 3 * NS], in_=cvst[:3 * NS, kc * 128:(kc + 1) * 128],
                                                                       identity=self.ident[:3 * NS, :3 * NS]),
                          reads=["cvst", "ident"], writes=[kpt])
                kb.op("dve", lambda e, pt=pt, half=half: e.tensor_copy(out=convT[:, half * 4:half * 4 + 4, :3 * NS],
                                                                     in_=pt[:, 0:192].rearrange("p (k c) -> p k c", k=4)[:, :, :3 * NS]),
                      reads=[kpt], writes=["convT"])

        zt_x = lambda bi: [("zT", r, bi) for r in range(0, 1024, 128)]
        zt_z = lambda bi: [("zT", O_ZM + r, bi) for r in range(0, 1024, 128)]
        blk_of = lambda t: next(i for i, (b0, bn) in enumerate(self.blocks) if b0 <= t < b0 + bn)

        chunks = [(c * 128, 128, None) for c in range(T // 128)] + [(T + b, 1, b) for b in range(NS)]
        def chunk_gen(ci, t0, L, sb_, W):
            mprev = W["mprev"]; xext = W["xext"]; ctmp = W["ctmp"]; cacc = W["cacc"]; xcT = W["xcT"]; xcb = W["xcb"]; xmb = W["xmb"]
            qTb = W["qTb"]; kTb = W["kTb"]; kw = W["kw"]; vaug = W["vaug"]; G = W["G"]; gsm = W["gsm"]; dg = W["dg"]; gc = W["gc"]; dbc = W["dbc"]
            DT = W["DT"]; Stl = W["Stl"]; mmb = W["mmb"]; Asb = W["Asb"]; nd = W["nd"]; dsm = W["dsm"]; sog = W["sog"]; hm = W["hm"]; hst = W["hst"]
            hT = W["hT"]; zmt = W["zmt"]; aob = W["aob"]; Cb = W["Cb"]
            bi = blk_of(t0)
            ti = t0 // 128
            cs = (1 + self.rr("Cset", 2)) if sb_ is not None else 0
            C = Cst[cs]
            kC = [("C", cs, h) for h in range(4)]
            if ci == 0:
                for h in range(4):
                    kb.op("pool", lambda e, h=h: e.memset(C[h][:], 0.0), writes=[kC[h]])
                kb.op("dve", lambda e: e.memset(mprev[:, 0:1], NEG), writes=["mprev"])
            if sb_ is not None:
                for h in range(4):
                    kb.dma("sp", C[h][:, :, 0:256], I["state_mlstm_c"][l, sb_, h].rearrange("(c p) v -> p c v", p=128), writes=[kC[h]])
                    kb.dma("sp", C[h][:, :, 256:257], I["state_mlstm_n"][l, sb_, h].rearrange("(c p o) -> p c o", p=128, o=1), writes=[kC[h]], **sl)
                kb.dma("sp", mprev[:, 0:1], I["state_mlstm_m"][l, sb_].rearrange("(h o) -> h o", o=1), writes=["mprev"], **sl)
            for h in range(4):
                kb.op("act", lambda e, h=h: e.activation(out=Cb[h][:], in_=C[h][:], func=AF.Copy), reads=[kC[h]], writes=[("Cb", h)])
            xj = self.rr("xext" + W["tag"], 2)
            xe, kxe = xext[xj], "xext%d" % xj
            if sb_ is None:
                if t0 == 0:
                    kb.op("pool", lambda e, xe=xe: e.memset(xe[:, :, 0:3], 0.0), writes=[kxe])
                    kb.dma("sp", xe[:, :, 3:3 + L], S["zT"][0:D, t0:t0 + L].rearrange("(k p) t -> p k t", p=128), reads=zt_x(bi), writes=[kxe])
                else:
                    rd = zt_x(bi) + (zt_x(blk_of(t0 - 3)) if blk_of(t0 - 3) != bi else [])
                    kb.dma("sp", xe[:, :, 0:3 + L], S["zT"][0:D, t0 - 3:t0 + L].rearrange("(k p) t -> p k t", p=128), reads=rd, writes=[kxe])
            else:
                kb.op("pool", lambda e, xe=xe, sb_=sb_: e.tensor_copy(out=xe[:, :, 0:3], in_=convT[:, :, 3 * sb_:3 * sb_ + 3]), reads=["convT"], writes=[kxe])
                kb.dma("sp", xe[:, :, 3:4], S["zT"][0:D, t0:t0 + 1].rearrange("(k p) t -> p k t", p=128), reads=zt_x(bi), writes=[kxe], **sl)
            V = lambda x: x[:, :, :L]
            for j in range(4):
                dst = cacc if j == 0 else ctmp
                kd = "cvacc" if j == 0 else "cvtmp"
                kb.op("dve", lambda e, j=j, dst=dst, xe=xe: e.tensor_tensor(out=V(dst), in0=xe[:, :, j:j + L],
                                                                         in1=cw[:, j, :, None].to_broadcast([128, 8, L]), op=ALU.mult),
                      reads=[kxe, "cw"], writes=[kd])
                if j > 0:
                    kb.op("dve", lambda e: e.tensor_tensor(out=V(cacc), in0=V(cacc), in1=V(ctmp), op=ALU.add), reads=["cvacc", "cvtmp"], writes=["cvacc"])
            kb.op("dve", lambda e: e.tensor_tensor(out=V(cacc), in0=V(cacc), in1=cb[:, :, None].to_broadcast([128, 8, L]), op=ALU.add),
                  reads=["cvacc", "cb"], writes=["cvacc"])
            kb.op("act", lambda e: e.activation(out=V(xcT), in_=V(cacc), func=AF.Silu), reads=["cvacc"], writes=["xcT"])
            kb.op("dve", lambda e: e.tensor_copy(out=V(xcb), in_=V(xcT)), reads=["xcT"], writes=["xcb"])
            kb.op("dve", lambda e, xe=xe: e.tensor_copy(out=V(xmb), in_=xe[:, :, 3:3 + L]), reads=[kxe], writes=["xmb"])
            yield
            Gr = lambda i: G[:, i, :L]
            kb.op("dve", lambda e: e.tensor_tensor_scan(out=Gr(0), data0=ones4[:, :L], data1=lfA[:, t0:t0 + L], initial=0.0, op0=ALU.mult, op1=ALU.add),
                  reads=["ones4", "lfA"], writes=["G0"])
            kb.op("dve", lambda e: e.tensor_tensor(out=Gr(3), in0=igA[:, t0:t0 + L], in1=Gr(0), op=ALU.subtract), reads=["igA", "G0"], writes=["G3"])
            kb.op("dve", lambda e: e.tensor_tensor_scan(out=Gr(1), data0=Gr(3), data1=Gr(3), initial=-3.0e38, op0=ALU.max, op1=ALU.max),
                  reads=["G3"], writes=["G1"])
            kb.op("dve", lambda e: e.tensor_scalar(out=Gr(1), in0=Gr(1), scalar1=mprev[:, 0:1], scalar2=None, op0=ALU.max), reads=["G1", "mprev"], writes=["G1"])
            kb.op("dve", lambda e: e.tensor_scalar(out=gsm[:, 0:1], in0=G[:, 1, L - 1:L], scalar1=-1.0, scalar2=None, op0=ALU.mult), reads=["G1"], writes=["gsm"])
            kb.op("act", lambda e: e.activation(out=Gr(4), in_=Gr(1), func=AF.Exp, scale=-1.0, bias=mprev[:, 0:1]), reads=["G1", "mprev"], writes=["G4"])
            kb.op("dve", lambda e: e.tensor_tensor(out=Gr(5), in0=Gr(0), in1=Gr(1), op=ALU.add), reads=["G0", "G1"], writes=["G5"])
            kb.op("act", lambda e: e.activation(out=Gr(5), in_=Gr(5), func=AF.Exp, scale=-1.0), reads=["G5"], writes=["G5"])
            kb.op("act", lambda e: e.activation(out=Gr(6), in_=Gr(3), func=AF.Exp, bias=gsm[:, 0:1]), reads=["G3", "gsm"], writes=["G6"])
            kb.op("dve", lambda e: e.tensor_tensor(out=mprev[:, 1:2], in0=G[:, 0, L - 1:L], in1=G[:, 1, L - 1:L], op=ALU.add), reads=["G0", "G1"], writes=["mnew"])
            p = self.rr("ptr", 2)
            pt, kpt = self.ptr[p], "ptr%d" % p
            for ki, gi in enumerate((4, 5, 3, 6)):
                kb.op("pe", lambda e, ki=ki, gi=gi, pt=pt: e.transpose(out=pt[:L, ki * 4:ki * 4 + 4], in_=G[:, gi, :L], identity=self.ident[:4, :4]),
                      reads=["G%d" % gi, "ident"], writes=[kpt])
            kb.op("dve", lambda e, pt=pt: e.tensor_copy(out=gc[:L, :], in_=pt[:L, 0:16]), reads=[kpt], writes=["gc"])
            kb.op("dve", lambda e: e.tensor_scalar(out=dg[:, :], in0=self.ident[:4, :4], scalar1=G[:, 4, L - 1:L], scalar2=None, op0=ALU.mult),
                  reads=["G4", "ident"], writes=["dg"])
            p2 = self.rr("ptr", 2)
            pt2, kpt2 = self.ptr[p2], "ptr%d" % p2
            kb.op("pe", lambda e, pt2=pt2: e.matmul(pt2[:, 0:4], lhsT=ones4[:, :], rhs=dg[:, :], start=True, stop=True), reads=["ones4", "dg"], writes=[kpt2])
            kb.op("dve", lambda e, pt2=pt2: e.tensor_copy(out=dbc[:, :], in_=pt2[:, 0:4]), reads=[kpt2], writes=["dbc"])
            pnn = self.rr("pp", 4)
            pn, kpn = self.pp[pnn], "pp%d" % pnn
            for h in range(4):
                kb.op("pe", lambda e, h=h, pn=pn: e.matmul(pn[:L, h * 128:h * 128 + L], lhsT=selh[:, h, :L], rhs=G[:, 1, :L], start=True, stop=True),
                      reads=["selh", "G1"], writes=[kpn])
            kb.op("act", lambda e, pn=pn: e.activation(out=mmb[:L, :, :L], in_=pn[:L, :].rearrange("p (h t) -> p h t", h=4)[:, :, :L], func=AF.Copy),
                  reads=[kpn], writes=["mmb"])
            sj = self.rr("sog" + W["tag"], len(sog))
            so, kso = sog[sj], "sog%d" % sj
            kb.dma("sp", so[:L, :], S["z"][t0:t0 + L, O_OG:O_OG + D], reads=[("z", ti)], writes=[kso])
            kb.op("act", lambda e, so=so: e.activation(out=so[:L, :], in_=so[:L, :], func=AF.Sigmoid), reads=[kso], writes=[kso])
            yield
            for h in range(4):
                for (Wt, wn, dstT, kd) in ((Wq, "m_wq", qTb, "qTb"), (Wk, "m_wk", kTb, "kTb")):
                    pq = self.rr("pp", 4)
                    ppq, kpq = self.pp[pq], "pp%d" % pq
                    for ec in range(2):
                        for dc in range(2):
                            kb.op("pe", lambda e, h=h, ec=ec, dc=dc, Wt=Wt, ppq=ppq: e.matmul(
                                ppq[:, ec * 128:ec * 128 + L], lhsT=Wt[:, h, dc, ec * 128:(ec + 1) * 128], rhs=xcb[:, 2 * h + dc, :L],
                                start=(dc == 0), stop=(dc == 1)), reads=[wn, "xcb"], writes=[kpq])
                    self.evac(dstT[:, h, :, :L], ppq[:, 0:256].rearrange("p (c t) -> p c t", c=2)[:, :, :L], [kpq], [(kd, h)])
            for h in range(4):
                yield
                pk = self.rr("pp", 4)
                ppk, kpk = self.pp[pk], "pp%d" % pk
                for dc in range(2):
                    kb.op("pe", lambda e, h=h, dc=dc, ppk=ppk: e.matmul(ppk[:L, 0:256], lhsT=xcb[:, 2 * h + dc, :L], rhs=Wk[:, h, dc, :],
                                                                      start=(dc == 0), stop=(dc == 1)), reads=["m_wk", "xcb"], writes=[kpk])
                kb.op("act", lambda e, h=h, ppk=ppk: e.activation(out=kw[:L, :], in_=ppk[:L, 0:256], func=AF.Copy, scale=gc[:L, 12 + h:13 + h]),
                      reads=[kpk, "gc"], writes=["kw"])
                pv = self.rr("pp", 4)
                ppv, kpv = self.pp[pv], "pp%d" % pv
                for dc in range(2):
                    kb.op("pe", lambda e, h=h, dc=dc, ppv=ppv: e.matmul(ppv[:L, 0:256], lhsT=xmb[:, 2 * h + dc, :L], rhs=Wv[:, h, dc, :],
                                                                      start=(dc == 0), stop=(dc == 1)), reads=["m_wv", "xmb"], writes=[kpv])
                kb.op("dve", lambda e, ppv=ppv: e.tensor_copy(out=vaug[:L, 0:256], in_=ppv[:L, 0:256]), reads=[kpv], writes=["vaug"])
                kb.op("dve", lambda e: e.memset(vaug[:L, 256:257], 1.0), writes=["vaug"])
                kb.op("dve", lambda e, h=h: e.tensor_tensor(out=DT[:L, :L], in0=mmb[:L, h, :L], in1=causneg[:L, :L], op=ALU.add),
                      reads=["mmb", "causneg"], writes=["DT"])
                kb.op("act", lambda e, h=h: e.activation(out=DT[:L, :L], in_=DT[:L, :L], func=AF.Exp, scale=-1.0, bias=gc[:L, 8 + h:9 + h]),
                      reads=["DT", "gc"], writes=["DT"])
                ps_ = self.rr("pp", 4)
                pps, kps = self.pp[ps_], "pp%d" % ps_
                for ec in range(2):
                    kb.op("pe", lambda e, h=h, ec=ec, pps=pps: e.matmul(pps[:L, :L], lhsT=kTb[:, h, ec, :L], rhs=qTb[:, h, ec, :L],
                                                                      start=(ec == 0), stop=(ec == 1)), reads=[("kTb", h), ("qTb", h)], writes=[kps])
                kb.op("dve", lambda e, pps=pps: e.tensor_tensor(out=Stl[:L, :L], in0=pps[:L, :L], in1=DT[:L, :L], op=ALU.mult), reads=[kps, "DT"], writes=["Stl"])
                pa = self.rr("pp", 4)
                ppa, kpa = self.pp[pa], "pp%d" % pa
                kb.op("pe", lambda e, ppa=ppa: e.matmul(ppa[:L, 0:257], lhsT=Stl[:L, :L], rhs=vaug[:L, :], start=True, stop=True), reads=["Stl", "vaug"], writes=[kpa])
                pb = self.rr("ptr", 2)
                ppb, kpb = self.ptr[pb], "ptr%d" % pb
                for ec in range(2):
                    kb.op("pe", lambda e, h=h, ec=ec, ppb=ppb, C=C: e.matmul(ppb[:L, 0:257], lhsT=qTb[:, h, ec, :L], rhs=Cb[h][:, ec, :],
                                                                      start=(ec == 0), stop=(ec == 1)), reads=[("qTb", h), ("Cb", h)], writes=[kpb])
                kb.op("act", lambda e, ppa=ppa: e.activation(out=Asb[:L, :], in_=ppa[:L, 0:257], func=AF.Copy), reads=[kpa], writes=["Asb"])
                kb.op("dve", lambda e, h=h, ppb=ppb: e.scalar_tensor_tensor(out=nd[:L, :], in0=ppb[:L, 0:257], scalar=gc[:L, h:h + 1], in1=Asb[:L, :],
                                                                        op0=ALU.mult, op1=ALU.add), reads=[kpb, "gc", "Asb"], writes=["nd"])
                kb.op("act", lambda e: e.activation(out=dsm[:L, 2:3], in_=nd[:L, 256:257], func=AF.Abs), reads=["nd"], writes=["dsm"])
                kb.op("dve", lambda e, h=h: e.tensor_scalar(out=dsm[:L, 0:1], in0=dsm[:L, 2:3], scalar1=gc[:L, 4 + h:5 + h], scalar2=None,
                                                            op0=ALU.max), reads=["dsm", "gc"], writes=["dsm"])
                kb.op("dve", lambda e: e.reciprocal(out=dsm[:L, 1:2], in_=dsm[:L, 0:1]), reads=["dsm"], writes=["dsm"])
                kb.op("dve", lambda e, h=h, so=so: e.scalar_tensor_tensor(out=hm[:L, h * 256:(h + 1) * 256], in0=nd[:L, 0:256], scalar=dsm[:L, 1:2],
                                                                      in1=so[:L, h * 256:(h + 1) * 256], op0=ALU.mult, op1=ALU.mult),
                      reads=["nd", "dsm", kso], writes=[("hm", h)])
                for ec in range(2):
                    pu = self.rr("pp", 4)
                    ppu, kpu = self.pp[pu], "pp%d" % pu
                    kb.op("pe", lambda e, ec=ec, ppu=ppu: e.matmul(ppu[:, 0:257], lhsT=kw[:L, ec * 128:(ec + 1) * 128], rhs=vaug[:L, :], start=True, stop=True),
                          reads=["kw", "vaug"], writes=[kpu])
                    kb.op("dve", lambda e, h=h, ec=ec, ppu=ppu, C=C: e.scalar_tensor_tensor(out=C[h][:, ec, :], in0=C[h][:, ec, :], scalar=dbc[:, h:h + 1],
                                                                                      in1=ppu[:, 0:257], op0=ALU.mult, op1=ALU.add),
                          reads=[kC[h], "dbc", kpu], writes=[kC[h]])
            kb.op("dve", lambda e: e.tensor_copy(out=mprev[:, 0:1], in_=mprev[:, 1:2]), reads=["mnew"], writes=["mprev"])
            yield
            hmk = [("hm", h) for h in range(4)]
            hv = hm[:L, :].rearrange("t (h d) -> t h d", h=4)
            kb.op("dve", lambda e: e.tensor_reduce(out=hst[:L, 0:4], in_=hv, axis=AX.X, op=ALU.add), reads=hmk, writes=["hst"])
            kb.op("dve", lambda e: e.tensor_scalar(out=hst[:L, 0:4], in0=hst[:L, 0:4], scalar1=-1.0 / 256.0, scalar2=None, op0=ALU.mult), reads=["hst"], writes=["hst"])
            kb.op("dve", lambda e: e.tensor_tensor(out=hv, in0=hv, in1=hst[:L, 0:4].unsqueeze(2).to_broadcast([L, 4, 256]), op=ALU.add),
                  reads=hmk + ["hst"], writes=hmk)
            kb.op("dve", lambda e, so=so: e.tensor_tensor(out=so[:L, :], in0=hm[:L, :], in1=hm[:L, :], op=ALU.mult), reads=hmk, writes=[kso])
            kb.op("dve", lambda e, so=so: e.tensor_reduce(out=hst[:L, 4:8], in_=so[:L, :].rearrange("t (h d) -> t h d", h=4), axis=AX.X, op=ALU.add),
                  reads=[kso], writes=["hst"])
            kb.op("act", lambda e: e.activation(out=hst[:L, 8:12], in_=hst[:L, 4:8], func=AF.Sqrt, scale=1.0 / 256.0, bias=self.epsc[:L, 0:1]),
                  reads=["hst", "epsc"], writes=["hst"])
            kb.op("dve", lambda e: e.reciprocal(out=hst[:L, 12:16], in_=hst[:L, 8:12]), reads=["hst"], writes=["hst"])
            kb.op("dve", lambda e: e.tensor_tensor(out=hv, in0=hv, in1=hst[:L, 12:16].unsqueeze(2).to_broadcast([L, 4, 256]), op=ALU.mult),
                  reads=hmk + ["hst"], writes=hmk)
            zj = self.rr("zmt" + W["tag"], 2)
            zm_, kzm = zmt[zj], "zmt%d" % zj
            kb.dma("sp", zm_[:, :, :L], S["zT"][O_ZM:O_ZM + D, t0:t0 + L].rearrange("(k p) t -> p k t", p=128), reads=zt_z(bi), writes=[kzm],
                   **(sl if L == 1 else {}))
            kb.op("act", lambda e, zm_=zm_: e.activation(out=zm_[:, :, :L], in_=zm_[:, :, :L], func=AF.Silu), reads=[kzm], writes=[kzm])
            for half in range(2):
                p = self.rr("ptr", 2)
                pt, kpt = self.ptr[p], "ptr%d" % p
                for k4 in range(4):
                    kc = half * 4 + k4
                    kb.op("pe", lambda e, k4=k4, kc=kc, pt=pt: e.transpose(out=pt[:, k4 * 128:k4 * 128 + L], in_=hm[:L, kc * 128:(kc + 1) * 128],
                                                                       identity=self.ident[:L, :L]), reads=hmk + ["ident"], writes=[kpt])
                kb.op("dve", lambda e, half=half, pt=pt: e.tensor_tensor(out=hT[:, half * 4:half * 4 + 4, :L],
                                                                       in0=pt[:, :].rearrange("p (k t) -> p k t", k=4)[:, :, :L],
                                                                       in1=mg[:, half * 4:half * 4 + 4, None].to_broadcast([128, 4, L]), op=ALU.mult),
                      reads=[kpt, "mg"], writes=[("hT", half)])
            kb.op("dve", lambda e: e.tensor_tensor(out=V(ctmp), in0=V(xcT), in1=msk[:, :, None].to_broadcast([128, 8, L]), op=ALU.mult),
                  reads=["xcT", "msk"], writes=["cvtmp"])
            kb.op("dve", lambda e: e.tensor_tensor(out=V(hT), in0=V(hT), in1=V(ctmp), op=ALU.add), reads=["cvtmp", ("hT", 0), ("hT", 1)], writes=[("hT", 0), ("hT", 1)])
            aj = self.rr("aob" + W["tag"], 2)
            kb.op("dve", lambda e, aj=aj, zm_=zm_: e.tensor_tensor(out=aob[aj][:, :, :L], in0=V(hT), in1=zm_[:, :, :L], op=ALU.mult),
                  reads=[("hT", 0), ("hT", 1), kzm], writes=["aob%d" % aj])
            kb.dma("pool", S["act_m"][:, t0:t0 + L].rearrange("(k p) t -> p k t", p=128), aob[aj][:, :, :L], reads=["aob%d" % aj],
                   writes=[("act", "m", bi)], **(sl if L == 1 else {}))
            if sb_ is not None or ci == T // 128 - 1:
                if sb_ is None:
                    oc_, on_, om_ = O["c_prompt"][l], O["n_prompt"][l], O["m_prompt"][l]
                else:
                    oc_, on_, om_ = O["c_sample"][l, sb_], O["n_sample"][l, sb_], O["m_sample"][l, sb_]
                for h in range(4):
                    kb.dma("pool", oc_[h].rearrange("(c p) v -> p c v", p=128), C[h][:, :, 0:256], reads=[kC[h]], writes=[], is_output=True)
                    kb.dma("pool", on_[h].rearrange("(c p o) -> p c o", p=128, o=1), C[h][:, :, 256:257], reads=[kC[h]], writes=[], is_output=True, **sl)
                kb.dma("pool", om_.rearrange("(h o) -> h o", o=1), mprev[:, 0:1], reads=["mprev"], writes=[], is_output=True, **sl)
        pq_ = [(ci, t0, L, sb_) for ci, (t0, L, sb_) in enumerate(chunks) if sb_ is None]
        sq_ = [(ci, t0, L, sb_) for ci, (t0, L, sb_) in enumerate(chunks) if sb_ is not None]
        streams = [[pq_, WP, None], [sq_, WS_, None]]
        while any(st[0] or st[2] is not None for st in streams):
            for st in streams:
                if st[2] is None and st[0]:
                    args = st[0].pop(0)
                    st[2] = chunk_gen(*args, st[1])
                if st[2] is not None:
                    kb.ns = st[1]["tag"]
                    try:
                        next(st[2])
                    except StopIteration:
                        st[2] = None
                    kb.ns = None
        self.end_phase()

    def rwkv_phase(self, l):
        kb = self.kb
        I, S, O = self.I, self.S, self.O
        T, NS, NT = self.T, self.NS, self.NT
        self.begin_phase()
        sbp = self.sbp
        sl = dict(allow_slow_non_contiguous=True)
        NPI = self.NPI
        EW = -math.exp(-0.5)
        P_ = {}
        for nm in ("r_k_k", "r_k_a", "r_ln_g", "r_ln_b"):
            P_[nm] = sbp("bc_" + nm, (128, D))
            kb.dma("sp", P_[nm][:], I[nm][l].partition_broadcast(128), writes=[nm])
        P_["r_r_k"] = sbp("bc_rk", (128, D))
        kb.dma("sp", P_["r_r_k"][:], I["r_r_k"][l].rearrange("h j -> (h j)").partition_broadcast(128), writes=["r_r_k"])
        mu = sbp("bc_mu", (128, R_SHIFT_W))
        kb.dma("sp", mu[:], I["r_mu"][l].partition_broadcast(128), writes=["mu"])
        w2 = sbp("w2", (65, 2, D))
        kb.dma("sp", w2[0:64, 0, :], I["r_w2"][l], writes=["w2"])
        kb.dma("sp", w2[0:64, 1, :], I["r_a2"][l], writes=["w2"])
        kb.dma("sp", w2[64:65, 0, :], I["r_w0"][l:l + 1, :], writes=["w2"])
        kb.dma("sp", w2[64:65, 1, :], I["r_a0"][l:l + 1, :], writes=["w2"])
        mks = sbp("mks", (128, 384))
        kb.dma("sp", mks[:], I["rmasks"], writes=["mks"])
        e12 = sbp("e12", (128, 1))
        kb.op("dve", lambda e: e.memset(e12[:], RWKV_GN_EPS), writes=["e12"])
        xr = sbp("xr", (128, R_SHIFT_W)); rp = sbp("rp", (128, R_SHIFT_W))
        lt = sbp("lt", (65, 2, 128))
        kb.op("pool", lambda e: e.memset(lt[64:65, :, :], 1.0), writes=["lt1"])
        tv = {n: sbp("tv_" + n, (128, D)) for n in ("lw", "a", "an", "b", "k")}
        ssq = sbp("ssq", (128, 64))
        fmall = sbp("fmall", (128, 8, 6, 128))
        fm = {n: fmall[:, :, i_, :] for i_, n in enumerate(("at", "rt", "bh", "kh", "bc", "kc"))}
        fm["cum"] = sbp("fm_cum", (128, 8, 128)); fm["et"] = sbp("fm_et", (128, 8, 128))
        wc = sbp("wc", (128, 8, 16))
        ST = sbp("STt", (128, 8, 64))
        Vp = sbp("Vp", (128, 8, 64)); Ych = sbp("Ych", (128, 8, 64))
        nat = sbp("rnat", (128, 8, 64)); nato = sbp("rnato", (128, 8, 64))
        CDT = BF16 if self.CORE_BF16 else F32
        NG = 2
        G_ = []
        BK = sbp("g_BK", (128, 4, 2, 128), CDT)
        for gi in range(NG):
            d = {}
            d["UBD"] = sbp("g%d_UBD" % gi, (128, 4, 4, 128), CDT)
            d["Btm"] = sbp("g%d_Btm" % gi, (128, 4, 128), CDT); d["Ktm"] = sbp("g%d_Ktm" % gi, (128, 4, 128), CDT)
            d["MNb"] = sbp("g%d_MNb" % gi, (128, 4, 256), CDT); d["MNk"] = sbp("g%d_MNk" % gi, (128, 4, 256), CDT)
            d["X"] = sbp("g%d_X" % gi, (128, 4, 128), CDT); d["Xt"] = sbp("g%d_Xt" % gi, (128, 4, 128), CDT); d["P"] = sbp("g%d_P" % gi, (128, 4, 128), CDT)
            d["RHS"] = sbp("g%d_RHS" % gi, (128, 4, 64), CDT); d["U"] = sbp("g%d_U" % gi, (128, 4, 64), CDT)
            G_.append(d)
        self.bank8 = list(self.pp) + list(self.ptr) + list(self.pex)
        self.bank8k = ["pp%d" % i for i in range(4)] + ["ptr%d" % i for i in range(2)] + ["pex%d" % i for i in range(2)]
        if self.CORE_BF16:
            STb = sbp("STb", (128, 8, 64), BF16); Vpb = sbp("Vpb", (128, 8, 64), BF16); identc = sbp("identc", (128, 128), BF16)
            kb.op("pool", lambda e: e.tensor_copy(out=identc[:], in_=self.ident[:]), reads=["ident"], writes=["identc"])
            kb.op("pool", lambda e: e.memset(STb[:], 0.0), writes=[("STb", h) for h in range(8)])
        else:
            STb, Vpb, identc = ST, Vp, self.ident
        kSTb = (lambda hh: ("STb", hh)) if self.CORE_BF16 else (lambda hh: ("ST", hh))
        kVpb = (lambda hh: ("Vpb", hh)) if self.CORE_BF16 else (lambda hh: ("Vp", hh))

        def vp_ready():
            if self.CORE_BF16:
                kb.op("act", lambda e: e.activation(out=Vpb[:], in_=Vp[:], func=AF.Copy), reads=[("Vp", h) for h in range(8)], writes=[("Vpb", h) for h in range(8)])
        ytm = rp[:, D:2 * D]; zrt = rp[:, 0:D]; yst = sbp("yst", (128, 64))
        aob = [sbp("raob%d" % i, (128, 8, 128), BF16) for i in range(2)]

        def zero_units():
            for gi in range(NG):
                kb.op("pool", lambda e, gi=gi: e.memset(G_[gi]["UBD"][:], 0.0), writes=[("g", gi, "UBD")])
            kb.op("pool", lambda e: e.memset(BK[:], 0.0), writes=["BK"])
        zero_units()
        kb.op("pool", lambda e: e.memset(ST[:], 0.0), writes=[("ST", h) for h in range(8)])

        z_all = lambda ti: [("z", ti)]

        def prep(t0, L, ti, is_s):
            kb.dma("sp", xr[:L, :], S["z"][t0:t0 + L, O_RC:O_RC + R_SHIFT_W], reads=z_all(ti), writes=["xr"])
            if is_s:
                kb.dma("sp", rp[:L, :], I["state_rwkv_shift"][l], writes=["rp"])
            elif t0 == 0:
                kb.op("pool", lambda e: e.memset(rp[0:1, :], 0.0), writes=["rp"])
                kb.dma("sp", rp[1:L, :], S["z"][0:L - 1, O_RC:O_RC + R_SHIFT_W], reads=z_all(ti), writes=["rp"])
            else:
                kb.dma("sp", rp[:L, :], S["z"][t0 - 1:t0 + L - 1, O_RC:O_RC + R_SHIFT_W], reads=z_all(ti) + z_all(ti - 1), writes=["rp"])
            kb.op("dve", lambda e: e.tensor_tensor(out=rp[:L, :], in0=rp[:L, :], in1=xr[:L, :], op=ALU.subtract), reads=["rp", "xr"], writes=["rp"])
            kb.op("dve", lambda e: e.tensor_tensor(out=rp[:L, :], in0=rp[:L, :], in1=mu[:L, :], op=ALU.mult), reads=["rp", "mu"], writes=["rp"])
            kb.op("dve", lambda e: e.tensor_tensor(out=xr[:L, :], in0=xr[:L, :], in1=rp[:L, :], op=ALU.add), reads=["rp", "xr"], writes=["xr"])
            r_ = xr[:L, 0:D]; kr = xr[:L, D:2 * D]; vr = xr[:L, 2 * D:3 * D]
            kb.dma("pool", S["vs"][t0:t0 + L, :], vr, reads=["xr"], writes=[("vs", ti)])
            kb.op("act", lambda e: e.activation(out=xr[:L, 3 * D:3 * D + 64], in_=xr[:L, 3 * D:3 * D + 64], func=AF.Tanh), reads=["xr"], writes=["xr"])
            p = self.rr("ptr", 2)
            pt, kpt = self.ptr[p], "ptr%d" % p
            for i2 in range(2):
                kb.op("pe", lambda e, i2=i2, pt=pt: e.transpose(out=pt[:64, i2 * 128:i2 * 128 + L], in_=xr[:L, 3 * D + 64 * i2:3 * D + 64 * i2 + 64],
                                                             identity=self.ident[:L, :L]), reads=["xr", "ident"], writes=[kpt])
            kb.op("dve", lambda e, pt=pt: e.tensor_copy(out=lt[0:64, :, :L], in_=pt[:64, 0:256].rearrange("p (a t) -> p a t", a=2)[:, :, :L]), reads=[kpt], writes=["lt"])
            for i2, dst in enumerate(("lw", "a")):
                for hf in range(2):
                    pq = self.rr("pp", 4)
                    pp, kpp = self.pp[pq], "pp%d" % pq
                    kb.op("pe", lambda e, i2=i2, hf=hf, pp=pp: e.matmul(pp[:L, :], lhsT=lt[:, i2, :L], rhs=w2[:, i2, hf * 512:(hf + 1) * 512], start=True, stop=True),
                          reads=["lt", "lt1", "w2"], writes=[kpp])
                    kb.op("act", lambda e, dst=dst, hf=hf, pp=pp: e.activation(out=tv[dst][:L, hf * 512:(hf + 1) * 512], in_=pp[:L, :], func=AF.Sigmoid),
                          reads=[kpp], writes=[("tv", dst)])
            kb.op("dve", lambda e: e.tensor_scalar(out=tv["lw"][:L, :], in0=tv["lw"][:L, :], scalar1=EW, scalar2=None, op0=ALU.mult), reads=[("tv", "lw")], writes=[("tv", "lw")])
            kb.op("dve", lambda e: e.tensor_tensor(out=tv["an"][:L, :], in0=kr, in1=P_["r_k_k"][:L, :], op=ALU.mult), reads=["xr", "r_k_k"], writes=[("tv", "an")])
            kb.op("dve", lambda e: e.tensor_tensor(out=tv["b"][:L, :], in0=tv["an"][:L, :], in1=tv["an"][:L, :], op=ALU.mult), reads=[("tv", "an")], writes=[("tv", "b")])
            kb.op("dve", lambda e: e.tensor_reduce(out=ssq[:L, 0:16], in_=tv["b"][:L, :].rearrange("t (h j) -> t h j", h=16), axis=AX.X, op=ALU.add),
                  reads=[("tv", "b")], writes=["ssq"])
            kb.op("act", lambda e: e.activation(out=ssq[:L, 0:16], in_=ssq[:L, 0:16], func=AF.Sqrt), reads=["ssq"], writes=["ssq"])
            kb.op("dve", lambda e: e.tensor_scalar(out=ssq[:L, 0:16], in0=ssq[:L, 0:16], scalar1=1e-12, scalar2=None, op0=ALU.max), reads=["ssq"], writes=["ssq"])
            kb.op("dve", lambda e: e.reciprocal(out=ssq[:L, 16:32], in_=ssq[:L, 0:16]), reads=["ssq"], writes=["ssq"])
            hv = lambda x: x[:L, :].rearrange("t (h j) -> t h j", h=16)
            kb.op("dve", lambda e: e.tensor_tensor(out=hv(tv["an"]), in0=hv(tv["an"]), in1=ssq[:L, 16:32].unsqueeze(2).to_broadcast([L, 16, 64]), op=ALU.mult),
                  reads=[("tv", "an"), "ssq"], writes=[("tv", "an")])
            kb.op("dve", lambda e: e.tensor_tensor(out=tv["b"][:L, :], in0=tv["an"][:L, :], in1=tv["a"][:L, :], op=ALU.mult),
                  reads=[("tv", "an"), ("tv", "a")], writes=[("tv", "b")])
            kb.op("dve", lambda e: e.tensor_scalar(out=tv["an"][:L, :], in0=tv["an"][:L, :], scalar1=-1.0, scalar2=None, op0=ALU.mult),
                  reads=[("tv", "an"), ("tv", "b")], writes=[("tv", "an")])
            kb.op("dve", lambda e: e.scalar_tensor_tensor(out=tv["k"][:L, :], in0=tv["a"][:L, :], scalar=-1.0, in1=P_["r_k_a"][:L, :], op0=ALU.add, op1=ALU.mult),
                  reads=[("tv", "a"), "r_k_a"], writes=[("tv", "k")])
            kb.op("dve", lambda e: e.scalar_tensor_tensor(out=tv["k"][:L, :], in0=tv["k"][:L, :], scalar=1.0, in1=kr, op0=ALU.add, op1=ALU.mult),
                  reads=[("tv", "k"), "xr"], writes=[("tv", "k")])
            kb.op("dve", lambda e: e.tensor_tensor(out=tv["a"][:L, :], in0=tv["k"][:L, :], in1=P_["r_r_k"][:L, :], op=ALU.mult),
                  reads=[("tv", "k"), "r_r_k", ("tv", "b")], writes=[("tv", "a")])
            kb.op("dve", lambda e: e.tensor_tensor(out=tv["a"][:L, :], in0=tv["a"][:L, :], in1=r_, op=ALU.mult), reads=[("tv", "a"), "xr"], writes=[("tv", "a")])
            kb.op("dve", lambda e: e.tensor_reduce(out=ssq[:L, 32:48], in_=hv(tv["a"]), axis=AX.X, op=ALU.add), reads=[("tv", "a")], writes=["ssq2"])
            for (dst, src, ksrc) in (("at", tv["an"][:L, :], ("tv", "an")), ("rt", r_, "xr"), ("bh", tv["b"][:L, :], ("tv", "b")),
                                     ("kh", tv["k"][:L, :], ("tv", "k")), ("cum", tv["lw"][:L, :], ("tv", "lw"))):
                for half in range(2):
                    p = self.rr("ptr", 2)
                    pt, kpt = self.ptr[p], "ptr%d" % p
                    for k4 in range(4):
                        kc = half * 4 + k4
                        kb.op("pe", lambda e, k4=k4, kc=kc, pt=pt, src=src: e.transpose(out=pt[:, k4 * 128:k4 * 128 + L], in_=src[:, kc * 128:(kc + 1) * 128],
                                                                                  identity=self.ident[:L, :L]), reads=[ksrc, "ident"], writes=[kpt])
                    self.evac(fm[dst][:, half * 4:half * 4 + 4, :L], pt[:, :].rearrange("p (k t) -> p k t", k=4)[:, :, :L], [kpt], [("fm", dst)])
            Lc = 1 if is_s else 64
            nch = L // Lc
            lwT = fm["cum"]
            kb.op("dve", lambda e: e.tensor_copy(out=fm["et"][:, :, :L], in_=lwT[:, :, :L]), reads=[("fm", "cum")], writes=[("fm", "et")])
            if Lc > 1:
                for hh in range(8):
                    for c in range(nch):
                        kb.op("dve", lambda e, hh=hh, c=c: e.tensor_tensor_scan(out=fm["cum"][:, hh, c * Lc:(c + 1) * Lc], data0=self.onesf[:, :Lc],
                                                                             data1=fm["et"][:, hh, c * Lc:(c + 1) * Lc], initial=0.0, op0=ALU.mult, op1=ALU.add),
                              reads=[("fm", "et"), "onesf"], writes=[("fm", "cum")])
            F = lambda n: fm[n][:, :, :L]
            C4 = lambda n: fm[n][:, :, :L].rearrange("p k (c t) -> p k c t", t=Lc)
            kb.op("dve", lambda e: e.tensor_tensor(out=F("et"), in0=F("cum"), in1=F("et"), op=ALU.subtract), reads=[("fm", "cum"), ("fm", "et")], writes=[("fm", "et")])
            kb.op("act", lambda e: e.activation(out=F("et"), in_=F("et"), func=AF.Exp), reads=[("fm", "et")], writes=[("fm", "et")])
            kb.op("dve", lambda e: e.tensor_tensor(out=F("at"), in0=F("at"), in1=F("et"), op=ALU.mult), reads=[("fm", "at"), ("fm", "et")], writes=[("fm", "at")])
            kb.op("act", lambda e: e.activation(out=F("et"), in_=F("cum"), func=AF.Exp), reads=[("fm", "cum"), ("fm", "at")], writes=[("fm", "et")])
            kb.op("dve", lambda e: e.tensor_tensor(out=F("rt"), in0=F("rt"), in1=F("et"), op=ALU.mult), reads=[("fm", "rt"), ("fm", "et")], writes=[("fm", "rt")])
            kb.op("pool", lambda e: e.tensor_copy(out=wc[:, :, :nch], in_=C4("et")[:, :, :, Lc - 1]), reads=[("fm", "et")], writes=["wc"])
            kb.op("dve", lambda e: e.tensor_tensor(out=C4("et"), in0=C4("cum")[:, :, :, Lc - 1:Lc].to_broadcast([128, 8, nch, Lc]), in1=C4("cum"), op=ALU.subtract),
                  reads=[("fm", "cum"), ("fm", "rt"), "wc"], writes=[("fm", "et")])
            kb.op("act", lambda e: e.activation(out=F("et"), in_=F("et"), func=AF.Exp), reads=[("fm", "et")], writes=[("fm", "et")])
            kb.op("dve", lambda e: e.tensor_tensor(out=F("bc"), in0=F("bh"), in1=F("et"), op=ALU.mult), reads=[("fm", "bh"), ("fm", "et")], writes=[("fm", "bc")])
            kb.op("dve", lambda e: e.tensor_tensor(out=F("kc"), in0=F("kh"), in1=F("et"), op=ALU.mult), reads=[("fm", "kh"), ("fm", "et")], writes=[("fm", "kc")])
            kb.op("act", lambda e: e.activation(out=F("et"), in_=F("cum"), func=AF.Exp, scale=-1.0), reads=[("fm", "cum"), ("fm", "bc"), ("fm", "kc")], writes=[("fm", "et")])
            kb.op("dve", lambda e: e.tensor_tensor(out=F("bh"), in0=F("bh"), in1=F("et"), op=ALU.mult), reads=[("fm", "bh"), ("fm", "et")], writes=[("fm", "bh")])
            kb.op("dve", lambda e: e.tensor_tensor(out=F("kh"), in0=F("kh"), in1=F("et"), op=ALU.mult), reads=[("fm", "kh"), ("fm", "et")], writes=[("fm", "kh")])

        def bank():
            q_ = self.rr("bank8", 8)
            return self.bank8[q_], self.bank8k[q_]

        def core(gi, col0, Lc, cidx):
            g = G_[gi]
            K_ = lambda n: ("g", gi, n)
            hs = slice(4 * gi, 4 * gi + 4)
            cs_ = slice(col0, col0 + Lc)
            for half in range(2):
                ps = slice(half * 64, half * 64 + 64)
                if half == 0:
                    kb.op("dve", lambda e, ps=ps, half=half: e.tensor_copy(
                        out=g["UBD"][ps, :, :, half * 64:half * 64 + Lc], in_=fmall[ps, hs, 0:4, cs_]),
                        reads=[("fm", n) for n in ("at", "rt", "bh", "kh")], writes=[K_("UBD")])
                else:
                    kb.op("act", lambda e, ps=ps, half=half: e.activation(
                        out=g["UBD"][ps, :, :, half * 64:half * 64 + Lc], in_=fmall[ps, hs, 0:4, cs_], func=AF.Copy),
                        reads=[("fm", n) for n in ("at", "rt", "bh", "kh")], writes=[K_("UBD")])
            yield
            for half in range(2):
                ps = slice(half * 64, half * 64 + 64)
                if half == 1:
                    kb.op("dve", lambda e, ps=ps, half=half: e.tensor_copy(
                        out=BK[ps, :, :, half * 64:half * 64 + Lc], in_=fmall[ps, hs, 4:6, cs_]),
                        reads=[("fm", "bc"), ("fm", "kc")], writes=["BK"])
                else:
                    kb.op("act", lambda e, ps=ps, half=half: e.activation(
                        out=BK[ps, :, :, half * 64:half * 64 + Lc], in_=fmall[ps, hs, 4:6, cs_], func=AF.Copy),
                        reads=[("fm", "bc"), ("fm", "kc")], writes=["BK"])
            for vi, dst in enumerate(("Btm", "Ktm")):
                pb_, kpb = bank()
                for u in range(4):
                    kb.op("pe", lambda e, u=u, vi=vi, pb_=pb_: e.matmul(pb_[:, u * 128:(u + 1) * 128], lhsT=BK[:, u, vi, :], rhs=identc[:, :], start=True, stop=True),
                          reads=["BK", "identc"], writes=[kpb])
                self.evac(g[dst][:, :, :], pb_[:, :].rearrange("p (u m) -> p u m", u=4), [kpb], [K_(dst)])
            yield
            for (vec, dst) in ((2, "MNb"), (3, "MNk")):
                for pr in range(2):
                    pb_, kpb = bank()
                    for u2 in range(2):
                        u = pr * 2 + u2
                        kb.op("pe", lambda e, u=u, u2=u2, vec=vec, pb_=pb_: e.matmul(pb_[:, u2 * 256:(u2 + 1) * 256], lhsT=g["UBD"][:, u, vec, :],
                                                                              rhs=g["UBD"][:, u, 0:2, :].rearrange("p a m -> p (a m)"), start=True, stop=True),
                              reads=[K_("UBD")], writes=[kpb])
                    kb.op("dve", lambda e, pr=pr, dst=dst, pb_=pb_: e.tensor_tensor(out=g[dst][:, pr * 2:pr * 2 + 2, :], in0=pb_[:, :].rearrange("p (u m) -> p u m", u=2),
                                                                           in1=mks[:, None, 0:256].to_broadcast([128, 2, 256]), op=ALU.mult),
                          reads=[kpb, "mks"], writes=[K_(dst)])
            pb_, kpb = bank()
            for u in range(4):
                kb.op("pe", lambda e, u=u, pb_=pb_: e.matmul(pb_[:, u * 128:(u + 1) * 128], lhsT=g["UBD"][:, u, 0, :], rhs=g["UBD"][:, u, 2, :], start=True, stop=True),
                      reads=[K_("UBD")], writes=[kpb])
            kb.op("dve", lambda e, pb_=pb_: e.tensor_tensor(out=g["Xt"][:, :, :], in0=pb_[:, :].rearrange("p (u m) -> p u m", u=4),
                                                       in1=mks[:, None, 256:384].to_broadcast([128, 4, 128]), op=ALU.mult),
                  reads=[kpb, "mks"], writes=[K_("Xt")])
            kb.op("act", lambda e: e.activation(out=g["X"][:, :, :], in_=g["MNb"][:, :, 0:128], func=AF.Copy), reads=[K_("MNb")], writes=[K_("X")])
            kb.op("dve", lambda e: e.tensor_tensor(out=g["P"][:, :, :], in0=g["MNb"][:, :, 0:128], in1=identc[:, None, :].to_broadcast([128, 4, 128]), op=ALU.add),
                  reads=[K_("MNb"), "identc"], writes=[K_("P")])
            yield
            nlev = 0
            while (1 << (nlev + 1)) < Lc:
                nlev += 1
            for lev in range(nlev if Lc > 1 else 0):
                lastlev = (lev == nlev - 1)
                pb1 = kp1 = None
                if not lastlev:
                    pb1, kp1 = bank()
                    for u in range(4):
                        kb.op("pe", lambda e, u=u, pb1=pb1: e.matmul(pb1[:, u * 128:(u + 1) * 128], lhsT=g["Xt"][:, u, :], rhs=g["X"][:, u, :], start=True, stop=True),
                              reads=[K_("X"), K_("Xt")], writes=[kp1])
                pb2, kp2 = bank()
                for u in range(4):
                    kb.op("pe", lambda e, u=u, pb2=pb2: e.matmul(pb2[:, u * 128:(u + 1) * 128], lhsT=g["X"][:, u, :], rhs=g["Xt"][:, u, :], start=True, stop=True),
                          reads=[K_("X"), K_("Xt")], writes=[kp2])
                if not lastlev:
                    self.evac(g["X"][:, :, :], pb1[:, :].rearrange("p (u m) -> p u m", u=4), [kp1], [K_("X")])
                self.evac(g["Xt"][:, :, :], pb2[:, :].rearrange("p (u m) -> p u m", u=4), [kp2], [K_("Xt")])
                pb3, kp3 = bank()
                for u in range(4):
                    kb.op("pe", lambda e, u=u, pb3=pb3: e.matmul(pb3[:, u * 128:(u + 1) * 128], lhsT=g["Xt"][:, u, :], rhs=g["P"][:, u, :], start=True, stop=True),
                          reads=[K_("Xt"), K_("P")], writes=[kp3])
                kb.op("dve", lambda e, pb3=pb3: e.tensor_tensor(out=g["P"][:, :, :], in0=pb3[:, :].rearrange("p (u m) -> p u m", u=4), in1=g["P"][:, :, :], op=ALU.add),
                      reads=[kp3, K_("P")], writes=[K_("P")])
                yield
            kS = [("ST", 4 * gi + u) for u in range(4)]
            kSb = [kSTb(4 * gi + u) for u in range(4)]
            kVb = [kVpb(4 * gi + u) for u in range(4)]
            pb_, kpb = bank()
            for u in range(4):
                hh = 4 * gi + u
                kb.op("pe", lambda e, u=u, hh=hh, pb_=pb_: e.matmul(pb_[:, u * 64:(u + 1) * 64], lhsT=g["UBD"][:, u, 0, :], rhs=STb[:, hh, :], start=True, stop=False),
                      reads=[K_("UBD"), kSb[u]], writes=[kpb])
                kb.op("pe", lambda e, u=u, hh=hh, pb_=pb_: e.matmul(pb_[:, u * 64:(u + 1) * 64], lhsT=g["MNk"][:, u, 0:128], rhs=Vpb[:, hh, :], start=False, stop=True),
                      reads=[K_("MNk"), kVb[u]], writes=[kpb])
            self.evac(g["RHS"][:, :, :], pb_[:, 0:256].rearrange("p (u m) -> p u m", u=4), [kpb], [K_("RHS")])
            pb_, kpb = bank()
            for u in range(4):
                kb.op("pe", lambda e, u=u, pb_=pb_: e.matmul(pb_[:, u * 64:(u + 1) * 64], lhsT=g["P"][:, u, :], rhs=g["RHS"][:, u, :], start=True, stop=True),
                      reads=[K_("P"), K_("RHS")], writes=[kpb])
            self.evac(g["U"][:, :, :], pb_[:, 0:256].rearrange("p (u m) -> p u m", u=4), [kpb], [K_("U")])
            yield
            pb_, kpb = bank()
            for u in range(4):
                hh = 4 * gi + u
                kb.op("pe", lambda e, u=u, hh=hh, pb_=pb_: e.matmul(pb_[:, u * 64:(u + 1) * 64], lhsT=g["UBD"][:, u, 1, :], rhs=STb[:, hh, :], start=True, stop=False),
                      reads=[K_("UBD"), kSb[u]], writes=[kpb])
                kb.op("pe", lambda e, u=u, pb_=pb_: e.matmul(pb_[:, u * 64:(u + 1) * 64], lhsT=g["MNb"][:, u, 128:256], rhs=g["U"][:, u, :], start=False, stop=False),
                      reads=[K_("MNb"), K_("U")], writes=[kpb])
                kb.op("pe", lambda e, u=u, hh=hh, pb_=pb_: e.matmul(pb_[:, u * 64:(u + 1) * 64], lhsT=g["MNk"][:, u, 128:256], rhs=Vpb[:, hh, :], start=False, stop=True),
                      reads=[K_("MNk"), kVb[u]], writes=[kpb])
            self.evac(Ych[:, hs, :], pb_[:, 0:256].rearrange("p (u m) -> p u m", u=4), [kpb], [("Ych", 4 * gi + u) for u in range(4)])
            pb_, kpb = bank()
            for u in range(4):
                hh = 4 * gi + u
                kb.op("pe", lambda e, u=u, pb_=pb_: e.matmul(pb_[:, u * 64:(u + 1) * 64], lhsT=g["Btm"][:, u, :], rhs=g["U"][:, u, :], start=True, stop=False),
                      reads=[K_("Btm"), K_("U")], writes=[kpb])
                kb.op("pe", lambda e, u=u, hh=hh, pb_=pb_: e.matmul(pb_[:, u * 64:(u + 1) * 64], lhsT=g["Ktm"][:, u, :], rhs=Vpb[:, hh, :], start=False, stop=True),
                      reads=[K_("Ktm"), kVb[u]], writes=[kpb])
            kb.op("dve", lambda e: e.tensor_tensor(out=ST[:, hs, :], in0=ST[:, hs, :], in1=wc[:, hs, cidx:cidx + 1].to_broadcast([128, 4, 64]), op=ALU.mult),
                  reads=kS + ["wc"], writes=kS)
            kb.op("dve", lambda e, pb_=pb_: e.tensor_tensor(out=ST[:, hs, :], in0=ST[:, hs, :], in1=pb_[:, 0:256].rearrange("p (u m) -> p u m", u=4), op=ALU.add),
                  reads=kS + [kpb], writes=kS)
            if self.CORE_BF16:
                kb.op("act", lambda e: e.activation(out=STb[:, hs, :], in_=ST[:, hs, :], func=AF.Copy), reads=kS, writes=kSb)
            yield

        def run_core(col0, Lc, cidx):
            alive = [core(gi, col0, Lc, cidx) for gi in range(NG)]
            while alive:
                for g_ in list(alive):
                    try:
                        next(g_)
                    except StopIteration:
                        alive.remove(g_)

        def post(t0, L, ti, bi):
            kb.dma("sp", ytm[:L, :], S["ys"][t0:t0 + L, :], reads=[("ys", ti)], writes=["rp"])
            kb.dma("sp", zrt[:L, :], S["z"][t0:t0 + L, O_ZR:O_ZR + D], reads=z_all(ti), writes=["rp"])
            kb.op("act", lambda e: e.activation(out=zrt[:L, :], in_=zrt[:L, :], func=AF.Silu), reads=["rp"], writes=["rp"])
            hv = lambda x: x[:L, :].rearrange("t (h j) -> t h j", h=16)
            kb.op("dve", lambda e: e.tensor_reduce(out=yst[:L, 0:16], in_=hv(ytm), axis=AX.X, op=ALU.add), reads=["rp"], writes=["yst"])
            kb.op("dve", lambda e: e.tensor_scalar(out=yst[:L, 0:16], in0=yst[:L, 0:16], scalar1=-1.0 / 64.0, scalar2=None, op0=ALU.mult), reads=["yst"], writes=["yst"])
            kb.op("dve", lambda e: e.tensor_tensor(out=hv(ytm), in0=hv(ytm), in1=yst[:L, 0:16].unsqueeze(2).to_broadcast([L, 16, 64]), op=ALU.add),
                  reads=["rp", "yst"], writes=["rp"])
            kb.op("dve", lambda e: e.tensor_tensor(out=tv["lw"][:L, :], in0=ytm[:L, :], in1=ytm[:L, :], op=ALU.mult), reads=["rp"], writes=[("tv", "lw")])
            kb.op("dve", lambda e: e.tensor_reduce(out=yst[:L, 16:32], in_=hv(tv["lw"]), axis=AX.X, op=ALU.add), reads=[("tv", "lw")], writes=["yst"])
            kb.op("act", lambda e: e.activation(out=yst[:L, 32:48], in_=yst[:L, 16:32], func=AF.Sqrt, scale=1.0 / 64.0, bias=e12[:L, 0:1]), reads=["yst", "e12"], writes=["yst"])
            kb.op("dve", lambda e: e.reciprocal(out=yst[:L, 48:64], in_=yst[:L, 32:48]), reads=["yst"], writes=["yst"])
            kb.op("dve", lambda e: e.tensor_tensor(out=hv(ytm), in0=hv(ytm), in1=yst[:L, 48:64].unsqueeze(2).to_broadcast([L, 16, 64]), op=ALU.mult),
                  reads=["rp", "yst"], writes=["rp"])
            kb.op("dve", lambda e: e.tensor_tensor(out=ytm[:L, :], in0=ytm[:L, :], in1=P_["r_ln_g"][:L, :], op=ALU.mult), reads=["rp", "r_ln_g"], writes=["rp"])
            kb.op("dve", lambda e: e.tensor_tensor(out=ytm[:L, :], in0=ytm[:L, :], in1=P_["r_ln_b"][:L, :], op=ALU.add), reads=["rp", "r_ln_b"], writes=["rp"])
            kb.op("dve", lambda e: e.tensor_tensor(out=hv(tv["lw"]), in0=xr[:L, 2 * D:3 * D].rearrange("t (h j) -> t h j", h=16),
                                                   in1=ssq[:L, 32:48].unsqueeze(2).to_broadcast([L, 16, 64]), op=ALU.mult), reads=["xr", "ssq2"], writes=[("tv", "lw")])
            kb.op("dve", lambda e: e.tensor_tensor(out=ytm[:L, :], in0=ytm[:L, :], in1=tv["lw"][:L, :], op=ALU.add), reads=["rp", ("tv", "lw")], writes=["rp"])
            kb.op("dve", lambda e: e.tensor_tensor(out=ytm[:L, :], in0=ytm[:L, :], in1=zrt[:L, :], op=ALU.mult), reads=["rp", "rp"], writes=["rp"])
            aj = self.rr("raob", 2)
            for half in range(2):
                p = self.rr("ptr", 2)
                pt, kpt = self.ptr[p], "ptr%d" % p
                for k4 in range(4):
                    kc = half * 4 + k4
                    kb.op("pe", lambda e, k4=k4, kc=kc, pt=pt: e.transpose(out=pt[:, k4 * 128:k4 * 128 + L], in_=ytm[:L, kc * 128:(kc + 1) * 128],
                                                                       identity=self.ident[:L, :L]), reads=["rp", "ident"], writes=[kpt])
                self.evac(aob[aj][:, half * 4:half * 4 + 4, :L], pt[:, :].rearrange("p (k t) -> p k t", k=4)[:, :, :L], [kpt], ["raob%d" % aj])
            kb.dma("pool", S["act_r"][:, t0:t0 + L].rearrange("(k p) t -> p k t", p=128), aob[aj][:, :, :L], reads=["raob%d" % aj], writes=[("act", "r", bi)])

        natx8 = sbp("rnatx8", (128, 8, 128))
        kb.op("pool", lambda e: e.memset(natx8[:], 0.0), writes=["natx8"])

        def bd_transpose(src, ksrc, dst, kdst, also_b=None):
            for half in range(2):
                ps = slice(half * 64, half * 64 + 64)
                kb.op("pool" if half else "dve", lambda e, ps=ps: e.tensor_copy(out=natx8[ps, :, ps], in_=src[ps, :, :]), reads=ksrc, writes=["natx8"])
            for q4 in range(2):
                pb_, kpb = bank()
                for u in range(4):
                    kb.op("pe", lambda e, u=u, q4=q4, pb_=pb_: e.transpose(out=pb_[:, u * 128:(u + 1) * 128], in_=natx8[:, q4 * 4 + u, :], identity=self.ident[:, :]),
                          reads=["natx8", "ident"], writes=[kpb])
                for half in range(2):
                    ps = slice(half * 64, half * 64 + 64)
                    kb.op("dve" if half else "act", (lambda e, ps=ps, q4=q4, pb_=pb_: e.tensor_copy(out=dst[ps, q4 * 4:q4 * 4 + 4, :], in_=pb_[:, :].rearrange("p (u m) -> p u m", u=4)[ps, :, ps]))
                          if half else (lambda e, ps=ps, q4=q4, pb_=pb_: e.activation(out=dst[ps, q4 * 4:q4 * 4 + 4, :], in_=pb_[:, :].rearrange("p (u m) -> p u m", u=4)[ps, :, ps], func=AF.Copy)),
                          reads=[kpb], writes=kdst)

        def state_out(dst):
            bd_transpose(ST, [("ST", h) for h in range(8)], nato, ["nato"])
            kb.dma("pool", dst.rearrange("(hh hl) i j -> (hl i) hh j", hl=2), nato[:, :, :], reads=["nato"], writes=[], is_output=True)

        def state_in(src):
            kb.dma("sp", nat[:, :, :], src.rearrange("(hh hl) i j -> (hl i) hh j", hl=2), writes=["nat"])
            bd_transpose(nat, ["nat"], ST, [("ST", h) for h in range(8)])
            if self.CORE_BF16:
                kb.op("pool", lambda e: e.tensor_copy(out=STb[:, :, :], in_=ST[:, :, :]), reads=[("ST", h) for h in range(8)], writes=[("STb", h) for h in range(8)])

        blk_of = lambda t: next(i for i, (b0, bn) in enumerate(self.blocks) if b0 <= t < b0 + bn)
        for ti in range(T // 128):
            t0 = ti * 128
            prep(t0, 128, ti, False)
            for c in range(2):
                for half in range(2):
                    kb.dma("sp", Vp[half * 64:half * 64 + 64, :, :],
                           S["vs"][t0 + c * 64:t0 + c * 64 + 64, :].rearrange("t (hh hl i) -> hl t hh i", hl=2, i=64)[half],
                           reads=[("vs", ti)], writes=[("Vp", h) for h in range(8)])
                vp_ready()
                run_core(c * 64, 64, c)
                for half in range(2):
                    kb.dma("pool", S["ys"][t0 + c * 64:t0 + c * 64 + 64, :].rearrange("t (hh hl i) -> hl t hh i", hl=2, i=64)[half],
                           Ych[half * 64:half * 64 + 64, :, :], reads=[("Ych", h) for h in range(8)], writes=[("ys", ti)])
            post(t0, 128, ti, blk_of(t0))
        state_out(O["wkv_prompt"][l])
        if NS:
            ti = T // 128
            prep(T, NS, ti, True)
            zero_units()
            kb.op("pool", lambda e: e.memset(Vp[:], 0.0), writes=[("Vp", h) for h in range(8)])
            for b in range(NS):
                state_in(I["state_rwkv_wkv"][l, b])
                for half in range(2):
                    kb.dma("sp", Vp[half * 64:half * 64 + 1, :, :],
                           S["vs"][T + b:T + b + 1, :].rearrange("t (hh hl i) -> hl t hh i", hl=2, i=64)[half],
                           reads=[("vs", ti)], writes=[("Vp", h) for h in range(8)])
                vp_ready()
                run_core(b, 1, b)
                for half in range(2):
                    kb.dma("pool", S["ys"][T + b:T + b + 1, :].rearrange("t (hh hl i) -> hl t hh i", hl=2, i=64)[half],
                           Ych[half * 64:half * 64 + 1, :, :], reads=[("Ych", h) for h in range(8)], writes=[("ys", ti)])
                state_out(O["wkv_sample"][l, b])
            post(T, NS, ti, blk_of(T))
        self.end_phase()

    def sin_to(self, tkey, out, ang, tf, ti, tm, key_out, key_ang, shift=0.0):
        kb = self.kb
        TWO_PI = 2.0 * math.pi
        kt = ("sin_tmp", tkey)
        kb.op("dve", lambda e: e.tensor_scalar(out=tm, in0=ang, scalar1=shift, scalar2=None, op0=ALU.add),
              reads=[key_ang], writes=[(kt, "m")])
        kb.op("dve", lambda e: e.tensor_scalar(out=tf, in0=tm, scalar1=1.0 / TWO_PI, scalar2=None, op0=ALU.mult),
              reads=[(kt, "m")], writes=[(kt, "f")])
        kb.op("dve", lambda e: e.tensor_copy(out=ti, in_=tf), reads=[(kt, "f")], writes=[(kt, "i")])
        kb.op("dve", lambda e: e.tensor_copy(out=tf, in_=ti), reads=[(kt, "i")], writes=[(kt, "f")])
        kb.op("dve", lambda e: e.scalar_tensor_tensor(out=tm, in0=tf, scalar=-TWO_PI, in1=tm, op0=ALU.mult, op1=ALU.add),
              reads=[(kt, "f"), (kt, "m")], writes=[(kt, "m")])
        kb.op("dve", lambda e: e.tensor_scalar(out=tf, in0=tm, scalar1=math.pi, scalar2=-TWO_PI, op0=ALU.is_gt, op1=ALU.mult),
              reads=[(kt, "m")], writes=[(kt, "f")])
        kb.op("dve", lambda e: e.tensor_tensor(out=tm, in0=tm, in1=tf, op=ALU.add), reads=[(kt, "f"), (kt, "m")],
              writes=[(kt, "m")])
        kb.op("dve", lambda e: e.tensor_scalar(out=tf, in0=tm, scalar1=-math.pi, scalar2=TWO_PI, op0=ALU.is_lt, op1=ALU.mult),
              reads=[(kt, "m")], writes=[(kt, "f")])
        kb.op("dve", lambda e: e.tensor_tensor(out=tm, in0=tm, in1=tf, op=ALU.add), reads=[(kt, "f"), (kt, "m")],
              writes=[(kt, "m")])
        kb.op("dve", lambda e: e.tensor_scalar(out=tm, in0=tm, scalar1=-3.1415925, scalar2=3.1415925, op0=ALU.max, op1=ALU.min),
              reads=[(kt, "m")], writes=[(kt, "m")])
        kb.op("act", lambda e: e.activation(out=out, in_=tm, func=AF.Sin), reads=[(kt, "m")], writes=[key_out])

    def s5_phase(self, l):
        kb = self.kb
        I, S, O = self.I, self.S, self.O
        T, NS = self.T, self.NS
        self.begin_phase()
        sbp = self.sbp
        PT = sbp("PT", (128, 6, 32))
        Bz = [sbp("Bz%d" % i, (128, 32, 128), BF16) for i in range(2)]
        Cz = [sbp("Cz%d" % i, (128, 32, 128), BF16) for i in range(2)]
        cosT = sbp("cosT", (128, 32, 256)); sinT = sbp("sinT", (128, 32, 256))
        car = [sbp("car%d" % i, (128, 32)) for i in range(2)]
        ctmp = sbp("ctmp", (128, 4))
        dcol = sbp("dcol", (128, 8)); gbcol = sbp("gbcol", (128, 8))
        wg = sbp("wglu", (128, KC, D), BF16)
        s0T = [sbp("s0T%d" % i, (128, 32, 16)) for i in range(2)]
        snw = [sbp("snw%d" % i, (128, 32, 16)) for i in range(2)]
        sst = sbp("sst", (32, 512))
        outer_pes = self.pes
        self.pes = ExitStack()
        nat = sbp("nat", (32, 14, 128))
        nati = sbp("nati", (32, 128), I32)
        BR = sbp("BR", (128, 32, 16)); BI = sbp("BI", (128, 32, 16))
        bbr = sbp("bbr", (128, 32, 16)); bbi = sbp("bbi", (128, 32, 16))
        xin = sbp("xin", (128, 512))
        btmp = xin[:, :].rearrange("p (s c) -> p s c", c=16)
        Cn = [sbp("Cn%d" % i, (128, 8, 64)) for i in range(2)]
        ang = sbp("ang", (128, 2, 256)); angf = sbp("angf", (128, 2, 256)); angm = sbp("angm", (128, 2, 256))
        angi = sbp("angi", (128, 2, 256), I32)
        m3 = sbp("m3", (128, 4, 2)); m4 = sbp("m4", (128, 4, 8)); iota1 = sbp("iota1", (128, 256))
        wstg = [sbp("wstg%d" % i, (128, KC, 128)) for i in range(1)]

        kb.dma("sp", m3[:], I["mask3"], writes=["m3"])
        kb.dma("sp", m4[:], I["mask4"], writes=["m4"])
        kb.dma("sp", iota1[:], I["iota1"], writes=["iota1"])
        kb.dma("sp", nat[:, 0, :], I["s_lam_re"][l].rearrange("(s g) p -> s (g p)", g=2), writes=["nat0"])
        kb.dma("sp", nat[:, 1, :], I["s_lam_im"][l].rearrange("(s g) p -> s (g p)", g=2), writes=["nat1"])
        kb.dma("sp", nat[:, 2, 0:2], I["s_log_dt"][l].rearrange("(s g) -> s g", g=2), writes=["nat2"])
        kb.dma("sp", dcol[:], I["s_d"][l].rearrange("(k p) -> p k", p=128), writes=["dcol"], allow_slow_non_contiguous=True)
        kb.dma("sp", gbcol[:], I["s_glu_b"][l].rearrange("(k p) -> p k", p=128), writes=["gbcol"], allow_slow_non_contiguous=True)
        kb.dma("sp", BR[:], I["s_b_re"][l].rearrange("(s g) p c -> (g p) s c", g=2), writes=["BR"])
        kb.dma("sp", BI[:], I["s_b_im"][l].rearrange("(s g) p c -> (g p) s c", g=2), writes=["BI"])
        kb.dma("sp", Cn[0][:], I["s_c_re"][l].rearrange("(o g) c p -> (g c) o p", g=8), writes=["Cn0"])
        kb.dma("sp", Cn[1][:], I["s_c_im"][l].rearrange("(o g) c p -> (g c) o p", g=8), writes=["Cn1"])
        for h in range(8):
            kb.dma("sp", wstg[0][:], I["s_glu_w"][l][:, h * 128:(h + 1) * 128].rearrange("(k p) c -> p k c", p=128),
                   writes=["wstg"])
            kb.op("dve", lambda e, h=h: e.tensor_copy(out=wg[:, :, h * 128:(h + 1) * 128], in_=wstg[0][:]),
                  reads=["wstg"], writes=["wg"])
        N = lambda i: nat[:, i, :]
        kb.op("act", lambda e: e.activation(out=nat[:, 2, 2:4], in_=nat[:, 2, 0:2], func=AF.Exp), reads=["nat2"], writes=["nat2"])
        kb.op("dve", lambda e: e.tensor_copy(out=nat[:, 3, :].rearrange("s (g p) -> s g p", g=2),
                                             in_=nat[:, 2, 2:4].unsqueeze(2).to_broadcast([32, 2, 64])),
              reads=["nat2"], writes=["nat3"])
        kb.op("dve", lambda e: e.tensor_scalar(out=N(0), in0=N(0), scalar1=-1e-4, scalar2=None, op0=ALU.min),
              reads=["nat0"], writes=["nat0"])
        kb.op("dve", lambda e: e.tensor_tensor(out=N(4), in0=N(0), in1=N(3), op=ALU.mult), reads=["nat0", "nat3"], writes=["nat4"])
        kb.op("act", lambda e: e.activation(out=N(4), in_=N(4), func=AF.Exp), reads=["nat4"], writes=["nat4"])
        kb.op("dve", lambda e: e.tensor_tensor(out=N(5), in0=N(1), in1=N(3), op=ALU.mult), reads=["nat1", "nat3"], writes=["nat5"])
        self.sin_to("nat", N(6), N(5), N(12), nati[:, :], N(13), "nat6", "nat5")
        self.sin_to("nat", N(7), N(5), N(12), nati[:, :], N(13), "nat7", "nat5", shift=math.pi / 2)
        kb.op("dve", lambda e: e.tensor_tensor(out=N(8), in0=N(4), in1=N(7), op=ALU.mult), reads=["nat4", "nat7"], writes=["nat8"])
        kb.op("dve", lambda e: e.tensor_tensor(out=N(9), in0=N(4), in1=N(6), op=ALU.mult), reads=["nat4", "nat6"], writes=["nat9"])
        kb.op("dve", lambda e: e.tensor_tensor(out=N(12), in0=N(0), in1=N(0), op=ALU.mult), reads=["nat0"], writes=["nat12"])
        kb.op("dve", lambda e: e.tensor_tensor(out=N(13), in0=N(1), in1=N(1), op=ALU.mult), reads=["nat1"], writes=["nat13"])
        kb.op("dve", lambda e: e.tensor_tensor(out=N(12), in0=N(12), in1=N(13), op=ALU.add), reads=["nat12", "nat13"], writes=["nat12"])
        kb.op("dve", lambda e: e.reciprocal(out=N(12), in_=N(12)), reads=["nat12"], writes=["nat12"])
        kb.op("dve", lambda e: e.tensor_scalar(out=N(13), in0=N(8), scalar1=-1.0, scalar2=None, op0=ALU.add), reads=["nat8"], writes=["nat13"])
        kb.op("dve", lambda e: e.tensor_tensor(out=N(10), in0=N(13), in1=N(0), op=ALU.mult), reads=["nat13", "nat0"], writes=["nat10"])
        kb.op("dve", lambda e: e.tensor_tensor(out=N(11), in0=N(9), in1=N(1), op=ALU.mult), reads=["nat9", "nat1"], writes=["nat11"])
        kb.op("dve", lambda e: e.tensor_tensor(out=N(10), in0=N(10), in1=N(11), op=ALU.add), reads=["nat10", "nat11"], writes=["nat10"])
        kb.op("dve", lambda e: e.tensor_tensor(out=N(10), in0=N(10), in1=N(12), op=ALU.mult), reads=["nat10", "nat12"], writes=["nat10"])
        kb.op("dve", lambda e: e.tensor_tensor(out=N(11), in0=N(9), in1=N(0), op=ALU.mult), reads=["nat9", "nat0"], writes=["nat11"])
        kb.op("dve", lambda e: e.tensor_tensor(out=N(13), in0=N(13), in1=N(1), op=ALU.mult), reads=["nat13", "nat1"], writes=["nat13"])
        kb.op("dve", lambda e: e.tensor_tensor(out=N(11), in0=N(11), in1=N(13), op=ALU.subtract), reads=["nat11", "nat13"], writes=["nat11"])
        kb.op("dve", lambda e: e.tensor_tensor(out=N(11), in0=N(11), in1=N(12), op=ALU.mult), reads=["nat11", "nat12"], writes=["nat11"])
        p = self.rr("ptr", 2)
        pt, kpt = self.ptr[p], "ptr%d" % p
        for si, ni in enumerate((4, 5, 8, 9, 10, 11)):
            kb.op("pe", lambda e, si=si, ni=ni, pt=pt: e.transpose(out=pt[:, si * 32:(si + 1) * 32], in_=nat[:, ni, :],
                                                                  identity=self.ident[:32, :32]),
                  reads=["nat%d" % ni, "ident"], writes=[kpt])
        kb.op("dve", lambda e, pt=pt: e.tensor_copy(out=PT[:, :, :], in_=pt[:, 0:192].rearrange("p (s c) -> p s c", s=6)),
              reads=[kpt], writes=["PT"])
        bc = lambda si: PT[:, si, :].unsqueeze(2).to_broadcast([128, 32, 16])
        kb.op("dve", lambda e: e.tensor_tensor(out=bbr[:], in0=BR[:], in1=bc(4), op=ALU.mult), reads=["BR", "PT"], writes=["bbr"])
        kb.op("dve", lambda e: e.tensor_tensor(out=btmp, in0=BI[:], in1=bc(5), op=ALU.mult), reads=["BI", "PT"], writes=["xin"])
        kb.op("dve", lambda e: e.tensor_tensor(out=bbr[:], in0=bbr[:], in1=btmp, op=ALU.subtract), reads=["bbr", "xin"], writes=["bbr"])
        kb.op("dve", lambda e: e.tensor_tensor(out=bbi[:], in0=BI[:], in1=bc(4), op=ALU.mult), reads=["BI", "PT"], writes=["bbi"])
        kb.op("dve", lambda e: e.tensor_tensor(out=btmp, in0=BR[:], in1=bc(5), op=ALU.mult), reads=["BR", "PT", "bbr"], writes=["xin"])
        kb.op("dve", lambda e: e.tensor_tensor(out=bbi[:], in0=bbi[:], in1=btmp, op=ALU.add), reads=["bbi", "xin"], writes=["bbi"])
        for ri, (bb, kbb) in enumerate(((bbr, "bbr"), (bbi, "bbi"))):
            for oc in range(8):
                kb.op("dve", lambda e, bb=bb, oc=oc: e.tensor_tensor(
                    out=xin[:, :].rearrange("p (q g c) -> p q g c", q=4, g=8),
                    in0=bb[:, oc * 4:oc * 4 + 4, None, :].to_broadcast([128, 4, 8, 16]),
                    in1=m4[:, :, :, None].to_broadcast([128, 4, 8, 16]), op=ALU.mult),
                    reads=[kbb, "m4"], writes=["xin"])
                p = self.rr("ptr", 2)
                pt, kpt = self.ptr[p], "ptr%d" % p
                for q in range(4):
                    kb.op("pe", lambda e, q=q, pt=pt: e.transpose(out=pt[:, q * 128:(q + 1) * 128], in_=xin[:, q * 128:(q + 1) * 128],
                                                                  identity=self.ident[:, :]),
                          reads=["xin", "ident"], writes=[kpt])
                kb.op("act", lambda e, pt=pt, oc=oc, ri=ri: e.activation(
                    out=Bz[ri][:, oc * 4:oc * 4 + 4, :], in_=pt[:, :].rearrange("p (q m) -> p q m", q=4), func=AF.Copy),
                    reads=[kpt], writes=[("Bz", ri)])
        for ri in range(2):
            for oc in range(8):
                kb.op("dve", lambda e, ri=ri, oc=oc: e.tensor_tensor(
                    out=xin[:, :].rearrange("p (q g s) -> p q g s", q=4, g=2),
                    in0=Cn[ri][:, oc, None, None, :].to_broadcast([128, 4, 2, 64]),
                    in1=m3[:, :, :, None].to_broadcast([128, 4, 2, 64]), op=ALU.mult),
                    reads=["Cn%d" % ri, "m3"], writes=["xin"])
                p = self.rr("ptr", 2)
                pt, kpt = self.ptr[p], "ptr%d" % p
                for q in range(4):
                    kb.op("pe", lambda e, q=q, pt=pt: e.transpose(out=pt[:, q * 128:(q + 1) * 128], in_=xin[:, q * 128:(q + 1) * 128],
                                                                  identity=self.ident[:, :]),
                          reads=["xin", "ident"], writes=[kpt])
                kb.op("act", lambda e, pt=pt, oc=oc, ri=ri: e.activation(
                    out=Cz[ri][:, oc * 4:oc * 4 + 4, :], in_=pt[:, :].rearrange("p (q m) -> p q m", q=4), func=AF.Copy,
                    scale=(1.0 if ri == 0 else -1.0)),
                    reads=[kpt], writes=[("Cz", ri)])
        for g in range(16):
            for s8 in range(2):
                sc = g * 2 + s8
                kb.op("dve", lambda e, s8=s8, sc=sc: e.tensor_scalar(out=ang[:, s8, :], in0=iota1[:, :], scalar1=PT[:, 1, sc:sc + 1],
                                                                     scalar2=None, op0=ALU.mult),
                      reads=["iota1", "PT"], writes=["ang"])
            self.sin_to("ang", sinT[:, g * 2:(g + 1) * 2, :], ang[:], angf[:], angi[:], angm[:], ("sinT", g), "ang")
            self.sin_to("ang", cosT[:, g * 2:(g + 1) * 2, :], ang[:], angf[:], angi[:], angm[:], ("cosT", g), "ang",
                        shift=math.pi / 2)
        tabs = [("sinT", g) for g in range(16)] + [("cosT", g) for g in range(16)]
        kb.op("dve", lambda e: e.memset(car[0][:], 0.0), writes=[("car", 0, sc_) for sc_ in range(32)])
        kb.op("dve", lambda e: e.memset(car[1][:], 0.0), writes=[("car", 1, sc_) for sc_ in range(32)])

        if NS:
            for ri, nm in enumerate(("state_s5_re", "state_s5_im")):
                p = self.rr("ptr", 2)
                pt, kpt = self.ptr[p], "ptr%d" % p
                for qq in range(8):
                    kb.dma("sp", sst[:NS, :], I[nm][l].rearrange("b g p -> b (g p)")[:, qq * 512:(qq + 1) * 512], writes=["sst"])
                    for s8 in range(4):
                        sc = qq * 4 + s8
                        kb.op("pe", lambda e, sc=sc, s8=s8, pt=pt: e.transpose(out=pt[:, sc * NS:(sc + 1) * NS], in_=sst[:NS, s8 * 128:(s8 + 1) * 128],
                                                                      identity=self.ident[:NS, :NS]),
                              reads=["sst", "ident"], writes=[kpt])
                kb.op("dve", lambda e, pt=pt, ri=ri: e.tensor_copy(out=s0T[ri][:, :, :NS],
                                                                  in_=pt[:, :32 * NS].rearrange("p (s b) -> p s b", s=32)),
                      reads=[kpt], writes=[("s0T", ri)])

        kb.barrier()
        self.pes.close()
        self.pes = outer_pes
        uf = [sbp("uf%d" % i, (128, 256)) for i in range(2)]
        ub = [sbp("ub%d" % i, (128, 256), BF16) for i in range(2)]
        WS = []
        for w_ in range(2):
            WS.append(([sbp("tq%d_%d" % (w_, i), (128, 256)) for i in range(4)],
                       [sbp("bh%d_%d" % (w_, i), (128, 256)) for i in range(2)],
                       [sbp("sh%d_%d" % (w_, i), (128, 256)) for i in range(2)],
                       [sbp("sbf%d_%d" % (w_, i), (128, 256), BF16) for i in range(2)]))
        ysg = sbp("ysg", (128, KC, 256)); ysb = sbp("ysb", (128, KC, 256), BF16)
        yt = [sbp("yt%d" % i, (128, 256)) for i in range(2)]
        zst = [sbp("zst%d" % i, (128, 256)) for i in range(2)]
        aout = [sbp("aout%d" % i, (128, 256), BF16) for i in range(2)]
        subblocks = []
        for bi, (t0b, nb) in enumerate(self.blocks):
            for t0 in range(t0b, t0b + nb, 256):
                subblocks.append((bi, t0, min(256, t0b + nb - t0)))
        for (bi, t0, n) in subblocks:
            is_s = (t0 >= T)
            nsub = n // 256
            for oc in range(8):
                j = self.rr("uf", 2)
                kuf, kub = "uf%d" % j, "ub%d" % j
                row = O_U + oc * 128
                kb.dma("sp", uf[j][:, :n], S["zT"][row:row + 128, t0:t0 + n], reads=[("zT", row, bi)], writes=[kuf])
                kb.op("pool", lambda e, j=j: e.tensor_copy(out=ub[j][:, :n], in_=uf[j][:, :n]), reads=[kuf], writes=[kub])
                py = self.rr("ptr", 2)
                pys, kpys = self.ptr[py], "ptr%d" % py
                def sc_gen(q, W):
                    tq, bh, sh, sbf = W
                    wk = lambda n_: (n_, id(W))
                    sc = oc * 4 + q
                    pa = self.rr("pp", 4); pb = self.rr("pp", 4)
                    A, B = self.pp[pa], self.pp[pb]
                    kA, kB = "pp%d" % pa, "pp%d" % pb
                    kb.op("pe", lambda e, A=A, sc=sc, j=j: e.matmul(A[:, :n], lhsT=Bz[0][:, sc, :], rhs=ub[j][:, :n], start=True, stop=True),
                          reads=[("Bz", 0), kub], writes=[kA])
                    kb.op("pe", lambda e, B=B, sc=sc, j=j: e.matmul(B[:, :n], lhsT=Bz[1][:, sc, :], rhs=ub[j][:, :n], start=True, stop=True),
                          reads=[("Bz", 1), kub], writes=[kB])
                    yield
                    if not is_s:
                        v3 = lambda x: x[:, :n].rearrange("p (m j) -> p m j", j=256)
                        cosv = cosT[:, sc, None, :].to_broadcast([128, nsub, 256])
                        sinv = sinT[:, sc, None, :].to_broadcast([128, nsub, 256])
                        kb.op("dve", lambda e, A=A, cosv=cosv: e.tensor_tensor(out=v3(tq[0]), in0=v3(A), in1=cosv, op=ALU.mult),
                              reads=[kA] + tabs, writes=[wk("tq0")])
                        kb.op("dve", lambda e, B=B, sinv=sinv: e.tensor_tensor(out=v3(tq[1]), in0=v3(B), in1=sinv, op=ALU.mult),
                              reads=[kB] + tabs, writes=[wk("tq1")])
                        kb.op("dve", lambda e, B=B, cosv=cosv: e.tensor_tensor(out=v3(tq[2]), in0=v3(B), in1=cosv, op=ALU.mult),
                              reads=[kB] + tabs, writes=[wk("tq2")])
                        kb.op("dve", lambda e, A=A, sinv=sinv: e.tensor_tensor(out=v3(tq[3]), in0=v3(A), in1=sinv, op=ALU.mult),
                              reads=[kA] + tabs, writes=[wk("tq3")])
                        kb.op("pool", lambda e: e.tensor_tensor(out=bh[0][:, :n], in0=tq[0][:, :n], in1=tq[1][:, :n], op=ALU.add),
                              reads=[wk("tq0"), wk("tq1")], writes=[wk("bh0")])
                        kb.op("pool", lambda e: e.tensor_tensor(out=bh[1][:, :n], in0=tq[2][:, :n], in1=tq[3][:, :n], op=ALU.subtract),
                              reads=[wk("tq2"), wk("tq3")], writes=[wk("bh1")])
                        yield
                        rho = PT[:, 0, sc:sc + 1].to_broadcast([128, 256])
                        c128 = cosT[:, sc, 255:256]
                        s128 = sinT[:, sc, 255:256]
                        for m in range(nsub):
                            sl = slice(m * 256, (m + 1) * 256)
                            for ri in range(2):
                                kb.op("dve", lambda e, ri=ri, sl=sl, sc=sc, rho=rho: e.tensor_tensor_scan(
                                    out=sh[ri][:, sl], data0=rho, data1=bh[ri][:, sl], initial=car[ri][:, sc:sc + 1],
                                    op0=ALU.mult, op1=ALU.add), reads=[wk("bh%d" % ri), ("car", ri, sc), "PT"], writes=[wk("sh%d" % ri)])
                            lr = sh[0][:, m * 256 + 255:m * 256 + 256]
                            li = sh[1][:, m * 256 + 255:m * 256 + 256]
                            kb.op("dve", lambda e, li=li, s128=s128: e.tensor_scalar(out=ctmp[:, 2 * (q % 2):2 * (q % 2) + 1], in0=li, scalar1=s128, scalar2=None, op0=ALU.mult),
                                  reads=[wk("sh1")] + tabs, writes=[wk("ctmp")])
                            kb.op("dve", lambda e, li=li, c128=c128: e.tensor_scalar(out=ctmp[:, 2 * (q % 2) + 1:2 * (q % 2) + 2], in0=li, scalar1=c128, scalar2=None, op0=ALU.mult),
                                  reads=[wk("sh1")] + tabs, writes=[wk("ctmp")])
                            kb.op("dve", lambda e, lr=lr, c128=c128, sc=sc: e.scalar_tensor_tensor(
                                out=car[0][:, sc:sc + 1], in0=lr, scalar=c128, in1=ctmp[:, 2 * (q % 2):2 * (q % 2) + 1], op0=ALU.mult, op1=ALU.subtract),
                                reads=[wk("sh0"), wk("ctmp")] + tabs, writes=[("car", 0, sc)])
                            kb.op("dve", lambda e, lr=lr, s128=s128, sc=sc: e.scalar_tensor_tensor(
                                out=car[1][:, sc:sc + 1], in0=lr, scalar=s128, in1=ctmp[:, 2 * (q % 2) + 1:2 * (q % 2) + 2], op0=ALU.mult, op1=ALU.add),
                                reads=[wk("sh0"), wk("ctmp")] + tabs, writes=[("car", 1, sc)])
                        yield
                        kb.op("pool", lambda e, cosv=cosv: e.tensor_tensor(out=v3(tq[0]), in0=v3(sh[0]), in1=cosv, op=ALU.mult),
                              reads=[wk("sh0")] + tabs, writes=[wk("tq0")])
                        kb.op("pool", lambda e, sinv=sinv: e.tensor_tensor(out=v3(tq[1]), in0=v3(sh[1]), in1=sinv, op=ALU.mult),
                              reads=[wk("sh1")] + tabs, writes=[wk("tq1")])
                        kb.op("dve", lambda e, sinv=sinv: e.tensor_tensor(out=v3(tq[2]), in0=v3(sh[0]), in1=sinv, op=ALU.mult),
                              reads=[wk("sh0")] + tabs, writes=[wk("tq2")])
                        kb.op("dve", lambda e, cosv=cosv: e.tensor_tensor(out=v3(tq[3]), in0=v3(sh[1]), in1=cosv, op=ALU.mult),
                              reads=[wk("sh1")] + tabs, writes=[wk("tq3")])
                        kb.op("pool", lambda e: e.tensor_tensor(out=sbf[0][:, :n], in0=tq[0][:, :n], in1=tq[1][:, :n], op=ALU.subtract),
                              reads=[wk("tq0"), wk("tq1")], writes=[wk("sbf0")])
                        kb.op("dve", lambda e: e.tensor_tensor(out=sbf[1][:, :n], in0=tq[2][:, :n], in1=tq[3][:, :n], op=ALU.add),
                              reads=[wk("tq2"), wk("tq3")], writes=[wk("sbf1")])
                    else:
                        lbre = PT[:, 2, sc:sc + 1]
                        lbim = PT[:, 3, sc:sc + 1]
                        s0r, s0i = s0T[0][:, sc, :n], s0T[1][:, sc, :n]
                        kb.op("dve", lambda e, s0i=s0i, lbim=lbim: e.tensor_scalar(out=tq[0][:, :n], in0=s0i, scalar1=lbim, scalar2=None, op0=ALU.mult),
                              reads=[("s0T", 1), "PT"], writes=[wk("tq0")])
                        kb.op("dve", lambda e, s0r=s0r, lbre=lbre: e.scalar_tensor_tensor(out=tq[0][:, :n], in0=s0r, scalar=lbre, in1=tq[0][:, :n],
                                                                               op0=ALU.mult, op1=ALU.subtract),
                              reads=[("s0T", 0), "PT", wk("tq0")], writes=[wk("tq0")])
                        kb.op("dve", lambda e, A=A, sc=sc: e.tensor_tensor(out=snw[0][:, sc, :n], in0=A[:, :n], in1=tq[0][:, :n], op=ALU.add),
                              reads=[kA, wk("tq0")], writes=[("snw", 0)])
                        kb.op("dve", lambda e, s0r=s0r, lbim=lbim: e.tensor_scalar(out=tq[1][:, :n], in0=s0r, scalar1=lbim, scalar2=None, op0=ALU.mult),
                              reads=[("s0T", 0), "PT"], writes=[wk("tq1")])
                        kb.op("dve", lambda e, s0i=s0i, lbre=lbre: e.scalar_tensor_tensor(out=tq[1][:, :n], in0=s0i, scalar=lbre, in1=tq[1][:, :n],
                                                                               op0=ALU.mult, op1=ALU.add),
                              reads=[("s0T", 1), "PT", wk("tq1")], writes=[wk("tq1")])
                        kb.op("dve", lambda e, B=B, sc=sc: e.tensor_tensor(out=snw[1][:, sc, :n], in0=B[:, :n], in1=tq[1][:, :n], op=ALU.add),
                              reads=[kB, wk("tq1")], writes=[("snw", 1)])
                        for ri in range(2):
                            kb.op("pool", lambda e, ri=ri, sc=sc: e.tensor_copy(out=sbf[ri][:, :n], in_=snw[ri][:, sc, :n]),
                                  reads=[("snw", ri)], writes=[wk("sbf%d" % ri)])
                    yield
                    for ri in range(2):
                        kb.op("pe", lambda e, ri=ri, sc=sc, q=q, pys=pys: e.matmul(pys[:, :n], lhsT=Cz[ri][:, sc, :], rhs=sbf[ri][:, :n],
                                                                                start=(q == 0 and ri == 0), stop=(q == 3 and ri == 1)),
                              reads=[("Cz", ri), wk("sbf%d" % ri)], writes=[kpys])
                for qp in range(2):
                    alive = [sc_gen(2 * qp + w_, WS[w_]) for w_ in range(2)]
                    while alive:
                        for g_ in list(alive):
                            try:
                                next(g_)
                            except StopIteration:
                                alive.remove(g_)
                y0, y1 = yt[0], yt[1]
                kb.op("dve", lambda e, j=j, oc=oc, pys=pys: e.scalar_tensor_tensor(out=y0[:, :n], in0=uf[j][:, :n], scalar=dcol[:, oc:oc + 1],
                                                                             in1=pys[:, :n], op0=ALU.mult, op1=ALU.add),
                      reads=[kuf, "dcol", kpys], writes=["yt0"])
                kb.op("act", lambda e: e.activation(out=y1[:, :n], in_=y0[:, :n], func=AF.Square), reads=["yt0"], writes=["yt1"])
                kb.op("dve", lambda e: e.tensor_scalar(out=y1[:, :n], in0=y1[:, :n], scalar1=0.044715, scalar2=1.0, op0=ALU.mult, op1=ALU.add),
                      reads=["yt1"], writes=["yt1"])
                kb.op("dve", lambda e: e.tensor_tensor(out=y1[:, :n], in0=y1[:, :n], in1=y0[:, :n], op=ALU.mult), reads=["yt1", "yt0"], writes=["yt1"])
                kb.op("act", lambda e: e.activation(out=y1[:, :n], in_=y1[:, :n], func=AF.Sigmoid, scale=2.0 * math.sqrt(2.0 / math.pi)),
                      reads=["yt1"], writes=["yt1"])
                kb.op("dve", lambda e, oc=oc: e.tensor_tensor(out=ysg[:, oc, :n], in0=y1[:, :n], in1=y0[:, :n], op=ALU.mult),
                      reads=["yt1", "yt0"], writes=[("ysg", oc)])
                kb.op("pool", lambda e, oc=oc: e.tensor_copy(out=ysb[:, oc, :n], in_=ysg[:, oc, :n]), reads=[("ysg", oc)], writes=[("ysb", oc)])
            for ec in range(8):
                p = self.rr("pp", 4)
                pp, kpp = self.pp[p], "pp%d" % p
                for kc in range(KC):
                    kb.op("pe", lambda e, kc=kc, ec=ec, pp=pp: e.matmul(pp[:, :n], lhsT=wg[:, kc, ec * 128:(ec + 1) * 128], rhs=ysb[:, kc, :n],
                                                                      start=(kc == 0), stop=(kc == KC - 1)),
                          reads=["wg"] + [("ysb", k) for k in range(KC)], writes=[kpp])
                zj = self.rr("zst", 2)
                kz = "zst%d" % zj
                row = O_ZS + ec * 128
                kb.dma("sp", zst[zj][:, :n], S["zT"][row:row + 128, t0:t0 + n], reads=[("zT", row, bi)], writes=[kz])
                kb.op("act", lambda e, zj=zj: e.activation(out=zst[zj][:, :n], in_=zst[zj][:, :n], func=AF.Silu), reads=[kz], writes=[kz])
                kb.op("act", lambda e, pp=pp, ec=ec: e.activation(out=yt[0][:, :n], in_=pp[:, :n], func=AF.Sigmoid, bias=gbcol[:, ec:ec + 1]),
                      reads=[kpp, "gbcol"], writes=["yt0"])
                kb.op("dve", lambda e, ec=ec: e.tensor_tensor(out=yt[0][:, :n], in0=yt[0][:, :n], in1=ysg[:, ec, :n], op=ALU.mult),
                      reads=["yt0", ("ysg", ec)], writes=["yt0"])
                aj = self.rr("aout", 2)
                kb.op("dve", lambda e, aj=aj, zj=zj: e.tensor_tensor(out=aout[aj][:, :n], in0=yt[0][:, :n], in1=zst[zj][:, :n], op=ALU.mult),
                      reads=["yt0", kz], writes=["aout%d" % aj])
                kb.dma("pool", S["act_s"][ec * 128:(ec + 1) * 128, t0:t0 + n], aout[aj][:, :n],
                       reads=["aout%d" % aj], writes=[("act", "s", bi)])
        for ri, nm in enumerate(("s5_re", "s5_im")):
            p = self.rr("ptr", 2)
            pt, kpt = self.ptr[p], "ptr%d" % p
            kb.op("pe", lambda e, pt=pt, ri=ri: e.transpose(out=pt[:32, 0:128], in_=car[ri][:, :], identity=self.ident[:, :]),
                  reads=[("car", ri, sc_) for sc_ in range(32)] + ["ident"], writes=[kpt])
            kb.op("dve", lambda e, pt=pt: e.tensor_copy(out=sst[:32, 0:128], in_=pt[:32, 0:128]), reads=[kpt], writes=["sst"])
            kb.dma("pool", O[nm + "_prompt"][l].rearrange("(s q) -> s q", q=128), sst[:32, 0:128], reads=["sst"], writes=[],
                   is_output=True)
            if NS:
                for g in range(8):
                    p = self.rr("ptr", 2)
                    pt, kpt = self.ptr[p], "ptr%d" % p
                    for s4 in range(4):
                        sc = g * 4 + s4
                        kb.op("pe", lambda e, pt=pt, ri=ri, sc=sc, s4=s4: e.transpose(out=pt[:NS, s4 * 128:(s4 + 1) * 128], in_=snw[ri][:, sc, :NS],
                                                                                   identity=self.ident[:, :]),
                              reads=[("snw", ri), "ident"], writes=[kpt])
                    kb.op("dve", lambda e, pt=pt, g=g: e.tensor_copy(out=sst[:NS, 0:512], in_=pt[:NS, :]), reads=[kpt], writes=["sst"])
                    kb.dma("pool", O[nm + "_sample"][l].rearrange("b g p -> b (g p)")[:, g * 512:(g + 1) * 512], sst[:NS, 0:512],
                           reads=["sst"], writes=[], is_output=True)
        self.end_phase()

    def load_sq(self, name, dram):
        kb = self.kb
        dst = self.wsq[name]
        for h in range(2):
            i = self.rr("w", 2)
            ws, kws = self.wst[i], "wst%d" % i
            kb.dma("sp", ws[:, :, :512], dram[:, h * 512:(h + 1) * 512].rearrange("(k p) c -> p k c", p=128),
                   writes=[kws])
            kb.op("dve", lambda e, ws=ws, h=h: e.tensor_copy(out=dst[:, :, h * 512:(h + 1) * 512], in_=ws[:, :, :512]),
                  reads=[kws], writes=[("wsq", name)])

    def phase3(self, l):
        kb = self.kb
        I, S = self.I, self.S
        last = (l == self.DEPTH - 1)
        self.begin_phase()
        self.wst = [self.sbp("wst%d" % i, (128, KC, 512)) for i in range(2)]
        self.wsq = {n: self.sbp("wsq_" + n, (128, KC, D), BF16) for n in ("bm", "br", "bs", "out")}
        self.actb = [self.sbp("actb%d" % i, (128, KC, 512), BF16) for i in range(2)]
        self.gmt = [self.sbp("gmt%d" % i, (128, 512)) for i in range(2)]
        self.mrg = self.sbp("mrg", (128, KC, 512), BF16)
        self.macc = self.sbp("macc", (128, KC, 512))
        self.ln_alloc()
        for nme in ("bm", "br", "bs", "out"):
            self.load_sq(nme, I["w_" + nme][l])
        self.load_ln_params(I["ln_g"][l], I["ln_b"][l])
        for bi, (t0, n) in enumerate(self.blocks):
            for ec in range(KC):
                for b, bn in enumerate("mrs"):
                    pass
            acts = {}
            for b, bn in enumerate("mrs"):
                if not self.have_branch(bn):
                    continue
                j = self.rr("actb", 2)
                at, kat = self.actb[j], "actb%d" % j
                kb.dma("sp", at[:, :, :n], S["act_" + bn][:, t0:t0 + n].rearrange("(k p) t -> p k t", p=128),
                       reads=[("act", bn, bi)], writes=[kat])
                for ec in range(KC):
                    p = self.rr("pp", 4)
                    pp, kpp = self.pp[p], "pp%d" % p
                    for kc in range(KC):
                        kb.op("pe", lambda e, kc=kc, pp=pp, ec=ec, at=at, bn=bn: e.matmul(
                            pp[:, :n], lhsT=self.wsq["b" + bn][:, kc, ec * 128:(ec + 1) * 128], rhs=at[:, kc, :n],
                            start=(kc == 0), stop=(kc == KC - 1)),
                            reads=[kat, ("wsq", "b" + bn)], writes=[kpp])
                    g = self.rr("gmt", 2)
                    gt, kgt = self.gmt[g], "gmt%d" % g
                    row = O_GM + b * D + ec * 128
                    kb.dma("sp", gt[:, :n], S["zT"][row:row + 128, t0:t0 + n],
                           reads=[("zT", row, bi)], writes=[kgt])
                    kb.op("act", lambda e, gt=gt: e.activation(out=gt[:, :n], in_=gt[:, :n], func=AF.Sigmoid),
                          reads=[kgt], writes=[kgt])
                    first = (bn == self.first_branch())
                    lastb = (bn == self.last_branch())
                    kmf = ("mrgf", ec)
                    acc = self.macc[:, ec, :n]
                    if first:
                        kb.op("dve", lambda e, acc=acc, gt=gt, pp=pp: e.tensor_tensor(out=acc, in0=pp[:, :n], in1=gt[:, :n],
                                                                                 op=ALU.mult),
                              reads=[kpp, kgt], writes=[("macc", ec)])
                    else:
                        kb.op("dve", lambda e, gt=gt, pp=pp: e.tensor_tensor(out=gt[:, :n], in0=pp[:, :n], in1=gt[:, :n],
                                                                        op=ALU.mult),
                              reads=[kpp, kgt], writes=[kgt])
                        kb.op("dve", lambda e, acc=acc, gt=gt: e.tensor_tensor(out=acc, in0=acc, in1=gt[:, :n], op=ALU.add),
                              reads=[kgt, ("macc", ec)], writes=[("macc", ec)])
                    if lastb:
                        kb.op("dve", lambda e, acc=acc, ec=ec: e.tensor_copy(out=self.mrg[:, ec, :n], in_=acc),
                              reads=[("macc", ec)], writes=[("mrg", ec)])
            for tt in range(0, n, 128):
                nr = min(128, n - tt)
                r0 = t0 + tt
                ti = r0 // 128
                j = self.rr("xt", 2)
                xt, kxt = self.xt[j], "xt%d" % j
                kb.dma("sp", xt[:nr, :], S["xs"][r0:r0 + nr, :], reads=[("xs", ti)], writes=[kxt])
                if self.first_branch() is not None:
                    for h in range(2):
                        p = self.rr("pp", 4)
                        pp, kpp = self.pp[p], "pp%d" % p
                        for kc in range(KC):
                            kb.op("pe", lambda e, kc=kc, pp=pp, h=h, tt=tt, nr=nr: e.matmul(
                                pp[:nr, :], lhsT=self.mrg[:, kc, tt:tt + nr], rhs=self.wsq["out"][:, kc, h * 512:(h + 1) * 512],
                                start=(kc == 0), stop=(kc == KC - 1)),
                                reads=[("mrg", k) for k in range(KC)] + [("wsq", "out")], writes=[kpp])
                        kb.op("dve", lambda e, xt=xt, pp=pp, h=h, nr=nr: e.scalar_tensor_tensor(
                            out=xt[:nr, h * 512:(h + 1) * 512], in0=xt[:nr, h * 512:(h + 1) * 512], scalar=ALPHA,
                            in1=pp[:nr, :], op0=ALU.mult, op1=ALU.add), reads=[kxt, kpp], writes=[kxt])
                else:
                    kb.op("dve", lambda e, xt=xt, nr=nr: e.tensor_scalar(out=xt[:nr, :], in0=xt[:nr, :], scalar1=ALPHA,
                                                                         scalar2=None, op0=ALU.mult),
                          reads=[kxt], writes=[kxt])
                fo = None
                if last:
                    fo = self.O["y_prompt"][r0:r0 + nr, :] if r0 < self.T else self.O["y_sample"][:, :]
                self.ln_tile(xt, kxt, r0, nr, ti, S["xs"][r0:r0 + nr, :], final_out=fo)
        self.end_phase()

    branches = ""
    NPI = 4
    CORE_BF16 = True

    def have_branch(self, bn):
        return bn in self.branches

    def first_branch(self):
        return self.branches[0] if self.branches else None

    def last_branch(self):
        return self.branches[-1] if self.branches else None


def make_in_map(inputs, c, NS, T, consts):
    m = {}
    m["x_prompt"] = np.ascontiguousarray(inputs["x_prompt"][c, :T])
    m["x_sample"] = np.ascontiguousarray(inputs["x_sample"][c * NS:(c + 1) * NS, 0])
    for n in ("ln_in_g", "ln_in_b", "w_in", "w_out", "ln_g", "ln_b", "w_bm", "w_br", "w_bs", "s_lam_re", "s_lam_im",
              "s_log_dt", "s_b_re", "s_b_im", "s_c_re", "s_c_im", "s_d", "s_glu_w", "s_glu_b",
              "m_conv_w", "m_conv_b", "m_wq", "m_wk", "m_wv", "m_ig_b", "m_fg_b", "m_norm_g", "m_skip",
              "r_mu", "r_w0", "r_w2", "r_a0", "r_a2", "r_k_k", "r_k_a", "r_r_k", "r_ln_g", "r_ln_b"):
        m[n] = np.ascontiguousarray(inputs[n])
    for n in ("state_s5_re", "state_s5_im", "state_mlstm_conv", "state_mlstm_c", "state_mlstm_n", "state_mlstm_m",
              "state_rwkv_wkv", "state_rwkv_shift"):
        m[n] = np.ascontiguousarray(inputs[n][:, c * NS:(c + 1) * NS])
    m.update(consts)
    return m


def kernel(**inputs):
    inputs = {k: np.asarray(v) for k, v in inputs.items()}
    T, NS, L = 2048, 16, 4
    Prog.branches = BRANCHES
    prog = Prog(T, NS, L)
    nc = prog.build()
    consts = host_consts()
    in_maps = [make_in_map(inputs, c, NS, T, consts) for c in range(8)]
    res = run_bass_kernel_spmd(nc, in_maps, core_ids=list(range(8)))
    rs = res.results
    f = np.float32

    def pstack(name, shape):
        if name in rs[0]:
            return np.stack([np.asarray(rs[c][name]).reshape((L,) + shape) for c in range(8)], 1).astype(f)
        return np.zeros((L, 8) + shape, f)

    def sstack(name, shape):
        if name in rs[0]:
            return np.concatenate([np.asarray(rs[c][name]).reshape((L, NS) + shape) for c in range(8)], 1).astype(f)
        return np.zeros((L, 8 * NS) + shape, f)

    y_prompt = np.stack([rs[c]["y_prompt"] for c in range(8)], 0).astype(f)
    y_sample = np.concatenate([rs[c]["y_sample"] for c in range(8)], 0)[:, None, :].astype(f)
    return (y_prompt, y_sample,
            pstack("c_prompt", (4, 256, 256)), sstack("c_sample", (4, 256, 256)),
            pstack("n_prompt", (4, 256)), sstack("n_sample", (4, 256)),
            pstack("m_prompt", (4,)), sstack("m_sample", (4,)),
            pstack("conv_prompt", (3, D)), sstack("conv_sample", (3, D)),
            pstack("wkv_prompt", (16, 64, 64)), sstack("wkv_sample", (16, 64, 64)),
            pstack("shift_prompt", (R_SHIFT_W,)), sstack("shift_sample", (R_SHIFT_W,)),
            pstack("s5_re_prompt", (64, 64)), sstack("s5_re_sample", (64, 64)),
            pstack("s5_im_prompt", (64, 64)), sstack("s5_im_sample", (64, 64)))
```

```python
import math
from contextlib import ExitStack
import numpy as np
import concourse.bass as bass
import concourse.mybir as mybir
from concourse.bass_utils import run_bass_kernel_spmd

F32 = mybir.dt.float32
BF16 = mybir.dt.bfloat16
I32 = mybir.dt.int32
ALU = mybir.AluOpType
AF = mybir.ActivationFunctionType
AX = mybir.AxisListType

D = 1024
KC = 8
N_IN = 12424
M_HEADS = 4
M_HD = 256
R_HEADS = 16
R_HD = 64
R_SHIFT_W = 3200
S_GROUPS = 64
S_STATE = 64
DEPTH_FULL = 4
ALPHA = (2.0 * DEPTH_FULL) ** 0.25
LN_EPS = 1e-5
RWKV_GN_EPS = 64e-5
NEG = -1e30

O_XM, O_IG, O_FG, O_OG, O_ZM = 0, 1024, 1028, 1032, 2056
O_RC, O_ZR, O_U, O_ZS, O_GM = 3080, 6280, 7304, 8328, 9352


class KB:
    def __init__(self, nc, es):
        self.nc = nc
        self.eng = dict(pe=nc.tensor, act=nc.scalar, dve=nc.vector, pool=nc.gpsimd, sp=nc.sync)
        self.stream = {e: [] for e in self.eng}
        self.csem = {e: es.enter_context(nc.semaphore("c_" + e)) for e in ("pe", "act", "dve", "pool")}
        self.ccnt = {e: 0 for e in self.csem}
        self.ring = {}
        for q, n in (("sp", 24), ("pool", 12), ("act", 6)):
            self.ring[q] = [[es.enter_context(nc.semaphore("d_%s%d" % (q, i))), 0] for i in range(n)]
        self.rpos = {q: 0 for q in self.ring}
        self.known = {e: {} for e in self.eng}
        self.lastw = {}
        self.readers = {}
        self.out_tokens = []
        self.n_ops = 0

    def _need(self, e, tok, same_ok=False):
        if tok is None:
            return
        sem, val, owner = tok
        if owner == e and (e == "pe" or same_ok):
            return
        k = id(sem)
        if self.known[e].get(k, 0) >= val:
            return
        self.known[e][k] = val
        self.eng[e].wait_ge(sem, val)

    ns = None
    local = frozenset()

    def _k(self, b):
        if self.ns is None:
            return b
        base = b[0] if isinstance(b, tuple) else b
        return (self.ns, b) if base in self.local else b

    def _deps(self, e, reads, writes):
        reads = [self._k(b) for b in reads]
        writes = [self._k(b) for b in writes]
        for b in reads:
            self._need(e, self.lastw.get(b))
        for b in writes:
            self._need(e, self.lastw.get(b))
            for tok in self.readers.get(b, ()):
                self._need(e, tok)

    def _commit(self, tok, reads, writes):
        reads = [self._k(b) for b in reads]
        writes = [self._k(b) for b in writes]
        for b in writes:
            self.lastw[b] = tok
            self.readers[b] = []
        for b in reads:
            self.readers.setdefault(b, []).append(tok)

    def op(self, e, fn, reads=(), writes=()):
        self._deps(e, reads, writes)
        self.ccnt[e] += 1
        tok = (self.csem[e], self.ccnt[e], e)
        fn(self.eng[e]).then_inc(self.csem[e], 1)
        self._commit(tok, reads, writes)
        self.n_ops += 1

    def dma(self, q, out, in_, reads=(), writes=(), is_output=False, **kw):
        self._deps(q, reads, writes)
        ring = self.ring[q]
        slot = ring[self.rpos[q] % len(ring)]
        self.rpos[q] += 1
        sem = slot[0]
        if slot[1] > 0:
            self._need(q, (sem, slot[1], None))
        slot[1] += 16
        tok = (sem, slot[1], None)
        self.eng[q].dma_start(out=out, in_=in_, **kw).then_inc(sem, 16)
        self._commit(tok, reads, writes)
        if is_output:
            self.out_tokens.append(tok)
        self.n_ops += 1

    def finish(self):
        for q in self.ring:
            for sem, val in self.ring[q]:
                if val > 0:
                    self._need("sp", (sem, val, None))
        for e in self.csem:
            if self.ccnt[e] > 0:
                self._need("sp", (self.csem[e], self.ccnt[e], e))

    def barrier(self):
        for e in self.eng:
            for q in self.ring:
                for sem, val in self.ring[q]:
                    if val > 0:
                        self._need(e, (sem, val, None))
            for e2 in self.csem:
                if self.ccnt[e2] > 0 and e2 != e:
                    self._need(e, (self.csem[e2], self.ccnt[e2], e2))

    def emit(self):
        pass


BRANCHES = "mrs"


def host_consts():
    c = {}
    c["ident"] = np.eye(128, dtype=np.float32)
    m3 = np.zeros((128, 4, 2), np.float32)
    m4 = np.zeros((128, 4, 8), np.float32)
    for p in range(128):
        g8 = p // 16
        gl = p // 64
        for q in range(4):
            for g in range(2):
                if g8 == 2 * q + g:
                    m3[p, q, g] = 1.0
            for gg in range(8):
                if gg == 2 * q + gl:
                    m4[p, q, gg] = 1.0
    c["mask3"], c["mask4"] = m3, m4
    selh = np.zeros((4, 4, 128), np.float32)
    for h in range(4):
        selh[h, h, :] = 1.0
    c["selh"] = selh
    c["ones4"] = np.ones((4, 128), np.float32)
    cn = np.zeros((128, 128), np.float32)
    for s_ in range(128):
        cn[s_, :s_] = 1.0e30
    c["causneg"] = cn
    rm = np.zeros((128, 384), np.float32)
    for p in range(128):
        for q in range(128):
            if p // 64 == q // 64:
                s_, t_ = p % 64, q % 64
                rm[p, q] = 1.0 if s_ < t_ else 0.0
                rm[p, 128 + q] = 1.0 if s_ <= t_ else 0.0
                rm[p, 256 + q] = 1.0 if s_ > t_ else 0.0
    c["rmasks"] = rm
    c["iota1"] = np.tile(np.arange(1, 257, dtype=np.float32)[None, :], (128, 1))
    return c


class Prog:
    def __init__(self, T, NS, DEPTH):
        self.T, self.NS, self.DEPTH = T, NS, DEPTH
        self.NT = T + NS
        assert T % 128 == 0
        self.ntile = T // 128
        self.tiles = [(i * 128, 128) for i in range(self.ntile)] + [(T, NS)]
        self.blocks = []
        t = 0
        while t < T:
            n = min(512, T - t)
            self.blocks.append((t, n))
            t += n
        self.blocks.append((T, NS))

    def build(self):
        nc = bass.Bass("TRN2", target_bir_lowering=False)
        self.nc = nc
        T, NS, L, NT = self.T, self.NS, self.DEPTH, self.NT
        dt = nc.dram_tensor

        def inp(name, shape, dtype=F32):
            return dt(name, list(shape), dtype, kind="ExternalInput").ap()

        def outp(name, shape):
            return dt(name, list(shape), F32, kind="ExternalOutput").ap()

        def scr(name, shape, dtype=F32):
            return dt(name, list(shape), dtype, kind="Internal").ap()

        I = {}
        I["x_prompt"] = inp("x_prompt", (T, D))
        I["x_sample"] = inp("x_sample", (NS, D))
        I["ident"] = inp("ident", (128, 128))
        for n, s in (("ln_in_g", (D,)), ("ln_in_b", (D,)), ("w_in", (L, D, N_IN)), ("w_out", (L, D, D)),
                     ("ln_g", (L, D)), ("ln_b", (L, D)), ("w_bm", (L, D, D)), ("w_br", (L, D, D)),
                     ("w_bs", (L, D, D)), ("s_lam_re", (L, 64, 64)), ("s_lam_im", (L, 64, 64)), ("s_log_dt", (L, 64)),
                     ("s_b_re", (L, 64, 64, 16)), ("s_b_im", (L, 64, 64, 16)), ("s_c_re", (L, 64, 16, 64)),
                     ("s_c_im", (L, 64, 16, 64)), ("s_d", (L, D)), ("s_glu_w", (L, D, D)), ("s_glu_b", (L, D)),
                     ("state_s5_re", (L, NS, 64, 64)), ("state_s5_im", (L, NS, 64, 64)),
                     ("state_mlstm_conv", (L, NS, 3, D)), ("state_mlstm_c", (L, NS, 4, 256, 256)),
                     ("state_mlstm_n", (L, NS, 4, 256)), ("state_mlstm_m", (L, NS, 4)),
                     ("m_conv_w", (L, 4, D)), ("m_conv_b", (L, D)), ("m_wq", (L, 4, 256, 256)), ("m_wk", (L, 4, 256, 256)),
                     ("m_wv", (L, 4, 256, 256)), ("m_ig_b", (L, 4)), ("m_fg_b", (L, 4)), ("m_norm_g", (L, D)), ("m_skip", (L, D)),
                     ("state_rwkv_wkv", (L, NS, 16, 64, 64)), ("state_rwkv_shift", (L, NS, R_SHIFT_W)),
                     ("r_mu", (L, R_SHIFT_W)), ("r_w0", (L, D)), ("r_w2", (L, 64, D)), ("r_a0", (L, D)), ("r_a2", (L, 64, D)),
                     ("r_k_k", (L, D)), ("r_k_a", (L, D)), ("r_r_k", (L, 16, 64)), ("r_ln_g", (L, D)), ("r_ln_b", (L, D)),
                     ("rmasks", (128, 384)),
                     ("selh", (4, 4, 128)), ("causneg", (128, 128)), ("ones4", (4, 128)),
                     ("mask3", (128, 4, 2)), ("mask4", (128, 4, 8)), ("iota1", (128, 256))):
            I[n] = inp(n, s)
        self.I = I
        O = {}
        O["y_prompt"] = outp("y_prompt", (T, D))
        O["y_sample"] = outp("y_sample", (NS, D))
        O["c_prompt"] = outp("c_prompt", (L, 4, 256, 256))
        O["c_sample"] = outp("c_sample", (L, NS, 4, 256, 256))
        O["n_prompt"] = outp("n_prompt", (L, 4, 256))
        O["n_sample"] = outp("n_sample", (L, NS, 4, 256))
        O["m_prompt"] = outp("m_prompt", (L, 4))
        O["m_sample"] = outp("m_sample", (L, NS, 4))
        O["conv_prompt"] = outp("conv_prompt", (L, 3, D))
        O["conv_sample"] = outp("conv_sample", (L, NS, 3, D))
        O["wkv_prompt"] = outp("wkv_prompt", (L, 16, 64, 64))
        O["wkv_sample"] = outp("wkv_sample", (L, NS, 16, 64, 64))
        O["shift_prompt"] = outp("shift_prompt", (L, R_SHIFT_W))
        O["shift_sample"] = outp("shift_sample", (L, NS, R_SHIFT_W))
        for nm in ("s5_re", "s5_im"):
            O[nm + "_prompt"] = outp(nm + "_prompt", (L, 4096))
            O[nm + "_sample"] = outp(nm + "_sample", (L, NS, 64, 64))
        self.O = O
        S = {}
        S["xs"] = scr("xs", (NT, D))
        S["z"] = scr("z", (NT, N_IN))
        S["zT"] = scr("zT", (N_IN, NT))
        for b in "mrs":
            S["act_" + b] = scr("act_" + b, (D, NT), BF16)
        S["vs"] = scr("vs", (NT, D))
        S["ys"] = scr("ys", (NT, D))
        self.S = S

        with ExitStack() as es:
            self.es = es
            kb = KB(nc, es)
            self.kb = kb
            self.alloc()
            self.phase0()
            for l in range(L):
                self.phase1(l)
                self.easy_states(l)
                if self.have_branch("m"):
                    self.mlstm_phase(l)
                if self.have_branch("r"):
                    self.rwkv_phase(l)
                if self.have_branch("s"):
                    self.s5_phase(l)
                self.phase3(l)
            kb.finish()
            kb.emit()
        return nc

    def sb(self, name, shape, dtype=F32):
        return self.es.enter_context(self.nc.sbuf_tensor("sb_" + name, list(shape), dtype))

    def sbp(self, name, shape, dtype=F32):
        self.uid = getattr(self, "uid", 0) + 1
        return self.pes.enter_context(self.nc.sbuf_tensor("sp%d_%s" % (self.uid, name), list(shape), dtype))

    def begin_phase(self):
        self.pes = ExitStack()

    def end_phase(self):
        self.kb.barrier()
        self.pes.close()

    def ps(self, name, shape, dtype=F32):
        return self.es.enter_context(self.nc.psum_tensor("ps_" + name, list(shape), dtype))

    def alloc(self):
        NT = self.NT
        self.ident = self.sb("ident", (128, 128))
        self.xT = self.sb("xT", (128, KC, NT), BF16)
        self.pp = [self.ps("pp%d" % i, (128, 512)) for i in range(4)]
        self.ptr = [self.ps("ptr%d" % i, (128, 512)) for i in range(2)]
        self.pex = [self.ps("pex%d" % i, (128, 512)) for i in range(2)]
        self.cnt = {}
        kb = self.kb
        kb.dma("sp", self.ident[:], self.I["ident"], writes=["ident"])

    def rr(self, key, n):
        v = self.cnt.get(key, 0)
        self.cnt[key] = v + 1
        return v % n

    def ln_alloc(self):
        self.gbc = self.sbp("gbc", (128, D))
        self.bbc = self.sbp("bbc", (128, D))
        self.xt = [self.sbp("xt%d" % i, (128, D)) for i in range(2)]
        self.xc = [self.sbp("xc%d" % i, (128, D)) for i in range(2)]
        self.st = [self.sbp("st%d" % i, (128, 8)) for i in range(2)]

    def ln_tile(self, src, srckey, row0, nr, ti, xs_out, final_out=None):
        kb = self.kb
        i = self.rr("ln", 2)
        xc, st = self.xc[i], self.st[i]
        kxc, kst = "xc%d" % i, "st%d" % i
        kb.op("dve", lambda e: e.tensor_reduce(out=st[:nr, 0:1], in_=src[:nr, :], axis=AX.X, op=ALU.add),
              reads=[srckey], writes=[kst])
        kb.op("dve", lambda e: e.tensor_scalar(out=st[:nr, 1:2], in0=st[:nr, 0:1], scalar1=-1.0 / D, scalar2=None,
                                               op0=ALU.mult), reads=[kst], writes=[kst])
        kb.op("dve", lambda e: e.tensor_scalar(out=xc[:nr, :], in0=src[:nr, :], scalar1=st[:nr, 1:2], scalar2=None,
                                               op0=ALU.add), reads=[srckey, kst], writes=[kxc])
        j = self.rr("tmpsq", 2)
        sq = self.xt[j]
        ksq = "xt%d" % j
        kb.op("act", lambda e: e.activation(out=sq[:nr, :], in_=xc[:nr, :], func=AF.Square, accum_out=st[:nr, 2:3]),
              reads=[kxc], writes=[ksq, kst])
        kb.op("act", lambda e: e.activation(out=st[:nr, 3:4], in_=st[:nr, 2:3], func=AF.Sqrt, scale=1.0 / D,
                                            bias=self.epsc[:nr, 0:1]), reads=[kst, "epsc"], writes=[kst])
        kb.op("dve", lambda e: e.reciprocal(out=st[:nr, 4:5], in_=st[:nr, 3:4]), reads=[kst], writes=[kst])
        kb.op("dve", lambda e: e.scalar_tensor_tensor(out=xc[:nr, :], in0=xc[:nr, :], scalar=st[:nr, 4:5],
                                                      in1=self.gbc[:nr, :], op0=ALU.mult, op1=ALU.mult),
              reads=[kxc, kst, "gbc"], writes=[kxc])
        kb.op("dve", lambda e: e.tensor_tensor(out=xc[:nr, :], in0=xc[:nr, :], in1=self.bbc[:nr, :], op=ALU.add),
              reads=[kxc, "bbc"], writes=[kxc])
        kb.dma("pool", xs_out, xc[:nr, :], reads=[kxc], writes=[("xs", ti)])
        if final_out is not None:
            kb.dma("pool", final_out, xc[:nr, :], reads=[kxc], writes=[], is_output=True)
        for half in range(2):
            p = self.rr("ptr", 2)
            pt, kpt = self.ptr[p], "ptr%d" % p
            for k4 in range(4):
                kc = half * 4 + k4
                kb.op("pe", lambda e, kc=kc, k4=k4, pt=pt: e.transpose(out=pt[:, k4 * 128:k4 * 128 + nr],
                                                                      in_=xc[:nr, kc * 128:(kc + 1) * 128],
                                                                      identity=self.ident[:nr, :nr]),
                      reads=[kxc, "ident"], writes=[kpt])
            dst = self.xT[:, half * 4:half * 4 + 4, row0:row0 + nr]
            srcp = pt[:, :].rearrange("p (k t) -> p k t", k=4)[:, :, :nr]
            eng = "act" if half == 0 else "dve"
            if eng == "act":
                kb.op("act", lambda e, dst=dst, srcp=srcp: e.activation(out=dst, in_=srcp, func=AF.Copy),
                      reads=[kpt], writes=[("xT", ti)])
            else:
                kb.op("dve", lambda e, dst=dst, srcp=srcp: e.tensor_copy(out=dst, in_=srcp),
                      reads=[kpt], writes=[("xT", ti)])

    def load_ln_params(self, g_ap, b_ap):
        kb = self.kb
        kb.dma("sp", self.gbc[:], g_ap.partition_broadcast(128), writes=["gbc"])
        kb.dma("sp", self.bbc[:], b_ap.partition_broadcast(128), writes=["bbc"])

    def phase0(self):
        kb = self.kb
        self.epsc = self.sb("epsc", (128, 1))
        kb.op("dve", lambda e: e.memset(self.epsc[:], LN_EPS), writes=["epsc"])
        self.onesf = self.sb("onesf", (128, 64))
        kb.op("dve", lambda e: e.memset(self.onesf[:], 1.0), writes=["onesf"])
        self.begin_phase()
        self.ln_alloc()
        self.load_ln_params(self.I["ln_in_g"], self.I["ln_in_b"])
        for ti, (r0, nr) in enumerate(self.tiles):
            j = self.rr("xt", 2)
            xt, kxt = self.xt[j], "xt%d" % j
            src = self.I["x_prompt"][r0:r0 + nr, :] if r0 < self.T else self.I["x_sample"][:, :]
            kb.dma("sp", xt[:nr, :], src, writes=[kxt])
            self.ln_tile(xt, kxt, r0, nr, ti, self.S["xs"][r0:r0 + nr, :])
        self.end_phase()

    def load_w(self, dram_cols, width):
        kb = self.kb
        i = self.rr("w", 2)
        ws, wb = self.wst[i], self.wbf[i]
        kws, kwb = "wst%d" % i, "wbf%d" % i
        kb.dma("sp", ws[:, :, :width], dram_cols.rearrange("(k p) c -> p k c", p=128), writes=[kws])
        if self.rr("wcast", 2) == 0:
            kb.op("dve", lambda e: e.tensor_copy(out=wb[:, :, :width], in_=ws[:, :, :width]), reads=[kws], writes=[kwb])
        else:
            kb.op("act", lambda e: e.activation(out=wb[:, :, :width], in_=ws[:, :, :width], func=AF.Copy), reads=[kws], writes=[kwb])
        return wb, kwb

    def evac(self, dst, src, reads, writes):
        kb = self.kb
        if self.rr("evac", 2) == 0:
            kb.op("act", lambda e: e.activation(out=dst, in_=src, func=AF.Copy), reads=reads, writes=writes)
        else:
            kb.op("dve", lambda e: e.tensor_copy(out=dst, in_=src), reads=reads, writes=writes)

    def phase1(self, l):
        kb = self.kb
        self.begin_phase()
        self.wst = [self.sbp("wst%d" % i, (128, KC, 512)) for i in range(2)]
        self.wbf = [self.sbp("wbf%d" % i, (128, KC, 512), BF16) for i in range(2)]
        self.ev = [self.sbp("ev%d" % i, (128, 512)) for i in range(4)]
        W = self.I["w_in"][l]
        tm_segs = [(O_OG, O_ZM), (O_RC, O_U)]
        fm_segs = [(O_XM, O_IG), (O_IG, O_FG), (O_FG, O_OG), (O_ZM, O_RC), (O_U, O_ZS), (O_ZS, O_GM), (O_GM, N_IN)]
        for (c0, c1) in tm_segs:
            c = c0
            while c < c1:
                w = min(512, c1 - c)
                wb, kwb = self.load_w(W[:, c:c + w], w)
                for ti, (r0, nr) in enumerate(self.tiles):
                    p = self.rr("pp", 4)
                    pp, kpp = self.pp[p], "pp%d" % p
                    for kc in range(KC):
                        kb.op("pe", lambda e, kc=kc, pp=pp, r0=r0, nr=nr, wb=wb, w=w: e.matmul(
                            pp[:nr, :w], lhsT=self.xT[:, kc, r0:r0 + nr], rhs=wb[:, kc, :w],
                            start=(kc == 0), stop=(kc == KC - 1)),
                            reads=[("xT", ti), kwb], writes=[kpp])
                    v = self.rr("ev", 4)
                    ev, kev = self.ev[v], "ev%d" % v
                    self.evac(ev[:nr, :w], pp[:nr, :w], [kpp], [kev])
                    kb.dma("pool", self.S["z"][r0:r0 + nr, c:c + w], ev[:nr, :w], reads=[kev],
                           writes=[("z", ti)])
                c += w
        for (c0, c1) in fm_segs:
            c = c0
            while c < c1:
                w = min(512, c1 - c)
                wb, kwb = self.load_w(W[:, c:c + w], w)
                for s0 in range(0, w, 128):
                    m = min(128, w - s0)
                    for bi, (t0, n) in enumerate(self.blocks):
                        p = self.rr("pp", 4)
                        pp, kpp = self.pp[p], "pp%d" % p
                        tis = list(range(t0 // 128, (t0 + n + 127) // 128))
                        for kc in range(KC):
                            kb.op("pe", lambda e, kc=kc, pp=pp, t0=t0, n=n, wb=wb, s0=s0, m=m: e.matmul(
                                pp[:m, :n], lhsT=wb[:, kc, s0:s0 + m], rhs=self.xT[:, kc, t0:t0 + n],
                                start=(kc == 0), stop=(kc == KC - 1)),
                                reads=[("xT", t) for t in tis] + [kwb], writes=[kpp])
                        v = self.rr("ev", 4)
                        ev, kev = self.ev[v], "ev%d" % v
                        self.evac(ev[:m, :n], pp[:m, :n], [kpp], [kev])
                        kb.dma("pool", self.S["zT"][c + s0:c + s0 + m, t0:t0 + n], ev[:m, :n], reads=[kev],
                               writes=[("zT", c + s0, bi)])
                c += w
        self.end_phase()

    def easy_states(self, l):
        kb = self.kb
        I, S, O = self.I, self.S, self.O
        T, NS = self.T, self.NS
        nb = len(self.blocks)
        zt_all = [("zT", r, b) for r in range(0, 1024, 128) for b in range(nb)]
        z_all = [("z", ti) for ti in range(len(self.tiles))]
        kb.dma("pool", O["conv_prompt"][l].rearrange("j c -> c j"), S["zT"][0:D, T - 3:T], reads=zt_all, writes=[],
               is_output=True, allow_slow_non_contiguous=True)
        kb.dma("pool", O["shift_prompt"][l:l + 1, :], S["z"][T - 1:T, O_RC:O_RC + R_SHIFT_W], reads=z_all, writes=[], is_output=True)
        if NS:
            kb.dma("pool", O["conv_sample"][l][:, 0:2, :], I["state_mlstm_conv"][l][:, 1:3, :], writes=[], is_output=True)
            for hh in range(4):
                kb.dma("pool", O["conv_sample"][l][:, 2, hh * 256:(hh + 1) * 256].rearrange("b c -> c b"),
                       S["zT"][hh * 256:(hh + 1) * 256, T:T + NS], reads=zt_all, writes=[],
                       is_output=True, allow_slow_non_contiguous=True)
            kb.dma("pool", O["shift_sample"][l], S["z"][T:T + NS, O_RC:O_RC + R_SHIFT_W], reads=z_all, writes=[], is_output=True)

    def mlstm_phase(self, l):
        kb = self.kb
        I, S, O = self.I, self.S, self.O
        T, NS, NT = self.T, self.NS, self.NT
        self.begin_phase()
        sbp = self.sbp
        Wst = sbp("Wst", (128, 4, 2, 256))
        Wq = sbp("Wq", (128, 4, 2, 256), BF16); Wk = sbp("Wk", (128, 4, 2, 256), BF16); Wv = sbp("Wv", (128, 4, 2, 256), BF16)
        cw = sbp("cw", (128, 4, 8)); cb = sbp("cb", (128, 8)); mg = sbp("mg", (128, 8)); msk = sbp("msk", (128, 8))
        gb = sbp("gb", (4, 2))
        igA = sbp("igA", (4, NT)); lfA = sbp("lfA", (4, NT))
        selh = sbp("selh", (4, 4, 128)); causneg = sbp("causneg", (128, 128)); ones4 = sbp("ones4", (4, 128))
        Cst = [[sbp("C%d_%d" % (s_, h), (128, 2, 257)) for h in range(4)] for s_ in range(3)]

        def mk_set(tag, Lm):
            W = {"tag": tag}
            W["mprev"] = sbp(tag + "mprev", (4, 2))
            W["xext"] = [sbp(tag + "xext%d" % i, (128, 8, Lm + 3)) for i in range(2)]
            W["ctmp"] = sbp(tag + "cvtmp", (128, 8, Lm)); W["cacc"] = sbp(tag + "cvacc", (128, 8, Lm))
            W["xcT"] = sbp(tag + "xcT", (128, 8, Lm)); W["xcb"] = sbp(tag + "xcb", (128, 8, Lm), BF16); W["xmb"] = sbp(tag + "xmb", (128, 8, Lm), BF16)
            W["qTb"] = sbp(tag + "qTb", (128, 4, 2, Lm), BF16); W["kTb"] = sbp(tag + "kTb", (128, 4, 2, Lm), BF16)
            W["Cb"] = [sbp(tag + "Cb%d" % h, (128, 2, 257), BF16) for h in range(4)]
            W["kw"] = sbp(tag + "kw", (128, 256), BF16); W["vaug"] = sbp(tag + "vaug", (128, 257), BF16)
            W["G"] = sbp(tag + "G", (4, 8, Lm)); W["gsm"] = sbp(tag + "gsm", (4, 8)); W["dg"] = sbp(tag + "dg", (4, 4))
            W["gc"] = sbp(tag + "gc", (128, 16)); W["dbc"] = sbp(tag + "dbc", (128, 4))
            W["DT"] = sbp(tag + "DT", (128, Lm)); W["Stl"] = sbp(tag + "Stl", (128, Lm), BF16); W["mmb"] = sbp(tag + "mmb", (128, 4, Lm))
            W["Asb"] = sbp(tag + "Asb", (128, 257)); W["nd"] = sbp(tag + "nd", (128, 257)); W["dsm"] = sbp(tag + "dsm", (128, 4))
            W["sog"] = [sbp(tag + "sog%d" % i, (128, D)) for i in range(2 if Lm > 1 else 1)]
            W["hm"] = sbp(tag + "hm", (128, D)); W["hst"] = sbp(tag + "hst", (128, 16))
            W["hT"] = sbp(tag + "hT", (128, 8, Lm)); W["zmt"] = [sbp(tag + "zmt%d" % i, (128, 8, Lm)) for i in range(2)]
            W["aob"] = [sbp(tag + "aob%d" % i, (128, 8, Lm), BF16) for i in range(2)]
            return W
        WP = mk_set("P", 128)
        WS_ = mk_set("S", 1) if NS else None
        kb.local = frozenset(["mprev", "mnew", "xext0", "xext1", "cvacc", "cvtmp", "xcT", "xcb", "xmb", "G0", "G1", "G3", "G4", "G5", "G6", "gsm", "gc", "dg",
                              "dbc", "mmb", "DT", "Stl", "kw", "vaug", "Asb", "nd", "dsm", "sog0", "sog1", "hst", "zmt0", "zmt1", "aob0", "aob1",
                              "hm", "hT", "qTb", "kTb", "Cb"])
        cvst = sbp("cvst", (48, D)); convT = sbp("convT", (128, 8, 48))

        for (nm, dst, sc_) in (("m_wq", Wq, 1.0 / 16.0), ("m_wk", Wk, 1.0), ("m_wv", Wv, 1.0)):
            kb.dma("sp", Wst[:], I[nm][l].rearrange("h (c p) e -> p h c e", p=128), writes=["Wst"])
            kb.op("act", lambda e, dst=dst, sc_=sc_: e.activation(out=dst[:], in_=Wst[:], func=AF.Copy, scale=sc_),
                  reads=["Wst"], writes=[nm])
        sl = dict(allow_slow_non_contiguous=True)
        kb.dma("sp", cw[:], I["m_conv_w"][l].rearrange("j (k p) -> p j k", p=128), writes=["cw"], **sl)
        kb.dma("sp", cb[:], I["m_conv_b"][l].rearrange("(k p) -> p k", p=128), writes=["cb"], **sl)
        kb.dma("sp", mg[:], I["m_norm_g"][l].rearrange("(k p) -> p k", p=128), writes=["mg"], **sl)
        kb.dma("sp", msk[:], I["m_skip"][l].rearrange("(k p) -> p k", p=128), writes=["msk"], **sl)
        kb.dma("sp", gb[:, 0:1], I["m_ig_b"][l].rearrange("(h o) -> h o", o=1), writes=["gb"], **sl)
        kb.dma("sp", gb[:, 1:2], I["m_fg_b"][l].rearrange("(h o) -> h o", o=1), writes=["gb"], **sl)
        kb.dma("sp", selh[:], I["selh"], writes=["selh"])
        kb.dma("sp", causneg[:], I["causneg"], writes=["causneg"])
        kb.dma("sp", ones4[:], I["ones4"], writes=["ones4"])
        nb = len(self.blocks)
        kb.dma("sp", igA[:], S["zT"][O_IG:O_IG + 4, :], reads=[("zT", O_IG, b) for b in range(nb)], writes=["igA"])
        kb.dma("sp", lfA[:], S["zT"][O_FG:O_FG + 4, :], reads=[("zT", O_FG, b) for b in range(nb)], writes=["lfA"])
        kb.op("dve", lambda e: e.tensor_scalar(out=igA[:], in0=igA[:], scalar1=gb[:, 0:1], scalar2=None, op0=ALU.add),
              reads=["igA", "gb"], writes=["igA"])
        kb.op("dve", lambda e: e.tensor_scalar(out=lfA[:], in0=lfA[:], scalar1=gb[:, 1:2], scalar2=-1.0, op0=ALU.add, op1=ALU.mult),
              reads=["lfA", "gb"], writes=["lfA"])
        kb.op("act", lambda e: e.activation(out=lfA[:], in_=lfA[:], func=AF.Exp), reads=["lfA"], writes=["lfA"])
        kb.op("dve", lambda e: e.tensor_scalar(out=lfA[:], in0=lfA[:], scalar1=1.0, scalar2=None, op0=ALU.add), reads=["lfA"], writes=["lfA"])
        kb.op("act", lambda e: e.activation(out=lfA[:], in_=lfA[:], func=AF.Ln), reads=["lfA"], writes=["lfA"])
        kb.op("dve", lambda e: e.tensor_scalar(out=lfA[:], in0=lfA[:], scalar1=-1.0, scalar2=None, op0=ALU.mult), reads=["lfA"], writes=["lfA"])
        if NS:
            kb.dma("sp", cvst[:3 * NS, :], I["state_mlstm_conv"][l].rearrange("b j c -> (b j) c"), writes=["cvst"])
            for half in range(2):
                p = self.rr("ptr", 2)
                pt, kpt = self.ptr[p], "ptr%d" % p
                for k4 in range(4):
                    kc = half * 4 + k4
                    kb.op("pe", lambda e, pt=pt, k4=k4, kc=kc: e.transpose(out=pt[:, k4 * 48:k4 * 48 + 3 * NS], in_=cvst[:3 * NS, kc * 128:(kc + 1) * 128],
                                                                       identity=self.ident[:3 * NS, :3 * NS]),
                          reads=["cvst", "ident"], writes=[kpt])
                kb.op("dve", lambda e, pt=pt, half=half: e.tensor_copy(out=convT[:, half * 4:half * 4 + 4, :3 * NS],
                                                                     in_=pt[:, 0:192].rearrange("p (k c) -> p k c", k=4)[:, :, :3 * NS]),
                      reads=[kpt], writes=["convT"])

        zt_x = lambda bi: [("zT", r, bi) for r in range(0, 1024, 128)]
        zt_z = lambda bi: [("zT", O_ZM + r, bi) for r in range(0, 1024, 128)]
        blk_of = lambda t: next(i for i, (b0, bn) in enumerate(self.blocks) if b0 <= t < b0 + bn)

        chunks = [(c * 128, 128, None) for c in range(T // 128)] + [(T + b, 1, b) for b in range(NS)]
        def chunk_gen(ci, t0, L, sb_, W):
            mprev = W["mprev"]; xext = W["xext"]; ctmp = W["ctmp"]; cacc = W["cacc"]; xcT = W["xcT"]; xcb = W["xcb"]; xmb = W["xmb"]
            qTb = W["qTb"]; kTb = W["kTb"]; kw = W["kw"]; vaug = W["vaug"]; G = W["G"]; gsm = W["gsm"]; dg = W["dg"]; gc = W["gc"]; dbc = W["dbc"]
            DT = W["DT"]; Stl = W["Stl"]; mmb = W["mmb"]; Asb = W["Asb"]; nd = W["nd"]; dsm = W["dsm"]; sog = W["sog"]; hm = W["hm"]; hst = W["hst"]
            hT = W["hT"]; zmt = W["zmt"]; aob = W["aob"]; Cb = W["Cb"]
            bi = blk_of(t0)
            ti = t0 // 128
            cs = (1 + self.rr("Cset", 2)) if sb_ is not None else 0
            C = Cst[cs]
            kC = [("C", cs, h) for h in range(4)]
            if ci == 0:
                for h in range(4):
                    kb.op("pool", lambda e, h=h: e.memset(C[h][:], 0.0), writes=[kC[h]])
                kb.op("dve", lambda e: e.memset(mprev[:, 0:1], NEG), writes=["mprev"])
            if sb_ is not None:
                for h in range(4):
                    kb.dma("sp", C[h][:, :, 0:256], I["state_mlstm_c"][l, sb_, h].rearrange("(c p) v -> p c v", p=128), writes=[kC[h]])
                    kb.dma("sp", C[h][:, :, 256:257], I["state_mlstm_n"][l, sb_, h].rearrange("(c p o) -> p c o", p=128, o=1), writes=[kC[h]], **sl)
                kb.dma("sp", mprev[:, 0:1], I["state_mlstm_m"][l, sb_].rearrange("(h o) -> h o", o=1), writes=["mprev"], **sl)
            for h in range(4):
                kb.op("act", lambda e, h=h: e.activation(out=Cb[h][:], in_=C[h][:], func=AF.Copy), reads=[kC[h]], writes=[("Cb", h)])
            xj = self.rr("xext" + W["tag"], 2)
            xe, kxe = xext[xj], "xext%d" % xj
            if sb_ is None:
                if t0 == 0:
                    kb.op("pool", lambda e, xe=xe: e.memset(xe[:, :, 0:3], 0.0), writes=[kxe])
                    kb.dma("sp", xe[:, :, 3:3 + L], S["zT"][0:D, t0:t0 + L].rearrange("(k p) t -> p k t", p=128), reads=zt_x(bi), writes=[kxe])
                else:
                    rd = zt_x(bi) + (zt_x(blk_of(t0 - 3)) if blk_of(t0 - 3) != bi else [])
                    kb.dma("sp", xe[:, :, 0:3 + L], S["zT"][0:D, t0 - 3:t0 + L].rearrange("(k p) t -> p k t", p=128), reads=rd, writes=[kxe])
            else:
                kb.op("pool", lambda e, xe=xe, sb_=sb_: e.tensor_copy(out=xe[:, :, 0:3], in_=convT[:, :, 3 * sb_:3 * sb_ + 3]), reads=["convT"], writes=[kxe])
                kb.dma("sp", xe[:, :, 3:4], S["zT"][0:D, t0:t0 + 1].rearrange("(k p) t -> p k t", p=128), reads=zt_x(bi), writes=[kxe], **sl)
            V = lambda x: x[:, :, :L]
            for j in range(4):
                dst = cacc if j == 0 else ctmp
                kd = "cvacc" if j == 0 else "cvtmp"
                kb.op("dve", lambda e, j=j, dst=dst, xe=xe: e.tensor_tensor(out=V(dst), in0=xe[:, :, j:j + L],
                                                                         in1=cw[:, j, :, None].to_broadcast([128, 8, L]), op=ALU.mult),
                      reads=[kxe, "cw"], writes=[kd])
                if j > 0:
                    kb.op("dve", lambda e: e.tensor_tensor(out=V(cacc), in0=V(cacc), in1=V(ctmp), op=ALU.add), reads=["cvacc", "cvtmp"], writes=["cvacc"])
            kb.op("dve", lambda e: e.tensor_tensor(out=V(cacc), in0=V(cacc), in1=cb[:, :, None].to_broadcast([128, 8, L]), op=ALU.add),
                  reads=["cvacc", "cb"], writes=["cvacc"])
            kb.op("act", lambda e: e.activation(out=V(xcT), in_=V(cacc), func=AF.Silu), reads=["cvacc"], writes=["xcT"])
            kb.op("dve", lambda e: e.tensor_copy(out=V(xcb), in_=V(xcT)), reads=["xcT"], writes=["xcb"])
            kb.op("dve", lambda e, xe=xe: e.tensor_copy(out=V(xmb), in_=xe[:, :, 3:3 + L]), reads=[kxe], writes=["xmb"])
            yield
            Gr = lambda i: G[:, i, :L]
            kb.op("dve", lambda e: e.tensor_tensor_scan(out=Gr(0), data0=ones4[:, :L], data1=lfA[:, t0:t0 + L], initial=0.0, op0=ALU.mult, op1=ALU.add),
                  reads=["ones4", "lfA"], writes=["G0"])
            kb.op("dve", lambda e: e.tensor_tensor(out=Gr(3), in0=igA[:, t0:t0 + L], in1=Gr(0), op=ALU.subtract), reads=["igA", "G0"], writes=["G3"])
            kb.op("dve", lambda e: e.tensor_tensor_scan(out=Gr(1), data0=Gr(3), data1=Gr(3), initial=-3.0e38, op0=ALU.max, op1=ALU.max),
                  reads=["G3"], writes=["G1"])
            kb.op("dve", lambda e: e.tensor_scalar(out=Gr(1), in0=Gr(1), scalar1=mprev[:, 0:1], scalar2=None, op0=ALU.max), reads=["G1", "mprev"], writes=["G1"])
            kb.op("dve", lambda e: e.tensor_scalar(out=gsm[:, 0:1], in0=G[:, 1, L - 1:L], scalar1=-1.0, scalar2=None, op0=ALU.mult), reads=["G1"], writes=["gsm"])
            kb.op("act", lambda e: e.activation(out=Gr(4), in_=Gr(1), func=AF.Exp, scale=-1.0, bias=mprev[:, 0:1]), reads=["G1", "mprev"], writes=["G4"])
            kb.op("dve", lambda e: e.tensor_tensor(out=Gr(5), in0=Gr(0), in1=Gr(1), op=ALU.add), reads=["G0", "G1"], writes=["G5"])
            kb.op("act", lambda e: e.activation(out=Gr(5), in_=Gr(5), func=AF.Exp, scale=-1.0), reads=["G5"], writes=["G5"])
            kb.op("act", lambda e: e.activation(out=Gr(6), in_=Gr(3), func=AF.Exp, bias=gsm[:, 0:1]), reads=["G3", "gsm"], writes=["G6"])
            kb.op("dve", lambda e: e.tensor_tensor(out=mprev[:, 1:2], in0=G[:, 0, L - 1:L], in1=G[:, 1, L - 1:L], op=ALU.add), reads=["G0", "G1"], writes=["mnew"])
            p = self.rr("ptr", 2)
            pt, kpt = self.ptr[p], "ptr%d" % p
            for ki, gi in enumerate((4, 5, 3, 6)):
                kb.op("pe", lambda e, ki=ki, gi=gi, pt=pt: e.transpose(out=pt[:L, ki * 4:ki * 4 + 4], in_=G[:, gi, :L], identity=self.ident[:4, :4]),
                      reads=["G%d" % gi, "ident"], writes=[kpt])
            kb.op("dve", lambda e, pt=pt: e.tensor_copy(out=gc[:L, :], in_=pt[:L, 0:16]), reads=[kpt], writes=["gc"])
            kb.op("dve", lambda e: e.tensor_scalar(out=dg[:, :], in0=self.ident[:4, :4], scalar1=G[:, 4, L - 1:L], scalar2=None, op0=ALU.mult),
                  reads=["G4", "ident"], writes=["dg"])
            p2 = self.rr("ptr", 2)
            pt2, kpt2 = self.ptr[p2], "ptr%d" % p2
            kb.op("pe", lambda e, pt2=pt2: e.matmul(pt2[:, 0:4], lhsT=ones4[:, :], rhs=dg[:, :], start=True, stop=True), reads=["ones4", "dg"], writes=[kpt2])
            kb.op("dve", lambda e, pt2=pt2: e.tensor_copy(out=dbc[:, :], in_=pt2[:, 0:4]), reads=[kpt2], writes=["dbc"])
            pnn = self.rr("pp", 4)
            pn, kpn = self.pp[pnn], "pp%d" % pnn
            for h in range(4):
                kb.op("pe", lambda e, h=h, pn=pn: e.matmul(pn[:L, h * 128:h * 128 + L], lhsT=selh[:, h, :L], rhs=G[:, 1, :L], start=True, stop=True),
                      reads=["selh", "G1"], writes=[kpn])
            kb.op("act", lambda e, pn=pn: e.activation(out=mmb[:L, :, :L], in_=pn[:L, :].rearrange("p (h t) -> p h t", h=4)[:, :, :L], func=AF.Copy),
                  reads=[kpn], writes=["mmb"])
            sj = self.rr("sog" + W["tag"], len(sog))
            so, kso = sog[sj], "sog%d" % sj
            kb.dma("sp", so[:L, :], S["z"][t0:t0 + L, O_OG:O_OG + D], reads=[("z", ti)], writes=[kso])
            kb.op("act", lambda e, so=so: e.activation(out=so[:L, :], in_=so[:L, :], func=AF.Sigmoid), reads=[kso], writes=[kso])
            yield
            for h in range(4):
                for (Wt, wn, dstT, kd) in ((Wq, "m_wq", qTb, "qTb"), (Wk, "m_wk", kTb, "kTb")):
                    pq = self.rr("pp", 4)
                    ppq, kpq = self.pp[pq], "pp%d" % pq
                    for ec in range(2):
                        for dc in range(2):
                            kb.op("pe", lambda e, h=h, ec=ec, dc=dc, Wt=Wt, ppq=ppq: e.matmul(
                                ppq[:, ec * 128:ec * 128 + L], lhsT=Wt[:, h, dc, ec * 128:(ec + 1) * 128], rhs=xcb[:, 2 * h + dc, :L],
                                start=(dc == 0), stop=(dc == 1)), reads=[wn, "xcb"], writes=[kpq])
                    self.evac(dstT[:, h, :, :L], ppq[:, 0:256].rearrange("p (c t) -> p c t", c=2)[:, :, :L], [kpq], [(kd, h)])
            for h in range(4):
                yield
                pk = self.rr("pp", 4)
                ppk, kpk = self.pp[pk], "pp%d" % pk
                for dc in range(2):
                    kb.op("pe", lambda e, h=h, dc=dc, ppk=ppk: e.matmul(ppk[:L, 0:256], lhsT=xcb[:, 2 * h + dc, :L], rhs=Wk[:, h, dc, :],
                                                                      start=(dc == 0), stop=(dc == 1)), reads=["m_wk", "xcb"], writes=[kpk])
                kb.op("act", lambda e, h=h, ppk=ppk: e.activation(out=kw[:L, :], in_=ppk[:L, 0:256], func=AF.Copy, scale=gc[:L, 12 + h:13 + h]),
                      reads=[kpk, "gc"], writes=["kw"])
                pv = self.rr("pp", 4)
                ppv, kpv = self.pp[pv], "pp%d" % pv
                for dc in range(2):
                    kb.op("pe", lambda e, h=h, dc=dc, ppv=ppv: e.matmul(ppv[:L, 0:256], lhsT=xmb[:, 2 * h + dc, :L], rhs=Wv[:, h, dc, :],
                                                                      start=(dc == 0), stop=(dc == 1)), reads=["m_wv", "xmb"], writes=[kpv])
                kb.op("dve", lambda e, ppv=ppv: e.tensor_copy(out=vaug[:L, 0:256], in_=ppv[:L, 0:256]), reads=[kpv], writes=["vaug"])
                kb.op("dve", lambda e: e.memset(vaug[:L, 256:257], 1.0), writes=["vaug"])
                kb.op("dve", lambda e, h=h: e.tensor_tensor(out=DT[:L, :L], in0=mmb[:L, h, :L], in1=causneg[:L, :L], op=ALU.add),
                      reads=["mmb", "causneg"], writes=["DT"])
                kb.op("act", lambda e, h=h: e.activation(out=DT[:L, :L], in_=DT[:L, :L], func=AF.Exp, scale=-1.0, bias=gc[:L, 8 + h:9 + h]),
                      reads=["DT", "gc"], writes=["DT"])
                ps_ = self.rr("pp", 4)
                pps, kps = self.pp[ps_], "pp%d" % ps_
                for ec in range(2):
                    kb.op("pe", lambda e, h=h, ec=ec, pps=pps: e.matmul(pps[:L, :L], lhsT=kTb[:, h, ec, :L], rhs=qTb[:, h, ec, :L],
                                                                      start=(ec == 0), stop=(ec == 1)), reads=[("kTb", h), ("qTb", h)], writes=[kps])
                kb.op("dve", lambda e, pps=pps: e.tensor_tensor(out=Stl[:L, :L], in0=pps[:L, :L], in1=DT[:L, :L], op=ALU.mult), reads=[kps, "DT"], writes=["Stl"])
                pa = self.rr("pp", 4)
                ppa, kpa = self.pp[pa], "pp%d" % pa
                kb.op("pe", lambda e, ppa=ppa: e.matmul(ppa[:L, 0:257], lhsT=Stl[:L, :L], rhs=vaug[:L, :], start=True, stop=True), reads=["Stl", "vaug"], writes=[kpa])
                pb = self.rr("ptr", 2)
                ppb, kpb = self.ptr[pb], "ptr%d" % pb
                for ec in range(2):
                    kb.op("pe", lambda e, h=h, ec=ec, ppb=ppb, C=C: e.matmul(ppb[:L, 0:257], lhsT=qTb[:, h, ec, :L], rhs=Cb[h][:, ec, :],
                                                                      start=(ec == 0), stop=(ec == 1)), reads=[("qTb", h), ("Cb", h)], writes=[kpb])
                kb.op("act", lambda e, ppa=ppa: e.activation(out=Asb[:L, :], in_=ppa[:L, 0:257], func=AF.Copy), reads=[kpa], writes=["Asb"])
                kb.op("dve", lambda e, h=h, ppb=ppb: e.scalar_tensor_tensor(out=nd[:L, :], in0=ppb[:L, 0:257], scalar=gc[:L, h:h + 1], in1=Asb[:L, :],
                                                                        op0=ALU.mult, op1=ALU.add), reads=[kpb, "gc", "Asb"], writes=["nd"])
                kb.op("act", lambda e: e.activation(out=dsm[:L, 2:3], in_=nd[:L, 256:257], func=AF.Abs), reads=["nd"], writes=["dsm"])
                kb.op("dve", lambda e, h=h: e.tensor_scalar(out=dsm[:L, 0:1], in0=dsm[:L, 2:3], scalar1=gc[:L, 4 + h:5 + h], scalar2=None,
                                                            op0=ALU.max), reads=["dsm", "gc"], writes=["dsm"])
                kb.op("dve", lambda e: e.reciprocal(out=dsm[:L, 1:2], in_=dsm[:L, 0:1]), reads=["dsm"], writes=["dsm"])
                kb.op("dve", lambda e, h=h, so=so: e.scalar_tensor_tensor(out=hm[:L, h * 256:(h + 1) * 256], in0=nd[:L, 0:256], scalar=dsm[:L, 1:2],
                                                                      in1=so[:L, h * 256:(h + 1) * 256], op0=ALU.mult, op1=ALU.mult),
                      reads=["nd", "dsm", kso], writes=[("hm", h)])
                for ec in range(2):
                    pu = self.rr("pp", 4)
                    ppu, kpu = self.pp[pu], "pp%d" % pu
                    kb.op("pe", lambda e, ec=ec, ppu=ppu: e.matmul(ppu[:, 0:257], lhsT=kw[:L, ec * 128:(ec + 1) * 128], rhs=vaug[:L, :], start=True, stop=True),
                          reads=["kw", "vaug"], writes=[kpu])
                    kb.op("dve", lambda e, h=h, ec=ec, ppu=ppu, C=C: e.scalar_tensor_tensor(out=C[h][:, ec, :], in0=C[h][:, ec, :], scalar=dbc[:, h:h + 1],
                                                                                      in1=ppu[:, 0:257], op0=ALU.mult, op1=ALU.add),
                          reads=[kC[h], "dbc", kpu], writes=[kC[h]])
            kb.op("dve", lambda e: e.tensor_copy(out=mprev[:, 0:1], in_=mprev[:, 1:2]), reads=["mnew"], writes=["mprev"])
            yield
            hmk = [("hm", h) for h in range(4)]
            hv = hm[:L, :].rearrange("t (h d) -> t h d", h=4)
            kb.op("dve", lambda e: e.tensor_reduce(out=hst[:L, 0:4], in_=hv, axis=AX.X, op=ALU.add), reads=hmk, writes=["hst"])
            kb.op("dve", lambda e: e.tensor_scalar(out=hst[:L, 0:4], in0=hst[:L, 0:4], scalar1=-1.0 / 256.0, scalar2=None, op0=ALU.mult), reads=["hst"], writes=["hst"])
            kb.op("dve", lambda e: e.tensor_tensor(out=hv, in0=hv, in1=hst[:L, 0:4].unsqueeze(2).to_broadcast([L, 4, 256]), op=ALU.add),
                  reads=hmk + ["hst"], writes=hmk)
            kb.op("dve", lambda e, so=so: e.tensor_tensor(out=so[:L, :], in0=hm[:L, :], in1=hm[:L, :], op=ALU.mult), reads=hmk, writes=[kso])
            kb.op("dve", lambda e, so=so: e.tensor_reduce(out=hst[:L, 4:8], in_=so[:L, :].rearrange("t (h d) -> t h d", h=4), axis=AX.X, op=ALU.add),
                  reads=[kso], writes=["hst"])
            kb.op("act", lambda e: e.activation(out=hst[:L, 8:12], in_=hst[:L, 4:8], func=AF.Sqrt, scale=1.0 / 256.0, bias=self.epsc[:L, 0:1]),
                  reads=["hst", "epsc"], writes=["hst"])
            kb.op("dve", lambda e: e.reciprocal(out=hst[:L, 12:16], in_=hst[:L, 8:12]), reads=["hst"], writes=["hst"])
            kb.op("dve", lambda e: e.tensor_tensor(out=hv, in0=hv, in1=hst[:L, 12:16].unsqueeze(2).to_broadcast([L, 4, 256]), op=ALU.mult),
                  reads=hmk + ["hst"], writes=hmk)
            zj = self.rr("zmt" + W["tag"], 2)
            zm_, kzm = zmt[zj], "zmt%d" % zj
            kb.dma("sp", zm_[:, :, :L], S["zT"][O_ZM:O_ZM + D, t0:t0 + L].rearrange("(k p) t -> p k t", p=128), reads=zt_z(bi), writes=[kzm],
                   **(sl if L == 1 else {}))
            kb.op("act", lambda e, zm_=zm_: e.activation(out=zm_[:, :, :L], in_=zm_[:, :, :L], func=AF.Silu), reads=[kzm], writes=[kzm])
            for half in range(2):
                p = self.rr("ptr", 2)
                pt, kpt = self.ptr[p], "ptr%d" % p
                for k4 in range(4):
                    kc = half * 4 + k4
                    kb.op("pe", lambda e, k4=k4, kc=kc, pt=pt: e.transpose(out=pt[:, k4 * 128:k4 * 128 + L], in_=hm[:L, kc * 128:(kc + 1) * 128],
                                                                       identity=self.ident[:L, :L]), reads=hmk + ["ident"], writes=[kpt])
                kb.op("dve", lambda e, half=half, pt=pt: e.tensor_tensor(out=hT[:, half * 4:half * 4 + 4, :L],
                                                                       in0=pt[:, :].rearrange("p (k t) -> p k t", k=4)[:, :, :L],
                                                                       in1=mg[:, half * 4:half * 4 + 4, None].to_broadcast([128, 4, L]), op=ALU.mult),
                      reads=[kpt, "mg"], writes=[("hT", half)])
            kb.op("dve", lambda e: e.tensor_tensor(out=V(ctmp), in0=V(xcT), in1=msk[:, :, None].to_broadcast([128, 8, L]), op=ALU.mult),
                  reads=["xcT", "msk"], writes=["cvtmp"])
            kb.op("dve", lambda e: e.tensor_tensor(out=V(hT), in0=V(hT), in1=V(ctmp), op=ALU.add), reads=["cvtmp", ("hT", 0), ("hT", 1)], writes=[("hT", 0), ("hT", 1)])
            aj = self.rr("aob" + W["tag"], 2)
            kb.op("dve", lambda e, aj=aj, zm_=zm_: e.tensor_tensor(out=aob[aj][:, :, :L], in0=V(hT), in1=zm_[:, :, :L], op=ALU.mult),
                  reads=[("hT", 0), ("hT", 1), kzm], writes=["aob%d" % aj])
            kb.dma("pool", S["act_m"][:, t0:t0 + L].rearrange("(k p) t -> p k t", p=128), aob[aj][:, :, :L], reads=["aob%d" % aj],
                   writes=[("act", "m", bi)], **(sl if L == 1 else {}))
            if sb_ is not None or ci == T // 128 - 1:
                if sb_ is None:
                    oc_, on_, om_ = O["c_prompt"][l], O["n_prompt"][l], O["m_prompt"][l]
                else:
                    oc_, on_, om_ = O["c_sample"][l, sb_], O["n_sample"][l, sb_], O["m_sample"][l, sb_]
                for h in range(4):
                    kb.dma("pool", oc_[h].rearrange("(c p) v -> p c v", p=128), C[h][:, :, 0:256], reads=[kC[h]], writes=[], is_output=True)
                    kb.dma("pool", on_[h].rearrange("(c p o) -> p c o", p=128, o=1), C[h][:, :, 256:257], reads=[kC[h]], writes=[], is_output=True, **sl)
                kb.dma("pool", om_.rearrange("(h o) -> h o", o=1), mprev[:, 0:1], reads=["mprev"], writes=[], is_output=True, **sl)
        pq_ = [(ci, t0, L, sb_) for ci, (t0, L, sb_) in enumerate(chunks) if sb_ is None]
        sq_ = [(ci, t0, L, sb_) for ci, (t0, L, sb_) in enumerate(chunks) if sb_ is not None]
        streams = [[pq_, WP, None], [sq_, WS_, None]]
        while any(st[0] or st[2] is not None for st in streams):
            for st in streams:
                if st[2] is None and st[0]:
                    args = st[0].pop(0)
                    st[2] = chunk_gen(*args, st[1])
                if st[2] is not None:
                    kb.ns = st[1]["tag"]
                    try:
                        next(st[2])
                    except StopIteration:
                        st[2] = None
                    kb.ns = None
        self.end_phase()

    def rwkv_phase(self, l):
        kb = self.kb
        I, S, O = self.I, self.S, self.O
        T, NS, NT = self.T, self.NS, self.NT
        self.begin_phase()
        sbp = self.sbp
        sl = dict(allow_slow_non_contiguous=True)
        NPI = self.NPI
        EW = -math.exp(-0.5)
        P_ = {}
        for nm in ("r_k_k", "r_k_a", "r_ln_g", "r_ln_b"):
            P_[nm] = sbp("bc_" + nm, (128, D))
            kb.dma("sp", P_[nm][:], I[nm][l].partition_broadcast(128), writes=[nm])
        P_["r_r_k"] = sbp("bc_rk", (128, D))
        kb.dma("sp", P_["r_r_k"][:], I["r_r_k"][l].rearrange("h j -> (h j)").partition_broadcast(128), writes=["r_r_k"])
        mu = sbp("bc_mu", (128, R_SHIFT_W))
        kb.dma("sp", mu[:], I["r_mu"][l].partition_broadcast(128), writes=["mu"])
        w2 = sbp("w2", (65, 2, D))
        kb.dma("sp", w2[0:64, 0, :], I["r_w2"][l], writes=["w2"])
        kb.dma("sp", w2[0:64, 1, :], I["r_a2"][l], writes=["w2"])
        kb.dma("sp", w2[64:65, 0, :], I["r_w0"][l:l + 1, :], writes=["w2"])
        kb.dma("sp", w2[64:65, 1, :], I["r_a0"][l:l + 1, :], writes=["w2"])
        mks = sbp("mks", (128, 384))
        kb.dma("sp", mks[:], I["rmasks"], writes=["mks"])
        e12 = sbp("e12", (128, 1))
        kb.op("dve", lambda e: e.memset(e12[:], RWKV_GN_EPS), writes=["e12"])
        xr = sbp("xr", (128, R_SHIFT_W)); rp = sbp("rp", (128, R_SHIFT_W))
        lt = sbp("lt", (65, 2, 128))
        kb.op("pool", lambda e: e.memset(lt[64:65, :, :], 1.0), writes=["lt1"])
        tv = {n: sbp("tv_" + n, (128, D)) for n in ("lw", "a", "an", "b", "k")}
        ssq = sbp("ssq", (128, 64))
        fmall = sbp("fmall", (128, 8, 6, 128))
        fm = {n: fmall[:, :, i_, :] for i_, n in enumerate(("at", "rt", "bh", "kh", "bc", "kc"))}
        fm["cum"] = sbp("fm_cum", (128, 8, 128)); fm["et"] = sbp("fm_et", (128, 8, 128))
        wc = sbp("wc", (128, 8, 16))
        ST = sbp("STt", (128, 8, 64))
        Vp = sbp("Vp", (128, 8, 64)); Ych = sbp("Ych", (128, 8, 64))
        nat = sbp("rnat", (128, 8, 64)); nato = sbp("rnato", (128, 8, 64))
        CDT = BF16 if self.CORE_BF16 else F32
        NG = 2
        G_ = []
        BK = sbp("g_BK", (128, 4, 2, 128), CDT)
        for gi in range(NG):
            d = {}
            d["UBD"] = sbp("g%d_UBD" % gi, (128, 4, 4, 128), CDT)
            d["Btm"] = sbp("g%d_Btm" % gi, (128, 4, 128), CDT); d["Ktm"] = sbp("g%d_Ktm" % gi, (128, 4, 128), CDT)
            d["MNb"] = sbp("g%d_MNb" % gi, (128, 4, 256), CDT); d["MNk"] = sbp("g%d_MNk" % gi, (128, 4, 256), CDT)
            d["X"] = sbp("g%d_X" % gi, (128, 4, 128), CDT); d["Xt"] = sbp("g%d_Xt" % gi, (128, 4, 128), CDT); d["P"] = sbp("g%d_P" % gi, (128, 4, 128), CDT)
            d["RHS"] = sbp("g%d_RHS" % gi, (128, 4, 64), CDT); d["U"] = sbp("g%d_U" % gi, (128, 4, 64), CDT)
            G_.append(d)
        self.bank8 = list(self.pp) + list(self.ptr) + list(self.pex)
        self.bank8k = ["pp%d" % i for i in range(4)] + ["ptr%d" % i for i in range(2)] + ["pex%d" % i for i in range(2)]
        if self.CORE_BF16:
            STb = sbp("STb", (128, 8, 64), BF16); Vpb = sbp("Vpb", (128, 8, 64), BF16); identc = sbp("identc", (128, 128), BF16)
            kb.op("pool", lambda e: e.tensor_copy(out=identc[:], in_=self.ident[:]), reads=["ident"], writes=["identc"])
            kb.op("pool", lambda e: e.memset(STb[:], 0.0), writes=[("STb", h) for h in range(8)])
        else:
            STb, Vpb, identc = ST, Vp, self.ident
        kSTb = (lambda hh: ("STb", hh)) if self.CORE_BF16 else (lambda hh: ("ST", hh))
        kVpb = (lambda hh: ("Vpb", hh)) if self.CORE_BF16 else (lambda hh: ("Vp", hh))

        def vp_ready():
            if self.CORE_BF16:
                kb.op("act", lambda e: e.activation(out=Vpb[:], in_=Vp[:], func=AF.Copy), reads=[("Vp", h) for h in range(8)], writes=[("Vpb", h) for h in range(8)])
        ytm = rp[:, D:2 * D]; zrt = rp[:, 0:D]; yst = sbp("yst", (128, 64))
        aob = [sbp("raob%d" % i, (128, 8, 128), BF16) for i in range(2)]

        def zero_units():
            for gi in range(NG):
                kb.op("pool", lambda e, gi=gi: e.memset(G_[gi]["UBD"][:], 0.0), writes=[("g", gi, "UBD")])
            kb.op("pool", lambda e: e.memset(BK[:], 0.0), writes=["BK"])
        zero_units()
        kb.op("pool", lambda e: e.memset(ST[:], 0.0), writes=[("ST", h) for h in range(8)])

        z_all = lambda ti: [("z", ti)]

        def prep(t0, L, ti, is_s):
            kb.dma("sp", xr[:L, :], S["z"][t0:t0 + L, O_RC:O_RC + R_SHIFT_W], reads=z_all(ti), writes=["xr"])
            if is_s:
                kb.dma("sp", rp[:L, :], I["state_rwkv_shift"][l], writes=["rp"])
            elif t0 == 0:
                kb.op("pool", lambda e: e.memset(rp[0:1, :], 0.0), writes=["rp"])
                kb.dma("sp", rp[1:L, :], S["z"][0:L - 1, O_RC:O_RC + R_SHIFT_W], reads=z_all(ti), writes=["rp"])
            else:
                kb.dma("sp", rp[:L, :], S["z"][t0 - 1:t0 + L - 1, O_RC:O_RC + R_SHIFT_W], reads=z_all(ti) + z_all(ti - 1), writes=["rp"])
            kb.op("dve", lambda e: e.tensor_tensor(out=rp[:L, :], in0=rp[:L, :], in1=xr[:L, :], op=ALU.subtract), reads=["rp", "xr"], writes=["rp"])
            kb.op("dve", lambda e: e.tensor_tensor(out=rp[:L, :], in0=rp[:L, :], in1=mu[:L, :], op=ALU.mult), reads=["rp", "mu"], writes=["rp"])
            kb.op("dve", lambda e: e.tensor_tensor(out=xr[:L, :], in0=xr[:L, :], in1=rp[:L, :], op=ALU.add), reads=["rp", "xr"], writes=["xr"])
            r_ = xr[:L, 0:D]; kr = xr[:L, D:2 * D]; vr = xr[:L, 2 * D:3 * D]
            kb.dma("pool", S["vs"][t0:t0 + L, :], vr, reads=["xr"], writes=[("vs", ti)])
            kb.op("act", lambda e: e.activation(out=xr[:L, 3 * D:3 * D + 64], in_=xr[:L, 3 * D:3 * D + 64], func=AF.Tanh), reads=["xr"], writes=["xr"])
            p = self.rr("ptr", 2)
            pt, kpt = self.ptr[p], "ptr%d" % p
            for i2 in range(2):
                kb.op("pe", lambda e, i2=i2, pt=pt: e.transpose(out=pt[:64, i2 * 128:i2 * 128 + L], in_=xr[:L, 3 * D + 64 * i2:3 * D + 64 * i2 + 64],
                                                             identity=self.ident[:L, :L]), reads=["xr", "ident"], writes=[kpt])
            kb.op("dve", lambda e, pt=pt: e.tensor_copy(out=lt[0:64, :, :L], in_=pt[:64, 0:256].rearrange("p (a t) -> p a t", a=2)[:, :, :L]), reads=[kpt], writes=["lt"])
            for i2, dst in enumerate(("lw", "a")):
                for hf in range(2):
                    pq = self.rr("pp", 4)
                    pp, kpp = self.pp[pq], "pp%d" % pq
                    kb.op("pe", lambda e, i2=i2, hf=hf, pp=pp: e.matmul(pp[:L, :], lhsT=lt[:, i2, :L], rhs=w2[:, i2, hf * 512:(hf + 1) * 512], start=True, stop=True),
                          reads=["lt", "lt1", "w2"], writes=[kpp])
                    kb.op("act", lambda e, dst=dst, hf=hf, pp=pp: e.activation(out=tv[dst][:L, hf * 512:(hf + 1) * 512], in_=pp[:L, :], func=AF.Sigmoid),
                          reads=[kpp], writes=[("tv", dst)])
            kb.op("dve", lambda e: e.tensor_scalar(out=tv["lw"][:L, :], in0=tv["lw"][:L, :], scalar1=EW, scalar2=None, op0=ALU.mult), reads=[("tv", "lw")], writes=[("tv", "lw")])
            kb.op("dve", lambda e: e.tensor_tensor(out=tv["an"][:L, :], in0=kr, in1=P_["r_k_k"][:L, :], op=ALU.mult), reads=["xr", "r_k_k"], writes=[("tv", "an")])
            kb.op("dve", lambda e: e.tensor_tensor(out=tv["b"][:L, :], in0=tv["an"][:L, :], in1=tv["an"][:L, :], op=ALU.mult), reads=[("tv", "an")], writes=[("tv", "b")])
            kb.op("dve", lambda e: e.tensor_reduce(out=ssq[:L, 0:16], in_=tv["b"][:L, :].rearrange("t (h j) -> t h j", h=16), axis=AX.X, op=ALU.add),
                  reads=[("tv", "b")], writes=["ssq"])
            kb.op("act", lambda e: e.activation(out=ssq[:L, 0:16], in_=ssq[:L, 0:16], func=AF.Sqrt), reads=["ssq"], writes=["ssq"])
            kb.op("dve", lambda e: e.tensor_scalar(out=ssq[:L, 0:16], in0=ssq[:L, 0:16], scalar1=1e-12, scalar2=None, op0=ALU.max), reads=["ssq"], writes=["ssq"])
            kb.op("dve", lambda e: e.reciprocal(out=ssq[:L, 16:32], in_=ssq[:L, 0:16]), reads=["ssq"], writes=["ssq"])
            hv = lambda x: x[:L, :].rearrange("t (h j) -> t h j", h=16)
            kb.op("dve", lambda e: e.tensor_tensor(out=hv(tv["an"]), in0=hv(tv["an"]), in1=ssq[:L, 16:32].unsqueeze(2).to_broadcast([L, 16, 64]), op=ALU.mult),
                  reads=[("tv", "an"), "ssq"], writes=[("tv", "an")])
            kb.op("dve", lambda e: e.tensor_tensor(out=tv["b"][:L, :], in0=tv["an"][:L, :], in1=tv["a"][:L, :], op=ALU.mult),
                  reads=[("tv", "an"), ("tv", "a")], writes=[("tv", "b")])
            kb.op("dve", lambda e: e.tensor_scalar(out=tv["an"][:L, :], in0=tv["an"][:L, :], scalar1=-1.0, scalar2=None, op0=ALU.mult),
                  reads=[("tv", "an"), ("tv", "b")], writes=[("tv", "an")])
            kb.op("dve", lambda e: e.scalar_tensor_tensor(out=tv["k"][:L, :], in0=tv["a"][:L, :], scalar=-1.0, in1=P_["r_k_a"][:L, :], op0=ALU.add, op1=ALU.mult),
                  reads=[("tv", "a"), "r_k_a"], writes=[("tv", "k")])
            kb.op("dve", lambda e: e.scalar_tensor_tensor(out=tv["k"][:L, :], in0=tv["k"][:L, :], scalar=1.0, in1=kr, op0=ALU.add, op1=ALU.mult),
                  reads=[("tv", "k"), "xr"], writes=[("tv", "k")])
            kb.op("dve", lambda e: e.tensor_tensor(out=tv["a"][:L, :], in0=tv["k"][:L, :], in1=P_["r_r_k"][:L, :], op=ALU.mult),
                  reads=[("tv", "k"), "r_r_k", ("tv", "b")], writes=[("tv", "a")])
            kb.op("dve", lambda e: e.tensor_tensor(out=tv["a"][:L, :], in0=tv["a"][:L, :], in1=r_, op=ALU.mult), reads=[("tv", "a"), "xr"], writes=[("tv", "a")])
            kb.op("dve", lambda e: e.tensor_reduce(out=ssq[:L, 32:48], in_=hv(tv["a"]), axis=AX.X, op=ALU.add), reads=[("tv", "a")], writes=["ssq2"])
            for (dst, src, ksrc) in (("at", tv["an"][:L, :], ("tv", "an")), ("rt", r_, "xr"), ("bh", tv["b"][:L, :], ("tv", "b")),
                                     ("kh", tv["k"][:L, :], ("tv", "k")), ("cum", tv["lw"][:L, :], ("tv", "lw"))):
                for half in range(2):
                    p = self.rr("ptr", 2)
                    pt, kpt = self.ptr[p], "ptr%d" % p
                    for k4 in range(4):
                        kc = half * 4 + k4
                        kb.op("pe", lambda e, k4=k4, kc=kc, pt=pt, src=src: e.transpose(out=pt[:, k4 * 128:k4 * 128 + L], in_=src[:, kc * 128:(kc + 1) * 128],
                                                                                  identity=self.ident[:L, :L]), reads=[ksrc, "ident"], writes=[kpt])
                    self.evac(fm[dst][:, half * 4:half * 4 + 4, :L], pt[:, :].rearrange("p (k t) -> p k t", k=4)[:, :, :L], [kpt], [("fm", dst)])
            Lc = 1 if is_s else 64
            nch = L // Lc
            lwT = fm["cum"]
            kb.op("dve", lambda e: e.tensor_copy(out=fm["et"][:, :, :L], in_=lwT[:, :, :L]), reads=[("fm", "cum")], writes=[("fm", "et")])
            if Lc > 1:
                for hh in range(8):
                    for c in range(nch):
                        kb.op("dve", lambda e, hh=hh, c=c: e.tensor_tensor_scan(out=fm["cum"][:, hh, c * Lc:(c + 1) * Lc], data0=self.onesf[:, :Lc],
                                                                             data1=fm["et"][:, hh, c * Lc:(c + 1) * Lc], initial=0.0, op0=ALU.mult, op1=ALU.add),
                              reads=[("fm", "et"), "onesf"], writes=[("fm", "cum")])
            F = lambda n: fm[n][:, :, :L]
            C4 = lambda n: fm[n][:, :, :L].rearrange("p k (c t) -> p k c t", t=Lc)
            kb.op("dve", lambda e: e.tensor_tensor(out=F("et"), in0=F("cum"), in1=F("et"), op=ALU.subtract), reads=[("fm", "cum"), ("fm", "et")], writes=[("fm", "et")])
            kb.op("act", lambda e: e.activation(out=F("et"), in_=F("et"), func=AF.Exp), reads=[("fm", "et")], writes=[("fm", "et")])
            kb.op("dve", lambda e: e.tensor_tensor(out=F("at"), in0=F("at"), in1=F("et"), op=ALU.mult), reads=[("fm", "at"), ("fm", "et")], writes=[("fm", "at")])
            kb.op("act", lambda e: e.activation(out=F("et"), in_=F("cum"), func=AF.Exp), reads=[("fm", "cum"), ("fm", "at")], writes=[("fm", "et")])
            kb.op("dve", lambda e: e.tensor_tensor(out=F("rt"), in0=F("rt"), in1=F("et"), op=ALU.mult), reads=[("fm", "rt"), ("fm", "et")], writes=[("fm", "rt")])
            kb.op("pool", lambda e: e.tensor_copy(out=wc[:, :, :nch], in_=C4("et")[:, :, :, Lc - 1]), reads=[("fm", "et")], writes=["wc"])
            kb.op("dve", lambda e: e.tensor_tensor(out=C4("et"), in0=C4("cum")[:, :, :, Lc - 1:Lc].to_broadcast([128, 8, nch, Lc]), in1=C4("cum"), op=ALU.subtract),
                  reads=[("fm", "cum"), ("fm", "rt"), "wc"], writes=[("fm", "et")])
            kb.op("act", lambda e: e.activation(out=F("et"), in_=F("et"), func=AF.Exp), reads=[("fm", "et")], writes=[("fm", "et")])
            kb.op("dve", lambda e: e.tensor_tensor(out=F("bc"), in0=F("bh"), in1=F("et"), op=ALU.mult), reads=[("fm", "bh"), ("fm", "et")], writes=[("fm", "bc")])
            kb.op("dve", lambda e: e.tensor_tensor(out=F("kc"), in0=F("kh"), in1=F("et"), op=ALU.mult), reads=[("fm", "kh"), ("fm", "et")], writes=[("fm", "kc")])
            kb.op("act", lambda e: e.activation(out=F("et"), in_=F("cum"), func=AF.Exp, scale=-1.0), reads=[("fm", "cum"), ("fm", "bc"), ("fm", "kc")], writes=[("fm", "et")])
            kb.op("dve", lambda e: e.tensor_tensor(out=F("bh"), in0=F("bh"), in1=F("et"), op=ALU.mult), reads=[("fm", "bh"), ("fm", "et")], writes=[("fm", "bh")])
            kb.op("dve", lambda e: e.tensor_tensor(out=F("kh"), in0=F("kh"), in1=F("et"), op=ALU.mult), reads=[("fm", "kh"), ("fm", "et")], writes=[("fm", "kh")])

        def bank():
            q_ = self.rr("bank8", 8)
            return self.bank8[q_], self.bank8k[q_]

        def core(gi, col0, Lc, cidx):
            g = G_[gi]
            K_ = lambda n: ("g", gi, n)
            hs = slice(4 * gi, 4 * gi + 4)
            cs_ = slice(col0, col0 + Lc)
            for half in range(2):
                ps = slice(half * 64, half * 64 + 64)
                if half == 0:
                    kb.op("dve", lambda e, ps=ps, half=half: e.tensor_copy(
                        out=g["UBD"][ps, :, :, half * 64:half * 64 + Lc], in_=fmall[ps, hs, 0:4, cs_]),
                        reads=[("fm", n) for n in ("at", "rt", "bh", "kh")], writes=[K_("UBD")])
                else:
                    kb.op("act", lambda e, ps=ps, half=half: e.activation(
                        out=g["UBD"][ps, :, :, half * 64:half * 64 + Lc], in_=fmall[ps, hs, 0:4, cs_], func=AF.Copy),
                        reads=[("fm", n) for n in ("at", "rt", "bh", "kh")], writes=[K_("UBD")])
            yield
            for half in range(2):
                ps = slice(half * 64, half * 64 + 64)
                if half == 1:
                    kb.op("dve", lambda e, ps=ps, half=half: e.tensor_copy(
                        out=BK[ps, :, :, half * 64:half * 64 + Lc], in_=fmall[ps, hs, 4:6, cs_]),
                        reads=[("fm", "bc"), ("fm", "kc")], writes=["BK"])
                else:
                    kb.op("act", lambda e, ps=ps, half=half: e.activation(
                        out=BK[ps, :, :, half * 64:half * 64 + Lc], in_=fmall[ps, hs, 4:6, cs_], func=AF.Copy),
                        reads=[("fm", "bc"), ("fm", "kc")], writes=["BK"])
            for vi, dst in enumerate(("Btm", "Ktm")):
                pb_, kpb = bank()
                for u in range(4):
                    kb.op("pe", lambda e, u=u, vi=vi, pb_=pb_: e.matmul(pb_[:, u * 128:(u + 1) * 128], lhsT=BK[:, u, vi, :], rhs=identc[:, :], start=True, stop=True),
                          reads=["BK", "identc"], writes=[kpb])
                self.evac(g[dst][:, :, :], pb_[:, :].rearrange("p (u m) -> p u m", u=4), [kpb], [K_(dst)])
            yield
            for (vec, dst) in ((2, "MNb"), (3, "MNk")):
                for pr in range(2):
                    pb_, kpb = bank()
                    for u2 in range(2):
                        u = pr * 2 + u2
                        kb.op("pe", lambda e, u=u, u2=u2, vec=vec, pb_=pb_: e.matmul(pb_[:, u2 * 256:(u2 + 1) * 256], lhsT=g["UBD"][:, u, vec, :],
                                                                              rhs=g["UBD"][:, u, 0:2, :].rearrange("p a m -> p (a m)"), start=True, stop=True),
                              reads=[K_("UBD")], writes=[kpb])
                    kb.op("dve", lambda e, pr=pr, dst=dst, pb_=pb_: e.tensor_tensor(out=g[dst][:, pr * 2:pr * 2 + 2, :], in0=pb_[:, :].rearrange("p (u m) -> p u m", u=2),
                                                                           in1=mks[:, None, 0:256].to_broadcast([128, 2, 256]), op=ALU.mult),
                          reads=[kpb, "mks"], writes=[K_(dst)])
            pb_, kpb = bank()
            for u in range(4):
                kb.op("pe", lambda e, u=u, pb_=pb_: e.matmul(pb_[:, u * 128:(u + 1) * 128], lhsT=g["UBD"][:, u, 0, :], rhs=g["UBD"][:, u, 2, :], start=True, stop=True),
                      reads=[K_("UBD")], writes=[kpb])
            kb.op("dve", lambda e, pb_=pb_: e.tensor_tensor(out=g["Xt"][:, :, :], in0=pb_[:, :].rearrange("p (u m) -> p u m", u=4),
                                                       in1=mks[:, None, 256:384].to_broadcast([128, 4, 128]), op=ALU.mult),
                  reads=[kpb, "mks"], writes=[K_("Xt")])
            kb.op("act", lambda e: e.activation(out=g["X"][:, :, :], in_=g["MNb"][:, :, 0:128], func=AF.Copy), reads=[K_("MNb")], writes=[K_("X")])
            kb.op("dve", lambda e: e.tensor_tensor(out=g["P"][:, :, :], in0=g["MNb"][:, :, 0:128], in1=identc[:, None, :].to_broadcast([128, 4, 128]), op=ALU.add),
                  reads=[K_("MNb"), "identc"], writes=[K_("P")])
            yield
            nlev = 0
            while (1 << (nlev + 1)) < Lc:
                nlev += 1
            for lev in range(nlev if Lc > 1 else 0):
                lastlev = (lev == nlev - 1)
                pb1 = kp1 = None
                if not lastlev:
                    pb1, kp1 = bank()
                    for u in range(4):
                        kb.op("pe", lambda e, u=u, pb1=pb1: e.matmul(pb1[:, u * 128:(u + 1) * 128], lhsT=g["Xt"][:, u, :], rhs=g["X"][:, u, :], start=True, stop=True),
                              reads=[K_("X"), K_("Xt")], writes=[kp1])
                pb2, kp2 = bank()
                for u in range(4):
                    kb.op("pe", lambda e, u=u, pb2=pb2: e.matmul(pb2[:, u * 128:(u + 1) * 128], lhsT=g["X"][:, u, :], rhs=g["Xt"][:, u, :], start=True, stop=True),
                          reads=[K_("X"), K_("Xt")], writes=[kp2])
                if not lastlev:
                    self.evac(g["X"][:, :, :], pb1[:, :].rearrange("p (u m) -> p u m", u=4), [kp1], [K_("X")])
                self.evac(g["Xt"][:, :, :], pb2[:, :].rearrange("p (u m) -> p u m", u=4), [kp2], [K_("Xt")])
                pb3, kp3 = bank()
                for u in range(4):
                    kb.op("pe", lambda e, u=u, pb3=pb3: e.matmul(pb3[:, u * 128:(u + 1) * 128], lhsT=g["Xt"][:, u, :], rhs=g["P"][:, u, :], start=True, stop=True),
                          reads=[K_("Xt"), K_("P")], writes=[kp3])
                kb.op("dve", lambda e, pb3=pb3: e.tensor_tensor(out=g["P"][:, :, :], in0=pb3[:, :].rearrange("p (u m) -> p u m", u=4), in1=g["P"][:, :, :], op=ALU.add),
                      reads=[kp3, K_("P")], writes=[K_("P")])
                yield
            kS = [("ST", 4 * gi + u) for u in range(4)]
            kSb = [kSTb(4 * gi + u) for u in range(4)]
            kVb = [kVpb(4 * gi + u) for u in range(4)]
            pb_, kpb = bank()
            for u in range(4):
                hh = 4 * gi + u
                kb.op("pe", lambda e, u=u, hh=hh, pb_=pb_: e.matmul(pb_[:, u * 64:(u + 1) * 64], lhsT=g["UBD"][:, u, 0, :], rhs=STb[:, hh, :], start=True, stop=False),
                      reads=[K_("UBD"), kSb[u]], writes=[kpb])
                kb.op("pe", lambda e, u=u, hh=hh, pb_=pb_: e.matmul(pb_[:, u * 64:(u + 1) * 64], lhsT=g["MNk"][:, u, 0:128], rhs=Vpb[:, hh, :], start=False, stop=True),
                      reads=[K_("MNk"), kVb[u]], writes=[kpb])
            self.evac(g["RHS"][:, :, :], pb_[:, 0:256].rearrange("p (u m) -> p u m", u=4), [kpb], [K_("RHS")])
            pb_, kpb = bank()
            for u in range(4):
                kb.op("pe", lambda e, u=u, pb_=pb_: e.matmul(pb_[:, u * 64:(u + 1) * 64], lhsT=g["P"][:, u, :], rhs=g["RHS"][:, u, :], start=True, stop=True),
                      reads=[K_("P"), K_("RHS")], writes=[kpb])
            self.evac(g["U"][:, :, :], pb_[:, 0:256].rearrange("p (u m) -> p u m", u=4), [kpb], [K_("U")])
            yield
            pb_, kpb = bank()
            for u in range(4):
                hh = 4 * gi + u
                kb.op("pe", lambda e, u=u, hh=hh, pb_=pb_: e.matmul(pb_[:, u * 64:(u + 1) * 64], lhsT=g["UBD"][:, u, 1, :], rhs=STb[:, hh, :], start=True, stop=False),
                      reads=[K_("UBD"), kSb[u]], writes=[kpb])
                kb.op("pe", lambda e, u=u, pb_=pb_: e.matmul(pb_[:, u * 64:(u + 1) * 64], lhsT=g["MNb"][:, u, 128:256], rhs=g["U"][:, u, :], start=False, stop=False),
                      reads=[K_("MNb"), K_("U")], writes=[kpb])
                kb.op("pe", lambda e, u=u, hh=hh, pb_=pb_: e.matmul(pb_[:, u * 64:(u + 1) * 64], lhsT=g["MNk"][:, u, 128:256], rhs=Vpb[:, hh, :], start=False, stop=True),
                      reads=[K_("MNk"), kVb[u]], writes=[kpb])
            self.evac(Ych[:, hs, :], pb_[:, 0:256].rearrange("p (u m) -> p u m", u=4), [kpb], [("Ych", 4 * gi + u) for u in range(4)])
            pb_, kpb = bank()
            for u in range(4):
                hh = 4 * gi + u
                kb.op("pe", lambda e, u=u, pb_=pb_: e.matmul(pb_[:, u * 64:(u + 1) * 64], lhsT=g["Btm"][:, u, :], rhs=g["U"][:, u, :], start=True, stop=False),
                      reads=[K_("Btm"), K_("U")], writes=[kpb])
                kb.op("pe", lambda e, u=u, hh=hh, pb_=pb_: e.matmul(pb_[:, u * 64:(u + 1) * 64], lhsT=g["Ktm"][:, u, :], rhs=Vpb[:, hh, :], start=False, stop=True),
                      reads=[K_("Ktm"), kVb[u]], writes=[kpb])
            kb.op("dve", lambda e: e.tensor_tensor(out=ST[:, hs, :], in0=ST[:, hs, :], in1=wc[:, hs, cidx:cidx + 1].to_broadcast([128, 4, 64]), op=ALU.mult),
                  reads=kS + ["wc"], writes=kS)
            kb.op("dve", lambda e, pb_=pb_: e.tensor_tensor(out=ST[:, hs, :], in0=ST[:, hs, :], in1=pb_[:, 0:256].rearrange("p (u m) -> p u m", u=4), op=ALU.add),
                  reads=kS + [kpb], writes=kS)
            if self.CORE_BF16:
                kb.op("act", lambda e: e.activation(out=STb[:, hs, :], in_=ST[:, hs, :], func=AF.Copy), reads=kS, writes=kSb)
            yield

        def run_core(col0, Lc, cidx):
            alive = [core(gi, col0, Lc, cidx) for gi in range(NG)]
            while alive:
                for g_ in list(alive):
                    try:
                        next(g_)
                    except StopIteration:
                        alive.remove(g_)

        def post(t0, L, ti, bi):
            kb.dma("sp", ytm[:L, :], S["ys"][t0:t0 + L, :], reads=[("ys", ti)], writes=["rp"])
            kb.dma("sp", zrt[:L, :], S["z"][t0:t0 + L, O_ZR:O_ZR + D], reads=z_all(ti), writes=["rp"])
            kb.op("act", lambda e: e.activation(out=zrt[:L, :], in_=zrt[:L, :], func=AF.Silu), reads=["rp"], writes=["rp"])
            hv = lambda x: x[:L, :].rearrange("t (h j) -> t h j", h=16)
            kb.op("dve", lambda e: e.tensor_reduce(out=yst[:L, 0:16], in_=hv(ytm), axis=AX.X, op=ALU.add), reads=["rp"], writes=["yst"])
            kb.op("dve", lambda e: e.tensor_scalar(out=yst[:L, 0:16], in0=yst[:L, 0:16], scalar1=-1.0 / 64.0, scalar2=None, op0=ALU.mult), reads=["yst"], writes=["yst"])
            kb.op("dve", lambda e: e.tensor_tensor(out=hv(ytm), in0=hv(ytm), in1=yst[:L, 0:16].unsqueeze(2).to_broadcast([L, 16, 64]), op=ALU.add),
                  reads=["rp", "yst"], writes=["rp"])
            kb.op("dve", lambda e: e.tensor_tensor(out=tv["lw"][:L, :], in0=ytm[:L, :], in1=ytm[:L, :], op=ALU.mult), reads=["rp"], writes=[("tv", "lw")])
            kb.op("dve", lambda e: e.tensor_reduce(out=yst[:L, 16:32], in_=hv(tv["lw"]), axis=AX.X, op=ALU.add), reads=[("tv", "lw")], writes=["yst"])
            kb.op("act", lambda e: e.activation(out=yst[:L, 32:48], in_=yst[:L, 16:32], func=AF.Sqrt, scale=1.0 / 64.0, bias=e12[:L, 0:1]), reads=["yst", "e12"], writes=["yst"])
            kb.op("dve", lambda e: e.reciprocal(out=yst[:L, 48:64], in_=yst[:L, 32:48]), reads=["yst"], writes=["yst"])
            kb.op("dve", lambda e: e.tensor_tensor(out=hv(ytm), in0=hv(ytm), in1=yst[:L, 48:64].unsqueeze(2).to_broadcast([L, 16, 64]), op=ALU.mult),
                  reads=["rp", "yst"], writes=["rp"])
            kb.op("dve", lambda e: e.tensor_tensor(out=ytm[:L, :], in0=ytm[:L, :], in1=P_["r_ln_g"][:L, :], op=ALU.mult), reads=["rp", "r_ln_g"], writes=["rp"])
            kb.op("dve", lambda e: e.tensor_tensor(out=ytm[:L, :], in0=ytm[:L, :], in1=P_["r_ln_b"][:L, :], op=ALU.add), reads=["rp", "r_ln_b"], writes=["rp"])
            kb.op("dve", lambda e: e.tensor_tensor(out=hv(tv["lw"]), in0=xr[:L, 2 * D:3 * D].rearrange("t (h j) -> t h j", h=16),
                                                   in1=ssq[:L, 32:48].unsqueeze(2).to_broadcast([L, 16, 64]), op=ALU.mult), reads=["xr", "ssq2"], writes=[("tv", "lw")])
            kb.op("dve", lambda e: e.tensor_tensor(out=ytm[:L, :], in0=ytm[:L, :], in1=tv["lw"][:L, :], op=ALU.add), reads=["rp", ("tv", "lw")], writes=["rp"])
            kb.op("dve", lambda e: e.tensor_tensor(out=ytm[:L, :], in0=ytm[:L, :], in1=zrt[:L, :], op=ALU.mult), reads=["rp", "rp"], writes=["rp"])
            aj = self.rr("raob", 2)
            for half in range(2):
                p = self.rr("ptr", 2)
                pt, kpt = self.ptr[p], "ptr%d" % p
                for k4 in range(4):
                    kc = half * 4 + k4
                    kb.op("pe", lambda e, k4=k4, kc=kc, pt=pt: e.transpose(out=pt[:, k4 * 128:k4 * 128 + L], in_=ytm[:L, kc * 128:(kc + 1) * 128],
                                                                       identity=self.ident[:L, :L]), reads=["rp", "ident"], writes=[kpt])
                self.evac(aob[aj][:, half * 4:half * 4 + 4, :L], pt[:, :].rearrange("p (k t) -> p k t", k=4)[:, :, :L], [kpt], ["raob%d" % aj])
            kb.dma("pool", S["act_r"][:, t0:t0 + L].rearrange("(k p) t -> p k t", p=128), aob[aj][:, :, :L], reads=["raob%d" % aj], writes=[("act", "r", bi)])

        natx8 = sbp("rnatx8", (128, 8, 128))
        kb.op("pool", lambda e: e.memset(natx8[:], 0.0), writes=["natx8"])

        def bd_transpose(src, ksrc, dst, kdst, also_b=None):
            for half in range(2):
                ps = slice(half * 64, half * 64 + 64)
                if half:
                    kb.op("act", lambda e, ps=ps: e.activation(out=natx8[ps, :, ps], in_=src[ps, :, :], func=AF.Copy), reads=ksrc, writes=["natx8"])
                else:
                    kb.op("dve", lambda e, ps=ps: e.tensor_copy(out=natx8[ps, :, ps], in_=src[ps, :, :]), reads=ksrc, writes=["natx8"])
            for q4 in range(2):
                pb_, kpb = bank()
                for u in range(4):
                    kb.op("pe", lambda e, u=u, q4=q4, pb_=pb_: e.transpose(out=pb_[:, u * 128:(u + 1) * 128], in_=natx8[:, q4 * 4 + u, :], identity=self.ident[:, :]),
                          reads=["natx8", "ident"], writes=[kpb])
                for half in range(2):
                    ps = slice(half * 64, half * 64 + 64)
                    kb.op("dve" if half else "act", (lambda e, ps=ps, q4=q4, pb_=pb_: e.tensor_copy(out=dst[ps, q4 * 4:q4 * 4 + 4, :], in_=pb_[:, :].rearrange("p (u m) -> p u m", u=4)[ps, :, ps]))
                          if half else (lambda e, ps=ps, q4=q4, pb_=pb_: e.activation(out=dst[ps, q4 * 4:q4 * 4 + 4, :], in_=pb_[:, :].rearrange("p (u m) -> p u m", u=4)[ps, :, ps], func=AF.Copy)),
                          reads=[kpb], writes=kdst)

        def state_out(dst):
            bd_transpose(ST, [("ST", h) for h in range(8)], nato, ["nato"])
            kb.dma("pool", dst.rearrange("(hh hl) i j -> (hl i) hh j", hl=2), nato[:, :, :], reads=["nato"], writes=[], is_output=True)

        def state_in(src):
            kb.dma("sp", nat[:, :, :], src.rearrange("(hh hl) i j -> (hl i) hh j", hl=2), writes=["nat"])
            bd_transpose(nat, ["nat"], ST, [("ST", h) for h in range(8)])
            if self.CORE_BF16:
                kb.op("act", lambda e: e.activation(out=STb[:, :, :], in_=ST[:, :, :], func=AF.Copy), reads=[("ST", h) for h in range(8)], writes=[("STb", h) for h in range(8)])

        blk_of = lambda t: next(i for i, (b0, bn) in enumerate(self.blocks) if b0 <= t < b0 + bn)
        for ti in range(T // 128):
            t0 = ti * 128
            prep(t0, 128, ti, False)
            for c in range(2):
                for half in range(2):
                    kb.dma("sp", Vp[half * 64:half * 64 + 64, :, :],
                           S["vs"][t0 + c * 64:t0 + c * 64 + 64, :].rearrange("t (hh hl i) -> hl t hh i", hl=2, i=64)[half],
                           reads=[("vs", ti)], writes=[("Vp", h) for h in range(8)])
                vp_ready()
                run_core(c * 64, 64, c)
                for half in range(2):
                    kb.dma("pool", S["ys"][t0 + c * 64:t0 + c * 64 + 64, :].rearrange("t (hh hl i) -> hl t hh i", hl=2, i=64)[half],
                           Ych[half * 64:half * 64 + 64, :, :], reads=[("Ych", h) for h in range(8)], writes=[("ys", ti)])
            post(t0, 128, ti, blk_of(t0))
        state_out(O["wkv_prompt"][l])
        if NS:
            ti = T // 128
            prep(T, NS, ti, True)
            zero_units()
            kb.op("pool", lambda e: e.memset(Vp[:], 0.0), writes=[("Vp", h) for h in range(8)])
            for b in range(NS):
                state_in(I["state_rwkv_wkv"][l, b])
                for half in range(2):
                    kb.dma("sp", Vp[half * 64:half * 64 + 1, :, :],
                           S["vs"][T + b:T + b + 1, :].rearrange("t (hh hl i) -> hl t hh i", hl=2, i=64)[half],
                           reads=[("vs", ti)], writes=[("Vp", h) for h in range(8)])
                vp_ready()
                run_core(b, 1, b)
                for half in range(2):
                    kb.dma("pool", S["ys"][T + b:T + b + 1, :].rearrange("t (hh hl i) -> hl t hh i", hl=2, i=64)[half],
                           Ych[half * 64:half * 64 + 1, :, :], reads=[("Ych", h) for h in range(8)], writes=[("ys", ti)])
                state_out(O["wkv_sample"][l, b])
            post(T, NS, ti, blk_of(T))
        self.end_phase()

    def sin_to(self, tkey, out, ang, tf, ti, tm, key_out, key_ang, shift=0.0):
        kb = self.kb
        TWO_PI = 2.0 * math.pi
        kt = ("sin_tmp", tkey)
        kb.op("dve", lambda e: e.tensor_scalar(out=tm, in0=ang, scalar1=shift, scalar2=None, op0=ALU.add),
              reads=[key_ang], writes=[(kt, "m")])
        kb.op("dve", lambda e: e.tensor_scalar(out=tf, in0=tm, scalar1=1.0 / TWO_PI, scalar2=None, op0=ALU.mult),
              reads=[(kt, "m")], writes=[(kt, "f")])
        kb.op("dve", lambda e: e.tensor_copy(out=ti, in_=tf), reads=[(kt, "f")], writes=[(kt, "i")])
        kb.op("dve", lambda e: e.tensor_copy(out=tf, in_=ti), reads=[(kt, "i")], writes=[(kt, "f")])
        kb.op("dve", lambda e: e.scalar_tensor_tensor(out=tm, in0=tf, scalar=-TWO_PI, in1=tm, op0=ALU.mult, op1=ALU.add),
              reads=[(kt, "f"), (kt, "m")], writes=[(kt, "m")])
        kb.op("dve", lambda e: e.tensor_scalar(out=tf, in0=tm, scalar1=math.pi, scalar2=-TWO_PI, op0=ALU.is_gt, op1=ALU.mult),
              reads=[(kt, "m")], writes=[(kt, "f")])
        kb.op("dve", lambda e: e.tensor_tensor(out=tm, in0=tm, in1=tf, op=ALU.add), reads=[(kt, "f"), (kt, "m")],
              writes=[(kt, "m")])
        kb.op("dve", lambda e: e.tensor_scalar(out=tf, in0=tm, scalar1=-math.pi, scalar2=TWO_PI, op0=ALU.is_lt, op1=ALU.mult),
              reads=[(kt, "m")], writes=[(kt, "f")])
        kb.op("dve", lambda e: e.tensor_tensor(out=tm, in0=tm, in1=tf, op=ALU.add), reads=[(kt, "f"), (kt, "m")],
              writes=[(kt, "m")])
        kb.op("dve", lambda e: e.tensor_scalar(out=tm, in0=tm, scalar1=-3.1415925, scalar2=3.1415925, op0=ALU.max, op1=ALU.min),
              reads=[(kt, "m")], writes=[(kt, "m")])
        kb.op("act", lambda e: e.activation(out=out, in_=tm, func=AF.Sin), reads=[(kt, "m")], writes=[key_out])

    def s5_phase(self, l):
        kb = self.kb
        I, S, O = self.I, self.S, self.O
        T, NS = self.T, self.NS
        self.begin_phase()
        sbp = self.sbp
        PT = sbp("PT", (128, 6, 32))
        Bz = [sbp("Bz%d" % i, (128, 32, 128), BF16) for i in range(2)]
        Cz = [sbp("Cz%d" % i, (128, 32, 128), BF16) for i in range(2)]
        cosT = sbp("cosT", (128, 32, 256)); sinT = sbp("sinT", (128, 32, 256))
        car = [sbp("car%d" % i, (128, 32)) for i in range(2)]
        ctmp = sbp("ctmp", (128, 4))
        dcol = sbp("dcol", (128, 8)); gbcol = sbp("gbcol", (128, 8))
        wg = sbp("wglu", (128, KC, D), BF16)
        s0T = [sbp("s0T%d" % i, (128, 32, 16)) for i in range(2)]
        snw = [sbp("snw%d" % i, (128, 32, 16)) for i in range(2)]
        sst = sbp("sst", (32, 512))
        outer_pes = self.pes
        self.pes = ExitStack()
        nat = sbp("nat", (32, 14, 128))
        nati = sbp("nati", (32, 128), I32)
        BR = sbp("BR", (128, 32, 16)); BI = sbp("BI", (128, 32, 16))
        bbr = sbp("bbr", (128, 32, 16)); bbi = sbp("bbi", (128, 32, 16))
        xin = sbp("xin", (128, 512))
        btmp = xin[:, :].rearrange("p (s c) -> p s c", c=16)
        Cn = [sbp("Cn%d" % i, (128, 8, 64)) for i in range(2)]
        ang = sbp("ang", (128, 2, 256)); angf = sbp("angf", (128, 2, 256)); angm = sbp("angm", (128, 2, 256))
        angi = sbp("angi", (128, 2, 256), I32)
        m3 = sbp("m3", (128, 4, 2)); m4 = sbp("m4", (128, 4, 8)); iota1 = sbp("iota1", (128, 256))
        wstg = [sbp("wstg%d" % i, (128, KC, 128)) for i in range(1)]

        kb.dma("sp", m3[:], I["mask3"], writes=["m3"])
        kb.dma("sp", m4[:], I["mask4"], writes=["m4"])
        kb.dma("sp", iota1[:], I["iota1"], writes=["iota1"])
        kb.dma("sp", nat[:, 0, :], I["s_lam_re"][l].rearrange("(s g) p -> s (g p)", g=2), writes=["nat0"])
        kb.dma("sp", nat[:, 1, :], I["s_lam_im"][l].rearrange("(s g) p -> s (g p)", g=2), writes=["nat1"])
        kb.dma("sp", nat[:, 2, 0:2], I["s_log_dt"][l].rearrange("(s g) -> s g", g=2), writes=["nat2"])
        kb.dma("sp", dcol[:], I["s_d"][l].rearrange("(k p) -> p k", p=128), writes=["dcol"], allow_slow_non_contiguous=True)
        kb.dma("sp", gbcol[:], I["s_glu_b"][l].rearrange("(k p) -> p k", p=128), writes=["gbcol"], allow_slow_non_contiguous=True)
        kb.dma("sp", BR[:], I["s_b_re"][l].rearrange("(s g) p c -> (g p) s c", g=2), writes=["BR"])
        kb.dma("sp", BI[:], I["s_b_im"][l].rearrange("(s g) p c -> (g p) s c", g=2), writes=["BI"])
        kb.dma("sp", Cn[0][:], I["s_c_re"][l].rearrange("(o g) c p -> (g c) o p", g=8), writes=["Cn0"])
        kb.dma("sp", Cn[1][:], I["s_c_im"][l].rearrange("(o g) c p -> (g c) o p", g=8), writes=["Cn1"])
        for h in range(8):
            kb.dma("sp", wstg[0][:], I["s_glu_w"][l][:, h * 128:(h + 1) * 128].rearrange("(k p) c -> p k c", p=128),
                   writes=["wstg"])
            kb.op("dve", lambda e, h=h: e.tensor_copy(out=wg[:, :, h * 128:(h + 1) * 128], in_=wstg[0][:]),
                  reads=["wstg"], writes=["wg"])
        N = lambda i: nat[:, i, :]
        kb.op("act", lambda e: e.activation(out=nat[:, 2, 2:4], in_=nat[:, 2, 0:2], func=AF.Exp), reads=["nat2"], writes=["nat2"])
        kb.op("dve", lambda e: e.tensor_copy(out=nat[:, 3, :].rearrange("s (g p) -> s g p", g=2),
                                             in_=nat[:, 2, 2:4].unsqueeze(2).to_broadcast([32, 2, 64])),
              reads=["nat2"], writes=["nat3"])
        kb.op("dve", lambda e: e.tensor_scalar(out=N(0), in0=N(0), scalar1=-1e-4, scalar2=None, op0=ALU.min),
              reads=["nat0"], writes=["nat0"])
        kb.op("dve", lambda e: e.tensor_tensor(out=N(4), in0=N(0), in1=N(3), op=ALU.mult), reads=["nat0", "nat3"], writes=["nat4"])
        kb.op("act", lambda e: e.activation(out=N(4), in_=N(4), func=AF.Exp), reads=["nat4"], writes=["nat4"])
        kb.op("dve", lambda e: e.tensor_tensor(out=N(5), in0=N(1), in1=N(3), op=ALU.mult), reads=["nat1", "nat3"], writes=["nat5"])
        self.sin_to("nat", N(6), N(5), N(12), nati[:, :], N(13), "nat6", "nat5")
        self.sin_to("nat", N(7), N(5), N(12), nati[:, :], N(13), "nat7", "nat5", shift=math.pi / 2)
        kb.op("dve", lambda e: e.tensor_tensor(out=N(8), in0=N(4), in1=N(7), op=ALU.mult), reads=["nat4", "nat7"], writes=["nat8"])
        kb.op("dve", lambda e: e.tensor_tensor(out=N(9), in0=N(4), in1=N(6), op=ALU.mult), reads=["nat4", "nat6"], writes=["nat9"])
        kb.op("dve", lambda e: e.tensor_tensor(out=N(12), in0=N(0), in1=N(0), op=ALU.mult), reads=["nat0"], writes=["nat12"])
        kb.op("dve", lambda e: e.tensor_tensor(out=N(13), in0=N(1), in1=N(1), op=ALU.mult), reads=["nat1"], writes=["nat13"])
        kb.op("dve", lambda e: e.tensor_tensor(out=N(12), in0=N(12), in1=N(13), op=ALU.add), reads=["nat12", "nat13"], writes=["nat12"])
        kb.op("dve", lambda e: e.reciprocal(out=N(12), in_=N(12)), reads=["nat12"], writes=["nat12"])
        kb.op("dve", lambda e: e.tensor_scalar(out=N(13), in0=N(8), scalar1=-1.0, scalar2=None, op0=ALU.add), reads=["nat8"], writes=["nat13"])
        kb.op("dve", lambda e: e.tensor_tensor(out=N(10), in0=N(13), in1=N(0), op=ALU.mult), reads=["nat13", "nat0"], writes=["nat10"])
        kb.op("dve", lambda e: e.tensor_tensor(out=N(11), in0=N(9), in1=N(1), op=ALU.mult), reads=["nat9", "nat1"], writes=["nat11"])
        kb.op("dve", lambda e: e.tensor_tensor(out=N(10), in0=N(10), in1=N(11), op=ALU.add), reads=["nat10", "nat11"], writes=["nat10"])
        kb.op("dve", lambda e: e.tensor_tensor(out=N(10), in0=N(10), in1=N(12), op=ALU.mult), reads=["nat10", "nat12"], writes=["nat10"])
        kb.op("dve", lambda e: e.tensor_tensor(out=N(11), in0=N(9), in1=N(0), op=ALU.mult), reads=["nat9", "nat0"], writes=["nat11"])
        kb.op("dve", lambda e: e.tensor_tensor(out=N(13), in0=N(13), in1=N(1), op=ALU.mult), reads=["nat13", "nat1"], writes=["nat13"])
        kb.op("dve", lambda e: e.tensor_tensor(out=N(11), in0=N(11), in1=N(13), op=ALU.subtract), reads=["nat11", "nat13"], writes=["nat11"])
        kb.op("dve", lambda e: e.tensor_tensor(out=N(11), in0=N(11), in1=N(12), op=ALU.mult), reads=["nat11", "nat12"], writes=["nat11"])
        p = self.rr("ptr", 2)
        pt, kpt = self.ptr[p], "ptr%d" % p
        for si, ni in enumerate((4, 5, 8, 9, 10, 11)):
            kb.op("pe", lambda e, si=si, ni=ni, pt=pt: e.transpose(out=pt[:, si * 32:(si + 1) * 32], in_=nat[:, ni, :],
                                                                  identity=self.ident[:32, :32]),
                  reads=["nat%d" % ni, "ident"], writes=[kpt])
        kb.op("dve", lambda e, pt=pt: e.tensor_copy(out=PT[:, :, :], in_=pt[:, 0:192].rearrange("p (s c) -> p s c", s=6)),
              reads=[kpt], writes=["PT"])
        bc = lambda si: PT[:, si, :].unsqueeze(2).to_broadcast([128, 32, 16])
        kb.op("dve", lambda e: e.tensor_tensor(out=bbr[:], in0=BR[:], in1=bc(4), op=ALU.mult), reads=["BR", "PT"], writes=["bbr"])
        kb.op("dve", lambda e: e.tensor_tensor(out=btmp, in0=BI[:], in1=bc(5), op=ALU.mult), reads=["BI", "PT"], writes=["xin"])
        kb.op("dve", lambda e: e.tensor_tensor(out=bbr[:], in0=bbr[:], in1=btmp, op=ALU.subtract), reads=["bbr", "xin"], writes=["bbr"])
        kb.op("dve", lambda e: e.tensor_tensor(out=bbi[:], in0=BI[:], in1=bc(4), op=ALU.mult), reads=["BI", "PT"], writes=["bbi"])
        kb.op("dve", lambda e: e.tensor_tensor(out=btmp, in0=BR[:], in1=bc(5), op=ALU.mult), reads=["BR", "PT", "bbr"], writes=["xin"])
        kb.op("dve", lambda e: e.tensor_tensor(out=bbi[:], in0=bbi[:], in1=btmp, op=ALU.add), reads=["bbi", "xin"], writes=["bbi"])
        for ri, (bb, kbb) in enumerate(((bbr, "bbr"), (bbi, "bbi"))):
            for oc in range(8):
                kb.op("dve", lambda e, bb=bb, oc=oc: e.tensor_tensor(
                    out=xin[:, :].rearrange("p (q g c) -> p q g c", q=4, g=8),
                    in0=bb[:, oc * 4:oc * 4 + 4, None, :].to_broadcast([128, 4, 8, 16]),
                    in1=m4[:, :, :, None].to_broadcast([128, 4, 8, 16]), op=ALU.mult),
                    reads=[kbb, "m4"], writes=["xin"])
                p = self.rr("ptr", 2)
                pt, kpt = self.ptr[p], "ptr%d" % p
                for q in range(4):
                    kb.op("pe", lambda e, q=q, pt=pt: e.transpose(out=pt[:, q * 128:(q + 1) * 128], in_=xin[:, q * 128:(q + 1) * 128],
                                                                  identity=self.ident[:, :]),
                          reads=["xin", "ident"], writes=[kpt])
                kb.op("act", lambda e, pt=pt, oc=oc, ri=ri: e.activation(
                    out=Bz[ri][:, oc * 4:oc * 4 + 4, :], in_=pt[:, :].rearrange("p (q m) -> p q m", q=4), func=AF.Copy),
                    reads=[kpt], writes=[("Bz", ri)])
        for ri in range(2):
            for oc in range(8):
                kb.op("dve", lambda e, ri=ri, oc=oc: e.tensor_tensor(
                    out=xin[:, :].rearrange("p (q g s) -> p q g s", q=4, g=2),
                    in0=Cn[ri][:, oc, None, None, :].to_broadcast([128, 4, 2, 64]),
                    in1=m3[:, :, :, None].to_broadcast([128, 4, 2, 64]), op=ALU.mult),
                    reads=["Cn%d" % ri, "m3"], writes=["xin"])
                p = self.rr("ptr", 2)
                pt, kpt = self.ptr[p], "ptr%d" % p
                for q in range(4):
                    kb.op("pe", lambda e, q=q, pt=pt: e.transpose(out=pt[:, q * 128:(q + 1) * 128], in_=xin[:, q * 128:(q + 1) * 128],
                                                                  identity=self.ident[:, :]),
                          reads=["xin", "ident"], writes=[kpt])
                kb.op("act", lambda e, pt=pt, oc=oc, ri=ri: e.activation(
                    out=Cz[ri][:, oc * 4:oc * 4 + 4, :], in_=pt[:, :].rearrange("p (q m) -> p q m", q=4), func=AF.Copy,
                    scale=(1.0 if ri == 0 else -1.0)),
                    reads=[kpt], writes=[("Cz", ri)])
        for g in range(16):
            for s8 in range(2):
                sc = g * 2 + s8
                kb.op("dve", lambda e, s8=s8, sc=sc: e.tensor_scalar(out=ang[:, s8, :], in0=iota1[:, :], scalar1=PT[:, 1, sc:sc + 1],
                                                                     scalar2=None, op0=ALU.mult),
                      reads=["iota1", "PT"], writes=["ang"])
            self.sin_to("ang", sinT[:, g * 2:(g + 1) * 2, :], ang[:], angf[:], angi[:], angm[:], ("sinT", g), "ang")
            self.sin_to("ang", cosT[:, g * 2:(g + 1) * 2, :], ang[:], angf[:], angi[:], angm[:], ("cosT", g), "ang",
                        shift=math.pi / 2)
        tabs = [("sinT", g) for g in range(16)] + [("cosT", g) for g in range(16)]
        kb.op("dve", lambda e: e.memset(car[0][:], 0.0), writes=[("car", 0, sc_) for sc_ in range(32)])
        kb.op("dve", lambda e: e.memset(car[1][:], 0.0), writes=[("car", 1, sc_) for sc_ in range(32)])

        if NS:
            for ri, nm in enumerate(("state_s5_re", "state_s5_im")):
                p = self.rr("ptr", 2)
                pt, kpt = self.ptr[p], "ptr%d" % p
                for qq in range(8):
                    kb.dma("sp", sst[:NS, :], I[nm][l].rearrange("b g p -> b (g p)")[:, qq * 512:(qq + 1) * 512], writes=["sst"])
                    for s8 in range(4):
                        sc = qq * 4 + s8
                        kb.op("pe", lambda e, sc=sc, s8=s8, pt=pt: e.transpose(out=pt[:, sc * NS:(sc + 1) * NS], in_=sst[:NS, s8 * 128:(s8 + 1) * 128],
                                                                      identity=self.ident[:NS, :NS]),
                              reads=["sst", "ident"], writes=[kpt])
                kb.op("dve", lambda e, pt=pt, ri=ri: e.tensor_copy(out=s0T[ri][:, :, :NS],
                                                                  in_=pt[:, :32 * NS].rearrange("p (s b) -> p s b", s=32)),
                      reads=[kpt], writes=[("s0T", ri)])

        kb.barrier()
        self.pes.close()
        self.pes = outer_pes
        uf = [sbp("uf%d" % i, (128, 256)) for i in range(2)]
        ub = [sbp("ub%d" % i, (128, 256), BF16) for i in range(2)]
        WS = []
        for w_ in range(2):
            WS.append(([sbp("tq%d_%d" % (w_, i), (128, 256)) for i in range(4)],
                       [sbp("bh%d_%d" % (w_, i), (128, 256)) for i in range(2)],
                       [sbp("sh%d_%d" % (w_, i), (128, 256)) for i in range(2)],
                       [sbp("sbf%d_%d" % (w_, i), (128, 256), BF16) for i in range(2)]))
        ysg = sbp("ysg", (128, KC, 256)); ysb = sbp("ysb", (128, KC, 256), BF16)
        yt = [sbp("yt%d" % i, (128, 256)) for i in range(2)]
        zst = [sbp("zst%d" % i, (128, 256)) for i in range(2)]
        aout = [sbp("aout%d" % i, (128, 256), BF16) for i in range(2)]
        subblocks = []
        for bi, (t0b, nb) in enumerate(self.blocks):
            for t0 in range(t0b, t0b + nb, 256):
                subblocks.append((bi, t0, min(256, t0b + nb - t0)))
        for (bi, t0, n) in subblocks:
            is_s = (t0 >= T)
            nsub = n // 256
            for oc in range(8):
                j = self.rr("uf", 2)
                kuf, kub = "uf%d" % j, "ub%d" % j
                row = O_U + oc * 128
                kb.dma("sp", uf[j][:, :n], S["zT"][row:row + 128, t0:t0 + n], reads=[("zT", row, bi)], writes=[kuf])
                kb.op("act", lambda e, j=j: e.activation(out=ub[j][:, :n], in_=uf[j][:, :n], func=AF.Copy), reads=[kuf], writes=[kub])
                py = self.rr("ptr", 2)
                pys, kpys = self.ptr[py], "ptr%d" % py
                def sc_gen(q, W):
                    tq, bh, sh, sbf = W
                    wk = lambda n_: (n_, id(W))
                    sc = oc * 4 + q
                    pa = self.rr("pp", 4); pb = self.rr("pp", 4)
                    A, B = self.pp[pa], self.pp[pb]
                    kA, kB = "pp%d" % pa, "pp%d" % pb
                    kb.op("pe", lambda e, A=A, sc=sc, j=j: e.matmul(A[:, :n], lhsT=Bz[0][:, sc, :], rhs=ub[j][:, :n], start=True, stop=True),
                          reads=[("Bz", 0), kub], writes=[kA])
                    kb.op("pe", lambda e, B=B, sc=sc, j=j: e.matmul(B[:, :n], lhsT=Bz[1][:, sc, :], rhs=ub[j][:, :n], start=True, stop=True),
                          reads=[("Bz", 1), kub], writes=[kB])
                    yield
                    if not is_s:
                        v3 = lambda x: x[:, :n].rearrange("p (m j) -> p m j", j=256)
                        cosv = cosT[:, sc, None, :].to_broadcast([128, nsub, 256])
                        sinv = sinT[:, sc, None, :].to_broadcast([128, nsub, 256])
                        kb.op("dve", lambda e, A=A, cosv=cosv: e.tensor_tensor(out=v3(tq[0]), in0=v3(A), in1=cosv, op=ALU.mult),
                              reads=[kA] + tabs, writes=[wk("tq0")])
                        kb.op("dve", lambda e, B=B, sinv=sinv: e.tensor_tensor(out=v3(tq[1]), in0=v3(B), in1=sinv, op=ALU.mult),
                              reads=[kB] + tabs, writes=[wk("tq1")])
                        kb.op("dve", lambda e, B=B, cosv=cosv: e.tensor_tensor(out=v3(tq[2]), in0=v3(B), in1=cosv, op=ALU.mult),
                              reads=[kB] + tabs, writes=[wk("tq2")])
                        kb.op("dve", lambda e, A=A, sinv=sinv: e.tensor_tensor(out=v3(tq[3]), in0=v3(A), in1=sinv, op=ALU.mult),
                              reads=[kA] + tabs, writes=[wk("tq3")])
                        kb.op("pool", lambda e: e.tensor_tensor(out=bh[0][:, :n], in0=tq[0][:, :n], in1=tq[1][:, :n], op=ALU.add),
                              reads=[wk("tq0"), wk("tq1")], writes=[wk("bh0")])
                        kb.op("pool", lambda e: e.tensor_tensor(out=bh[1][:, :n], in0=tq[2][:, :n], in1=tq[3][:, :n], op=ALU.subtract),
                              reads=[wk("tq2"), wk("tq3")], writes=[wk("bh1")])
                        yield
                        rho = PT[:, 0, sc:sc + 1].to_broadcast([128, 256])
                        c128 = cosT[:, sc, 255:256]
                        s128 = sinT[:, sc, 255:256]
                        for m in range(nsub):
                            sl = slice(m * 256, (m + 1) * 256)
                            for ri in range(2):
                                kb.op("dve", lambda e, ri=ri, sl=sl, sc=sc, rho=rho: e.tensor_tensor_scan(
                                    out=sh[ri][:, sl], data0=rho, data1=bh[ri][:, sl], initial=car[ri][:, sc:sc + 1],
                                    op0=ALU.mult, op1=ALU.add), reads=[wk("bh%d" % ri), ("car", ri, sc), "PT"], writes=[wk("sh%d" % ri)])
                            lr = sh[0][:, m * 256 + 255:m * 256 + 256]
                            li = sh[1][:, m * 256 + 255:m * 256 + 256]
                            kb.op("dve", lambda e, li=li, s128=s128: e.tensor_scalar(out=ctmp[:, 2 * (q % 2):2 * (q % 2) + 1], in0=li, scalar1=s128, scalar2=None, op0=ALU.mult),
                                  reads=[wk("sh1")] + tabs, writes=[wk("ctmp")])
                            kb.op("dve", lambda e, li=li, c128=c128: e.tensor_scalar(out=ctmp[:, 2 * (q % 2) + 1:2 * (q % 2) + 2], in0=li, scalar1=c128, scalar2=None, op0=ALU.mult),
                                  reads=[wk("sh1")] + tabs, writes=[wk("ctmp")])
                            kb.op("dve", lambda e, lr=lr, c128=c128, sc=sc: e.scalar_tensor_tensor(
                                out=car[0][:, sc:sc + 1], in0=lr, scalar=c128, in1=ctmp[:, 2 * (q % 2):2 * (q % 2) + 1], op0=ALU.mult, op1=ALU.subtract),
                                reads=[wk("sh0"), wk("ctmp")] + tabs, writes=[("car", 0, sc)])
                            kb.op("dve", lambda e, lr=lr, s128=s128, sc=sc: e.scalar_tensor_tensor(
                                out=car[1][:, sc:sc + 1], in0=lr, scalar=s128, in1=ctmp[:, 2 * (q % 2) + 1:2 * (q % 2) + 2], op0=ALU.mult, op1=ALU.add),
                                reads=[wk("sh0"), wk("ctmp")] + tabs, writes=[("car", 1, sc)])
                        yield
                        kb.op("pool", lambda e, cosv=cosv: e.tensor_tensor(out=v3(tq[0]), in0=v3(sh[0]), in1=cosv, op=ALU.mult),
                              reads=[wk("sh0")] + tabs, writes=[wk("tq0")])
                        kb.op("pool", lambda e, sinv=sinv: e.tensor_tensor(out=v3(tq[1]), in0=v3(sh[1]), in1=sinv, op=ALU.mult),
                              reads=[wk("sh1")] + tabs, writes=[wk("tq1")])
                        kb.op("dve", lambda e, sinv=sinv: e.tensor_tensor(out=v3(tq[2]), in0=v3(sh[0]), in1=sinv, op=ALU.mult),
                              reads=[wk("sh0")] + tabs, writes=[wk("tq2")])
                        kb.op("dve", lambda e, cosv=cosv: e.tensor_tensor(out=v3(tq[3]), in0=v3(sh[1]), in1=cosv, op=ALU.mult),
                              reads=[wk("sh1")] + tabs, writes=[wk("tq3")])
                        kb.op("pool", lambda e: e.tensor_tensor(out=sbf[0][:, :n], in0=tq[0][:, :n], in1=tq[1][:, :n], op=ALU.subtract),
                              reads=[wk("tq0"), wk("tq1")], writes=[wk("sbf0")])
                        kb.op("dve", lambda e: e.tensor_tensor(out=sbf[1][:, :n], in0=tq[2][:, :n], in1=tq[3][:, :n], op=ALU.add),
                              reads=[wk("tq2"), wk("tq3")], writes=[wk("sbf1")])
                    else:
                        lbre = PT[:, 2, sc:sc + 1]
                        lbim = PT[:, 3, sc:sc + 1]
                        s0r, s0i = s0T[0][:, sc, :n], s0T[1][:, sc, :n]
                        kb.op("dve", lambda e, s0i=s0i, lbim=lbim: e.tensor_scalar(out=tq[0][:, :n], in0=s0i, scalar1=lbim, scalar2=None, op0=ALU.mult),
                              reads=[("s0T", 1), "PT"], writes=[wk("tq0")])
                        kb.op("dve", lambda e, s0r=s0r, lbre=lbre: e.scalar_tensor_tensor(out=tq[0][:, :n], in0=s0r, scalar=lbre, in1=tq[0][:, :n],
                                                                               op0=ALU.mult, op1=ALU.subtract),
                              reads=[("s0T", 0), "PT", wk("tq0")], writes=[wk("tq0")])
                        kb.op("dve", lambda e, A=A, sc=sc: e.tensor_tensor(out=snw[0][:, sc, :n], in0=A[:, :n], in1=tq[0][:, :n], op=ALU.add),
                              reads=[kA, wk("tq0")], writes=[("snw", 0)])
                        kb.op("dve", lambda e, s0r=s0r, lbim=lbim: e.tensor_scalar(out=tq[1][:, :n], in0=s0r, scalar1=lbim, scalar2=None, op0=ALU.mult),
                              reads=[("s0T", 0), "PT"], writes=[wk("tq1")])
                        kb.op("dve", lambda e, s0i=s0i, lbre=lbre: e.scalar_tensor_tensor(out=tq[1][:, :n], in0=s0i, scalar=lbre, in1=tq[1][:, :n],
                                                                               op0=ALU.mult, op1=ALU.add),
                              reads=[("s0T", 1), "PT", wk("tq1")], writes=[wk("tq1")])
                        kb.op("dve", lambda e, B=B, sc=sc: e.tensor_tensor(out=snw[1][:, sc, :n], in0=B[:, :n], in1=tq[1][:, :n], op=ALU.add),
                              reads=[kB, wk("tq1")], writes=[("snw", 1)])
                        for ri in range(2):
                            kb.op("pool", lambda e, ri=ri, sc=sc: e.tensor_copy(out=sbf[ri][:, :n], in_=snw[ri][:, sc, :n]),
                                  reads=[("snw", ri)], writes=[wk("sbf%d" % ri)])
                    yield
                    for ri in range(2):
                        kb.op("pe", lambda e, ri=ri, sc=sc, q=q, pys=pys: e.matmul(pys[:, :n], lhsT=Cz[ri][:, sc, :], rhs=sbf[ri][:, :n],
                                                                                start=(q == 0 and ri == 0), stop=(q == 3 and ri == 1)),
                              reads=[("Cz", ri), wk("sbf%d" % ri)], writes=[kpys])
                for qp in range(2):
                    alive = [sc_gen(2 * qp + w_, WS[w_]) for w_ in range(2)]
                    while alive:
                        for g_ in list(alive):
                            try:
                                next(g_)
                            except StopIteration:
                                alive.remove(g_)
                y0, y1 = yt[0], yt[1]
                kb.op("dve", lambda e, j=j, oc=oc, pys=pys: e.scalar_tensor_tensor(out=y0[:, :n], in0=uf[j][:, :n], scalar=dcol[:, oc:oc + 1],
                                                                             in1=pys[:, :n], op0=ALU.mult, op1=ALU.add),
                      reads=[kuf, "dcol", kpys], writes=["yt0"])
                kb.op("act", lambda e: e.activation(out=y1[:, :n], in_=y0[:, :n], func=AF.Square), reads=["yt0"], writes=["yt1"])
                kb.op("dve", lambda e: e.tensor_scalar(out=y1[:, :n], in0=y1[:, :n], scalar1=0.044715, scalar2=1.0, op0=ALU.mult, op1=ALU.add),
                      reads=["yt1"], writes=["yt1"])
                kb.op("dve", lambda e: e.tensor_tensor(out=y1[:, :n], in0=y1[:, :n], in1=y0[:, :n], op=ALU.mult), reads=["yt1", "yt0"], writes=["yt1"])
                kb.op("act", lambda e: e.activation(out=y1[:, :n], in_=y1[:, :n], func=AF.Sigmoid, scale=2.0 * math.sqrt(2.0 / math.pi)),
                      reads=["yt1"], writes=["yt1"])
                kb.op("dve", lambda e, oc=oc: e.tensor_tensor(out=ysg[:, oc, :n], in0=y1[:, :n], in1=y0[:, :n], op=ALU.mult),
                      reads=["yt1", "yt0"], writes=[("ysg", oc)])
                kb.op("act", lambda e, oc=oc: e.activation(out=ysb[:, oc, :n], in_=ysg[:, oc, :n], func=AF.Copy), reads=[("ysg", oc)], writes=[("ysb", oc)])
            for ec in range(8):
                p = self.rr("pp", 4)
                pp, kpp = self.pp[p], "pp%d" % p
                for kc in range(KC):
                    kb.op("pe", lambda e, kc=kc, ec=ec, pp=pp: e.matmul(pp[:, :n], lhsT=wg[:, kc, ec * 128:(ec + 1) * 128], rhs=ysb[:, kc, :n],
                                                                      start=(kc == 0), stop=(kc == KC - 1)),
                          reads=["wg"] + [("ysb", k) for k in range(KC)], writes=[kpp])
                zj = self.rr("zst", 2)
                kz = "zst%d" % zj
                row = O_ZS + ec * 128
                kb.dma("sp", zst[zj][:, :n], S["zT"][row:row + 128, t0:t0 + n], reads=[("zT", row, bi)], writes=[kz])
                kb.op("act", lambda e, zj=zj: e.activation(out=zst[zj][:, :n], in_=zst[zj][:, :n], func=AF.Silu), reads=[kz], writes=[kz])
                kb.op("act", lambda e, pp=pp, ec=ec: e.activation(out=yt[0][:, :n], in_=pp[:, :n], func=AF.Sigmoid, bias=gbcol[:, ec:ec + 1]),
                      reads=[kpp, "gbcol"], writes=["yt0"])
                kb.op("dve", lambda e, ec=ec: e.tensor_tensor(out=yt[0][:, :n], in0=yt[0][:, :n], in1=ysg[:, ec, :n], op=ALU.mult),
                      reads=["yt0", ("ysg", ec)], writes=["yt0"])
                aj = self.rr("aout", 2)
                kb.op("dve", lambda e, aj=aj, zj=zj: e.tensor_tensor(out=aout[aj][:, :n], in0=yt[0][:, :n], in1=zst[zj][:, :n], op=ALU.mult),
                      reads=["yt0", kz], writes=["aout%d" % aj])
                kb.dma("pool", S["act_s"][ec * 128:(ec + 1) * 128, t0:t0 + n], aout[aj][:, :n],
                       reads=["aout%d" % aj], writes=[("act", "s", bi)])
        for ri, nm in enumerate(("s5_re", "s5_im")):
            p = self.rr("ptr", 2)
            pt, kpt = self.ptr[p], "ptr%d" % p
            kb.op("pe", lambda e, pt=pt, ri=ri: e.transpose(out=pt[:32, 0:128], in_=car[ri][:, :], identity=self.ident[:, :]),
                  reads=[("car", ri, sc_) for sc_ in range(32)] + ["ident"], writes=[kpt])
            kb.op("dve", lambda e, pt=pt: e.tensor_copy(out=sst[:32, 0:128], in_=pt[:32, 0:128]), reads=[kpt], writes=["sst"])
            kb.dma("pool", O[nm + "_prompt"][l].rearrange("(s q) -> s q", q=128), sst[:32, 0:128], reads=["sst"], writes=[],
                   is_output=True)
            if NS:
                for g in range(8):
                    p = self.rr("ptr", 2)
                    pt, kpt = self.ptr[p], "ptr%d" % p
                    for s4 in range(4):
                        sc = g * 4 + s4
                        kb.op("pe", lambda e, pt=pt, ri=ri, sc=sc, s4=s4: e.transpose(out=pt[:NS, s4 * 128:(s4 + 1) * 128], in_=snw[ri][:, sc, :NS],
                                                                                   identity=self.ident[:, :]),
                              reads=[("snw", ri), "ident"], writes=[kpt])
                    kb.op("dve", lambda e, pt=pt, g=g: e.tensor_copy(out=sst[:NS, 0:512], in_=pt[:NS, :]), reads=[kpt], writes=["sst"])
                    kb.dma("pool", O[nm + "_sample"][l].rearrange("b g p -> b (g p)")[:, g * 512:(g + 1) * 512], sst[:NS, 0:512],
                           reads=["sst"], writes=[], is_output=True)
        self.end_phase()

    def load_sq(self, name, dram):
        kb = self.kb
        dst = self.wsq[name]
        for h in range(2):
            i = self.rr("w", 2)
            ws, kws = self.wst[i], "wst%d" % i
            kb.dma("sp", ws[:, :, :512], dram[:, h * 512:(h + 1) * 512].rearrange("(k p) c -> p k c", p=128),
                   writes=[kws])
            kb.op("dve", lambda e, ws=ws, h=h: e.tensor_copy(out=dst[:, :, h * 512:(h + 1) * 512], in_=ws[:, :, :512]),
                  reads=[kws], writes=[("wsq", name)])

    def phase3(self, l):
        kb = self.kb
        I, S = self.I, self.S
        last = (l == self.DEPTH - 1)
        self.begin_phase()
        self.wst = [self.sbp("wst%d" % i, (128, KC, 512)) for i in range(2)]
        self.wsq = {n: self.sbp("wsq_" + n, (128, KC, D), BF16) for n in ("bm", "br", "bs", "out")}
        self.actb = [self.sbp("actb%d" % i, (128, KC, 512), BF16) for i in range(2)]
        self.gmt = [self.sbp("gmt%d" % i, (128, 512)) for i in range(2)]
        self.mrg = self.sbp("mrg", (128, KC, 512), BF16)
        self.macc = self.sbp("macc", (128, KC, 512))
        self.ln_alloc()
        for nme in ("bm", "br", "bs", "out"):
            self.load_sq(nme, I["w_" + nme][l])
        self.load_ln_params(I["ln_g"][l], I["ln_b"][l])
        for bi, (t0, n) in enumerate(self.blocks):
            for ec in range(KC):
                for b, bn in enumerate("mrs"):
                    pass
            acts = {}
            for b, bn in enumerate("mrs"):
                if not self.have_branch(bn):
                    continue
                j = self.rr("actb", 2)
                at, kat = self.actb[j], "actb%d" % j
                kb.dma("sp", at[:, :, :n], S["act_" + bn][:, t0:t0 + n].rearrange("(k p) t -> p k t", p=128),
                       reads=[("act", bn, bi)], writes=[kat])
                for ec in range(KC):
                    p = self.rr("pp", 4)
                    pp, kpp = self.pp[p], "pp%d" % p
                    for kc in range(KC):
                        kb.op("pe", lambda e, kc=kc, pp=pp, ec=ec, at=at, bn=bn: e.matmul(
                            pp[:, :n], lhsT=self.wsq["b" + bn][:, kc, ec * 128:(ec + 1) * 128], rhs=at[:, kc, :n],
                            start=(kc == 0), stop=(kc == KC - 1)),
                            reads=[kat, ("wsq", "b" + bn)], writes=[kpp])
                    g = self.rr("gmt", 2)
                    gt, kgt = self.gmt[g], "gmt%d" % g
                    row = O_GM + b * D + ec * 128
                    kb.dma("sp", gt[:, :n], S["zT"][row:row + 128, t0:t0 + n],
                           reads=[("zT", row, bi)], writes=[kgt])
                    kb.op("act", lambda e, gt=gt: e.activation(out=gt[:, :n], in_=gt[:, :n], func=AF.Sigmoid),
                          reads=[kgt], writes=[kgt])
                    first = (bn == self.first_branch())
                    lastb = (bn == self.last_branch())
                    kmf = ("mrgf", ec)
                    acc = self.macc[:, ec, :n]
                    if first:
                        kb.op("dve", lambda e, acc=acc, gt=gt, pp=pp: e.tensor_tensor(out=acc, in0=pp[:, :n], in1=gt[:, :n],
                                                                                 op=ALU.mult),
                              reads=[kpp, kgt], writes=[("macc", ec)])
                    else:
                        kb.op("dve", lambda e, gt=gt, pp=pp: e.tensor_tensor(out=gt[:, :n], in0=pp[:, :n], in1=gt[:, :n],
                                                                        op=ALU.mult),
                              reads=[kpp, kgt], writes=[kgt])
                        kb.op("dve", lambda e, acc=acc, gt=gt: e.tensor_tensor(out=acc, in0=acc, in1=gt[:, :n], op=ALU.add),
                              reads=[kgt, ("macc", ec)], writes=[("macc", ec)])
                    if lastb:
                        kb.op("dve", lambda e, acc=acc, ec=ec: e.tensor_copy(out=self.mrg[:, ec, :n], in_=acc),
                              reads=[("macc", ec)], writes=[("mrg", ec)])
            for tt in range(0, n, 128):
                nr = min(128, n - tt)
                r0 = t0 + tt
                ti = r0 // 128
                j = self.rr("xt", 2)
                xt, kxt = self.xt[j], "xt%d" % j
                kb.dma("sp", xt[:nr, :], S["xs"][r0:r0 + nr, :], reads=[("xs", ti)], writes=[kxt])
                if self.first_branch() is not None:
                    for h in range(2):
                        p = self.rr("pp", 4)
                        pp, kpp = self.pp[p], "pp%d" % p
                        for kc in range(KC):
                            kb.op("pe", lambda e, kc=kc, pp=pp, h=h, tt=tt, nr=nr: e.matmul(
                                pp[:nr, :], lhsT=self.mrg[:, kc, tt:tt + nr], rhs=self.wsq["out"][:, kc, h * 512:(h + 1) * 512],
                                start=(kc == 0), stop=(kc == KC - 1)),
                                reads=[("mrg", k) for k in range(KC)] + [("wsq", "out")], writes=[kpp])
                        kb.op("dve", lambda e, xt=xt, pp=pp, h=h, nr=nr: e.scalar_tensor_tensor(
                            out=xt[:nr, h * 512:(h + 1) * 512], in0=xt[:nr, h * 512:(h + 1) * 512], scalar=ALPHA,
                            in1=pp[:nr, :], op0=ALU.mult, op1=ALU.add), reads=[kxt, kpp], writes=[kxt])
                else:
                    kb.op("dve", lambda e, xt=xt, nr=nr: e.tensor_scalar(out=xt[:nr, :], in0=xt[:nr, :], scalar1=ALPHA,
                                                                         scalar2=None, op0=ALU.mult),
                          reads=[kxt], writes=[kxt])
                fo = None
                if last:
                    fo = self.O["y_prompt"][r0:r0 + nr, :] if r0 < self.T else self.O["y_sample"][:, :]
                self.ln_tile(xt, kxt, r0, nr, ti, S["xs"][r0:r0 + nr, :], final_out=fo)
        self.end_phase()

    branches = ""
    NPI = 4
    CORE_BF16 = True

    def have_branch(self, bn):
        return bn in self.branches

    def first_branch(self):
        return self.branches[0] if self.branches else None

    def last_branch(self):
        return self.branches[-1] if self.branches else None


def make_in_map(inputs, c, NS, T, consts):
    m = {}
    m["x_prompt"] = np.ascontiguousarray(inputs["x_prompt"][c, :T])
    m["x_sample"] = np.ascontiguousarray(inputs["x_sample"][c * NS:(c + 1) * NS, 0])
    for n in ("ln_in_g", "ln_in_b", "w_in", "w_out", "ln_g", "ln_b", "w_bm", "w_br", "w_bs", "s_lam_re", "s_lam_im",
              "s_log_dt", "s_b_re", "s_b_im", "s_c_re", "s_c_im", "s_d", "s_glu_w", "s_glu_b",
              "m_conv_w", "m_conv_b", "m_wq", "m_wk", "m_wv", "m_ig_b", "m_fg_b", "m_norm_g", "m_skip",
              "r_mu", "r_w0", "r_w2", "r_a0", "r_a2", "r_k_k", "r_k_a", "r_r_k", "r_ln_g", "r_ln_b"):
        m[n] = np.ascontiguousarray(inputs[n])
    for n in ("state_s5_re", "state_s5_im", "state_mlstm_conv", "state_mlstm_c", "state_mlstm_n", "state_mlstm_m",
              "state_rwkv_wkv", "state_rwkv_shift"):
        m[n] = np.ascontiguousarray(inputs[n][:, c * NS:(c + 1) * NS])
    m.update(consts)
    return m


def kernel(**inputs):
    inputs = {k: np.asarray(v) for k, v in inputs.items()}
    T, NS, L = 2048, 16, 4
    Prog.branches = BRANCHES
    prog = Prog(T, NS, L)
    nc = prog.build()
    consts = host_consts()
    in_maps = [make_in_map(inputs, c, NS, T, consts) for c in range(8)]
    res = run_bass_kernel_spmd(nc, in_maps, core_ids=list(range(8)))
    rs = res.results
    f = np.float32

    def pstack(name, shape):
        if name in rs[0]:
            return np.stack([np.asarray(rs[c][name]).reshape((L,) + shape) for c in range(8)], 1).astype(f)
        return np.zeros((L, 8) + shape, f)

    def sstack(name, shape):
        if name in rs[0]:
            return np.concatenate([np.asarray(rs[c][name]).reshape((L, NS) + shape) for c in range(8)], 1).astype(f)
        return np.zeros((L, 8 * NS) + shape, f)

    y_prompt = np.stack([rs[c]["y_prompt"] for c in range(8)], 0).astype(f)
    y_sample = np.concatenate([rs[c]["y_sample"] for c in range(8)], 0)[:, None, :].astype(f)
    return (y_prompt, y_sample,
            pstack("c_prompt", (4, 256, 256)), sstack("c_sample", (4, 256, 256)),
            pstack("n_prompt", (4, 256)), sstack("n_sample", (4, 256)),
            pstack("m_prompt", (4,)), sstack("m_sample", (4,)),
            pstack("conv_prompt", (3, D)), sstack("conv_sample", (3, D)),
            pstack("wkv_prompt", (16, 64, 64)), sstack("wkv_sample", (16, 64, 64)),
            pstack("shift_prompt", (R_SHIFT_W,)), sstack("shift_sample", (R_SHIFT_W,)),
            pstack("s5_re_prompt", (64, 64)), sstack("s5_re_sample", (64, 64)),
            pstack("s5_im_prompt", (64, 64)), sstack("s5_im_sample", (64, 64)))
```
